# Optimizing a Trainium2 kernel written in Bass

```python
import math
import jax, jax.numpy as jnp
from jax import lax
import numpy as np

D_MODEL = 1024
BATCH = 8
SEQ = 4096
DEPTH = 1
DEC_BATCH = 32
DEC_SEQ = 4
PAST_LEN = 16384
PAGE_SIZE = 128

NSA_HEADS = 8
NSA_KV_HEADS = 2
NSA_DH = 64
NSA_GROUP = NSA_HEADS // NSA_KV_HEADS
NSA_W = NSA_HEADS * NSA_DH
KV_W = NSA_KV_HEADS * NSA_DH
CMP_STRIDE = 16
CMP_LEN = 2 * CMP_STRIDE
CMP_HIDDEN = 256
SEL_BLOCK = 64
SEL_TOPK = 16
WINDOW = 512
Q_BLOCK = 64
MLSTM_HEADS = 4
MLSTM_DH = 128
MLSTM_W = MLSTM_HEADS * MLSTM_DH
CONV_W = 4
MLSTM_CHUNK = 64
MEM_TOKENS = 256
XA_HEADS = 4
XA_DH = D_MODEL // XA_HEADS
D_FF = -(-(8 * D_MODEL) // (3 * 256)) * 256
IN_SPLITS = (NSA_W, KV_W, KV_W, KV_W, KV_W, KV_W, KV_W, 3 * NSA_HEADS, MLSTM_W, MLSTM_W, MLSTM_W, MLSTM_HEADS, MLSTM_HEADS)
IN_COLS = NSA_W + 6 * KV_W + 3 * NSA_HEADS + 3 * MLSTM_W + 2 * MLSTM_HEADS
EPS = 1e-6
NEG_INF = -1e30
FORCE_SCORE = 1e9

kernel_name = 'hymba_nsa_mlstm_decode_step'


def rmsnorm(x, g):
    xf = x.astype(jnp.float32)
    y = xf * lax.rsqrt(jnp.mean(xf * xf, axis=-1, keepdims=True) + EPS)
    return (y * g).astype(x.dtype)


def alibi_slopes(n):
    return 2.0 ** (-8.0 * jnp.arange(1, n + 1, dtype=jnp.float32) / n)


def masked_softmax(s, valid):
    s = jnp.where(valid, s.astype(jnp.float32), NEG_INF)
    mx = jnp.max(s, axis=-1, keepdims=True)
    p = jnp.where(valid, jnp.exp(s - mx), 0.0)
    return p / jnp.maximum(jnp.sum(p, axis=-1, keepdims=True), 1e-30)


def split_cols(a):
    cuts = np.cumsum(IN_SPLITS)[:-1].tolist()
    return jnp.split(a, cuts, axis=-1)


def pad_rows(a, mult):
    extra = (-a.shape[1]) % mult
    return jnp.pad(a, ((0, 0), (0, extra), (0, 0), (0, 0)))


def gather_pages(pool, page_table):
    g = pool[page_table]
    return g.reshape((page_table.shape[0], -1) + pool.shape[2:])


def compress(kv, pe, w1, b1, w2):
    b, tk, hk, dh = kv.shape
    nc = tk // CMP_STRIDE
    half = CMP_STRIDE * dh
    ch = kv.reshape(b, nc, CMP_STRIDE, hk, dh).transpose(0, 1, 3, 2, 4).reshape(b, nc, hk, half)
    first = ch @ w1[:half]
    second = ch @ w1[half:]
    second = jnp.concatenate([second[:, 1:], jnp.zeros_like(second[:, :1])], axis=1)
    hid = first + second + (pe.reshape(-1) @ w1 + b1)
    return jax.nn.gelu(hid) @ w2


def cmp_to_sel(p):
    r = SEL_BLOCK // CMP_STRIDE
    pp = jnp.pad(p, [(0, 0)] * (p.ndim - 1) + [(1, r - 1)])
    pp = pp.reshape(p.shape[:-1] + (p.shape[-1] // r + 1, r))
    return pp[..., :-1, 1:].sum(-1) + 0.5 * (pp[..., :-1, 0] + pp[..., 1:, 0])


def nsa_attend(q, k_cmp, v_cmp, k_slc, v_slc, k_win, v_win, gates, offset, w_pre):
    b, tq, h, dh = q.shape
    qb = math.gcd(tq, Q_BLOCK)
    nb = tq // qb
    nc = k_cmp.shape[1]
    ns = k_slc.shape[1] // SEL_BLOCK
    n_top = min(SEL_TOPK, ns)
    slopes = alibi_slopes(h).reshape(1, NSA_KV_HEADS, NSA_GROUP, 1, 1)
    qg = (q * dh ** -0.5).reshape(b, nb, qb, NSA_KV_HEADS, NSA_GROUP, dh).transpose(1, 0, 3, 4, 2, 5)
    kc = k_cmp.transpose(0, 2, 1, 3)
    vc = v_cmp.transpose(0, 2, 1, 3)
    ks = k_slc.reshape(b, ns, SEL_BLOCK, NSA_KV_HEADS, dh).transpose(0, 3, 1, 2, 4)
    vs = v_slc.reshape(b, ns, SEL_BLOCK, NSA_KV_HEADS, dh).transpose(0, 3, 1, 2, 4)
    kw = k_win.transpose(0, 2, 1, 3)
    vw = v_win.transpose(0, 2, 1, 3)
    cmp_end = jnp.arange(nc) * CMP_STRIDE + (CMP_LEN - 1)
    blk_id = jnp.arange(ns)
    gather = jax.vmap(jax.vmap(lambda blocks, idx: blocks[idx]))

    def one_block(args):
        i, qi = args
        t = offset + i * qb + jnp.arange(qb)
        d_c = t[:, None] - cmp_end[None, :]
        s_c = jnp.einsum('bkgqd,bkcd->bkgqc', qi, kc) - slopes * d_c
        p_c = masked_softmax(s_c, d_c >= 0)
        o_c = jnp.einsum('bkgqc,bkcd->bkgqd', p_c, vc)
        imp = cmp_to_sel(p_c.sum(2))
        cur = t[:, None] // SEL_BLOCK
        forced = (blk_id == 0) | (blk_id == cur) | (blk_id == cur - 1)
        imp = jnp.where(forced, FORCE_SCORE, imp)
        imp = jnp.where(blk_id * SEL_BLOCK <= t[:, None], imp, NEG_INF)
        _, idx = lax.top_k(imp, n_top)
        k_sel = gather(ks, idx).reshape(b, NSA_KV_HEADS, qb, n_top * SEL_BLOCK, dh)
        v_sel = gather(vs, idx).reshape(b, NSA_KV_HEADS, qb, n_top * SEL_BLOCK, dh)
        pos_s = (idx[..., None] * SEL_BLOCK + jnp.arange(SEL_BLOCK)).reshape(b, NSA_KV_HEADS, 1, qb, n_top * SEL_BLOCK)
        d_s = t[:, None] - pos_s
        s_s = jnp.einsum('bkgqd,bkqsd->bkgqs', qi, k_sel) - slopes * d_s
        p_s = masked_softmax(s_s, d_s >= 0)
        o_s = jnp.einsum('bkgqs,bkqsd->bkgqd', p_s, v_sel)
        kwi = lax.dynamic_slice_in_dim(kw, i * qb, w_pre + qb, axis=2)
        vwi = lax.dynamic_slice_in_dim(vw, i * qb, w_pre + qb, axis=2)
        pos_w = offset - w_pre + i * qb + jnp.arange(w_pre + qb)
        d_w = t[:, None] - pos_w[None, :]
        valid_w = (d_w >= 0) & (d_w < WINDOW) & (pos_w[None, :] >= 0)
        s_w = jnp.einsum('bkgqd,bkwd->bkgqw', qi, kwi) - slopes * d_w
        p_w = masked_softmax(s_w, valid_w)
        o_w = jnp.einsum('bkgqw,bkwd->bkgqd', p_w, vwi)
        return jnp.stack([o_c, o_s, o_w], axis=-1)

    o = lax.map(one_block, (jnp.arange(nb), qg))
    o = o.transpose(1, 0, 4, 2, 3, 5, 6).reshape(b, tq, h, dh, 3)
    return jnp.sum(o * gates[:, :, :, None, :], axis=-1).astype(q.dtype)


def mlstm_chunked(q, k, v, i_pre, f_pre, C0, n0, m0):
    b, t, nh, dh = q.shape
    L = math.gcd(t, MLSTM_CHUNK)
    nck = t // L

    def to_chunks(a):
        return a.astype(jnp.float32).reshape((b, nck, L) + a.shape[2:]).swapaxes(0, 1)

    qc, kc, vc, ic = to_chunks(q), to_chunks(k), to_chunks(v), to_chunks(i_pre)
    lf = to_chunks(jax.nn.log_sigmoid(f_pre.astype(jnp.float32)))
    causal = jnp.tril(jnp.ones((L, L), dtype=bool))[None, :, :, None]

    def step(carry, xs):
        C, n, m = carry
        qi, ki, vi, ii, lfi = xs
        bcum = jnp.cumsum(lfi, axis=1)
        dmat = jnp.where(causal, bcum[:, :, None, :] - bcum[:, None, :, :] + ii[:, None, :, :], NEG_INF)
        inter = bcum + m[:, None, :]
        m_q = jnp.maximum(inter, jnp.max(dmat, axis=2))
        a = jnp.exp(dmat - m_q[:, :, None, :]) * jnp.einsum('blhd,bshd->blsh', qi, ki)
        w_inter = jnp.exp(inter - m_q)
        num = jnp.einsum('blsh,bshd->blhd', a, vi) + w_inter[..., None] * jnp.einsum('bhed,blhd->blhe', C, qi)
        den = jnp.sum(a, axis=2) + w_inter * jnp.einsum('bhd,blhd->blh', n, qi)
        h_out = num / jnp.maximum(jnp.abs(den), jnp.exp(-m_q))[..., None]
        btot = bcum[:, -1]
        dec = btot[:, None, :] - bcum + ii
        m_new = jnp.maximum(btot + m, jnp.max(dec, axis=1))
        w_s = jnp.exp(dec - m_new[:, None, :])
        w_c = jnp.exp(btot + m - m_new)
        C_new = w_c[..., None, None] * C + jnp.einsum('bsh,bshe,bshd->bhed', w_s, vi, ki)
        n_new = w_c[..., None] * n + jnp.einsum('bsh,bshd->bhd', w_s, ki)
        return (C_new, n_new, m_new), h_out

    init = (C0.astype(jnp.float32), n0.astype(jnp.float32), m0.astype(jnp.float32))
    (C, n, m), h = lax.scan(step, init, (qc, kc, vc, ic, lf))
    return h.swapaxes(0, 1).reshape(b, t, nh, dh), C, n, m


def decoder_layer(x, past_kc, past_vc, past_ks, past_vs, win_k, win_v, conv_buf, C0, n0, m0, mem_k, mem_v, w):
    b, t, _ = x.shape
    offset = past_kc.shape[1]
    w_pre = win_k.shape[1]
    h = rmsnorm(x, w['g_mix'])
    (q, kc_new, vc_new, ks_new, vs_new, kw_new, vw_new, g_nsa, xm, vm, om, i_pre, f_pre) = split_cols(h @ w['w_in'])
    kvs = (b, t, NSA_KV_HEADS, NSA_DH)
    q = q.reshape(b, t, NSA_HEADS, NSA_DH)
    kc_new, vc_new, ks_new, vs_new, kw_new, vw_new = [a.reshape(kvs) for a in (kc_new, vc_new, ks_new, vs_new, kw_new, vw_new)]
    kc_all = pad_rows(jnp.concatenate([past_kc.astype(x.dtype), kc_new], axis=1), SEL_BLOCK)
    vc_all = pad_rows(jnp.concatenate([past_vc.astype(x.dtype), vc_new], axis=1), SEL_BLOCK)
    ks_all = pad_rows(jnp.concatenate([past_ks.astype(x.dtype), ks_new], axis=1), SEL_BLOCK)
    vs_all = pad_rows(jnp.concatenate([past_vs.astype(x.dtype), vs_new], axis=1), SEL_BLOCK)
    k_cmp = compress(kc_all, *w['cmp_k'])
    v_cmp = compress(vc_all, *w['cmp_v'])
    kw_all = jnp.concatenate([win_k.astype(x.dtype), kw_new], axis=1)
    vw_all = jnp.concatenate([win_v.astype(x.dtype), vw_new], axis=1)
    gates = jax.nn.sigmoid(g_nsa + w['b_gate']).reshape(b, t, NSA_HEADS, 3)
    o_nsa = nsa_attend(q, k_cmp, v_cmp, ks_all, vs_all, kw_all, vw_all, gates, offset, w_pre)
    o_nsa = rmsnorm(o_nsa, w['g_head_nsa']).reshape(b, t, NSA_W)
    xm_all = jnp.concatenate([conv_buf.astype(x.dtype), xm], axis=1)
    xconv = sum(w['conv_w'][j] * xm_all[:, j:j + t] for j in range(CONV_W)) + w['conv_b']
    xconv = jax.nn.silu(xconv).reshape(b, t, MLSTM_HEADS, MLSTM_DH)
    qm = jnp.einsum('bthd,hde->bthe', xconv, w['w_qm'])
    km = jnp.einsum('bthd,hde->bthe', xconv, w['w_km']) * MLSTM_DH ** -0.5
    vm = vm.reshape(b, t, MLSTM_HEADS, MLSTM_DH)
    hm, C, n, m = mlstm_chunked(qm, km, vm, i_pre + w['b_i'], f_pre + w['b_f'], C0, n0, m0)
    hm = rmsnorm(hm, w['g_head_m']) * jax.nn.sigmoid(om.astype(jnp.float32)).reshape(b, t, MLSTM_HEADS, MLSTM_DH)
    mix = jnp.concatenate([o_nsa, hm.reshape(b, t, MLSTM_W).astype(x.dtype)], axis=-1) @ w['w_out']
    x = x + mix
    h = rmsnorm(x, w['g_xa'])
    qx = (h @ w['w_xq']).reshape(b, t, XA_HEADS, XA_DH) * XA_DH ** -0.5
    p = jax.nn.softmax(jnp.einsum('bthd,bmhd->bhtm', qx, mem_k).astype(jnp.float32), axis=-1)
    ox = jnp.einsum('bhtm,bmhd->bthd', p, mem_v).reshape(b, t, D_MODEL).astype(x.dtype)
    x = x + ox @ w['w_xo']
    h = rmsnorm(x, w['g_ffn'])
    x = x + (jax.nn.silu(h @ w['w_gate']) * (h @ w['w_up'])) @ w['w_down']
    keep = min(WINDOW, offset + t)
    return x, (kc_new, vc_new, ks_new, vs_new, kw_all[:, kw_all.shape[1] - keep:], vw_all[:, vw_all.shape[1] - keep:], C, n, m, xm_all[:, -(CONV_W - 1):])


def setup_inputs(seed: int = 0) -> dict:
    key = jax.random.key(seed)
    keys = iter(jax.random.split(key, 96))

    def nrm(shape, scale):
        return jax.random.normal(next(keys), shape, jnp.float32) * scale

    def gain(shape):
        return 1.0 + nrm(shape, 0.05)

    L = DEPTH
    n_pages = PAST_LEN // PAGE_SIZE
    n_pool = (DEC_BATCH * n_pages * 5) // 4
    win_buf = min(WINDOW, PAST_LEN)
    pool_shape = (L, n_pool, PAGE_SIZE, NSA_KV_HEADS, NSA_DH)
    page_table = jax.random.permutation(next(keys), n_pool)[: DEC_BATCH * n_pages].reshape(DEC_BATCH, n_pages).astype(jnp.int32)
    return {
        'x_prompt': nrm((BATCH, SEQ, D_MODEL), 1.0),
        'x_sample': nrm((DEC_BATCH, DEC_SEQ, D_MODEL), 1.0),
        'cache_k_cmp': nrm(pool_shape, 1.0),
        'cache_v_cmp': nrm(pool_shape, 1.0),
        'cache_k_slc': nrm(pool_shape, 1.0),
        'cache_v_slc': nrm(pool_shape, 1.0),
        'state_k_win': nrm((L, DEC_BATCH, win_buf, NSA_KV_HEADS, NSA_DH), 1.0),
        'state_v_win': nrm((L, DEC_BATCH, win_buf, NSA_KV_HEADS, NSA_DH), 1.0),
        'state_conv': nrm((L, DEC_BATCH, CONV_W - 1, MLSTM_W), 1.0),
        'state_C': nrm((L, DEC_BATCH, MLSTM_HEADS, MLSTM_DH, MLSTM_DH), 0.3),
        'state_n': jnp.abs(nrm((L, DEC_BATCH, MLSTM_HEADS, MLSTM_DH), 1.0)),
        'state_m': nrm((L, DEC_BATCH, MLSTM_HEADS), 0.5),
        'cache_mem_k': nrm((L, DEC_BATCH, MEM_TOKENS, XA_HEADS, XA_DH), 1.0),
        'cache_mem_v': nrm((L, DEC_BATCH, MEM_TOKENS, XA_HEADS, XA_DH), 1.0),
        'page_table': page_table,
        'mem_prompt': nrm((BATCH, MEM_TOKENS, D_MODEL), 1.0),
        'g_mix': gain((L, D_MODEL)),
        'w_in': nrm((L, D_MODEL, IN_COLS), D_MODEL ** -0.5),
        'b_gate': nrm((L, 3 * NSA_HEADS), 0.1),
        'cmp_pe_k': nrm((L, CMP_LEN, NSA_DH), 0.1),
        'cmp_w1_k': nrm((L, CMP_LEN * NSA_DH, CMP_HIDDEN), (CMP_LEN * NSA_DH) ** -0.5),
        'cmp_b1_k': nrm((L, CMP_HIDDEN), 0.02),
        'cmp_w2_k': nrm((L, CMP_HIDDEN, NSA_DH), CMP_HIDDEN ** -0.5),
        'cmp_pe_v': nrm((L, CMP_LEN, NSA_DH), 0.1),
        'cmp_w1_v': nrm((L, CMP_LEN * NSA_DH, CMP_HIDDEN), (CMP_LEN * NSA_DH) ** -0.5),
        'cmp_b1_v': nrm((L, CMP_HIDDEN), 0.02),
        'cmp_w2_v': nrm((L, CMP_HIDDEN, NSA_DH), CMP_HIDDEN ** -0.5),
        'g_head_nsa': gain((L, NSA_HEADS, NSA_DH)),
        'conv_w': nrm((L, CONV_W, MLSTM_W), CONV_W ** -0.5),
        'conv_b': nrm((L, MLSTM_W), 0.02),
        'w_qm': nrm((L, MLSTM_HEADS, MLSTM_DH, MLSTM_DH), MLSTM_DH ** -0.5),
        'w_km': nrm((L, MLSTM_HEADS, MLSTM_DH, MLSTM_DH), MLSTM_DH ** -0.5),
        'b_i': nrm((L, MLSTM_HEADS), 0.1),
        'b_f': jnp.linspace(3.0, 6.0, MLSTM_HEADS)[None, :] + nrm((L, MLSTM_HEADS), 0.1),
        'g_head_m': gain((L, MLSTM_HEADS, MLSTM_DH)),
        'w_out': nrm((L, D_MODEL, D_MODEL), D_MODEL ** -0.5),
        'g_xa': gain((L, D_MODEL)),
        'g_mem': gain((L, D_MODEL)),
        'w_xq': nrm((L, D_MODEL, D_MODEL), D_MODEL ** -0.5),
        'w_xk': nrm((L, D_MODEL, D_MODEL), D_MODEL ** -0.5),
        'w_xv': nrm((L, D_MODEL, D_MODEL), D_MODEL ** -0.5),
        'w_xo': nrm((L, D_MODEL, D_MODEL), D_MODEL ** -0.5),
        'g_ffn': gain((L, D_MODEL)),
        'w_gate': nrm((L, D_MODEL, D_FF), D_MODEL ** -0.5),
        'w_up': nrm((L, D_MODEL, D_FF), D_MODEL ** -0.5),
        'w_down': nrm((L, D_FF, D_MODEL), D_FF ** -0.5),
        'g_final': gain((D_MODEL,)),
    }


def reference(x_prompt, x_sample, cache_k_cmp, cache_v_cmp, cache_k_slc, cache_v_slc, state_k_win, state_v_win,
              state_conv, state_C, state_n, state_m, cache_mem_k, cache_mem_v, page_table, mem_prompt,
              g_mix, w_in, b_gate, cmp_pe_k, cmp_w1_k, cmp_b1_k, cmp_w2_k, cmp_pe_v, cmp_w1_v, cmp_b1_v, cmp_w2_v,
              g_head_nsa, conv_w, conv_b, w_qm, w_km, b_i, b_f, g_head_m, w_out, g_xa, g_mem, w_xq, w_xk, w_xv, w_xo,
              g_ffn, w_gate, w_up, w_down, g_final):
    xp, xs = x_prompt, x_sample
    b = xp.shape[0]
    outs_p, outs_s = [], []
    for l in range(DEPTH):
        w = dict(g_mix=g_mix[l], w_in=w_in[l], b_gate=b_gate[l],
                 cmp_k=(cmp_pe_k[l], cmp_w1_k[l], cmp_b1_k[l], cmp_w2_k[l]),
                 cmp_v=(cmp_pe_v[l], cmp_w1_v[l], cmp_b1_v[l], cmp_w2_v[l]),
                 g_head_nsa=g_head_nsa[l], conv_w=conv_w[l], conv_b=conv_b[l], w_qm=w_qm[l], w_km=w_km[l],
                 b_i=b_i[l], b_f=b_f[l], g_head_m=g_head_m[l], w_out=w_out[l], g_xa=g_xa[l], w_xq=w_xq[l],
                 w_xo=w_xo[l], g_ffn=g_ffn[l], w_gate=w_gate[l], w_up=w_up[l], w_down=w_down[l])
        z_past = jnp.zeros((b, 0, NSA_KV_HEADS, NSA_DH), xp.dtype)
        z_win = jnp.zeros((b, WINDOW, NSA_KV_HEADS, NSA_DH), xp.dtype)
        z_conv = jnp.zeros((b, CONV_W - 1, MLSTM_W), xp.dtype)
        z_C = jnp.zeros((b, MLSTM_HEADS, MLSTM_DH, MLSTM_DH), jnp.float32)
        z_n = jnp.zeros((b, MLSTM_HEADS, MLSTM_DH), jnp.float32)
        z_m = jnp.zeros((b, MLSTM_HEADS), jnp.float32)
        mem_n = rmsnorm(mem_prompt, g_mem[l])
        mk = (mem_n @ w_xk[l]).reshape(b, MEM_TOKENS, XA_HEADS, XA_DH)
        mv = (mem_n @ w_xv[l]).reshape(b, MEM_TOKENS, XA_HEADS, XA_DH)
        xp, st_p = decoder_layer(xp, z_past, z_past, z_past, z_past, z_win, z_win, z_conv, z_C, z_n, z_m, mk, mv, w)
        outs_p.append(st_p + (mk, mv))
        xs, st_s = decoder_layer(xs, gather_pages(cache_k_cmp[l], page_table), gather_pages(cache_v_cmp[l], page_table),
                                 gather_pages(cache_k_slc[l], page_table), gather_pages(cache_v_slc[l], page_table),
                                 state_k_win[l], state_v_win[l], state_conv[l], state_C[l], state_n[l], state_m[l],
                                 cache_mem_k[l], cache_mem_v[l], w)
        outs_s.append(st_s)
    y_prompt = rmsnorm(xp, g_final)
    y_sample = rmsnorm(xs, g_final)
    (p_k_cmp, p_v_cmp, p_k_slc, p_v_slc, p_k_win, p_v_win, p_C, p_n, p_m, p_conv, p_mem_k, p_mem_v) = [jnp.stack(a) for a in zip(*outs_p)]
    (s_k_cmp, s_v_cmp, s_k_slc, s_v_slc, s_k_win, s_v_win, s_C, s_n, s_m, s_conv) = [jnp.stack(a) for a in zip(*outs_s)]
    return (y_prompt, y_sample, p_k_cmp, p_v_cmp, p_k_slc, p_v_slc, p_k_win, p_v_win, p_C, p_n, p_m, p_conv, p_mem_k, p_mem_v,
            s_k_cmp, s_v_cmp, s_k_slc, s_v_slc, s_k_win, s_v_win, s_C, s_n, s_m, s_conv)
```

```python
import numpy as np
from contextlib import ExitStack
import concourse.bass as bass
import concourse.mybir as mybir
from concourse.bass_utils import run_bass_kernel_spmd

F32 = mybir.dt.float32
BF16 = mybir.dt.bfloat16
I32 = mybir.dt.int32
AF = mybir.ActivationFunctionType
ALU = mybir.AluOpType
AX = mybir.AxisListType

D = 1024
NEG = -30000.0
EPS = 1e-6
IN_COLS = 2848
DFF = 2816


class V:
    __slots__ = ("b", "a")

    def __init__(self, b, a):
        self.b = b
        self.a = a

    def __getitem__(self, k):
        return V(self.b, self.a[k])

    def re(self, p, **kw):
        return V(self.b, self.a.rearrange(p, **kw))

    def bc(self, shape):
        return V(self.b, self.a.to_broadcast(list(shape)))

    def un(self, ax):
        return V(self.b, self.a.unsqueeze(ax))


class Buf:
    __slots__ = ("t", "w", "r", "name", "psum", "fresh", "quads")

    def __init__(self, t, name="", psum=False):
        self.t = t
        self.w = None
        self.r = []
        self.name = name
        self.psum = psum
        self.fresh = True
        self.quads = set()

    def __getitem__(self, k):
        return V(self, self.t[k])


class DSem:
    def __init__(self, nc, name):
        self.sem = nc.alloc_semaphore(name)
        self.val = 0


class FW:
    ENG = ("pe", "act", "dve", "pool", "sp")

    def __init__(self, nc, n_dsem=10, same_engine_sync=True):
        self.nc = nc
        self.e = {"pe": nc.tensor, "act": nc.scalar, "dve": nc.vector, "pool": nc.gpsimd, "sp": nc.sync}
        self.gen = {k: 0 for k in self.ENG}
        self.sem = {k: nc.alloc_semaphore("S_" + k) for k in self.ENG}
        self.cnt = {k: 0 for k in self.ENG}
        self.seen = {k: {} for k in self.ENG}
        self.same = same_engine_sync
        self.dsems = {q: [DSem(nc, f"D{q}{i}") for i in range(n_dsem)] for q in ("sp", "pool", "act")}
        self.dnext = {q: 0 for q in self.dsems}
        self.nbuf = 0
        self.nins = 0

    def sb(self, stack, shape, dt=F32, name=None):
        self.nbuf += 1
        name = name or f"b{self.nbuf}"
        return Buf(stack.enter_context(self.nc.sbuf_tensor(name, list(shape), dt)), name)

    def ps(self, stack, shape, dt=F32, name=None):
        self.nbuf += 1
        name = name or f"p{self.nbuf}"
        return Buf(stack.enter_context(self.nc.psum_tensor(name, list(shape), dt)), name, psum=True)

    def _need(self, e, dep, waits):
        if dep is None:
            return
        kind, key, val, semh = dep
        if kind == "e" and key[0] == e and (not self.same or e == "pe"):
            return
        k = (kind, key if kind == "e" else id(key))
        if self.seen[e].get(k, 0) >= val:
            return
        cur = waits.get(k)
        if cur is None or cur[1] < val:
            waits[k] = (semh, val)

    def _emit_waits(self, e, reads, writes):
        waits = {}
        for b in reads:
            if b is not None:
                self._need(e, b.w, waits)
        for b in writes:
            if b is not None:
                self._need(e, b.w, waits)
                for d in b.r:
                    self._need(e, d, waits)
        eng = self.e[e]
        for k, (semh, val) in waits.items():
            eng.wait_ge(semh, val)
            self.seen[e][k] = val

    def op(self, e, fn, reads=(), writes=()):
        px = [b for b in reads if b is not None and b.psum]
        if px:
            reads = [b for b in reads if not (b is not None and b.psum)]
            writes = list(writes) + [b for b in px if b not in writes]
            if e != "pe":
                for b in px:
                    b.fresh = True
        self._emit_waits(e, reads, writes)
        ins = fn(self.e[e])
        if self.cnt[e] >= 50000:
            self.gen[e] += 1
            self.sem[e] = self.nc.alloc_semaphore(f"S_{e}_{self.gen[e]}")
            self.cnt[e] = 0
        self.cnt[e] += 1
        self.nins += 1
        ins.then_inc(self.sem[e], 1)
        dep = ("e", (e, self.gen[e]), self.cnt[e], self.sem[e])
        for b in reads:
            if b is not None:
                b.r.append(dep)
                if len(b.r) > 16:
                    b.r = self._compact(b.r)
        for b in writes:
            if b is not None:
                b.w = dep
                b.r = []
        return ins

    @staticmethod
    def _compact(lst):
        best = {}
        for d in lst:
            k = (d[0], d[1] if d[0] == "e" else id(d[1]))
            if k not in best or best[k][2] < d[2]:
                best[k] = d
        return list(best.values())

    def dma(self, q, o, i, fn=None, extra_reads=(), **kw):
        reads = [i.b] + list(extra_reads)
        writes = [o.b]
        self._emit_waits(q, reads, writes)
        ds = self.dsems[q][self.dnext[q]]
        self.dnext[q] = (self.dnext[q] + 1) % len(self.dsems[q])
        if ds.val > 0 and self.seen[q].get(("d", id(ds)), 0) < ds.val:
            self.e[q].wait_ge(ds.sem, ds.val)
            self.seen[q][("d", id(ds))] = ds.val
        if fn is None:
            ins = self.e[q].dma_start(out=o.a, in_=i.a, **kw)
        else:
            ins = fn(self.e[q])
        ds.val += 16
        self.nins += 1
        ins.then_inc(ds.sem, 16)
        dep = ("d", ds, ds.val, ds.sem)
        for b in reads:
            if b is not None:
                b.r.append(dep)
                if len(b.r) > 16:
                    b.r = self._compact(b.r)
        for b in writes:
            if b is not None:
                b.w = dep
                b.r = []
        return ins

    def barrier(self):
        for e in self.ENG:
            eng = self.e[e]
            for f in self.ENG:
                if f != e and self.cnt[f] > 0:
                    k = ("e", (f, self.gen[f]))
                    if self.seen[e].get(k, 0) < self.cnt[f]:
                        eng.wait_ge(self.sem[f], self.cnt[f])
                        self.seen[e][k] = self.cnt[f]
            for q in self.dsems:
                for ds in self.dsems[q]:
                    k = ("d", id(ds))
                    if ds.val > 0 and self.seen[e].get(k, 0) < ds.val:
                        eng.wait_ge(ds.sem, ds.val)
                        self.seen[e][k] = ds.val

    def finish(self):
        eng = self.e["sp"]
        for q in self.dsems:
            for ds in self.dsems[q]:
                if ds.val > 0:
                    eng.wait_ge(ds.sem, ds.val)


class RR:
    def __init__(self, items):
        self.items = list(items)
        self.i = 0

    def __call__(self):
        x = self.items[self.i]
        self.i = (self.i + 1) % len(self.items)
        return x


class Builder:
    def __init__(self, nc, NT=32, sample=True, debug=False):
        self.debug = debug
        self.nc = nc
        self.fw = FW(nc)
        self.NT = NT
        self.T = NT * 128
        self.sample = sample
        self.io = {}

    def din(self, name, shape, dt=F32):
        t = self.nc.dram_tensor(name, list(shape), dt, kind="ExternalInput").ap()
        self.io[name] = t
        return V(None, t)

    def dout(self, name, shape, dt=F32):
        t = self.nc.dram_tensor(name, list(shape), dt, kind="ExternalOutput").ap()
        self.io[name] = t
        return V(None, t)

    def dscr(self, name, shape, dt=F32):
        t = self.nc.dram_tensor(name, list(shape), dt, kind="ExternalOutput" if self.debug else "Internal").ap()
        if self.debug:
            self.io[name] = t
        return Buf(t, name)

    def dbg(self, name, v, dt=F32):
        if not self.debug:
            return
        o = self.dout("dbg_" + name, list(v.a.shape), dt)
        self.fw.dma("sp", o, v)

    def mm(self, o, l, r, st=True, sp=True):
        b = o.b
        p0 = o.a.base_partition() if hasattr(o.a, "base_partition") else 0
        q = set(range(p0 // 32, (p0 + o.a.shape[0] + 31) // 32))
        start = False
        if st:
            if b.fresh:
                start = True
                b.fresh = False
                b.quads = set(q)
            else:
                assert q <= b.quads, (b.name, q, b.quads)
        self.fw.op("pe", lambda e: e.matmul(o.a, lhsT=l.a, rhs=r.a, start=start, stop=sp, skip_group_check=True),
                   reads=[l.b, r.b], writes=[o.b])

    def tr(self, o, i, ident):
        self.fw.op("pe", lambda e: e.transpose(out=o.a, in_=i.a, identity=ident.a), reads=[i.b, ident.b], writes=[o.b])

    def act(self, o, i, f, scale=1.0, bias=0.0, acc=None):
        reads = [i.b]
        writes = [o.b]
        kw = {}
        if isinstance(bias, V):
            reads.append(bias.b)
            kw["bias"] = bias.a
        elif bias != 0.0:
            kw["bias"] = float(bias)
        if isinstance(scale, V):
            reads.append(scale.b)
            kw["scale"] = scale.a
        elif scale != 1.0:
            kw["scale"] = float(scale)
        if acc is not None:
            writes.append(acc.b)
            kw["accum_out"] = acc.a
        self.fw.op("act", lambda e: e.activation(out=o.a, in_=i.a, func=f, **kw), reads=reads, writes=writes)

    def ts(self, eng, o, i, s1, s2=None, op0=ALU.mult, op1=None):
        reads = [i.b]
        a1 = s1
        a2 = s2
        if isinstance(s1, V):
            reads.append(s1.b)
            a1 = s1.a
        if isinstance(s2, V):
            reads.append(s2.b)
            a2 = s2.a
        kw = {}
        if op1 is not None:
            kw["op1"] = op1
        self.fw.op(eng, lambda e: e.tensor_scalar(out=o.a, in0=i.a, scalar1=a1, scalar2=a2, op0=op0, **kw), reads=reads, writes=[o.b])

    def tt(self, eng, o, a, b, op):
        self.fw.op(eng, lambda e: e.tensor_tensor(out=o.a, in0=a.a, in1=b.a, op=op), reads=[a.b, b.b], writes=[o.b])

    def stt(self, eng, o, a, s, b, op0, op1):
        reads = [a.b, b.b]
        sa = s
        if isinstance(s, V):
            reads.append(s.b)
            sa = s.a
        self.fw.op(eng, lambda e: e.scalar_tensor_tensor(out=o.a, in0=a.a, scalar=sa, in1=b.a, op0=op0, op1=op1), reads=reads, writes=[o.b])

    def cp(self, eng, o, i):
        if eng == "act":
            self.fw.op("act", lambda e: e.copy(out=o.a, in_=i.a), reads=[i.b], writes=[o.b])
        else:
            self.fw.op(eng, lambda e: e.tensor_copy(out=o.a, in_=i.a), reads=[i.b], writes=[o.b])

    def ms(self, eng, o, val):
        self.fw.op(eng, lambda e: e.memset(o.a, val), writes=[o.b])

    def iota(self, o, pattern, base=0, cm=0):
        self.fw.op("pool", lambda e: e.iota(o.a, pattern=pattern, base=base, channel_multiplier=cm,
                                            allow_small_or_imprecise_dtypes=True), writes=[o.b])

    def recip(self, o, i):
        self.fw.op("dve", lambda e: e.reciprocal(out=o.a, in_=i.a), reads=[i.b], writes=[o.b])

    def rsum(self, o, i):
        self.fw.op("dve", lambda e: e.reduce_sum(out=o.a, in_=i.a, axis=AX.X), reads=[i.b], writes=[o.b])

    def rmax(self, o, i):
        self.fw.op("dve", lambda e: e.reduce_max(out=o.a, in_=i.a, axis=AX.X), reads=[i.b], writes=[o.b])

    def dma(self, o, i, q="sp", **kw):
        self.fw.dma(q, o, i, **kw)

    def dmas(self, o, i, q="sp"):
        self.fw.dma(q, o, i, allow_slow_non_contiguous=True)

    def put_row(self, dst, pattern, base, n, const=None):
        rowt, rowb = self.rowt, self.rowb
        if const is None:
            rv = rowt[0:1, 0:n]
            if len(pattern) == 2:
                rv = rv.re("p (a b) -> p a b", b=pattern[1][1])
            self.iota(rv, pattern, base=base, cm=0)
        else:
            self.ms("pool", rowt[0:1, 0:n], const)
        self.cp("pool", rowb[0:1, 0:n], rowt[0:1, 0:n])
        self.dma(dst, rowb[0:1, 0:n])

    def rstd(self, o, ss, inv_n):
        self.act(o, ss, AF.Sqrt, scale=inv_n, bias=self.epsc[0:o.a.shape[0], :])
        self.recip(o, o)

    def build(self):
        nc, fw, NT, T = self.nc, self.fw, self.NT, self.T
        din, dout = self.din, self.dout
        mm, tr, act, ts, tt, stt, cp, ms, iota, dma, dmas = (self.mm, self.tr, self.act, self.ts, self.tt, self.stt,
                                                             self.cp, self.ms, self.iota, self.dma, self.dmas)
        xp = din("xp", [T, D])
        memp = din("memp", [256, D])
        w_in = din("w_in", [D, IN_COLS])
        g_mix = din("g_mix", [D])
        b_gate = din("b_gate", [24])
        cmp_in = {}
        for kv in "kv":
            cmp_in[kv] = (din(f"cmp_pe_{kv}", [32, 64]), din(f"cmp_w1_{kv}", [2048, 256]),
                          din(f"cmp_b1_{kv}", [256]), din(f"cmp_w2_{kv}", [256, 64]))
        g_head_nsa = din("g_head_nsa", [512])
        conv_w = din("conv_w", [4, 512])
        conv_b = din("conv_b", [512])
        w_qm = din("w_qm", [4, 128, 128])
        w_km = din("w_km", [4, 128, 128])
        b_i = din("b_i", [4])
        b_f = din("b_f", [4])
        g_head_m = din("g_head_m", [512])
        w_out = din("w_out", [D, D])
        g_xa = din("g_xa", [D])
        g_mem = din("g_mem", [D])
        w_xq = din("w_xq", [D, D])
        w_xk = din("w_xk", [D, D])
        w_xv = din("w_xv", [D, D])
        w_xo = din("w_xo", [D, D])
        g_ffn = din("g_ffn", [D])
        w_gate = din("w_gate", [D, DFF])
        w_up = din("w_up", [D, DFF])
        w_down = din("w_down", [DFF, D])
        g_final = din("g_final", [D])

        y_p = dout("y_p", [T, D])
        p_kv = {n: dout(n, [T, 128]) for n in ("p_kc", "p_vc", "p_ks", "p_vs")}
        WT = min(512, T)
        p_kw = dout("p_kw", [WT, 128])
        p_vw = dout("p_vw", [WT, 128])
        p_C = dout("p_C", [4, 128, 128])
        p_n = dout("p_n", [4, 128])
        p_m = dout("p_m", [4])
        p_conv = dout("p_conv", [3, 512])
        p_mk = dout("p_mk", [256, D])
        p_mv = dout("p_mv", [256, D])

        if self.sample:
            self.sample_io()
        x1d = self.dscr("x1_scr", [T, D])
        x2d = self.dscr("x2_scr", [T, D])

        top = ExitStack()
        with top:
            psf = [fw.ps(top, [128, 512], F32, f"psf{i}") for i in range(4)]
            pst = [fw.ps(top, [128, 1024], BF16, f"pst{i}") for i in range(2)]
            psa = [fw.ps(top, [128, 512], F32, f"psa{i}") for i in range(2)]
            nps = RR(psf)
            npt = RR(pst)

            ident = fw.sb(top, [128, 128], BF16, "ident")
            identf = fw.sb(top, [128, 128], F32, "identf")
            for idt in (ident, identf):
                ms("pool", idt[:], 0.0)
                fw.op("pool", lambda e, idt=idt: e.affine_select(out=idt[:].a, in_=idt[:].a, pattern=[[-1, 128]],
                                                                compare_op=ALU.not_equal, fill=1.0, base=0,
                                                                channel_multiplier=1), reads=[idt], writes=[idt])
            self.epsc = fw.sb(top, [128, 1], F32, "epsc")[:]
            ms("pool", self.epsc, EPS)
            if self.sample:
                self.sample_s0(locals())
            sw = ExitStack()
            tmpf = fw.sb(sw, [128, 512], F32, "tmpf")
            iota(tmpf[:].re("p (g r) -> p g r", g=4), [[0, 4], [-1, 128]], base=0, cm=1)
            caus_add = fw.sb(sw, [128, 512], BF16, "caus_add")
            ts("pool", caus_add[:], tmpf[:], 0.0, NEG, op0=ALU.is_gt, op1=ALU.mult)
            win_add = fw.sb(sw, [128, 512], BF16, "win_add")
            ts("pool", win_add[:], tmpf[:], 0.0, NEG, op0=ALU.is_le, op1=ALU.mult)
            bd01 = fw.sb(sw, [128, 128], BF16, "bd01")
            ts("pool", bd01[:], tmpf[:, 0:128], 0.0, None, op0=ALU.is_le)
            ms("pool", bd01[0:64, 64:128], 0.0)
            tri_le = fw.sb(sw, [128, 128], F32, "tri_le")
            ts("pool", tri_le[:], tmpf[:, 0:128], 0.0, None, op0=ALU.is_le)
            tri2 = fw.sb(sw, [128, 128], F32, "tri2")
            cp("pool", tri2[:], bd01[:])
            csel = fw.sb(sw, [128, 2, 128], F32, "csel")
            ms("pool", csel[:], 0.0)
            ms("pool", csel[0:64, 0, :], 1.0)
            ms("pool", csel[64:128, 1, :], 1.0)
            e0 = fw.sb(sw, [128, 512], F32, "e0")
            iota(e0[:].re("p (g r) -> p g r", g=4), [[0, 4], [-1, 128]], base=0, cm=16)
            mimp = fw.sb(sw, [128, 2, 64], BF16, "mimp")
            for ct in range(2):
                iota(tmpf[:, 0:64], [[-4, 64]], base=ct * 128 - 1, cm=1)
                stt("dve", tmpf[:, 64:128], tmpf[:, 0:64], -1.0, tmpf[:, 0:64], ALU.mult, ALU.max)
                ts("pool", tmpf[:, 128:192], tmpf[:, 64:128], 2.0, 0.5, op0=ALU.is_le, op1=ALU.mult)
                ts("pool", tmpf[:, 192:256], tmpf[:, 64:128], 1.0, 0.5, op0=ALU.is_le, op1=ALU.mult)
                tt("pool", mimp[:, ct, :], tmpf[:, 128:192], tmpf[:, 192:256], ALU.add)
            expand = fw.sb(sw, [64, T], BF16, "expand")
            for c0 in range(0, T, 512):
                iota(tmpf[0:64, :], [[1, 512]], base=c0, cm=-64)
                ts("pool", tmpf[0:64, :], tmpf[0:64, :], 31.5, None, op0=ALU.subtract)
                stt("dve", tmpf[0:64, :], tmpf[0:64, :], -1.0, tmpf[0:64, :], ALU.mult, ALU.max)
                ts("pool", expand[:, c0:c0 + 512], tmpf[0:64, :], 32.0, None, op0=ALU.is_le)

            if True:
                win_b = fw.sb(sw, [128, 8, IN_COLS], BF16, "win_b")
                wout_b = fw.sb(sw, [128, 8, D], BF16, "wout_b")
                wqm_b = fw.sb(sw, [128, 4, 128], BF16, "wqm_b")
                wkm_b = fw.sb(sw, [128, 4, 128], BF16, "wkm_b")
                gcol = fw.sb(sw, [128, 16], F32, "gcol")
                dmas(gcol[:, 0:8], V(None, g_mix.a.rearrange("(k p) -> p k", p=128)))
                dmas(gcol[:, 8:12], V(None, g_head_nsa.a.rearrange("(k p) -> p k", p=128)))
                dmas(gcol[:, 12:16], V(None, g_head_m.a.rearrange("(k p) -> p k", p=128)))
                cw = fw.sb(sw, [128, 4, 4], F32, "cw")
                for j in range(4):
                    dmas(cw[:, :, j], V(None, conv_w.a[j].rearrange("(c p) -> p c", p=128)))
                cb = fw.sb(sw, [128, 4], F32, "cb")
                dmas(cb[:], V(None, conv_b.a.rearrange("(c p) -> p c", p=128)))
                bgate = fw.sb(sw, [128, 24], F32, "bgate")
                dma(bgate[:], V(None, b_gate.a.partition_broadcast(128)))
                bif = fw.sb(sw, [128, 8], F32, "bif")
                dma(bif[:, 0:4], V(None, b_i.a.partition_broadcast(128)))
                dma(bif[:, 4:8], V(None, b_f.a.partition_broadcast(128)))
                with ExitStack() as s0:
                    stg = [fw.sb(s0, [128, IN_COLS], F32, f"stg{i}") for i in range(2)]
                    for k in range(8):
                        st = stg[k % 2]
                        dma(st[:], w_in[k * 128:(k + 1) * 128, :], q="sp" if k % 2 == 0 else "act")
                        ts("dve" if k % 2 == 0 else "pool", win_b[:, k, :], st[:], gcol[:, k:k + 1], None, op0=ALU.mult)
                    for k in range(8):
                        st = stg[k % 2]
                        dma(st[:, 0:D], w_out[k * 128:(k + 1) * 128, :])
                        ts("dve" if k % 2 == 0 else "pool", wout_b[:, k, :], st[:, 0:D], gcol[:, 8 + k:9 + k], None, op0=ALU.mult)
                    st = stg[0]
                    dma(st[:, 0:512].re("p (h e) -> p h e", h=4), V(None, w_qm.a.rearrange("h d e -> d h e")))
                    cp("dve", wqm_b[:], st[:, 0:512].re("p (h e) -> p h e", h=4))
                    st = stg[1]
                    dma(st[:, 0:512].re("p (h e) -> p h e", h=4), V(None, w_km.a.rearrange("h d e -> d h e")))
                    cp("dve", wkm_b[:], st[:, 0:512].re("p (h e) -> p h e", h=4))
                    fw.barrier()

                kcp = fw.sb(sw, [68, 2, 256], BF16, "kcp")
                vcp = fw.sb(sw, [128, 2, 2, 65], BF16, "vcp")
                ms("pool", vcp[:], 1.0)
                put_row = self.put_row
                with ExitStack() as tmps:
                    self.rowt = fw.sb(tmps, [1, 4096], F32, "rowt")
                    self.rowb = fw.sb(tmps, [1, 4096], BF16, "rowb")
                    for kvh in range(2):
                        put_row(kcp[64:65, kvh, :], [[128, 32], [0, 8]], 0, 256)
                        put_row(kcp[65:66, kvh, :], [[0, 32], [16, 8]], 31, 256)
                        put_row(kcp[66:67, kvh, :], None, 0, 256, const=1.0)
                        put_row(kcp[67:68, kvh, :], None, 0, 256, const=1.0)
                    fw.barrier()

                self.pass0_prompt(sw, xp, win_b, cmp_in, kcp, vcp, ident, nps, npt)
                self.dbg("kcp", kcp[:], BF16)
                self.dbg("vcp", vcp[:], BF16)
                self.pass1_prompt(sw, locals())
                if self.sample:
                    self.sample_pass1(locals())
            fw.barrier()
            sw.close()
            self.pass2(top, locals())
            fw.finish()

    def norm_T(self, src, xt, nb, hT, ident, npt, rows=128):
        if rows < 128:
            self.ms("pool", xt[:], 0.0)
        self.dma(xt[0:rows, :], src)
        self.ms("dve", nb["ss"][:], 0.0)
        self.act(nb["junk"][:], xt[:], AF.Square, acc=nb["ss"][:])
        self.rstd(nb["rs"][:], nb["ss"][:], 1.0 / D)
        self.ts("dve", nb["xn"][:], xt[:], nb["rs"][:, 0:1], None, op0=ALU.mult)
        pt = npt()
        for k in range(8):
            self.tr(pt[:, k * 128:(k + 1) * 128], nb["xn"][:, k * 128:(k + 1) * 128], ident[:])
        self.cp("act", hT[:].re("p k t -> p (k t)"), pt[:])

    def norm_bufs(self, s, tag):
        fw = self.fw
        xn = fw.sb(s, [128, D], BF16, "xn" + tag)
        return {"junk": xn, "ss": fw.sb(s, [128, 1], F32, "ss" + tag), "rs": fw.sb(s, [128, 1], F32, "rs" + tag), "xn": xn}

    def pass0_prompt(self, sw, xp, win_b, cmp_in, kcp, vcp, ident, nps, npt):
        fw, NT, T = self.fw, self.NT, self.T
        mm, act, tt, cp, ms, dma, dmas = self.mm, self.act, self.tt, self.cp, self.ms, self.dma, self.dmas
        NCB = T // 16
        with ExitStack() as s:
            srcT = {kv: fw.sb(s, [64, 2, 16, NCB + 1], BF16, "srcT" + kv) for kv in "kv"}
            for kv in "kv":
                ms("pool", srcT[kv][:, :, :, NCB:NCB + 1], 0.0)
            ms("pool", kcp[0:64, :, :], 0.0)
            xts = [fw.sb(s, [128, D], F32, f"x0_{i}") for i in range(2)]
            nb = self.norm_bufs(s, "0")
            hT = fw.sb(s, [128, 8, 128], BF16, "hT0")
            for t_ in range(NT):
                xt = xts[t_ % 2]
                self.norm_T(xp[t_ * 128:(t_ + 1) * 128, :], xt, nb, hT, ident, npt)
                ps = nps()
                for gi in range(4):
                    for k in range(8):
                        mm(ps[0:64, gi * 128:(gi + 1) * 128], win_b[:, k, 512 + 64 * gi:576 + 64 * gi], hT[:, k, :],
                           st=(k == 0), sp=(k == 7))
                for h in range(2):
                    cp("act", srcT["k"][:, h, :, 8 * t_:8 * t_ + 8], ps[0:64, h * 128:(h + 1) * 128].re("p (c j) -> p j c", j=16))
                    cp("dve", srcT["v"][:, h, :, 8 * t_:8 * t_ + 8], ps[0:64, 256 + h * 128:384 + h * 128].re("p (c j) -> p j c", j=16))
            w1s = [fw.sb(s, [64, 8, 256], F32, f"w1s{i}") for i in range(2)]
            for kv in "kv":
                pe, w1, b1, w2 = cmp_in[kv]
                w1b = fw.sb(s, [64, 32, 256], BF16, "w1b" + kv)
                for jb in range(4):
                    st = w1s[jb % 2]
                    dma(st[:], V(None, w1.a.rearrange("(j d) n -> d j n", d=64)[:, jb * 8:(jb + 1) * 8, :]))
                    cp("pool", w1b[:, jb * 8:(jb + 1) * 8, :], st[:])
                peT = fw.sb(s, [64, 32], F32, "peT" + kv)
                dmas(peT[:], V(None, pe.a.rearrange("j d -> d j")))
                peTb = fw.sb(s, [64, 32], BF16, "peTb" + kv)
                cp("dve", peTb[:], peT[:])
                b1c = fw.sb(s, [128, 2], F32, "b1c" + kv)
                dmas(b1c[:], V(None, b1.a.rearrange("(c p) -> p c", p=128)))
                w2s = fw.sb(s, [128, 2, 64], F32, "w2s" + kv)
                dma(w2s[:], V(None, w2.a.rearrange("(c p) n -> p c n", p=128)))
                w2b = fw.sb(s, [128, 2, 64], BF16, "w2b" + kv)
                cp("dve", w2b[:], w2s[:])
                cst = fw.sb(s, [128, 2], F32, "cst" + kv)
                for hc in range(2):
                    ps = nps()
                    for j in range(32):
                        mm(ps[:, 0:1], w1b[:, j, hc * 128:(hc + 1) * 128], peTb[:, j:j + 1], st=(j == 0), sp=(j == 31))
                    tt("dve", cst[:, hc:hc + 1], ps[:, 0:1], b1c[:, hc:hc + 1], ALU.add)
                gT = fw.sb(s, [128, 2, 256], BF16, "gT" + kv)
                if NCB < 256:
                    ms("pool", gT[:], 0.0)
                for kvh in range(2):
                    for hc in range(2):
                        ps = nps()
                        for j in range(32):
                            rv = srcT[kv][:, kvh, j, 0:NCB] if j < 16 else srcT[kv][:, kvh, j - 16, 1:NCB + 1]
                            mm(ps[:, 0:NCB], w1b[:, j, hc * 128:(hc + 1) * 128], rv, st=(j == 0), sp=(j == 31))
                        act(gT[:, hc, 0:NCB], ps[:, 0:NCB], AF.Gelu_apprx_tanh, bias=cst[:, hc:hc + 1])
                    if kv == "k":
                        ps = nps()
                        for hc in range(2):
                            mm(ps[0:64, 0:256], w2b[:, hc, :], gT[:, hc, :], st=(hc == 0), sp=(hc == 1))
                        cp("dve", kcp[0:64, kvh, :], ps[0:64, 0:256])
                    else:
                        for ct in range(2):
                            ps = nps()
                            for hc in range(2):
                                mm(ps[:, 0:64], gT[:, hc, ct * 128:(ct + 1) * 128], w2b[:, hc, :], st=(hc == 0), sp=(hc == 1))
                            cp("dve", vcp[:, ct, kvh, 0:64], ps[:, 0:64])
            fw.barrier()

    def pass1_prompt(self, sw, L):
        fw, NT, T = self.fw, self.NT, self.T
        mm, tr, act, ts, tt, stt, cp, ms, iota, dma, dmas = (self.mm, self.tr, self.act, self.ts, self.tt, self.stt,
                                                             self.cp, self.ms, self.iota, self.dma, self.dmas)
        xp, win_b, wout_b, wqm_b, wkm_b = L["xp"], L["win_b"], L["wout_b"], L["wqm_b"], L["wkm_b"]
        kcp, vcp, ident, identf, nps, npt, psa = L["kcp"], L["vcp"], L["ident"], L["identf"], L["nps"], L["npt"], L["psa"]
        caus_add, win_add, bd01, tri2, csel, e0, mimp, expand = (L["caus_add"], L["win_add"], L["bd01"], L["tri2"],
                                                                 L["csel"], L["e0"], L["mimp"], L["expand"])
        cw, cb, bgate, bif, put_row, x1d = L["cw"], L["cb"], L["bgate"], L["bif"], L["put_row"], L["x1d"]
        p_kv, p_kw, p_vw, p_C, p_n, p_m, p_conv = L["p_kv"], L["p_kw"], L["p_vw"], L["p_C"], L["p_n"], L["p_m"], L["p_conv"]
        with ExitStack() as s:
            ksT = fw.sb(s, [68, 2, T], BF16, "ksT")
            NW = min(8, NT)
            kwT = fw.sb(s, [68, 2, NW * 128], BF16, "kwT")
            phr = fw.sb(s, [1, 128], BF16, "phr")
            vsp = fw.sb(s, [128, NT, 2, 65], BF16, "vsp")
            vwp = fw.sb(s, [128, NW, 2, 65], BF16, "vwp")
            ms("pool", vsp[:], 1.0)
            ms("pool", vwp[:], 1.0)
            with ExitStack() as tmps:
                self.rowt = fw.sb(tmps, [1, 4096], F32, "rowt1")
                self.rowb = fw.sb(tmps, [1, 4096], BF16, "rowb1")
                for kvh in range(2):
                    put_row(ksT[64:65, kvh, :], [[128, NT], [0, 128]], 0, T)
                    put_row(ksT[65:66, kvh, :], [[0, NT], [1, 128]], 0, T)
                    put_row(ksT[66:67, kvh, :], None, 0, T, const=1.0)
                    put_row(ksT[67:68, kvh, :], None, 0, T, const=1.0)
                    put_row(kwT[65:66, kvh, :], [[0, NW], [1, 128]], 0, NW * 128)
                    put_row(kwT[66:67, kvh, :], None, 0, NW * 128, const=1.0)
                    put_row(kwT[67:68, kvh, :], None, 0, NW * 128, const=1.0)
                fw.barrier()
            qps = [fw.sb(s, [68, 2, 4, 128], BF16, f"qp{i}") for i in range(2)]
            srow = fw.sb(s, [1, 8, 128], F32, "srow")
            for h in range(8):
                ms("pool", srow[0:1, h, :], 2.0 ** (-(h + 1)))
            r67 = fw.sb(s, [1, 8, 128], F32, "r67")
            iota(r67[:], [[0, 8], [1, 128]], base=0, cm=0)
            tt("pool", r67[:], r67[:], srow[:], ALU.mult)
            ts("pool", r67[:], r67[:], -1.0, None, op0=ALU.mult)
            srb = fw.sb(s, [1, 8, 128], BF16, "srb")
            r67b = fw.sb(s, [1, 8, 128], BF16, "r67b")
            r66b = fw.sb(s, [1, 8, 128], BF16, "r66b")
            cp("pool", srb[:], srow[:])
            cp("pool", r67b[:], r67[:])
            for qp in qps:
                for kvh in range(2):
                    dma(qp[64:65, kvh], srb[0:1, 4 * kvh:4 * kvh + 4, :])
                    dma(qp[65:66, kvh], srb[0:1, 4 * kvh:4 * kvh + 4, :])
                    dma(qp[67:68, kvh], r67b[0:1, 4 * kvh:4 * kvh + 4, :])
            xts = [fw.sb(s, [128, D], F32, f"x1_{i}") for i in range(2)]
            nb = self.norm_bufs(s, "1")
            hT = fw.sb(s, [128, 8, 128], BF16, "hT1")
            pkv = fw.sb(s, [128, 792], F32, "pkv")
            gt = fw.sb(s, [128, 24], F32, "gt")
            pts = RR([fw.sb(s, [128, 512], BF16, f"pt{i}") for i in range(3)])
            mks = RR([fw.sb(s, [128, 512], BF16, f"mk{i}") for i in range(2)])
            obr = [fw.sb(s, [128, 4, 65], F32, f"obr{i}") for i in range(3)]
            imp4 = fw.sb(s, [128, 4, 64], F32, "imp4")
            imp = fw.sb(s, [128, 64], F32, "imp")
            imp2 = fw.sb(s, [128, 64], F32, "imp2")
            mx1 = fw.sb(s, [128, 8], F32, "mx1")
            mx2 = fw.sb(s, [128, 8], F32, "mx2")
            selm = fw.sb(s, [128, 64], BF16, "selm")
            selT = fw.sb(s, [64, 4, 128], BF16, "selT")
            fpb = fw.sb(s, [128, 3], F32, "fpb")
            ms("pool", fpb[:], -1.0)
            rd = fw.sb(s, [128, 3, 4], F32, "rd")
            sc3 = fw.sb(s, [128, 3, 4], F32, "sc3")
            onsa = fw.sb(s, [128, 4, 64], F32, "onsa")
            otmp = fw.sb(s, [128, 4, 64], F32, "otmp")
            ss4 = fw.sb(s, [128, 4], F32, "ss4")
            rs4 = fw.sb(s, [128, 4], F32, "rs4")
            mixin = fw.sb(s, [128, D], BF16, "mixin")
            mT = fw.sb(s, [128, 8, 128], BF16, "mT")
            x1t = fw.sb(s, [128, D], F32, "x1t")
            gif = fw.sb(s, [128, 8], F32, "gif")
            l1 = fw.sb(s, [128, 4], F32, "l1")
            gsb = fw.sb(s, [128, 12], F32, "gsb")
            wl = fw.sb(s, [128, 4], F32, "wl")
            ul = fw.sb(s, [128, 4], F32, "ul")
            tmp4 = fw.sb(s, [128, 4], F32, "tmp4")
            dec = fw.sb(s, [128, 4], F32, "dec")
            ebt = fw.sb(s, [128, 8], F32, "ebt")
            vmu = fw.sb(s, [128, 4, 129], BF16, "vmu")
            sigo = fw.sb(s, [128, 512], F32, "sigo")
            xcv = [fw.sb(s, [128, 4, 131], F32, f"xcv{i}") for i in range(2)]
            ms("pool", xcv[0][:], 0.0)
            cacc = fw.sb(s, [128, 4, 128], F32, "cacc")
            xc = fw.sb(s, [128, 4, 128], BF16, "xc")
            qmT = fw.sb(s, [128, 4, 128], BF16, "qmT")
            qmS = [fw.sb(s, [128, 4, 128], BF16, f"qmS{i}") for i in range(2)]
            for q_ in qmS:
                ms("pool", q_[:], 0.0)
            kmT = fw.sb(s, [128, 4, 128], BF16, "kmT")
            kmS = [fw.sb(s, [128, 4, 128], BF16, f"kmS{i}") for i in range(2)]
            mqk = fw.sb(s, [128, 4, 128], BF16, "mqk")
            Sf = fw.sb(s, [128, 4, 129], F32, "Sf")
            ms("pool", Sf[:], 0.0)
            Sb = [fw.sb(s, [128, 4, 129], BF16, f"Sb{i}") for i in range(3)]
            ms("pool", Sb[0][:], 0.0)
            dS = fw.sb(s, [128, 4, 129], F32, "dS")
            dn = fw.sb(s, [128, 4], F32, "dn")
            hout = fw.sb(s, [128, 4, 128], F32, "hout")
            hsq = cacc
            m4 = fw.sb(s, [4, 8], F32, "m4")
            R = fw.sb(s, [4, 1], F32, "Rm")
            ms("pool", R[:], 0.0)
            tsb = fw.sb(s, [4, 384], F32, "tsb")
            segs = [(0, 64), (64, 128)]
            KSC = 128.0 ** -0.5

            for t_ in range(NT):
                xt = xts[t_ % 2]
                qp = qps[t_ % 2]
                self.norm_T(xp[t_ * 128:(t_ + 1) * 128, :], xt, nb, hT, ident, npt)
                psA = nps()
                psB = nps()
                for k in range(8):
                    mm(psA[:, 0:512], hT[:, k, :], win_b[:, k, 512:1024], st=(k == 0), sp=(k == 7))
                for k in range(8):
                    mm(psB[:, 0:280], hT[:, k, :], win_b[:, k, 1024:1304], st=(k == 0), sp=(k == 7))
                cp("dve", pkv[:, 0:512], psA[:, 0:512])
                cp("act", pkv[:, 512:792], psB[:, 0:280])
                r0 = t_ * 128
                for i_, n_ in enumerate(("p_kc", "p_vc", "p_ks", "p_vs")):
                    dma(p_kv[n_][r0:r0 + 128, :], pkv[:, i_ * 128:(i_ + 1) * 128])
                if r0 >= T - 512:
                    w0 = r0 - (T - min(512, T))
                    dma(p_kw[w0:w0 + 128, :], pkv[:, 512:640])
                    dma(p_vw[w0:w0 + 128, :], pkv[:, 640:768])
                cp("pool", vsp[:, t_, :, 0:64], pkv[:, 384:512].re("p (h d) -> p h d", h=2))
                ws = t_ % NW
                cp("pool", vwp[:, ws, :, 0:64], pkv[:, 640:768].re("p (h d) -> p h d", h=2))
                tt("dve", gt[:], pkv[:, 768:792], bgate[:], ALU.add)
                act(gt[:], gt[:], AF.Sigmoid)
                psQ0 = nps()
                psQ1 = nps()
                for h in range(8):
                    ps = psQ0 if h < 4 else psQ1
                    for k in range(8):
                        mm(ps[0:64, (h % 4) * 128:(h % 4 + 1) * 128], win_b[:, k, 64 * h:64 * h + 64], hT[:, k, :],
                           st=(k == 0), sp=(k == 7))
                act(qp[0:64, 0].re("p g t -> p (g t)"), psQ0[0:64, :], AF.Copy, scale=0.125)
                act(qp[0:64, 1].re("p g t -> p (g t)"), psQ1[0:64, :], AF.Copy, scale=0.125)
                ts("pool", r66b[:], srow[:], -128.0 * t_, None, op0=ALU.mult)
                for kvh in range(2):
                    dma(qp[66:67, kvh], r66b[0:1, 4 * kvh:4 * kvh + 4, :])
                psK = nps()
                for gi, c0 in enumerate((768, 832, 1024, 1088)):
                    for k in range(8):
                        mm(psK[0:64, gi * 128:(gi + 1) * 128], win_b[:, k, c0:c0 + 64], hT[:, k, :], st=(k == 0), sp=(k == 7))
                cp("dve", ksT[0:64, :, r0:r0 + 128], psK[0:64, 0:256].re("p (h t) -> p h t", h=2))
                cp("dve", kwT[0:64, :, ws * 128:(ws + 1) * 128], psK[0:64, 256:512].re("p (h t) -> p h t", h=2))
                ms("pool", phr[:], 128.0 * t_)
                for kvh in range(2):
                    dma(kwT[64:65, kvh, ws * 128:(ws + 1) * 128], phr[:])

                for kvh in range(2):
                    qv = qp[:, kvh].re("p g t -> p (g t)")
                    accC, impP = psa[0], psa[1]
                    cts = [0] if t_ < 16 else [0, 1]
                    for ci, ct in enumerate(cts):
                        S = nps()
                        Kq = 128 * t_ - 2048 * ct - 31
                        need_mask = Kq < 2032
                        mm(S[:, :], kcp[:, kvh, ct * 128:(ct + 1) * 128], qv, st=True, sp=not need_mask)
                        if need_mask:
                            mk = mks()
                            ts("pool", mk[:], e0[:], float(Kq), NEG, op0=ALU.is_gt, op1=ALU.mult)
                            mm(S[:, :], ident[:], mk[:], st=False, sp=True)
                        pt = pts()
                        act(pt[:], S[:, :], AF.Exp)
                        for g in range(4):
                            mm(accC[:, g * 65:(g + 1) * 65], pt[:, g * 128:(g + 1) * 128], vcp[:, ct, kvh, :],
                               st=(ci == 0), sp=(ci == len(cts) - 1))
                            mm(impP[:, g * 64:(g + 1) * 64], pt[:, g * 128:(g + 1) * 128], mimp[:, ct, :],
                               st=(ci == 0), sp=(ci == len(cts) - 1))
                    cp("dve", obr[0][:].re("p g d -> p (g d)"), accC[:, 0:260])
                    cp("act", imp4[:].re("p g j -> p (g j)"), impP[:, 0:256])
                    ts("dve", rd[:, 0, :], obr[0][:, :, 64], 1e-30, None, op0=ALU.max)
                    self.recip(rd[:, 0, :], rd[:, 0, :])
                    ts("dve", imp[:], imp4[:, 0, :], rd[:, 0, 0:1], None, op0=ALU.mult)
                    for g in range(1, 4):
                        stt("dve", imp[:], imp4[:, g, :], rd[:, 0, g:g + 1], imp[:], ALU.mult, ALU.add)
                    base = 1e9 + 1e6 * (2 * t_)
                    ms("dve", fpb[0:64, 0:1], base - 1e6)
                    ms("dve", fpb[0:64, 1:2], base)
                    ms("dve", fpb[64:128, 1:2], base)
                    ms("dve", fpb[64:128, 2:3], base + 1e6)
                    if t_ == 0:
                        tt("dve", imp[:, 0:2], imp[:, 0:2], fpb[:, 1:3], ALU.max)
                    else:
                        tt("dve", imp[:, 2 * t_ - 1:2 * t_ + 2], imp[:, 2 * t_ - 1:2 * t_ + 2], fpb[:, 0:3], ALU.max)
                        ms("dve", imp[:, 0:1], 3e9)
                    fw.op("dve", lambda e: e.max(out=mx1[:].a, in_=imp[:].a), reads=[imp], writes=[mx1])
                    fw.op("dve", lambda e: e.match_replace(out=imp2[:].a, in_to_replace=mx1[:].a, in_values=imp[:].a,
                                                           imm_value=-1e30), reads=[imp, mx1], writes=[imp2])
                    fw.op("dve", lambda e: e.max(out=mx2[:].a, in_=imp2[:].a), reads=[imp2], writes=[mx2])
                    ts("dve", selm[:], imp[:], mx2[:, 7:8], NEG, op0=ALU.is_lt, op1=ALU.mult)
                    ptr = npt()
                    tr(ptr[0:64, 0:128], selm[:], ident[:])
                    cp("dve", selT[:], ptr[0:64, 0:128].un(1).bc([64, 4, 128]))
                    accS = psa[0]
                    for kt in range(t_ + 1):
                        S = nps()
                        mm(S[:, :], ksT[:, kvh, kt * 128:(kt + 1) * 128], qv, st=True, sp=False)
                        if kt < t_:
                            mm(S[:, :], expand[:, kt * 128:(kt + 1) * 128], selT[:].re("p g t -> p (g t)"), st=False, sp=True)
                        else:
                            mm(S[:, :], ident[:], caus_add[:], st=False, sp=True)
                        pt = pts()
                        act(pt[:], S[:, :], AF.Exp)
                        for g in range(4):
                            mm(accS[:, g * 65:(g + 1) * 65], pt[:, g * 128:(g + 1) * 128], vsp[:, kt, kvh, :],
                               st=(kt == 0), sp=(kt == t_))
                    cp("dve", obr[1][:].re("p g d -> p (g d)"), accS[:, 0:260])
                    accW = psa[1]
                    k0 = max(0, t_ - 4)
                    for kt in range(k0, t_ + 1):
                        S = nps()
                        madd = caus_add if kt == t_ else (win_add if kt == t_ - 4 else None)
                        mm(S[:, :], kwT[:, kvh, (kt % NW) * 128:(kt % NW + 1) * 128], qv, st=True, sp=(madd is None))
                        if madd is not None:
                            mm(S[:, :], ident[:], madd[:], st=False, sp=True)
                        pt = pts()
                        act(pt[:], S[:, :], AF.Exp)
                        for g in range(4):
                            mm(accW[:, g * 65:(g + 1) * 65], pt[:, g * 128:(g + 1) * 128], vwp[:, kt % NW, kvh, :],
                               st=(kt == k0), sp=(kt == t_))
                    cp("act", obr[2][:].re("p g d -> p (g d)"), accW[:, 0:260])
                    for br in (1, 2):
                        ts("dve", rd[:, br, :], obr[br][:, :, 64], 1e-30, None, op0=ALU.max)
                        self.recip(rd[:, br, :], rd[:, br, :])
                    gv = gt[:, 12 * kvh:12 * kvh + 12].re("p (g b) -> p b g", b=3)
                    tt("dve", sc3[:], rd[:], gv, ALU.mult)
                    tt("dve", onsa[:], obr[0][:, :, 0:64], sc3[:, 0, :].un(2).bc([128, 4, 64]), ALU.mult)
                    for br in (1, 2):
                        tt("dve", otmp[:], obr[br][:, :, 0:64], sc3[:, br, :].un(2).bc([128, 4, 64]), ALU.mult)
                        tt("dve", onsa[:], onsa[:], otmp[:], ALU.add)
                    tt("dve", otmp[:], onsa[:], onsa[:], ALU.mult)
                    self.rsum(ss4[:], otmp[:])
                    self.rstd(rs4[:], ss4[:], 1.0 / 64)
                    tt("dve", mixin[:, 256 * kvh:256 * kvh + 256].re("p (g d) -> p g d", g=4), onsa[:],
                       rs4[:].un(2).bc([128, 4, 64]), ALU.mult)

                psV = nps()
                psO = nps()
                psG = nps()
                for k in range(8):
                    mm(psV[:, 0:512], hT[:, k, :], win_b[:, k, 1816:2328], st=(k == 0), sp=(k == 7))
                for k in range(8):
                    mm(psO[:, 0:512], hT[:, k, :], win_b[:, k, 2328:2840], st=(k == 0), sp=(k == 7))
                for k in range(8):
                    mm(psG[:, 0:8], hT[:, k, :], win_b[:, k, 2840:2848], st=(k == 0), sp=(k == 7))
                tt("dve", gif[:], psG[:, 0:8], bif[:], ALU.add)
                act(l1[:], gif[:, 4:8], AF.Exp, scale=-1.0)
                act(l1[:], l1[:], AF.Ln, bias=1.0)
                act(sigo[:], psO[:, 0:512], AF.Sigmoid)
                psC = nps()
                mm(psC[:, 0:4], tri2[:], l1[:])
                mm(psC[:, 4:8], csel[:, 0, :], l1[:])
                mm(psC[:, 8:12], csel[:, 1, :], l1[:])
                cp("dve", gsb[:], psC[:, 0:12])
                act(wl[:], gsb[:, 0:4], AF.Exp, scale=-1.0)
                tt("dve", tmp4[:], gif[:, 0:4], gsb[:, 0:4], ALU.add)
                act(ul[:], tmp4[:], AF.Exp)
                act(ebt[:], gsb[:, 4:12], AF.Exp, scale=-1.0)
                tt("dve", dec[0:64, :], tmp4[0:64, :], gsb[0:64, 4:8], ALU.subtract)
                tt("dve", dec[64:128, :], tmp4[64:128, :], gsb[64:128, 8:12], ALU.subtract)
                tt("dve", vmu[:, :, 0:128], psV[:, 0:512].re("p (h e) -> p h e", h=4), ul[:].un(2).bc([128, 4, 128]), ALU.mult)
                cp("dve", vmu[:, :, 128], ul[:])
                xcur, xnext = xcv[t_ % 2], xcv[(t_ + 1) % 2]
                psX = nps()
                for ch in range(4):
                    for k in range(8):
                        mm(psX[:, ch * 128:(ch + 1) * 128], win_b[:, k, 1304 + ch * 128:1432 + ch * 128], hT[:, k, :],
                           st=(k == 0), sp=(k == 7))
                cp("act", xcur[:, :, 3:131], psX[:, :].re("p (c t) -> p c t", c=4))
                cp("pool", xnext[:, :, 0:3], xcur[:, :, 128:131])
                for ch in range(4):
                    ts("dve", cacc[:, ch, :], xcur[:, ch, 0:128], cw[:, ch, 0:1], cb[:, ch:ch + 1], op0=ALU.mult, op1=ALU.add)
                    for j in range(1, 4):
                        stt("dve", cacc[:, ch, :], xcur[:, ch, j:j + 128], cw[:, ch, j:j + 1], cacc[:, ch, :], ALU.mult, ALU.add)
                act(xc[:], cacc[:], AF.Silu)
                if t_ == NT - 1:
                    for j in range(3):
                        dmas(V(None, p_conv.a[j].rearrange("(c p) -> p c", p=128)), xcur[:, :, 128 + j])
                psq = nps()
                for h in range(4):
                    mm(psq[:, h * 128:(h + 1) * 128], wqm_b[:, h, :], xc[:, h, :])
                cp("act", qmT[:].re("p h t -> p (h t)"), psq[:, :])
                for si, (a_, b_) in enumerate(segs):
                    cp("dve", qmS[si][:, :, a_:b_], psq[:, :].re("p (h t) -> p h t", h=4)[:, :, a_:b_])
                psk = nps()
                for h in range(4):
                    mm(psk[:, h * 128:(h + 1) * 128], wkm_b[:, h, :], xc[:, h, :])
                act(kmT[:].re("p h t -> p (h t)"), psk[:, :], AF.Copy, scale=KSC)
                pskt = nps()
                for h in range(4):
                    mm(pskt[:, h * 128:(h + 1) * 128], xc[:, h, :], wkm_b[:, h, :])
                for si in range(2):
                    ts("dve", kmS[si][:].re("p h t -> p (h t)"), pskt[:, :], csel[:, si, 0:1], KSC, op0=ALU.mult, op1=ALU.mult)
                psqk = nps()
                for h in range(4):
                    mm(psqk[:, h * 128:(h + 1) * 128], kmT[:, h, :], qmT[:, h, :])
                tt("dve", mqk[:], psqk[:, :].re("p (h t) -> p h t", h=4), bd01[:].un(1).bc([128, 4, 128]), ALU.mult)
                sbs = [Sb[(2 * t_) % 3], Sb[(2 * t_ + 1) % 3], Sb[(2 * t_ + 2) % 3]]
                for si in range(2):
                    pd = [nps(), nps()]
                    for h in range(4):
                        mm(pd[h // 2][:, (h % 2) * 129:(h % 2) * 129 + 129], kmS[si][:, h, :], vmu[:, h, :])
                    eb = ebt[:, 4 * si:4 * si + 4].un(2).bc([128, 4, 129])
                    tt("dve", Sf[:], Sf[:], eb, ALU.mult)
                    for hh in range(2):
                        tt("dve", dS[:, 2 * hh:2 * hh + 2, :], pd[hh][:, 0:258].re("p (h e) -> p h e", h=2),
                           ebt[:, 4 * si + 2 * hh:4 * si + 2 * hh + 2].un(2).bc([128, 2, 129]), ALU.mult)
                    tt("dve", Sf[:], Sf[:], dS[:], ALU.add)
                    cp("pool", sbs[si + 1][:], Sf[:])
                pa = [nps(), nps()]
                for h in range(4):
                    o_ = pa[h // 2][:, (h % 2) * 129:(h % 2) * 129 + 129]
                    mm(o_, mqk[:, h, :], vmu[:, h, :], st=True, sp=False)
                    mm(o_, qmS[0][:, h, :], sbs[0][:, h, :], st=False, sp=False)
                    mm(o_, qmS[1][:, h, :], sbs[1][:, h, :], st=False, sp=True)
                for hh in range(2):
                    av = pa[hh][:, 0:258].re("p (h e) -> p h e", h=2)
                    tt("dve", dn[:, 2 * hh:2 * hh + 2], av[:, :, 128], wl[:, 2 * hh:2 * hh + 2], ALU.mult)
                stt("dve", tmp4[:], dn[:], -1.0, dn[:], ALU.mult, ALU.max)
                ts("dve", tmp4[:], tmp4[:], 1.0, None, op0=ALU.max)
                self.recip(tmp4[:], tmp4[:])
                tt("dve", tmp4[:], tmp4[:], wl[:], ALU.mult)
                for hh in range(2):
                    av = pa[hh][:, 0:258].re("p (h e) -> p h e", h=2)
                    tt("dve", hout[:, 2 * hh:2 * hh + 2, :], av[:, :, 0:128],
                       tmp4[:, 2 * hh:2 * hh + 2].un(2).bc([128, 2, 128]), ALU.mult)
                tt("pool", hsq[:], hout[:], hout[:], ALU.mult)
                self.rsum(ss4[:], hsq[:])
                self.rstd(rs4[:], ss4[:], 1.0 / 128)
                tt("dve", hout[:], hout[:], rs4[:].un(2).bc([128, 4, 128]), ALU.mult)
                tt("dve", mixin[:, 512:1024], hout[:].re("p h e -> p (h e)"), sigo[:], ALU.mult)
                psT = nps()
                tr(psT[0:4, 0:128], dec[:], identf[:])
                tr(psT[0:4, 128:256], gsb[:, 4:8], identf[:])
                tr(psT[0:4, 256:384], gsb[:, 8:12], identf[:])
                cp("dve", tsb[:], psT[0:4, 0:384])
                self.rmax(m4[:, 0:1], tsb[:, 0:64])
                self.rmax(m4[:, 1:2], tsb[:, 64:128])
                stt("dve", R[:], R[:], tsb[:, 128:129], m4[:, 0:1], ALU.subtract, ALU.max)
                stt("dve", R[:], R[:], tsb[:, 256:257], m4[:, 1:2], ALU.subtract, ALU.max)
                ptm = npt()
                for k in range(8):
                    tr(ptm[:, k * 128:(k + 1) * 128], mixin[:, k * 128:(k + 1) * 128], ident[:])
                cp("act", mT[:].re("p k t -> p (k t)"), ptm[:])
                for g in range(2):
                    ps = nps()
                    for k in range(8):
                        mm(ps[:, :], mT[:, k, :], wout_b[:, k, g * 512:(g + 1) * 512], st=(k == 0), sp=(k == 7))
                    tt("dve", x1t[:, g * 512:(g + 1) * 512], xt[:, g * 512:(g + 1) * 512], ps[:, :], ALU.add)
                dma(x1d[r0:r0 + 128, :], x1t[:])

            dma(V(None, p_m.a.rearrange("(h o) -> h o", o=1)), R[:])
            ones4 = fw.sb(s, [4, 128], F32, "ones4")
            ms("pool", ones4[:], 1.0)
            ts("dve", ones4[:], ones4[:], R[:, 0:1], None, op0=ALU.mult)
            ps = nps()
            mm(ps[:, 0:4], ones4[:], identf[0:4, 0:4])
            act(tmp4[:], ps[:, 0:4], AF.Exp, scale=-1.0)
            tt("dve", Sf[:], Sf[:], tmp4[:].un(2).bc([128, 4, 129]), ALU.mult)
            dmas(V(None, p_n.a.rearrange("h d -> d h")), Sf[:, :, 128])
            for h in range(4):
                ps = nps()
                tr(ps[:, 0:128], Sf[:, h, 0:128], identf[:])
                cp("dve", hsq[:, h, :], ps[:, 0:128])
                dma(p_C[h], hsq[:, h, :])
            fw.barrier()

    def load_w(self, dst, src, rows_chunks, ncols, stg, gcol=None, g0=0):
        for k in range(rows_chunks):
            st = stg[k % 2]
            self.dma(st[:, 0:ncols], src[k * 128:(k + 1) * 128, :], q="sp" if k % 2 == 0 else "act")
            eng = "dve" if k % 2 == 0 else "pool"
            if gcol is None:
                self.cp(eng, dst[:, k, :], st[:, 0:ncols])
            else:
                self.ts(eng, dst[:, k, :], st[:, 0:ncols], gcol[:, g0 + k:g0 + k + 1], None, op0=ALU.mult)

    def pass2(self, top, L):
        fw, NT, T = self.fw, self.NT, self.T
        mm, tr, act, ts, tt, stt, cp, ms, dma, dmas = (self.mm, self.tr, self.act, self.ts, self.tt, self.stt,
                                                       self.cp, self.ms, self.dma, self.dmas)
        ident, nps, npt, psa = L["ident"], L["nps"], L["npt"], L["psa"]
        x1d, x2d, y_p = L["x1d"], L["x2d"], L["y_p"]
        g2 = fw.sb(top, [128, 32], F32, "g2")
        for i_, g_ in enumerate((L["g_xa"], L["g_mem"], L["g_ffn"])):
            dmas(g2[:, 8 * i_:8 * i_ + 8], V(None, g_.a.rearrange("(k p) -> p k", p=128)))
        with ExitStack() as s:
            wxq_b = fw.sb(s, [128, 8, D], BF16, "wxq_b")
            wxo_b = fw.sb(s, [128, 8, D], BF16, "wxo_b")
            mkT = fw.sb(s, [128, 8, 256], BF16, "mkT")
            mvp = fw.sb(s, [128, 2, 4, 257], BF16, "mvp")
            ms("pool", mvp[:], 1.0)
            xts = [fw.sb(s, [128, D], F32, f"x2_{i}") for i in range(2)]
            nb = self.norm_bufs(s, "2")
            hT = fw.sb(s, [128, 8, 128], BF16, "hT2")
            with ExitStack() as s2:
                stg = [fw.sb(s2, [128, D], F32, f"stg2{i}") for i in range(2)]
                wxk_b = fw.sb(s2, [128, 8, D], BF16, "wxk_b")
                wxv_b = fw.sb(s2, [128, 8, D], BF16, "wxv_b")
                self.load_w(wxq_b, L["w_xq"], 8, D, stg, g2, 0)
                self.load_w(wxo_b, L["w_xo"], 8, D, stg)
                self.load_w(wxk_b, L["w_xk"], 8, D, stg, g2, 8)
                self.load_w(wxv_b, L["w_xv"], 8, D, stg, g2, 8)
                mo = fw.sb(s2, [128, D], F32, "mo")
                for mt in range(2):
                    xt = xts[mt % 2]
                    self.norm_T(L["memp"][mt * 128:(mt + 1) * 128, :], xt, nb, hT, ident, npt)
                    for wi, (wb_, po) in enumerate(((wxk_b, L["p_mk"]), (wxv_b, L["p_mv"]))):
                        for g in range(2):
                            ps = nps()
                            for k in range(8):
                                mm(ps[:, :], hT[:, k, :], wb_[:, k, g * 512:(g + 1) * 512], st=(k == 0), sp=(k == 7))
                            cp("dve" if g == 0 else "act", mo[:, g * 512:(g + 1) * 512], ps[:, :])
                        dma(po[mt * 128:(mt + 1) * 128, :], mo[:])
                        if wi == 1:
                            cp("pool", mvp[:, mt, :, 0:256], mo[:].re("p (h d) -> p h d", h=4))
                    for c4 in range(2):
                        ps = nps()
                        for cc in range(4):
                            c = c4 * 4 + cc
                            for k in range(8):
                                mm(ps[:, cc * 128:(cc + 1) * 128], wxk_b[:, k, c * 128:(c + 1) * 128], hT[:, k, :],
                                   st=(k == 0), sp=(k == 7))
                        cp("dve", mkT[:, c4 * 4:c4 * 4 + 4, mt * 128:(mt + 1) * 128], ps[:, :].re("p (c t) -> p c t", c=4))
                fw.barrier()
            qxT = fw.sb(s, [128, 8, 128], BF16, "qxT")
            pts = [fw.sb(s, [128, 512], BF16, f"pxt{i}") for i in range(2)]
            ox = fw.sb(s, [128, D], BF16, "ox")
            oxT = fw.sb(s, [128, 8, 128], BF16, "oxT")
            rdx = fw.sb(s, [128, 1], F32, "rdx")
            x2t = fw.sb(s, [128, D], F32, "x2t")
            for t_ in range(NT):
                xt = xts[t_ % 2]
                r0 = t_ * 128
                self.norm_T(x1d[r0:r0 + 128, :], xt, nb, hT, ident, npt)
                for c4 in range(2):
                    ps = nps()
                    for cc in range(4):
                        c = c4 * 4 + cc
                        for k in range(8):
                            mm(ps[:, cc * 128:(cc + 1) * 128], wxq_b[:, k, c * 128:(c + 1) * 128], hT[:, k, :],
                               st=(k == 0), sp=(k == 7))
                    act(qxT[:, c4 * 4:c4 * 4 + 4, :].re("p c t -> p (c t)"), ps[:, :], AF.Copy, scale=1.0 / 16)
                for mt in range(2):
                    S = nps()
                    for h in range(4):
                        for hf in range(2):
                            mm(S[:, h * 128:(h + 1) * 128], mkT[:, 2 * h + hf, mt * 128:(mt + 1) * 128], qxT[:, 2 * h + hf, :],
                               st=(hf == 0), sp=(hf == 1))
                    act(pts[mt][:], S[:, :], AF.Exp)
                for h in range(4):
                    acc = psa[h % 2]
                    for mt in range(2):
                        mm(acc[:, 0:257], pts[mt][:, h * 128:(h + 1) * 128], mvp[:, mt, h, :], st=(mt == 0), sp=(mt == 1))
                    self.recip(rdx[:], acc[:, 256:257])
                    ts("dve", ox[:, h * 256:(h + 1) * 256], acc[:, 0:256], rdx[:, 0:1], None, op0=ALU.mult)
                pto = npt()
                for k in range(8):
                    tr(pto[:, k * 128:(k + 1) * 128], ox[:, k * 128:(k + 1) * 128], ident[:])
                cp("act", oxT[:].re("p k t -> p (k t)"), pto[:])
                for g in range(2):
                    ps = nps()
                    for k in range(8):
                        mm(ps[:, :], oxT[:, k, :], wxo_b[:, k, g * 512:(g + 1) * 512], st=(k == 0), sp=(k == 7))
                    tt("dve", x2t[:, g * 512:(g + 1) * 512], xt[:, g * 512:(g + 1) * 512], ps[:, :], ALU.add)
                dma(x2d[r0:r0 + 128, :], x2t[:])
            if self.sample:
                S = self.S
                xt = xts[0]
                self.norm_T(S["x1s"][:, :], xt, nb, hT, ident, npt, rows=16)
                qxs = fw.sb(s, [128, 8, 16], BF16, "qxs")
                ps = nps()
                for c in range(8):
                    for k in range(8):
                        mm(ps[:, c * 16:(c + 1) * 16], wxq_b[:, k, c * 128:(c + 1) * 128], hT[:, k, 0:16], st=(k == 0), sp=(k == 7))
                act(qxs[:].re("p c t -> p (c t)"), ps[:, 0:128], AF.Copy, scale=1.0 / 16)
                ms("pool", ox[:], 0.0)
                msg = fw.sb(s, [128, 2, D], F32, "msg")
                mkb = fw.sb(s, [128, 2, D], BF16, "mkb")
                ptx = [fw.sb(s, [128, 16], BF16, f"ptx{i}") for i in range(2)]
                oxb = fw.sb(s, [4, D], BF16, "oxb")
                for b in range(4):
                    dma(msg[:], V(None, S["cmk"].a[b].rearrange("(t p) f -> p t f", p=128)))
                    cp("dve", mkb[:, 0, :], msg[:, 0, :])
                    cp("pool", mkb[:, 1, :], msg[:, 1, :])
                    for mt in range(2):
                        pt = npt()
                        for c in range(8):
                            tr(pt[:, c * 128:(c + 1) * 128], mkb[:, mt, c * 128:(c + 1) * 128], ident[:])
                        cp("act", mkT[:, :, mt * 128:(mt + 1) * 128], pt[:, :].re("p (c t) -> p c t", c=8))
                    dma(msg[:], V(None, S["cmv"].a[b].rearrange("(t p) f -> p t f", p=128)))
                    for mt in range(2):
                        cp("dve" if mt == 0 else "pool", mvp[:, mt, :, 0:256], msg[:, mt, :].re("p (h d) -> p h d", h=4))
                    for mt in range(2):
                        Sx = nps()
                        for h in range(4):
                            for hf in range(2):
                                mm(Sx[:, h * 4:(h + 1) * 4], mkT[:, 2 * h + hf, mt * 128:(mt + 1) * 128],
                                   qxs[:, 2 * h + hf, 4 * b:4 * b + 4], st=(hf == 0), sp=(hf == 1))
                        act(ptx[mt][:], Sx[:, 0:16], AF.Exp)
                    for h in range(4):
                        acc = psa[h % 2]
                        for mt in range(2):
                            mm(acc[0:4, 0:257], ptx[mt][:, 4 * h:4 * h + 4], mvp[:, mt, h, :], st=(mt == 0), sp=(mt == 1))
                        self.recip(rdx[0:4, :], acc[0:4, 256:257])
                        ts("dve", oxb[:, h * 256:(h + 1) * 256], acc[0:4, 0:256], rdx[0:4, 0:1], None, op0=ALU.mult)
                    dma(ox[4 * b:4 * b + 4, :], oxb[:])
                pto = npt()
                for k in range(8):
                    tr(pto[:, k * 128:(k + 1) * 128], ox[:, k * 128:(k + 1) * 128], ident[:])
                cp("act", oxT[:].re("p k t -> p (k t)"), pto[:])
                for g in range(2):
                    ps = nps()
                    for k in range(8):
                        mm(ps[:, :], oxT[:, k, :], wxo_b[:, k, g * 512:(g + 1) * 512], st=(k == 0), sp=(k == 7))
                    tt("dve", x2t[:, g * 512:(g + 1) * 512], xt[:, g * 512:(g + 1) * 512], ps[:, :], ALU.add)
                dma(S["x2s"][:, :], x2t[0:16, :])
            fw.barrier()
        with ExitStack() as s:
            wg_b = fw.sb(s, [128, 8, DFF], BF16, "wg_b")
            wu_b = fw.sb(s, [128, 8, DFF], BF16, "wu_b")
            wd_b = fw.sb(s, [128, 22, D], BF16, "wd_b")
            gfin = fw.sb(s, [128, D], F32, "gfin")
            dma(gfin[:], V(None, L["g_final"].a.partition_broadcast(128)))
            with ExitStack() as s2:
                stg = [fw.sb(s2, [128, DFF], F32, f"stg3{i}") for i in range(2)]
                self.load_w(wg_b, L["w_gate"], 8, DFF, stg, g2, 16)
                self.load_w(wu_b, L["w_up"], 8, DFF, stg, g2, 16)
                self.load_w(wd_b, L["w_down"], 22, D, stg)
                fw.barrier()
            xts = [fw.sb(s, [128, D], F32, f"x3_{i}") for i in range(2)]
            nb = self.norm_bufs(s, "3")
            hT = fw.sb(s, [128, 8, 128], BF16, "hT3")
            aT = fw.sb(s, [128, 22, 128], BF16, "aT")
            sg = fw.sb(s, [128, 512], F32, "sg")
            x3t = fw.sb(s, [128, D], F32, "x3t")
            tiles = [(x2d[t_ * 128:(t_ + 1) * 128, :], y_p[t_ * 128:(t_ + 1) * 128, :], 128) for t_ in range(NT)]
            if self.sample:
                tiles.append((self.S["x2s"][:, :], self.S["y_s"], 16))
            for t_, (src_, dst_, rows_) in enumerate(tiles):
                xt = xts[t_ % 2]
                self.norm_T(src_, xt, nb, hT, ident, npt, rows=rows_)
                for c0 in range(0, 22, 4):
                    n = min(4, 22 - c0)
                    pg = nps()
                    pu = nps()
                    for cc in range(n):
                        c = c0 + cc
                        for k in range(8):
                            mm(pg[:, cc * 128:(cc + 1) * 128], wg_b[:, k, c * 128:(c + 1) * 128], hT[:, k, :], st=(k == 0), sp=(k == 7))
                        for k in range(8):
                            mm(pu[:, cc * 128:(cc + 1) * 128], wu_b[:, k, c * 128:(c + 1) * 128], hT[:, k, :], st=(k == 0), sp=(k == 7))
                    act(sg[:, 0:n * 128], pg[:, 0:n * 128], AF.Silu)
                    tt("dve", aT[:, c0:c0 + n, :].re("p c t -> p (c t)"), sg[:, 0:n * 128], pu[:, 0:n * 128], ALU.mult)
                for g in range(2):
                    ps = nps()
                    for c in range(22):
                        mm(ps[:, :], aT[:, c, :], wd_b[:, c, g * 512:(g + 1) * 512], st=(c == 0), sp=(c == 21))
                    tt("dve", x3t[:, g * 512:(g + 1) * 512], xt[:, g * 512:(g + 1) * 512], ps[:, :], ALU.add)
                ms("dve", nb["ss"][:], 0.0)
                act(nb["junk"][:], x3t[:], AF.Square, acc=nb["ss"][:])
                self.rstd(nb["rs"][:], nb["ss"][:], 1.0 / D)
                stt("dve", x3t[:], x3t[:], nb["rs"][:, 0:1], gfin[:], ALU.mult, ALU.mult)
                dma(dst_, x3t[0:rows_, :])
            fw.barrier()


def sample_io(self):
    din, dout = self.din, self.dout
    S = {"xs": din("xs", [16, D]), "ptab": din("ptab", [4, 128], I32)}
    for n in ("pool_kc", "pool_vc", "pool_ks", "pool_vs"):
        S[n] = din(n, [5120, 16384])
    S["stk"] = din("stk", [4, 512, 128])
    S["stv"] = din("stv", [4, 512, 128])
    S["sconv"] = din("sconv", [4, 3, 512])
    S["sC"] = din("sC", [4, 4, 128, 128])
    S["sn"] = din("sn", [4, 4, 128])
    S["sm"] = din("sm", [16])
    S["cmk"] = din("cmk", [4, 256, D])
    S["cmv"] = din("cmv", [4, 256, D])
    S["y_s"] = dout("y_s", [16, D])
    for n in ("s_kc", "s_vc", "s_ks", "s_vs"):
        S[n] = dout(n, [16, 128])
    S["s_kw"] = dout("s_kw", [4, 512, 128])
    S["s_vw"] = dout("s_vw", [4, 512, 128])
    S["s_C"] = dout("s_C", [4, 4, 128, 128])
    S["s_n"] = dout("s_n", [4, 4, 128])
    S["s_m"] = dout("s_m", [16])
    S["s_conv"] = dout("s_conv", [4, 3, 512])
    S["kcS_d"] = self.dscr("kcS_d", [4, 2, 64, 1024], BF16)
    S["vcS_d"] = self.dscr("vcS_d", [4, 128, 8, 2, 64], BF16)
    S["x1s"] = self.dscr("x1s", [16, D])
    S["x2s"] = self.dscr("x2s", [16, D])
    self.S = S
    return S


def load_cmp(self, s, kv, cmp_in, nps):
    fw = self.fw
    mm, tt, cp, dma, dmas = self.mm, self.tt, self.cp, self.dma, self.dmas
    pe, w1, b1, w2 = cmp_in[kv]
    w1b = fw.sb(s, [64, 32, 256], BF16, "Sw1b" + kv)
    with ExitStack() as t:
        w1s = [fw.sb(t, [64, 8, 256], F32, f"Sw1s{kv}{i}") for i in range(2)]
        for jb in range(4):
            st = w1s[jb % 2]
            dma(st[:], V(None, w1.a.rearrange("(j d) n -> d j n", d=64)[:, jb * 8:(jb + 1) * 8, :]))
            cp("pool", w1b[:, jb * 8:(jb + 1) * 8, :], st[:])
        fw.barrier()
    peT = fw.sb(s, [64, 32], F32, "SpeT" + kv)
    dmas(peT[:], V(None, pe.a.rearrange("j d -> d j")))
    peTb = fw.sb(s, [64, 32], BF16, "SpeTb" + kv)
    cp("dve", peTb[:], peT[:])
    b1c = fw.sb(s, [128, 2], F32, "Sb1c" + kv)
    dmas(b1c[:], V(None, b1.a.rearrange("(c p) -> p c", p=128)))
    w2s = fw.sb(s, [128, 2, 64], F32, "Sw2s" + kv)
    dma(w2s[:], V(None, w2.a.rearrange("(c p) n -> p c n", p=128)))
    w2b = fw.sb(s, [128, 2, 64], BF16, "Sw2b" + kv)
    cp("dve", w2b[:], w2s[:])
    cst = fw.sb(s, [128, 2], F32, "Scst" + kv)
    for hc in range(2):
        ps = nps()
        for j in range(32):
            mm(ps[:, 0:1], w1b[:, j, hc * 128:(hc + 1) * 128], peTb[:, j:j + 1], st=(j == 0), sp=(j == 31))
        tt("dve", cst[:, hc:hc + 1], ps[:, 0:1], b1c[:, hc:hc + 1], ALU.add)
    return w1b, w2b, cst


def gather(self, dst, pool, idx, r0):
    self.fw.dma("pool", dst, pool, extra_reads=[idx.b],
                fn=lambda e: e.indirect_dma_start(out=dst.a, out_offset=None, in_=pool.a,
                                                  in_offset=bass.IndirectOffsetOnAxis(ap=idx.a, axis=0),
                                                  element_offset=r0 * 128))


def sample_s0(self, L):
    fw, S = self.fw, self.S
    mm, tr, act, cp, ms, dma = self.mm, self.tr, self.act, self.cp, self.ms, self.dma
    ident, nps, npt, cmp_in = L["ident"], L["nps"], L["npt"], L["cmp_in"]
    with ExitStack() as s:
        idx = [fw.sb(s, [128, 1], I32, f"S0idx{b}") for b in range(4)]
        for b in range(4):
            dma(idx[b][:], V(None, S["ptab"].a[b].rearrange("(p o) -> p o", o=1)))
        cw_ = {kv: load_cmp(self, s, kv, cmp_in, nps) for kv in "kv"}
        srcS = fw.sb(s, [64, 2, 128, 129], BF16, "srcS")
        ms("pool", srcS[:, :, :, 128:129], 0.0)
        gch = fw.sb(s, [128, 4096], F32, "gch")
        gbf = fw.sb(s, [128, 32, 128], BF16, "gbf")
        gT = fw.sb(s, [128, 2, 1024], BF16, "SgT")
        ko = fw.sb(s, [64, 1024], BF16, "Sko")
        vo = fw.sb(s, [128, 8, 64], BF16, "Svo")
        for b in range(4):
            for kv in "kv":
                w1b, w2b, cst = cw_[kv]
                pool = S["pool_" + kv + "c"]
                for i in range(4):
                    gather(self, gch[:], pool, idx[b][:, :], 32 * i)
                    cp("dve", gbf[:, 0:16, :].re("p r f -> p (r f)"), gch[:, 0:2048])
                    cp("pool", gbf[:, 16:32, :].re("p r f -> p (r f)"), gch[:, 2048:4096])
                    for kvh in range(2):
                        for g8 in range(4):
                            pt = npt()
                            for r8 in range(8):
                                tr(pt[0:64, r8 * 128:(r8 + 1) * 128], gbf[:, 8 * g8 + r8, kvh * 64:(kvh + 1) * 64], ident[:])
                            cp("act" if g8 % 2 == 0 else "dve", srcS[:, kvh, 32 * i + 8 * g8:32 * i + 8 * g8 + 8, 0:128],
                               pt[0:64, :].re("p (r t) -> p r t", r=8))
                for kvh in range(2):
                    for hc in range(2):
                        wv = lambda j: w1b[:, j, hc * 128:(hc + 1) * 128]
                        for bank in range(2):
                            ps = nps()
                            o4 = ps[:, :].re("p (a t) -> p a t", a=4)
                            for j in range(32):
                                if j < 16:
                                    r0 = 64 * bank + j
                                    mm(o4, wv(j), srcS[:, kvh, r0:r0 + 49:16, 0:128], st=(j == 0), sp=False)
                                elif bank == 0:
                                    r0 = 16 + (j - 16)
                                    mm(o4, wv(j), srcS[:, kvh, r0:r0 + 49:16, 0:128], st=False, sp=(j == 31))
                                else:
                                    r0 = 80 + (j - 16)
                                    mm(o4[:, 0:3, :], wv(j), srcS[:, kvh, r0:r0 + 33:16, 0:128], st=False, sp=False)
                                    mm(ps[:, 384:512], wv(j), srcS[:, kvh, j - 16, 1:129], st=False, sp=(j == 31))
                            act(gT[:, hc, bank * 512:(bank + 1) * 512], ps[:, :], AF.Gelu_apprx_tanh, bias=cst[:, hc:hc + 1])
                    if kv == "k":
                        for bank in range(2):
                            ps = nps()
                            for hc in range(2):
                                mm(ps[0:64, :], w2b[:, hc, :], gT[:, hc, bank * 512:(bank + 1) * 512], st=(hc == 0), sp=(hc == 1))
                            cp("dve", ko[:, bank * 512:(bank + 1) * 512], ps[0:64, :])
                        dma(S["kcS_d"][b, kvh], ko[:])
                    else:
                        ps = nps()
                        for rb in range(8):
                            for hc in range(2):
                                mm(ps[:, rb * 64:(rb + 1) * 64], gT[:, hc, rb * 128:(rb + 1) * 128], w2b[:, hc, :],
                                   st=(hc == 0), sp=(hc == 1))
                        cp("dve", vo[:].re("p r d -> p (r d)"), ps[:, :])
                        dma(S["vcS_d"][b][:, :, kvh, :], vo[:])
        fw.barrier()


Builder.sample_io = sample_io

def sample_pass1(self, L):
    fw, S = self.fw, self.S
    mm, tr, act, ts, tt, stt, cp, ms, iota, dma, dmas = (self.mm, self.tr, self.act, self.ts, self.tt, self.stt,
                                                         self.cp, self.ms, self.iota, self.dma, self.dmas)
    win_b, wout_b, wqm_b, wkm_b = L["win_b"], L["wout_b"], L["wqm_b"], L["wkm_b"]
    ident, identf, nps, npt, psa = L["ident"], L["identf"], L["nps"], L["npt"], L["psa"]
    cw, cb, bgate, bif, tmpf = L["cw"], L["cb"], L["bgate"], L["bif"], L["tmpf"]
    put_row = self.put_row
    KSC = 128.0 ** -0.5
    with ExitStack() as s:
        xt = fw.sb(s, [128, D], F32, "xS")
        nb = self.norm_bufs(s, "S")
        hT = fw.sb(s, [128, 8, 128], BF16, "hTS")
        pkv = fw.sb(s, [128, 792], F32, "pkvS")
        gt = fw.sb(s, [128, 24], F32, "gtS")
        vnS = fw.sb(s, [16, 2, 2, 65], BF16, "vnS")
        qTs = fw.sb(s, [64, 8, 16], BF16, "qTs")
        mixin = fw.sb(s, [128, D], BF16, "mixinS")
        s2 = ExitStack()
        QS = fw.sb(s2, [68, 4, 2, 16], BF16, "QS")
        kcSb = fw.sb(s2, [68, 2, 8, 128], BF16, "kcSb")
        vcSb = fw.sb(s2, [128, 8, 2, 65], BF16, "vcSb")
        ksS = fw.sb(s2, [68, 2, 32, 128], BF16, "ksS")
        kwS = fw.sb(s2, [68, 2, 4, 128], BF16, "kwS")
        KnS = fw.sb(s2, [68, 2, 2, 16], BF16, "KnS")
        ms("pool", vcSb[:], 1.0)
        with ExitStack() as tmps:
            self.rowt = fw.sb(tmps, [1, 4096], F32, "rowtS")
            self.rowb = fw.sb(tmps, [1, 4096], BF16, "rowbS")
            sr = fw.sb(tmps, [1, 2, 4, 4], F32, "srS")
            for h in range(8):
                ms("pool", sr[0:1, h // 4, h % 4, :], 2.0 ** (-(h + 1)))
            qi = fw.sb(tmps, [1, 2, 4, 4], F32, "qiS")
            iota(qi[:].re("p k g q -> p (k g) q"), [[0, 8], [1, 4]], base=0, cm=0)
            rw = fw.sb(tmps, [1, 4, 2, 16], F32, "rwS")
            rwb = fw.sb(tmps, [1, 4, 2, 16], BF16, "rwbS")
            srv = sr[:].re("p k g q -> p k (g q)")
            for row in range(4):
                for i in range(4):
                    if row == 0:
                        ts("pool", rw[0:1, i], srv, -1.0, None, op0=ALU.mult)
                    elif row == 1:
                        cp("pool", rw[0:1, i], srv)
                    elif row == 2:
                        tt("pool", rw[0:1, i], srv, qi[:].re("p k g q -> p k (g q)"), ALU.mult)
                        ts("pool", rw[0:1, i], rw[0:1, i], -1.0, None, op0=ALU.mult)
                    else:
                        ts("pool", rw[0:1, i], srv, 32.0 * i, None, op0=ALU.mult)
                cp("pool", rwb[:], rw[:])
                dma(QS[64 + row:65 + row], rwb[:])
            for kvh in range(2):
                put_row(kcSb[64:65, kvh].re("p r t -> p (r t)"), [[0, 8], [-128, 128]], 16384, 1024)
                put_row(kcSb[65:66, kvh].re("p r t -> p (r t)"), [[16, 8], [0, 128]], 31, 1024)
                put_row(kcSb[66:67, kvh].re("p r t -> p (r t)"), None, 0, 1024, const=1.0)
                put_row(kcSb[67:68, kvh].re("p r t -> p (r t)"), None, 0, 1024, const=0.0)
                put_row(ksS[64:65, kvh].re("p r t -> p (r t)"), [[0, 32], [-128, 128]], 16384, 4096)
                put_row(ksS[65:66, kvh].re("p r t -> p (r t)"), [[1, 32], [0, 128]], 0, 4096)
                put_row(ksS[66:67, kvh].re("p r t -> p (r t)"), None, 0, 4096, const=1.0)
                put_row(ksS[67:68, kvh].re("p r t -> p (r t)"), None, 0, 4096, const=1.0)
                put_row(kwS[64:65, kvh].re("p r t -> p (r t)"), [[-128, 4], [0, 128]], 512, 512)
                put_row(kwS[65:66, kvh].re("p r t -> p (r t)"), [[0, 4], [1, 128]], 0, 512)
                put_row(kwS[66:67, kvh].re("p r t -> p (r t)"), None, 0, 512, const=1.0)
                put_row(kwS[67:68, kvh].re("p r t -> p (r t)"), None, 0, 512, const=0.0)
                for sw_ in range(2):
                    put_row(KnS[64:65, sw_, kvh], None, 0, 16, const=0.0)
                    put_row(KnS[65:66, sw_, kvh], [[0, 4], [1, 4]], 0, 16)
                    put_row(KnS[66:67, sw_, kvh], None, 0, 16, const=1.0)
                    put_row(KnS[67:68, sw_, kvh], None, 0, 16, const=0.0)
            fw.barrier()
        maskC7 = fw.sb(s2, [128, 1], F32, "maskC7")
        iota(tmpf[:, 0:1], [[0, 1]], base=0, cm=1)
        ts("pool", maskC7[:], tmpf[:, 0:1], 127.0, None, op0=ALU.is_lt)
        winm0 = fw.sb(s2, [128, 4], BF16, "winm0")
        iota(tmpf[:, 0:4], [[-1, 4]], base=0, cm=1)
        ts("pool", winm0[:], tmpf[:, 0:4], 0.0, None, op0=ALU.is_gt)
        newm = fw.sb(s2, [16, 4, 4], BF16, "newm")
        for b in range(4):
            iota(tmpf[0:16, 0:4], [[-1, 4]], base=-4 * b, cm=1)
            ts("pool", tmpf[0:16, 4:8], tmpf[0:16, 0:4], 0.0, None, op0=ALU.is_le)
            iota(tmpf[0:16, 8:12], [[0, 4]], base=-4 * b, cm=1)
            ts("pool", tmpf[0:16, 8:12], tmpf[0:16, 8:12], 0.0, None, op0=ALU.is_ge)
            tt("pool", newm[:, b, :], tmpf[0:16, 4:8], tmpf[0:16, 8:12], ALU.mult)
        mimpS = fw.sb(s2, [128, 8, 256], BF16, "mimpS")
        mtmp = fw.sb(s2, [128, 3, 256], F32, "mtmp")
        for rb in range(8):
            iota(mtmp[:, 0, :], [[-4, 256]], base=rb - 1, cm=8)
            stt("dve", mtmp[:, 1, :], mtmp[:, 0, :], -1.0, mtmp[:, 0, :], ALU.mult, ALU.max)
            ts("pool", mtmp[:, 0, :], mtmp[:, 1, :], 2.0, 0.5, op0=ALU.is_le, op1=ALU.mult)
            ts("pool", mtmp[:, 2, :], mtmp[:, 1, :], 1.0, 0.5, op0=ALU.is_le, op1=ALU.mult)
            tt("pool", mimpS[:, rb, :], mtmp[:, 0, :], mtmp[:, 2, :], ALU.add)
        idx = [fw.sb(s2, [128, 1], I32, f"S1idx{b}") for b in range(4)]
        for b in range(4):
            dma(idx[b][:], V(None, S["ptab"].a[b].rearrange("(p o) -> p o", o=1)))

        self.norm_T(S["xs"], xt, nb, hT, ident, npt, rows=16)
        psA, psB = nps(), nps()
        for k in range(8):
            mm(psA[:, 0:512], hT[:, k, :], win_b[:, k, 512:1024], st=(k == 0), sp=(k == 7))
        for k in range(8):
            mm(psB[:, 0:280], hT[:, k, :], win_b[:, k, 1024:1304], st=(k == 0), sp=(k == 7))
        cp("dve", pkv[:, 0:512], psA[:, 0:512])
        cp("act", pkv[:, 512:792], psB[:, 0:280])
        for i_, n_ in enumerate(("s_kc", "s_vc", "s_ks", "s_vs")):
            dma(S[n_], pkv[0:16, i_ * 128:(i_ + 1) * 128])
        tt("dve", gt[:], pkv[:, 768:792], bgate[:], ALU.add)
        act(gt[:], gt[:], AF.Sigmoid)
        ms("pool", vnS[:], 1.0)
        cp("dve", vnS[:, 0, :, 0:64], pkv[0:16, 384:512].re("p (h d) -> p h d", h=2))
        cp("dve", vnS[:, 1, :, 0:64], pkv[0:16, 640:768].re("p (h d) -> p h d", h=2))
        psQ = nps()
        for h in range(8):
            for k in range(8):
                mm(psQ[0:64, h * 16:(h + 1) * 16], win_b[:, k, 64 * h:64 * h + 64], hT[:, k, 0:16], st=(k == 0), sp=(k == 7))
        act(qTs[:].re("p h t -> p (h t)"), psQ[0:64, 0:128], AF.Copy, scale=0.125)
        psK = nps()
        for gi, c0 in enumerate((768, 832, 1024, 1088)):
            for k in range(8):
                mm(psK[0:64, gi * 16:(gi + 1) * 16], win_b[:, k, c0:c0 + 64], hT[:, k, 0:16], st=(k == 0), sp=(k == 7))
        cp("dve", KnS[0:64].re("p a k t -> p (a k t)"), psK[0:64, 0:64])

        gk = fw.sb(s2, [128, 4096], F32, "gk")
        gv = fw.sb(s2, [128, 4096], F32, "gv")
        kb = fw.sb(s2, [128, 32, 128], BF16, "kbS")
        vbp = fw.sb(s2, [128, 32, 2, 65], BF16, "vbp")
        ms("pool", vbp[:], 1.0)
        vwS = fw.sb(s2, [128, 4, 2, 65], BF16, "vwS")
        ms("pool", vwS[:], 1.0)
        ptc = fw.sb(s2, [128, 8, 16], BF16, "ptc")
        ptsb = fw.sb(s2, [128, 32, 16], BF16, "ptsb")
        ptw = fw.sb(s2, [128, 4, 16], BF16, "ptw")
        ptn = fw.sb(s2, [16, 16], BF16, "ptn")
        maskEO = [fw.sb(s2, [128, 2, 4, 4], BF16, f"maskEO{k}") for k in range(2)]
        obr = [[fw.sb(s2, [4, 4, 65], F32, f"obrS{k}{i}") for i in range(3)] for k in range(2)]
        imp4 = fw.sb(s2, [4, 4, 256], F32, "imp4S")
        imp = fw.sb(s2, [4, 256], F32, "impS")
        imp2 = fw.sb(s2, [4, 256], F32, "imp2S")
        mx1 = fw.sb(s2, [4, 8], F32, "mx1S")
        mx2 = fw.sb(s2, [4, 8], F32, "mx2S")
        sel01 = fw.sb(s2, [4, 256], F32, "sel01")
        selT = fw.sb(s2, [128, 8], F32, "selTS")
        gtb = fw.sb(s2, [4, 24], F32, "gtb")
        rd = fw.sb(s2, [4, 3, 4], F32, "rdS")
        sc3 = fw.sb(s2, [4, 3, 4], F32, "sc3S")
        onsa = fw.sb(s2, [4, 4, 64], F32, "onsaS")
        otmp = fw.sb(s2, [4, 4, 64], F32, "otmpS")
        ss4 = fw.sb(s2, [4, 4], F32, "ss4S")
        rs4 = fw.sb(s2, [4, 4], F32, "rs4S")
        onb = fw.sb(s2, [4, 512], BF16, "onbS")
        ms("pool", mixin[:], 0.0)
        for b in range(4):
            cp("dve", QS[0:64].re("p i k (g q) -> p i (k g) q", g=4),
               qTs[:, :, 4 * b:4 * b + 4].un(1).bc([64, 4, 8, 4]))
            dma(kcSb[0:64].re("p k r t -> p k (r t)"), V(S["kcS_d"], S["kcS_d"][b].a.rearrange("k d n -> d k n")))
            dma(vcSb[:].re("p r k d -> p (r k) d")[:, :, 0:64], V(S["vcS_d"], S["vcS_d"][b].a.rearrange("p r k d -> p (r k) d")))
            dma(gtb[:], gt[4 * b:4 * b + 4, :])
            for kvh in range(2):
                Sc = nps()
                for rb in range(8):
                    mm(Sc[:, rb * 16:(rb + 1) * 16], kcSb[:, kvh, rb, :], QS[:, 0, kvh, :])
                act(ptc[:].re("p r c -> p (r c)"), Sc[:, 0:128], AF.Exp)
                ts("dve", ptc[:, 7, :], ptc[:, 7, :], maskC7[:, 0:1], None, op0=ALU.mult)
                accC = psa[0]
                impP = [nps(), nps()]
                for rb in range(8):
                    for g in range(4):
                        mm(accC[0:4, g * 65:(g + 1) * 65], ptc[:, rb, 4 * g:4 * g + 4], vcSb[:, rb, kvh, :],
                           st=(rb == 0), sp=(rb == 7))
                        mm(impP[g // 2][0:4, (g % 2) * 256:(g % 2) * 256 + 256], ptc[:, rb, 4 * g:4 * g + 4],
                           mimpS[:, rb, :], st=(rb == 0), sp=(rb == 7))
                cp("dve", obr[kvh][0][:].re("p g d -> p (g d)"), accC[0:4, 0:260])
                for hh in range(2):
                    cp("act", imp4[:, 2 * hh:2 * hh + 2, :].re("p g j -> p (g j)"), impP[hh][0:4, 0:512])
                ts("dve", rd[:, 0, :], obr[kvh][0][:, :, 64], 1e-30, None, op0=ALU.max)
                self.recip(rd[:, 0, :], rd[:, 0, :])
                ts("dve", imp[:], imp4[:, 0, :], rd[:, 0, 0:1], None, op0=ALU.mult)
                for g in range(1, 4):
                    stt("dve", imp[:], imp4[:, g, :], rd[:, 0, g:g + 1], imp[:], ALU.mult, ALU.add)
                ms("dve", imp[:, 0:1], 3e9)
                ms("dve", imp[:, 255:256], 1e9)
                fw.op("dve", lambda e: e.max(out=mx1[:].a, in_=imp[:].a), reads=[imp], writes=[mx1])
                fw.op("dve", lambda e: e.match_replace(out=imp2[:].a, in_to_replace=mx1[:].a, in_values=imp[:].a,
                                                       imm_value=-1e30), reads=[imp, mx1], writes=[imp2])
                fw.op("dve", lambda e: e.max(out=mx2[:].a, in_=imp2[:].a), reads=[imp2], writes=[mx2])
                ts("dve", sel01[:], imp[:], mx2[:, 6:7], None, op0=ALU.is_ge)
                psT = nps()
                tr(psT[:, 0:4], sel01[0:4, 0:256:2], identf[0:4, 0:4])
                tr(psT[:, 4:8], sel01[0:4, 1:256:2], identf[0:4, 0:4])
                cp("dve", selT[:], psT[:, 0:8])
                cp("dve", maskEO[kvh][:], selT[:].re("p (e q) -> p e q", e=2).un(2).bc([128, 2, 4, 4]))
            accS = [psa[0], psa[1]]
            for i in range(4):
                gather(self, gk[:], S["pool_ks"], idx[b][:, :], 32 * i)
                gather(self, gv[:], S["pool_vs"], idx[b][:, :], 32 * i)
                cp("dve", kb[:, 0:16, :].re("p r f -> p (r f)"), gk[:, 0:2048])
                cp("pool", kb[:, 16:32, :].re("p r f -> p (r f)"), gk[:, 2048:4096])
                cp("pool", vbp[:, :, :, 0:64], gv[:].re("p (r h d) -> p r h d", r=32, h=2))
                for kvh in range(2):
                    for g8 in range(4):
                        pt = npt()
                        for r8 in range(8):
                            tr(pt[0:64, r8 * 128:(r8 + 1) * 128], kb[:, 8 * g8 + r8, kvh * 64:(kvh + 1) * 64], ident[:])
                        cp("act" if g8 % 2 == 0 else "dve", ksS[0:64, kvh, 8 * g8:8 * g8 + 8, :],
                           pt[0:64, :].re("p (r t) -> p r t", r=8))
                for kvh in range(2):
                    Ss = nps()
                    for r_ in range(32):
                        mm(Ss[:, r_ * 16:(r_ + 1) * 16], ksS[:, kvh, r_, :], QS[:, i, kvh, :])
                    act(ptsb[:].re("p r c -> p (r c)"), Ss[:, :], AF.Exp)
                    tt("dve", ptsb[:].re("p r (g q) -> p r g q", g=4), ptsb[:].re("p r (g q) -> p r g q", g=4),
                       maskEO[kvh][:, i // 2].un(1).bc([128, 32, 4, 4]), ALU.mult)
                    for r_ in range(32):
                        for g in range(4):
                            mm(accS[kvh][0:4, g * 65:(g + 1) * 65], ptsb[:, r_, 4 * g:4 * g + 4], vbp[:, r_, kvh, :],
                               st=(i == 0 and r_ == 0), sp=False)
            for kvh in range(2):
                Sn = nps()
                mm(Sn[0:16, 0:16], KnS[:, 0, kvh, :], QS[:, 0, kvh, :])
                act(ptn[:], Sn[0:16, 0:16], AF.Exp)
                tt("dve", ptn[:].re("p (g q) -> p g q", g=4), ptn[:].re("p (g q) -> p g q", g=4),
                   newm[:, b, :].un(1).bc([16, 4, 4]), ALU.mult)
                for g in range(4):
                    mm(accS[kvh][0:4, g * 65:(g + 1) * 65], ptn[:, 4 * g:4 * g + 4], vnS[:, 0, kvh, :], st=False, sp=True)
                cp("dve", obr[kvh][1][:].re("p g d -> p (g d)"), accS[kvh][0:4, 0:260])
            dma(gk[:, 0:512].re("p (a f) -> p a f", a=4), V(None, S["stk"].a[b].rearrange("(a p) f -> p a f", p=128)))
            dma(gv[:, 0:512].re("p (a f) -> p a f", a=4), V(None, S["stv"].a[b].rearrange("(a p) f -> p a f", p=128)))
            cp("dve", kb[:, 0:4, :].re("p r f -> p (r f)"), gk[:, 0:512])
            cp("pool", vwS[:, :, :, 0:64], gv[:, 0:512].re("p (a h d) -> p a h d", a=4, h=2))
            pt = npt()
            for kvh in range(2):
                for a in range(4):
                    tr(pt[0:64, (kvh * 4 + a) * 128:(kvh * 4 + a + 1) * 128], kb[:, a, kvh * 64:(kvh + 1) * 64], ident[:])
            cp("act", kwS[0:64].re("p k a t -> p (k a t)"), pt[0:64, :])
            accW = [psa[0], psa[1]]
            for kvh in range(2):
                Sw = nps()
                for a in range(4):
                    mm(Sw[:, a * 16:(a + 1) * 16], kwS[:, kvh, a, :], QS[:, 0, kvh, :])
                act(ptw[:].re("p a c -> p (a c)"), Sw[:, 0:64], AF.Exp)
                tt("dve", ptw[:, 0, :].re("p (g q) -> p g q", g=4), ptw[:, 0, :].re("p (g q) -> p g q", g=4),
                   winm0[:].un(1).bc([128, 4, 4]), ALU.mult)
                for a in range(4):
                    for g in range(4):
                        mm(accW[kvh][0:4, g * 65:(g + 1) * 65], ptw[:, a, 4 * g:4 * g + 4], vwS[:, a, kvh, :],
                           st=(a == 0), sp=False)
                Sn = nps()
                mm(Sn[0:16, 0:16], KnS[:, 1, kvh, :], QS[:, 0, kvh, :])
                act(ptn[:], Sn[0:16, 0:16], AF.Exp)
                tt("dve", ptn[:].re("p (g q) -> p g q", g=4), ptn[:].re("p (g q) -> p g q", g=4),
                   newm[:, b, :].un(1).bc([16, 4, 4]), ALU.mult)
                for g in range(4):
                    mm(accW[kvh][0:4, g * 65:(g + 1) * 65], ptn[:, 4 * g:4 * g + 4], vnS[:, 1, kvh, :], st=False, sp=True)
                cp("dve", obr[kvh][2][:].re("p g d -> p (g d)"), accW[kvh][0:4, 0:260])
            for nm_, src_, c0 in (("s_kw", "stk", 512), ("s_vw", "stv", 640)):
                dma(V(None, S[nm_].a[b, 0:508, :]), V(None, S[src_].a[b, 4:512, :]))
                dma(V(None, S[nm_].a[b, 508:512, :]), pkv[4 * b:4 * b + 4, c0:c0 + 128])
            for kvh in range(2):
                for br in range(1, 3):
                    ts("dve", rd[:, br, :], obr[kvh][br][:, :, 64], 1e-30, None, op0=ALU.max)
                    self.recip(rd[:, br, :], rd[:, br, :])
                ts("dve", rd[:, 0, :], obr[kvh][0][:, :, 64], 1e-30, None, op0=ALU.max)
                self.recip(rd[:, 0, :], rd[:, 0, :])
                gvw = gtb[:, 12 * kvh:12 * kvh + 12].re("p (g b) -> p b g", b=3)
                tt("dve", sc3[:], rd[:], gvw, ALU.mult)
                tt("dve", onsa[:], obr[kvh][0][:, :, 0:64], sc3[:, 0, :].un(2).bc([4, 4, 64]), ALU.mult)
                for br in (1, 2):
                    tt("dve", otmp[:], obr[kvh][br][:, :, 0:64], sc3[:, br, :].un(2).bc([4, 4, 64]), ALU.mult)
                    tt("dve", onsa[:], onsa[:], otmp[:], ALU.add)
                tt("dve", otmp[:], onsa[:], onsa[:], ALU.mult)
                self.rsum(ss4[:], otmp[:])
                self.rstd(rs4[:], ss4[:], 1.0 / 64)
                tt("dve", onb[:, 256 * kvh:256 * kvh + 256].re("p (g d) -> p g d", g=4), onsa[:],
                   rs4[:].un(2).bc([4, 4, 64]), ALU.mult)
            dma(mixin[4 * b:4 * b + 4, 0:512], onb[:])
        fw.barrier()
        s2.close()
        self.sample_mlstm(s, L, hT, mixin)
        mT = fw.sb(s, [128, 8, 128], BF16, "mTS")
        x1t = fw.sb(s, [128, D], F32, "x1tS")
        ptm = npt()
        for k in range(8):
            tr(ptm[:, k * 128:(k + 1) * 128], mixin[:, k * 128:(k + 1) * 128], ident[:])
        cp("act", mT[:].re("p k t -> p (k t)"), ptm[:])
        for g in range(2):
            ps = nps()
            for k in range(8):
                mm(ps[:, :], mT[:, k, :], wout_b[:, k, g * 512:(g + 1) * 512], st=(k == 0), sp=(k == 7))
            tt("dve", x1t[:, g * 512:(g + 1) * 512], xt[:, g * 512:(g + 1) * 512], ps[:, :], ALU.add)
        dma(S["x1s"][:, :], x1t[0:16, :])
        fw.barrier()


Builder.sample_pass1 = sample_pass1


def sample_mlstm(self, s, L, hT, mixin):
    fw, S = self.fw, self.S
    mm, tr, act, ts, tt, stt, cp, ms, iota, dma, dmas = (self.mm, self.tr, self.act, self.ts, self.tt, self.stt,
                                                         self.cp, self.ms, self.iota, self.dma, self.dmas)
    win_b, wqm_b, wkm_b = L["win_b"], L["wqm_b"], L["wkm_b"]
    identf, nps, cw, cb, bif, tmpf = L["identf"], L["nps"], L["cw"], L["cb"], L["bif"], L["tmpf"]
    KSC = 128.0 ** -0.5
    E = fw.sb(s, [4, 128], F32, "E4")
    iota(E[:], [[1, 128]], base=0, cm=-4)
    Eb = fw.sb(s, [4, 128], F32, "E4b")
    ts("pool", Eb[:], E[:], 0.0, None, op0=ALU.is_ge)
    ts("pool", E[:], E[:], 3.0, None, op0=ALU.is_le)
    tt("pool", E[:], E[:], Eb[:], ALU.mult)
    triS = fw.sb(s, [128, 128], F32, "triS")
    ps = nps()
    mm(ps[:, 0:128], E[:], E[:])
    tt("dve", triS[:], ps[:, 0:128], L["tri_le"][:], ALU.mult)
    bdS = fw.sb(s, [16, 16], BF16, "bdS")
    cp("dve", bdS[:], triS[0:16, 0:16])
    cselS = fw.sb(s, [128, 4, 128], F32, "cselS")
    d4 = fw.sb(s, [4, 4, 128], F32, "d4")
    cp("dve", d4[:], identf[0:4, 0:4].un(2).bc([4, 4, 128]))
    ps = nps()
    for b in range(4):
        mm(ps[:, b * 128:(b + 1) * 128], E[:], d4[:, b, :])
    cp("dve", cselS[:].re("p b m -> p (b m)"), ps[:, :])
    psV, psO, psG = nps(), nps(), nps()
    for k in range(8):
        mm(psV[:, 0:512], hT[:, k, :], win_b[:, k, 1816:2328], st=(k == 0), sp=(k == 7))
    for k in range(8):
        mm(psO[:, 0:512], hT[:, k, :], win_b[:, k, 2328:2840], st=(k == 0), sp=(k == 7))
    for k in range(8):
        mm(psG[:, 0:8], hT[:, k, :], win_b[:, k, 2840:2848], st=(k == 0), sp=(k == 7))
    gif = fw.sb(s, [128, 8], F32, "gifS")
    l1 = fw.sb(s, [128, 4], F32, "l1S")
    sigo = fw.sb(s, [128, 512], F32, "sigoS")
    tt("dve", gif[:], psG[:, 0:8], bif[:], ALU.add)
    act(l1[:], gif[:, 4:8], AF.Exp, scale=-1.0)
    act(l1[:], l1[:], AF.Ln, bias=1.0)
    act(sigo[:], psO[:, 0:512], AF.Sigmoid)
    psC = nps()
    mm(psC[:, 0:4], triS[:], l1[:])
    for b in range(4):
        mm(psC[:, 4 + 4 * b:8 + 4 * b], cselS[:, b, :], l1[:])
    gsb = fw.sb(s, [128, 20], F32, "gsbS")
    cp("dve", gsb[:], psC[:, 0:20])
    wl = fw.sb(s, [128, 4], F32, "wlS")
    ul = fw.sb(s, [128, 4], F32, "ulS")
    tmp4 = fw.sb(s, [128, 4], F32, "tmp4S")
    own = fw.sb(s, [128, 4], F32, "ownS")
    dec = fw.sb(s, [128, 4], F32, "decS")
    ebt = fw.sb(s, [128, 16], F32, "ebtS")
    act(wl[:], gsb[:, 0:4], AF.Exp, scale=-1.0)
    tt("dve", tmp4[:], gif[:, 0:4], gsb[:, 0:4], ALU.add)
    act(ul[:], tmp4[:], AF.Exp)
    act(ebt[:], gsb[:, 4:20], AF.Exp, scale=-1.0)
    ts("dve", own[:], gsb[:, 4:8], cselS[:, 0, 0:1], None, op0=ALU.mult)
    for b in range(1, 4):
        stt("dve", own[:], gsb[:, 4 + 4 * b:8 + 4 * b], cselS[:, b, 0:1], own[:], ALU.mult, ALU.add)
    tt("dve", dec[:], tmp4[:], own[:], ALU.subtract)
    vmu = fw.sb(s, [128, 4, 129], BF16, "vmuS")
    tt("dve", vmu[:, :, 0:128], psV[:, 0:512].re("p (h e) -> p h e", h=4), ul[:].un(2).bc([128, 4, 128]), ALU.mult)
    cp("dve", vmu[:, :, 128], ul[:])
    psT = nps()
    tr(psT[0:4, 0:128], dec[:], identf[:])
    mm(psT[0:4, 128:132], l1[:], cselS[:, :, 0])
    tsb = fw.sb(s, [4, 132], F32, "tsbS")
    cp("dve", tsb[:], psT[0:4, 0:132])
    Dm = fw.sb(s, [4, 4], F32, "DmS")
    self.rmax(Dm[:], tsb[:, 0:16].re("p (b i) -> p b i", b=4))
    R = fw.sb(s, [4, 4], F32, "RS")
    dmas(R[:], V(None, S["sm"].a.rearrange("(b h) -> h b", h=4)))
    tt("dve", R[:], R[:], tsb[:, 128:132], ALU.subtract)
    tt("dve", R[:], R[:], Dm[:], ALU.max)
    dmas(V(None, S["s_m"].a.rearrange("(b h) -> h b", h=4)), R[:])
    xcv = fw.sb(s, [128, 4, 4, 7], F32, "xcvS")
    for b in range(4):
        for ch in range(4):
            dmas(xcv[:, ch, b, 0:3], V(None, S["sconv"].a[b, :, ch * 128:(ch + 1) * 128].rearrange("j p -> p j")))
    psX = nps()
    for ch in range(4):
        for k in range(8):
            mm(psX[:, ch * 16:(ch + 1) * 16], win_b[:, k, 1304 + ch * 128:1432 + ch * 128], hT[:, k, 0:16],
               st=(k == 0), sp=(k == 7))
    cp("act", xcv[:, :, :, 3:7], psX[:, 0:64].re("p (c b i) -> p c b i", c=4, b=4))
    cacc = fw.sb(s, [128, 4, 16], F32, "caccS")
    for ch in range(4):
        cv = cacc[:, ch, :].re("p (b i) -> p b i", b=4)
        ts("dve", cv, xcv[:, ch, :, 0:4], cw[:, ch, 0:1], cb[:, ch:ch + 1], op0=ALU.mult, op1=ALU.add)
        for j in range(1, 4):
            stt("dve", cv, xcv[:, ch, :, j:j + 4], cw[:, ch, j:j + 1], cv, ALU.mult, ALU.add)
    xc = fw.sb(s, [128, 4, 16], BF16, "xcS")
    act(xc[:], cacc[:], AF.Silu)
    for b in range(4):
        for j in range(3):
            dmas(V(None, S["s_conv"].a[b, j].rearrange("(c p) -> p c", p=128)), xcv[:, :, b, 4 + j])
    qmT = fw.sb(s, [128, 4, 16], BF16, "qmTS")
    kmT = fw.sb(s, [128, 4, 16], BF16, "kmTS")
    qmS = [fw.sb(s, [128, 4, 16], BF16, f"qmSS{b}") for b in range(4)]
    kmS = [fw.sb(s, [16, 4, 128], BF16, f"kmSS{b}") for b in range(4)]
    psq = nps()
    for h in range(4):
        mm(psq[:, h * 16:(h + 1) * 16], wqm_b[:, h, :], xc[:, h, :])
    cp("act", qmT[:].re("p h t -> p (h t)"), psq[:, 0:64])
    for b in range(4):
        ms("pool", qmS[b][:], 0.0)
        cp("dve", qmS[b][:, :, 4 * b:4 * b + 4], psq[:, 0:64].re("p (h t) -> p h t", h=4)[:, :, 4 * b:4 * b + 4])
    psk = nps()
    for h in range(4):
        mm(psk[:, h * 16:(h + 1) * 16], wkm_b[:, h, :], xc[:, h, :])
    act(kmT[:].re("p h t -> p (h t)"), psk[:, 0:64], AF.Copy, scale=KSC)
    pskt = nps()
    for h in range(4):
        mm(pskt[0:16, h * 128:(h + 1) * 128], xc[:, h, :], wkm_b[:, h, :])
    for b in range(4):
        ts("dve", kmS[b][:].re("p h t -> p (h t)"), pskt[0:16, :], cselS[0:16, b, 0:1], KSC, op0=ALU.mult, op1=ALU.mult)
    psqk = nps()
    for h in range(4):
        mm(psqk[0:16, h * 16:(h + 1) * 16], kmT[:, h, :], qmT[:, h, :])
    mqk = fw.sb(s, [16, 4, 16], BF16, "mqkS")
    tt("dve", mqk[:], psqk[0:16, 0:64].re("p (h t) -> p h t", h=4), bdS[:].un(1).bc([16, 4, 16]), ALU.mult)
    em0 = fw.sb(s, [128, 16], F32, "em0")
    dma(em0[:], V(None, S["sm"].a.partition_broadcast(128)))
    act(em0[:], em0[:], AF.Exp)
    Sf = [fw.sb(s, [128, 4, 129], F32, f"SfS{b}") for b in range(4)]
    Sb0 = [fw.sb(s, [128, 4, 129], BF16, f"Sb0S{b}") for b in range(4)]
    cst_ = [fw.sb(s, [128, 128], F32, f"c0st{i}") for i in range(2)]
    dS = fw.sb(s, [128, 4, 129], F32, "dSS")
    for b in range(4):
        for h in range(4):
            st = cst_[h % 2]
            dma(st[:], V(None, S["sC"].a[b, h]))
            ps = nps()
            tr(ps[:, 0:128], st[:], identf[:])
            cp("dve", Sf[b][:, h, 0:128], ps[:, 0:128])
        dmas(Sf[b][:, :, 128], V(None, S["sn"].a[b].rearrange("h d -> d h")))
        tt("dve", Sf[b][:], Sf[b][:], em0[:, 4 * b:4 * b + 4].un(2).bc([128, 4, 129]), ALU.mult)
        cp("pool", Sb0[b][:], Sf[b][:])
        pd = [nps(), nps()]
        for h in range(4):
            mm(pd[h // 2][:, (h % 2) * 129:(h % 2) * 129 + 129], kmS[b][:, h, :], vmu[0:16, h, :])
        tt("dve", Sf[b][:], Sf[b][:], ebt[:, 4 * b:4 * b + 4].un(2).bc([128, 4, 129]), ALU.mult)
        for hh in range(2):
            tt("dve", dS[:, 2 * hh:2 * hh + 2, :], pd[hh][:, 0:258].re("p (h e) -> p h e", h=2),
               ebt[:, 4 * b + 2 * hh:4 * b + 2 * hh + 2].un(2).bc([128, 2, 129]), ALU.mult)
        tt("dve", Sf[b][:], Sf[b][:], dS[:], ALU.add)
    pa = [nps(), nps()]
    for h in range(4):
        o_ = pa[h // 2][0:16, (h % 2) * 129:(h % 2) * 129 + 129]
        mm(o_, mqk[:, h, :], vmu[0:16, h, :], st=True, sp=False)
        for b in range(4):
            mm(o_, qmS[b][:, h, :], Sb0[b][:, h, :], st=False, sp=(b == 3))
    dn = fw.sb(s, [16, 4], F32, "dnS")
    t4 = fw.sb(s, [16, 4], F32, "t4S")
    hout = fw.sb(s, [16, 4, 128], F32, "houtS")
    hsq = fw.sb(s, [16, 4, 128], F32, "hsqS")
    ss4 = fw.sb(s, [16, 4], F32, "ss4m")
    rs4 = fw.sb(s, [16, 4], F32, "rs4m")
    for hh in range(2):
        av = pa[hh][0:16, 0:258].re("p (h e) -> p h e", h=2)
        tt("dve", dn[:, 2 * hh:2 * hh + 2], av[:, :, 128], wl[0:16, 2 * hh:2 * hh + 2], ALU.mult)
    stt("dve", t4[:], dn[:], -1.0, dn[:], ALU.mult, ALU.max)
    ts("dve", t4[:], t4[:], 1.0, None, op0=ALU.max)
    self.recip(t4[:], t4[:])
    tt("dve", t4[:], t4[:], wl[0:16, :], ALU.mult)
    for hh in range(2):
        av = pa[hh][0:16, 0:258].re("p (h e) -> p h e", h=2)
        tt("dve", hout[:, 2 * hh:2 * hh + 2, :], av[:, :, 0:128], t4[:, 2 * hh:2 * hh + 2].un(2).bc([16, 2, 128]), ALU.mult)
    tt("dve", hsq[:], hout[:], hout[:], ALU.mult)
    self.rsum(ss4[:], hsq[:])
    self.rstd(rs4[:], ss4[:], 1.0 / 128)
    tt("dve", hout[:], hout[:], rs4[:].un(2).bc([16, 4, 128]), ALU.mult)
    tt("dve", mixin[0:16, 512:1024], hout[:].re("p h e -> p (h e)"), sigo[0:16, :], ALU.mult)
    Rd = fw.sb(s, [4, 4, 4], F32, "RdS")
    tt("dve", Rd[:], R[:].un(2).bc([4, 4, 4]), identf[0:4, 0:4].un(1).bc([4, 4, 4]), ALU.mult)
    ones4 = fw.sb(s, [4, 128], F32, "ones4S")
    ms("pool", ones4[:], 1.0)
    ps = nps()
    mm(ps[:, 0:16], ones4[:], Rd[:].re("p b h -> p (b h)"))
    esc = fw.sb(s, [128, 16], F32, "escS")
    act(esc[:], ps[:, 0:16], AF.Exp, scale=-1.0)
    for b in range(4):
        tt("dve", Sf[b][:], Sf[b][:], esc[:, 4 * b:4 * b + 4].un(2).bc([128, 4, 129]), ALU.mult)
        dmas(V(None, S["s_n"].a[b].rearrange("h d -> d h")), Sf[b][:, :, 128])
        for h in range(4):
            ps = nps()
            tr(ps[:, 0:128], Sf[b][:, h, 0:128], identf[:])
            st = cst_[h % 2]
            cp("dve", st[:], ps[:, 0:128])
            dma(V(None, S["s_C"].a[b, h]), st[:])


Builder.sample_mlstm = sample_mlstm
Builder.sample_s0 = sample_s0

W_NAMES = ["w_in", "g_mix", "b_gate", "cmp_pe_k", "cmp_w1_k", "cmp_b1_k", "cmp_w2_k", "cmp_pe_v", "cmp_w1_v",
           "cmp_b1_v", "cmp_w2_v", "g_head_nsa", "conv_w", "conv_b", "w_qm", "w_km", "b_i", "b_f", "g_head_m",
           "w_out", "g_xa", "g_mem", "w_xq", "w_xk", "w_xv", "w_xo", "g_ffn", "w_gate", "w_up", "w_down", "g_final"]


def build_program(NT=32, sample=True, debug=False):
    nc = bass.Bass("TRN2", target_bir_lowering=False)
    b = Builder(nc, NT=NT, sample=sample, debug=debug)
    b.build()
    return nc, b


def core_inputs(inp, c, b, NT=32):
    f = lambda a: np.ascontiguousarray(a, dtype=np.float32)
    T = NT * 128
    m = {"xp": f(inp["x_prompt"][c, :T]), "memp": f(inp["mem_prompt"][c])}
    for n in W_NAMES:
        a = np.asarray(inp[n])
        if n != "g_final":
            a = a[0]
        m[n] = f(a).reshape(b.io[n].shape)
    if b.sample:
        sl = slice(4 * c, 4 * c + 4)
        m["xs"] = f(inp["x_sample"][sl]).reshape(16, D)
        for n, k in (("pool_kc", "cache_k_cmp"), ("pool_vc", "cache_v_cmp"), ("pool_ks", "cache_k_slc"), ("pool_vs", "cache_v_slc")):
            m[n] = np.asarray(inp[k][0], dtype=np.float32).reshape(5120, 16384)
        m["ptab"] = np.ascontiguousarray(inp["page_table"][sl], dtype=np.int32)
        m["stk"] = f(inp["state_k_win"][0, sl]).reshape(4, 512, 128)
        m["stv"] = f(inp["state_v_win"][0, sl]).reshape(4, 512, 128)
        m["sconv"] = f(inp["state_conv"][0, sl])
        m["sC"] = f(inp["state_C"][0, sl])
        m["sn"] = f(inp["state_n"][0, sl])
        m["sm"] = f(inp["state_m"][0, sl]).reshape(16)
        m["cmk"] = f(inp["cache_mem_k"][0, sl]).reshape(4, 256, D)
        m["cmv"] = f(inp["cache_mem_v"][0, sl]).reshape(4, 256, D)
    return {k: v for k, v in m.items() if k in b.io}


_PROG = {}


def kernel(**inp):
    n = 8
    if "p" not in _PROG:
        _PROG["p"] = build_program()
    nc, b = _PROG["p"]
    in_maps = [core_inputs(inp, c, b) for c in range(n)]
    res = run_bass_kernel_spmd(nc, in_maps, core_ids=list(range(n)))
    R = res.results

    def st(name, shp, lead):
        a = np.stack([np.asarray(R[c][name], dtype=np.float32).reshape(shp) for c in range(n)])
        return np.ascontiguousarray(a.reshape(lead))

    outs = [st("y_p", (4096, D), (8, 4096, D)), st("y_s", (4, 4, D), (32, 4, D))]
    for nm in ("p_kc", "p_vc", "p_ks", "p_vs"):
        outs.append(st(nm, (4096, 2, 64), (1, 8, 4096, 2, 64)))
    for nm in ("p_kw", "p_vw"):
        outs.append(st(nm, (512, 2, 64), (1, 8, 512, 2, 64)))
    outs.append(st("p_C", (4, 128, 128), (1, 8, 4, 128, 128)))
    outs.append(st("p_n", (4, 128), (1, 8, 4, 128)))
    outs.append(st("p_m", (4,), (1, 8, 4)))
    outs.append(st("p_conv", (3, 512), (1, 8, 3, 512)))
    outs.append(st("p_mk", (256, 4, 256), (1, 8, 256, 4, 256)))
    outs.append(st("p_mv", (256, 4, 256), (1, 8, 256, 4, 256)))
    for nm in ("s_kc", "s_vc", "s_ks", "s_vs"):
        outs.append(st(nm, (4, 4, 2, 64), (1, 32, 4, 2, 64)))
    for nm in ("s_kw", "s_vw"):
        outs.append(st(nm, (4, 512, 2, 64), (1, 32, 512, 2, 64)))
    outs.append(st("s_C", (4, 4, 128, 128), (1, 32, 4, 128, 128)))
    outs.append(st("s_n", (4, 4, 128), (1, 32, 4, 128)))
    outs.append(st("s_m", (4, 4), (1, 32, 4)))
    outs.append(st("s_conv", (4, 3, 512), (1, 32, 3, 512)))
    return tuple(outs)
```

```python
import numpy as np
from contextlib import ExitStack
import concourse.bass as bass
import concourse.mybir as mybir
from concourse.bass_utils import run_bass_kernel_spmd

F32 = mybir.dt.float32
BF16 = mybir.dt.bfloat16
I32 = mybir.dt.int32
AF = mybir.ActivationFunctionType
ALU = mybir.AluOpType
AX = mybir.AxisListType

D = 1024
NEG = -30000.0
EPS = 1e-6
IN_COLS = 2848
DFF = 2816


class V:
    __slots__ = ("b", "a")

    def __init__(self, b, a):
        self.b = b
        self.a = a

    def __getitem__(self, k):
        return V(self.b, self.a[k])

    def re(self, p, **kw):
        return V(self.b, self.a.rearrange(p, **kw))

    def bc(self, shape):
        return V(self.b, self.a.to_broadcast(list(shape)))

    def un(self, ax):
        return V(self.b, self.a.unsqueeze(ax))


class Buf:
    __slots__ = ("t", "w", "r", "name", "psum", "fresh", "quads")

    def __init__(self, t, name="", psum=False):
        self.t = t
        self.w = None
        self.r = []
        self.name = name
        self.psum = psum
        self.fresh = True
        self.quads = set()

    def __getitem__(self, k):
        return V(self, self.t[k])


class DSem:
    def __init__(self, nc, name):
        self.sem = nc.alloc_semaphore(name)
        self.val = 0


class FW:
    ENG = ("pe", "act", "dve", "pool", "sp")

    def __init__(self, nc, n_dsem=10, same_engine_sync=True):
        self.nc = nc
        self.e = {"pe": nc.tensor, "act": nc.scalar, "dve": nc.vector, "pool": nc.gpsimd, "sp": nc.sync}
        self.gen = {k: 0 for k in self.ENG}
        self.sem = {k: nc.alloc_semaphore("S_" + k) for k in self.ENG}
        self.cnt = {k: 0 for k in self.ENG}
        self.seen = {k: {} for k in self.ENG}
        self.same = same_engine_sync
        self.dsems = {q: [DSem(nc, f"D{q}{i}") for i in range(n_dsem)] for q in ("sp", "pool", "act")}
        self.dnext = {q: 0 for q in self.dsems}
        self.nbuf = 0
        self.nins = 0

    def sb(self, stack, shape, dt=F32, name=None):
        self.nbuf += 1
        name = name or f"b{self.nbuf}"
        return Buf(stack.enter_context(self.nc.sbuf_tensor(name, list(shape), dt)), name)

    def ps(self, stack, shape, dt=F32, name=None):
        self.nbuf += 1
        name = name or f"p{self.nbuf}"
        return Buf(stack.enter_context(self.nc.psum_tensor(name, list(shape), dt)), name, psum=True)

    def _need(self, e, dep, waits):
        if dep is None:
            return
        kind, key, val, semh = dep
        if kind == "e" and key[0] == e and (not self.same or e == "pe"):
            return
        k = (kind, key if kind == "e" else id(key))
        if self.seen[e].get(k, 0) >= val:
            return
        cur = waits.get(k)
        if cur is None or cur[1] < val:
            waits[k] = (semh, val)

    def _emit_waits(self, e, reads, writes):
        waits = {}
        for b in reads:
            if b is not None:
                self._need(e, b.w, waits)
        for b in writes:
            if b is not None:
                self._need(e, b.w, waits)
                for d in b.r:
                    self._need(e, d, waits)
        eng = self.e[e]
        for k, (semh, val) in waits.items():
            eng.wait_ge(semh, val)
            self.seen[e][k] = val

    def op(self, e, fn, reads=(), writes=()):
        px = [b for b in reads if b is not None and b.psum]
        if px:
            reads = [b for b in reads if not (b is not None and b.psum)]
            writes = list(writes) + [b for b in px if b not in writes]
            if e != "pe":
                for b in px:
                    b.fresh = True
        self._emit_waits(e, reads, writes)
        ins = fn(self.e[e])
        if self.cnt[e] >= 50000:
            self.gen[e] += 1
            self.sem[e] = self.nc.alloc_semaphore(f"S_{e}_{self.gen[e]}")
            self.cnt[e] = 0
        self.cnt[e] += 1
        self.nins += 1
        ins.then_inc(self.sem[e], 1)
        dep = ("e", (e, self.gen[e]), self.cnt[e], self.sem[e])
        for b in reads:
            if b is not None:
                b.r.append(dep)
                if len(b.r) > 16:
                    b.r = self._compact(b.r)
        for b in writes:
            if b is not None:
                b.w = dep
                b.r = []
        return ins

    @staticmethod
    def _compact(lst):
        best = {}
        for d in lst:
            k = (d[0], d[1] if d[0] == "e" else id(d[1]))
            if k not in best or best[k][2] < d[2]:
                best[k] = d
        return list(best.values())

    def dma(self, q, o, i, fn=None, extra_reads=(), **kw):
        reads = [i.b] + list(extra_reads)
        writes = [o.b]
        self._emit_waits(q, reads, writes)
        ds = self.dsems[q][self.dnext[q]]
        self.dnext[q] = (self.dnext[q] + 1) % len(self.dsems[q])
        if ds.val > 0 and self.seen[q].get(("d", id(ds)), 0) < ds.val:
            self.e[q].wait_ge(ds.sem, ds.val)
            self.seen[q][("d", id(ds))] = ds.val
        if fn is None:
            ins = self.e[q].dma_start(out=o.a, in_=i.a, **kw)
        else:
            ins = fn(self.e[q])
        ds.val += 16
        self.nins += 1
        ins.then_inc(ds.sem, 16)
        dep = ("d", ds, ds.val, ds.sem)
        for b in reads:
            if b is not None:
                b.r.append(dep)
                if len(b.r) > 16:
                    b.r = self._compact(b.r)
        for b in writes:
            if b is not None:
                b.w = dep
                b.r = []
        return ins

    def barrier(self):
        for e in self.ENG:
            eng = self.e[e]
            for f in self.ENG:
                if f != e and self.cnt[f] > 0:
                    k = ("e", (f, self.gen[f]))
                    if self.seen[e].get(k, 0) < self.cnt[f]:
                        eng.wait_ge(self.sem[f], self.cnt[f])
                        self.seen[e][k] = self.cnt[f]
            for q in self.dsems:
                for ds in self.dsems[q]:
                    k = ("d", id(ds))
                    if ds.val > 0 and self.seen[e].get(k, 0) < ds.val:
                        eng.wait_ge(ds.sem, ds.val)
                        self.seen[e][k] = ds.val

    def finish(self):
        eng = self.e["sp"]
        for q in self.dsems:
            for ds in self.dsems[q]:
                if ds.val > 0:
                    eng.wait_ge(ds.sem, ds.val)


class RR:
    def __init__(self, items):
        self.items = list(items)
        self.i = 0

    def __call__(self):
        x = self.items[self.i]
        self.i = (self.i + 1) % len(self.items)
        return x


class Builder:
    def __init__(self, nc, NT=32, sample=True, debug=False):
        self.debug = debug
        self.nc = nc
        self.fw = FW(nc)
        self.NT = NT
        self.T = NT * 128
        self.sample = sample
        self.io = {}

    def din(self, name, shape, dt=F32):
        t = self.nc.dram_tensor(name, list(shape), dt, kind="ExternalInput").ap()
        self.io[name] = t
        return V(None, t)

    def dout(self, name, shape, dt=F32):
        t = self.nc.dram_tensor(name, list(shape), dt, kind="ExternalOutput").ap()
        self.io[name] = t
        return V(None, t)

    def dscr(self, name, shape, dt=F32):
        t = self.nc.dram_tensor(name, list(shape), dt, kind="ExternalOutput" if self.debug else "Internal").ap()
        if self.debug:
            self.io[name] = t
        return Buf(t, name)

    def dbg(self, name, v, dt=F32):
        if not self.debug:
            return
        o = self.dout("dbg_" + name, list(v.a.shape), dt)
        self.fw.dma("sp", o, v)

    def mm(self, o, l, r, st=True, sp=True):
        b = o.b
        p0 = o.a.base_partition() if hasattr(o.a, "base_partition") else 0
        q = set(range(p0 // 32, (p0 + o.a.shape[0] + 31) // 32))
        start = False
        if st:
            if b.fresh:
                start = True
                b.fresh = False
                b.quads = set(q)
            else:
                assert q <= b.quads, (b.name, q, b.quads)
        self.fw.op("pe", lambda e: e.matmul(o.a, lhsT=l.a, rhs=r.a, start=start, stop=sp, skip_group_check=True),
                   reads=[l.b, r.b], writes=[o.b])

    def tr(self, o, i, ident):
        self.fw.op("pe", lambda e: e.transpose(out=o.a, in_=i.a, identity=ident.a), reads=[i.b, ident.b], writes=[o.b])

    def act(self, o, i, f, scale=1.0, bias=0.0, acc=None):
        reads = [i.b]
        writes = [o.b]
        kw = {}
        if isinstance(bias, V):
            reads.append(bias.b)
            kw["bias"] = bias.a
        elif bias != 0.0:
            kw["bias"] = float(bias)
        if isinstance(scale, V):
            reads.append(scale.b)
            kw["scale"] = scale.a
        elif scale != 1.0:
            kw["scale"] = float(scale)
        if acc is not None:
            writes.append(acc.b)
            kw["accum_out"] = acc.a
        self.fw.op("act", lambda e: e.activation(out=o.a, in_=i.a, func=f, **kw), reads=reads, writes=writes)

    def ts(self, eng, o, i, s1, s2=None, op0=ALU.mult, op1=None):
        reads = [i.b]
        a1 = s1
        a2 = s2
        if isinstance(s1, V):
            reads.append(s1.b)
            a1 = s1.a
        if isinstance(s2, V):
            reads.append(s2.b)
            a2 = s2.a
        kw = {}
        if op1 is not None:
            kw["op1"] = op1
        self.fw.op(eng, lambda e: e.tensor_scalar(out=o.a, in0=i.a, scalar1=a1, scalar2=a2, op0=op0, **kw), reads=reads, writes=[o.b])

    def tt(self, eng, o, a, b, op):
        self.fw.op(eng, lambda e: e.tensor_tensor(out=o.a, in0=a.a, in1=b.a, op=op), reads=[a.b, b.b], writes=[o.b])

    def stt(self, eng, o, a, s, b, op0, op1):
        reads = [a.b, b.b]
        sa = s
        if isinstance(s, V):
            reads.append(s.b)
            sa = s.a
        self.fw.op(eng, lambda e: e.scalar_tensor_tensor(out=o.a, in0=a.a, scalar=sa, in1=b.a, op0=op0, op1=op1), reads=reads, writes=[o.b])

    def cp(self, eng, o, i):
        if eng == "act":
            self.fw.op("act", lambda e: e.copy(out=o.a, in_=i.a), reads=[i.b], writes=[o.b])
        else:
            self.fw.op(eng, lambda e: e.tensor_copy(out=o.a, in_=i.a), reads=[i.b], writes=[o.b])

    def ms(self, eng, o, val):
        self.fw.op(eng, lambda e: e.memset(o.a, val), writes=[o.b])

    def iota(self, o, pattern, base=0, cm=0):
        self.fw.op("pool", lambda e: e.iota(o.a, pattern=pattern, base=base, channel_multiplier=cm,
                                            allow_small_or_imprecise_dtypes=True), writes=[o.b])

    def sigm(self, o, i):
        self.act(o, i, AF.Exp, scale=-1.0)
        self.ts("dve", o, o, 1.0, None, op0=ALU.add)
        self.recip(o, o)

    def recip(self, o, i):
        self.fw.op("dve", lambda e: e.reciprocal(out=o.a, in_=i.a), reads=[i.b], writes=[o.b])

    def rsum(self, o, i):
        self.fw.op("dve", lambda e: e.reduce_sum(out=o.a, in_=i.a, axis=AX.X), reads=[i.b], writes=[o.b])

    def rmax(self, o, i):
        self.fw.op("dve", lambda e: e.reduce_max(out=o.a, in_=i.a, axis=AX.X), reads=[i.b], writes=[o.b])

    def dma(self, o, i, q="sp", **kw):
        self.fw.dma(q, o, i, **kw)

    def dmas(self, o, i, q="sp"):
        self.fw.dma(q, o, i, allow_slow_non_contiguous=True)

    def put_row(self, dst, pattern, base, n, const=None):
        rowt, rowb = self.rowt, self.rowb
        if const is None:
            rv = rowt[0:1, 0:n]
            if len(pattern) == 2:
                rv = rv.re("p (a b) -> p a b", b=pattern[1][1])
            self.iota(rv, pattern, base=base, cm=0)
        else:
            self.ms("pool", rowt[0:1, 0:n], const)
        self.cp("pool", rowb[0:1, 0:n], rowt[0:1, 0:n])
        self.dma(dst, rowb[0:1, 0:n])

    def rstd(self, o, ss, inv_n):
        self.act(o, ss, AF.Ln, scale=inv_n, bias=self.epsc[0:o.a.shape[0], :])
        self.act(o, o, AF.Exp, scale=-0.5)

    def build(self):
        nc, fw, NT, T = self.nc, self.fw, self.NT, self.T
        din, dout = self.din, self.dout
        mm, tr, act, ts, tt, stt, cp, ms, iota, dma, dmas = (self.mm, self.tr, self.act, self.ts, self.tt, self.stt,
                                                             self.cp, self.ms, self.iota, self.dma, self.dmas)
        xp = din("xp", [T, D])
        memp = din("memp", [256, D])
        w_in = din("w_in", [D, IN_COLS])
        g_mix = din("g_mix", [D])
        b_gate = din("b_gate", [24])
        cmp_in = {}
        for kv in "kv":
            cmp_in[kv] = (din(f"cmp_pe_{kv}", [32, 64]), din(f"cmp_w1_{kv}", [2048, 256]),
                          din(f"cmp_b1_{kv}", [256]), din(f"cmp_w2_{kv}", [256, 64]))
        g_head_nsa = din("g_head_nsa", [512])
        conv_w = din("conv_w", [4, 512])
        conv_b = din("conv_b", [512])
        w_qm = din("w_qm", [4, 128, 128])
        w_km = din("w_km", [4, 128, 128])
        b_i = din("b_i", [4])
        b_f = din("b_f", [4])
        g_head_m = din("g_head_m", [512])
        w_out = din("w_out", [D, D])
        g_xa = din("g_xa", [D])
        g_mem = din("g_mem", [D])
        w_xq = din("w_xq", [D, D])
        w_xk = din("w_xk", [D, D])
        w_xv = din("w_xv", [D, D])
        w_xo = din("w_xo", [D, D])
        g_ffn = din("g_ffn", [D])
        w_gate = din("w_gate", [D, DFF])
        w_up = din("w_up", [D, DFF])
        w_down = din("w_down", [DFF, D])
        g_final = din("g_final", [D])

        y_p = dout("y_p", [T, D])
        p_kv = {n: dout(n, [T, 128]) for n in ("p_kc", "p_vc", "p_ks", "p_vs")}
        WT = min(512, T)
        p_kw = dout("p_kw", [WT, 128])
        p_vw = dout("p_vw", [WT, 128])
        p_C = dout("p_C", [4, 128, 128])
        p_n = dout("p_n", [4, 128])
        p_m = dout("p_m", [4])
        p_conv = dout("p_conv", [3, 512])
        p_mk = dout("p_mk", [256, D])
        p_mv = dout("p_mv", [256, D])

        if self.sample:
            self.sample_io()
        x1d = self.dscr("x1_scr", [T, D])
        x2d = self.dscr("x2_scr", [T, D])

        top = ExitStack()
        with top:
            psf = [fw.ps(top, [128, 512], F32, f"psf{i}") for i in range(4)]
            pst = [fw.ps(top, [128, 1024], BF16, f"pst{i}") for i in range(1)]
            psa = [fw.ps(top, [128, 512], F32, f"psa{i}") for i in range(3)]
            nps = RR(psf)
            npt = RR(pst)

            ident = fw.sb(top, [128, 128], BF16, "ident")
            identf = fw.sb(top, [128, 128], F32, "identf")
            for idt in (ident, identf):
                ms("pool", idt[:], 0.0)
                fw.op("pool", lambda e, idt=idt: e.affine_select(out=idt[:].a, in_=idt[:].a, pattern=[[-1, 128]],
                                                                compare_op=ALU.not_equal, fill=1.0, base=0,
                                                                channel_multiplier=1), reads=[idt], writes=[idt])
            self.epsc = fw.sb(top, [128, 1], F32, "epsc")[:]
            ms("pool", self.epsc, EPS)
            if self.sample:
                self.sample_s0(locals())
            sw = ExitStack()
            tmpf = fw.sb(sw, [128, 512], F32, "tmpf")
            iota(tmpf[:].re("p (g r) -> p g r", g=4), [[0, 4], [-1, 128]], base=0, cm=1)
            caus_add = fw.sb(sw, [128, 512], BF16, "caus_add")
            ts("pool", caus_add[:], tmpf[:], 0.0, NEG, op0=ALU.is_gt, op1=ALU.mult)
            win_add = fw.sb(sw, [128, 512], BF16, "win_add")
            ts("pool", win_add[:], tmpf[:], 0.0, NEG, op0=ALU.is_le, op1=ALU.mult)
            bd01 = fw.sb(sw, [128, 128], BF16, "bd01")
            ts("pool", bd01[:], tmpf[:, 0:128], 0.0, None, op0=ALU.is_le)
            ms("pool", bd01[0:64, 64:128], 0.0)
            tri_le = fw.sb(sw, [128, 128], F32, "tri_le")
            ts("pool", tri_le[:], tmpf[:, 0:128], 0.0, None, op0=ALU.is_le)
            tri2 = fw.sb(sw, [128, 128], F32, "tri2")
            cp("pool", tri2[:], bd01[:])
            csel = fw.sb(sw, [128, 2, 128], F32, "csel")
            ms("pool", csel[:], 0.0)
            ms("pool", csel[0:64, 0, :], 1.0)
            ms("pool", csel[64:128, 1, :], 1.0)
            e0 = fw.sb(sw, [128, 512], F32, "e0")
            iota(e0[:].re("p (g r) -> p g r", g=4), [[0, 4], [-1, 128]], base=0, cm=16)
            mimp = fw.sb(sw, [128, 2, 64], BF16, "mimp")
            for ct in range(2):
                iota(tmpf[:, 0:64], [[-4, 64]], base=ct * 128 - 1, cm=1)
                stt("dve", tmpf[:, 64:128], tmpf[:, 0:64], -1.0, tmpf[:, 0:64], ALU.mult, ALU.max)
                ts("pool", tmpf[:, 128:192], tmpf[:, 64:128], 2.0, 0.5, op0=ALU.is_le, op1=ALU.mult)
                ts("pool", tmpf[:, 192:256], tmpf[:, 64:128], 1.0, 0.5, op0=ALU.is_le, op1=ALU.mult)
                tt("pool", mimp[:, ct, :], tmpf[:, 128:192], tmpf[:, 192:256], ALU.add)
            expand = fw.sb(sw, [64, T], BF16, "expand")
            for c0 in range(0, T, 512):
                iota(tmpf[0:64, :], [[1, 512]], base=c0, cm=-64)
                ts("pool", tmpf[0:64, :], tmpf[0:64, :], 31.5, None, op0=ALU.subtract)
                stt("dve", tmpf[0:64, :], tmpf[0:64, :], -1.0, tmpf[0:64, :], ALU.mult, ALU.max)
                ts("pool", expand[:, c0:c0 + 512], tmpf[0:64, :], 32.0, None, op0=ALU.is_le)

            if True:
                win_b = fw.sb(sw, [128, 8, IN_COLS], BF16, "win_b")
                wout_b = fw.sb(sw, [128, 8, D], BF16, "wout_b")
                wqm_b = fw.sb(sw, [128, 4, 128], BF16, "wqm_b")
                wkm_b = fw.sb(sw, [128, 4, 128], BF16, "wkm_b")
                gcol = fw.sb(sw, [128, 16], F32, "gcol")
                dmas(gcol[:, 0:8], V(None, g_mix.a.rearrange("(k p) -> p k", p=128)))
                dmas(gcol[:, 8:12], V(None, g_head_nsa.a.rearrange("(k p) -> p k", p=128)))
                dmas(gcol[:, 12:16], V(None, g_head_m.a.rearrange("(k p) -> p k", p=128)))
                cw = fw.sb(sw, [128, 4, 4], F32, "cw")
                for j in range(4):
                    dmas(cw[:, :, j], V(None, conv_w.a[j].rearrange("(c p) -> p c", p=128)))
                cb = fw.sb(sw, [128, 4], F32, "cb")
                dmas(cb[:], V(None, conv_b.a.rearrange("(c p) -> p c", p=128)))
                bgate = fw.sb(sw, [128, 24], F32, "bgate")
                dma(bgate[:], V(None, b_gate.a.partition_broadcast(128)))
                bif = fw.sb(sw, [128, 8], F32, "bif")
                dma(bif[:, 0:4], V(None, b_i.a.partition_broadcast(128)))
                dma(bif[:, 4:8], V(None, b_f.a.partition_broadcast(128)))
                with ExitStack() as s0:
                    stg = [fw.sb(s0, [128, IN_COLS], F32, f"stg{i}") for i in range(2)]
                    for k in range(8):
                        st = stg[k % 2]
                        dma(st[:], w_in[k * 128:(k + 1) * 128, :])
                        if k % 2 == 0:
                            ts("dve", win_b[:, k, :], st[:], gcol[:, k:k + 1], None, op0=ALU.mult)
                        else:
                            act(win_b[:, k, :], st[:], AF.Copy, scale=gcol[:, k:k + 1])
                    for k in range(8):
                        st = stg[k % 2]
                        dma(st[:, 0:D], w_out[k * 128:(k + 1) * 128, :])
                        if k % 2 == 0:
                            ts("dve", wout_b[:, k, :], st[:, 0:D], gcol[:, 8 + k:9 + k], None, op0=ALU.mult)
                        else:
                            act(wout_b[:, k, :], st[:, 0:D], AF.Copy, scale=gcol[:, 8 + k:9 + k])
                    st = stg[0]
                    dma(st[:, 0:512].re("p (h e) -> p h e", h=4), V(None, w_qm.a.rearrange("h d e -> d h e")))
                    cp("dve", wqm_b[:], st[:, 0:512].re("p (h e) -> p h e", h=4))
                    st = stg[1]
                    dma(st[:, 0:512].re("p (h e) -> p h e", h=4), V(None, w_km.a.rearrange("h d e -> d h e")))
                    cp("dve", wkm_b[:], st[:, 0:512].re("p (h e) -> p h e", h=4))
                    fw.barrier()

                kcp = fw.sb(sw, [68, 2, 256], BF16, "kcp")
                vcp = fw.sb(sw, [128, 2, 2, 65], BF16, "vcp")
                ms("pool", vcp[:], 1.0)
                put_row = self.put_row
                with ExitStack() as tmps:
                    self.rowt = fw.sb(tmps, [1, 4096], F32, "rowt")
                    self.rowb = fw.sb(tmps, [1, 4096], BF16, "rowb")
                    for kvh in range(2):
                        put_row(kcp[64:65, kvh, :], [[128, 32], [0, 8]], 0, 256)
                        put_row(kcp[65:66, kvh, :], [[0, 32], [16, 8]], 31, 256)
                        put_row(kcp[66:67, kvh, :], None, 0, 256, const=1.0)
                        put_row(kcp[67:68, kvh, :], None, 0, 256, const=1.0)
                    fw.barrier()

                self.pass0_prompt(sw, xp, win_b, cmp_in, kcp, vcp, ident, nps, npt)
                self.dbg("kcp", kcp[:], BF16)
                self.dbg("vcp", vcp[:], BF16)
                self.pass1_prompt(sw, locals())
                if self.sample:
                    self.sample_pass1(locals())
            fw.barrier()
            sw.close()
            self.pass2(top, locals())
            fw.finish()

    def norm_T(self, src, xt, nb, hT, ident, npt, rows=128):
        if rows < 128:
            self.ms("pool", xt[:], 0.0)
        self.dma(xt[0:rows, :], src)
        self.ms("dve", nb["ss"][:], 0.0)
        self.act(nb["junk"][:], xt[:], AF.Square, acc=nb["ss"][:])
        self.rstd(nb["rs"][:], nb["ss"][:], 1.0 / D)
        self.ts("dve", nb["xn"][:], xt[:], nb["rs"][:, 0:1], None, op0=ALU.mult)
        pt = npt()
        for k in range(8):
            self.tr(pt[:, k * 128:(k + 1) * 128], nb["xn"][:, k * 128:(k + 1) * 128], ident[:])
        self.cp("act", hT[:].re("p k t -> p (k t)"), pt[:])

    def norm_bufs(self, s, tag):
        fw = self.fw
        xn = fw.sb(s, [128, D], BF16, "xn" + tag)
        return {"junk": xn, "ss": fw.sb(s, [128, 1], F32, "ss" + tag), "rs": fw.sb(s, [128, 1], F32, "rs" + tag), "xn": xn}

    def pass0_prompt(self, sw, xp, win_b, cmp_in, kcp, vcp, ident, nps, npt):
        fw, NT, T = self.fw, self.NT, self.T
        mm, act, tt, cp, ms, dma, dmas = self.mm, self.act, self.tt, self.cp, self.ms, self.dma, self.dmas
        NCB = T // 16
        with ExitStack() as s:
            srcT = {kv: fw.sb(s, [64, 2, 16, NCB + 1], BF16, "srcT" + kv) for kv in "kv"}
            for kv in "kv":
                ms("pool", srcT[kv][:, :, :, NCB:NCB + 1], 0.0)
            ms("pool", kcp[0:64, :, :], 0.0)
            xts = [fw.sb(s, [128, D], F32, f"x0_{i}") for i in range(2)]
            nb = self.norm_bufs(s, "0")
            hT = fw.sb(s, [128, 8, 128], BF16, "hT0")
            for t_ in range(NT):
                xt = xts[t_ % 2]
                self.norm_T(xp[t_ * 128:(t_ + 1) * 128, :], xt, nb, hT, ident, npt)
                ps = nps()
                for gi in range(4):
                    for k in range(8):
                        mm(ps[0:64, gi * 128:(gi + 1) * 128], win_b[:, k, 512 + 64 * gi:576 + 64 * gi], hT[:, k, :],
                           st=(k == 0), sp=(k == 7))
                for h in range(2):
                    cp("act", srcT["k"][:, h, :, 8 * t_:8 * t_ + 8], ps[0:64, h * 128:(h + 1) * 128].re("p (c j) -> p j c", j=16))
                    cp("dve", srcT["v"][:, h, :, 8 * t_:8 * t_ + 8], ps[0:64, 256 + h * 128:384 + h * 128].re("p (c j) -> p j c", j=16))
            w1s = [fw.sb(s, [64, 8, 256], F32, f"w1s{i}") for i in range(2)]
            for kv in "kv":
                pe, w1, b1, w2 = cmp_in[kv]
                w1b = fw.sb(s, [64, 32, 256], BF16, "w1b" + kv)
                for jb in range(4):
                    st = w1s[jb % 2]
                    dma(st[:], V(None, w1.a.rearrange("(j d) n -> d j n", d=64)[:, jb * 8:(jb + 1) * 8, :]))
                    cp("dve", w1b[:, jb * 8:(jb + 1) * 8, :], st[:])
                peT = fw.sb(s, [64, 32], F32, "peT" + kv)
                dmas(peT[:], V(None, pe.a.rearrange("j d -> d j")))
                peTb = fw.sb(s, [64, 32], BF16, "peTb" + kv)
                cp("dve", peTb[:], peT[:])
                b1c = fw.sb(s, [128, 2], F32, "b1c" + kv)
                dmas(b1c[:], V(None, b1.a.rearrange("(c p) -> p c", p=128)))
                w2s = fw.sb(s, [128, 2, 64], F32, "w2s" + kv)
                dma(w2s[:], V(None, w2.a.rearrange("(c p) n -> p c n", p=128)))
                w2b = fw.sb(s, [128, 2, 64], BF16, "w2b" + kv)
                cp("dve", w2b[:], w2s[:])
                cst = fw.sb(s, [128, 2], F32, "cst" + kv)
                for hc in range(2):
                    ps = nps()
                    for j in range(32):
                        mm(ps[:, 0:1], w1b[:, j, hc * 128:(hc + 1) * 128], peTb[:, j:j + 1], st=(j == 0), sp=(j == 31))
                    tt("dve", cst[:, hc:hc + 1], ps[:, 0:1], b1c[:, hc:hc + 1], ALU.add)
                gT = fw.sb(s, [128, 2, 256], BF16, "gT" + kv)
                if NCB < 256:
                    ms("pool", gT[:], 0.0)
                for kvh in range(2):
                    for hc in range(2):
                        ps = nps()
                        for j in range(32):
                            rv = srcT[kv][:, kvh, j, 0:NCB] if j < 16 else srcT[kv][:, kvh, j - 16, 1:NCB + 1]
                            mm(ps[:, 0:NCB], w1b[:, j, hc * 128:(hc + 1) * 128], rv, st=(j == 0), sp=(j == 31))
                        act(gT[:, hc, 0:NCB], ps[:, 0:NCB], AF.Gelu_apprx_tanh, bias=cst[:, hc:hc + 1])
                    if kv == "k":
                        ps = nps()
                        for hc in range(2):
                            mm(ps[0:64, 0:256], w2b[:, hc, :], gT[:, hc, :], st=(hc == 0), sp=(hc == 1))
                        cp("dve", kcp[0:64, kvh, :], ps[0:64, 0:256])
                    else:
                        for ct in range(2):
                            ps = nps()
                            for hc in range(2):
                                mm(ps[:, 0:64], gT[:, hc, ct * 128:(ct + 1) * 128], w2b[:, hc, :], st=(hc == 0), sp=(hc == 1))
                            cp("dve", vcp[:, ct, kvh, 0:64], ps[:, 0:64])
            fw.barrier()

    def pass1_prompt(self, sw, L):
        fw, NT, T = self.fw, self.NT, self.T
        mm, tr, act, ts, tt, stt, cp, ms, iota, dma, dmas = (self.mm, self.tr, self.act, self.ts, self.tt, self.stt,
                                                             self.cp, self.ms, self.iota, self.dma, self.dmas)
        xp, win_b, wout_b, wqm_b, wkm_b = L["xp"], L["win_b"], L["wout_b"], L["wqm_b"], L["wkm_b"]
        kcp, vcp, ident, identf, nps, npt, psa = L["kcp"], L["vcp"], L["ident"], L["identf"], L["nps"], L["npt"], L["psa"]
        caus_add, win_add, bd01, tri2, csel, e0, mimp, expand = (L["caus_add"], L["win_add"], L["bd01"], L["tri2"],
                                                                 L["csel"], L["e0"], L["mimp"], L["expand"])
        cw, cb, bgate, bif, put_row, x1d = L["cw"], L["cb"], L["bgate"], L["bif"], L["put_row"], L["x1d"]
        p_kv, p_kw, p_vw, p_C, p_n, p_m, p_conv = L["p_kv"], L["p_kw"], L["p_vw"], L["p_C"], L["p_n"], L["p_m"], L["p_conv"]
        with ExitStack() as s:
            ksT = fw.sb(s, [68, 2, T], BF16, "ksT")
            NW = min(8, NT)
            kwT = fw.sb(s, [68, 2, NW * 128], BF16, "kwT")
            phr = fw.sb(s, [1, 128], BF16, "phr")
            vsp = fw.sb(s, [128, NT, 2, 65], BF16, "vsp")
            vwp = fw.sb(s, [128, NW, 2, 65], BF16, "vwp")
            ms("pool", vsp[:], 1.0)
            ms("pool", vwp[:], 1.0)
            with ExitStack() as tmps:
                self.rowt = fw.sb(tmps, [1, 4096], F32, "rowt1")
                self.rowb = fw.sb(tmps, [1, 4096], BF16, "rowb1")
                for kvh in range(2):
                    put_row(ksT[64:65, kvh, :], [[128, NT], [0, 128]], 0, T)
                    put_row(ksT[65:66, kvh, :], [[0, NT], [1, 128]], 0, T)
                    put_row(ksT[66:67, kvh, :], None, 0, T, const=1.0)
                    put_row(ksT[67:68, kvh, :], None, 0, T, const=1.0)
                    put_row(kwT[65:66, kvh, :], [[0, NW], [1, 128]], 0, NW * 128)
                    put_row(kwT[66:67, kvh, :], None, 0, NW * 128, const=1.0)
                    put_row(kwT[67:68, kvh, :], None, 0, NW * 128, const=1.0)
                fw.barrier()
            qps = [fw.sb(s, [68, 2, 4, 128], BF16, f"qp{i}") for i in range(2)]
            srow = fw.sb(s, [1, 8, 128], F32, "srow")
            for h in range(8):
                ms("pool", srow[0:1, h, :], 2.0 ** (-(h + 1)))
            r67 = fw.sb(s, [1, 8, 128], F32, "r67")
            iota(r67[:], [[0, 8], [1, 128]], base=0, cm=0)
            tt("pool", r67[:], r67[:], srow[:], ALU.mult)
            ts("pool", r67[:], r67[:], -1.0, None, op0=ALU.mult)
            srb = fw.sb(s, [1, 8, 128], BF16, "srb")
            r67b = fw.sb(s, [1, 8, 128], BF16, "r67b")
            r66b = fw.sb(s, [1, 8, 128], BF16, "r66b")
            cp("pool", srb[:], srow[:])
            cp("pool", r67b[:], r67[:])
            for qp in qps:
                for kvh in range(2):
                    dma(qp[64:65, kvh], srb[0:1, 4 * kvh:4 * kvh + 4, :])
                    dma(qp[65:66, kvh], srb[0:1, 4 * kvh:4 * kvh + 4, :])
                    dma(qp[67:68, kvh], r67b[0:1, 4 * kvh:4 * kvh + 4, :])
            xts = [fw.sb(s, [128, D], F32, f"x1_{i}") for i in range(2)]
            nb = self.norm_bufs(s, "1")
            hT = fw.sb(s, [128, 8, 128], BF16, "hT1")
            pkv = fw.sb(s, [128, 792], F32, "pkv")
            gt = fw.sb(s, [128, 24], F32, "gt")
            pts = RR([fw.sb(s, [128, 512], BF16, f"pt{i}") for i in range(3)])
            mks = RR([fw.sb(s, [128, 512], BF16, f"mk{i}") for i in range(2)])
            obr = [[fw.sb(s, [128, 4, 65], F32, f"obr{k}{i}") for i in range(3)] for k in range(2)]
            rdc = fw.sb(s, [128, 4], F32, "rdc")
            imp4 = fw.sb(s, [128, 4, 64], F32, "imp4")
            imp = fw.sb(s, [128, 64], F32, "imp")
            imp2 = fw.sb(s, [128, 64], F32, "imp2")
            mx1 = fw.sb(s, [128, 8], F32, "mx1")
            mx2 = fw.sb(s, [128, 8], F32, "mx2")
            selm = fw.sb(s, [128, 64], BF16, "selm")
            selT = [fw.sb(s, [64, 4, 128], BF16, f"selT{k}") for k in range(2)]
            fpb = fw.sb(s, [128, 3], F32, "fpb")
            ms("pool", fpb[:], -1.0)
            rd = fw.sb(s, [128, 3, 4], F32, "rd")
            sc3 = fw.sb(s, [128, 3, 4], F32, "sc3")
            onsa = fw.sb(s, [128, 4, 64], F32, "onsa")
            otmp = fw.sb(s, [128, 4, 64], F32, "otmp")
            ss4 = fw.sb(s, [128, 4], F32, "ss4")
            rs4 = fw.sb(s, [128, 4], F32, "rs4")
            mixin = fw.sb(s, [128, D], BF16, "mixin")
            mT = fw.sb(s, [128, 8, 128], BF16, "mT")
            x1t = fw.sb(s, [128, D], F32, "x1t")
            gif = fw.sb(s, [128, 8], F32, "gif")
            l1 = fw.sb(s, [128, 4], F32, "l1")
            gsb = fw.sb(s, [128, 12], F32, "gsb")
            wl = fw.sb(s, [128, 4], F32, "wl")
            ul = fw.sb(s, [128, 4], F32, "ul")
            tmp4 = fw.sb(s, [128, 4], F32, "tmp4")
            dec = fw.sb(s, [128, 4], F32, "dec")
            ebt = fw.sb(s, [128, 8], F32, "ebt")
            vmu = fw.sb(s, [128, 4, 129], BF16, "vmu")
            sigo = fw.sb(s, [128, 512], F32, "sigo")
            xcv = [fw.sb(s, [128, 4, 131], F32, f"xcv{i}") for i in range(2)]
            ms("pool", xcv[0][:], 0.0)
            cacc = fw.sb(s, [128, 4, 128], F32, "cacc")
            xc = fw.sb(s, [128, 4, 128], BF16, "xc")
            qmT = fw.sb(s, [128, 4, 128], BF16, "qmT")
            qmS = [fw.sb(s, [128, 4, 128], BF16, f"qmS{i}") for i in range(2)]
            for q_ in qmS:
                ms("pool", q_[:], 0.0)
            kmT = fw.sb(s, [128, 4, 128], BF16, "kmT")
            kmS = [fw.sb(s, [128, 4, 128], BF16, f"kmS{i}") for i in range(2)]
            mqk = fw.sb(s, [128, 4, 128], BF16, "mqk")
            Sf = fw.sb(s, [128, 4, 129], F32, "Sf")
            ms("pool", Sf[:], 0.0)
            Sb = [fw.sb(s, [128, 4, 129], BF16, f"Sb{i}") for i in range(3)]
            ms("pool", Sb[0][:], 0.0)
            dS = fw.sb(s, [128, 4, 129], F32, "dS")
            dn = fw.sb(s, [128, 4], F32, "dn")
            hout = fw.sb(s, [128, 4, 128], F32, "hout")
            hsq = cacc
            m4 = fw.sb(s, [4, 8], F32, "m4")
            R = fw.sb(s, [4, 1], F32, "Rm")
            ms("pool", R[:], 0.0)
            tsb = fw.sb(s, [4, 384], F32, "tsb")
            segs = [(0, 64), (64, 128)]
            KSC = 128.0 ** -0.5

            for t_ in range(NT):
                xt = xts[t_ % 2]
                qp = qps[t_ % 2]
                ts("pool", r66b[:], srow[:], -128.0 * t_, None, op0=ALU.mult)
                for kvh in range(2):
                    dma(qp[66:67, kvh], r66b[0:1, 4 * kvh:4 * kvh + 4, :])
                self.norm_T(xp[t_ * 128:(t_ + 1) * 128, :], xt, nb, hT, ident, npt)
                psA = nps()
                psB = nps()
                for k in range(8):
                    mm(psA[:, 0:512], hT[:, k, :], win_b[:, k, 512:1024], st=(k == 0), sp=(k == 7))
                for k in range(8):
                    mm(psB[:, 0:280], hT[:, k, :], win_b[:, k, 1024:1304], st=(k == 0), sp=(k == 7))
                cp("dve", pkv[:, 0:512], psA[:, 0:512])
                cp("act", pkv[:, 512:792], psB[:, 0:280])
                r0 = t_ * 128
                for i_, n_ in enumerate(("p_kc", "p_vc", "p_ks", "p_vs")):
                    dma(p_kv[n_][r0:r0 + 128, :], pkv[:, i_ * 128:(i_ + 1) * 128])
                if r0 >= T - 512:
                    w0 = r0 - (T - min(512, T))
                    dma(p_kw[w0:w0 + 128, :], pkv[:, 512:640])
                    dma(p_vw[w0:w0 + 128, :], pkv[:, 640:768])
                cp("pool", vsp[:, t_, :, 0:64], pkv[:, 384:512].re("p (h d) -> p h d", h=2))
                ws = t_ % NW
                cp("pool", vwp[:, ws, :, 0:64], pkv[:, 640:768].re("p (h d) -> p h d", h=2))
                tt("dve", gt[:], pkv[:, 768:792], bgate[:], ALU.add)
                self.sigm(gt[:], gt[:])
                psQ0 = nps()
                psQ1 = nps()
                for h in range(8):
                    ps = psQ0 if h < 4 else psQ1
                    for k in range(8):
                        mm(ps[0:64, (h % 4) * 128:(h % 4 + 1) * 128], win_b[:, k, 64 * h:64 * h + 64], hT[:, k, :],
                           st=(k == 0), sp=(k == 7))
                act(qp[0:64, 0].re("p g t -> p (g t)"), psQ0[0:64, :], AF.Copy, scale=0.125)
                act(qp[0:64, 1].re("p g t -> p (g t)"), psQ1[0:64, :], AF.Copy, scale=0.125)
                psK = nps()
                for gi, c0 in enumerate((768, 832, 1024, 1088)):
                    for k in range(8):
                        mm(psK[0:64, gi * 128:(gi + 1) * 128], win_b[:, k, c0:c0 + 64], hT[:, k, :], st=(k == 0), sp=(k == 7))
                cp("dve", ksT[0:64, :, r0:r0 + 128], psK[0:64, 0:256].re("p (h t) -> p h t", h=2))
                cp("dve", kwT[0:64, :, ws * 128:(ws + 1) * 128], psK[0:64, 256:512].re("p (h t) -> p h t", h=2))
                ms("pool", phr[:], 128.0 * t_)
                for kvh in range(2):
                    dma(kwT[64:65, kvh, ws * 128:(ws + 1) * 128], phr[:])

                steps = []
                for kvh in range(2):
                    cts = [0] if t_ < 16 else [0, 1]
                    for ci, ct in enumerate(cts):
                        steps.append(("cmp", kvh, ct, ci == 0, ci == len(cts) - 1))
                for kvh in range(2):
                    k0 = max(0, t_ - 4)
                    for kt in range(k0, t_ + 1):
                        steps.append(("win", kvh, kt, kt == k0, kt == t_))
                for kvh in range(2):
                    for kt in range(t_ + 1):
                        steps.append(("sel", kvh, kt, kt == 0, kt == t_))
                pend = {}

                def score(i):
                    kind, kvh, k, first, last = steps[i]
                    qv = qp[:, kvh].re("p g t -> p (g t)")
                    S = nps()
                    if kind == "cmp":
                        Kq = 128 * t_ - 2048 * k - 31
                        need_mask = Kq < 2032
                        mm(S[:, :], kcp[:, kvh, k * 128:(k + 1) * 128], qv, st=True, sp=not need_mask)
                        if need_mask:
                            mk = mks()
                            ts("dve", mk[:], e0[:], float(Kq), NEG, op0=ALU.is_gt, op1=ALU.mult)
                            mm(S[:, :], ident[:], mk[:], st=False, sp=True)
                    elif kind == "win":
                        madd = caus_add if k == t_ else (win_add if k == t_ - 4 else None)
                        mm(S[:, :], kwT[:, kvh, (k % NW) * 128:(k % NW + 1) * 128], qv, st=True, sp=(madd is None))
                        if madd is not None:
                            mm(S[:, :], ident[:], madd[:], st=False, sp=True)
                    else:
                        mm(S[:, :], ksT[:, kvh, k * 128:(k + 1) * 128], qv, st=True, sp=False)
                        if k < t_:
                            mm(S[:, :], expand[:, k * 128:(k + 1) * 128], selT[kvh][:].re("p g t -> p (g t)"), st=False, sp=True)
                        else:
                            mm(S[:, :], ident[:], caus_add[:], st=False, sp=True)
                    pt = pts()
                    act(pt[:], S[:, :], AF.Exp)
                    pend[i] = pt

                ACC = {("cmp", 0): 0, ("cmp", 1): 2, ("win", 0): 0, ("win", 1): 2, ("sel", 0): 1, ("sel", 1): 0}

                def finish_cmp(kvh):
                    accC, impP = psa[ACC[("cmp", kvh)]], psa[1]
                    cp("dve", obr[kvh][0][:].re("p g d -> p (g d)"), accC[:, 0:260])
                    cp("act", imp4[:].re("p g j -> p (g j)"), impP[:, 0:256])
                    ts("dve", rdc[:], obr[kvh][0][:, :, 64], 1e-30, None, op0=ALU.max)
                    self.recip(rdc[:], rdc[:])
                    ts("dve", imp[:], imp4[:, 0, :], rdc[:, 0:1], None, op0=ALU.mult)
                    for g in range(1, 4):
                        stt("dve", imp[:], imp4[:, g, :], rdc[:, g:g + 1], imp[:], ALU.mult, ALU.add)
                    if t_ == 0:
                        tt("dve", imp[:, 0:2], imp[:, 0:2], fpb[:, 1:3], ALU.max)
                    else:
                        tt("dve", imp[:, 2 * t_ - 1:2 * t_ + 2], imp[:, 2 * t_ - 1:2 * t_ + 2], fpb[:, 0:3], ALU.max)
                        ms("dve", imp[:, 0:1], 3e9)
                    fw.op("dve", lambda e: e.max(out=mx1[:].a, in_=imp[:].a), reads=[imp], writes=[mx1])
                    fw.op("dve", lambda e: e.match_replace(out=imp2[:].a, in_to_replace=mx1[:].a, in_values=imp[:].a,
                                                           imm_value=-1e30), reads=[imp, mx1], writes=[imp2])
                    fw.op("dve", lambda e: e.max(out=mx2[:].a, in_=imp2[:].a), reads=[imp2], writes=[mx2])
                    ts("dve", selm[:], imp[:], mx2[:, 7:8], NEG, op0=ALU.is_lt, op1=ALU.mult)
                    ptr = npt()
                    tr(ptr[0:64, 0:128], selm[:], ident[:])
                    cp("dve", selT[kvh][:], ptr[0:64, 0:128].un(1).bc([64, 4, 128]))

                def pv(i):
                    kind, kvh, k, first, last = steps[i]
                    pt = pend.pop(i)
                    acc = psa[ACC[(kind, kvh)]]
                    if kind == "cmp":
                        vv = vcp[:, k, kvh, :]
                    elif kind == "win":
                        vv = vwp[:, k % NW, kvh, :]
                    else:
                        vv = vsp[:, k, kvh, :]
                    for g in range(4):
                        mm(acc[:, g * 65:(g + 1) * 65], pt[:, g * 128:(g + 1) * 128], vv, st=first, sp=last)
                        if kind == "cmp":
                            mm(psa[1][:, g * 64:(g + 1) * 64], pt[:, g * 128:(g + 1) * 128], mimp[:, k, :],
                               st=first, sp=last)
                    if last:
                        if kind == "cmp":
                            finish_cmp(kvh)
                        elif kind == "win":
                            cp("act", obr[kvh][2][:].re("p g d -> p (g d)"), acc[:, 0:260])
                        else:
                            cp("dve", obr[kvh][1][:].re("p g d -> p (g d)"), acc[:, 0:260])

                base = 1e9 + 1e6 * (2 * t_)
                ms("dve", fpb[0:64, 0:1], base - 1e6)
                ms("dve", fpb[0:64, 1:2], base)
                ms("dve", fpb[64:128, 1:2], base)
                ms("dve", fpb[64:128, 2:3], base + 1e6)
                score(0)
                for i in range(len(steps)):
                    if i + 1 < len(steps):
                        score(i + 1)
                    pv(i)
                for kvh in range(2):
                    for br in range(3):
                        ts("dve", rd[:, br, :], obr[kvh][br][:, :, 64], 1e-30, None, op0=ALU.max)
                    self.recip(rd[:].re("p b g -> p (b g)"), rd[:].re("p b g -> p (b g)"))
                    gv = gt[:, 12 * kvh:12 * kvh + 12].re("p (g b) -> p b g", b=3)
                    tt("dve", sc3[:], rd[:], gv, ALU.mult)
                    tt("dve", onsa[:], obr[kvh][0][:, :, 0:64], sc3[:, 0, :].un(2).bc([128, 4, 64]), ALU.mult)
                    for br in (1, 2):
                        tt("dve", otmp[:], obr[kvh][br][:, :, 0:64], sc3[:, br, :].un(2).bc([128, 4, 64]), ALU.mult)
                        tt("dve", onsa[:], onsa[:], otmp[:], ALU.add)
                    tt("dve", otmp[:], onsa[:], onsa[:], ALU.mult)
                    self.rsum(ss4[:], otmp[:])
                    self.rstd(rs4[:], ss4[:], 1.0 / 64)
                    tt("dve", mixin[:, 256 * kvh:256 * kvh + 256].re("p (g d) -> p g d", g=4), onsa[:],
                       rs4[:].un(2).bc([128, 4, 64]), ALU.mult)

                psV = nps()
                psO = nps()
                psG = nps()
                for k in range(8):
                    mm(psV[:, 0:512], hT[:, k, :], win_b[:, k, 1816:2328], st=(k == 0), sp=(k == 7))
                for k in range(8):
                    mm(psO[:, 0:512], hT[:, k, :], win_b[:, k, 2328:2840], st=(k == 0), sp=(k == 7))
                for k in range(8):
                    mm(psG[:, 0:8], hT[:, k, :], win_b[:, k, 2840:2848], st=(k == 0), sp=(k == 7))
                tt("dve", gif[:], psG[:, 0:8], bif[:], ALU.add)
                act(l1[:], gif[:, 4:8], AF.Exp, scale=-1.0)
                act(l1[:], l1[:], AF.Ln, bias=1.0)
                self.sigm(sigo[:], psO[:, 0:512])
                psC = nps()
                mm(psC[:, 0:4], tri2[:], l1[:])
                mm(psC[:, 4:8], csel[:, 0, :], l1[:])
                mm(psC[:, 8:12], csel[:, 1, :], l1[:])
                cp("dve", gsb[:], psC[:, 0:12])
                act(wl[:], gsb[:, 0:4], AF.Exp, scale=-1.0)
                tt("dve", tmp4[:], gif[:, 0:4], gsb[:, 0:4], ALU.add)
                act(ul[:], tmp4[:], AF.Exp)
                act(ebt[:], gsb[:, 4:12], AF.Exp, scale=-1.0)
                tt("dve", dec[0:64, :], tmp4[0:64, :], gsb[0:64, 4:8], ALU.subtract)
                tt("dve", dec[64:128, :], tmp4[64:128, :], gsb[64:128, 8:12], ALU.subtract)
                tt("dve", vmu[:, :, 0:128], psV[:, 0:512].re("p (h e) -> p h e", h=4), ul[:].un(2).bc([128, 4, 128]), ALU.mult)
                cp("dve", vmu[:, :, 128], ul[:])
                xcur, xnext = xcv[t_ % 2], xcv[(t_ + 1) % 2]
                psX = nps()
                for ch in range(4):
                    for k in range(8):
                        mm(psX[:, ch * 128:(ch + 1) * 128], win_b[:, k, 1304 + ch * 128:1432 + ch * 128], hT[:, k, :],
                           st=(k == 0), sp=(k == 7))
                cp("act", xcur[:, :, 3:131], psX[:, :].re("p (c t) -> p c t", c=4))
                cp("pool", xnext[:, :, 0:3], xcur[:, :, 128:131])
                for ch in range(4):
                    ts("dve", cacc[:, ch, :], xcur[:, ch, 0:128], cw[:, ch, 0:1], cb[:, ch:ch + 1], op0=ALU.mult, op1=ALU.add)
                    for j in range(1, 4):
                        stt("dve", cacc[:, ch, :], xcur[:, ch, j:j + 128], cw[:, ch, j:j + 1], cacc[:, ch, :], ALU.mult, ALU.add)
                self.sigm(hout[:], cacc[:])
                tt("dve", xc[:], cacc[:], hout[:], ALU.mult)
                if t_ == NT - 1:
                    for j in range(3):
                        dmas(V(None, p_conv.a[j].rearrange("(c p) -> p c", p=128)), xcur[:, :, 128 + j])
                psq = nps()
                for h in range(4):
                    mm(psq[:, h * 128:(h + 1) * 128], wqm_b[:, h, :], xc[:, h, :])
                cp("act", qmT[:].re("p h t -> p (h t)"), psq[:, :])
                for si, (a_, b_) in enumerate(segs):
                    cp("dve", qmS[si][:, :, a_:b_], psq[:, :].re("p (h t) -> p h t", h=4)[:, :, a_:b_])
                psk = nps()
                for h in range(4):
                    mm(psk[:, h * 128:(h + 1) * 128], wkm_b[:, h, :], xc[:, h, :])
                act(kmT[:].re("p h t -> p (h t)"), psk[:, :], AF.Copy, scale=KSC)
                pskt = nps()
                for h in range(4):
                    mm(pskt[:, h * 128:(h + 1) * 128], xc[:, h, :], wkm_b[:, h, :])
                for si in range(2):
                    ts("dve", kmS[si][:].re("p h t -> p (h t)"), pskt[:, :], csel[:, si, 0:1], KSC, op0=ALU.mult, op1=ALU.mult)
                psqk = nps()
                for h in range(4):
                    mm(psqk[:, h * 128:(h + 1) * 128], kmT[:, h, :], qmT[:, h, :])
                tt("dve", mqk[:], psqk[:, :].re("p (h t) -> p h t", h=4), bd01[:].un(1).bc([128, 4, 128]), ALU.mult)
                sbs = [Sb[(2 * t_) % 3], Sb[(2 * t_ + 1) % 3], Sb[(2 * t_ + 2) % 3]]
                for si in range(2):
                    pd = [nps(), nps()]
                    for h in range(4):
                        mm(pd[h // 2][:, (h % 2) * 129:(h % 2) * 129 + 129], kmS[si][:, h, :], vmu[:, h, :])
                    eb = ebt[:, 4 * si:4 * si + 4].un(2).bc([128, 4, 129])
                    tt("dve", Sf[:], Sf[:], eb, ALU.mult)
                    for hh in range(2):
                        tt("dve", dS[:, 2 * hh:2 * hh + 2, :], pd[hh][:, 0:258].re("p (h e) -> p h e", h=2),
                           ebt[:, 4 * si + 2 * hh:4 * si + 2 * hh + 2].un(2).bc([128, 2, 129]), ALU.mult)
                    tt("dve", Sf[:], Sf[:], dS[:], ALU.add)
                    cp("act", sbs[si + 1][:], Sf[:])
                pa = [nps(), nps()]
                for h in range(4):
                    o_ = pa[h // 2][:, (h % 2) * 129:(h % 2) * 129 + 129]
                    mm(o_, mqk[:, h, :], vmu[:, h, :], st=True, sp=False)
                    mm(o_, qmS[0][:, h, :], sbs[0][:, h, :], st=False, sp=False)
                    mm(o_, qmS[1][:, h, :], sbs[1][:, h, :], st=False, sp=True)
                for hh in range(2):
                    av = pa[hh][:, 0:258].re("p (h e) -> p h e", h=2)
                    tt("dve", dn[:, 2 * hh:2 * hh + 2], av[:, :, 128], wl[:, 2 * hh:2 * hh + 2], ALU.mult)
                stt("dve", tmp4[:], dn[:], -1.0, dn[:], ALU.mult, ALU.max)
                ts("dve", tmp4[:], tmp4[:], 1.0, None, op0=ALU.max)
                self.recip(tmp4[:], tmp4[:])
                tt("dve", tmp4[:], tmp4[:], wl[:], ALU.mult)
                for hh in range(2):
                    av = pa[hh][:, 0:258].re("p (h e) -> p h e", h=2)
                    tt("dve", hout[:, 2 * hh:2 * hh + 2, :], av[:, :, 0:128],
                       tmp4[:, 2 * hh:2 * hh + 2].un(2).bc([128, 2, 128]), ALU.mult)
                tt("dve", hsq[:], hout[:], hout[:], ALU.mult)
                self.rsum(ss4[:], hsq[:])
                self.rstd(rs4[:], ss4[:], 1.0 / 128)
                tt("dve", hout[:], hout[:], rs4[:].un(2).bc([128, 4, 128]), ALU.mult)
                tt("dve", mixin[:, 512:1024], hout[:].re("p h e -> p (h e)"), sigo[:], ALU.mult)
                psT = nps()
                tr(psT[0:4, 0:128], dec[:], identf[:])
                tr(psT[0:4, 128:256], gsb[:, 4:8], identf[:])
                tr(psT[0:4, 256:384], gsb[:, 8:12], identf[:])
                cp("dve", tsb[:], psT[0:4, 0:384])
                self.rmax(m4[:, 0:1], tsb[:, 0:64])
                self.rmax(m4[:, 1:2], tsb[:, 64:128])
                stt("dve", R[:], R[:], tsb[:, 128:129], m4[:, 0:1], ALU.subtract, ALU.max)
                stt("dve", R[:], R[:], tsb[:, 256:257], m4[:, 1:2], ALU.subtract, ALU.max)
                ptm = npt()
                for k in range(8):
                    tr(ptm[:, k * 128:(k + 1) * 128], mixin[:, k * 128:(k + 1) * 128], ident[:])
                cp("act", mT[:].re("p k t -> p (k t)"), ptm[:])
                for g in range(2):
                    ps = nps()
                    for k in range(8):
                        mm(ps[:, :], mT[:, k, :], wout_b[:, k, g * 512:(g + 1) * 512], st=(k == 0), sp=(k == 7))
                    tt("dve", x1t[:, g * 512:(g + 1) * 512], xt[:, g * 512:(g + 1) * 512], ps[:, :], ALU.add)
                dma(x1d[r0:r0 + 128, :], x1t[:])

            dma(V(None, p_m.a.rearrange("(h o) -> h o", o=1)), R[:])
            ones4 = fw.sb(s, [4, 128], F32, "ones4")
            ms("pool", ones4[:], 1.0)
            ts("dve", ones4[:], ones4[:], R[:, 0:1], None, op0=ALU.mult)
            ps = nps()
            mm(ps[:, 0:4], ones4[:], identf[0:4, 0:4])
            act(tmp4[:], ps[:, 0:4], AF.Exp, scale=-1.0)
            tt("dve", Sf[:], Sf[:], tmp4[:].un(2).bc([128, 4, 129]), ALU.mult)
            dmas(V(None, p_n.a.rearrange("h d -> d h")), Sf[:, :, 128])
            for h in range(4):
                ps = nps()
                tr(ps[:, 0:128], Sf[:, h, 0:128], identf[:])
                cp("dve", hsq[:, h, :], ps[:, 0:128])
                dma(p_C[h], hsq[:, h, :])
            fw.barrier()

    def load_w(self, dst, src, rows_chunks, ncols, stg, gcol=None, g0=0):
        for k in range(rows_chunks):
            st = stg[k % 2]
            self.dma(st[:, 0:ncols], src[k * 128:(k + 1) * 128, :])
            if k % 2 == 0:
                if gcol is None:
                    self.cp("dve", dst[:, k, :], st[:, 0:ncols])
                else:
                    self.ts("dve", dst[:, k, :], st[:, 0:ncols], gcol[:, g0 + k:g0 + k + 1], None, op0=ALU.mult)
            elif gcol is None:
                self.cp("act", dst[:, k, :], st[:, 0:ncols])
            else:
                self.act(dst[:, k, :], st[:, 0:ncols], AF.Copy, scale=gcol[:, g0 + k:g0 + k + 1])

    def pass2(self, top, L):
        fw, NT, T = self.fw, self.NT, self.T
        mm, tr, act, ts, tt, stt, cp, ms, dma, dmas = (self.mm, self.tr, self.act, self.ts, self.tt, self.stt,
                                                       self.cp, self.ms, self.dma, self.dmas)
        ident, nps, npt, psa = L["ident"], L["nps"], L["npt"], L["psa"]
        x1d, x2d, y_p = L["x1d"], L["x2d"], L["y_p"]
        g2 = fw.sb(top, [128, 32], F32, "g2")
        for i_, g_ in enumerate((L["g_xa"], L["g_mem"], L["g_ffn"])):
            dmas(g2[:, 8 * i_:8 * i_ + 8], V(None, g_.a.rearrange("(k p) -> p k", p=128)))
        with ExitStack() as s:
            wxq_b = fw.sb(s, [128, 8, D], BF16, "wxq_b")
            wxo_b = fw.sb(s, [128, 8, D], BF16, "wxo_b")
            mkT = fw.sb(s, [128, 8, 256], BF16, "mkT")
            mvp = fw.sb(s, [128, 2, 4, 257], BF16, "mvp")
            ms("pool", mvp[:], 1.0)
            xts = [fw.sb(s, [128, D], F32, f"x2_{i}") for i in range(2)]
            nb = self.norm_bufs(s, "2")
            hT = fw.sb(s, [128, 8, 128], BF16, "hT2")
            with ExitStack() as s2:
                stg = [fw.sb(s2, [128, D], F32, f"stg2{i}") for i in range(2)]
                wxk_b = fw.sb(s2, [128, 8, D], BF16, "wxk_b")
                wxv_b = fw.sb(s2, [128, 8, D], BF16, "wxv_b")
                self.load_w(wxq_b, L["w_xq"], 8, D, stg, g2, 0)
                self.load_w(wxo_b, L["w_xo"], 8, D, stg)
                self.load_w(wxk_b, L["w_xk"], 8, D, stg, g2, 8)
                self.load_w(wxv_b, L["w_xv"], 8, D, stg, g2, 8)
                mo = fw.sb(s2, [128, D], F32, "mo")
                for mt in range(2):
                    xt = xts[mt % 2]
                    self.norm_T(L["memp"][mt * 128:(mt + 1) * 128, :], xt, nb, hT, ident, npt)
                    for wi, (wb_, po) in enumerate(((wxk_b, L["p_mk"]), (wxv_b, L["p_mv"]))):
                        for g in range(2):
                            ps = nps()
                            for k in range(8):
                                mm(ps[:, :], hT[:, k, :], wb_[:, k, g * 512:(g + 1) * 512], st=(k == 0), sp=(k == 7))
                            cp("dve" if g == 0 else "act", mo[:, g * 512:(g + 1) * 512], ps[:, :])
                        dma(po[mt * 128:(mt + 1) * 128, :], mo[:])
                        if wi == 1:
                            cp("pool", mvp[:, mt, :, 0:256], mo[:].re("p (h d) -> p h d", h=4))
                    for c4 in range(2):
                        ps = nps()
                        for cc in range(4):
                            c = c4 * 4 + cc
                            for k in range(8):
                                mm(ps[:, cc * 128:(cc + 1) * 128], wxk_b[:, k, c * 128:(c + 1) * 128], hT[:, k, :],
                                   st=(k == 0), sp=(k == 7))
                        cp("dve", mkT[:, c4 * 4:c4 * 4 + 4, mt * 128:(mt + 1) * 128], ps[:, :].re("p (c t) -> p c t", c=4))
                fw.barrier()
            qxT = fw.sb(s, [128, 8, 128], BF16, "qxT")
            pts = [fw.sb(s, [128, 512], BF16, f"pxt{i}") for i in range(2)]
            ox = fw.sb(s, [128, D], BF16, "ox")
            oxT = fw.sb(s, [128, 8, 128], BF16, "oxT")
            rdx = fw.sb(s, [128, 1], F32, "rdx")
            x2t = fw.sb(s, [128, D], F32, "x2t")
            for t_ in range(NT):
                xt = xts[t_ % 2]
                r0 = t_ * 128
                self.norm_T(x1d[r0:r0 + 128, :], xt, nb, hT, ident, npt)
                for c4 in range(2):
                    ps = nps()
                    for cc in range(4):
                        c = c4 * 4 + cc
                        for k in range(8):
                            mm(ps[:, cc * 128:(cc + 1) * 128], wxq_b[:, k, c * 128:(c + 1) * 128], hT[:, k, :],
                               st=(k == 0), sp=(k == 7))
                    act(qxT[:, c4 * 4:c4 * 4 + 4, :].re("p c t -> p (c t)"), ps[:, :], AF.Copy, scale=1.0 / 16)
                for mt in range(2):
                    S = nps()
                    for h in range(4):
                        for hf in range(2):
                            mm(S[:, h * 128:(h + 1) * 128], mkT[:, 2 * h + hf, mt * 128:(mt + 1) * 128], qxT[:, 2 * h + hf, :],
                               st=(hf == 0), sp=(hf == 1))
                    act(pts[mt][:], S[:, :], AF.Exp)
                for h in range(4):
                    acc = psa[h % 2]
                    for mt in range(2):
                        mm(acc[:, 0:257], pts[mt][:, h * 128:(h + 1) * 128], mvp[:, mt, h, :], st=(mt == 0), sp=(mt == 1))
                    self.recip(rdx[:], acc[:, 256:257])
                    ts("dve", ox[:, h * 256:(h + 1) * 256], acc[:, 0:256], rdx[:, 0:1], None, op0=ALU.mult)
                pto = npt()
                for k in range(8):
                    tr(pto[:, k * 128:(k + 1) * 128], ox[:, k * 128:(k + 1) * 128], ident[:])
                cp("act", oxT[:].re("p k t -> p (k t)"), pto[:])
                for g in range(2):
                    ps = nps()
                    for k in range(8):
                        mm(ps[:, :], oxT[:, k, :], wxo_b[:, k, g * 512:(g + 1) * 512], st=(k == 0), sp=(k == 7))
                    tt("dve", x2t[:, g * 512:(g + 1) * 512], xt[:, g * 512:(g + 1) * 512], ps[:, :], ALU.add)
                dma(x2d[r0:r0 + 128, :], x2t[:])
            if self.sample:
                S = self.S
                xt = xts[0]
                self.norm_T(S["x1s"][:, :], xt, nb, hT, ident, npt, rows=16)
                qxs = fw.sb(s, [128, 8, 16], BF16, "qxs")
                ps = nps()
                for c in range(8):
                    for k in range(8):
                        mm(ps[:, c * 16:(c + 1) * 16], wxq_b[:, k, c * 128:(c + 1) * 128], hT[:, k, 0:16], st=(k == 0), sp=(k == 7))
                act(qxs[:].re("p c t -> p (c t)"), ps[:, 0:128], AF.Copy, scale=1.0 / 16)
                ms("pool", ox[:], 0.0)
                msg = fw.sb(s, [128, 2, D], F32, "msg")
                mkb = fw.sb(s, [128, 2, D], BF16, "mkb")
                ptx = [fw.sb(s, [128, 16], BF16, f"ptx{i}") for i in range(2)]
                oxb = fw.sb(s, [4, D], BF16, "oxb")
                for b in range(4):
                    dma(msg[:], V(None, S["cmk"].a[b].rearrange("(t p) f -> p t f", p=128)))
                    cp("dve", mkb[:, 0, :], msg[:, 0, :])
                    cp("pool", mkb[:, 1, :], msg[:, 1, :])
                    for mt in range(2):
                        pt = npt()
                        for c in range(8):
                            tr(pt[:, c * 128:(c + 1) * 128], mkb[:, mt, c * 128:(c + 1) * 128], ident[:])
                        cp("act", mkT[:, :, mt * 128:(mt + 1) * 128], pt[:, :].re("p (c t) -> p c t", c=8))
                    dma(msg[:], V(None, S["cmv"].a[b].rearrange("(t p) f -> p t f", p=128)))
                    for mt in range(2):
                        cp("dve" if mt == 0 else "pool", mvp[:, mt, :, 0:256], msg[:, mt, :].re("p (h d) -> p h d", h=4))
                    for mt in range(2):
                        Sx = nps()
                        for h in range(4):
                            for hf in range(2):
                                mm(Sx[:, h * 4:(h + 1) * 4], mkT[:, 2 * h + hf, mt * 128:(mt + 1) * 128],
                                   qxs[:, 2 * h + hf, 4 * b:4 * b + 4], st=(hf == 0), sp=(hf == 1))
                        act(ptx[mt][:], Sx[:, 0:16], AF.Exp)
                    for h in range(4):
                        acc = psa[h % 2]
                        for mt in range(2):
                            mm(acc[0:4, 0:257], ptx[mt][:, 4 * h:4 * h + 4], mvp[:, mt, h, :], st=(mt == 0), sp=(mt == 1))
                        self.recip(rdx[0:4, :], acc[0:4, 256:257])
                        ts("dve", oxb[:, h * 256:(h + 1) * 256], acc[0:4, 0:256], rdx[0:4, 0:1], None, op0=ALU.mult)
                    dma(ox[4 * b:4 * b + 4, :], oxb[:])
                pto = npt()
                for k in range(8):
                    tr(pto[:, k * 128:(k + 1) * 128], ox[:, k * 128:(k + 1) * 128], ident[:])
                cp("act", oxT[:].re("p k t -> p (k t)"), pto[:])
                for g in range(2):
                    ps = nps()
                    for k in range(8):
                        mm(ps[:, :], oxT[:, k, :], wxo_b[:, k, g * 512:(g + 1) * 512], st=(k == 0), sp=(k == 7))
                    tt("dve", x2t[:, g * 512:(g + 1) * 512], xt[:, g * 512:(g + 1) * 512], ps[:, :], ALU.add)
                dma(S["x2s"][:, :], x2t[0:16, :])
            fw.barrier()
        with ExitStack() as s:
            wg_b = fw.sb(s, [128, 8, DFF], BF16, "wg_b")
            wu_b = fw.sb(s, [128, 8, DFF], BF16, "wu_b")
            wd_b = fw.sb(s, [128, 22, D], BF16, "wd_b")
            gfin = fw.sb(s, [128, D], F32, "gfin")
            dma(gfin[:], V(None, L["g_final"].a.partition_broadcast(128)))
            with ExitStack() as s2:
                stg = [fw.sb(s2, [128, DFF], F32, f"stg3{i}") for i in range(2)]
                self.load_w(wg_b, L["w_gate"], 8, DFF, stg, g2, 16)
                self.load_w(wu_b, L["w_up"], 8, DFF, stg, g2, 16)
                self.load_w(wd_b, L["w_down"], 22, D, stg)
                fw.barrier()
            xts = [fw.sb(s, [128, D], F32, f"x3_{i}") for i in range(2)]
            nb = self.norm_bufs(s, "3")
            hT = fw.sb(s, [128, 8, 128], BF16, "hT3")
            aT = fw.sb(s, [128, 22, 128], BF16, "aT")
            sg = fw.sb(s, [128, 512], F32, "sg")
            x3t = fw.sb(s, [128, D], F32, "x3t")
            tiles = [(x2d[t_ * 128:(t_ + 1) * 128, :], y_p[t_ * 128:(t_ + 1) * 128, :], 128) for t_ in range(NT)]
            if self.sample:
                tiles.append((self.S["x2s"][:, :], self.S["y_s"], 16))
            for t_, (src_, dst_, rows_) in enumerate(tiles):
                xt = xts[t_ % 2]
                self.norm_T(src_, xt, nb, hT, ident, npt, rows=rows_)
                for c0 in range(0, 22, 4):
                    n = min(4, 22 - c0)
                    pg = nps()
                    pu = nps()
                    for cc in range(n):
                        c = c0 + cc
                        for k in range(8):
                            mm(pg[:, cc * 128:(cc + 1) * 128], wg_b[:, k, c * 128:(c + 1) * 128], hT[:, k, :], st=(k == 0), sp=(k == 7))
                        for k in range(8):
                            mm(pu[:, cc * 128:(cc + 1) * 128], wu_b[:, k, c * 128:(c + 1) * 128], hT[:, k, :], st=(k == 0), sp=(k == 7))
                    act(sg[:, 0:n * 128], pg[:, 0:n * 128], AF.Silu)
                    tt("dve", aT[:, c0:c0 + n, :].re("p c t -> p (c t)"), sg[:, 0:n * 128], pu[:, 0:n * 128], ALU.mult)
                for g in range(2):
                    ps = nps()
                    for c in range(22):
                        mm(ps[:, :], aT[:, c, :], wd_b[:, c, g * 512:(g + 1) * 512], st=(c == 0), sp=(c == 21))
                    tt("dve", x3t[:, g * 512:(g + 1) * 512], xt[:, g * 512:(g + 1) * 512], ps[:, :], ALU.add)
                ms("dve", nb["ss"][:], 0.0)
                act(nb["junk"][:], x3t[:], AF.Square, acc=nb["ss"][:])
                self.rstd(nb["rs"][:], nb["ss"][:], 1.0 / D)
                stt("dve", x3t[:], x3t[:], nb["rs"][:, 0:1], gfin[:], ALU.mult, ALU.mult)
                dma(dst_, x3t[0:rows_, :])
            fw.barrier()


def sample_io(self):
    din, dout = self.din, self.dout
    S = {"xs": din("xs", [16, D]), "ptab": din("ptab", [4, 128], I32)}
    for n in ("pool_kc", "pool_vc", "pool_ks", "pool_vs"):
        S[n] = din(n, [5120, 16384])
    S["stk"] = din("stk", [4, 512, 128])
    S["stv"] = din("stv", [4, 512, 128])
    S["sconv"] = din("sconv", [4, 3, 512])
    S["sC"] = din("sC", [4, 4, 128, 128])
    S["sn"] = din("sn", [4, 4, 128])
    S["sm"] = din("sm", [16])
    S["cmk"] = din("cmk", [4, 256, D])
    S["cmv"] = din("cmv", [4, 256, D])
    S["y_s"] = dout("y_s", [16, D])
    for n in ("s_kc", "s_vc", "s_ks", "s_vs"):
        S[n] = dout(n, [16, 128])
    S["s_kw"] = dout("s_kw", [4, 512, 128])
    S["s_vw"] = dout("s_vw", [4, 512, 128])
    S["s_C"] = dout("s_C", [4, 4, 128, 128])
    S["s_n"] = dout("s_n", [4, 4, 128])
    S["s_m"] = dout("s_m", [16])
    S["s_conv"] = dout("s_conv", [4, 3, 512])
    S["kcS_d"] = self.dscr("kcS_d", [4, 2, 64, 1024], BF16)
    S["vcS_d"] = self.dscr("vcS_d", [4, 128, 8, 2, 64], BF16)
    S["x1s"] = self.dscr("x1s", [16, D])
    S["x2s"] = self.dscr("x2s", [16, D])
    self.S = S
    return S


def load_cmp(self, s, kv, cmp_in, nps):
    fw = self.fw
    mm, tt, cp, dma, dmas = self.mm, self.tt, self.cp, self.dma, self.dmas
    pe, w1, b1, w2 = cmp_in[kv]
    w1b = fw.sb(s, [64, 32, 256], BF16, "Sw1b" + kv)
    with ExitStack() as t:
        w1s = [fw.sb(t, [64, 8, 256], F32, f"Sw1s{kv}{i}") for i in range(2)]
        for jb in range(4):
            st = w1s[jb % 2]
            dma(st[:], V(None, w1.a.rearrange("(j d) n -> d j n", d=64)[:, jb * 8:(jb + 1) * 8, :]))
            cp("dve", w1b[:, jb * 8:(jb + 1) * 8, :], st[:])
        fw.barrier()
    peT = fw.sb(s, [64, 32], F32, "SpeT" + kv)
    dmas(peT[:], V(None, pe.a.rearrange("j d -> d j")))
    peTb = fw.sb(s, [64, 32], BF16, "SpeTb" + kv)
    cp("dve", peTb[:], peT[:])
    b1c = fw.sb(s, [128, 2], F32, "Sb1c" + kv)
    dmas(b1c[:], V(None, b1.a.rearrange("(c p) -> p c", p=128)))
    w2s = fw.sb(s, [128, 2, 64], F32, "Sw2s" + kv)
    dma(w2s[:], V(None, w2.a.rearrange("(c p) n -> p c n", p=128)))
    w2b = fw.sb(s, [128, 2, 64], BF16, "Sw2b" + kv)
    cp("dve", w2b[:], w2s[:])
    cst = fw.sb(s, [128, 2], F32, "Scst" + kv)
    for hc in range(2):
        ps = nps()
        for j in range(32):
            mm(ps[:, 0:1], w1b[:, j, hc * 128:(hc + 1) * 128], peTb[:, j:j + 1], st=(j == 0), sp=(j == 31))
        tt("dve", cst[:, hc:hc + 1], ps[:, 0:1], b1c[:, hc:hc + 1], ALU.add)
    return w1b, w2b, cst


def gather(self, dst, pool, idx, r0):
    self.fw.dma("pool", dst, pool, extra_reads=[idx.b],
                fn=lambda e: e.indirect_dma_start(out=dst.a, out_offset=None, in_=pool.a,
                                                  in_offset=bass.IndirectOffsetOnAxis(ap=idx.a, axis=0),
                                                  element_offset=r0 * 128))


def sample_s0(self, L):
    fw, S = self.fw, self.S
    mm, tr, act, cp, ms, dma = self.mm, self.tr, self.act, self.cp, self.ms, self.dma
    ident, nps, npt, cmp_in = L["ident"], L["nps"], L["npt"], L["cmp_in"]
    with ExitStack() as s:
        idx = [fw.sb(s, [128, 1], I32, f"S0idx{b}") for b in range(4)]
        for b in range(4):
            dma(idx[b][:], V(None, S["ptab"].a[b].rearrange("(p o) -> p o", o=1)))
        cw_ = {kv: load_cmp(self, s, kv, cmp_in, nps) for kv in "kv"}
        srcS = fw.sb(s, [64, 2, 128, 129], BF16, "srcS")
        ms("pool", srcS[:, :, :, 128:129], 0.0)
        gch = fw.sb(s, [128, 4096], F32, "gch")
        gbf = fw.sb(s, [128, 32, 128], BF16, "gbf")
        gT = fw.sb(s, [128, 2, 1024], BF16, "SgT")
        ko = fw.sb(s, [64, 1024], BF16, "Sko")
        vo = fw.sb(s, [128, 8, 64], BF16, "Svo")
        for b in range(4):
            for kv in "kv":
                w1b, w2b, cst = cw_[kv]
                pool = S["pool_" + kv + "c"]
                for i in range(4):
                    gather(self, gch[:], pool, idx[b][:, :], 32 * i)
                    cp("dve", gbf[:, 0:16, :].re("p r f -> p (r f)"), gch[:, 0:2048])
                    cp("act", gbf[:, 16:32, :].re("p r f -> p (r f)"), gch[:, 2048:4096])
                    for kvh in range(2):
                        for g8 in range(4):
                            pt = npt()
                            for r8 in range(8):
                                tr(pt[0:64, r8 * 128:(r8 + 1) * 128], gbf[:, 8 * g8 + r8, kvh * 64:(kvh + 1) * 64], ident[:])
                            cp("act" if g8 % 2 == 0 else "dve", srcS[:, kvh, 32 * i + 8 * g8:32 * i + 8 * g8 + 8, 0:128],
                               pt[0:64, :].re("p (r t) -> p r t", r=8))
                for kvh in range(2):
                    for hc in range(2):
                        wv = lambda j: w1b[:, j, hc * 128:(hc + 1) * 128]
                        for bank in range(2):
                            ps = nps()
                            o4 = ps[:, :].re("p (a t) -> p a t", a=4)
                            for j in range(32):
                                if j < 16:
                                    r0 = 64 * bank + j
                                    mm(o4, wv(j), srcS[:, kvh, r0:r0 + 49:16, 0:128], st=(j == 0), sp=False)
                                elif bank == 0:
                                    r0 = 16 + (j - 16)
                                    mm(o4, wv(j), srcS[:, kvh, r0:r0 + 49:16, 0:128], st=False, sp=(j == 31))
                                else:
                                    r0 = 80 + (j - 16)
                                    mm(o4[:, 0:3, :], wv(j), srcS[:, kvh, r0:r0 + 33:16, 0:128], st=False, sp=False)
                                    mm(ps[:, 384:512], wv(j), srcS[:, kvh, j - 16, 1:129], st=False, sp=(j == 31))
                            act(gT[:, hc, bank * 512:(bank + 1) * 512], ps[:, :], AF.Gelu_apprx_tanh, bias=cst[:, hc:hc + 1])
                    if kv == "k":
                        for bank in range(2):
                            ps = nps()
                            for hc in range(2):
                                mm(ps[0:64, :], w2b[:, hc, :], gT[:, hc, bank * 512:(bank + 1) * 512], st=(hc == 0), sp=(hc == 1))
                            cp("dve", ko[:, bank * 512:(bank + 1) * 512], ps[0:64, :])
                        dma(S["kcS_d"][b, kvh], ko[:])
                    else:
                        ps = nps()
                        for rb in range(8):
                            for hc in range(2):
                                mm(ps[:, rb * 64:(rb + 1) * 64], gT[:, hc, rb * 128:(rb + 1) * 128], w2b[:, hc, :],
                                   st=(hc == 0), sp=(hc == 1))
                        cp("dve", vo[:].re("p r d -> p (r d)"), ps[:, :])
                        dma(S["vcS_d"][b][:, :, kvh, :], vo[:])
        fw.barrier()


Builder.sample_io = sample_io

def sample_pass1(self, L):
    fw, S = self.fw, self.S
    mm, tr, act, ts, tt, stt, cp, ms, iota, dma, dmas = (self.mm, self.tr, self.act, self.ts, self.tt, self.stt,
                                                         self.cp, self.ms, self.iota, self.dma, self.dmas)
    win_b, wout_b, wqm_b, wkm_b = L["win_b"], L["wout_b"], L["wqm_b"], L["wkm_b"]
    ident, identf, nps, npt, psa = L["ident"], L["identf"], L["nps"], L["npt"], L["psa"]
    cw, cb, bgate, bif, tmpf = L["cw"], L["cb"], L["bgate"], L["bif"], L["tmpf"]
    put_row = self.put_row
    KSC = 128.0 ** -0.5
    with ExitStack() as s:
        xt = fw.sb(s, [128, D], F32, "xS")
        nb = self.norm_bufs(s, "S")
        hT = fw.sb(s, [128, 8, 128], BF16, "hTS")
        pkv = fw.sb(s, [128, 792], F32, "pkvS")
        gt = fw.sb(s, [128, 24], F32, "gtS")
        vnS = fw.sb(s, [16, 2, 2, 65], BF16, "vnS")
        qTs = fw.sb(s, [64, 8, 16], BF16, "qTs")
        mixin = fw.sb(s, [128, D], BF16, "mixinS")
        s2 = ExitStack()
        QS = fw.sb(s2, [68, 4, 2, 16], BF16, "QS")
        kcSb = fw.sb(s2, [68, 2, 8, 128], BF16, "kcSb")
        vcSb = fw.sb(s2, [128, 8, 2, 65], BF16, "vcSb")
        ksS = fw.sb(s2, [68, 2, 32, 128], BF16, "ksS")
        kwS = fw.sb(s2, [68, 2, 4, 128], BF16, "kwS")
        KnS = fw.sb(s2, [68, 2, 2, 16], BF16, "KnS")
        ms("pool", vcSb[:], 1.0)
        with ExitStack() as tmps:
            self.rowt = fw.sb(tmps, [1, 4096], F32, "rowtS")
            self.rowb = fw.sb(tmps, [1, 4096], BF16, "rowbS")
            sr = fw.sb(tmps, [1, 2, 4, 4], F32, "srS")
            for h in range(8):
                ms("pool", sr[0:1, h // 4, h % 4, :], 2.0 ** (-(h + 1)))
            qi = fw.sb(tmps, [1, 2, 4, 4], F32, "qiS")
            iota(qi[:].re("p k g q -> p (k g) q"), [[0, 8], [1, 4]], base=0, cm=0)
            rw = fw.sb(tmps, [1, 4, 2, 16], F32, "rwS")
            rwb = fw.sb(tmps, [1, 4, 2, 16], BF16, "rwbS")
            srv = sr[:].re("p k g q -> p k (g q)")
            for row in range(4):
                for i in range(4):
                    if row == 0:
                        ts("pool", rw[0:1, i], srv, -1.0, None, op0=ALU.mult)
                    elif row == 1:
                        cp("pool", rw[0:1, i], srv)
                    elif row == 2:
                        tt("pool", rw[0:1, i], srv, qi[:].re("p k g q -> p k (g q)"), ALU.mult)
                        ts("pool", rw[0:1, i], rw[0:1, i], -1.0, None, op0=ALU.mult)
                    else:
                        ts("pool", rw[0:1, i], srv, 32.0 * i, None, op0=ALU.mult)
                cp("pool", rwb[:], rw[:])
                dma(QS[64 + row:65 + row], rwb[:])
            for kvh in range(2):
                put_row(kcSb[64:65, kvh].re("p r t -> p (r t)"), [[0, 8], [-128, 128]], 16384, 1024)
                put_row(kcSb[65:66, kvh].re("p r t -> p (r t)"), [[16, 8], [0, 128]], 31, 1024)
                put_row(kcSb[66:67, kvh].re("p r t -> p (r t)"), None, 0, 1024, const=1.0)
                put_row(kcSb[67:68, kvh].re("p r t -> p (r t)"), None, 0, 1024, const=0.0)
                put_row(ksS[64:65, kvh].re("p r t -> p (r t)"), [[0, 32], [-128, 128]], 16384, 4096)
                put_row(ksS[65:66, kvh].re("p r t -> p (r t)"), [[1, 32], [0, 128]], 0, 4096)
                put_row(ksS[66:67, kvh].re("p r t -> p (r t)"), None, 0, 4096, const=1.0)
                put_row(ksS[67:68, kvh].re("p r t -> p (r t)"), None, 0, 4096, const=1.0)
                put_row(kwS[64:65, kvh].re("p r t -> p (r t)"), [[-128, 4], [0, 128]], 512, 512)
                put_row(kwS[65:66, kvh].re("p r t -> p (r t)"), [[0, 4], [1, 128]], 0, 512)
                put_row(kwS[66:67, kvh].re("p r t -> p (r t)"), None, 0, 512, const=1.0)
                put_row(kwS[67:68, kvh].re("p r t -> p (r t)"), None, 0, 512, const=0.0)
                for sw_ in range(2):
                    put_row(KnS[64:65, sw_, kvh], None, 0, 16, const=0.0)
                    put_row(KnS[65:66, sw_, kvh], [[0, 4], [1, 4]], 0, 16)
                    put_row(KnS[66:67, sw_, kvh], None, 0, 16, const=1.0)
                    put_row(KnS[67:68, sw_, kvh], None, 0, 16, const=0.0)
            fw.barrier()
        maskC7 = fw.sb(s2, [128, 1], F32, "maskC7")
        iota(tmpf[:, 0:1], [[0, 1]], base=0, cm=1)
        ts("pool", maskC7[:], tmpf[:, 0:1], 127.0, None, op0=ALU.is_lt)
        winm0 = fw.sb(s2, [128, 4], BF16, "winm0")
        iota(tmpf[:, 0:4], [[-1, 4]], base=0, cm=1)
        ts("pool", winm0[:], tmpf[:, 0:4], 0.0, None, op0=ALU.is_gt)
        newm = fw.sb(s2, [16, 4, 4], BF16, "newm")
        for b in range(4):
            iota(tmpf[0:16, 0:4], [[-1, 4]], base=-4 * b, cm=1)
            ts("pool", tmpf[0:16, 4:8], tmpf[0:16, 0:4], 0.0, None, op0=ALU.is_le)
            iota(tmpf[0:16, 8:12], [[0, 4]], base=-4 * b, cm=1)
            ts("pool", tmpf[0:16, 8:12], tmpf[0:16, 8:12], 0.0, None, op0=ALU.is_ge)
            tt("pool", newm[:, b, :], tmpf[0:16, 4:8], tmpf[0:16, 8:12], ALU.mult)
        mimpS = fw.sb(s2, [128, 8, 256], BF16, "mimpS")
        mtmp = fw.sb(s2, [128, 3, 256], F32, "mtmp")
        for rb in range(8):
            iota(mtmp[:, 0, :], [[-4, 256]], base=rb - 1, cm=8)
            stt("dve", mtmp[:, 1, :], mtmp[:, 0, :], -1.0, mtmp[:, 0, :], ALU.mult, ALU.max)
            ts("pool", mtmp[:, 0, :], mtmp[:, 1, :], 2.0, 0.5, op0=ALU.is_le, op1=ALU.mult)
            ts("pool", mtmp[:, 2, :], mtmp[:, 1, :], 1.0, 0.5, op0=ALU.is_le, op1=ALU.mult)
            tt("pool", mimpS[:, rb, :], mtmp[:, 0, :], mtmp[:, 2, :], ALU.add)
        idx = [fw.sb(s2, [128, 1], I32, f"S1idx{b}") for b in range(4)]
        for b in range(4):
            dma(idx[b][:], V(None, S["ptab"].a[b].rearrange("(p o) -> p o", o=1)))

        self.norm_T(S["xs"], xt, nb, hT, ident, npt, rows=16)
        psA, psB = nps(), nps()
        for k in range(8):
            mm(psA[:, 0:512], hT[:, k, :], win_b[:, k, 512:1024], st=(k == 0), sp=(k == 7))
        for k in range(8):
            mm(psB[:, 0:280], hT[:, k, :], win_b[:, k, 1024:1304], st=(k == 0), sp=(k == 7))
        cp("dve", pkv[:, 0:512], psA[:, 0:512])
        cp("act", pkv[:, 512:792], psB[:, 0:280])
        for i_, n_ in enumerate(("s_kc", "s_vc", "s_ks", "s_vs")):
            dma(S[n_], pkv[0:16, i_ * 128:(i_ + 1) * 128])
        tt("dve", gt[:], pkv[:, 768:792], bgate[:], ALU.add)
        self.sigm(gt[:], gt[:])
        ms("pool", vnS[:], 1.0)
        cp("dve", vnS[:, 0, :, 0:64], pkv[0:16, 384:512].re("p (h d) -> p h d", h=2))
        cp("dve", vnS[:, 1, :, 0:64], pkv[0:16, 640:768].re("p (h d) -> p h d", h=2))
        psQ = nps()
        for h in range(8):
            for k in range(8):
                mm(psQ[0:64, h * 16:(h + 1) * 16], win_b[:, k, 64 * h:64 * h + 64], hT[:, k, 0:16], st=(k == 0), sp=(k == 7))
        act(qTs[:].re("p h t -> p (h t)"), psQ[0:64, 0:128], AF.Copy, scale=0.125)
        psK = nps()
        for gi, c0 in enumerate((768, 832, 1024, 1088)):
            for k in range(8):
                mm(psK[0:64, gi * 16:(gi + 1) * 16], win_b[:, k, c0:c0 + 64], hT[:, k, 0:16], st=(k == 0), sp=(k == 7))
        cp("dve", KnS[0:64].re("p a k t -> p (a k t)"), psK[0:64, 0:64])

        gk = fw.sb(s2, [128, 4096], F32, "gk")
        gv = fw.sb(s2, [128, 4096], F32, "gv")
        kb = fw.sb(s2, [128, 32, 128], BF16, "kbS")
        vbp = fw.sb(s2, [128, 32, 2, 65], BF16, "vbp")
        ms("pool", vbp[:], 1.0)
        vwS = fw.sb(s2, [128, 4, 2, 65], BF16, "vwS")
        ms("pool", vwS[:], 1.0)
        ptc = fw.sb(s2, [128, 8, 16], BF16, "ptc")
        ptsb = fw.sb(s2, [128, 32, 16], BF16, "ptsb")
        ptw = fw.sb(s2, [128, 4, 16], BF16, "ptw")
        ptn = fw.sb(s2, [16, 16], BF16, "ptn")
        maskEO = [fw.sb(s2, [128, 2, 4, 4], BF16, f"maskEO{k}") for k in range(2)]
        obr = [[fw.sb(s2, [4, 4, 65], F32, f"obrS{k}{i}") for i in range(3)] for k in range(2)]
        imp4 = fw.sb(s2, [4, 4, 256], F32, "imp4S")
        imp = fw.sb(s2, [4, 256], F32, "impS")
        imp2 = fw.sb(s2, [4, 256], F32, "imp2S")
        mx1 = fw.sb(s2, [4, 8], F32, "mx1S")
        mx2 = fw.sb(s2, [4, 8], F32, "mx2S")
        sel01 = fw.sb(s2, [4, 256], F32, "sel01")
        selT = fw.sb(s2, [128, 8], F32, "selTS")
        gtb = fw.sb(s2, [4, 24], F32, "gtb")
        rd = fw.sb(s2, [4, 3, 4], F32, "rdS")
        sc3 = fw.sb(s2, [4, 3, 4], F32, "sc3S")
        onsa = fw.sb(s2, [4, 4, 64], F32, "onsaS")
        otmp = fw.sb(s2, [4, 4, 64], F32, "otmpS")
        ss4 = fw.sb(s2, [4, 4], F32, "ss4S")
        rs4 = fw.sb(s2, [4, 4], F32, "rs4S")
        onb = fw.sb(s2, [4, 512], BF16, "onbS")
        ms("pool", mixin[:], 0.0)
        for b in range(4):
            cp("dve", QS[0:64].re("p i k (g q) -> p i (k g) q", g=4),
               qTs[:, :, 4 * b:4 * b + 4].un(1).bc([64, 4, 8, 4]))
            dma(kcSb[0:64].re("p k r t -> p k (r t)"), V(S["kcS_d"], S["kcS_d"][b].a.rearrange("k d n -> d k n")))
            dma(vcSb[:].re("p r k d -> p (r k) d")[:, :, 0:64], V(S["vcS_d"], S["vcS_d"][b].a.rearrange("p r k d -> p (r k) d")))
            dma(gtb[:], gt[4 * b:4 * b + 4, :])
            for kvh in range(2):
                Sc = nps()
                for rb in range(8):
                    mm(Sc[:, rb * 16:(rb + 1) * 16], kcSb[:, kvh, rb, :], QS[:, 0, kvh, :])
                act(ptc[:].re("p r c -> p (r c)"), Sc[:, 0:128], AF.Exp)
                ts("dve", ptc[:, 7, :], ptc[:, 7, :], maskC7[:, 0:1], None, op0=ALU.mult)
                accC = psa[0]
                impP = [nps(), nps()]
                for rb in range(8):
                    for g in range(4):
                        mm(accC[0:4, g * 65:(g + 1) * 65], ptc[:, rb, 4 * g:4 * g + 4], vcSb[:, rb, kvh, :],
                           st=(rb == 0), sp=(rb == 7))
                        mm(impP[g // 2][0:4, (g % 2) * 256:(g % 2) * 256 + 256], ptc[:, rb, 4 * g:4 * g + 4],
                           mimpS[:, rb, :], st=(rb == 0), sp=(rb == 7))
                cp("dve", obr[kvh][0][:].re("p g d -> p (g d)"), accC[0:4, 0:260])
                for hh in range(2):
                    cp("act", imp4[:, 2 * hh:2 * hh + 2, :].re("p g j -> p (g j)"), impP[hh][0:4, 0:512])
                ts("dve", rd[:, 0, :], obr[kvh][0][:, :, 64], 1e-30, None, op0=ALU.max)
                self.recip(rd[:, 0, :], rd[:, 0, :])
                ts("dve", imp[:], imp4[:, 0, :], rd[:, 0, 0:1], None, op0=ALU.mult)
                for g in range(1, 4):
                    stt("dve", imp[:], imp4[:, g, :], rd[:, 0, g:g + 1], imp[:], ALU.mult, ALU.add)
                ms("dve", imp[:, 0:1], 3e9)
                ms("dve", imp[:, 255:256], 1e9)
                fw.op("dve", lambda e: e.max(out=mx1[:].a, in_=imp[:].a), reads=[imp], writes=[mx1])
                fw.op("dve", lambda e: e.match_replace(out=imp2[:].a, in_to_replace=mx1[:].a, in_values=imp[:].a,
                                                       imm_value=-1e30), reads=[imp, mx1], writes=[imp2])
                fw.op("dve", lambda e: e.max(out=mx2[:].a, in_=imp2[:].a), reads=[imp2], writes=[mx2])
                ts("dve", sel01[:], imp[:], mx2[:, 6:7], None, op0=ALU.is_ge)
                psT = nps()
                tr(psT[:, 0:4], sel01[0:4, 0:256:2], identf[0:4, 0:4])
                tr(psT[:, 4:8], sel01[0:4, 1:256:2], identf[0:4, 0:4])
                cp("dve", selT[:], psT[:, 0:8])
                cp("dve", maskEO[kvh][:], selT[:].re("p (e q) -> p e q", e=2).un(2).bc([128, 2, 4, 4]))
            accS = [psa[0], psa[1]]
            for i in range(4):
                gather(self, gk[:], S["pool_ks"], idx[b][:, :], 32 * i)
                gather(self, gv[:], S["pool_vs"], idx[b][:, :], 32 * i)
                cp("dve", kb[:, 0:16, :].re("p r f -> p (r f)"), gk[:, 0:2048])
                cp("act", kb[:, 16:32, :].re("p r f -> p (r f)"), gk[:, 2048:4096])
                cp("dve", vbp[:, 0:16, :, 0:64], gv[:, 0:2048].re("p (r h d) -> p r h d", r=16, h=2))
                cp("act", vbp[:, 16:32, :, 0:64], gv[:, 2048:4096].re("p (r h d) -> p r h d", r=16, h=2))
                for kvh in range(2):
                    for g8 in range(4):
                        pt = npt()
                        for r8 in range(8):
                            tr(pt[0:64, r8 * 128:(r8 + 1) * 128], kb[:, 8 * g8 + r8, kvh * 64:(kvh + 1) * 64], ident[:])
                        cp("act" if g8 % 2 == 0 else "dve", ksS[0:64, kvh, 8 * g8:8 * g8 + 8, :],
                           pt[0:64, :].re("p (r t) -> p r t", r=8))
                for kvh in range(2):
                    Ss = nps()
                    for r_ in range(32):
                        mm(Ss[:, r_ * 16:(r_ + 1) * 16], ksS[:, kvh, r_, :], QS[:, i, kvh, :])
                    act(ptsb[:].re("p r c -> p (r c)"), Ss[:, :], AF.Exp)
                    tt("dve", ptsb[:].re("p r (g q) -> p r g q", g=4), ptsb[:].re("p r (g q) -> p r g q", g=4),
                       maskEO[kvh][:, i // 2].un(1).bc([128, 32, 4, 4]), ALU.mult)
                    for r_ in range(32):
                        for g in range(4):
                            mm(accS[kvh][0:4, g * 65:(g + 1) * 65], ptsb[:, r_, 4 * g:4 * g + 4], vbp[:, r_, kvh, :],
                               st=(i == 0 and r_ == 0), sp=False)
            for kvh in range(2):
                Sn = nps()
                mm(Sn[0:16, 0:16], KnS[:, 0, kvh, :], QS[:, 0, kvh, :])
                act(ptn[:], Sn[0:16, 0:16], AF.Exp)
                tt("dve", ptn[:].re("p (g q) -> p g q", g=4), ptn[:].re("p (g q) -> p g q", g=4),
                   newm[:, b, :].un(1).bc([16, 4, 4]), ALU.mult)
                for g in range(4):
                    mm(accS[kvh][0:4, g * 65:(g + 1) * 65], ptn[:, 4 * g:4 * g + 4], vnS[:, 0, kvh, :], st=False, sp=True)
                cp("dve", obr[kvh][1][:].re("p g d -> p (g d)"), accS[kvh][0:4, 0:260])
            dma(gk[:, 0:512].re("p (a f) -> p a f", a=4), V(None, S["stk"].a[b].rearrange("(a p) f -> p a f", p=128)))
            dma(gv[:, 0:512].re("p (a f) -> p a f", a=4), V(None, S["stv"].a[b].rearrange("(a p) f -> p a f", p=128)))
            cp("dve", kb[:, 0:4, :].re("p r f -> p (r f)"), gk[:, 0:512])
            cp("pool", vwS[:, :, :, 0:64], gv[:, 0:512].re("p (a h d) -> p a h d", a=4, h=2))
            pt = npt()
            for kvh in range(2):
                for a in range(4):
                    tr(pt[0:64, (kvh * 4 + a) * 128:(kvh * 4 + a + 1) * 128], kb[:, a, kvh * 64:(kvh + 1) * 64], ident[:])
            cp("act", kwS[0:64].re("p k a t -> p (k a t)"), pt[0:64, :])
            accW = [psa[0], psa[1]]
            for kvh in range(2):
                Sw = nps()
                for a in range(4):
                    mm(Sw[:, a * 16:(a + 1) * 16], kwS[:, kvh, a, :], QS[:, 0, kvh, :])
                act(ptw[:].re("p a c -> p (a c)"), Sw[:, 0:64], AF.Exp)
                tt("dve", ptw[:, 0, :].re("p (g q) -> p g q", g=4), ptw[:, 0, :].re("p (g q) -> p g q", g=4),
                   winm0[:].un(1).bc([128, 4, 4]), ALU.mult)
                for a in range(4):
                    for g in range(4):
                        mm(accW[kvh][0:4, g * 65:(g + 1) * 65], ptw[:, a, 4 * g:4 * g + 4], vwS[:, a, kvh, :],
                           st=(a == 0), sp=False)
                Sn = nps()
                mm(Sn[0:16, 0:16], KnS[:, 1, kvh, :], QS[:, 0, kvh, :])
                act(ptn[:], Sn[0:16, 0:16], AF.Exp)
                tt("dve", ptn[:].re("p (g q) -> p g q", g=4), ptn[:].re("p (g q) -> p g q", g=4),
                   newm[:, b, :].un(1).bc([16, 4, 4]), ALU.mult)
                for g in range(4):
                    mm(accW[kvh][0:4, g * 65:(g + 1) * 65], ptn[:, 4 * g:4 * g + 4], vnS[:, 1, kvh, :], st=False, sp=True)
                cp("dve", obr[kvh][2][:].re("p g d -> p (g d)"), accW[kvh][0:4, 0:260])
            for nm_, src_, c0 in (("s_kw", "stk", 512), ("s_vw", "stv", 640)):
                dma(V(None, S[nm_].a[b, 0:508, :]), V(None, S[src_].a[b, 4:512, :]))
                dma(V(None, S[nm_].a[b, 508:512, :]), pkv[4 * b:4 * b + 4, c0:c0 + 128])
            for kvh in range(2):
                for br in range(1, 3):
                    ts("dve", rd[:, br, :], obr[kvh][br][:, :, 64], 1e-30, None, op0=ALU.max)
                    self.recip(rd[:, br, :], rd[:, br, :])
                ts("dve", rd[:, 0, :], obr[kvh][0][:, :, 64], 1e-30, None, op0=ALU.max)
                self.recip(rd[:, 0, :], rd[:, 0, :])
                gvw = gtb[:, 12 * kvh:12 * kvh + 12].re("p (g b) -> p b g", b=3)
                tt("dve", sc3[:], rd[:], gvw, ALU.mult)
                tt("dve", onsa[:], obr[kvh][0][:, :, 0:64], sc3[:, 0, :].un(2).bc([4, 4, 64]), ALU.mult)
                for br in (1, 2):
                    tt("dve", otmp[:], obr[kvh][br][:, :, 0:64], sc3[:, br, :].un(2).bc([4, 4, 64]), ALU.mult)
                    tt("dve", onsa[:], onsa[:], otmp[:], ALU.add)
                tt("dve", otmp[:], onsa[:], onsa[:], ALU.mult)
                self.rsum(ss4[:], otmp[:])
                self.rstd(rs4[:], ss4[:], 1.0 / 64)
                tt("dve", onb[:, 256 * kvh:256 * kvh + 256].re("p (g d) -> p g d", g=4), onsa[:],
                   rs4[:].un(2).bc([4, 4, 64]), ALU.mult)
            dma(mixin[4 * b:4 * b + 4, 0:512], onb[:])
        fw.barrier()
        s2.close()
        self.sample_mlstm(s, L, hT, mixin)
        mT = fw.sb(s, [128, 8, 128], BF16, "mTS")
        x1t = fw.sb(s, [128, D], F32, "x1tS")
        ptm = npt()
        for k in range(8):
            tr(ptm[:, k * 128:(k + 1) * 128], mixin[:, k * 128:(k + 1) * 128], ident[:])
        cp("act", mT[:].re("p k t -> p (k t)"), ptm[:])
        for g in range(2):
            ps = nps()
            for k in range(8):
                mm(ps[:, :], mT[:, k, :], wout_b[:, k, g * 512:(g + 1) * 512], st=(k == 0), sp=(k == 7))
            tt("dve", x1t[:, g * 512:(g + 1) * 512], xt[:, g * 512:(g + 1) * 512], ps[:, :], ALU.add)
        dma(S["x1s"][:, :], x1t[0:16, :])
        fw.barrier()


Builder.sample_pass1 = sample_pass1


def sample_mlstm(self, s, L, hT, mixin):
    fw, S = self.fw, self.S
    mm, tr, act, ts, tt, stt, cp, ms, iota, dma, dmas = (self.mm, self.tr, self.act, self.ts, self.tt, self.stt,
                                                         self.cp, self.ms, self.iota, self.dma, self.dmas)
    win_b, wqm_b, wkm_b = L["win_b"], L["wqm_b"], L["wkm_b"]
    identf, nps, cw, cb, bif, tmpf = L["identf"], L["nps"], L["cw"], L["cb"], L["bif"], L["tmpf"]
    KSC = 128.0 ** -0.5
    E = fw.sb(s, [4, 128], F32, "E4")
    iota(E[:], [[1, 128]], base=0, cm=-4)
    Eb = fw.sb(s, [4, 128], F32, "E4b")
    ts("pool", Eb[:], E[:], 0.0, None, op0=ALU.is_ge)
    ts("pool", E[:], E[:], 3.0, None, op0=ALU.is_le)
    tt("pool", E[:], E[:], Eb[:], ALU.mult)
    triS = fw.sb(s, [128, 128], F32, "triS")
    ps = nps()
    mm(ps[:, 0:128], E[:], E[:])
    tt("dve", triS[:], ps[:, 0:128], L["tri_le"][:], ALU.mult)
    bdS = fw.sb(s, [16, 16], BF16, "bdS")
    cp("dve", bdS[:], triS[0:16, 0:16])
    cselS = fw.sb(s, [128, 4, 128], F32, "cselS")
    d4 = fw.sb(s, [4, 4, 128], F32, "d4")
    cp("dve", d4[:], identf[0:4, 0:4].un(2).bc([4, 4, 128]))
    ps = nps()
    for b in range(4):
        mm(ps[:, b * 128:(b + 1) * 128], E[:], d4[:, b, :])
    cp("dve", cselS[:].re("p b m -> p (b m)"), ps[:, :])
    psV, psO, psG = nps(), nps(), nps()
    for k in range(8):
        mm(psV[:, 0:512], hT[:, k, :], win_b[:, k, 1816:2328], st=(k == 0), sp=(k == 7))
    for k in range(8):
        mm(psO[:, 0:512], hT[:, k, :], win_b[:, k, 2328:2840], st=(k == 0), sp=(k == 7))
    for k in range(8):
        mm(psG[:, 0:8], hT[:, k, :], win_b[:, k, 2840:2848], st=(k == 0), sp=(k == 7))
    gif = fw.sb(s, [128, 8], F32, "gifS")
    l1 = fw.sb(s, [128, 4], F32, "l1S")
    sigo = fw.sb(s, [128, 512], F32, "sigoS")
    tt("dve", gif[:], psG[:, 0:8], bif[:], ALU.add)
    act(l1[:], gif[:, 4:8], AF.Exp, scale=-1.0)
    act(l1[:], l1[:], AF.Ln, bias=1.0)
    self.sigm(sigo[:], psO[:, 0:512])
    psC = nps()
    mm(psC[:, 0:4], triS[:], l1[:])
    for b in range(4):
        mm(psC[:, 4 + 4 * b:8 + 4 * b], cselS[:, b, :], l1[:])
    gsb = fw.sb(s, [128, 20], F32, "gsbS")
    cp("dve", gsb[:], psC[:, 0:20])
    wl = fw.sb(s, [128, 4], F32, "wlS")
    ul = fw.sb(s, [128, 4], F32, "ulS")
    tmp4 = fw.sb(s, [128, 4], F32, "tmp4S")
    own = fw.sb(s, [128, 4], F32, "ownS")
    dec = fw.sb(s, [128, 4], F32, "decS")
    ebt = fw.sb(s, [128, 16], F32, "ebtS")
    act(wl[:], gsb[:, 0:4], AF.Exp, scale=-1.0)
    tt("dve", tmp4[:], gif[:, 0:4], gsb[:, 0:4], ALU.add)
    act(ul[:], tmp4[:], AF.Exp)
    act(ebt[:], gsb[:, 4:20], AF.Exp, scale=-1.0)
    ts("dve", own[:], gsb[:, 4:8], cselS[:, 0, 0:1], None, op0=ALU.mult)
    for b in range(1, 4):
        stt("dve", own[:], gsb[:, 4 + 4 * b:8 + 4 * b], cselS[:, b, 0:1], own[:], ALU.mult, ALU.add)
    tt("dve", dec[:], tmp4[:], own[:], ALU.subtract)
    vmu = fw.sb(s, [128, 4, 129], BF16, "vmuS")
    tt("dve", vmu[:, :, 0:128], psV[:, 0:512].re("p (h e) -> p h e", h=4), ul[:].un(2).bc([128, 4, 128]), ALU.mult)
    cp("dve", vmu[:, :, 128], ul[:])
    psT = nps()
    tr(psT[0:4, 0:128], dec[:], identf[:])
    mm(psT[0:4, 128:132], l1[:], cselS[:, :, 0])
    tsb = fw.sb(s, [4, 132], F32, "tsbS")
    cp("dve", tsb[:], psT[0:4, 0:132])
    Dm = fw.sb(s, [4, 4], F32, "DmS")
    self.rmax(Dm[:], tsb[:, 0:16].re("p (b i) -> p b i", b=4))
    R = fw.sb(s, [4, 4], F32, "RS")
    dmas(R[:], V(None, S["sm"].a.rearrange("(b h) -> h b", h=4)))
    tt("dve", R[:], R[:], tsb[:, 128:132], ALU.subtract)
    tt("dve", R[:], R[:], Dm[:], ALU.max)
    dmas(V(None, S["s_m"].a.rearrange("(b h) -> h b", h=4)), R[:])
    xcv = fw.sb(s, [128, 4, 4, 7], F32, "xcvS")
    for b in range(4):
        for ch in range(4):
            dmas(xcv[:, ch, b, 0:3], V(None, S["sconv"].a[b, :, ch * 128:(ch + 1) * 128].rearrange("j p -> p j")))
    psX = nps()
    for ch in range(4):
        for k in range(8):
            mm(psX[:, ch * 16:(ch + 1) * 16], win_b[:, k, 1304 + ch * 128:1432 + ch * 128], hT[:, k, 0:16],
               st=(k == 0), sp=(k == 7))
    cp("act", xcv[:, :, :, 3:7], psX[:, 0:64].re("p (c b i) -> p c b i", c=4, b=4))
    cacc = fw.sb(s, [128, 4, 16], F32, "caccS")
    for ch in range(4):
        cv = cacc[:, ch, :].re("p (b i) -> p b i", b=4)
        ts("dve", cv, xcv[:, ch, :, 0:4], cw[:, ch, 0:1], cb[:, ch:ch + 1], op0=ALU.mult, op1=ALU.add)
        for j in range(1, 4):
            stt("dve", cv, xcv[:, ch, :, j:j + 4], cw[:, ch, j:j + 1], cv, ALU.mult, ALU.add)
    xc = fw.sb(s, [128, 4, 16], BF16, "xcS")
    sgc = fw.sb(s, [128, 4, 16], F32, "sgcS")
    self.sigm(sgc[:], cacc[:])
    tt("dve", xc[:], cacc[:], sgc[:], ALU.mult)
    for b in range(4):
        for j in range(3):
            dmas(V(None, S["s_conv"].a[b, j].rearrange("(c p) -> p c", p=128)), xcv[:, :, b, 4 + j])
    qmT = fw.sb(s, [128, 4, 16], BF16, "qmTS")
    kmT = fw.sb(s, [128, 4, 16], BF16, "kmTS")
    qmS = [fw.sb(s, [128, 4, 16], BF16, f"qmSS{b}") for b in range(4)]
    kmS = [fw.sb(s, [16, 4, 128], BF16, f"kmSS{b}") for b in range(4)]
    psq = nps()
    for h in range(4):
        mm(psq[:, h * 16:(h + 1) * 16], wqm_b[:, h, :], xc[:, h, :])
    cp("act", qmT[:].re("p h t -> p (h t)"), psq[:, 0:64])
    for b in range(4):
        ms("pool", qmS[b][:], 0.0)
        cp("dve", qmS[b][:, :, 4 * b:4 * b + 4], psq[:, 0:64].re("p (h t) -> p h t", h=4)[:, :, 4 * b:4 * b + 4])
    psk = nps()
    for h in range(4):
        mm(psk[:, h * 16:(h + 1) * 16], wkm_b[:, h, :], xc[:, h, :])
    act(kmT[:].re("p h t -> p (h t)"), psk[:, 0:64], AF.Copy, scale=KSC)
    pskt = nps()
    for h in range(4):
        mm(pskt[0:16, h * 128:(h + 1) * 128], xc[:, h, :], wkm_b[:, h, :])
    for b in range(4):
        ts("dve", kmS[b][:].re("p h t -> p (h t)"), pskt[0:16, :], cselS[0:16, b, 0:1], KSC, op0=ALU.mult, op1=ALU.mult)
    psqk = nps()
    for h in range(4):
        mm(psqk[0:16, h * 16:(h + 1) * 16], kmT[:, h, :], qmT[:, h, :])
    mqk = fw.sb(s, [16, 4, 16], BF16, "mqkS")
    tt("dve", mqk[:], psqk[0:16, 0:64].re("p (h t) -> p h t", h=4), bdS[:].un(1).bc([16, 4, 16]), ALU.mult)
    em0 = fw.sb(s, [128, 16], F32, "em0")
    dma(em0[:], V(None, S["sm"].a.partition_broadcast(128)))
    act(em0[:], em0[:], AF.Exp)
    Sf = [fw.sb(s, [128, 4, 129], F32, f"SfS{b}") for b in range(4)]
    Sb0 = [fw.sb(s, [128, 4, 129], BF16, f"Sb0S{b}") for b in range(4)]
    cst_ = [fw.sb(s, [128, 128], F32, f"c0st{i}") for i in range(2)]
    dS = fw.sb(s, [128, 4, 129], F32, "dSS")
    for b in range(4):
        for h in range(4):
            st = cst_[h % 2]
            dma(st[:], V(None, S["sC"].a[b, h]))
            ps = nps()
            tr(ps[:, 0:128], st[:], identf[:])
            cp("dve", Sf[b][:, h, 0:128], ps[:, 0:128])
        dmas(Sf[b][:, :, 128], V(None, S["sn"].a[b].rearrange("h d -> d h")))
        tt("dve", Sf[b][:], Sf[b][:], em0[:, 4 * b:4 * b + 4].un(2).bc([128, 4, 129]), ALU.mult)
        cp("pool", Sb0[b][:], Sf[b][:])
        pd = [nps(), nps()]
        for h in range(4):
            mm(pd[h // 2][:, (h % 2) * 129:(h % 2) * 129 + 129], kmS[b][:, h, :], vmu[0:16, h, :])
        tt("dve", Sf[b][:], Sf[b][:], ebt[:, 4 * b:4 * b + 4].un(2).bc([128, 4, 129]), ALU.mult)
        for hh in range(2):
            tt("dve", dS[:, 2 * hh:2 * hh + 2, :], pd[hh][:, 0:258].re("p (h e) -> p h e", h=2),
               ebt[:, 4 * b + 2 * hh:4 * b + 2 * hh + 2].un(2).bc([128, 2, 129]), ALU.mult)
        tt("dve", Sf[b][:], Sf[b][:], dS[:], ALU.add)
    pa = [nps(), nps()]
    for h in range(4):
        o_ = pa[h // 2][0:16, (h % 2) * 129:(h % 2) * 129 + 129]
        mm(o_, mqk[:, h, :], vmu[0:16, h, :], st=True, sp=False)
        for b in range(4):
            mm(o_, qmS[b][:, h, :], Sb0[b][:, h, :], st=False, sp=(b == 3))
    dn = fw.sb(s, [16, 4], F32, "dnS")
    t4 = fw.sb(s, [16, 4], F32, "t4S")
    hout = fw.sb(s, [16, 4, 128], F32, "houtS")
    hsq = fw.sb(s, [16, 4, 128], F32, "hsqS")
    ss4 = fw.sb(s, [16, 4], F32, "ss4m")
    rs4 = fw.sb(s, [16, 4], F32, "rs4m")
    for hh in range(2):
        av = pa[hh][0:16, 0:258].re("p (h e) -> p h e", h=2)
        tt("dve", dn[:, 2 * hh:2 * hh + 2], av[:, :, 128], wl[0:16, 2 * hh:2 * hh + 2], ALU.mult)
    stt("dve", t4[:], dn[:], -1.0, dn[:], ALU.mult, ALU.max)
    ts("dve", t4[:], t4[:], 1.0, None, op0=ALU.max)
    self.recip(t4[:], t4[:])
    tt("dve", t4[:], t4[:], wl[0:16, :], ALU.mult)
    for hh in range(2):
        av = pa[hh][0:16, 0:258].re("p (h e) -> p h e", h=2)
        tt("dve", hout[:, 2 * hh:2 * hh + 2, :], av[:, :, 0:128], t4[:, 2 * hh:2 * hh + 2].un(2).bc([16, 2, 128]), ALU.mult)
    tt("dve", hsq[:], hout[:], hout[:], ALU.mult)
    self.rsum(ss4[:], hsq[:])
    self.rstd(rs4[:], ss4[:], 1.0 / 128)
    tt("dve", hout[:], hout[:], rs4[:].un(2).bc([16, 4, 128]), ALU.mult)
    tt("dve", mixin[0:16, 512:1024], hout[:].re("p h e -> p (h e)"), sigo[0:16, :], ALU.mult)
    Rd = fw.sb(s, [4, 4, 4], F32, "RdS")
    tt("dve", Rd[:], R[:].un(2).bc([4, 4, 4]), identf[0:4, 0:4].un(1).bc([4, 4, 4]), ALU.mult)
    ones4 = fw.sb(s, [4, 128], F32, "ones4S")
    ms("pool", ones4[:], 1.0)
    ps = nps()
    mm(ps[:, 0:16], ones4[:], Rd[:].re("p b h -> p (b h)"))
    esc = fw.sb(s, [128, 16], F32, "escS")
    act(esc[:], ps[:, 0:16], AF.Exp, scale=-1.0)
    for b in range(4):
        tt("dve", Sf[b][:], Sf[b][:], esc[:, 4 * b:4 * b + 4].un(2).bc([128, 4, 129]), ALU.mult)
        dmas(V(None, S["s_n"].a[b].rearrange("h d -> d h")), Sf[b][:, :, 128])
        for h in range(4):
            ps = nps()
            tr(ps[:, 0:128], Sf[b][:, h, 0:128], identf[:])
            st = cst_[h % 2]
            cp("dve", st[:], ps[:, 0:128])
            dma(V(None, S["s_C"].a[b, h]), st[:])


Builder.sample_mlstm = sample_mlstm
Builder.sample_s0 = sample_s0

W_NAMES = ["w_in", "g_mix", "b_gate", "cmp_pe_k", "cmp_w1_k", "cmp_b1_k", "cmp_w2_k", "cmp_pe_v", "cmp_w1_v",
           "cmp_b1_v", "cmp_w2_v", "g_head_nsa", "conv_w", "conv_b", "w_qm", "w_km", "b_i", "b_f", "g_head_m",
           "w_out", "g_xa", "g_mem", "w_xq", "w_xk", "w_xv", "w_xo", "g_ffn", "w_gate", "w_up", "w_down", "g_final"]


def build_program(NT=32, sample=True, debug=False):
    nc = bass.Bass("TRN2", target_bir_lowering=False)
    b = Builder(nc, NT=NT, sample=sample, debug=debug)
    b.build()
    return nc, b


def core_inputs(inp, c, b, NT=32):
    f = lambda a: np.ascontiguousarray(a, dtype=np.float32)
    T = NT * 128
    m = {"xp": f(inp["x_prompt"][c, :T]), "memp": f(inp["mem_prompt"][c])}
    for n in W_NAMES:
        a = np.asarray(inp[n])
        if n != "g_final":
            a = a[0]
        m[n] = f(a).reshape(b.io[n].shape)
    if b.sample:
        sl = slice(4 * c, 4 * c + 4)
        m["xs"] = f(inp["x_sample"][sl]).reshape(16, D)
        for n, k in (("pool_kc", "cache_k_cmp"), ("pool_vc", "cache_v_cmp"), ("pool_ks", "cache_k_slc"), ("pool_vs", "cache_v_slc")):
            m[n] = np.asarray(inp[k][0], dtype=np.float32).reshape(5120, 16384)
        m["ptab"] = np.ascontiguousarray(inp["page_table"][sl], dtype=np.int32)
        m["stk"] = f(inp["state_k_win"][0, sl]).reshape(4, 512, 128)
        m["stv"] = f(inp["state_v_win"][0, sl]).reshape(4, 512, 128)
        m["sconv"] = f(inp["state_conv"][0, sl])
        m["sC"] = f(inp["state_C"][0, sl])
        m["sn"] = f(inp["state_n"][0, sl])
        m["sm"] = f(inp["state_m"][0, sl]).reshape(16)
        m["cmk"] = f(inp["cache_mem_k"][0, sl]).reshape(4, 256, D)
        m["cmv"] = f(inp["cache_mem_v"][0, sl]).reshape(4, 256, D)
    return {k: v for k, v in m.items() if k in b.io}


_PROG = {}


def kernel(**inp):
    n = 8
    if "p" not in _PROG:
        _PROG["p"] = build_program()
    nc, b = _PROG["p"]
    in_maps = [core_inputs(inp, c, b) for c in range(n)]
    res = run_bass_kernel_spmd(nc, in_maps, core_ids=list(range(n)))
    R = res.results

    def st(name, shp, lead):
        a = np.stack([np.asarray(R[c][name], dtype=np.float32).reshape(shp) for c in range(n)])
        return np.ascontiguousarray(a.reshape(lead))

    outs = [st("y_p", (4096, D), (8, 4096, D)), st("y_s", (4, 4, D), (32, 4, D))]
    for nm in ("p_kc", "p_vc", "p_ks", "p_vs"):
        outs.append(st(nm, (4096, 2, 64), (1, 8, 4096, 2, 64)))
    for nm in ("p_kw", "p_vw"):
        outs.append(st(nm, (512, 2, 64), (1, 8, 512, 2, 64)))
    outs.append(st("p_C", (4, 128, 128), (1, 8, 4, 128, 128)))
    outs.append(st("p_n", (4, 128), (1, 8, 4, 128)))
    outs.append(st("p_m", (4,), (1, 8, 4)))
    outs.append(st("p_conv", (3, 512), (1, 8, 3, 512)))
    outs.append(st("p_mk", (256, 4, 256), (1, 8, 256, 4, 256)))
    outs.append(st("p_mv", (256, 4, 256), (1, 8, 256, 4, 256)))
    for nm in ("s_kc", "s_vc", "s_ks", "s_vs"):
        outs.append(st(nm, (4, 4, 2, 64), (1, 32, 4, 2, 64)))
    for nm in ("s_kw", "s_vw"):
        outs.append(st(nm, (4, 512, 2, 64), (1, 32, 512, 2, 64)))
    outs.append(st("s_C", (4, 4, 128, 128), (1, 32, 4, 128, 128)))
    outs.append(st("s_n", (4, 4, 128), (1, 32, 4, 128)))
    outs.append(st("s_m", (4, 4), (1, 32, 4)))
    outs.append(st("s_conv", (4, 3, 512), (1, 32, 3, 512)))
    return tuple(outs)
```

```python
import numpy as np
from contextlib import ExitStack
import concourse.bass as bass
import concourse.mybir as mybir
from concourse.bass_utils import run_bass_kernel_spmd

F32 = mybir.dt.float32
BF16 = mybir.dt.bfloat16
I32 = mybir.dt.int32
AF = mybir.ActivationFunctionType
ALU = mybir.AluOpType
AX = mybir.AxisListType

D = 1024
NEG = -30000.0
EPS = 1e-6
IN_COLS = 2848
DFF = 2816


class V:
    __slots__ = ("b", "a")

    def __init__(self, b, a):
        self.b = b
        self.a = a

    def __getitem__(self, k):
        return V(self.b, self.a[k])

    def re(self, p, **kw):
        return V(self.b, self.a.rearrange(p, **kw))

    def bc(self, shape):
        return V(self.b, self.a.to_broadcast(list(shape)))

    def un(self, ax):
        return V(self.b, self.a.unsqueeze(ax))


class Buf:
    __slots__ = ("t", "w", "r", "name", "psum", "fresh", "quads")

    def __init__(self, t, name="", psum=False):
        self.t = t
        self.w = None
        self.r = []
        self.name = name
        self.psum = psum
        self.fresh = True
        self.quads = set()

    def __getitem__(self, k):
        return V(self, self.t[k])


class DSem:
    def __init__(self, nc, name):
        self.sem = nc.alloc_semaphore(name)
        self.val = 0


class FW:
    ENG = ("pe", "act", "dve", "pool", "sp")

    def __init__(self, nc, n_dsem=10, same_engine_sync=True):
        self.nc = nc
        self.e = {"pe": nc.tensor, "act": nc.scalar, "dve": nc.vector, "pool": nc.gpsimd, "sp": nc.sync}
        self.gen = {k: 0 for k in self.ENG}
        self.sem = {k: nc.alloc_semaphore("S_" + k) for k in self.ENG}
        self.cnt = {k: 0 for k in self.ENG}
        self.seen = {k: {} for k in self.ENG}
        self.same = same_engine_sync
        self.dsems = {q: [DSem(nc, f"D{q}{i}") for i in range(n_dsem)] for q in ("sp", "pool", "act")}
        self.dnext = {q: 0 for q in self.dsems}
        self.nbuf = 0
        self.nins = 0

    def sb(self, stack, shape, dt=F32, name=None):
        self.nbuf += 1
        name = name or f"b{self.nbuf}"
        return Buf(stack.enter_context(self.nc.sbuf_tensor(name, list(shape), dt)), name)

    def ps(self, stack, shape, dt=F32, name=None):
        self.nbuf += 1
        name = name or f"p{self.nbuf}"
        return Buf(stack.enter_context(self.nc.psum_tensor(name, list(shape), dt)), name, psum=True)

    def _need(self, e, dep, waits):
        if dep is None:
            return
        kind, key, val, semh = dep
        if kind == "e" and key[0] == e and (not self.same or e == "pe"):
            return
        k = (kind, key if kind == "e" else id(key))
        if self.seen[e].get(k, 0) >= val:
            return
        cur = waits.get(k)
        if cur is None or cur[1] < val:
            waits[k] = (semh, val)

    def _emit_waits(self, e, reads, writes):
        waits = {}
        for b in reads:
            if b is not None:
                self._need(e, b.w, waits)
        for b in writes:
            if b is not None:
                self._need(e, b.w, waits)
                for d in b.r:
                    self._need(e, d, waits)
        eng = self.e[e]
        for k, (semh, val) in waits.items():
            eng.wait_ge(semh, val)
            self.seen[e][k] = val

    def op(self, e, fn, reads=(), writes=()):
        px = [b for b in reads if b is not None and b.psum]
        if px:
            reads = [b for b in reads if not (b is not None and b.psum)]
            writes = list(writes) + [b for b in px if b not in writes]
            if e != "pe":
                for b in px:
                    b.fresh = True
        self._emit_waits(e, reads, writes)
        ins = fn(self.e[e])
        if self.cnt[e] >= 50000:
            self.gen[e] += 1
            self.sem[e] = self.nc.alloc_semaphore(f"S_{e}_{self.gen[e]}")
            self.cnt[e] = 0
        self.cnt[e] += 1
        self.nins += 1
        ins.then_inc(self.sem[e], 1)
        dep = ("e", (e, self.gen[e]), self.cnt[e], self.sem[e])
        for b in reads:
            if b is not None:
                b.r.append(dep)
                if len(b.r) > 16:
                    b.r = self._compact(b.r)
        for b in writes:
            if b is not None:
                b.w = dep
                b.r = []
        return ins

    @staticmethod
    def _compact(lst):
        best = {}
        for d in lst:
            k = (d[0], d[1] if d[0] == "e" else id(d[1]))
            if k not in best or best[k][2] < d[2]:
                best[k] = d
        return list(best.values())

    def dma(self, q, o, i, fn=None, extra_reads=(), **kw):
        reads = [i.b] + list(extra_reads)
        writes = [o.b]
        self._emit_waits(q, reads, writes)
        ds = self.dsems[q][self.dnext[q]]
        self.dnext[q] = (self.dnext[q] + 1) % len(self.dsems[q])
        if ds.val > 0 and self.seen[q].get(("d", id(ds)), 0) < ds.val:
            self.e[q].wait_ge(ds.sem, ds.val)
            self.seen[q][("d", id(ds))] = ds.val
        if fn is None:
            ins = self.e[q].dma_start(out=o.a, in_=i.a, **kw)
        else:
            ins = fn(self.e[q])
        ds.val += 16
        self.nins += 1
        ins.then_inc(ds.sem, 16)
        dep = ("d", ds, ds.val, ds.sem)
        for b in reads:
            if b is not None:
                b.r.append(dep)
                if len(b.r) > 16:
                    b.r = self._compact(b.r)
        for b in writes:
            if b is not None:
                b.w = dep
                b.r = []
        return ins

    def barrier(self):
        for e in self.ENG:
            eng = self.e[e]
            for f in self.ENG:
                if f != e and self.cnt[f] > 0:
                    k = ("e", (f, self.gen[f]))
                    if self.seen[e].get(k, 0) < self.cnt[f]:
                        eng.wait_ge(self.sem[f], self.cnt[f])
                        self.seen[e][k] = self.cnt[f]
            for q in self.dsems:
                for ds in self.dsems[q]:
                    k = ("d", id(ds))
                    if ds.val > 0 and self.seen[e].get(k, 0) < ds.val:
                        eng.wait_ge(ds.sem, ds.val)
                        self.seen[e][k] = ds.val

    def finish(self):
        eng = self.e["sp"]
        for q in self.dsems:
            for ds in self.dsems[q]:
                if ds.val > 0:
                    eng.wait_ge(ds.sem, ds.val)


class RR:
    def __init__(self, items):
        self.items = list(items)
        self.i = 0

    def __call__(self):
        x = self.items[self.i]
        self.i = (self.i + 1) % len(self.items)
        return x


class Builder:
    def __init__(self, nc, NT=32, sample=True, debug=False):
        self.debug = debug
        self.nc = nc
        self.fw = FW(nc)
        self.NT = NT
        self.T = NT * 128
        self.sample = sample
        self.io = {}

    def din(self, name, shape, dt=F32):
        t = self.nc.dram_tensor(name, list(shape), dt, kind="ExternalInput").ap()
        self.io[name] = t
        return V(None, t)

    def dout(self, name, shape, dt=F32):
        t = self.nc.dram_tensor(name, list(shape), dt, kind="ExternalOutput").ap()
        self.io[name] = t
        return V(None, t)

    def dscr(self, name, shape, dt=F32):
        t = self.nc.dram_tensor(name, list(shape), dt, kind="ExternalOutput" if self.debug else "Internal").ap()
        if self.debug:
            self.io[name] = t
        return Buf(t, name)

    def dbg(self, name, v, dt=F32):
        if not self.debug:
            return
        o = self.dout("dbg_" + name, list(v.a.shape), dt)
        self.fw.dma("sp", o, v)

    def mm(self, o, l, r, st=True, sp=True):
        b = o.b
        p0 = o.a.base_partition() if hasattr(o.a, "base_partition") else 0
        q = set(range(p0 // 32, (p0 + o.a.shape[0] + 31) // 32))
        start = False
        if st:
            if b.fresh:
                start = True
                b.fresh = False
                b.quads = set(q)
            else:
                assert q <= b.quads, (b.name, q, b.quads)
        self.fw.op("pe", lambda e: e.matmul(o.a, lhsT=l.a, rhs=r.a, start=start, stop=sp, skip_group_check=True),
                   reads=[l.b, r.b], writes=[o.b])

    def tr(self, o, i, ident):
        self.fw.op("pe", lambda e: e.transpose(out=o.a, in_=i.a, identity=ident.a), reads=[i.b, ident.b], writes=[o.b])

    def act(self, o, i, f, scale=1.0, bias=0.0, acc=None):
        reads = [i.b]
        writes = [o.b]
        kw = {}
        if isinstance(bias, V):
            reads.append(bias.b)
            kw["bias"] = bias.a
        elif bias != 0.0:
            kw["bias"] = float(bias)
        if isinstance(scale, V):
            reads.append(scale.b)
            kw["scale"] = scale.a
        elif scale != 1.0:
            kw["scale"] = float(scale)
        if acc is not None:
            writes.append(acc.b)
            kw["accum_out"] = acc.a
        self.fw.op("act", lambda e: e.activation(out=o.a, in_=i.a, func=f, **kw), reads=reads, writes=writes)

    def ts(self, eng, o, i, s1, s2=None, op0=ALU.mult, op1=None):
        reads = [i.b]
        a1 = s1
        a2 = s2
        if isinstance(s1, V):
            reads.append(s1.b)
            a1 = s1.a
        if isinstance(s2, V):
            reads.append(s2.b)
            a2 = s2.a
        kw = {}
        if op1 is not None:
            kw["op1"] = op1
        self.fw.op(eng, lambda e: e.tensor_scalar(out=o.a, in0=i.a, scalar1=a1, scalar2=a2, op0=op0, **kw), reads=reads, writes=[o.b])

    def tt(self, eng, o, a, b, op):
        self.fw.op(eng, lambda e: e.tensor_tensor(out=o.a, in0=a.a, in1=b.a, op=op), reads=[a.b, b.b], writes=[o.b])

    def stt(self, eng, o, a, s, b, op0, op1):
        reads = [a.b, b.b]
        sa = s
        if isinstance(s, V):
            reads.append(s.b)
            sa = s.a
        self.fw.op(eng, lambda e: e.scalar_tensor_tensor(out=o.a, in0=a.a, scalar=sa, in1=b.a, op0=op0, op1=op1), reads=reads, writes=[o.b])

    def cp(self, eng, o, i):
        if eng == "act":
            self.fw.op("act", lambda e: e.copy(out=o.a, in_=i.a), reads=[i.b], writes=[o.b])
        else:
            self.fw.op(eng, lambda e: e.tensor_copy(out=o.a, in_=i.a), reads=[i.b], writes=[o.b])

    def ms(self, eng, o, val):
        self.fw.op(eng, lambda e: e.memset(o.a, val), writes=[o.b])

    def iota(self, o, pattern, base=0, cm=0):
        self.fw.op("pool", lambda e: e.iota(o.a, pattern=pattern, base=base, channel_multiplier=cm,
                                            allow_small_or_imprecise_dtypes=True), writes=[o.b])

    def sigm(self, o, i):
        self.act(o, i, AF.Exp, scale=-1.0)
        self.ts("dve", o, o, 1.0, None, op0=ALU.add)
        self.recip(o, o)

    def recip(self, o, i):
        self.fw.op("dve", lambda e: e.reciprocal(out=o.a, in_=i.a), reads=[i.b], writes=[o.b])

    def rsum(self, o, i):
        self.fw.op("dve", lambda e: e.reduce_sum(out=o.a, in_=i.a, axis=AX.X), reads=[i.b], writes=[o.b])

    def rmax(self, o, i):
        self.fw.op("dve", lambda e: e.reduce_max(out=o.a, in_=i.a, axis=AX.X), reads=[i.b], writes=[o.b])

    def dma(self, o, i, q="sp", **kw):
        self.fw.dma(q, o, i, **kw)

    def dmas(self, o, i, q="sp"):
        self.fw.dma(q, o, i, allow_slow_non_contiguous=True)

    def put_row(self, dst, pattern, base, n, const=None):
        rowt, rowb = self.rowt, self.rowb
        if const is None:
            rv = rowt[0:1, 0:n]
            if len(pattern) == 2:
                rv = rv.re("p (a b) -> p a b", b=pattern[1][1])
            self.iota(rv, pattern, base=base, cm=0)
        else:
            self.ms("pool", rowt[0:1, 0:n], const)
        self.cp("pool", rowb[0:1, 0:n], rowt[0:1, 0:n])
        self.dma(dst, rowb[0:1, 0:n])

    def rstd(self, o, ss, inv_n):
        self.act(o, ss, AF.Ln, scale=inv_n, bias=self.epsc[0:o.a.shape[0], :])
        self.act(o, o, AF.Exp, scale=-0.5)

    def build(self):
        nc, fw, NT, T = self.nc, self.fw, self.NT, self.T
        din, dout = self.din, self.dout
        mm, tr, act, ts, tt, stt, cp, ms, iota, dma, dmas = (self.mm, self.tr, self.act, self.ts, self.tt, self.stt,
                                                             self.cp, self.ms, self.iota, self.dma, self.dmas)
        xp = din("xp", [T, D])
        memp = din("memp", [256, D])
        w_in = din("w_in", [D, IN_COLS])
        g_mix = din("g_mix", [D])
        b_gate = din("b_gate", [24])
        cmp_in = {}
        for kv in "kv":
            cmp_in[kv] = (din(f"cmp_pe_{kv}", [32, 64]), din(f"cmp_w1_{kv}", [2048, 256]),
                          din(f"cmp_b1_{kv}", [256]), din(f"cmp_w2_{kv}", [256, 64]))
        g_head_nsa = din("g_head_nsa", [512])
        conv_w = din("conv_w", [4, 512])
        conv_b = din("conv_b", [512])
        w_qm = din("w_qm", [4, 128, 128])
        w_km = din("w_km", [4, 128, 128])
        b_i = din("b_i", [4])
        b_f = din("b_f", [4])
        g_head_m = din("g_head_m", [512])
        w_out = din("w_out", [D, D])
        g_xa = din("g_xa", [D])
        g_mem = din("g_mem", [D])
        w_xq = din("w_xq", [D, D])
        w_xk = din("w_xk", [D, D])
        w_xv = din("w_xv", [D, D])
        w_xo = din("w_xo", [D, D])
        g_ffn = din("g_ffn", [D])
        w_gate = din("w_gate", [D, DFF])
        w_up = din("w_up", [D, DFF])
        w_down = din("w_down", [DFF, D])
        g_final = din("g_final", [D])

        y_p = dout("y_p", [T, D])
        p_kv = {n: dout(n, [T, 128]) for n in ("p_kc", "p_vc", "p_ks", "p_vs")}
        WT = min(512, T)
        p_kw = dout("p_kw", [WT, 128])
        p_vw = dout("p_vw", [WT, 128])
        p_C = dout("p_C", [4, 128, 128])
        p_n = dout("p_n", [4, 128])
        p_m = dout("p_m", [4])
        p_conv = dout("p_conv", [3, 512])
        p_mk = dout("p_mk", [256, D])
        p_mv = dout("p_mv", [256, D])

        if self.sample:
            self.sample_io()
        x1d = self.dscr("x1_scr", [T, D])
        x2d = self.dscr("x2_scr", [T, D])

        top = ExitStack()
        with top:
            psf = [fw.ps(top, [128, 512], F32, f"psf{i}") for i in range(4)]
            pst = [fw.ps(top, [128, 1024], BF16, f"pst{i}") for i in range(1)]
            psa = [fw.ps(top, [128, 512], F32, f"psa{i}") for i in range(3)]
            nps = RR(psf)
            nps_s = RR(psf[0:3])
            nps_m = RR([psf[3], psa[2]])
            npt = RR(pst)

            ident = fw.sb(top, [128, 128], BF16, "ident")
            identf = fw.sb(top, [128, 128], F32, "identf")
            for idt in (ident, identf):
                ms("pool", idt[:], 0.0)
                fw.op("pool", lambda e, idt=idt: e.affine_select(out=idt[:].a, in_=idt[:].a, pattern=[[-1, 128]],
                                                                compare_op=ALU.not_equal, fill=1.0, base=0,
                                                                channel_multiplier=1), reads=[idt], writes=[idt])
            self.epsc = fw.sb(top, [128, 1], F32, "epsc")[:]
            ms("pool", self.epsc, EPS)
            if self.sample:
                self.sample_s0(locals())
            sw = ExitStack()
            tmpf = fw.sb(sw, [128, 512], F32, "tmpf")
            iota(tmpf[:].re("p (g r) -> p g r", g=4), [[0, 4], [-1, 128]], base=0, cm=1)
            caus_add = fw.sb(sw, [128, 512], BF16, "caus_add")
            ts("pool", caus_add[:], tmpf[:], 0.0, NEG, op0=ALU.is_gt, op1=ALU.mult)
            win_add = fw.sb(sw, [128, 512], BF16, "win_add")
            ts("pool", win_add[:], tmpf[:], 0.0, NEG, op0=ALU.is_le, op1=ALU.mult)
            bd01 = fw.sb(sw, [128, 128], BF16, "bd01")
            ts("pool", bd01[:], tmpf[:, 0:128], 0.0, None, op0=ALU.is_le)
            ms("pool", bd01[0:64, 64:128], 0.0)
            tri_le = fw.sb(sw, [128, 128], F32, "tri_le")
            ts("pool", tri_le[:], tmpf[:, 0:128], 0.0, None, op0=ALU.is_le)
            tri2 = fw.sb(sw, [128, 128], F32, "tri2")
            cp("pool", tri2[:], bd01[:])
            csel = fw.sb(sw, [128, 2, 128], F32, "csel")
            ms("pool", csel[:], 0.0)
            ms("pool", csel[0:64, 0, :], 1.0)
            ms("pool", csel[64:128, 1, :], 1.0)
            e0 = fw.sb(sw, [128, 512], F32, "e0")
            iota(e0[:].re("p (g r) -> p g r", g=4), [[0, 4], [-1, 128]], base=0, cm=16)
            mimp = fw.sb(sw, [128, 2, 64], BF16, "mimp")
            for ct in range(2):
                iota(tmpf[:, 0:64], [[-4, 64]], base=ct * 128 - 1, cm=1)
                stt("dve", tmpf[:, 64:128], tmpf[:, 0:64], -1.0, tmpf[:, 0:64], ALU.mult, ALU.max)
                ts("pool", tmpf[:, 128:192], tmpf[:, 64:128], 2.0, 0.5, op0=ALU.is_le, op1=ALU.mult)
                ts("pool", tmpf[:, 192:256], tmpf[:, 64:128], 1.0, 0.5, op0=ALU.is_le, op1=ALU.mult)
                tt("pool", mimp[:, ct, :], tmpf[:, 128:192], tmpf[:, 192:256], ALU.add)
                ms("pool", mimp[:, ct, 63:64], 1.0)
            expand = fw.sb(sw, [64, T], BF16, "expand")
            for c0 in range(0, T, 512):
                iota(tmpf[0:64, :], [[1, 512]], base=c0, cm=-64)
                ts("pool", tmpf[0:64, :], tmpf[0:64, :], 31.5, None, op0=ALU.subtract)
                stt("dve", tmpf[0:64, :], tmpf[0:64, :], -1.0, tmpf[0:64, :], ALU.mult, ALU.max)
                ts("pool", expand[:, c0:c0 + 512], tmpf[0:64, :], 32.0, None, op0=ALU.is_le)

            if True:
                win_b = fw.sb(sw, [128, 8, IN_COLS], BF16, "win_b")
                wout_b = fw.sb(sw, [128, 8, D], BF16, "wout_b")
                wqm_b = fw.sb(sw, [128, 4, 128], BF16, "wqm_b")
                wkm_b = fw.sb(sw, [128, 4, 128], BF16, "wkm_b")
                gcol = fw.sb(sw, [128, 16], F32, "gcol")
                dmas(gcol[:, 0:8], V(None, g_mix.a.rearrange("(k p) -> p k", p=128)))
                dmas(gcol[:, 8:12], V(None, g_head_nsa.a.rearrange("(k p) -> p k", p=128)))
                dmas(gcol[:, 12:16], V(None, g_head_m.a.rearrange("(k p) -> p k", p=128)))
                cw = fw.sb(sw, [128, 4, 4], F32, "cw")
                for j in range(4):
                    dmas(cw[:, :, j], V(None, conv_w.a[j].rearrange("(c p) -> p c", p=128)))
                cb = fw.sb(sw, [128, 4], F32, "cb")
                dmas(cb[:], V(None, conv_b.a.rearrange("(c p) -> p c", p=128)))
                bgate = fw.sb(sw, [128, 24], F32, "bgate")
                dma(bgate[:], V(None, b_gate.a.partition_broadcast(128)))
                bif = fw.sb(sw, [128, 8], F32, "bif")
                dma(bif[:, 0:4], V(None, b_i.a.partition_broadcast(128)))
                dma(bif[:, 4:8], V(None, b_f.a.partition_broadcast(128)))
                with ExitStack() as s0:
                    stg = [fw.sb(s0, [128, IN_COLS], F32, f"stg{i}") for i in range(2)]
                    for k in range(8):
                        st = stg[k % 2]
                        dma(st[:], w_in[k * 128:(k + 1) * 128, :])
                        if k % 2 == 0:
                            ts("dve", win_b[:, k, :], st[:], gcol[:, k:k + 1], None, op0=ALU.mult)
                        else:
                            act(win_b[:, k, :], st[:], AF.Copy, scale=gcol[:, k:k + 1])
                    for k in range(8):
                        st = stg[k % 2]
                        dma(st[:, 0:D], w_out[k * 128:(k + 1) * 128, :])
                        if k % 2 == 0:
                            ts("dve", wout_b[:, k, :], st[:, 0:D], gcol[:, 8 + k:9 + k], None, op0=ALU.mult)
                        else:
                            act(wout_b[:, k, :], st[:, 0:D], AF.Copy, scale=gcol[:, 8 + k:9 + k])
                    st = stg[0]
                    dma(st[:, 0:512].re("p (h e) -> p h e", h=4), V(None, w_qm.a.rearrange("h d e -> d h e")))
                    cp("dve", wqm_b[:], st[:, 0:512].re("p (h e) -> p h e", h=4))
                    st = stg[1]
                    dma(st[:, 0:512].re("p (h e) -> p h e", h=4), V(None, w_km.a.rearrange("h d e -> d h e")))
                    cp("dve", wkm_b[:], st[:, 0:512].re("p (h e) -> p h e", h=4))
                    fw.barrier()

                kcp = fw.sb(sw, [68, 2, 256], BF16, "kcp")
                vcp = fw.sb(sw, [128, 2, 2, 65], BF16, "vcp")
                ms("pool", vcp[:], 1.0)
                put_row = self.put_row
                with ExitStack() as tmps:
                    self.rowt = fw.sb(tmps, [1, 4096], F32, "rowt")
                    self.rowb = fw.sb(tmps, [1, 4096], BF16, "rowb")
                    for kvh in range(2):
                        put_row(kcp[64:65, kvh, :], [[128, 32], [0, 8]], 0, 256)
                        put_row(kcp[65:66, kvh, :], [[0, 32], [16, 8]], 31, 256)
                        put_row(kcp[66:67, kvh, :], None, 0, 256, const=1.0)
                        put_row(kcp[67:68, kvh, :], None, 0, 256, const=1.0)
                    fw.barrier()

                self.pass0_prompt(sw, xp, win_b, cmp_in, kcp, vcp, ident, nps, npt)
                self.dbg("kcp", kcp[:], BF16)
                self.dbg("vcp", vcp[:], BF16)
                self.pass1_prompt(sw, locals())
                if self.sample:
                    self.sample_pass1(locals())
            fw.barrier()
            sw.close()
            self.pass2(top, locals())
            fw.finish()

    def norm_T(self, src, xt, nb, hT, ident, npt, rows=128):
        self.norm_A(src, xt, nb, rows)
        self.norm_B(nb, hT, ident, npt)

    def norm_A(self, src, xt, nb, rows=128):
        if rows < 128:
            self.ms("pool", xt[:], 0.0)
        self.dma(xt[0:rows, :], src)
        self.ms("dve", nb["ss"][:], 0.0)
        self.act(nb["junk"][:], xt[:], AF.Square, acc=nb["ss"][:])
        self.rstd(nb["rs"][:], nb["ss"][:], 1.0 / D)
        self.ts("dve", nb["xn"][:], xt[:], nb["rs"][:, 0:1], None, op0=ALU.mult)

    def norm_B(self, nb, hT, ident, npt):
        pt = npt()
        for k in range(8):
            self.tr(pt[:, k * 128:(k + 1) * 128], nb["xn"][:, k * 128:(k + 1) * 128], ident[:])
        self.cp("act", hT[:].re("p k t -> p (k t)"), pt[:])

    def norm_bufs(self, s, tag):
        fw = self.fw
        xn = fw.sb(s, [128, D], BF16, "xn" + tag)
        return {"junk": xn, "ss": fw.sb(s, [128, 1], F32, "ss" + tag), "rs": fw.sb(s, [128, 1], F32, "rs" + tag), "xn": xn}

    def pass0_prompt(self, sw, xp, win_b, cmp_in, kcp, vcp, ident, nps, npt):
        fw, NT, T = self.fw, self.NT, self.T
        mm, act, tt, cp, ms, dma, dmas = self.mm, self.act, self.tt, self.cp, self.ms, self.dma, self.dmas
        NCB = T // 16
        with ExitStack() as s:
            srcT = {kv: fw.sb(s, [64, 2, 16, NCB + 1], BF16, "srcT" + kv) for kv in "kv"}
            for kv in "kv":
                ms("pool", srcT[kv][:, :, :, NCB:NCB + 1], 0.0)
            ms("pool", kcp[0:64, :, :], 0.0)
            xts = [fw.sb(s, [128, D], F32, f"x0_{i}") for i in range(2)]
            nb = self.norm_bufs(s, "0")
            hT = fw.sb(s, [128, 8, 128], BF16, "hT0")
            for t_ in range(NT):
                xt = xts[t_ % 2]
                self.norm_T(xp[t_ * 128:(t_ + 1) * 128, :], xt, nb, hT, ident, npt)
                ps = nps()
                for gi in range(4):
                    for k in range(8):
                        mm(ps[0:64, gi * 128:(gi + 1) * 128], win_b[:, k, 512 + 64 * gi:576 + 64 * gi], hT[:, k, :],
                           st=(k == 0), sp=(k == 7))
                for h in range(2):
                    cp("act", srcT["k"][:, h, :, 8 * t_:8 * t_ + 8], ps[0:64, h * 128:(h + 1) * 128].re("p (c j) -> p j c", j=16))
                    cp("dve", srcT["v"][:, h, :, 8 * t_:8 * t_ + 8], ps[0:64, 256 + h * 128:384 + h * 128].re("p (c j) -> p j c", j=16))
            w1s = [fw.sb(s, [64, 8, 256], F32, f"w1s{i}") for i in range(2)]
            for kv in "kv":
                pe, w1, b1, w2 = cmp_in[kv]
                w1b = fw.sb(s, [64, 32, 256], BF16, "w1b" + kv)
                for jb in range(4):
                    st = w1s[jb % 2]
                    dma(st[:], V(None, w1.a.rearrange("(j d) n -> d j n", d=64)[:, jb * 8:(jb + 1) * 8, :]))
                    cp("dve", w1b[:, jb * 8:(jb + 1) * 8, :], st[:])
                peT = fw.sb(s, [64, 32], F32, "peT" + kv)
                dmas(peT[:], V(None, pe.a.rearrange("j d -> d j")))
                peTb = fw.sb(s, [64, 32], BF16, "peTb" + kv)
                cp("dve", peTb[:], peT[:])
                b1c = fw.sb(s, [128, 2], F32, "b1c" + kv)
                dmas(b1c[:], V(None, b1.a.rearrange("(c p) -> p c", p=128)))
                w2s = fw.sb(s, [128, 2, 64], F32, "w2s" + kv)
                dma(w2s[:], V(None, w2.a.rearrange("(c p) n -> p c n", p=128)))
                w2b = fw.sb(s, [128, 2, 64], BF16, "w2b" + kv)
                cp("dve", w2b[:], w2s[:])
                cst = fw.sb(s, [128, 2], F32, "cst" + kv)
                for hc in range(2):
                    ps = nps()
                    for j in range(32):
                        mm(ps[:, 0:1], w1b[:, j, hc * 128:(hc + 1) * 128], peTb[:, j:j + 1], st=(j == 0), sp=(j == 31))
                    tt("dve", cst[:, hc:hc + 1], ps[:, 0:1], b1c[:, hc:hc + 1], ALU.add)
                gT = fw.sb(s, [128, 2, 256], BF16, "gT" + kv)
                if NCB < 256:
                    ms("pool", gT[:], 0.0)
                for kvh in range(2):
                    for hc in range(2):
                        ps = nps()
                        for j in range(32):
                            rv = srcT[kv][:, kvh, j, 0:NCB] if j < 16 else srcT[kv][:, kvh, j - 16, 1:NCB + 1]
                            mm(ps[:, 0:NCB], w1b[:, j, hc * 128:(hc + 1) * 128], rv, st=(j == 0), sp=(j == 31))
                        act(gT[:, hc, 0:NCB], ps[:, 0:NCB], AF.Gelu_apprx_tanh, bias=cst[:, hc:hc + 1])
                    if kv == "k":
                        ps = nps()
                        for hc in range(2):
                            mm(ps[0:64, 0:256], w2b[:, hc, :], gT[:, hc, :], st=(hc == 0), sp=(hc == 1))
                        cp("dve", kcp[0:64, kvh, :], ps[0:64, 0:256])
                    else:
                        for ct in range(2):
                            ps = nps()
                            for hc in range(2):
                                mm(ps[:, 0:64], gT[:, hc, ct * 128:(ct + 1) * 128], w2b[:, hc, :], st=(hc == 0), sp=(hc == 1))
                            cp("dve", vcp[:, ct, kvh, 0:64], ps[:, 0:64])
            fw.barrier()

    def pass1_prompt(self, sw, L):
        fw, NT, T = self.fw, self.NT, self.T
        mm, tr, act, ts, tt, stt, cp, ms, iota, dma, dmas = (self.mm, self.tr, self.act, self.ts, self.tt, self.stt,
                                                             self.cp, self.ms, self.iota, self.dma, self.dmas)
        xp, win_b, wout_b, wqm_b, wkm_b = L["xp"], L["win_b"], L["wout_b"], L["wqm_b"], L["wkm_b"]
        kcp, vcp, ident, identf, nps, npt, psa = L["kcp"], L["vcp"], L["ident"], L["identf"], L["nps"], L["npt"], L["psa"]
        nps_s, nps_m = L["nps_s"], L["nps_m"]
        caus_add, win_add, bd01, tri2, csel, e0, mimp, expand = (L["caus_add"], L["win_add"], L["bd01"], L["tri2"],
                                                                 L["csel"], L["e0"], L["mimp"], L["expand"])
        cw, cb, bgate, bif, put_row, x1d = L["cw"], L["cb"], L["bgate"], L["bif"], L["put_row"], L["x1d"]
        p_kv, p_kw, p_vw, p_C, p_n, p_m, p_conv = L["p_kv"], L["p_kw"], L["p_vw"], L["p_C"], L["p_n"], L["p_m"], L["p_conv"]
        with ExitStack() as s:
            ksT = fw.sb(s, [68, 2, T], BF16, "ksT")
            NW = min(8, NT)
            kwT = fw.sb(s, [68, 2, NW * 128], BF16, "kwT")
            phr = fw.sb(s, [1, 128], BF16, "phr")
            vsp = fw.sb(s, [128, NT, 2, 65], BF16, "vsp")
            vwp = fw.sb(s, [128, NW, 2, 65], BF16, "vwp")
            ms("pool", vsp[:], 1.0)
            ms("pool", vwp[:], 1.0)
            with ExitStack() as tmps:
                self.rowt = fw.sb(tmps, [1, 4096], F32, "rowt1")
                self.rowb = fw.sb(tmps, [1, 4096], BF16, "rowb1")
                for kvh in range(2):
                    put_row(ksT[64:65, kvh, :], [[128, NT], [0, 128]], 0, T)
                    put_row(ksT[65:66, kvh, :], [[0, NT], [1, 128]], 0, T)
                    put_row(ksT[66:67, kvh, :], None, 0, T, const=1.0)
                    put_row(ksT[67:68, kvh, :], None, 0, T, const=1.0)
                    put_row(kwT[65:66, kvh, :], [[0, NW], [1, 128]], 0, NW * 128)
                    put_row(kwT[66:67, kvh, :], None, 0, NW * 128, const=1.0)
                    put_row(kwT[67:68, kvh, :], None, 0, NW * 128, const=1.0)
                fw.barrier()
            qps = [fw.sb(s, [68, 2, 4, 128], BF16, f"qp{i}") for i in range(2)]
            srow = fw.sb(s, [1, 8, 128], F32, "srow")
            for h in range(8):
                ms("pool", srow[0:1, h, :], 2.0 ** (-(h + 1)))
            r67 = fw.sb(s, [1, 8, 128], F32, "r67")
            iota(r67[:], [[0, 8], [1, 128]], base=0, cm=0)
            tt("pool", r67[:], r67[:], srow[:], ALU.mult)
            ts("pool", r67[:], r67[:], -1.0, None, op0=ALU.mult)
            srb = fw.sb(s, [1, 8, 128], BF16, "srb")
            r67b = fw.sb(s, [1, 8, 128], BF16, "r67b")
            r66b = fw.sb(s, [1, 8, 128], BF16, "r66b")
            cp("pool", srb[:], srow[:])
            cp("pool", r67b[:], r67[:])
            for qp in qps:
                for kvh in range(2):
                    dma(qp[64:65, kvh], srb[0:1, 4 * kvh:4 * kvh + 4, :])
                    dma(qp[65:66, kvh], srb[0:1, 4 * kvh:4 * kvh + 4, :])
                    dma(qp[67:68, kvh], r67b[0:1, 4 * kvh:4 * kvh + 4, :])
            xts = [fw.sb(s, [128, D], F32, f"x1_{i}") for i in range(2)]
            nb = self.norm_bufs(s, "1")
            hTs = [fw.sb(s, [128, 8, 128], BF16, f"hT1_{i}") for i in range(2)]
            pkv = fw.sb(s, [128, 792], F32, "pkv")
            gt = fw.sb(s, [128, 24], F32, "gt")
            pts = RR([fw.sb(s, [128, 512], BF16, f"pt{i}") for i in range(4)])
            mks = RR([fw.sb(s, [128, 512], BF16, f"mk{i}") for i in range(2)])
            obr = [[fw.sb(s, [128, 4, 65], F32, f"obr{k}{i}") for i in range(3)] for k in range(2)]
            rdc = fw.sb(s, [128, 4], F32, "rdc")
            imp4 = fw.sb(s, [128, 4, 64], F32, "imp4")
            imp = fw.sb(s, [128, 64], F32, "imp")
            imp2 = fw.sb(s, [128, 64], F32, "imp2")
            mx1 = fw.sb(s, [128, 8], F32, "mx1")
            mx2 = fw.sb(s, [128, 8], F32, "mx2")
            selm = fw.sb(s, [128, 64], BF16, "selm")
            selT = [fw.sb(s, [64, 4, 128], BF16, f"selT{k}") for k in range(2)]
            fpb = fw.sb(s, [128, 3], F32, "fpb")
            ms("pool", fpb[:], -1.0)
            rd = fw.sb(s, [128, 3, 4], F32, "rd")
            sc3 = fw.sb(s, [128, 3, 4], F32, "sc3")
            onsa = fw.sb(s, [128, 4, 64], F32, "onsa")
            otmp = fw.sb(s, [128, 4, 64], F32, "otmp")
            ss4m = fw.sb(s, [128, 4], F32, "ss4pm")
            rs4m = fw.sb(s, [128, 4], F32, "rs4pm")
            ss4 = fw.sb(s, [128, 4], F32, "ss4")
            rs4 = fw.sb(s, [128, 4], F32, "rs4")
            mixin = fw.sb(s, [128, D], BF16, "mixin")
            mT = fw.sb(s, [128, 8, 128], BF16, "mT")
            x1t = fw.sb(s, [128, D], F32, "x1t")
            gif = fw.sb(s, [128, 8], F32, "gif")
            l1 = fw.sb(s, [128, 4], F32, "l1")
            gsb = fw.sb(s, [128, 12], F32, "gsb")
            wl = fw.sb(s, [128, 4], F32, "wl")
            ul = fw.sb(s, [128, 4], F32, "ul")
            tmp4 = fw.sb(s, [128, 4], F32, "tmp4")
            dec = fw.sb(s, [128, 4], F32, "dec")
            ebt = fw.sb(s, [128, 8], F32, "ebt")
            vmu = fw.sb(s, [128, 4, 129], BF16, "vmu")
            sigo = fw.sb(s, [128, 512], F32, "sigo")
            xcv = [fw.sb(s, [128, 4, 131], F32, f"xcv{i}") for i in range(2)]
            ms("pool", xcv[0][:], 0.0)
            cacc = fw.sb(s, [128, 4, 128], F32, "cacc")
            xc = fw.sb(s, [128, 4, 128], BF16, "xc")
            qmT = fw.sb(s, [128, 4, 128], BF16, "qmT")
            qmS = [fw.sb(s, [128, 4, 128], BF16, f"qmS{i}") for i in range(2)]
            for q_ in qmS:
                ms("pool", q_[:], 0.0)
            kmT = fw.sb(s, [128, 4, 128], BF16, "kmT")
            kmS = [fw.sb(s, [128, 4, 128], BF16, f"kmS{i}") for i in range(2)]
            mqk = fw.sb(s, [128, 4, 128], BF16, "mqk")
            Sf = fw.sb(s, [128, 4, 129], F32, "Sf")
            ms("pool", Sf[:], 0.0)
            Sb = [fw.sb(s, [128, 4, 129], BF16, f"Sb{i}") for i in range(3)]
            ms("pool", Sb[0][:], 0.0)
            dS = fw.sb(s, [128, 4, 129], F32, "dS")
            dn = fw.sb(s, [128, 4], F32, "dn")
            hout = fw.sb(s, [128, 4, 128], F32, "hout")
            hsq = cacc
            m4 = fw.sb(s, [4, 8], F32, "m4")
            R = fw.sb(s, [4, 1], F32, "Rm")
            ms("pool", R[:], 0.0)
            tsb = fw.sb(s, [4, 384], F32, "tsb")
            segs = [(0, 64), (64, 128)]
            KSC = 128.0 ** -0.5

            for t_ in range(NT):
                xt = xts[t_ % 2]
                qp = qps[t_ % 2]
                ts("pool", r66b[:], srow[:], -128.0 * t_, None, op0=ALU.mult)
                for kvh in range(2):
                    dma(qp[66:67, kvh], r66b[0:1, 4 * kvh:4 * kvh + 4, :])
                hT = hTs[t_ % 2]
                if t_ == 0:
                    self.norm_T(xp[0:128, :], xt, nb, hT, ident, npt)
                psA = nps()
                psB = nps()
                for k in range(8):
                    mm(psA[:, 0:512], hT[:, k, :], win_b[:, k, 512:1024], st=(k == 0), sp=(k == 7))
                for k in range(8):
                    mm(psB[:, 0:280], hT[:, k, :], win_b[:, k, 1024:1304], st=(k == 0), sp=(k == 7))
                cp("dve", pkv[:, 0:512], psA[:, 0:512])
                cp("act", pkv[:, 512:792], psB[:, 0:280])
                r0 = t_ * 128
                for i_, n_ in enumerate(("p_kc", "p_vc", "p_ks", "p_vs")):
                    dma(p_kv[n_][r0:r0 + 128, :], pkv[:, i_ * 128:(i_ + 1) * 128])
                if r0 >= T - 512:
                    w0 = r0 - (T - min(512, T))
                    dma(p_kw[w0:w0 + 128, :], pkv[:, 512:640])
                    dma(p_vw[w0:w0 + 128, :], pkv[:, 640:768])
                cp("pool", vsp[:, t_, :, 0:64], pkv[:, 384:512].re("p (h d) -> p h d", h=2))
                ws = t_ % NW
                cp("pool", vwp[:, ws, :, 0:64], pkv[:, 640:768].re("p (h d) -> p h d", h=2))
                tt("dve", gt[:], pkv[:, 768:792], bgate[:], ALU.add)
                self.sigm(gt[:], gt[:])
                psQ0 = nps()
                psQ1 = nps()
                for h in range(8):
                    ps = psQ0 if h < 4 else psQ1
                    for k in range(8):
                        mm(ps[0:64, (h % 4) * 128:(h % 4 + 1) * 128], win_b[:, k, 64 * h:64 * h + 64], hT[:, k, :],
                           st=(k == 0), sp=(k == 7))
                act(qp[0:64, 0].re("p g t -> p (g t)"), psQ0[0:64, :], AF.Copy, scale=0.125)
                act(qp[0:64, 1].re("p g t -> p (g t)"), psQ1[0:64, :], AF.Copy, scale=0.125)
                psK = nps()
                for gi, c0 in enumerate((768, 832, 1024, 1088)):
                    for k in range(8):
                        mm(psK[0:64, gi * 128:(gi + 1) * 128], win_b[:, k, c0:c0 + 64], hT[:, k, :], st=(k == 0), sp=(k == 7))
                cp("dve", ksT[0:64, :, r0:r0 + 128], psK[0:64, 0:256].re("p (h t) -> p h t", h=2))
                cp("dve", kwT[0:64, :, ws * 128:(ws + 1) * 128], psK[0:64, 256:512].re("p (h t) -> p h t", h=2))
                ms("pool", phr[:], 128.0 * t_)
                for kvh in range(2):
                    dma(kwT[64:65, kvh, ws * 128:(ws + 1) * 128], phr[:])

                def mlstm_gen():
                    psG = nps_m()
                    for k in range(8):
                        mm(psG[:, 0:8], hT[:, k, :], win_b[:, k, 2840:2848], st=(k == 0), sp=(k == 7))
                    tt("dve", gif[:], psG[:, 0:8], bif[:], ALU.add)
                    act(l1[:], gif[:, 4:8], AF.Exp, scale=-1.0)
                    act(l1[:], l1[:], AF.Ln, bias=1.0)
                    psC = nps_m()
                    mm(psC[:, 0:4], tri2[:], l1[:])
                    mm(psC[:, 4:8], csel[:, 0, :], l1[:])
                    mm(psC[:, 8:12], csel[:, 1, :], l1[:])
                    cp("dve", gsb[:], psC[:, 0:12])
                    act(wl[:], gsb[:, 0:4], AF.Exp, scale=-1.0)
                    tt("dve", tmp4[:], gif[:, 0:4], gsb[:, 0:4], ALU.add)
                    act(ul[:], tmp4[:], AF.Exp)
                    act(ebt[:], gsb[:, 4:12], AF.Exp, scale=-1.0)
                    tt("dve", dec[0:64, :], tmp4[0:64, :], gsb[0:64, 4:8], ALU.subtract)
                    tt("dve", dec[64:128, :], tmp4[64:128, :], gsb[64:128, 8:12], ALU.subtract)
                    yield
                    psV = nps_m()
                    for k in range(8):
                        mm(psV[:, 0:512], hT[:, k, :], win_b[:, k, 1816:2328], st=(k == 0), sp=(k == 7))
                    tt("dve", vmu[:, :, 0:128], psV[:, 0:512].re("p (h e) -> p h e", h=4), ul[:].un(2).bc([128, 4, 128]), ALU.mult)
                    cp("dve", vmu[:, :, 128], ul[:])
                    yield
                    psO = nps_m()
                    for k in range(8):
                        mm(psO[:, 0:512], hT[:, k, :], win_b[:, k, 2328:2840], st=(k == 0), sp=(k == 7))
                    self.sigm(sigo[:], psO[:, 0:512])
                    yield
                    xcur, xnext = xcv[t_ % 2], xcv[(t_ + 1) % 2]
                    psX = nps_m()
                    for ch in range(4):
                        for k in range(8):
                            mm(psX[:, ch * 128:(ch + 1) * 128], win_b[:, k, 1304 + ch * 128:1432 + ch * 128], hT[:, k, :],
                               st=(k == 0), sp=(k == 7))
                    cp("act", xcur[:, :, 3:131], psX[:, :].re("p (c t) -> p c t", c=4))
                    cp("pool", xnext[:, :, 0:3], xcur[:, :, 128:131])
                    yield
                    for ch in range(4):
                        ts("dve", cacc[:, ch, :], xcur[:, ch, 0:128], cw[:, ch, 0:1], cb[:, ch:ch + 1], op0=ALU.mult, op1=ALU.add)
                        for j in range(1, 4):
                            stt("dve", cacc[:, ch, :], xcur[:, ch, j:j + 128], cw[:, ch, j:j + 1], cacc[:, ch, :], ALU.mult, ALU.add)
                        if ch % 2 == 1:
                            yield
                    self.sigm(hout[:], cacc[:])
                    tt("dve", xc[:], cacc[:], hout[:], ALU.mult)
                    if t_ == NT - 1:
                        for j in range(3):
                            dmas(V(None, p_conv.a[j].rearrange("(c p) -> p c", p=128)), xcur[:, :, 128 + j])
                    yield
                    psq = nps_m()
                    for h in range(4):
                        mm(psq[:, h * 128:(h + 1) * 128], wqm_b[:, h, :], xc[:, h, :])
                    cp("act", qmT[:].re("p h t -> p (h t)"), psq[:, :])
                    for si, (a_, b_) in enumerate(segs):
                        cp("dve", qmS[si][:, :, a_:b_], psq[:, :].re("p (h t) -> p h t", h=4)[:, :, a_:b_])
                    psk = nps_m()
                    for h in range(4):
                        mm(psk[:, h * 128:(h + 1) * 128], wkm_b[:, h, :], xc[:, h, :])
                    act(kmT[:].re("p h t -> p (h t)"), psk[:, :], AF.Copy, scale=KSC)
                    yield
                    pskt = nps_m()
                    for h in range(4):
                        mm(pskt[:, h * 128:(h + 1) * 128], xc[:, h, :], wkm_b[:, h, :])
                    for si in range(2):
                        ts("dve", kmS[si][:].re("p h t -> p (h t)"), pskt[:, :], csel[:, si, 0:1], KSC, op0=ALU.mult, op1=ALU.mult)
                    psqk = nps_m()
                    for h in range(4):
                        mm(psqk[:, h * 128:(h + 1) * 128], kmT[:, h, :], qmT[:, h, :])
                    tt("dve", mqk[:], psqk[:, :].re("p (h t) -> p h t", h=4), bd01[:].un(1).bc([128, 4, 128]), ALU.mult)
                    yield
                    sbs = [Sb[(2 * t_) % 3], Sb[(2 * t_ + 1) % 3], Sb[(2 * t_ + 2) % 3]]
                    for si in range(2):
                        pd = [nps_m(), nps_m()]
                        for h in range(4):
                            mm(pd[h // 2][:, (h % 2) * 129:(h % 2) * 129 + 129], kmS[si][:, h, :], vmu[:, h, :])
                        eb = ebt[:, 4 * si:4 * si + 4].un(2).bc([128, 4, 129])
                        tt("dve", Sf[:], Sf[:], eb, ALU.mult)
                        for hh in range(2):
                            tt("dve", dS[:, 2 * hh:2 * hh + 2, :], pd[hh][:, 0:258].re("p (h e) -> p h e", h=2),
                               ebt[:, 4 * si + 2 * hh:4 * si + 2 * hh + 2].un(2).bc([128, 2, 129]), ALU.mult)
                        tt("dve", Sf[:], Sf[:], dS[:], ALU.add)
                        cp("act", sbs[si + 1][:], Sf[:])
                        yield
                    pa = [nps_m(), nps_m()]
                    for h in range(4):
                        o_ = pa[h // 2][:, (h % 2) * 129:(h % 2) * 129 + 129]
                        mm(o_, mqk[:, h, :], vmu[:, h, :], st=True, sp=False)
                        mm(o_, qmS[0][:, h, :], sbs[0][:, h, :], st=False, sp=False)
                        mm(o_, qmS[1][:, h, :], sbs[1][:, h, :], st=False, sp=True)
                    for hh in range(2):
                        av = pa[hh][:, 0:258].re("p (h e) -> p h e", h=2)
                        tt("dve", dn[:, 2 * hh:2 * hh + 2], av[:, :, 128], wl[:, 2 * hh:2 * hh + 2], ALU.mult)
                    stt("dve", tmp4[:], dn[:], -1.0, dn[:], ALU.mult, ALU.max)
                    ts("dve", tmp4[:], tmp4[:], 1.0, None, op0=ALU.max)
                    self.recip(tmp4[:], tmp4[:])
                    tt("dve", tmp4[:], tmp4[:], wl[:], ALU.mult)
                    for hh in range(2):
                        av = pa[hh][:, 0:258].re("p (h e) -> p h e", h=2)
                        tt("dve", hout[:, 2 * hh:2 * hh + 2, :], av[:, :, 0:128],
                           tmp4[:, 2 * hh:2 * hh + 2].un(2).bc([128, 2, 128]), ALU.mult)
                    yield
                    tt("dve", hsq[:], hout[:], hout[:], ALU.mult)
                    self.rsum(ss4m[:], hsq[:])
                    self.rstd(rs4m[:], ss4m[:], 1.0 / 128)
                    tt("dve", hout[:], hout[:], rs4m[:].un(2).bc([128, 4, 128]), ALU.mult)
                    tt("dve", mixin[:, 512:1024], hout[:].re("p h e -> p (h e)"), sigo[:], ALU.mult)
                    yield
                    psT = nps_m()
                    tr(psT[0:4, 0:128], dec[:], identf[:])
                    tr(psT[0:4, 128:256], gsb[:, 4:8], identf[:])
                    tr(psT[0:4, 256:384], gsb[:, 8:12], identf[:])
                    cp("dve", tsb[:], psT[0:4, 0:384])
                    self.rmax(m4[:, 0:1], tsb[:, 0:64])
                    self.rmax(m4[:, 1:2], tsb[:, 64:128])
                    stt("dve", R[:], R[:], tsb[:, 128:129], m4[:, 0:1], ALU.subtract, ALU.max)
                    stt("dve", R[:], R[:], tsb[:, 256:257], m4[:, 1:2], ALU.subtract, ALU.max)

                mg = mlstm_gen()
                steps = []
                for kvh in range(2):
                    cts = [0] if t_ < 16 else [0, 1]
                    for ci, ct in enumerate(cts):
                        steps.append(("cmp", kvh, ct, ci == 0, ci == len(cts) - 1))
                for kvh in range(2):
                    k0 = max(0, t_ - 4)
                    for kt in range(k0, t_ + 1):
                        steps.append(("win", kvh, kt, kt == k0, kt == t_))
                for kvh in range(2):
                    for kt in range(t_ + 1):
                        steps.append(("sel", kvh, kt, kt == 0, kt == t_))
                pend = {}

                def score(i):
                    kind, kvh, k, first, last = steps[i]
                    qv = qp[:, kvh].re("p g t -> p (g t)")
                    S = nps_s()
                    if kind == "cmp":
                        Kq = 128 * t_ - 2048 * k - 31
                        need_mask = Kq < 2032
                        mm(S[:, :], kcp[:, kvh, k * 128:(k + 1) * 128], qv, st=True, sp=not need_mask)
                        if need_mask:
                            mk = mks()
                            ts("dve", mk[:], e0[:], float(Kq), NEG, op0=ALU.is_gt, op1=ALU.mult)
                            mm(S[:, :], ident[:], mk[:], st=False, sp=True)
                    elif kind == "win":
                        madd = caus_add if k == t_ else (win_add if k == t_ - 4 else None)
                        mm(S[:, :], kwT[:, kvh, (k % NW) * 128:(k % NW + 1) * 128], qv, st=True, sp=(madd is None))
                        if madd is not None:
                            mm(S[:, :], ident[:], madd[:], st=False, sp=True)
                    else:
                        mm(S[:, :], ksT[:, kvh, k * 128:(k + 1) * 128], qv, st=True, sp=False)
                        if k < t_:
                            mm(S[:, :], expand[:, k * 128:(k + 1) * 128], selT[kvh][:].re("p g t -> p (g t)"), st=False, sp=True)
                        else:
                            mm(S[:, :], ident[:], caus_add[:], st=False, sp=True)
                    pt = pts()
                    act(pt[:], S[:, :], AF.Exp)
                    pend[i] = pt

                def finish_cmp(kvh):
                    bank = psa[kvh]
                    cp("dve", obr[kvh][0][:, :, 0:64], bank[:, 0:256].re("p (g d) -> p g d", g=4))
                    cp("act", imp4[:].re("p g j -> p (g j)"), bank[:, 256:512])
                    cp("dve", obr[kvh][0][:, :, 64], imp4[:, :, 63])
                    ts("dve", rdc[:], obr[kvh][0][:, :, 64], 1e-30, None, op0=ALU.max)
                    self.recip(rdc[:], rdc[:])
                    ts("dve", imp[:], imp4[:, 0, :], rdc[:, 0:1], None, op0=ALU.mult)
                    for g in range(1, 4):
                        stt("dve", imp[:], imp4[:, g, :], rdc[:, g:g + 1], imp[:], ALU.mult, ALU.add)
                    ms("dve", imp[:, 63:64], 0.0)
                    if t_ == 0:
                        tt("dve", imp[:, 0:2], imp[:, 0:2], fpb[:, 1:3], ALU.max)
                    else:
                        tt("dve", imp[:, 2 * t_ - 1:2 * t_ + 2], imp[:, 2 * t_ - 1:2 * t_ + 2], fpb[:, 0:3], ALU.max)
                        ms("dve", imp[:, 0:1], 3e9)
                    fw.op("dve", lambda e: e.max(out=mx1[:].a, in_=imp[:].a), reads=[imp], writes=[mx1])
                    fw.op("dve", lambda e: e.match_replace(out=imp2[:].a, in_to_replace=mx1[:].a, in_values=imp[:].a,
                                                           imm_value=-1e30), reads=[imp, mx1], writes=[imp2])
                    fw.op("dve", lambda e: e.max(out=mx2[:].a, in_=imp2[:].a), reads=[imp2], writes=[mx2])
                    ts("dve", selm[:], imp[:], mx2[:, 7:8], NEG, op0=ALU.is_lt, op1=ALU.mult)
                    ptr = npt()
                    tr(ptr[0:64, 0:128], selm[:], ident[:])
                    cp("dve", selT[kvh][:], ptr[0:64, 0:128].un(1).bc([64, 4, 128]))

                def pv(i):
                    kind, kvh, k, first, last = steps[i]
                    pt = pend.pop(i)
                    acc = psa[kvh]
                    if kind == "cmp":
                        for g in range(4):
                            mm(acc[:, g * 64:(g + 1) * 64], pt[:, g * 128:(g + 1) * 128], vcp[:, k, kvh, 0:64], st=first, sp=last)
                            mm(acc[:, 256 + g * 64:320 + g * 64], pt[:, g * 128:(g + 1) * 128], mimp[:, k, :], st=first, sp=last)
                    else:
                        vv = vwp[:, k % NW, kvh, :] if kind == "win" else vsp[:, k, kvh, :]
                        for g in range(4):
                            mm(acc[:, g * 65:(g + 1) * 65], pt[:, g * 128:(g + 1) * 128], vv, st=first, sp=last)
                    if last:
                        if kind == "cmp":
                            finish_cmp(kvh)
                        elif kind == "win":
                            cp("act", obr[kvh][2][:].re("p g d -> p (g d)"), acc[:, 0:260])
                        else:
                            cp("dve", obr[kvh][1][:].re("p g d -> p (g d)"), acc[:, 0:260])

                base = 1e9 + 1e6 * (2 * t_)
                ms("dve", fpb[0:64, 0:1], base - 1e6)
                ms("dve", fpb[0:64, 1:2], base)
                ms("dve", fpb[64:128, 1:2], base)
                ms("dve", fpb[64:128, 2:3], base + 1e6)
                LA = 2
                for i in range(min(LA, len(steps))):
                    score(i)
                iB = min(8, len(steps) - 1)
                for i in range(len(steps)):
                    if i + LA < len(steps):
                        score(i + LA)
                    pv(i)
                    if i >= 1:
                        next(mg, None)
                    if t_ + 1 < NT:
                        if i == 0:
                            self.norm_A(xp[(t_ + 1) * 128:(t_ + 2) * 128, :], xts[(t_ + 1) % 2], nb)
                        if i == iB:
                            self.norm_B(nb, hTs[(t_ + 1) % 2], ident, npt)
                for _ in mg:
                    pass
                for kvh in range(2):
                    for br in range(3):
                        ts("dve", rd[:, br, :], obr[kvh][br][:, :, 64], 1e-30, None, op0=ALU.max)
                    self.recip(rd[:].re("p b g -> p (b g)"), rd[:].re("p b g -> p (b g)"))
                    gv = gt[:, 12 * kvh:12 * kvh + 12].re("p (g b) -> p b g", b=3)
                    tt("dve", sc3[:], rd[:], gv, ALU.mult)
                    tt("dve", onsa[:], obr[kvh][0][:, :, 0:64], sc3[:, 0, :].un(2).bc([128, 4, 64]), ALU.mult)
                    for br in (1, 2):
                        tt("dve", otmp[:], obr[kvh][br][:, :, 0:64], sc3[:, br, :].un(2).bc([128, 4, 64]), ALU.mult)
                        tt("dve", onsa[:], onsa[:], otmp[:], ALU.add)
                    tt("dve", otmp[:], onsa[:], onsa[:], ALU.mult)
                    self.rsum(ss4[:], otmp[:])
                    self.rstd(rs4[:], ss4[:], 1.0 / 64)
                    tt("dve", mixin[:, 256 * kvh:256 * kvh + 256].re("p (g d) -> p g d", g=4), onsa[:],
                       rs4[:].un(2).bc([128, 4, 64]), ALU.mult)

                ptm = npt()
                for k in range(8):
                    tr(ptm[:, k * 128:(k + 1) * 128], mixin[:, k * 128:(k + 1) * 128], ident[:])
                cp("act", mT[:].re("p k t -> p (k t)"), ptm[:])
                for g in range(2):
                    ps = nps()
                    for k in range(8):
                        mm(ps[:, :], mT[:, k, :], wout_b[:, k, g * 512:(g + 1) * 512], st=(k == 0), sp=(k == 7))
                    tt("dve", x1t[:, g * 512:(g + 1) * 512], xt[:, g * 512:(g + 1) * 512], ps[:, :], ALU.add)
                dma(x1d[r0:r0 + 128, :], x1t[:])

            dma(V(None, p_m.a.rearrange("(h o) -> h o", o=1)), R[:])
            ones4 = fw.sb(s, [4, 128], F32, "ones4")
            ms("pool", ones4[:], 1.0)
            ts("dve", ones4[:], ones4[:], R[:, 0:1], None, op0=ALU.mult)
            ps = nps()
            mm(ps[:, 0:4], ones4[:], identf[0:4, 0:4])
            act(tmp4[:], ps[:, 0:4], AF.Exp, scale=-1.0)
            tt("dve", Sf[:], Sf[:], tmp4[:].un(2).bc([128, 4, 129]), ALU.mult)
            dmas(V(None, p_n.a.rearrange("h d -> d h")), Sf[:, :, 128])
            for h in range(4):
                ps = nps()
                tr(ps[:, 0:128], Sf[:, h, 0:128], identf[:])
                cp("dve", hsq[:, h, :], ps[:, 0:128])
                dma(p_C[h], hsq[:, h, :])
            fw.barrier()

    def load_w(self, dst, src, rows_chunks, ncols, stg, gcol=None, g0=0):
        for k in range(rows_chunks):
            st = stg[k % 2]
            self.dma(st[:, 0:ncols], src[k * 128:(k + 1) * 128, :])
            if k % 2 == 0:
                if gcol is None:
                    self.cp("dve", dst[:, k, :], st[:, 0:ncols])
                else:
                    self.ts("dve", dst[:, k, :], st[:, 0:ncols], gcol[:, g0 + k:g0 + k + 1], None, op0=ALU.mult)
            elif gcol is None:
                self.cp("act", dst[:, k, :], st[:, 0:ncols])
            else:
                self.act(dst[:, k, :], st[:, 0:ncols], AF.Copy, scale=gcol[:, g0 + k:g0 + k + 1])

    def pass2(self, top, L):
        fw, NT, T = self.fw, self.NT, self.T
        mm, tr, act, ts, tt, stt, cp, ms, dma, dmas = (self.mm, self.tr, self.act, self.ts, self.tt, self.stt,
                                                       self.cp, self.ms, self.dma, self.dmas)
        ident, nps, npt, psa = L["ident"], L["nps"], L["npt"], L["psa"]
        x1d, x2d, y_p = L["x1d"], L["x2d"], L["y_p"]
        g2 = fw.sb(top, [128, 32], F32, "g2")
        for i_, g_ in enumerate((L["g_xa"], L["g_mem"], L["g_ffn"])):
            dmas(g2[:, 8 * i_:8 * i_ + 8], V(None, g_.a.rearrange("(k p) -> p k", p=128)))
        with ExitStack() as s:
            wxq_b = fw.sb(s, [128, 8, D], BF16, "wxq_b")
            wxo_b = fw.sb(s, [128, 8, D], BF16, "wxo_b")
            mkT = fw.sb(s, [128, 8, 256], BF16, "mkT")
            mvp = fw.sb(s, [128, 2, 4, 257], BF16, "mvp")
            ms("pool", mvp[:], 1.0)
            xts = [fw.sb(s, [128, D], F32, f"x2_{i}") for i in range(2)]
            nb = self.norm_bufs(s, "2")
            hT = fw.sb(s, [128, 8, 128], BF16, "hT2")
            with ExitStack() as s2:
                stg = [fw.sb(s2, [128, D], F32, f"stg2{i}") for i in range(2)]
                wxk_b = fw.sb(s2, [128, 8, D], BF16, "wxk_b")
                wxv_b = fw.sb(s2, [128, 8, D], BF16, "wxv_b")
                self.load_w(wxq_b, L["w_xq"], 8, D, stg, g2, 0)
                self.load_w(wxo_b, L["w_xo"], 8, D, stg)
                self.load_w(wxk_b, L["w_xk"], 8, D, stg, g2, 8)
                self.load_w(wxv_b, L["w_xv"], 8, D, stg, g2, 8)
                mo = fw.sb(s2, [128, D], F32, "mo")
                for mt in range(2):
                    xt = xts[mt % 2]
                    self.norm_T(L["memp"][mt * 128:(mt + 1) * 128, :], xt, nb, hT, ident, npt)
                    for wi, (wb_, po) in enumerate(((wxk_b, L["p_mk"]), (wxv_b, L["p_mv"]))):
                        for g in range(2):
                            ps = nps()
                            for k in range(8):
                                mm(ps[:, :], hT[:, k, :], wb_[:, k, g * 512:(g + 1) * 512], st=(k == 0), sp=(k == 7))
                            cp("dve" if g == 0 else "act", mo[:, g * 512:(g + 1) * 512], ps[:, :])
                        dma(po[mt * 128:(mt + 1) * 128, :], mo[:])
                        if wi == 1:
                            cp("pool", mvp[:, mt, :, 0:256], mo[:].re("p (h d) -> p h d", h=4))
                    for c4 in range(2):
                        ps = nps()
                        for cc in range(4):
                            c = c4 * 4 + cc
                            for k in range(8):
                                mm(ps[:, cc * 128:(cc + 1) * 128], wxk_b[:, k, c * 128:(c + 1) * 128], hT[:, k, :],
                                   st=(k == 0), sp=(k == 7))
                        cp("dve", mkT[:, c4 * 4:c4 * 4 + 4, mt * 128:(mt + 1) * 128], ps[:, :].re("p (c t) -> p c t", c=4))
                fw.barrier()
            qxT = fw.sb(s, [128, 8, 128], BF16, "qxT")
            pts = [fw.sb(s, [128, 512], BF16, f"pxt{i}") for i in range(2)]
            ox = fw.sb(s, [128, D], BF16, "ox")
            oxT = fw.sb(s, [128, 8, 128], BF16, "oxT")
            rdx = fw.sb(s, [128, 1], F32, "rdx")
            x2t = fw.sb(s, [128, D], F32, "x2t")
            for t_ in range(NT):
                xt = xts[t_ % 2]
                r0 = t_ * 128
                self.norm_T(x1d[r0:r0 + 128, :], xt, nb, hT, ident, npt)
                for c4 in range(2):
                    ps = nps()
                    for cc in range(4):
                        c = c4 * 4 + cc
                        for k in range(8):
                            mm(ps[:, cc * 128:(cc + 1) * 128], wxq_b[:, k, c * 128:(c + 1) * 128], hT[:, k, :],
                               st=(k == 0), sp=(k == 7))
                    act(qxT[:, c4 * 4:c4 * 4 + 4, :].re("p c t -> p (c t)"), ps[:, :], AF.Copy, scale=1.0 / 16)
                for mt in range(2):
                    S = nps()
                    for h in range(4):
                        for hf in range(2):
                            mm(S[:, h * 128:(h + 1) * 128], mkT[:, 2 * h + hf, mt * 128:(mt + 1) * 128], qxT[:, 2 * h + hf, :],
                               st=(hf == 0), sp=(hf == 1))
                    act(pts[mt][:], S[:, :], AF.Exp)
                for h in range(4):
                    acc = psa[h % 2]
                    for mt in range(2):
                        mm(acc[:, 0:257], pts[mt][:, h * 128:(h + 1) * 128], mvp[:, mt, h, :], st=(mt == 0), sp=(mt == 1))
                    self.recip(rdx[:], acc[:, 256:257])
                    ts("dve", ox[:, h * 256:(h + 1) * 256], acc[:, 0:256], rdx[:, 0:1], None, op0=ALU.mult)
                pto = npt()
                for k in range(8):
                    tr(pto[:, k * 128:(k + 1) * 128], ox[:, k * 128:(k + 1) * 128], ident[:])
                cp("act", oxT[:].re("p k t -> p (k t)"), pto[:])
                for g in range(2):
                    ps = nps()
                    for k in range(8):
                        mm(ps[:, :], oxT[:, k, :], wxo_b[:, k, g * 512:(g + 1) * 512], st=(k == 0), sp=(k == 7))
                    tt("dve", x2t[:, g * 512:(g + 1) * 512], xt[:, g * 512:(g + 1) * 512], ps[:, :], ALU.add)
                dma(x2d[r0:r0 + 128, :], x2t[:])
            if self.sample:
                S = self.S
                xt = xts[0]
                self.norm_T(S["x1s"][:, :], xt, nb, hT, ident, npt, rows=16)
                qxs = fw.sb(s, [128, 8, 16], BF16, "qxs")
                ps = nps()
                for c in range(8):
                    for k in range(8):
                        mm(ps[:, c * 16:(c + 1) * 16], wxq_b[:, k, c * 128:(c + 1) * 128], hT[:, k, 0:16], st=(k == 0), sp=(k == 7))
                act(qxs[:].re("p c t -> p (c t)"), ps[:, 0:128], AF.Copy, scale=1.0 / 16)
                ms("pool", ox[:], 0.0)
                msg = fw.sb(s, [128, 2, D], F32, "msg")
                mkb = fw.sb(s, [128, 2, D], BF16, "mkb")
                ptx = [fw.sb(s, [128, 16], BF16, f"ptx{i}") for i in range(2)]
                oxb = fw.sb(s, [4, D], BF16, "oxb")
                for b in range(4):
                    dma(msg[:], V(None, S["cmk"].a[b].rearrange("(t p) f -> p t f", p=128)))
                    cp("dve", mkb[:, 0, :], msg[:, 0, :])
                    cp("pool", mkb[:, 1, :], msg[:, 1, :])
                    for mt in range(2):
                        pt = npt()
                        for c in range(8):
                            tr(pt[:, c * 128:(c + 1) * 128], mkb[:, mt, c * 128:(c + 1) * 128], ident[:])
                        cp("act", mkT[:, :, mt * 128:(mt + 1) * 128], pt[:, :].re("p (c t) -> p c t", c=8))
                    dma(msg[:], V(None, S["cmv"].a[b].rearrange("(t p) f -> p t f", p=128)))
                    for mt in range(2):
                        cp("dve" if mt == 0 else "pool", mvp[:, mt, :, 0:256], msg[:, mt, :].re("p (h d) -> p h d", h=4))
                    for mt in range(2):
                        Sx = nps()
                        for h in range(4):
                            for hf in range(2):
                                mm(Sx[:, h * 4:(h + 1) * 4], mkT[:, 2 * h + hf, mt * 128:(mt + 1) * 128],
                                   qxs[:, 2 * h + hf, 4 * b:4 * b + 4], st=(hf == 0), sp=(hf == 1))
                        act(ptx[mt][:], Sx[:, 0:16], AF.Exp)
                    for h in range(4):
                        acc = psa[h % 2]
                        for mt in range(2):
                            mm(acc[0:4, 0:257], ptx[mt][:, 4 * h:4 * h + 4], mvp[:, mt, h, :], st=(mt == 0), sp=(mt == 1))
                        self.recip(rdx[0:4, :], acc[0:4, 256:257])
                        ts("dve", oxb[:, h * 256:(h + 1) * 256], acc[0:4, 0:256], rdx[0:4, 0:1], None, op0=ALU.mult)
                    dma(ox[4 * b:4 * b + 4, :], oxb[:])
                pto = npt()
                for k in range(8):
                    tr(pto[:, k * 128:(k + 1) * 128], ox[:, k * 128:(k + 1) * 128], ident[:])
                cp("act", oxT[:].re("p k t -> p (k t)"), pto[:])
                for g in range(2):
                    ps = nps()
                    for k in range(8):
                        mm(ps[:, :], oxT[:, k, :], wxo_b[:, k, g * 512:(g + 1) * 512], st=(k == 0), sp=(k == 7))
                    tt("dve", x2t[:, g * 512:(g + 1) * 512], xt[:, g * 512:(g + 1) * 512], ps[:, :], ALU.add)
                dma(S["x2s"][:, :], x2t[0:16, :])
            fw.barrier()
        with ExitStack() as s:
            wg_b = fw.sb(s, [128, 8, DFF], BF16, "wg_b")
            wu_b = fw.sb(s, [128, 8, DFF], BF16, "wu_b")
            wd_b = fw.sb(s, [128, 22, D], BF16, "wd_b")
            gfin = fw.sb(s, [128, D], F32, "gfin")
            dma(gfin[:], V(None, L["g_final"].a.partition_broadcast(128)))
            with ExitStack() as s2:
                stg = [fw.sb(s2, [128, DFF], F32, f"stg3{i}") for i in range(2)]
                self.load_w(wg_b, L["w_gate"], 8, DFF, stg, g2, 16)
                self.load_w(wu_b, L["w_up"], 8, DFF, stg, g2, 16)
                self.load_w(wd_b, L["w_down"], 22, D, stg)
                fw.barrier()
            xts = [fw.sb(s, [128, D], F32, f"x3_{i}") for i in range(2)]
            nb = self.norm_bufs(s, "3")
            hT = fw.sb(s, [128, 8, 128], BF16, "hT3")
            aT = fw.sb(s, [128, 22, 128], BF16, "aT")
            sg = fw.sb(s, [128, 512], F32, "sg")
            x3t = fw.sb(s, [128, D], F32, "x3t")
            tiles = [(x2d[t_ * 128:(t_ + 1) * 128, :], y_p[t_ * 128:(t_ + 1) * 128, :], 128) for t_ in range(NT)]
            if self.sample:
                tiles.append((self.S["x2s"][:, :], self.S["y_s"], 16))
            for t_, (src_, dst_, rows_) in enumerate(tiles):
                xt = xts[t_ % 2]
                self.norm_T(src_, xt, nb, hT, ident, npt, rows=rows_)
                for c0 in range(0, 22, 4):
                    n = min(4, 22 - c0)
                    pg = nps()
                    pu = nps()
                    for cc in range(n):
                        c = c0 + cc
                        for k in range(8):
                            mm(pg[:, cc * 128:(cc + 1) * 128], wg_b[:, k, c * 128:(c + 1) * 128], hT[:, k, :], st=(k == 0), sp=(k == 7))
                        for k in range(8):
                            mm(pu[:, cc * 128:(cc + 1) * 128], wu_b[:, k, c * 128:(c + 1) * 128], hT[:, k, :], st=(k == 0), sp=(k == 7))
                    act(sg[:, 0:n * 128], pg[:, 0:n * 128], AF.Silu)
                    tt("dve", aT[:, c0:c0 + n, :].re("p c t -> p (c t)"), sg[:, 0:n * 128], pu[:, 0:n * 128], ALU.mult)
                for g in range(2):
                    ps = nps()
                    for c in range(22):
                        mm(ps[:, :], aT[:, c, :], wd_b[:, c, g * 512:(g + 1) * 512], st=(c == 0), sp=(c == 21))
                    tt("dve", x3t[:, g * 512:(g + 1) * 512], xt[:, g * 512:(g + 1) * 512], ps[:, :], ALU.add)
                ms("dve", nb["ss"][:], 0.0)
                act(nb["junk"][:], x3t[:], AF.Square, acc=nb["ss"][:])
                self.rstd(nb["rs"][:], nb["ss"][:], 1.0 / D)
                stt("dve", x3t[:], x3t[:], nb["rs"][:, 0:1], gfin[:], ALU.mult, ALU.mult)
                dma(dst_, x3t[0:rows_, :])
            fw.barrier()


def sample_io(self):
    din, dout = self.din, self.dout
    S = {"xs": din("xs", [16, D]), "ptab": din("ptab", [4, 128], I32)}
    for n in ("pool_kc", "pool_vc", "pool_ks", "pool_vs"):
        S[n] = din(n, [5120, 16384])
    S["stk"] = din("stk", [4, 512, 128])
    S["stv"] = din("stv", [4, 512, 128])
    S["sconv"] = din("sconv", [4, 3, 512])
    S["sC"] = din("sC", [4, 4, 128, 128])
    S["sn"] = din("sn", [4, 4, 128])
    S["sm"] = din("sm", [16])
    S["cmk"] = din("cmk", [4, 256, D])
    S["cmv"] = din("cmv", [4, 256, D])
    S["y_s"] = dout("y_s", [16, D])
    for n in ("s_kc", "s_vc", "s_ks", "s_vs"):
        S[n] = dout(n, [16, 128])
    S["s_kw"] = dout("s_kw", [4, 512, 128])
    S["s_vw"] = dout("s_vw", [4, 512, 128])
    S["s_C"] = dout("s_C", [4, 4, 128, 128])
    S["s_n"] = dout("s_n", [4, 4, 128])
    S["s_m"] = dout("s_m", [16])
    S["s_conv"] = dout("s_conv", [4, 3, 512])
    S["kcS_d"] = self.dscr("kcS_d", [4, 2, 64, 1024], BF16)
    S["vcS_d"] = self.dscr("vcS_d", [4, 128, 8, 2, 64], BF16)
    S["x1s"] = self.dscr("x1s", [16, D])
    S["x2s"] = self.dscr("x2s", [16, D])
    self.S = S
    return S


def load_cmp(self, s, kv, cmp_in, nps):
    fw = self.fw
    mm, tt, cp, dma, dmas = self.mm, self.tt, self.cp, self.dma, self.dmas
    pe, w1, b1, w2 = cmp_in[kv]
    w1b = fw.sb(s, [64, 32, 256], BF16, "Sw1b" + kv)
    with ExitStack() as t:
        w1s = [fw.sb(t, [64, 8, 256], F32, f"Sw1s{kv}{i}") for i in range(2)]
        for jb in range(4):
            st = w1s[jb % 2]
            dma(st[:], V(None, w1.a.rearrange("(j d) n -> d j n", d=64)[:, jb * 8:(jb + 1) * 8, :]))
            cp("dve", w1b[:, jb * 8:(jb + 1) * 8, :], st[:])
        fw.barrier()
    peT = fw.sb(s, [64, 32], F32, "SpeT" + kv)
    dmas(peT[:], V(None, pe.a.rearrange("j d -> d j")))
    peTb = fw.sb(s, [64, 32], BF16, "SpeTb" + kv)
    cp("dve", peTb[:], peT[:])
    b1c = fw.sb(s, [128, 2], F32, "Sb1c" + kv)
    dmas(b1c[:], V(None, b1.a.rearrange("(c p) -> p c", p=128)))
    w2s = fw.sb(s, [128, 2, 64], F32, "Sw2s" + kv)
    dma(w2s[:], V(None, w2.a.rearrange("(c p) n -> p c n", p=128)))
    w2b = fw.sb(s, [128, 2, 64], BF16, "Sw2b" + kv)
    cp("dve", w2b[:], w2s[:])
    cst = fw.sb(s, [128, 2], F32, "Scst" + kv)
    for hc in range(2):
        ps = nps()
        for j in range(32):
            mm(ps[:, 0:1], w1b[:, j, hc * 128:(hc + 1) * 128], peTb[:, j:j + 1], st=(j == 0), sp=(j == 31))
        tt("dve", cst[:, hc:hc + 1], ps[:, 0:1], b1c[:, hc:hc + 1], ALU.add)
    return w1b, w2b, cst


def gather(self, dst, pool, idx, r0):
    self.fw.dma("pool", dst, pool, extra_reads=[idx.b],
                fn=lambda e: e.indirect_dma_start(out=dst.a, out_offset=None, in_=pool.a,
                                                  in_offset=bass.IndirectOffsetOnAxis(ap=idx.a, axis=0),
                                                  element_offset=r0 * 128))


def sample_s0(self, L):
    fw, S = self.fw, self.S
    mm, tr, act, cp, ms, dma = self.mm, self.tr, self.act, self.cp, self.ms, self.dma
    ident, nps, npt, cmp_in = L["ident"], L["nps"], L["npt"], L["cmp_in"]
    with ExitStack() as s:
        idx = [fw.sb(s, [128, 1], I32, f"S0idx{b}") for b in range(4)]
        for b in range(4):
            dma(idx[b][:], V(None, S["ptab"].a[b].rearrange("(p o) -> p o", o=1)))
        cw_ = {kv: load_cmp(self, s, kv, cmp_in, nps) for kv in "kv"}
        srcS = fw.sb(s, [64, 2, 128, 129], BF16, "srcS")
        ms("pool", srcS[:, :, :, 128:129], 0.0)
        gch = fw.sb(s, [128, 4096], F32, "gch")
        gbf = fw.sb(s, [128, 32, 128], BF16, "gbf")
        gT = fw.sb(s, [128, 2, 1024], BF16, "SgT")
        ko = fw.sb(s, [64, 1024], BF16, "Sko")
        vo = fw.sb(s, [128, 8, 64], BF16, "Svo")
        for b in range(4):
            for kv in "kv":
                w1b, w2b, cst = cw_[kv]
                pool = S["pool_" + kv + "c"]
                for i in range(4):
                    gather(self, gch[:], pool, idx[b][:, :], 32 * i)
                    cp("dve", gbf[:, 0:16, :].re("p r f -> p (r f)"), gch[:, 0:2048])
                    cp("act", gbf[:, 16:32, :].re("p r f -> p (r f)"), gch[:, 2048:4096])
                    for kvh in range(2):
                        for g8 in range(4):
                            pt = npt()
                            for r8 in range(8):
                                tr(pt[0:64, r8 * 128:(r8 + 1) * 128], gbf[:, 8 * g8 + r8, kvh * 64:(kvh + 1) * 64], ident[:])
                            cp("act" if g8 % 2 == 0 else "dve", srcS[:, kvh, 32 * i + 8 * g8:32 * i + 8 * g8 + 8, 0:128],
                               pt[0:64, :].re("p (r t) -> p r t", r=8))
                for kvh in range(2):
                    for hc in range(2):
                        wv = lambda j: w1b[:, j, hc * 128:(hc + 1) * 128]
                        for bank in range(2):
                            ps = nps()
                            o4 = ps[:, :].re("p (a t) -> p a t", a=4)
                            for j in range(32):
                                if j < 16:
                                    r0 = 64 * bank + j
                                    mm(o4, wv(j), srcS[:, kvh, r0:r0 + 49:16, 0:128], st=(j == 0), sp=False)
                                elif bank == 0:
                                    r0 = 16 + (j - 16)
                                    mm(o4, wv(j), srcS[:, kvh, r0:r0 + 49:16, 0:128], st=False, sp=(j == 31))
                                else:
                                    r0 = 80 + (j - 16)
                                    mm(o4[:, 0:3, :], wv(j), srcS[:, kvh, r0:r0 + 33:16, 0:128], st=False, sp=False)
                                    mm(ps[:, 384:512], wv(j), srcS[:, kvh, j - 16, 1:129], st=False, sp=(j == 31))
                            act(gT[:, hc, bank * 512:(bank + 1) * 512], ps[:, :], AF.Gelu_apprx_tanh, bias=cst[:, hc:hc + 1])
                    if kv == "k":
                        for bank in range(2):
                            ps = nps()
                            for hc in range(2):
                                mm(ps[0:64, :], w2b[:, hc, :], gT[:, hc, bank * 512:(bank + 1) * 512], st=(hc == 0), sp=(hc == 1))
                            cp("dve", ko[:, bank * 512:(bank + 1) * 512], ps[0:64, :])
                        dma(S["kcS_d"][b, kvh], ko[:])
                    else:
                        ps = nps()
                        for rb in range(8):
                            for hc in range(2):
                                mm(ps[:, rb * 64:(rb + 1) * 64], gT[:, hc, rb * 128:(rb + 1) * 128], w2b[:, hc, :],
                                   st=(hc == 0), sp=(hc == 1))
                        cp("dve", vo[:].re("p r d -> p (r d)"), ps[:, :])
                        dma(S["vcS_d"][b][:, :, kvh, :], vo[:])
        fw.barrier()


Builder.sample_io = sample_io

def sample_pass1(self, L):
    fw, S = self.fw, self.S
    mm, tr, act, ts, tt, stt, cp, ms, iota, dma, dmas = (self.mm, self.tr, self.act, self.ts, self.tt, self.stt,
                                                         self.cp, self.ms, self.iota, self.dma, self.dmas)
    win_b, wout_b, wqm_b, wkm_b = L["win_b"], L["wout_b"], L["wqm_b"], L["wkm_b"]
    ident, identf, nps, npt, psa = L["ident"], L["identf"], L["nps"], L["npt"], L["psa"]
    cw, cb, bgate, bif, tmpf = L["cw"], L["cb"], L["bgate"], L["bif"], L["tmpf"]
    put_row = self.put_row
    KSC = 128.0 ** -0.5
    with ExitStack() as s:
        xt = fw.sb(s, [128, D], F32, "xS")
        nb = self.norm_bufs(s, "S")
        hT = fw.sb(s, [128, 8, 128], BF16, "hTS")
        pkv = fw.sb(s, [128, 792], F32, "pkvS")
        gt = fw.sb(s, [128, 24], F32, "gtS")
        vnS = fw.sb(s, [16, 2, 2, 65], BF16, "vnS")
        qTs = fw.sb(s, [64, 8, 16], BF16, "qTs")
        mixin = fw.sb(s, [128, D], BF16, "mixinS")
        s2 = ExitStack()
        QS = fw.sb(s2, [68, 4, 2, 16], BF16, "QS")
        kcSb = fw.sb(s2, [68, 2, 8, 128], BF16, "kcSb")
        vcSb = fw.sb(s2, [128, 8, 2, 65], BF16, "vcSb")
        ksS = fw.sb(s2, [68, 2, 32, 128], BF16, "ksS")
        kwS = fw.sb(s2, [68, 2, 4, 128], BF16, "kwS")
        KnS = fw.sb(s2, [68, 2, 2, 16], BF16, "KnS")
        ms("pool", vcSb[:], 1.0)
        with ExitStack() as tmps:
            self.rowt = fw.sb(tmps, [1, 4096], F32, "rowtS")
            self.rowb = fw.sb(tmps, [1, 4096], BF16, "rowbS")
            sr = fw.sb(tmps, [1, 2, 4, 4], F32, "srS")
            for h in range(8):
                ms("pool", sr[0:1, h // 4, h % 4, :], 2.0 ** (-(h + 1)))
            qi = fw.sb(tmps, [1, 2, 4, 4], F32, "qiS")
            iota(qi[:].re("p k g q -> p (k g) q"), [[0, 8], [1, 4]], base=0, cm=0)
            rw = fw.sb(tmps, [1, 4, 2, 16], F32, "rwS")
            rwb = fw.sb(tmps, [1, 4, 2, 16], BF16, "rwbS")
            srv = sr[:].re("p k g q -> p k (g q)")
            for row in range(4):
                for i in range(4):
                    if row == 0:
                        ts("pool", rw[0:1, i], srv, -1.0, None, op0=ALU.mult)
                    elif row == 1:
                        cp("pool", rw[0:1, i], srv)
                    elif row == 2:
                        tt("pool", rw[0:1, i], srv, qi[:].re("p k g q -> p k (g q)"), ALU.mult)
                        ts("pool", rw[0:1, i], rw[0:1, i], -1.0, None, op0=ALU.mult)
                    else:
                        ts("pool", rw[0:1, i], srv, 32.0 * i, None, op0=ALU.mult)
                cp("pool", rwb[:], rw[:])
                dma(QS[64 + row:65 + row], rwb[:])
            for kvh in range(2):
                put_row(kcSb[64:65, kvh].re("p r t -> p (r t)"), [[0, 8], [-128, 128]], 16384, 1024)
                put_row(kcSb[65:66, kvh].re("p r t -> p (r t)"), [[16, 8], [0, 128]], 31, 1024)
                put_row(kcSb[66:67, kvh].re("p r t -> p (r t)"), None, 0, 1024, const=1.0)
                put_row(kcSb[67:68, kvh].re("p r t -> p (r t)"), None, 0, 1024, const=0.0)
                put_row(ksS[64:65, kvh].re("p r t -> p (r t)"), [[0, 32], [-128, 128]], 16384, 4096)
                put_row(ksS[65:66, kvh].re("p r t -> p (r t)"), [[1, 32], [0, 128]], 0, 4096)
                put_row(ksS[66:67, kvh].re("p r t -> p (r t)"), None, 0, 4096, const=1.0)
                put_row(ksS[67:68, kvh].re("p r t -> p (r t)"), None, 0, 4096, const=1.0)
                put_row(kwS[64:65, kvh].re("p r t -> p (r t)"), [[-128, 4], [0, 128]], 512, 512)
                put_row(kwS[65:66, kvh].re("p r t -> p (r t)"), [[0, 4], [1, 128]], 0, 512)
                put_row(kwS[66:67, kvh].re("p r t -> p (r t)"), None, 0, 512, const=1.0)
                put_row(kwS[67:68, kvh].re("p r t -> p (r t)"), None, 0, 512, const=0.0)
                for sw_ in range(2):
                    put_row(KnS[64:65, sw_, kvh], None, 0, 16, const=0.0)
                    put_row(KnS[65:66, sw_, kvh], [[0, 4], [1, 4]], 0, 16)
                    put_row(KnS[66:67, sw_, kvh], None, 0, 16, const=1.0)
                    put_row(KnS[67:68, sw_, kvh], None, 0, 16, const=0.0)
            fw.barrier()
        maskC7 = fw.sb(s2, [128, 1], F32, "maskC7")
        iota(tmpf[:, 0:1], [[0, 1]], base=0, cm=1)
        ts("pool", maskC7[:], tmpf[:, 0:1], 127.0, None, op0=ALU.is_lt)
        winm0 = fw.sb(s2, [128, 4], BF16, "winm0")
        iota(tmpf[:, 0:4], [[-1, 4]], base=0, cm=1)
        ts("pool", winm0[:], tmpf[:, 0:4], 0.0, None, op0=ALU.is_gt)
        newm = fw.sb(s2, [16, 4, 4], BF16, "newm")
        for b in range(4):
            iota(tmpf[0:16, 0:4], [[-1, 4]], base=-4 * b, cm=1)
            ts("pool", tmpf[0:16, 4:8], tmpf[0:16, 0:4], 0.0, None, op0=ALU.is_le)
            iota(tmpf[0:16, 8:12], [[0, 4]], base=-4 * b, cm=1)
            ts("pool", tmpf[0:16, 8:12], tmpf[0:16, 8:12], 0.0, None, op0=ALU.is_ge)
            tt("pool", newm[:, b, :], tmpf[0:16, 4:8], tmpf[0:16, 8:12], ALU.mult)
        mimpS = fw.sb(s2, [128, 8, 256], BF16, "mimpS")
        mtmp = fw.sb(s2, [128, 3, 256], F32, "mtmp")
        for rb in range(8):
            iota(mtmp[:, 0, :], [[-4, 256]], base=rb - 1, cm=8)
            stt("dve", mtmp[:, 1, :], mtmp[:, 0, :], -1.0, mtmp[:, 0, :], ALU.mult, ALU.max)
            ts("pool", mtmp[:, 0, :], mtmp[:, 1, :], 2.0, 0.5, op0=ALU.is_le, op1=ALU.mult)
            ts("pool", mtmp[:, 2, :], mtmp[:, 1, :], 1.0, 0.5, op0=ALU.is_le, op1=ALU.mult)
            tt("pool", mimpS[:, rb, :], mtmp[:, 0, :], mtmp[:, 2, :], ALU.add)
        idx = [fw.sb(s2, [128, 1], I32, f"S1idx{b}") for b in range(4)]
        for b in range(4):
            dma(idx[b][:], V(None, S["ptab"].a[b].rearrange("(p o) -> p o", o=1)))

        self.norm_T(S["xs"], xt, nb, hT, ident, npt, rows=16)
        psA, psB = nps(), nps()
        for k in range(8):
            mm(psA[:, 0:512], hT[:, k, :], win_b[:, k, 512:1024], st=(k == 0), sp=(k == 7))
        for k in range(8):
            mm(psB[:, 0:280], hT[:, k, :], win_b[:, k, 1024:1304], st=(k == 0), sp=(k == 7))
        cp("dve", pkv[:, 0:512], psA[:, 0:512])
        cp("act", pkv[:, 512:792], psB[:, 0:280])
        for i_, n_ in enumerate(("s_kc", "s_vc", "s_ks", "s_vs")):
            dma(S[n_], pkv[0:16, i_ * 128:(i_ + 1) * 128])
        tt("dve", gt[:], pkv[:, 768:792], bgate[:], ALU.add)
        self.sigm(gt[:], gt[:])
        ms("pool", vnS[:], 1.0)
        cp("dve", vnS[:, 0, :, 0:64], pkv[0:16, 384:512].re("p (h d) -> p h d", h=2))
        cp("dve", vnS[:, 1, :, 0:64], pkv[0:16, 640:768].re("p (h d) -> p h d", h=2))
        psQ = nps()
        for h in range(8):
            for k in range(8):
                mm(psQ[0:64, h * 16:(h + 1) * 16], win_b[:, k, 64 * h:64 * h + 64], hT[:, k, 0:16], st=(k == 0), sp=(k == 7))
        act(qTs[:].re("p h t -> p (h t)"), psQ[0:64, 0:128], AF.Copy, scale=0.125)
        psK = nps()
        for gi, c0 in enumerate((768, 832, 1024, 1088)):
            for k in range(8):
                mm(psK[0:64, gi * 16:(gi + 1) * 16], win_b[:, k, c0:c0 + 64], hT[:, k, 0:16], st=(k == 0), sp=(k == 7))
        cp("dve", KnS[0:64].re("p a k t -> p (a k t)"), psK[0:64, 0:64])

        gk = fw.sb(s2, [128, 4096], F32, "gk")
        gv = fw.sb(s2, [128, 4096], F32, "gv")
        kb = fw.sb(s2, [128, 32, 128], BF16, "kbS")
        vbp = fw.sb(s2, [128, 32, 2, 65], BF16, "vbp")
        ms("pool", vbp[:], 1.0)
        vwS = fw.sb(s2, [128, 4, 2, 65], BF16, "vwS")
        ms("pool", vwS[:], 1.0)
        ptc = fw.sb(s2, [128, 8, 16], BF16, "ptc")
        ptsb = fw.sb(s2, [128, 32, 16], BF16, "ptsb")
        ptw = fw.sb(s2, [128, 4, 16], BF16, "ptw")
        ptn = fw.sb(s2, [16, 16], BF16, "ptn")
        maskEO = [fw.sb(s2, [128, 2, 4, 4], BF16, f"maskEO{k}") for k in range(2)]
        obr = [[fw.sb(s2, [4, 4, 65], F32, f"obrS{k}{i}") for i in range(3)] for k in range(2)]
        imp4 = fw.sb(s2, [4, 4, 256], F32, "imp4S")
        imp = fw.sb(s2, [4, 256], F32, "impS")
        imp2 = fw.sb(s2, [4, 256], F32, "imp2S")
        mx1 = fw.sb(s2, [4, 8], F32, "mx1S")
        mx2 = fw.sb(s2, [4, 8], F32, "mx2S")
        sel01 = fw.sb(s2, [4, 256], F32, "sel01")
        selT = fw.sb(s2, [128, 8], F32, "selTS")
        gtb = fw.sb(s2, [4, 24], F32, "gtb")
        rd = fw.sb(s2, [4, 3, 4], F32, "rdS")
        sc3 = fw.sb(s2, [4, 3, 4], F32, "sc3S")
        onsa = fw.sb(s2, [4, 4, 64], F32, "onsaS")
        otmp = fw.sb(s2, [4, 4, 64], F32, "otmpS")
        ss4 = fw.sb(s2, [4, 4], F32, "ss4S")
        rs4 = fw.sb(s2, [4, 4], F32, "rs4S")
        onb = fw.sb(s2, [4, 512], BF16, "onbS")
        ms("pool", mixin[:], 0.0)
        for b in range(4):
            cp("dve", QS[0:64].re("p i k (g q) -> p i (k g) q", g=4),
               qTs[:, :, 4 * b:4 * b + 4].un(1).bc([64, 4, 8, 4]))
            dma(kcSb[0:64].re("p k r t -> p k (r t)"), V(S["kcS_d"], S["kcS_d"][b].a.rearrange("k d n -> d k n")))
            dma(vcSb[:].re("p r k d -> p (r k) d")[:, :, 0:64], V(S["vcS_d"], S["vcS_d"][b].a.rearrange("p r k d -> p (r k) d")))
            dma(gtb[:], gt[4 * b:4 * b + 4, :])
            for kvh in range(2):
                Sc = nps()
                for rb in range(8):
                    mm(Sc[:, rb * 16:(rb + 1) * 16], kcSb[:, kvh, rb, :], QS[:, 0, kvh, :])
                act(ptc[:].re("p r c -> p (r c)"), Sc[:, 0:128], AF.Exp)
                ts("dve", ptc[:, 7, :], ptc[:, 7, :], maskC7[:, 0:1], None, op0=ALU.mult)
                accC = psa[0]
                impP = [nps(), nps()]
                for rb in range(8):
                    for g in range(4):
                        mm(accC[0:4, g * 65:(g + 1) * 65], ptc[:, rb, 4 * g:4 * g + 4], vcSb[:, rb, kvh, :],
                           st=(rb == 0), sp=(rb == 7))
                        mm(impP[g // 2][0:4, (g % 2) * 256:(g % 2) * 256 + 256], ptc[:, rb, 4 * g:4 * g + 4],
                           mimpS[:, rb, :], st=(rb == 0), sp=(rb == 7))
                cp("dve", obr[kvh][0][:].re("p g d -> p (g d)"), accC[0:4, 0:260])
                for hh in range(2):
                    cp("act", imp4[:, 2 * hh:2 * hh + 2, :].re("p g j -> p (g j)"), impP[hh][0:4, 0:512])
                ts("dve", rd[:, 0, :], obr[kvh][0][:, :, 64], 1e-30, None, op0=ALU.max)
                self.recip(rd[:, 0, :], rd[:, 0, :])
                ts("dve", imp[:], imp4[:, 0, :], rd[:, 0, 0:1], None, op0=ALU.mult)
                for g in range(1, 4):
                    stt("dve", imp[:], imp4[:, g, :], rd[:, 0, g:g + 1], imp[:], ALU.mult, ALU.add)
                ms("dve", imp[:, 0:1], 3e9)
                ms("dve", imp[:, 255:256], 1e9)
                fw.op("dve", lambda e: e.max(out=mx1[:].a, in_=imp[:].a), reads=[imp], writes=[mx1])
                fw.op("dve", lambda e: e.match_replace(out=imp2[:].a, in_to_replace=mx1[:].a, in_values=imp[:].a,
                                                       imm_value=-1e30), reads=[imp, mx1], writes=[imp2])
                fw.op("dve", lambda e: e.max(out=mx2[:].a, in_=imp2[:].a), reads=[imp2], writes=[mx2])
                ts("dve", sel01[:], imp[:], mx2[:, 6:7], None, op0=ALU.is_ge)
                psT = nps()
                tr(psT[:, 0:4], sel01[0:4, 0:256:2], identf[0:4, 0:4])
                tr(psT[:, 4:8], sel01[0:4, 1:256:2], identf[0:4, 0:4])
                cp("dve", selT[:], psT[:, 0:8])
                cp("dve", maskEO[kvh][:], selT[:].re("p (e q) -> p e q", e=2).un(2).bc([128, 2, 4, 4]))
            accS = [psa[0], psa[1]]
            for i in range(4):
                gather(self, gk[:], S["pool_ks"], idx[b][:, :], 32 * i)
                gather(self, gv[:], S["pool_vs"], idx[b][:, :], 32 * i)
                cp("dve", kb[:, 0:16, :].re("p r f -> p (r f)"), gk[:, 0:2048])
                cp("act", kb[:, 16:32, :].re("p r f -> p (r f)"), gk[:, 2048:4096])
                cp("dve", vbp[:, 0:16, :, 0:64], gv[:, 0:2048].re("p (r h d) -> p r h d", r=16, h=2))
                cp("act", vbp[:, 16:32, :, 0:64], gv[:, 2048:4096].re("p (r h d) -> p r h d", r=16, h=2))
                for kvh in range(2):
                    for g8 in range(4):
                        pt = npt()
                        for r8 in range(8):
                            tr(pt[0:64, r8 * 128:(r8 + 1) * 128], kb[:, 8 * g8 + r8, kvh * 64:(kvh + 1) * 64], ident[:])
                        cp("act" if g8 % 2 == 0 else "dve", ksS[0:64, kvh, 8 * g8:8 * g8 + 8, :],
                           pt[0:64, :].re("p (r t) -> p r t", r=8))
                for kvh in range(2):
                    Ss = nps()
                    for r_ in range(32):
                        mm(Ss[:, r_ * 16:(r_ + 1) * 16], ksS[:, kvh, r_, :], QS[:, i, kvh, :])
                    act(ptsb[:].re("p r c -> p (r c)"), Ss[:, :], AF.Exp)
                    tt("dve", ptsb[:].re("p r (g q) -> p r g q", g=4), ptsb[:].re("p r (g q) -> p r g q", g=4),
                       maskEO[kvh][:, i // 2].un(1).bc([128, 32, 4, 4]), ALU.mult)
                    for r_ in range(32):
                        for g in range(4):
                            mm(accS[kvh][0:4, g * 65:(g + 1) * 65], ptsb[:, r_, 4 * g:4 * g + 4], vbp[:, r_, kvh, :],
                               st=(i == 0 and r_ == 0), sp=False)
            for kvh in range(2):
                Sn = nps()
                mm(Sn[0:16, 0:16], KnS[:, 0, kvh, :], QS[:, 0, kvh, :])
                act(ptn[:], Sn[0:16, 0:16], AF.Exp)
                tt("dve", ptn[:].re("p (g q) -> p g q", g=4), ptn[:].re("p (g q) -> p g q", g=4),
                   newm[:, b, :].un(1).bc([16, 4, 4]), ALU.mult)
                for g in range(4):
                    mm(accS[kvh][0:4, g * 65:(g + 1) * 65], ptn[:, 4 * g:4 * g + 4], vnS[:, 0, kvh, :], st=False, sp=True)
                cp("dve", obr[kvh][1][:].re("p g d -> p (g d)"), accS[kvh][0:4, 0:260])
            dma(gk[:, 0:512].re("p (a f) -> p a f", a=4), V(None, S["stk"].a[b].rearrange("(a p) f -> p a f", p=128)))
            dma(gv[:, 0:512].re("p (a f) -> p a f", a=4), V(None, S["stv"].a[b].rearrange("(a p) f -> p a f", p=128)))
            cp("dve", kb[:, 0:4, :].re("p r f -> p (r f)"), gk[:, 0:512])
            cp("pool", vwS[:, :, :, 0:64], gv[:, 0:512].re("p (a h d) -> p a h d", a=4, h=2))
            pt = npt()
            for kvh in range(2):
                for a in range(4):
                    tr(pt[0:64, (kvh * 4 + a) * 128:(kvh * 4 + a + 1) * 128], kb[:, a, kvh * 64:(kvh + 1) * 64], ident[:])
            cp("act", kwS[0:64].re("p k a t -> p (k a t)"), pt[0:64, :])
            accW = [psa[0], psa[1]]
            for kvh in range(2):
                Sw = nps()
                for a in range(4):
                    mm(Sw[:, a * 16:(a + 1) * 16], kwS[:, kvh, a, :], QS[:, 0, kvh, :])
                act(ptw[:].re("p a c -> p (a c)"), Sw[:, 0:64], AF.Exp)
                tt("dve", ptw[:, 0, :].re("p (g q) -> p g q", g=4), ptw[:, 0, :].re("p (g q) -> p g q", g=4),
                   winm0[:].un(1).bc([128, 4, 4]), ALU.mult)
                for a in range(4):
                    for g in range(4):
                        mm(accW[kvh][0:4, g * 65:(g + 1) * 65], ptw[:, a, 4 * g:4 * g + 4], vwS[:, a, kvh, :],
                           st=(a == 0), sp=False)
                Sn = nps()
                mm(Sn[0:16, 0:16], KnS[:, 1, kvh, :], QS[:, 0, kvh, :])
                act(ptn[:], Sn[0:16, 0:16], AF.Exp)
                tt("dve", ptn[:].re("p (g q) -> p g q", g=4), ptn[:].re("p (g q) -> p g q", g=4),
                   newm[:, b, :].un(1).bc([16, 4, 4]), ALU.mult)
                for g in range(4):
                    mm(accW[kvh][0:4, g * 65:(g + 1) * 65], ptn[:, 4 * g:4 * g + 4], vnS[:, 1, kvh, :], st=False, sp=True)
                cp("dve", obr[kvh][2][:].re("p g d -> p (g d)"), accW[kvh][0:4, 0:260])
            for nm_, src_, c0 in (("s_kw", "stk", 512), ("s_vw", "stv", 640)):
                dma(V(None, S[nm_].a[b, 0:508, :]), V(None, S[src_].a[b, 4:512, :]))
                dma(V(None, S[nm_].a[b, 508:512, :]), pkv[4 * b:4 * b + 4, c0:c0 + 128])
            for kvh in range(2):
                for br in range(1, 3):
                    ts("dve", rd[:, br, :], obr[kvh][br][:, :, 64], 1e-30, None, op0=ALU.max)
                    self.recip(rd[:, br, :], rd[:, br, :])
                ts("dve", rd[:, 0, :], obr[kvh][0][:, :, 64], 1e-30, None, op0=ALU.max)
                self.recip(rd[:, 0, :], rd[:, 0, :])
                gvw = gtb[:, 12 * kvh:12 * kvh + 12].re("p (g b) -> p b g", b=3)
                tt("dve", sc3[:], rd[:], gvw, ALU.mult)
                tt("dve", onsa[:], obr[kvh][0][:, :, 0:64], sc3[:, 0, :].un(2).bc([4, 4, 64]), ALU.mult)
                for br in (1, 2):
                    tt("dve", otmp[:], obr[kvh][br][:, :, 0:64], sc3[:, br, :].un(2).bc([4, 4, 64]), ALU.mult)
                    tt("dve", onsa[:], onsa[:], otmp[:], ALU.add)
                tt("dve", otmp[:], onsa[:], onsa[:], ALU.mult)
                self.rsum(ss4[:], otmp[:])
                self.rstd(rs4[:], ss4[:], 1.0 / 64)
                tt("dve", onb[:, 256 * kvh:256 * kvh + 256].re("p (g d) -> p g d", g=4), onsa[:],
                   rs4[:].un(2).bc([4, 4, 64]), ALU.mult)
            dma(mixin[4 * b:4 * b + 4, 0:512], onb[:])
        fw.barrier()
        s2.close()
        self.sample_mlstm(s, L, hT, mixin)
        mT = fw.sb(s, [128, 8, 128], BF16, "mTS")
        x1t = fw.sb(s, [128, D], F32, "x1tS")
        ptm = npt()
        for k in range(8):
            tr(ptm[:, k * 128:(k + 1) * 128], mixin[:, k * 128:(k + 1) * 128], ident[:])
        cp("act", mT[:].re("p k t -> p (k t)"), ptm[:])
        for g in range(2):
            ps = nps()
            for k in range(8):
                mm(ps[:, :], mT[:, k, :], wout_b[:, k, g * 512:(g + 1) * 512], st=(k == 0), sp=(k == 7))
            tt("dve", x1t[:, g * 512:(g + 1) * 512], xt[:, g * 512:(g + 1) * 512], ps[:, :], ALU.add)
        dma(S["x1s"][:, :], x1t[0:16, :])
        fw.barrier()


Builder.sample_pass1 = sample_pass1


def sample_mlstm(self, s, L, hT, mixin):
    fw, S = self.fw, self.S
    mm, tr, act, ts, tt, stt, cp, ms, iota, dma, dmas = (self.mm, self.tr, self.act, self.ts, self.tt, self.stt,
                                                         self.cp, self.ms, self.iota, self.dma, self.dmas)
    win_b, wqm_b, wkm_b = L["win_b"], L["wqm_b"], L["wkm_b"]
    identf, nps, cw, cb, bif, tmpf = L["identf"], L["nps"], L["cw"], L["cb"], L["bif"], L["tmpf"]
    KSC = 128.0 ** -0.5
    E = fw.sb(s, [4, 128], F32, "E4")
    iota(E[:], [[1, 128]], base=0, cm=-4)
    Eb = fw.sb(s, [4, 128], F32, "E4b")
    ts("pool", Eb[:], E[:], 0.0, None, op0=ALU.is_ge)
    ts("pool", E[:], E[:], 3.0, None, op0=ALU.is_le)
    tt("pool", E[:], E[:], Eb[:], ALU.mult)
    triS = fw.sb(s, [128, 128], F32, "triS")
    ps = nps()
    mm(ps[:, 0:128], E[:], E[:])
    tt("dve", triS[:], ps[:, 0:128], L["tri_le"][:], ALU.mult)
    bdS = fw.sb(s, [16, 16], BF16, "bdS")
    cp("dve", bdS[:], triS[0:16, 0:16])
    cselS = fw.sb(s, [128, 4, 128], F32, "cselS")
    d4 = fw.sb(s, [4, 4, 128], F32, "d4")
    cp("dve", d4[:], identf[0:4, 0:4].un(2).bc([4, 4, 128]))
    ps = nps()
    for b in range(4):
        mm(ps[:, b * 128:(b + 1) * 128], E[:], d4[:, b, :])
    cp("dve", cselS[:].re("p b m -> p (b m)"), ps[:, :])
    psV, psO, psG = nps(), nps(), nps()
    for k in range(8):
        mm(psV[:, 0:512], hT[:, k, :], win_b[:, k, 1816:2328], st=(k == 0), sp=(k == 7))
    for k in range(8):
        mm(psO[:, 0:512], hT[:, k, :], win_b[:, k, 2328:2840], st=(k == 0), sp=(k == 7))
    for k in range(8):
        mm(psG[:, 0:8], hT[:, k, :], win_b[:, k, 2840:2848], st=(k == 0), sp=(k == 7))
    gif = fw.sb(s, [128, 8], F32, "gifS")
    l1 = fw.sb(s, [128, 4], F32, "l1S")
    sigo = fw.sb(s, [128, 512], F32, "sigoS")
    tt("dve", gif[:], psG[:, 0:8], bif[:], ALU.add)
    act(l1[:], gif[:, 4:8], AF.Exp, scale=-1.0)
    act(l1[:], l1[:], AF.Ln, bias=1.0)
    self.sigm(sigo[:], psO[:, 0:512])
    psC = nps()
    mm(psC[:, 0:4], triS[:], l1[:])
    for b in range(4):
        mm(psC[:, 4 + 4 * b:8 + 4 * b], cselS[:, b, :], l1[:])
    gsb = fw.sb(s, [128, 20], F32, "gsbS")
    cp("dve", gsb[:], psC[:, 0:20])
    wl = fw.sb(s, [128, 4], F32, "wlS")
    ul = fw.sb(s, [128, 4], F32, "ulS")
    tmp4 = fw.sb(s, [128, 4], F32, "tmp4S")
    own = fw.sb(s, [128, 4], F32, "ownS")
    dec = fw.sb(s, [128, 4], F32, "decS")
    ebt = fw.sb(s, [128, 16], F32, "ebtS")
    act(wl[:], gsb[:, 0:4], AF.Exp, scale=-1.0)
    tt("dve", tmp4[:], gif[:, 0:4], gsb[:, 0:4], ALU.add)
    act(ul[:], tmp4[:], AF.Exp)
    act(ebt[:], gsb[:, 4:20], AF.Exp, scale=-1.0)
    ts("dve", own[:], gsb[:, 4:8], cselS[:, 0, 0:1], None, op0=ALU.mult)
    for b in range(1, 4):
        stt("dve", own[:], gsb[:, 4 + 4 * b:8 + 4 * b], cselS[:, b, 0:1], own[:], ALU.mult, ALU.add)
    tt("dve", dec[:], tmp4[:], own[:], ALU.subtract)
    vmu = fw.sb(s, [128, 4, 129], BF16, "vmuS")
    tt("dve", vmu[:, :, 0:128], psV[:, 0:512].re("p (h e) -> p h e", h=4), ul[:].un(2).bc([128, 4, 128]), ALU.mult)
    cp("dve", vmu[:, :, 128], ul[:])
    psT = nps()
    tr(psT[0:4, 0:128], dec[:], identf[:])
    mm(psT[0:4, 128:132], l1[:], cselS[:, :, 0])
    tsb = fw.sb(s, [4, 132], F32, "tsbS")
    cp("dve", tsb[:], psT[0:4, 0:132])
    Dm = fw.sb(s, [4, 4], F32, "DmS")
    self.rmax(Dm[:], tsb[:, 0:16].re("p (b i) -> p b i", b=4))
    R = fw.sb(s, [4, 4], F32, "RS")
    dmas(R[:], V(None, S["sm"].a.rearrange("(b h) -> h b", h=4)))
    tt("dve", R[:], R[:], tsb[:, 128:132], ALU.subtract)
    tt("dve", R[:], R[:], Dm[:], ALU.max)
    dmas(V(None, S["s_m"].a.rearrange("(b h) -> h b", h=4)), R[:])
    xcv = fw.sb(s, [128, 4, 4, 7], F32, "xcvS")
    for b in range(4):
        for ch in range(4):
            dmas(xcv[:, ch, b, 0:3], V(None, S["sconv"].a[b, :, ch * 128:(ch + 1) * 128].rearrange("j p -> p j")))
    psX = nps()
    for ch in range(4):
        for k in range(8):
            mm(psX[:, ch * 16:(ch + 1) * 16], win_b[:, k, 1304 + ch * 128:1432 + ch * 128], hT[:, k, 0:16],
               st=(k == 0), sp=(k == 7))
    cp("act", xcv[:, :, :, 3:7], psX[:, 0:64].re("p (c b i) -> p c b i", c=4, b=4))
    cacc = fw.sb(s, [128, 4, 16], F32, "caccS")
    for ch in range(4):
        cv = cacc[:, ch, :].re("p (b i) -> p b i", b=4)
        ts("dve", cv, xcv[:, ch, :, 0:4], cw[:, ch, 0:1], cb[:, ch:ch + 1], op0=ALU.mult, op1=ALU.add)
        for j in range(1, 4):
            stt("dve", cv, xcv[:, ch, :, j:j + 4], cw[:, ch, j:j + 1], cv, ALU.mult, ALU.add)
    xc = fw.sb(s, [128, 4, 16], BF16, "xcS")
    sgc = fw.sb(s, [128, 4, 16], F32, "sgcS")
    self.sigm(sgc[:], cacc[:])
    tt("dve", xc[:], cacc[:], sgc[:], ALU.mult)
    for b in range(4):
        for j in range(3):
            dmas(V(None, S["s_conv"].a[b, j].rearrange("(c p) -> p c", p=128)), xcv[:, :, b, 4 + j])
    qmT = fw.sb(s, [128, 4, 16], BF16, "qmTS")
    kmT = fw.sb(s, [128, 4, 16], BF16, "kmTS")
    qmS = [fw.sb(s, [128, 4, 16], BF16, f"qmSS{b}") for b in range(4)]
    kmS = [fw.sb(s, [16, 4, 128], BF16, f"kmSS{b}") for b in range(4)]
    psq = nps()
    for h in range(4):
        mm(psq[:, h * 16:(h + 1) * 16], wqm_b[:, h, :], xc[:, h, :])
    cp("act", qmT[:].re("p h t -> p (h t)"), psq[:, 0:64])
    for b in range(4):
        ms("pool", qmS[b][:], 0.0)
        cp("dve", qmS[b][:, :, 4 * b:4 * b + 4], psq[:, 0:64].re("p (h t) -> p h t", h=4)[:, :, 4 * b:4 * b + 4])
    psk = nps()
    for h in range(4):
        mm(psk[:, h * 16:(h + 1) * 16], wkm_b[:, h, :], xc[:, h, :])
    act(kmT[:].re("p h t -> p (h t)"), psk[:, 0:64], AF.Copy, scale=KSC)
    pskt = nps()
    for h in range(4):
        mm(pskt[0:16, h * 128:(h + 1) * 128], xc[:, h, :], wkm_b[:, h, :])
    for b in range(4):
        ts("dve", kmS[b][:].re("p h t -> p (h t)"), pskt[0:16, :], cselS[0:16, b, 0:1], KSC, op0=ALU.mult, op1=ALU.mult)
    psqk = nps()
    for h in range(4):
        mm(psqk[0:16, h * 16:(h + 1) * 16], kmT[:, h, :], qmT[:, h, :])
    mqk = fw.sb(s, [16, 4, 16], BF16, "mqkS")
    tt("dve", mqk[:], psqk[0:16, 0:64].re("p (h t) -> p h t", h=4), bdS[:].un(1).bc([16, 4, 16]), ALU.mult)
    em0 = fw.sb(s, [128, 16], F32, "em0")
    dma(em0[:], V(None, S["sm"].a.partition_broadcast(128)))
    act(em0[:], em0[:], AF.Exp)
    Sf = [fw.sb(s, [128, 4, 129], F32, f"SfS{b}") for b in range(4)]
    Sb0 = [fw.sb(s, [128, 4, 129], BF16, f"Sb0S{b}") for b in range(4)]
    cst_ = [fw.sb(s, [128, 128], F32, f"c0st{i}") for i in range(2)]
    dS = fw.sb(s, [128, 4, 129], F32, "dSS")
    for b in range(4):
        for h in range(4):
            st = cst_[h % 2]
            dma(st[:], V(None, S["sC"].a[b, h]))
            ps = nps()
            tr(ps[:, 0:128], st[:], identf[:])
            cp("dve", Sf[b][:, h, 0:128], ps[:, 0:128])
        dmas(Sf[b][:, :, 128], V(None, S["sn"].a[b].rearrange("h d -> d h")))
        tt("dve", Sf[b][:], Sf[b][:], em0[:, 4 * b:4 * b + 4].un(2).bc([128, 4, 129]), ALU.mult)
        cp("pool", Sb0[b][:], Sf[b][:])
        pd = [nps(), nps()]
        for h in range(4):
            mm(pd[h // 2][:, (h % 2) * 129:(h % 2) * 129 + 129], kmS[b][:, h, :], vmu[0:16, h, :])
        tt("dve", Sf[b][:], Sf[b][:], ebt[:, 4 * b:4 * b + 4].un(2).bc([128, 4, 129]), ALU.mult)
        for hh in range(2):
            tt("dve", dS[:, 2 * hh:2 * hh + 2, :], pd[hh][:, 0:258].re("p (h e) -> p h e", h=2),
               ebt[:, 4 * b + 2 * hh:4 * b + 2 * hh + 2].un(2).bc([128, 2, 129]), ALU.mult)
        tt("dve", Sf[b][:], Sf[b][:], dS[:], ALU.add)
    pa = [nps(), nps()]
    for h in range(4):
        o_ = pa[h // 2][0:16, (h % 2) * 129:(h % 2) * 129 + 129]
        mm(o_, mqk[:, h, :], vmu[0:16, h, :], st=True, sp=False)
        for b in range(4):
            mm(o_, qmS[b][:, h, :], Sb0[b][:, h, :], st=False, sp=(b == 3))
    dn = fw.sb(s, [16, 4], F32, "dnS")
    t4 = fw.sb(s, [16, 4], F32, "t4S")
    hout = fw.sb(s, [16, 4, 128], F32, "houtS")
    hsq = fw.sb(s, [16, 4, 128], F32, "hsqS")
    ss4 = fw.sb(s, [16, 4], F32, "ss4m")
    rs4 = fw.sb(s, [16, 4], F32, "rs4m")
    for hh in range(2):
        av = pa[hh][0:16, 0:258].re("p (h e) -> p h e", h=2)
        tt("dve", dn[:, 2 * hh:2 * hh + 2], av[:, :, 128], wl[0:16, 2 * hh:2 * hh + 2], ALU.mult)
    stt("dve", t4[:], dn[:], -1.0, dn[:], ALU.mult, ALU.max)
    ts("dve", t4[:], t4[:], 1.0, None, op0=ALU.max)
    self.recip(t4[:], t4[:])
    tt("dve", t4[:], t4[:], wl[0:16, :], ALU.mult)
    for hh in range(2):
        av = pa[hh][0:16, 0:258].re("p (h e) -> p h e", h=2)
        tt("dve", hout[:, 2 * hh:2 * hh + 2, :], av[:, :, 0:128], t4[:, 2 * hh:2 * hh + 2].un(2).bc([16, 2, 128]), ALU.mult)
    tt("dve", hsq[:], hout[:], hout[:], ALU.mult)
    self.rsum(ss4[:], hsq[:])
    self.rstd(rs4[:], ss4[:], 1.0 / 128)
    tt("dve", hout[:], hout[:], rs4[:].un(2).bc([16, 4, 128]), ALU.mult)
    tt("dve", mixin[0:16, 512:1024], hout[:].re("p h e -> p (h e)"), sigo[0:16, :], ALU.mult)
    Rd = fw.sb(s, [4, 4, 4], F32, "RdS")
    tt("dve", Rd[:], R[:].un(2).bc([4, 4, 4]), identf[0:4, 0:4].un(1).bc([4, 4, 4]), ALU.mult)
    ones4 = fw.sb(s, [4, 128], F32, "ones4S")
    ms("pool", ones4[:], 1.0)
    ps = nps()
    mm(ps[:, 0:16], ones4[:], Rd[:].re("p b h -> p (b h)"))
    esc = fw.sb(s, [128, 16], F32, "escS")
    act(esc[:], ps[:, 0:16], AF.Exp, scale=-1.0)
    for b in range(4):
        tt("dve", Sf[b][:], Sf[b][:], esc[:, 4 * b:4 * b + 4].un(2).bc([128, 4, 129]), ALU.mult)
        dmas(V(None, S["s_n"].a[b].rearrange("h d -> d h")), Sf[b][:, :, 128])
        for h in range(4):
            ps = nps()
            tr(ps[:, 0:128], Sf[b][:, h, 0:128], identf[:])
            st = cst_[h % 2]
            cp("dve", st[:], ps[:, 0:128])
            dma(V(None, S["s_C"].a[b, h]), st[:])


Builder.sample_mlstm = sample_mlstm
Builder.sample_s0 = sample_s0

W_NAMES = ["w_in", "g_mix", "b_gate", "cmp_pe_k", "cmp_w1_k", "cmp_b1_k", "cmp_w2_k", "cmp_pe_v", "cmp_w1_v",
           "cmp_b1_v", "cmp_w2_v", "g_head_nsa", "conv_w", "conv_b", "w_qm", "w_km", "b_i", "b_f", "g_head_m",
           "w_out", "g_xa", "g_mem", "w_xq", "w_xk", "w_xv", "w_xo", "g_ffn", "w_gate", "w_up", "w_down", "g_final"]


def build_program(NT=32, sample=True, debug=False):
    nc = bass.Bass("TRN2", target_bir_lowering=False)
    b = Builder(nc, NT=NT, sample=sample, debug=debug)
    b.build()
    return nc, b


def core_inputs(inp, c, b, NT=32):
    f = lambda a: np.ascontiguousarray(a, dtype=np.float32)
    T = NT * 128
    m = {"xp": f(inp["x_prompt"][c, :T]), "memp": f(inp["mem_prompt"][c])}
    for n in W_NAMES:
        a = np.asarray(inp[n])
        if n != "g_final":
            a = a[0]
        m[n] = f(a).reshape(b.io[n].shape)
    if b.sample:
        sl = slice(4 * c, 4 * c + 4)
        m["xs"] = f(inp["x_sample"][sl]).reshape(16, D)
        for n, k in (("pool_kc", "cache_k_cmp"), ("pool_vc", "cache_v_cmp"), ("pool_ks", "cache_k_slc"), ("pool_vs", "cache_v_slc")):
            m[n] = np.asarray(inp[k][0], dtype=np.float32).reshape(5120, 16384)
        m["ptab"] = np.ascontiguousarray(inp["page_table"][sl], dtype=np.int32)
        m["stk"] = f(inp["state_k_win"][0, sl]).reshape(4, 512, 128)
        m["stv"] = f(inp["state_v_win"][0, sl]).reshape(4, 512, 128)
        m["sconv"] = f(inp["state_conv"][0, sl])
        m["sC"] = f(inp["state_C"][0, sl])
        m["sn"] = f(inp["state_n"][0, sl])
        m["sm"] = f(inp["state_m"][0, sl]).reshape(16)
        m["cmk"] = f(inp["cache_mem_k"][0, sl]).reshape(4, 256, D)
        m["cmv"] = f(inp["cache_mem_v"][0, sl]).reshape(4, 256, D)
    return {k: v for k, v in m.items() if k in b.io}


_PROG = {}


def kernel(**inp):
    n = 8
    if "p" not in _PROG:
        _PROG["p"] = build_program()
    nc, b = _PROG["p"]
    in_maps = [core_inputs(inp, c, b) for c in range(n)]
    res = run_bass_kernel_spmd(nc, in_maps, core_ids=list(range(n)))
    R = res.results

    def st(name, shp, lead):
        a = np.stack([np.asarray(R[c][name], dtype=np.float32).reshape(shp) for c in range(n)])
        return np.ascontiguousarray(a.reshape(lead))

    outs = [st("y_p", (4096, D), (8, 4096, D)), st("y_s", (4, 4, D), (32, 4, D))]
    for nm in ("p_kc", "p_vc", "p_ks", "p_vs"):
        outs.append(st(nm, (4096, 2, 64), (1, 8, 4096, 2, 64)))
    for nm in ("p_kw", "p_vw"):
        outs.append(st(nm, (512, 2, 64), (1, 8, 512, 2, 64)))
    outs.append(st("p_C", (4, 128, 128), (1, 8, 4, 128, 128)))
    outs.append(st("p_n", (4, 128), (1, 8, 4, 128)))
    outs.append(st("p_m", (4,), (1, 8, 4)))
    outs.append(st("p_conv", (3, 512), (1, 8, 3, 512)))
    outs.append(st("p_mk", (256, 4, 256), (1, 8, 256, 4, 256)))
    outs.append(st("p_mv", (256, 4, 256), (1, 8, 256, 4, 256)))
    for nm in ("s_kc", "s_vc", "s_ks", "s_vs"):
        outs.append(st(nm, (4, 4, 2, 64), (1, 32, 4, 2, 64)))
    for nm in ("s_kw", "s_vw"):
        outs.append(st(nm, (4, 512, 2, 64), (1, 32, 512, 2, 64)))
    outs.append(st("s_C", (4, 4, 128, 128), (1, 32, 4, 128, 128)))
    outs.append(st("s_n", (4, 4, 128), (1, 32, 4, 128)))
    outs.append(st("s_m", (4, 4), (1, 32, 4)))
    outs.append(st("s_conv", (4, 3, 512), (1, 32, 3, 512)))
    return tuple(outs)
```

```python
import numpy as np
from contextlib import ExitStack
import concourse.bass as bass
import concourse.mybir as mybir
from concourse.bass_utils import run_bass_kernel_spmd

F32 = mybir.dt.float32
BF16 = mybir.dt.bfloat16
I32 = mybir.dt.int32
AF = mybir.ActivationFunctionType
ALU = mybir.AluOpType
AX = mybir.AxisListType

D = 1024
NEG = -30000.0
EPS = 1e-6
IN_COLS = 2848
DFF = 2816


class V:
    __slots__ = ("b", "a")

    def __init__(self, b, a):
        self.b = b
        self.a = a

    def __getitem__(self, k):
        return V(self.b, self.a[k])

    def re(self, p, **kw):
        return V(self.b, self.a.rearrange(p, **kw))

    def bc(self, shape):
        return V(self.b, self.a.to_broadcast(list(shape)))

    def un(self, ax):
        return V(self.b, self.a.unsqueeze(ax))


class Buf:
    __slots__ = ("t", "w", "r", "name", "psum", "fresh", "quads")

    def __init__(self, t, name="", psum=False):
        self.t = t
        self.w = None
        self.r = []
        self.name = name
        self.psum = psum
        self.fresh = True
        self.quads = set()

    def __getitem__(self, k):
        return V(self, self.t[k])


class DSem:
    def __init__(self, nc, name):
        self.sem = nc.alloc_semaphore(name)
        self.val = 0


class FW:
    ENG = ("pe", "act", "dve", "pool", "sp")

    def __init__(self, nc, n_dsem=10, same_engine_sync=True):
        self.nc = nc
        self.e = {"pe": nc.tensor, "act": nc.scalar, "dve": nc.vector, "pool": nc.gpsimd, "sp": nc.sync}
        self.gen = {k: 0 for k in self.ENG}
        self.sem = {k: nc.alloc_semaphore("S_" + k) for k in self.ENG}
        self.cnt = {k: 0 for k in self.ENG}
        self.seen = {k: {} for k in self.ENG}
        self.same = same_engine_sync
        self.dsems = {q: [DSem(nc, f"D{q}{i}") for i in range(n_dsem)] for q in ("sp", "pool", "act")}
        self.dnext = {q: 0 for q in self.dsems}
        self.nbuf = 0
        self.nins = 0

    def sb(self, stack, shape, dt=F32, name=None):
        self.nbuf += 1
        name = name or f"b{self.nbuf}"
        return Buf(stack.enter_context(self.nc.sbuf_tensor(name, list(shape), dt)), name)

    def ps(self, stack, shape, dt=F32, name=None):
        self.nbuf += 1
        name = name or f"p{self.nbuf}"
        return Buf(stack.enter_context(self.nc.psum_tensor(name, list(shape), dt)), name, psum=True)

    def _need(self, e, dep, waits):
        if dep is None:
            return
        kind, key, val, semh = dep
        if kind == "e" and key[0] == e and (not self.same or e == "pe"):
            return
        k = (kind, key if kind == "e" else id(key))
        if self.seen[e].get(k, 0) >= val:
            return
        cur = waits.get(k)
        if cur is None or cur[1] < val:
            waits[k] = (semh, val)

    def _emit_waits(self, e, reads, writes):
        waits = {}
        for b in reads:
            if b is not None:
                self._need(e, b.w, waits)
        for b in writes:
            if b is not None:
                self._need(e, b.w, waits)
                for d in b.r:
                    self._need(e, d, waits)
        eng = self.e[e]
        for k, (semh, val) in waits.items():
            eng.wait_ge(semh, val)
            self.seen[e][k] = val

    def op(self, e, fn, reads=(), writes=()):
        px = [b for b in reads if b is not None and b.psum]
        if px:
            reads = [b for b in reads if not (b is not None and b.psum)]
            writes = list(writes) + [b for b in px if b not in writes]
            if e != "pe":
                for b in px:
                    b.fresh = True
        self._emit_waits(e, reads, writes)
        ins = fn(self.e[e])
        if self.cnt[e] >= 50000:
            self.gen[e] += 1
            self.sem[e] = self.nc.alloc_semaphore(f"S_{e}_{self.gen[e]}")
            self.cnt[e] = 0
        self.cnt[e] += 1
        self.nins += 1
        ins.then_inc(self.sem[e], 1)
        dep = ("e", (e, self.gen[e]), self.cnt[e], self.sem[e])
        for b in reads:
            if b is not None:
                b.r.append(dep)
                if len(b.r) > 16:
                    b.r = self._compact(b.r)
        for b in writes:
            if b is not None:
                b.w = dep
                b.r = []
        return ins

    @staticmethod
    def _compact(lst):
        best = {}
        for d in lst:
            k = (d[0], d[1] if d[0] == "e" else id(d[1]))
            if k not in best or best[k][2] < d[2]:
                best[k] = d
        return list(best.values())

    def dma(self, q, o, i, fn=None, extra_reads=(), **kw):
        reads = [i.b] + list(extra_reads)
        writes = [o.b]
        self._emit_waits(q, reads, writes)
        ds = self.dsems[q][self.dnext[q]]
        self.dnext[q] = (self.dnext[q] + 1) % len(self.dsems[q])
        if ds.val > 0 and self.seen[q].get(("d", id(ds)), 0) < ds.val:
            self.e[q].wait_ge(ds.sem, ds.val)
            self.seen[q][("d", id(ds))] = ds.val
        if fn is None:
            ins = self.e[q].dma_start(out=o.a, in_=i.a, **kw)
        else:
            ins = fn(self.e[q])
        ds.val += 16
        self.nins += 1
        ins.then_inc(ds.sem, 16)
        dep = ("d", ds, ds.val, ds.sem)
        for b in reads:
            if b is not None:
                b.r.append(dep)
                if len(b.r) > 16:
                    b.r = self._compact(b.r)
        for b in writes:
            if b is not None:
                b.w = dep
                b.r = []
        return ins

    def barrier(self):
        for e in self.ENG:
            eng = self.e[e]
            for f in self.ENG:
                if f != e and self.cnt[f] > 0:
                    k = ("e", (f, self.gen[f]))
                    if self.seen[e].get(k, 0) < self.cnt[f]:
                        eng.wait_ge(self.sem[f], self.cnt[f])
                        self.seen[e][k] = self.cnt[f]
            for q in self.dsems:
                for ds in self.dsems[q]:
                    k = ("d", id(ds))
                    if ds.val > 0 and self.seen[e].get(k, 0) < ds.val:
                        eng.wait_ge(ds.sem, ds.val)
                        self.seen[e][k] = ds.val

    def finish(self):
        eng = self.e["sp"]
        for q in self.dsems:
            for ds in self.dsems[q]:
                if ds.val > 0:
                    eng.wait_ge(ds.sem, ds.val)


class RR:
    def __init__(self, items):
        self.items = list(items)
        self.i = 0

    def __call__(self):
        x = self.items[self.i]
        self.i = (self.i + 1) % len(self.items)
        return x


class Builder:
    def __init__(self, nc, NT=32, sample=True, debug=False):
        self.debug = debug
        self.nc = nc
        self.fw = FW(nc)
        self.NT = NT
        self.T = NT * 128
        self.sample = sample
        self.io = {}

    def din(self, name, shape, dt=F32):
        t = self.nc.dram_tensor(name, list(shape), dt, kind="ExternalInput").ap()
        self.io[name] = t
        return V(None, t)

    def dout(self, name, shape, dt=F32):
        t = self.nc.dram_tensor(name, list(shape), dt, kind="ExternalOutput").ap()
        self.io[name] = t
        return V(None, t)

    def dscr(self, name, shape, dt=F32):
        t = self.nc.dram_tensor(name, list(shape), dt, kind="ExternalOutput" if self.debug else "Internal").ap()
        if self.debug:
            self.io[name] = t
        return Buf(t, name)

    def dbg(self, name, v, dt=F32):
        if not self.debug:
            return
        o = self.dout("dbg_" + name, list(v.a.shape), dt)
        self.fw.dma("sp", o, v)

    def mm(self, o, l, r, st=True, sp=True):
        b = o.b
        p0 = o.a.base_partition() if hasattr(o.a, "base_partition") else 0
        q = set(range(p0 // 32, (p0 + o.a.shape[0] + 31) // 32))
        start = False
        if st:
            if b.fresh:
                start = True
                b.fresh = False
                b.quads = set(q)
            else:
                assert q <= b.quads, (b.name, q, b.quads)
        self.fw.op("pe", lambda e: e.matmul(o.a, lhsT=l.a, rhs=r.a, start=start, stop=sp, skip_group_check=True),
                   reads=[l.b, r.b], writes=[o.b])

    def tr(self, o, i, ident):
        self.fw.op("pe", lambda e: e.transpose(out=o.a, in_=i.a, identity=ident.a), reads=[i.b, ident.b], writes=[o.b])

    def act(self, o, i, f, scale=1.0, bias=0.0, acc=None):
        reads = [i.b]
        writes = [o.b]
        kw = {}
        if isinstance(bias, V):
            reads.append(bias.b)
            kw["bias"] = bias.a
        elif bias != 0.0:
            kw["bias"] = float(bias)
        if isinstance(scale, V):
            reads.append(scale.b)
            kw["scale"] = scale.a
        elif scale != 1.0:
            kw["scale"] = float(scale)
        if acc is not None:
            writes.append(acc.b)
            kw["accum_out"] = acc.a
        self.fw.op("act", lambda e: e.activation(out=o.a, in_=i.a, func=f, **kw), reads=reads, writes=writes)

    def ts(self, eng, o, i, s1, s2=None, op0=ALU.mult, op1=None):
        reads = [i.b]
        a1 = s1
        a2 = s2
        if isinstance(s1, V):
            reads.append(s1.b)
            a1 = s1.a
        if isinstance(s2, V):
            reads.append(s2.b)
            a2 = s2.a
        kw = {}
        if op1 is not None:
            kw["op1"] = op1
        self.fw.op(eng, lambda e: e.tensor_scalar(out=o.a, in0=i.a, scalar1=a1, scalar2=a2, op0=op0, **kw), reads=reads, writes=[o.b])

    def tt(self, eng, o, a, b, op):
        self.fw.op(eng, lambda e: e.tensor_tensor(out=o.a, in0=a.a, in1=b.a, op=op), reads=[a.b, b.b], writes=[o.b])

    def stt(self, eng, o, a, s, b, op0, op1):
        reads = [a.b, b.b]
        sa = s
        if isinstance(s, V):
            reads.append(s.b)
            sa = s.a
        self.fw.op(eng, lambda e: e.scalar_tensor_tensor(out=o.a, in0=a.a, scalar=sa, in1=b.a, op0=op0, op1=op1), reads=reads, writes=[o.b])

    def cp(self, eng, o, i):
        if eng == "act":
            self.fw.op("act", lambda e: e.copy(out=o.a, in_=i.a), reads=[i.b], writes=[o.b])
        else:
            self.fw.op(eng, lambda e: e.tensor_copy(out=o.a, in_=i.a), reads=[i.b], writes=[o.b])

    def ms(self, eng, o, val):
        self.fw.op(eng, lambda e: e.memset(o.a, val), writes=[o.b])

    def iota(self, o, pattern, base=0, cm=0):
        self.fw.op("pool", lambda e: e.iota(o.a, pattern=pattern, base=base, channel_multiplier=cm,
                                            allow_small_or_imprecise_dtypes=True), writes=[o.b])

    def sigm(self, o, i):
        self.act(o, i, AF.Exp, scale=-1.0)
        self.ts("dve", o, o, 1.0, None, op0=ALU.add)
        self.recip(o, o)

    def recip(self, o, i):
        self.fw.op("dve", lambda e: e.reciprocal(out=o.a, in_=i.a), reads=[i.b], writes=[o.b])

    def rsum(self, o, i):
        self.fw.op("dve", lambda e: e.reduce_sum(out=o.a, in_=i.a, axis=AX.X), reads=[i.b], writes=[o.b])

    def rmax(self, o, i):
        self.fw.op("dve", lambda e: e.reduce_max(out=o.a, in_=i.a, axis=AX.X), reads=[i.b], writes=[o.b])

    def dma(self, o, i, q="sp", **kw):
        self.fw.dma(q, o, i, **kw)

    def dmas(self, o, i, q="sp"):
        self.fw.dma(q, o, i, allow_slow_non_contiguous=True)

    def put_row(self, dst, pattern, base, n, const=None):
        rowt, rowb = self.rowt, self.rowb
        if const is None:
            rv = rowt[0:1, 0:n]
            if len(pattern) == 2:
                rv = rv.re("p (a b) -> p a b", b=pattern[1][1])
            self.iota(rv, pattern, base=base, cm=0)
        else:
            self.ms("pool", rowt[0:1, 0:n], const)
        self.cp("pool", rowb[0:1, 0:n], rowt[0:1, 0:n])
        self.dma(dst, rowb[0:1, 0:n])

    def rstd(self, o, ss, inv_n):
        self.act(o, ss, AF.Ln, scale=inv_n, bias=self.epsc[0:o.a.shape[0], :])
        self.act(o, o, AF.Exp, scale=-0.5)

    def build(self):
        nc, fw, NT, T = self.nc, self.fw, self.NT, self.T
        din, dout = self.din, self.dout
        mm, tr, act, ts, tt, stt, cp, ms, iota, dma, dmas = (self.mm, self.tr, self.act, self.ts, self.tt, self.stt,
                                                             self.cp, self.ms, self.iota, self.dma, self.dmas)
        xp = din("xp", [T, D])
        memp = din("memp", [256, D])
        w_in = din("w_in", [D, IN_COLS])
        g_mix = din("g_mix", [D])
        b_gate = din("b_gate", [24])
        cmp_in = {}
        for kv in "kv":
            cmp_in[kv] = (din(f"cmp_pe_{kv}", [32, 64]), din(f"cmp_w1_{kv}", [2048, 256]),
                          din(f"cmp_b1_{kv}", [256]), din(f"cmp_w2_{kv}", [256, 64]))
        g_head_nsa = din("g_head_nsa", [512])
        conv_w = din("conv_w", [4, 512])
        conv_b = din("conv_b", [512])
        w_qm = din("w_qm", [4, 128, 128])
        w_km = din("w_km", [4, 128, 128])
        b_i = din("b_i", [4])
        b_f = din("b_f", [4])
        g_head_m = din("g_head_m", [512])
        w_out = din("w_out", [D, D])
        g_xa = din("g_xa", [D])
        g_mem = din("g_mem", [D])
        w_xq = din("w_xq", [D, D])
        w_xk = din("w_xk", [D, D])
        w_xv = din("w_xv", [D, D])
        w_xo = din("w_xo", [D, D])
        g_ffn = din("g_ffn", [D])
        w_gate = din("w_gate", [D, DFF])
        w_up = din("w_up", [D, DFF])
        w_down = din("w_down", [DFF, D])
        g_final = din("g_final", [D])

        y_p = dout("y_p", [T, D])
        p_kv = {n: dout(n, [T, 128]) for n in ("p_kc", "p_vc", "p_ks", "p_vs")}
        WT = min(512, T)
        p_kw = dout("p_kw", [WT, 128])
        p_vw = dout("p_vw", [WT, 128])
        p_C = dout("p_C", [4, 128, 128])
        p_n = dout("p_n", [4, 128])
        p_m = dout("p_m", [4])
        p_conv = dout("p_conv", [3, 512])
        p_mk = dout("p_mk", [256, D])
        p_mv = dout("p_mv", [256, D])

        if self.sample:
            self.sample_io()
        x1d = self.dscr("x1_scr", [T, D])
        x2d = self.dscr("x2_scr", [T, D])

        top = ExitStack()
        with top:
            psf = [fw.ps(top, [128, 512], F32, f"psf{i}") for i in range(4)]
            pst = [fw.ps(top, [128, 1024], BF16, f"pst{i}") for i in range(1)]
            psa = [fw.ps(top, [128, 512], F32, f"psa{i}") for i in range(3)]
            nps = RR(psf)
            nps_s = RR(psf[0:3])
            nps_m = RR([psf[3], psa[2]])
            npt = RR(pst)

            ident = fw.sb(top, [128, 128], BF16, "ident")
            identf = fw.sb(top, [128, 128], F32, "identf")
            for idt in (ident, identf):
                ms("pool", idt[:], 0.0)
                fw.op("pool", lambda e, idt=idt: e.affine_select(out=idt[:].a, in_=idt[:].a, pattern=[[-1, 128]],
                                                                compare_op=ALU.not_equal, fill=1.0, base=0,
                                                                channel_multiplier=1), reads=[idt], writes=[idt])
            self.epsc = fw.sb(top, [128, 1], F32, "epsc")[:]
            ms("pool", self.epsc, EPS)
            if self.sample:
                self.sample_s0(locals())
            sw = ExitStack()
            tmpf = fw.sb(sw, [128, 512], F32, "tmpf")
            iota(tmpf[:].re("p (g r) -> p g r", g=4), [[0, 4], [-1, 128]], base=0, cm=1)
            caus_add = fw.sb(sw, [128, 512], BF16, "caus_add")
            ts("pool", caus_add[:], tmpf[:], 0.0, NEG, op0=ALU.is_gt, op1=ALU.mult)
            win_add = fw.sb(sw, [128, 512], BF16, "win_add")
            ts("pool", win_add[:], tmpf[:], 0.0, NEG, op0=ALU.is_le, op1=ALU.mult)
            bd01 = fw.sb(sw, [128, 128], BF16, "bd01")
            ts("pool", bd01[:], tmpf[:, 0:128], 0.0, None, op0=ALU.is_le)
            ms("pool", bd01[0:64, 64:128], 0.0)
            tri_le = fw.sb(sw, [128, 128], F32, "tri_le")
            ts("pool", tri_le[:], tmpf[:, 0:128], 0.0, None, op0=ALU.is_le)
            tri2 = fw.sb(sw, [128, 128], F32, "tri2")
            cp("pool", tri2[:], bd01[:])
            csel = fw.sb(sw, [128, 2, 128], F32, "csel")
            ms("pool", csel[:], 0.0)
            ms("pool", csel[0:64, 0, :], 1.0)
            ms("pool", csel[64:128, 1, :], 1.0)
            e0 = fw.sb(sw, [128, 512], F32, "e0")
            iota(e0[:].re("p (g r) -> p g r", g=4), [[0, 4], [-1, 128]], base=0, cm=16)
            mimp = fw.sb(sw, [128, 2, 64], BF16, "mimp")
            for ct in range(2):
                iota(tmpf[:, 0:64], [[-4, 64]], base=ct * 128 - 1, cm=1)
                stt("dve", tmpf[:, 64:128], tmpf[:, 0:64], -1.0, tmpf[:, 0:64], ALU.mult, ALU.max)
                ts("pool", tmpf[:, 128:192], tmpf[:, 64:128], 2.0, 0.5, op0=ALU.is_le, op1=ALU.mult)
                ts("pool", tmpf[:, 192:256], tmpf[:, 64:128], 1.0, 0.5, op0=ALU.is_le, op1=ALU.mult)
                tt("pool", mimp[:, ct, :], tmpf[:, 128:192], tmpf[:, 192:256], ALU.add)
                ms("pool", mimp[:, ct, 63:64], 1.0)
            expand = fw.sb(sw, [64, T], BF16, "expand")
            for c0 in range(0, T, 512):
                iota(tmpf[0:64, :], [[1, 512]], base=c0, cm=-64)
                ts("pool", tmpf[0:64, :], tmpf[0:64, :], 31.5, None, op0=ALU.subtract)
                stt("dve", tmpf[0:64, :], tmpf[0:64, :], -1.0, tmpf[0:64, :], ALU.mult, ALU.max)
                ts("pool", expand[:, c0:c0 + 512], tmpf[0:64, :], 32.0, None, op0=ALU.is_le)

            if True:
                win_b = fw.sb(sw, [128, 8, IN_COLS], BF16, "win_b")
                wout_b = fw.sb(sw, [128, 8, D], BF16, "wout_b")
                wqm_b = fw.sb(sw, [128, 4, 128], BF16, "wqm_b")
                wkm_b = fw.sb(sw, [128, 4, 128], BF16, "wkm_b")
                gcol = fw.sb(sw, [128, 16], F32, "gcol")
                dmas(gcol[:, 0:8], V(None, g_mix.a.rearrange("(k p) -> p k", p=128)))
                dmas(gcol[:, 8:12], V(None, g_head_nsa.a.rearrange("(k p) -> p k", p=128)))
                dmas(gcol[:, 12:16], V(None, g_head_m.a.rearrange("(k p) -> p k", p=128)))
                cw = fw.sb(sw, [128, 4, 4], F32, "cw")
                for j in range(4):
                    dmas(cw[:, :, j], V(None, conv_w.a[j].rearrange("(c p) -> p c", p=128)))
                cb = fw.sb(sw, [128, 4], F32, "cb")
                dmas(cb[:], V(None, conv_b.a.rearrange("(c p) -> p c", p=128)))
                bgate = fw.sb(sw, [128, 24], F32, "bgate")
                dma(bgate[:], V(None, b_gate.a.partition_broadcast(128)))
                bif = fw.sb(sw, [128, 8], F32, "bif")
                dma(bif[:, 0:4], V(None, b_i.a.partition_broadcast(128)))
                dma(bif[:, 4:8], V(None, b_f.a.partition_broadcast(128)))
                with ExitStack() as s0:
                    stg = [fw.sb(s0, [128, IN_COLS], F32, f"stg{i}") for i in range(2)]
                    for k in range(8):
                        st = stg[k % 2]
                        dma(st[:], w_in[k * 128:(k + 1) * 128, :])
                        if k % 2 == 0:
                            ts("dve", win_b[:, k, :], st[:], gcol[:, k:k + 1], None, op0=ALU.mult)
                        else:
                            act(win_b[:, k, :], st[:], AF.Copy, scale=gcol[:, k:k + 1])
                    for k in range(8):
                        st = stg[k % 2]
                        dma(st[:, 0:D], w_out[k * 128:(k + 1) * 128, :])
                        if k % 2 == 0:
                            ts("dve", wout_b[:, k, :], st[:, 0:D], gcol[:, 8 + k:9 + k], None, op0=ALU.mult)
                        else:
                            act(wout_b[:, k, :], st[:, 0:D], AF.Copy, scale=gcol[:, 8 + k:9 + k])
                    st = stg[0]
                    dma(st[:, 0:512].re("p (h e) -> p h e", h=4), V(None, w_qm.a.rearrange("h d e -> d h e")))
                    cp("dve", wqm_b[:], st[:, 0:512].re("p (h e) -> p h e", h=4))
                    st = stg[1]
                    dma(st[:, 0:512].re("p (h e) -> p h e", h=4), V(None, w_km.a.rearrange("h d e -> d h e")))
                    cp("dve", wkm_b[:], st[:, 0:512].re("p (h e) -> p h e", h=4))
                    fw.barrier()

                kcp = fw.sb(sw, [68, 2, 256], BF16, "kcp")
                vcp = fw.sb(sw, [128, 2, 2, 65], BF16, "vcp")
                ms("pool", vcp[:], 1.0)
                put_row = self.put_row
                with ExitStack() as tmps:
                    self.rowt = fw.sb(tmps, [1, 4096], F32, "rowt")
                    self.rowb = fw.sb(tmps, [1, 4096], BF16, "rowb")
                    for kvh in range(2):
                        put_row(kcp[64:65, kvh, :], [[128, 32], [0, 8]], 0, 256)
                        put_row(kcp[65:66, kvh, :], [[0, 32], [16, 8]], 31, 256)
                        put_row(kcp[66:67, kvh, :], None, 0, 256, const=1.0)
                        put_row(kcp[67:68, kvh, :], None, 0, 256, const=1.0)
                    fw.barrier()

                self.pass0_prompt(sw, xp, win_b, cmp_in, kcp, vcp, ident, nps, npt)
                self.dbg("kcp", kcp[:], BF16)
                self.dbg("vcp", vcp[:], BF16)
                self.pass1_prompt(sw, locals())
                if self.sample:
                    self.sample_pass1(locals())
            fw.barrier()
            sw.close()
            self.pass2(top, locals())
            fw.finish()

    def norm_T(self, src, xt, nb, hT, ident, npt, rows=128):
        self.norm_A(src, xt, nb, rows)
        self.norm_B(nb, hT, ident, npt)

    def norm_A(self, src, xt, nb, rows=128):
        if rows < 128:
            self.ms("pool", xt[:], 0.0)
        self.dma(xt[0:rows, :], src)
        self.ms("dve", nb["ss"][:], 0.0)
        self.act(nb["junk"][:], xt[:], AF.Square, acc=nb["ss"][:])
        self.rstd(nb["rs"][:], nb["ss"][:], 1.0 / D)
        self.ts("dve", nb["xn"][:], xt[:], nb["rs"][:, 0:1], None, op0=ALU.mult)

    def norm_B(self, nb, hT, ident, npt):
        pt = npt()
        for k in range(8):
            self.tr(pt[:, k * 128:(k + 1) * 128], nb["xn"][:, k * 128:(k + 1) * 128], ident[:])
        self.cp("act", hT[:].re("p k t -> p (k t)"), pt[:])

    def norm_bufs(self, s, tag):
        fw = self.fw
        xn = fw.sb(s, [128, D], BF16, "xn" + tag)
        return {"junk": xn, "ss": fw.sb(s, [128, 1], F32, "ss" + tag), "rs": fw.sb(s, [128, 1], F32, "rs" + tag), "xn": xn}

    def pass0_prompt(self, sw, xp, win_b, cmp_in, kcp, vcp, ident, nps, npt):
        fw, NT, T = self.fw, self.NT, self.T
        mm, act, tt, cp, ms, dma, dmas = self.mm, self.act, self.tt, self.cp, self.ms, self.dma, self.dmas
        NCB = T // 16
        with ExitStack() as s:
            srcT = {kv: fw.sb(s, [64, 2, 16, NCB + 1], BF16, "srcT" + kv) for kv in "kv"}
            for kv in "kv":
                ms("pool", srcT[kv][:, :, :, NCB:NCB + 1], 0.0)
            ms("pool", kcp[0:64, :, :], 0.0)
            xts = [fw.sb(s, [128, D], F32, f"x0_{i}") for i in range(2)]
            nb = self.norm_bufs(s, "0")
            hT = fw.sb(s, [128, 8, 128], BF16, "hT0")
            for t_ in range(NT):
                xt = xts[t_ % 2]
                self.norm_T(xp[t_ * 128:(t_ + 1) * 128, :], xt, nb, hT, ident, npt)
                ps = nps()
                for gi in range(4):
                    for k in range(8):
                        mm(ps[0:64, gi * 128:(gi + 1) * 128], win_b[:, k, 512 + 64 * gi:576 + 64 * gi], hT[:, k, :],
                           st=(k == 0), sp=(k == 7))
                for h in range(2):
                    cp("act", srcT["k"][:, h, :, 8 * t_:8 * t_ + 8], ps[0:64, h * 128:(h + 1) * 128].re("p (c j) -> p j c", j=16))
                    cp("dve", srcT["v"][:, h, :, 8 * t_:8 * t_ + 8], ps[0:64, 256 + h * 128:384 + h * 128].re("p (c j) -> p j c", j=16))
            w1s = [fw.sb(s, [64, 8, 256], F32, f"w1s{i}") for i in range(2)]
            for kv in "kv":
                pe, w1, b1, w2 = cmp_in[kv]
                w1b = fw.sb(s, [64, 32, 256], BF16, "w1b" + kv)
                for jb in range(4):
                    st = w1s[jb % 2]
                    dma(st[:], V(None, w1.a.rearrange("(j d) n -> d j n", d=64)[:, jb * 8:(jb + 1) * 8, :]))
                    cp("dve", w1b[:, jb * 8:(jb + 1) * 8, :], st[:])
                peT = fw.sb(s, [64, 32], F32, "peT" + kv)
                dmas(peT[:], V(None, pe.a.rearrange("j d -> d j")))
                peTb = fw.sb(s, [64, 32], BF16, "peTb" + kv)
                cp("dve", peTb[:], peT[:])
                b1c = fw.sb(s, [128, 2], F32, "b1c" + kv)
                dmas(b1c[:], V(None, b1.a.rearrange("(c p) -> p c", p=128)))
                w2s = fw.sb(s, [128, 2, 64], F32, "w2s" + kv)
                dma(w2s[:], V(None, w2.a.rearrange("(c p) n -> p c n", p=128)))
                w2b = fw.sb(s, [128, 2, 64], BF16, "w2b" + kv)
                cp("dve", w2b[:], w2s[:])
                cst = fw.sb(s, [128, 2], F32, "cst" + kv)
                for hc in range(2):
                    ps = nps()
                    for j in range(32):
                        mm(ps[:, 0:1], w1b[:, j, hc * 128:(hc + 1) * 128], peTb[:, j:j + 1], st=(j == 0), sp=(j == 31))
                    tt("dve", cst[:, hc:hc + 1], ps[:, 0:1], b1c[:, hc:hc + 1], ALU.add)
                gT = fw.sb(s, [128, 2, 256], BF16, "gT" + kv)
                if NCB < 256:
                    ms("pool", gT[:], 0.0)
                for kvh in range(2):
                    for hc in range(2):
                        ps = nps()
                        for j in range(32):
                            rv = srcT[kv][:, kvh, j, 0:NCB] if j < 16 else srcT[kv][:, kvh, j - 16, 1:NCB + 1]
                            mm(ps[:, 0:NCB], w1b[:, j, hc * 128:(hc + 1) * 128], rv, st=(j == 0), sp=(j == 31))
                        act(gT[:, hc, 0:NCB], ps[:, 0:NCB], AF.Gelu_apprx_tanh, bias=cst[:, hc:hc + 1])
                    if kv == "k":
                        ps = nps()
                        for hc in range(2):
                            mm(ps[0:64, 0:256], w2b[:, hc, :], gT[:, hc, :], st=(hc == 0), sp=(hc == 1))
                        cp("dve", kcp[0:64, kvh, :], ps[0:64, 0:256])
                    else:
                        for ct in range(2):
                            ps = nps()
                            for hc in range(2):
                                mm(ps[:, 0:64], gT[:, hc, ct * 128:(ct + 1) * 128], w2b[:, hc, :], st=(hc == 0), sp=(hc == 1))
                            cp("dve", vcp[:, ct, kvh, 0:64], ps[:, 0:64])
            fw.barrier()

    def pass1_prompt(self, sw, L):
        fw, NT, T = self.fw, self.NT, self.T
        mm, tr, act, ts, tt, stt, cp, ms, iota, dma, dmas = (self.mm, self.tr, self.act, self.ts, self.tt, self.stt,
                                                             self.cp, self.ms, self.iota, self.dma, self.dmas)
        xp, win_b, wout_b, wqm_b, wkm_b = L["xp"], L["win_b"], L["wout_b"], L["wqm_b"], L["wkm_b"]
        kcp, vcp, ident, identf, nps, npt, psa = L["kcp"], L["vcp"], L["ident"], L["identf"], L["nps"], L["npt"], L["psa"]
        nps_s, nps_m = L["nps_s"], L["nps_m"]
        caus_add, win_add, bd01, tri2, csel, e0, mimp, expand = (L["caus_add"], L["win_add"], L["bd01"], L["tri2"],
                                                                 L["csel"], L["e0"], L["mimp"], L["expand"])
        cw, cb, bgate, bif, put_row, x1d = L["cw"], L["cb"], L["bgate"], L["bif"], L["put_row"], L["x1d"]
        p_kv, p_kw, p_vw, p_C, p_n, p_m, p_conv = L["p_kv"], L["p_kw"], L["p_vw"], L["p_C"], L["p_n"], L["p_m"], L["p_conv"]
        with ExitStack() as s:
            ksT = fw.sb(s, [68, 2, T], BF16, "ksT")
            NW = min(8, NT)
            kwT = fw.sb(s, [68, 2, NW * 128], BF16, "kwT")
            phr = fw.sb(s, [1, 128], BF16, "phr")
            vsp = fw.sb(s, [128, NT, 2, 65], BF16, "vsp")
            vwp = fw.sb(s, [128, NW, 2, 65], BF16, "vwp")
            ms("pool", vsp[:], 1.0)
            ms("pool", vwp[:], 1.0)
            with ExitStack() as tmps:
                self.rowt = fw.sb(tmps, [1, 4096], F32, "rowt1")
                self.rowb = fw.sb(tmps, [1, 4096], BF16, "rowb1")
                for kvh in range(2):
                    put_row(ksT[64:65, kvh, :], [[128, NT], [0, 128]], 0, T)
                    put_row(ksT[65:66, kvh, :], [[0, NT], [1, 128]], 0, T)
                    put_row(ksT[66:67, kvh, :], None, 0, T, const=1.0)
                    put_row(ksT[67:68, kvh, :], None, 0, T, const=1.0)
                    put_row(kwT[65:66, kvh, :], [[0, NW], [1, 128]], 0, NW * 128)
                    put_row(kwT[66:67, kvh, :], None, 0, NW * 128, const=1.0)
                    put_row(kwT[67:68, kvh, :], None, 0, NW * 128, const=1.0)
                fw.barrier()
            qps = [fw.sb(s, [68, 2, 4, 128], BF16, f"qp{i}") for i in range(2)]
            srow = fw.sb(s, [1, 8, 128], F32, "srow")
            for h in range(8):
                ms("pool", srow[0:1, h, :], 2.0 ** (-(h + 1)))
            r67 = fw.sb(s, [1, 8, 128], F32, "r67")
            iota(r67[:], [[0, 8], [1, 128]], base=0, cm=0)
            tt("pool", r67[:], r67[:], srow[:], ALU.mult)
            ts("pool", r67[:], r67[:], -1.0, None, op0=ALU.mult)
            srb = fw.sb(s, [1, 8, 128], BF16, "srb")
            r67b = fw.sb(s, [1, 8, 128], BF16, "r67b")
            r66b = fw.sb(s, [1, 8, 128], BF16, "r66b")
            cp("pool", srb[:], srow[:])
            cp("pool", r67b[:], r67[:])
            for qp in qps:
                for kvh in range(2):
                    dma(qp[64:65, kvh], srb[0:1, 4 * kvh:4 * kvh + 4, :])
                    dma(qp[65:66, kvh], srb[0:1, 4 * kvh:4 * kvh + 4, :])
                    dma(qp[67:68, kvh], r67b[0:1, 4 * kvh:4 * kvh + 4, :])
            xts = [fw.sb(s, [128, D], F32, f"x1_{i}") for i in range(2)]
            nb = self.norm_bufs(s, "1")
            hTs = [fw.sb(s, [128, 8, 128], BF16, f"hT1_{i}") for i in range(2)]
            pkv = fw.sb(s, [128, 792], F32, "pkv")
            gt = fw.sb(s, [128, 24], F32, "gt")
            pts = RR([fw.sb(s, [128, 512], BF16, f"pt{i}") for i in range(4)])
            mks = RR([fw.sb(s, [128, 512], BF16, f"mk{i}") for i in range(2)])
            obr = [[fw.sb(s, [128, 4, 65], F32, f"obr{k}{i}") for i in range(3)] for k in range(2)]
            rdc = fw.sb(s, [128, 4], F32, "rdc")
            imp4 = fw.sb(s, [128, 4, 64], F32, "imp4")
            imp = fw.sb(s, [128, 64], F32, "imp")
            imp2 = fw.sb(s, [128, 64], F32, "imp2")
            mx1 = fw.sb(s, [128, 8], F32, "mx1")
            mx2 = fw.sb(s, [128, 8], F32, "mx2")
            selm = [fw.sb(s, [128, 64], BF16, f"selm{k}") for k in range(2)]
            selT = [fw.sb(s, [64, 4, 128], BF16, f"selT{k}") for k in range(2)]
            fpb = fw.sb(s, [128, 3], F32, "fpb")
            ms("pool", fpb[:], -1.0)
            rd = fw.sb(s, [128, 3, 4], F32, "rd")
            sc3 = fw.sb(s, [128, 3, 4], F32, "sc3")
            onsa = fw.sb(s, [128, 4, 64], F32, "onsa")
            otmp = fw.sb(s, [128, 4, 64], F32, "otmp")
            ss4m = fw.sb(s, [128, 4], F32, "ss4pm")
            rs4m = fw.sb(s, [128, 4], F32, "rs4pm")
            ss4 = fw.sb(s, [128, 4], F32, "ss4")
            rs4 = fw.sb(s, [128, 4], F32, "rs4")
            mixin = fw.sb(s, [128, D], BF16, "mixin")
            mT = fw.sb(s, [128, 8, 128], BF16, "mT")
            x1t = fw.sb(s, [128, D], F32, "x1t")
            gif = fw.sb(s, [128, 8], F32, "gif")
            l1 = fw.sb(s, [128, 4], F32, "l1")
            gsb = fw.sb(s, [128, 12], F32, "gsb")
            wl = fw.sb(s, [128, 4], F32, "wl")
            ul = fw.sb(s, [128, 4], F32, "ul")
            tmp4 = fw.sb(s, [128, 4], F32, "tmp4")
            dec = fw.sb(s, [128, 4], F32, "dec")
            ebt = fw.sb(s, [128, 8], F32, "ebt")
            vmu = fw.sb(s, [128, 4, 129], BF16, "vmu")
            sigo = fw.sb(s, [128, 512], F32, "sigo")
            xcv = [fw.sb(s, [128, 4, 131], F32, f"xcv{i}") for i in range(2)]
            ms("pool", xcv[0][:], 0.0)
            cacc = fw.sb(s, [128, 4, 128], F32, "cacc")
            xc = fw.sb(s, [128, 4, 128], BF16, "xc")
            qmT = fw.sb(s, [128, 4, 128], BF16, "qmT")
            qmS = [fw.sb(s, [128, 4, 128], BF16, f"qmS{i}") for i in range(2)]
            for q_ in qmS:
                ms("pool", q_[:], 0.0)
            kmT = fw.sb(s, [128, 4, 128], BF16, "kmT")
            kmS = [fw.sb(s, [128, 4, 128], BF16, f"kmS{i}") for i in range(2)]
            mqk = fw.sb(s, [128, 4, 128], BF16, "mqk")
            Sf = fw.sb(s, [128, 4, 129], F32, "Sf")
            ms("pool", Sf[:], 0.0)
            Sb = [fw.sb(s, [128, 4, 129], BF16, f"Sb{i}") for i in range(3)]
            ms("pool", Sb[0][:], 0.0)
            dS = fw.sb(s, [128, 4, 129], F32, "dS")
            dn = fw.sb(s, [128, 4], F32, "dn")
            hout = fw.sb(s, [128, 4, 128], F32, "hout")
            hsq = cacc
            m4 = fw.sb(s, [4, 8], F32, "m4")
            R = fw.sb(s, [4, 1], F32, "Rm")
            ms("pool", R[:], 0.0)
            tsb = fw.sb(s, [4, 384], F32, "tsb")
            segs = [(0, 64), (64, 128)]
            KSC = 128.0 ** -0.5

            for t_ in range(NT):
                xt = xts[t_ % 2]
                qp = qps[t_ % 2]
                ts("pool", r66b[:], srow[:], -128.0 * t_, None, op0=ALU.mult)
                for kvh in range(2):
                    dma(qp[66:67, kvh], r66b[0:1, 4 * kvh:4 * kvh + 4, :])
                hT = hTs[t_ % 2]
                if t_ == 0:
                    self.norm_T(xp[0:128, :], xt, nb, hT, ident, npt)
                psA = nps()
                psB = nps()
                for k in range(8):
                    mm(psA[:, 0:512], hT[:, k, :], win_b[:, k, 512:1024], st=(k == 0), sp=(k == 7))
                for k in range(8):
                    mm(psB[:, 0:280], hT[:, k, :], win_b[:, k, 1024:1304], st=(k == 0), sp=(k == 7))
                cp("dve", pkv[:, 0:512], psA[:, 0:512])
                cp("act", pkv[:, 512:792], psB[:, 0:280])
                r0 = t_ * 128
                for i_, n_ in enumerate(("p_kc", "p_vc", "p_ks", "p_vs")):
                    dma(p_kv[n_][r0:r0 + 128, :], pkv[:, i_ * 128:(i_ + 1) * 128])
                if r0 >= T - 512:
                    w0 = r0 - (T - min(512, T))
                    dma(p_kw[w0:w0 + 128, :], pkv[:, 512:640])
                    dma(p_vw[w0:w0 + 128, :], pkv[:, 640:768])
                cp("pool", vsp[:, t_, :, 0:64], pkv[:, 384:512].re("p (h d) -> p h d", h=2))
                ws = t_ % NW
                cp("pool", vwp[:, ws, :, 0:64], pkv[:, 640:768].re("p (h d) -> p h d", h=2))
                tt("dve", gt[:], pkv[:, 768:792], bgate[:], ALU.add)
                self.sigm(gt[:], gt[:])
                psQ0 = nps()
                psQ1 = nps()
                for h in range(8):
                    ps = psQ0 if h < 4 else psQ1
                    for k in range(8):
                        mm(ps[0:64, (h % 4) * 128:(h % 4 + 1) * 128], win_b[:, k, 64 * h:64 * h + 64], hT[:, k, :],
                           st=(k == 0), sp=(k == 7))
                act(qp[0:64, 0].re("p g t -> p (g t)"), psQ0[0:64, :], AF.Copy, scale=0.125)
                act(qp[0:64, 1].re("p g t -> p (g t)"), psQ1[0:64, :], AF.Copy, scale=0.125)
                psK = nps()
                for gi, c0 in enumerate((768, 832, 1024, 1088)):
                    for k in range(8):
                        mm(psK[0:64, gi * 128:(gi + 1) * 128], win_b[:, k, c0:c0 + 64], hT[:, k, :], st=(k == 0), sp=(k == 7))
                cp("dve", ksT[0:64, :, r0:r0 + 128], psK[0:64, 0:256].re("p (h t) -> p h t", h=2))
                cp("dve", kwT[0:64, :, ws * 128:(ws + 1) * 128], psK[0:64, 256:512].re("p (h t) -> p h t", h=2))
                ms("pool", phr[:], 128.0 * t_)
                for kvh in range(2):
                    dma(kwT[64:65, kvh, ws * 128:(ws + 1) * 128], phr[:])

                def mlstm_gen():
                    psG = nps_m()
                    for k in range(8):
                        mm(psG[:, 0:8], hT[:, k, :], win_b[:, k, 2840:2848], st=(k == 0), sp=(k == 7))
                    tt("dve", gif[:], psG[:, 0:8], bif[:], ALU.add)
                    act(l1[:], gif[:, 4:8], AF.Exp, scale=-1.0)
                    act(l1[:], l1[:], AF.Ln, bias=1.0)
                    yield
                    psC = nps_m()
                    mm(psC[:, 0:4], tri2[:], l1[:])
                    mm(psC[:, 4:8], csel[:, 0, :], l1[:])
                    mm(psC[:, 8:12], csel[:, 1, :], l1[:])
                    cp("dve", gsb[:], psC[:, 0:12])
                    act(wl[:], gsb[:, 0:4], AF.Exp, scale=-1.0)
                    tt("dve", tmp4[:], gif[:, 0:4], gsb[:, 0:4], ALU.add)
                    act(ul[:], tmp4[:], AF.Exp)
                    act(ebt[:], gsb[:, 4:12], AF.Exp, scale=-1.0)
                    tt("dve", dec[0:64, :], tmp4[0:64, :], gsb[0:64, 4:8], ALU.subtract)
                    tt("dve", dec[64:128, :], tmp4[64:128, :], gsb[64:128, 8:12], ALU.subtract)
                    yield
                    psV = nps_m()
                    for k in range(8):
                        mm(psV[:, 0:512], hT[:, k, :], win_b[:, k, 1816:2328], st=(k == 0), sp=(k == 7))
                    tt("dve", vmu[:, :, 0:128], psV[:, 0:512].re("p (h e) -> p h e", h=4), ul[:].un(2).bc([128, 4, 128]), ALU.mult)
                    cp("dve", vmu[:, :, 128], ul[:])
                    yield
                    psO = nps_m()
                    for k in range(8):
                        mm(psO[:, 0:512], hT[:, k, :], win_b[:, k, 2328:2840], st=(k == 0), sp=(k == 7))
                    self.sigm(sigo[:], psO[:, 0:512])
                    yield
                    xcur, xnext = xcv[t_ % 2], xcv[(t_ + 1) % 2]
                    psX = nps_m()
                    for ch in range(4):
                        for k in range(8):
                            mm(psX[:, ch * 128:(ch + 1) * 128], win_b[:, k, 1304 + ch * 128:1432 + ch * 128], hT[:, k, :],
                               st=(k == 0), sp=(k == 7))
                    cp("act", xcur[:, :, 3:131], psX[:, :].re("p (c t) -> p c t", c=4))
                    cp("pool", xnext[:, :, 0:3], xcur[:, :, 128:131])
                    yield
                    for ch in range(4):
                        ts("dve", cacc[:, ch, :], xcur[:, ch, 0:128], cw[:, ch, 0:1], cb[:, ch:ch + 1], op0=ALU.mult, op1=ALU.add)
                        for j in range(1, 4):
                            stt("dve", cacc[:, ch, :], xcur[:, ch, j:j + 128], cw[:, ch, j:j + 1], cacc[:, ch, :], ALU.mult, ALU.add)
                        if ch % 2 == 1:
                            yield
                    self.sigm(hout[:], cacc[:])
                    tt("dve", xc[:], cacc[:], hout[:], ALU.mult)
                    if t_ == NT - 1:
                        for j in range(3):
                            dmas(V(None, p_conv.a[j].rearrange("(c p) -> p c", p=128)), xcur[:, :, 128 + j])
                    yield
                    psq = nps_m()
                    for h in range(4):
                        mm(psq[:, h * 128:(h + 1) * 128], wqm_b[:, h, :], xc[:, h, :])
                    cp("act", qmT[:].re("p h t -> p (h t)"), psq[:, :])
                    for si, (a_, b_) in enumerate(segs):
                        cp("dve", qmS[si][:, :, a_:b_], psq[:, :].re("p (h t) -> p h t", h=4)[:, :, a_:b_])
                    psk = nps_m()
                    for h in range(4):
                        mm(psk[:, h * 128:(h + 1) * 128], wkm_b[:, h, :], xc[:, h, :])
                    act(kmT[:].re("p h t -> p (h t)"), psk[:, :], AF.Copy, scale=KSC)
                    yield
                    pskt = nps_m()
                    for h in range(4):
                        mm(pskt[:, h * 128:(h + 1) * 128], xc[:, h, :], wkm_b[:, h, :])
                    for si in range(2):
                        ts("dve", kmS[si][:].re("p h t -> p (h t)"), pskt[:, :], csel[:, si, 0:1], KSC, op0=ALU.mult, op1=ALU.mult)
                    psqk = nps_m()
                    for h in range(4):
                        mm(psqk[:, h * 128:(h + 1) * 128], kmT[:, h, :], qmT[:, h, :])
                    tt("dve", mqk[:], psqk[:, :].re("p (h t) -> p h t", h=4), bd01[:].un(1).bc([128, 4, 128]), ALU.mult)
                    yield
                    sbs = [Sb[(2 * t_) % 3], Sb[(2 * t_ + 1) % 3], Sb[(2 * t_ + 2) % 3]]
                    for si in range(2):
                        pd = [nps_m(), nps_m()]
                        for h in range(4):
                            mm(pd[h // 2][:, (h % 2) * 129:(h % 2) * 129 + 129], kmS[si][:, h, :], vmu[:, h, :])
                        eb = ebt[:, 4 * si:4 * si + 4].un(2).bc([128, 4, 129])
                        tt("dve", Sf[:], Sf[:], eb, ALU.mult)
                        for hh in range(2):
                            tt("dve", dS[:, 2 * hh:2 * hh + 2, :], pd[hh][:, 0:258].re("p (h e) -> p h e", h=2),
                               ebt[:, 4 * si + 2 * hh:4 * si + 2 * hh + 2].un(2).bc([128, 2, 129]), ALU.mult)
                        tt("dve", Sf[:], Sf[:], dS[:], ALU.add)
                        cp("act", sbs[si + 1][:], Sf[:])
                        yield
                    pa = [nps_m(), nps_m()]
                    for h in range(4):
                        o_ = pa[h // 2][:, (h % 2) * 129:(h % 2) * 129 + 129]
                        mm(o_, mqk[:, h, :], vmu[:, h, :], st=True, sp=False)
                        mm(o_, qmS[0][:, h, :], sbs[0][:, h, :], st=False, sp=False)
                        mm(o_, qmS[1][:, h, :], sbs[1][:, h, :], st=False, sp=True)
                    for hh in range(2):
                        av = pa[hh][:, 0:258].re("p (h e) -> p h e", h=2)
                        tt("dve", dn[:, 2 * hh:2 * hh + 2], av[:, :, 128], wl[:, 2 * hh:2 * hh + 2], ALU.mult)
                    stt("dve", tmp4[:], dn[:], -1.0, dn[:], ALU.mult, ALU.max)
                    ts("dve", tmp4[:], tmp4[:], 1.0, None, op0=ALU.max)
                    self.recip(tmp4[:], tmp4[:])
                    tt("dve", tmp4[:], tmp4[:], wl[:], ALU.mult)
                    for hh in range(2):
                        av = pa[hh][:, 0:258].re("p (h e) -> p h e", h=2)
                        tt("dve", hout[:, 2 * hh:2 * hh + 2, :], av[:, :, 0:128],
                           tmp4[:, 2 * hh:2 * hh + 2].un(2).bc([128, 2, 128]), ALU.mult)
                    yield
                    tt("dve", hsq[:], hout[:], hout[:], ALU.mult)
                    self.rsum(ss4m[:], hsq[:])
                    self.rstd(rs4m[:], ss4m[:], 1.0 / 128)
                    tt("dve", hout[:], hout[:], rs4m[:].un(2).bc([128, 4, 128]), ALU.mult)
                    tt("dve", mixin[:, 512:1024], hout[:].re("p h e -> p (h e)"), sigo[:], ALU.mult)
                    yield
                    psT = nps_m()
                    tr(psT[0:4, 0:128], dec[:], identf[:])
                    tr(psT[0:4, 128:256], gsb[:, 4:8], identf[:])
                    tr(psT[0:4, 256:384], gsb[:, 8:12], identf[:])
                    cp("dve", tsb[:], psT[0:4, 0:384])
                    self.rmax(m4[:, 0:1], tsb[:, 0:64])
                    self.rmax(m4[:, 1:2], tsb[:, 64:128])
                    stt("dve", R[:], R[:], tsb[:, 128:129], m4[:, 0:1], ALU.subtract, ALU.max)
                    stt("dve", R[:], R[:], tsb[:, 256:257], m4[:, 1:2], ALU.subtract, ALU.max)

                mg = mlstm_gen()
                steps = []
                for kvh in range(2):
                    cts = [0] if t_ < 16 else [0, 1]
                    for ci, ct in enumerate(cts):
                        steps.append(("cmp", kvh, ct, ci == 0, ci == len(cts) - 1))
                for kvh in range(2):
                    k0 = max(0, t_ - 4)
                    for kt in range(k0, t_ + 1):
                        steps.append(("win", kvh, kt, kt == k0, kt == t_))
                for kvh in range(2):
                    for kt in range(t_ + 1):
                        steps.append(("sel", kvh, kt, kt == 0, kt == t_))
                pend = {}

                def score(i):
                    kind, kvh, k, first, last = steps[i]
                    qv = qp[:, kvh].re("p g t -> p (g t)")
                    S = nps_s()
                    if kind == "cmp":
                        Kq = 128 * t_ - 2048 * k - 31
                        need_mask = Kq < 2032
                        mm(S[:, :], kcp[:, kvh, k * 128:(k + 1) * 128], qv, st=True, sp=not need_mask)
                        if need_mask:
                            mk = mks()
                            ts("dve", mk[:], e0[:], float(Kq), NEG, op0=ALU.is_gt, op1=ALU.mult)
                            mm(S[:, :], ident[:], mk[:], st=False, sp=True)
                    elif kind == "win":
                        madd = caus_add if k == t_ else (win_add if k == t_ - 4 else None)
                        mm(S[:, :], kwT[:, kvh, (k % NW) * 128:(k % NW + 1) * 128], qv, st=True, sp=(madd is None))
                        if madd is not None:
                            mm(S[:, :], ident[:], madd[:], st=False, sp=True)
                    else:
                        if first and kvh == 0:
                            flush_sel()
                        mm(S[:, :], ksT[:, kvh, k * 128:(k + 1) * 128], qv, st=True, sp=False)
                        if k < t_:
                            mm(S[:, :], expand[:, k * 128:(k + 1) * 128], selT[kvh][:].re("p g t -> p (g t)"), st=False, sp=True)
                        else:
                            mm(S[:, :], ident[:], caus_add[:], st=False, sp=True)
                    pt = pts()
                    act(pt[:], S[:, :], AF.Exp)
                    pend[i] = pt

                def finish_cmp(kvh):
                    bank = psa[kvh]
                    cp("dve", obr[kvh][0][:, :, 0:64], bank[:, 0:256].re("p (g d) -> p g d", g=4))
                    cp("act", imp4[:].re("p g j -> p (g j)"), bank[:, 256:512])
                    cp("dve", obr[kvh][0][:, :, 64], imp4[:, :, 63])
                    ts("dve", rdc[:], obr[kvh][0][:, :, 64], 1e-30, None, op0=ALU.max)
                    self.recip(rdc[:], rdc[:])
                    ts("dve", imp[:], imp4[:, 0, :], rdc[:, 0:1], None, op0=ALU.mult)
                    for g in range(1, 4):
                        stt("dve", imp[:], imp4[:, g, :], rdc[:, g:g + 1], imp[:], ALU.mult, ALU.add)
                    ms("dve", imp[:, 63:64], 0.0)
                    if t_ == 0:
                        tt("dve", imp[:, 0:2], imp[:, 0:2], fpb[:, 1:3], ALU.max)
                    else:
                        tt("dve", imp[:, 2 * t_ - 1:2 * t_ + 2], imp[:, 2 * t_ - 1:2 * t_ + 2], fpb[:, 0:3], ALU.max)
                        ms("dve", imp[:, 0:1], 3e9)
                    fw.op("dve", lambda e: e.max(out=mx1[:].a, in_=imp[:].a), reads=[imp], writes=[mx1])
                    fw.op("dve", lambda e: e.match_replace(out=imp2[:].a, in_to_replace=mx1[:].a, in_values=imp[:].a,
                                                           imm_value=-1e30), reads=[imp, mx1], writes=[imp2])
                    fw.op("dve", lambda e: e.max(out=mx2[:].a, in_=imp2[:].a), reads=[imp2], writes=[mx2])
                    ts("dve", selm[kvh][:], imp[:], mx2[:, 7:8], NEG, op0=ALU.is_lt, op1=ALU.mult)

                def flush_sel():
                    for kvh_ in range(2):
                        ptr = npt()
                        tr(ptr[0:64, 0:128], selm[kvh_][:], ident[:])
                        cp("dve", selT[kvh_][:], ptr[0:64, 0:128].un(1).bc([64, 4, 128]))

                def pv(i):
                    kind, kvh, k, first, last = steps[i]
                    pt = pend.pop(i)
                    acc = psa[kvh]
                    if kind == "cmp":
                        for g in range(4):
                            mm(acc[:, g * 64:(g + 1) * 64], pt[:, g * 128:(g + 1) * 128], vcp[:, k, kvh, 0:64], st=first, sp=last)
                            mm(acc[:, 256 + g * 64:320 + g * 64], pt[:, g * 128:(g + 1) * 128], mimp[:, k, :], st=first, sp=last)
                    else:
                        vv = vwp[:, k % NW, kvh, :] if kind == "win" else vsp[:, k, kvh, :]
                        for g in range(4):
                            mm(acc[:, g * 65:(g + 1) * 65], pt[:, g * 128:(g + 1) * 128], vv, st=first, sp=last)
                    if last:
                        if kind == "cmp":
                            finish_cmp(kvh)
                        elif kind == "win":
                            cp("act", obr[kvh][2][:].re("p g d -> p (g d)"), acc[:, 0:260])
                        else:
                            cp("dve", obr[kvh][1][:].re("p g d -> p (g d)"), acc[:, 0:260])

                base = 1e9 + 1e6 * (2 * t_)
                ms("dve", fpb[0:64, 0:1], base - 1e6)
                ms("dve", fpb[0:64, 1:2], base)
                ms("dve", fpb[64:128, 1:2], base)
                ms("dve", fpb[64:128, 2:3], base + 1e6)
                LA = 2
                for i in range(min(LA, len(steps))):
                    score(i)
                iB = min(8, len(steps) - 1)
                pace = max(1, (len(steps) - 2) // 18)
                for i in range(len(steps)):
                    if i + LA < len(steps):
                        score(i + LA)
                    pv(i)
                    if i >= 1 and (i - 1) % pace == 0:
                        next(mg, None)
                    if t_ + 1 < NT:
                        if i == 0:
                            self.norm_A(xp[(t_ + 1) * 128:(t_ + 2) * 128, :], xts[(t_ + 1) % 2], nb)
                        if i == iB:
                            self.norm_B(nb, hTs[(t_ + 1) % 2], ident, npt)
                for _ in mg:
                    pass
                for kvh in range(2):
                    for br in range(3):
                        ts("dve", rd[:, br, :], obr[kvh][br][:, :, 64], 1e-30, None, op0=ALU.max)
                    self.recip(rd[:].re("p b g -> p (b g)"), rd[:].re("p b g -> p (b g)"))
                    gv = gt[:, 12 * kvh:12 * kvh + 12].re("p (g b) -> p b g", b=3)
                    tt("dve", sc3[:], rd[:], gv, ALU.mult)
                    tt("dve", onsa[:], obr[kvh][0][:, :, 0:64], sc3[:, 0, :].un(2).bc([128, 4, 64]), ALU.mult)
                    for br in (1, 2):
                        tt("dve", otmp[:], obr[kvh][br][:, :, 0:64], sc3[:, br, :].un(2).bc([128, 4, 64]), ALU.mult)
                        tt("dve", onsa[:], onsa[:], otmp[:], ALU.add)
                    tt("dve", otmp[:], onsa[:], onsa[:], ALU.mult)
                    self.rsum(ss4[:], otmp[:])
                    self.rstd(rs4[:], ss4[:], 1.0 / 64)
                    tt("dve", mixin[:, 256 * kvh:256 * kvh + 256].re("p (g d) -> p g d", g=4), onsa[:],
                       rs4[:].un(2).bc([128, 4, 64]), ALU.mult)

                ptm = npt()
                for k in range(8):
                    tr(ptm[:, k * 128:(k + 1) * 128], mixin[:, k * 128:(k + 1) * 128], ident[:])
                cp("act", mT[:].re("p k t -> p (k t)"), ptm[:])
                for g in range(2):
                    ps = nps()
                    for k in range(8):
                        mm(ps[:, :], mT[:, k, :], wout_b[:, k, g * 512:(g + 1) * 512], st=(k == 0), sp=(k == 7))
                    tt("dve", x1t[:, g * 512:(g + 1) * 512], xt[:, g * 512:(g + 1) * 512], ps[:, :], ALU.add)
                dma(x1d[r0:r0 + 128, :], x1t[:])

            dma(V(None, p_m.a.rearrange("(h o) -> h o", o=1)), R[:])
            ones4 = fw.sb(s, [4, 128], F32, "ones4")
            ms("pool", ones4[:], 1.0)
            ts("dve", ones4[:], ones4[:], R[:, 0:1], None, op0=ALU.mult)
            ps = nps()
            mm(ps[:, 0:4], ones4[:], identf[0:4, 0:4])
            act(tmp4[:], ps[:, 0:4], AF.Exp, scale=-1.0)
            tt("dve", Sf[:], Sf[:], tmp4[:].un(2).bc([128, 4, 129]), ALU.mult)
            dmas(V(None, p_n.a.rearrange("h d -> d h")), Sf[:, :, 128])
            for h in range(4):
                ps = nps()
                tr(ps[:, 0:128], Sf[:, h, 0:128], identf[:])
                cp("dve", hsq[:, h, :], ps[:, 0:128])
                dma(p_C[h], hsq[:, h, :])
            fw.barrier()

    def load_w(self, dst, src, rows_chunks, ncols, stg, gcol=None, g0=0):
        for k in range(rows_chunks):
            st = stg[k % 2]
            self.dma(st[:, 0:ncols], src[k * 128:(k + 1) * 128, :])
            if k % 2 == 0:
                if gcol is None:
                    self.cp("dve", dst[:, k, :], st[:, 0:ncols])
                else:
                    self.ts("dve", dst[:, k, :], st[:, 0:ncols], gcol[:, g0 + k:g0 + k + 1], None, op0=ALU.mult)
            elif gcol is None:
                self.cp("act", dst[:, k, :], st[:, 0:ncols])
            else:
                self.act(dst[:, k, :], st[:, 0:ncols], AF.Copy, scale=gcol[:, g0 + k:g0 + k + 1])

    def pass2(self, top, L):
        fw, NT, T = self.fw, self.NT, self.T
        mm, tr, act, ts, tt, stt, cp, ms, dma, dmas = (self.mm, self.tr, self.act, self.ts, self.tt, self.stt,
                                                       self.cp, self.ms, self.dma, self.dmas)
        ident, nps, npt, psa = L["ident"], L["nps"], L["npt"], L["psa"]
        x1d, x2d, y_p = L["x1d"], L["x2d"], L["y_p"]
        g2 = fw.sb(top, [128, 32], F32, "g2")
        for i_, g_ in enumerate((L["g_xa"], L["g_mem"], L["g_ffn"])):
            dmas(g2[:, 8 * i_:8 * i_ + 8], V(None, g_.a.rearrange("(k p) -> p k", p=128)))
        with ExitStack() as s:
            wxq_b = fw.sb(s, [128, 8, D], BF16, "wxq_b")
            wxo_b = fw.sb(s, [128, 8, D], BF16, "wxo_b")
            mkT = fw.sb(s, [128, 8, 256], BF16, "mkT")
            mvp = fw.sb(s, [128, 2, 4, 257], BF16, "mvp")
            ms("pool", mvp[:], 1.0)
            xts = [fw.sb(s, [128, D], F32, f"x2_{i}") for i in range(2)]
            nb = self.norm_bufs(s, "2")
            hT = fw.sb(s, [128, 8, 128], BF16, "hT2")
            with ExitStack() as s2:
                stg = [fw.sb(s2, [128, D], F32, f"stg2{i}") for i in range(2)]
                wxk_b = fw.sb(s2, [128, 8, D], BF16, "wxk_b")
                wxv_b = fw.sb(s2, [128, 8, D], BF16, "wxv_b")
                self.load_w(wxq_b, L["w_xq"], 8, D, stg, g2, 0)
                self.load_w(wxo_b, L["w_xo"], 8, D, stg)
                self.load_w(wxk_b, L["w_xk"], 8, D, stg, g2, 8)
                self.load_w(wxv_b, L["w_xv"], 8, D, stg, g2, 8)
                mo = fw.sb(s2, [128, D], F32, "mo")
                for mt in range(2):
                    xt = xts[mt % 2]
                    self.norm_T(L["memp"][mt * 128:(mt + 1) * 128, :], xt, nb, hT, ident, npt)
                    for wi, (wb_, po) in enumerate(((wxk_b, L["p_mk"]), (wxv_b, L["p_mv"]))):
                        for g in range(2):
                            ps = nps()
                            for k in range(8):
                                mm(ps[:, :], hT[:, k, :], wb_[:, k, g * 512:(g + 1) * 512], st=(k == 0), sp=(k == 7))
                            cp("dve" if g == 0 else "act", mo[:, g * 512:(g + 1) * 512], ps[:, :])
                        dma(po[mt * 128:(mt + 1) * 128, :], mo[:])
                        if wi == 1:
                            cp("pool", mvp[:, mt, :, 0:256], mo[:].re("p (h d) -> p h d", h=4))
                    for c4 in range(2):
                        ps = nps()
                        for cc in range(4):
                            c = c4 * 4 + cc
                            for k in range(8):
                                mm(ps[:, cc * 128:(cc + 1) * 128], wxk_b[:, k, c * 128:(c + 1) * 128], hT[:, k, :],
                                   st=(k == 0), sp=(k == 7))
                        cp("dve", mkT[:, c4 * 4:c4 * 4 + 4, mt * 128:(mt + 1) * 128], ps[:, :].re("p (c t) -> p c t", c=4))
                fw.barrier()
            qxT = fw.sb(s, [128, 8, 128], BF16, "qxT")
            pts = [fw.sb(s, [128, 512], BF16, f"pxt{i}") for i in range(2)]
            ox = fw.sb(s, [128, D], BF16, "ox")
            oxT = fw.sb(s, [128, 8, 128], BF16, "oxT")
            rdx = fw.sb(s, [128, 1], F32, "rdx")
            x2t = fw.sb(s, [128, D], F32, "x2t")
            for t_ in range(NT):
                xt = xts[t_ % 2]
                r0 = t_ * 128
                self.norm_T(x1d[r0:r0 + 128, :], xt, nb, hT, ident, npt)
                for c4 in range(2):
                    ps = nps()
                    for cc in range(4):
                        c = c4 * 4 + cc
                        for k in range(8):
                            mm(ps[:, cc * 128:(cc + 1) * 128], wxq_b[:, k, c * 128:(c + 1) * 128], hT[:, k, :],
                               st=(k == 0), sp=(k == 7))
                    act(qxT[:, c4 * 4:c4 * 4 + 4, :].re("p c t -> p (c t)"), ps[:, :], AF.Copy, scale=1.0 / 16)
                for mt in range(2):
                    S = nps()
                    for h in range(4):
                        for hf in range(2):
                            mm(S[:, h * 128:(h + 1) * 128], mkT[:, 2 * h + hf, mt * 128:(mt + 1) * 128], qxT[:, 2 * h + hf, :],
                               st=(hf == 0), sp=(hf == 1))
                    act(pts[mt][:], S[:, :], AF.Exp)
                for h in range(4):
                    acc = psa[h % 2]
                    for mt in range(2):
                        mm(acc[:, 0:257], pts[mt][:, h * 128:(h + 1) * 128], mvp[:, mt, h, :], st=(mt == 0), sp=(mt == 1))
                    self.recip(rdx[:], acc[:, 256:257])
                    ts("dve", ox[:, h * 256:(h + 1) * 256], acc[:, 0:256], rdx[:, 0:1], None, op0=ALU.mult)
                pto = npt()
                for k in range(8):
                    tr(pto[:, k * 128:(k + 1) * 128], ox[:, k * 128:(k + 1) * 128], ident[:])
                cp("act", oxT[:].re("p k t -> p (k t)"), pto[:])
                for g in range(2):
                    ps = nps()
                    for k in range(8):
                        mm(ps[:, :], oxT[:, k, :], wxo_b[:, k, g * 512:(g + 1) * 512], st=(k == 0), sp=(k == 7))
                    tt("dve", x2t[:, g * 512:(g + 1) * 512], xt[:, g * 512:(g + 1) * 512], ps[:, :], ALU.add)
                dma(x2d[r0:r0 + 128, :], x2t[:])
            if self.sample:
                S = self.S
                xt = xts[0]
                self.norm_T(S["x1s"][:, :], xt, nb, hT, ident, npt, rows=16)
                qxs = fw.sb(s, [128, 8, 16], BF16, "qxs")
                ps = nps()
                for c in range(8):
                    for k in range(8):
                        mm(ps[:, c * 16:(c + 1) * 16], wxq_b[:, k, c * 128:(c + 1) * 128], hT[:, k, 0:16], st=(k == 0), sp=(k == 7))
                act(qxs[:].re("p c t -> p (c t)"), ps[:, 0:128], AF.Copy, scale=1.0 / 16)
                ms("pool", ox[:], 0.0)
                msg = fw.sb(s, [128, 2, D], F32, "msg")
                mkb = fw.sb(s, [128, 2, D], BF16, "mkb")
                ptx = [fw.sb(s, [128, 16], BF16, f"ptx{i}") for i in range(2)]
                oxb = fw.sb(s, [4, D], BF16, "oxb")
                for b in range(4):
                    dma(msg[:], V(None, S["cmk"].a[b].rearrange("(t p) f -> p t f", p=128)))
                    cp("dve", mkb[:, 0, :], msg[:, 0, :])
                    cp("pool", mkb[:, 1, :], msg[:, 1, :])
                    for mt in range(2):
                        pt = npt()
                        for c in range(8):
                            tr(pt[:, c * 128:(c + 1) * 128], mkb[:, mt, c * 128:(c + 1) * 128], ident[:])
                        cp("act", mkT[:, :, mt * 128:(mt + 1) * 128], pt[:, :].re("p (c t) -> p c t", c=8))
                    dma(msg[:], V(None, S["cmv"].a[b].rearrange("(t p) f -> p t f", p=128)))
                    for mt in range(2):
                        cp("dve" if mt == 0 else "pool", mvp[:, mt, :, 0:256], msg[:, mt, :].re("p (h d) -> p h d", h=4))
                    for mt in range(2):
                        Sx = nps()
                        for h in range(4):
                            for hf in range(2):
                                mm(Sx[:, h * 4:(h + 1) * 4], mkT[:, 2 * h + hf, mt * 128:(mt + 1) * 128],
                                   qxs[:, 2 * h + hf, 4 * b:4 * b + 4], st=(hf == 0), sp=(hf == 1))
                        act(ptx[mt][:], Sx[:, 0:16], AF.Exp)
                    for h in range(4):
                        acc = psa[h % 2]
                        for mt in range(2):
                            mm(acc[0:4, 0:257], ptx[mt][:, 4 * h:4 * h + 4], mvp[:, mt, h, :], st=(mt == 0), sp=(mt == 1))
                        self.recip(rdx[0:4, :], acc[0:4, 256:257])
                        ts("dve", oxb[:, h * 256:(h + 1) * 256], acc[0:4, 0:256], rdx[0:4, 0:1], None, op0=ALU.mult)
                    dma(ox[4 * b:4 * b + 4, :], oxb[:])
                pto = npt()
                for k in range(8):
                    tr(pto[:, k * 128:(k + 1) * 128], ox[:, k * 128:(k + 1) * 128], ident[:])
                cp("act", oxT[:].re("p k t -> p (k t)"), pto[:])
                for g in range(2):
                    ps = nps()
                    for k in range(8):
                        mm(ps[:, :], oxT[:, k, :], wxo_b[:, k, g * 512:(g + 1) * 512], st=(k == 0), sp=(k == 7))
                    tt("dve", x2t[:, g * 512:(g + 1) * 512], xt[:, g * 512:(g + 1) * 512], ps[:, :], ALU.add)
                dma(S["x2s"][:, :], x2t[0:16, :])
            fw.barrier()
        with ExitStack() as s:
            wg_b = fw.sb(s, [128, 8, DFF], BF16, "wg_b")
            wu_b = fw.sb(s, [128, 8, DFF], BF16, "wu_b")
            wd_b = fw.sb(s, [128, 22, D], BF16, "wd_b")
            gfin = fw.sb(s, [128, D], F32, "gfin")
            dma(gfin[:], V(None, L["g_final"].a.partition_broadcast(128)))
            with ExitStack() as s2:
                stg = [fw.sb(s2, [128, DFF], F32, f"stg3{i}") for i in range(2)]
                self.load_w(wg_b, L["w_gate"], 8, DFF, stg, g2, 16)
                self.load_w(wu_b, L["w_up"], 8, DFF, stg, g2, 16)
                self.load_w(wd_b, L["w_down"], 22, D, stg)
                fw.barrier()
            xts = [fw.sb(s, [128, D], F32, f"x3_{i}") for i in range(4)]
            nb = self.norm_bufs(s, "3")
            hT4 = fw.sb(s, [128, 8, 512], BF16, "hT3")
            hT = fw.sb(s, [128, 8, 128], BF16, "hT3s")
            aT4 = fw.sb(s, [128, 22, 512], BF16, "aT4")
            aT = fw.sb(s, [128, 22, 128], BF16, "aT")
            sgs = [fw.sb(s, [128, 512], F32, f"sg{i}") for i in range(2)]
            sg = sgs[0]
            x3t = fw.sb(s, [128, D], F32, "x3t")

            def final_norm(dst_, rows_):
                ms("dve", nb["ss"][:], 0.0)
                act(nb["junk"][:], x3t[:], AF.Square, acc=nb["ss"][:])
                self.rstd(nb["rs"][:], nb["ss"][:], 1.0 / D)
                stt("dve", x3t[:], x3t[:], nb["rs"][:, 0:1], gfin[:], ALU.mult, ALU.mult)
                dma(dst_, x3t[0:rows_, :])

            for st_ in range(NT // 4):
                for j in range(4):
                    r0 = (st_ * 4 + j) * 128
                    self.norm_A(x2d[r0:r0 + 128, :], xts[j], nb)
                    pt = npt()
                    for k in range(8):
                        tr(pt[:, k * 128:(k + 1) * 128], nb["xn"][:, k * 128:(k + 1) * 128], ident[:])
                    cp("act", hT4[:, :, j * 128:(j + 1) * 128], pt[:].re("p (k t) -> p k t", k=8))
                for c in range(22):
                    pg, pu = nps(), nps()
                    for k in range(8):
                        mm(pg[:, :], wg_b[:, k, c * 128:(c + 1) * 128], hT4[:, k, :], st=(k == 0), sp=(k == 7))
                    for k in range(8):
                        mm(pu[:, :], wu_b[:, k, c * 128:(c + 1) * 128], hT4[:, k, :], st=(k == 0), sp=(k == 7))
                    sgc = sgs[c % 2]
                    act(sgc[:], pg[:, :], AF.Silu)
                    tt("dve", aT4[:, c, :], sgc[:], pu[:, :], ALU.mult)
                for j in range(4):
                    r0 = (st_ * 4 + j) * 128
                    for g in range(2):
                        ps = nps()
                        for c in range(22):
                            mm(ps[:, :], aT4[:, c, j * 128:(j + 1) * 128], wd_b[:, c, g * 512:(g + 1) * 512], st=(c == 0), sp=(c == 21))
                        tt("dve", x3t[:, g * 512:(g + 1) * 512], xts[j][:, g * 512:(g + 1) * 512], ps[:, :], ALU.add)
                    final_norm(y_p[r0:r0 + 128, :], 128)
            tiles = []
            if self.sample:
                tiles.append((self.S["x2s"][:, :], self.S["y_s"], 16))
            for t_, (src_, dst_, rows_) in enumerate(tiles):
                xt = xts[t_ % 2]
                self.norm_T(src_, xt, nb, hT, ident, npt, rows=rows_)
                for c0 in range(0, 22, 4):
                    n = min(4, 22 - c0)
                    pg = nps()
                    pu = nps()
                    for cc in range(n):
                        c = c0 + cc
                        for k in range(8):
                            mm(pg[:, cc * 128:(cc + 1) * 128], wg_b[:, k, c * 128:(c + 1) * 128], hT[:, k, :], st=(k == 0), sp=(k == 7))
                        for k in range(8):
                            mm(pu[:, cc * 128:(cc + 1) * 128], wu_b[:, k, c * 128:(c + 1) * 128], hT[:, k, :], st=(k == 0), sp=(k == 7))
                    act(sg[:, 0:n * 128], pg[:, 0:n * 128], AF.Silu)
                    tt("dve", aT[:, c0:c0 + n, :].re("p c t -> p (c t)"), sg[:, 0:n * 128], pu[:, 0:n * 128], ALU.mult)
                for g in range(2):
                    ps = nps()
                    for c in range(22):
                        mm(ps[:, :], aT[:, c, :], wd_b[:, c, g * 512:(g + 1) * 512], st=(c == 0), sp=(c == 21))
                    tt("dve", x3t[:, g * 512:(g + 1) * 512], xt[:, g * 512:(g + 1) * 512], ps[:, :], ALU.add)
                ms("dve", nb["ss"][:], 0.0)
                act(nb["junk"][:], x3t[:], AF.Square, acc=nb["ss"][:])
                self.rstd(nb["rs"][:], nb["ss"][:], 1.0 / D)
                stt("dve", x3t[:], x3t[:], nb["rs"][:, 0:1], gfin[:], ALU.mult, ALU.mult)
                dma(dst_, x3t[0:rows_, :])
            fw.barrier()


def sample_io(self):
    din, dout = self.din, self.dout
    S = {"xs": din("xs", [16, D]), "ptab": din("ptab", [4, 128], I32)}
    for n in ("pool_kc", "pool_vc", "pool_ks", "pool_vs"):
        S[n] = din(n, [5120, 16384])
    S["stk"] = din("stk", [4, 512, 128])
    S["stv"] = din("stv", [4, 512, 128])
    S["sconv"] = din("sconv", [4, 3, 512])
    S["sC"] = din("sC", [4, 4, 128, 128])
    S["sn"] = din("sn", [4, 4, 128])
    S["sm"] = din("sm", [16])
    S["cmk"] = din("cmk", [4, 256, D])
    S["cmv"] = din("cmv", [4, 256, D])
    S["y_s"] = dout("y_s", [16, D])
    for n in ("s_kc", "s_vc", "s_ks", "s_vs"):
        S[n] = dout(n, [16, 128])
    S["s_kw"] = dout("s_kw", [4, 512, 128])
    S["s_vw"] = dout("s_vw", [4, 512, 128])
    S["s_C"] = dout("s_C", [4, 4, 128, 128])
    S["s_n"] = dout("s_n", [4, 4, 128])
    S["s_m"] = dout("s_m", [16])
    S["s_conv"] = dout("s_conv", [4, 3, 512])
    S["kcS_d"] = self.dscr("kcS_d", [4, 2, 64, 1024], BF16)
    S["vcS_d"] = self.dscr("vcS_d", [4, 128, 8, 2, 64], BF16)
    S["x1s"] = self.dscr("x1s", [16, D])
    S["x2s"] = self.dscr("x2s", [16, D])
    self.S = S
    return S


def load_cmp(self, s, kv, cmp_in, nps):
    fw = self.fw
    mm, tt, cp, dma, dmas = self.mm, self.tt, self.cp, self.dma, self.dmas
    pe, w1, b1, w2 = cmp_in[kv]
    w1b = fw.sb(s, [64, 32, 256], BF16, "Sw1b" + kv)
    with ExitStack() as t:
        w1s = [fw.sb(t, [64, 8, 256], F32, f"Sw1s{kv}{i}") for i in range(2)]
        for jb in range(4):
            st = w1s[jb % 2]
            dma(st[:], V(None, w1.a.rearrange("(j d) n -> d j n", d=64)[:, jb * 8:(jb + 1) * 8, :]))
            cp("dve", w1b[:, jb * 8:(jb + 1) * 8, :], st[:])
        fw.barrier()
    peT = fw.sb(s, [64, 32], F32, "SpeT" + kv)
    dmas(peT[:], V(None, pe.a.rearrange("j d -> d j")))
    peTb = fw.sb(s, [64, 32], BF16, "SpeTb" + kv)
    cp("dve", peTb[:], peT[:])
    b1c = fw.sb(s, [128, 2], F32, "Sb1c" + kv)
    dmas(b1c[:], V(None, b1.a.rearrange("(c p) -> p c", p=128)))
    w2s = fw.sb(s, [128, 2, 64], F32, "Sw2s" + kv)
    dma(w2s[:], V(None, w2.a.rearrange("(c p) n -> p c n", p=128)))
    w2b = fw.sb(s, [128, 2, 64], BF16, "Sw2b" + kv)
    cp("dve", w2b[:], w2s[:])
    cst = fw.sb(s, [128, 2], F32, "Scst" + kv)
    for hc in range(2):
        ps = nps()
        for j in range(32):
            mm(ps[:, 0:1], w1b[:, j, hc * 128:(hc + 1) * 128], peTb[:, j:j + 1], st=(j == 0), sp=(j == 31))
        tt("dve", cst[:, hc:hc + 1], ps[:, 0:1], b1c[:, hc:hc + 1], ALU.add)
    return w1b, w2b, cst


def gather(self, dst, pool, idx, r0):
    self.fw.dma("pool", dst, pool, extra_reads=[idx.b],
                fn=lambda e: e.indirect_dma_start(out=dst.a, out_offset=None, in_=pool.a,
                                                  in_offset=bass.IndirectOffsetOnAxis(ap=idx.a, axis=0),
                                                  element_offset=r0 * 128))


def sample_s0(self, L):
    fw, S = self.fw, self.S
    mm, tr, act, cp, ms, dma = self.mm, self.tr, self.act, self.cp, self.ms, self.dma
    ident, nps, npt, cmp_in = L["ident"], L["nps"], L["npt"], L["cmp_in"]
    with ExitStack() as s:
        idx = [fw.sb(s, [128, 1], I32, f"S0idx{b}") for b in range(4)]
        for b in range(4):
            dma(idx[b][:], V(None, S["ptab"].a[b].rearrange("(p o) -> p o", o=1)))
        cw_ = {kv: load_cmp(self, s, kv, cmp_in, nps) for kv in "kv"}
        srcS = fw.sb(s, [64, 2, 128, 129], BF16, "srcS")
        ms("pool", srcS[:, :, :, 128:129], 0.0)
        gch = fw.sb(s, [128, 4096], F32, "gch")
        gbf = fw.sb(s, [128, 32, 128], BF16, "gbf")
        gT = fw.sb(s, [128, 2, 1024], BF16, "SgT")
        ko = fw.sb(s, [64, 1024], BF16, "Sko")
        vo = fw.sb(s, [128, 8, 64], BF16, "Svo")
        for b in range(4):
            for kv in "kv":
                w1b, w2b, cst = cw_[kv]
                pool = S["pool_" + kv + "c"]
                for i in range(4):
                    gather(self, gch[:], pool, idx[b][:, :], 32 * i)
                    cp("dve", gbf[:, 0:16, :].re("p r f -> p (r f)"), gch[:, 0:2048])
                    cp("act", gbf[:, 16:32, :].re("p r f -> p (r f)"), gch[:, 2048:4096])
                    for kvh in range(2):
                        for g8 in range(4):
                            pt = npt()
                            for r8 in range(8):
                                tr(pt[0:64, r8 * 128:(r8 + 1) * 128], gbf[:, 8 * g8 + r8, kvh * 64:(kvh + 1) * 64], ident[:])
                            cp("act" if g8 % 2 == 0 else "dve", srcS[:, kvh, 32 * i + 8 * g8:32 * i + 8 * g8 + 8, 0:128],
                               pt[0:64, :].re("p (r t) -> p r t", r=8))
                for kvh in range(2):
                    for hc in range(2):
                        wv = lambda j: w1b[:, j, hc * 128:(hc + 1) * 128]
                        for bank in range(2):
                            ps = nps()
                            o4 = ps[:, :].re("p (a t) -> p a t", a=4)
                            for j in range(32):
                                if j < 16:
                                    r0 = 64 * bank + j
                                    mm(o4, wv(j), srcS[:, kvh, r0:r0 + 49:16, 0:128], st=(j == 0), sp=False)
                                elif bank == 0:
                                    r0 = 16 + (j - 16)
                                    mm(o4, wv(j), srcS[:, kvh, r0:r0 + 49:16, 0:128], st=False, sp=(j == 31))
                                else:
                                    r0 = 80 + (j - 16)
                                    mm(o4[:, 0:3, :], wv(j), srcS[:, kvh, r0:r0 + 33:16, 0:128], st=False, sp=False)
                                    mm(ps[:, 384:512], wv(j), srcS[:, kvh, j - 16, 1:129], st=False, sp=(j == 31))
                            act(gT[:, hc, bank * 512:(bank + 1) * 512], ps[:, :], AF.Gelu_apprx_tanh, bias=cst[:, hc:hc + 1])
                    if kv == "k":
                        for bank in range(2):
                            ps = nps()
                            for hc in range(2):
                                mm(ps[0:64, :], w2b[:, hc, :], gT[:, hc, bank * 512:(bank + 1) * 512], st=(hc == 0), sp=(hc == 1))
                            cp("dve", ko[:, bank * 512:(bank + 1) * 512], ps[0:64, :])
                        dma(S["kcS_d"][b, kvh], ko[:])
                    else:
                        ps = nps()
                        for rb in range(8):
                            for hc in range(2):
                                mm(ps[:, rb * 64:(rb + 1) * 64], gT[:, hc, rb * 128:(rb + 1) * 128], w2b[:, hc, :],
                                   st=(hc == 0), sp=(hc == 1))
                        cp("dve", vo[:].re("p r d -> p (r d)"), ps[:, :])
                        dma(S["vcS_d"][b][:, :, kvh, :], vo[:])
        fw.barrier()


Builder.sample_io = sample_io

def sample_pass1(self, L):
    fw, S = self.fw, self.S
    mm, tr, act, ts, tt, stt, cp, ms, iota, dma, dmas = (self.mm, self.tr, self.act, self.ts, self.tt, self.stt,
                                                         self.cp, self.ms, self.iota, self.dma, self.dmas)
    win_b, wout_b, wqm_b, wkm_b = L["win_b"], L["wout_b"], L["wqm_b"], L["wkm_b"]
    ident, identf, nps, npt, psa = L["ident"], L["identf"], L["nps"], L["npt"], L["psa"]
    cw, cb, bgate, bif, tmpf = L["cw"], L["cb"], L["bgate"], L["bif"], L["tmpf"]
    put_row = self.put_row
    KSC = 128.0 ** -0.5
    with ExitStack() as s:
        xt = fw.sb(s, [128, D], F32, "xS")
        nb = self.norm_bufs(s, "S")
        hT = fw.sb(s, [128, 8, 128], BF16, "hTS")
        pkv = fw.sb(s, [128, 792], F32, "pkvS")
        gt = fw.sb(s, [128, 24], F32, "gtS")
        vnS = fw.sb(s, [16, 2, 2, 65], BF16, "vnS")
        qTs = fw.sb(s, [64, 8, 16], BF16, "qTs")
        mixin = fw.sb(s, [128, D], BF16, "mixinS")
        s2 = ExitStack()
        QS = fw.sb(s2, [68, 4, 2, 16], BF16, "QS")
        kcSb = fw.sb(s2, [68, 2, 8, 128], BF16, "kcSb")
        vcSb = fw.sb(s2, [128, 8, 2, 65], BF16, "vcSb")
        ksS = fw.sb(s2, [68, 2, 32, 128], BF16, "ksS")
        kwS = fw.sb(s2, [68, 2, 4, 128], BF16, "kwS")
        KnS = fw.sb(s2, [68, 2, 2, 16], BF16, "KnS")
        ms("pool", vcSb[:], 1.0)
        with ExitStack() as tmps:
            self.rowt = fw.sb(tmps, [1, 4096], F32, "rowtS")
            self.rowb = fw.sb(tmps, [1, 4096], BF16, "rowbS")
            sr = fw.sb(tmps, [1, 2, 4, 4], F32, "srS")
            for h in range(8):
                ms("pool", sr[0:1, h // 4, h % 4, :], 2.0 ** (-(h + 1)))
            qi = fw.sb(tmps, [1, 2, 4, 4], F32, "qiS")
            iota(qi[:].re("p k g q -> p (k g) q"), [[0, 8], [1, 4]], base=0, cm=0)
            rw = fw.sb(tmps, [1, 4, 2, 16], F32, "rwS")
            rwb = fw.sb(tmps, [1, 4, 2, 16], BF16, "rwbS")
            srv = sr[:].re("p k g q -> p k (g q)")
            for row in range(4):
                for i in range(4):
                    if row == 0:
                        ts("pool", rw[0:1, i], srv, -1.0, None, op0=ALU.mult)
                    elif row == 1:
                        cp("pool", rw[0:1, i], srv)
                    elif row == 2:
                        tt("pool", rw[0:1, i], srv, qi[:].re("p k g q -> p k (g q)"), ALU.mult)
                        ts("pool", rw[0:1, i], rw[0:1, i], -1.0, None, op0=ALU.mult)
                    else:
                        ts("pool", rw[0:1, i], srv, 32.0 * i, None, op0=ALU.mult)
                cp("pool", rwb[:], rw[:])
                dma(QS[64 + row:65 + row], rwb[:])
            for kvh in range(2):
                put_row(kcSb[64:65, kvh].re("p r t -> p (r t)"), [[0, 8], [-128, 128]], 16384, 1024)
                put_row(kcSb[65:66, kvh].re("p r t -> p (r t)"), [[16, 8], [0, 128]], 31, 1024)
                put_row(kcSb[66:67, kvh].re("p r t -> p (r t)"), None, 0, 1024, const=1.0)
                put_row(kcSb[67:68, kvh].re("p r t -> p (r t)"), None, 0, 1024, const=0.0)
                put_row(ksS[64:65, kvh].re("p r t -> p (r t)"), [[0, 32], [-128, 128]], 16384, 4096)
                put_row(ksS[65:66, kvh].re("p r t -> p (r t)"), [[1, 32], [0, 128]], 0, 4096)
                put_row(ksS[66:67, kvh].re("p r t -> p (r t)"), None, 0, 4096, const=1.0)
                put_row(ksS[67:68, kvh].re("p r t -> p (r t)"), None, 0, 4096, const=1.0)
                put_row(kwS[64:65, kvh].re("p r t -> p (r t)"), [[-128, 4], [0, 128]], 512, 512)
                put_row(kwS[65:66, kvh].re("p r t -> p (r t)"), [[0, 4], [1, 128]], 0, 512)
                put_row(kwS[66:67, kvh].re("p r t -> p (r t)"), None, 0, 512, const=1.0)
                put_row(kwS[67:68, kvh].re("p r t -> p (r t)"), None, 0, 512, const=0.0)
                for sw_ in range(2):
                    put_row(KnS[64:65, sw_, kvh], None, 0, 16, const=0.0)
                    put_row(KnS[65:66, sw_, kvh], [[0, 4], [1, 4]], 0, 16)
                    put_row(KnS[66:67, sw_, kvh], None, 0, 16, const=1.0)
                    put_row(KnS[67:68, sw_, kvh], None, 0, 16, const=0.0)
            fw.barrier()
        maskC7 = fw.sb(s2, [128, 1], F32, "maskC7")
        iota(tmpf[:, 0:1], [[0, 1]], base=0, cm=1)
        ts("pool", maskC7[:], tmpf[:, 0:1], 127.0, None, op0=ALU.is_lt)
        winm0 = fw.sb(s2, [128, 4], BF16, "winm0")
        iota(tmpf[:, 0:4], [[-1, 4]], base=0, cm=1)
        ts("pool", winm0[:], tmpf[:, 0:4], 0.0, None, op0=ALU.is_gt)
        newm = fw.sb(s2, [16, 4, 4], BF16, "newm")
        for b in range(4):
            iota(tmpf[0:16, 0:4], [[-1, 4]], base=-4 * b, cm=1)
            ts("pool", tmpf[0:16, 4:8], tmpf[0:16, 0:4], 0.0, None, op0=ALU.is_le)
            iota(tmpf[0:16, 8:12], [[0, 4]], base=-4 * b, cm=1)
            ts("pool", tmpf[0:16, 8:12], tmpf[0:16, 8:12], 0.0, None, op0=ALU.is_ge)
            tt("pool", newm[:, b, :], tmpf[0:16, 4:8], tmpf[0:16, 8:12], ALU.mult)
        mimpS = fw.sb(s2, [128, 8, 256], BF16, "mimpS")
        mtmp = fw.sb(s2, [128, 3, 256], F32, "mtmp")
        for rb in range(8):
            iota(mtmp[:, 0, :], [[-4, 256]], base=rb - 1, cm=8)
            stt("dve", mtmp[:, 1, :], mtmp[:, 0, :], -1.0, mtmp[:, 0, :], ALU.mult, ALU.max)
            ts("pool", mtmp[:, 0, :], mtmp[:, 1, :], 2.0, 0.5, op0=ALU.is_le, op1=ALU.mult)
            ts("pool", mtmp[:, 2, :], mtmp[:, 1, :], 1.0, 0.5, op0=ALU.is_le, op1=ALU.mult)
            tt("pool", mimpS[:, rb, :], mtmp[:, 0, :], mtmp[:, 2, :], ALU.add)
        idx = [fw.sb(s2, [128, 1], I32, f"S1idx{b}") for b in range(4)]
        for b in range(4):
            dma(idx[b][:], V(None, S["ptab"].a[b].rearrange("(p o) -> p o", o=1)))

        self.norm_T(S["xs"], xt, nb, hT, ident, npt, rows=16)
        psA, psB = nps(), nps()
        for k in range(8):
            mm(psA[:, 0:512], hT[:, k, :], win_b[:, k, 512:1024], st=(k == 0), sp=(k == 7))
        for k in range(8):
            mm(psB[:, 0:280], hT[:, k, :], win_b[:, k, 1024:1304], st=(k == 0), sp=(k == 7))
        cp("dve", pkv[:, 0:512], psA[:, 0:512])
        cp("act", pkv[:, 512:792], psB[:, 0:280])
        for i_, n_ in enumerate(("s_kc", "s_vc", "s_ks", "s_vs")):
            dma(S[n_], pkv[0:16, i_ * 128:(i_ + 1) * 128])
        tt("dve", gt[:], pkv[:, 768:792], bgate[:], ALU.add)
        self.sigm(gt[:], gt[:])
        ms("pool", vnS[:], 1.0)
        cp("dve", vnS[:, 0, :, 0:64], pkv[0:16, 384:512].re("p (h d) -> p h d", h=2))
        cp("dve", vnS[:, 1, :, 0:64], pkv[0:16, 640:768].re("p (h d) -> p h d", h=2))
        psQ = nps()
        for h in range(8):
            for k in range(8):
                mm(psQ[0:64, h * 16:(h + 1) * 16], win_b[:, k, 64 * h:64 * h + 64], hT[:, k, 0:16], st=(k == 0), sp=(k == 7))
        act(qTs[:].re("p h t -> p (h t)"), psQ[0:64, 0:128], AF.Copy, scale=0.125)
        psK = nps()
        for gi, c0 in enumerate((768, 832, 1024, 1088)):
            for k in range(8):
                mm(psK[0:64, gi * 16:(gi + 1) * 16], win_b[:, k, c0:c0 + 64], hT[:, k, 0:16], st=(k == 0), sp=(k == 7))
        cp("dve", KnS[0:64].re("p a k t -> p (a k t)"), psK[0:64, 0:64])

        gk = fw.sb(s2, [128, 4096], F32, "gk")
        gv = fw.sb(s2, [128, 4096], F32, "gv")
        kb = fw.sb(s2, [128, 32, 128], BF16, "kbS")
        vbp = fw.sb(s2, [128, 32, 2, 65], BF16, "vbp")
        ms("pool", vbp[:], 1.0)
        vwS = fw.sb(s2, [128, 4, 2, 65], BF16, "vwS")
        ms("pool", vwS[:], 1.0)
        ptc = fw.sb(s2, [128, 8, 16], BF16, "ptc")
        ptsb = fw.sb(s2, [128, 32, 16], BF16, "ptsb")
        ptw = fw.sb(s2, [128, 4, 16], BF16, "ptw")
        ptn = fw.sb(s2, [16, 16], BF16, "ptn")
        maskEO = [fw.sb(s2, [128, 2, 4, 4], BF16, f"maskEO{k}") for k in range(2)]
        obr = [[fw.sb(s2, [4, 4, 65], F32, f"obrS{k}{i}") for i in range(3)] for k in range(2)]
        imp4 = fw.sb(s2, [4, 4, 256], F32, "imp4S")
        imp = fw.sb(s2, [4, 256], F32, "impS")
        imp2 = fw.sb(s2, [4, 256], F32, "imp2S")
        mx1 = fw.sb(s2, [4, 8], F32, "mx1S")
        mx2 = fw.sb(s2, [4, 8], F32, "mx2S")
        sel01 = fw.sb(s2, [4, 256], F32, "sel01")
        selT = fw.sb(s2, [128, 8], F32, "selTS")
        gtb = fw.sb(s2, [4, 24], F32, "gtb")
        rd = fw.sb(s2, [4, 3, 4], F32, "rdS")
        sc3 = fw.sb(s2, [4, 3, 4], F32, "sc3S")
        onsa = fw.sb(s2, [4, 4, 64], F32, "onsaS")
        otmp = fw.sb(s2, [4, 4, 64], F32, "otmpS")
        ss4 = fw.sb(s2, [4, 4], F32, "ss4S")
        rs4 = fw.sb(s2, [4, 4], F32, "rs4S")
        onb = fw.sb(s2, [4, 512], BF16, "onbS")
        ms("pool", mixin[:], 0.0)
        for b in range(4):
            cp("dve", QS[0:64].re("p i k (g q) -> p i (k g) q", g=4),
               qTs[:, :, 4 * b:4 * b + 4].un(1).bc([64, 4, 8, 4]))
            dma(kcSb[0:64].re("p k r t -> p k (r t)"), V(S["kcS_d"], S["kcS_d"][b].a.rearrange("k d n -> d k n")))
            dma(vcSb[:].re("p r k d -> p (r k) d")[:, :, 0:64], V(S["vcS_d"], S["vcS_d"][b].a.rearrange("p r k d -> p (r k) d")))
            dma(gtb[:], gt[4 * b:4 * b + 4, :])
            for kvh in range(2):
                Sc = nps()
                for rb in range(8):
                    mm(Sc[:, rb * 16:(rb + 1) * 16], kcSb[:, kvh, rb, :], QS[:, 0, kvh, :])
                act(ptc[:].re("p r c -> p (r c)"), Sc[:, 0:128], AF.Exp)
                ts("dve", ptc[:, 7, :], ptc[:, 7, :], maskC7[:, 0:1], None, op0=ALU.mult)
                accC = psa[0]
                impP = [nps(), nps()]
                for rb in range(8):
                    for g in range(4):
                        mm(accC[0:4, g * 65:(g + 1) * 65], ptc[:, rb, 4 * g:4 * g + 4], vcSb[:, rb, kvh, :],
                           st=(rb == 0), sp=(rb == 7))
                        mm(impP[g // 2][0:4, (g % 2) * 256:(g % 2) * 256 + 256], ptc[:, rb, 4 * g:4 * g + 4],
                           mimpS[:, rb, :], st=(rb == 0), sp=(rb == 7))
                cp("dve", obr[kvh][0][:].re("p g d -> p (g d)"), accC[0:4, 0:260])
                for hh in range(2):
                    cp("act", imp4[:, 2 * hh:2 * hh + 2, :].re("p g j -> p (g j)"), impP[hh][0:4, 0:512])
                ts("dve", rd[:, 0, :], obr[kvh][0][:, :, 64], 1e-30, None, op0=ALU.max)
                self.recip(rd[:, 0, :], rd[:, 0, :])
                ts("dve", imp[:], imp4[:, 0, :], rd[:, 0, 0:1], None, op0=ALU.mult)
                for g in range(1, 4):
                    stt("dve", imp[:], imp4[:, g, :], rd[:, 0, g:g + 1], imp[:], ALU.mult, ALU.add)
                ms("dve", imp[:, 0:1], 3e9)
                ms("dve", imp[:, 255:256], 1e9)
                fw.op("dve", lambda e: e.max(out=mx1[:].a, in_=imp[:].a), reads=[imp], writes=[mx1])
                fw.op("dve", lambda e: e.match_replace(out=imp2[:].a, in_to_replace=mx1[:].a, in_values=imp[:].a,
                                                       imm_value=-1e30), reads=[imp, mx1], writes=[imp2])
                fw.op("dve", lambda e: e.max(out=mx2[:].a, in_=imp2[:].a), reads=[imp2], writes=[mx2])
                ts("dve", sel01[:], imp[:], mx2[:, 6:7], None, op0=ALU.is_ge)
                psT = nps()
                tr(psT[:, 0:4], sel01[0:4, 0:256:2], identf[0:4, 0:4])
                tr(psT[:, 4:8], sel01[0:4, 1:256:2], identf[0:4, 0:4])
                cp("dve", selT[:], psT[:, 0:8])
                cp("dve", maskEO[kvh][:], selT[:].re("p (e q) -> p e q", e=2).un(2).bc([128, 2, 4, 4]))
            accS = [psa[0], psa[1]]
            for i in range(4):
                gather(self, gk[:], S["pool_ks"], idx[b][:, :], 32 * i)
                gather(self, gv[:], S["pool_vs"], idx[b][:, :], 32 * i)
                cp("dve", kb[:, 0:16, :].re("p r f -> p (r f)"), gk[:, 0:2048])
                cp("act", kb[:, 16:32, :].re("p r f -> p (r f)"), gk[:, 2048:4096])
                cp("dve", vbp[:, 0:16, :, 0:64], gv[:, 0:2048].re("p (r h d) -> p r h d", r=16, h=2))
                cp("act", vbp[:, 16:32, :, 0:64], gv[:, 2048:4096].re("p (r h d) -> p r h d", r=16, h=2))
                for kvh in range(2):
                    for g8 in range(4):
                        pt = npt()
                        for r8 in range(8):
                            tr(pt[0:64, r8 * 128:(r8 + 1) * 128], kb[:, 8 * g8 + r8, kvh * 64:(kvh + 1) * 64], ident[:])
                        cp("act" if g8 % 2 == 0 else "dve", ksS[0:64, kvh, 8 * g8:8 * g8 + 8, :],
                           pt[0:64, :].re("p (r t) -> p r t", r=8))
                for kvh in range(2):
                    Ss = nps()
                    for r_ in range(32):
                        mm(Ss[:, r_ * 16:(r_ + 1) * 16], ksS[:, kvh, r_, :], QS[:, i, kvh, :])
                    act(ptsb[:].re("p r c -> p (r c)"), Ss[:, :], AF.Exp)
                    tt("dve", ptsb[:].re("p r (g q) -> p r g q", g=4), ptsb[:].re("p r (g q) -> p r g q", g=4),
                       maskEO[kvh][:, i // 2].un(1).bc([128, 32, 4, 4]), ALU.mult)
                    for r_ in range(32):
                        for g in range(4):
                            mm(accS[kvh][0:4, g * 65:(g + 1) * 65], ptsb[:, r_, 4 * g:4 * g + 4], vbp[:, r_, kvh, :],
                               st=(i == 0 and r_ == 0), sp=False)
            for kvh in range(2):
                Sn = nps()
                mm(Sn[0:16, 0:16], KnS[:, 0, kvh, :], QS[:, 0, kvh, :])
                act(ptn[:], Sn[0:16, 0:16], AF.Exp)
                tt("dve", ptn[:].re("p (g q) -> p g q", g=4), ptn[:].re("p (g q) -> p g q", g=4),
                   newm[:, b, :].un(1).bc([16, 4, 4]), ALU.mult)
                for g in range(4):
                    mm(accS[kvh][0:4, g * 65:(g + 1) * 65], ptn[:, 4 * g:4 * g + 4], vnS[:, 0, kvh, :], st=False, sp=True)
                cp("dve", obr[kvh][1][:].re("p g d -> p (g d)"), accS[kvh][0:4, 0:260])
            dma(gk[:, 0:512].re("p (a f) -> p a f", a=4), V(None, S["stk"].a[b].rearrange("(a p) f -> p a f", p=128)))
            dma(gv[:, 0:512].re("p (a f) -> p a f", a=4), V(None, S["stv"].a[b].rearrange("(a p) f -> p a f", p=128)))
            cp("dve", kb[:, 0:4, :].re("p r f -> p (r f)"), gk[:, 0:512])
            cp("pool", vwS[:, :, :, 0:64], gv[:, 0:512].re("p (a h d) -> p a h d", a=4, h=2))
            pt = npt()
            for kvh in range(2):
                for a in range(4):
                    tr(pt[0:64, (kvh * 4 + a) * 128:(kvh * 4 + a + 1) * 128], kb[:, a, kvh * 64:(kvh + 1) * 64], ident[:])
            cp("act", kwS[0:64].re("p k a t -> p (k a t)"), pt[0:64, :])
            accW = [psa[0], psa[1]]
            for kvh in range(2):
                Sw = nps()
                for a in range(4):
                    mm(Sw[:, a * 16:(a + 1) * 16], kwS[:, kvh, a, :], QS[:, 0, kvh, :])
                act(ptw[:].re("p a c -> p (a c)"), Sw[:, 0:64], AF.Exp)
                tt("dve", ptw[:, 0, :].re("p (g q) -> p g q", g=4), ptw[:, 0, :].re("p (g q) -> p g q", g=4),
                   winm0[:].un(1).bc([128, 4, 4]), ALU.mult)
                for a in range(4):
                    for g in range(4):
                        mm(accW[kvh][0:4, g * 65:(g + 1) * 65], ptw[:, a, 4 * g:4 * g + 4], vwS[:, a, kvh, :],
                           st=(a == 0), sp=False)
                Sn = nps()
                mm(Sn[0:16, 0:16], KnS[:, 1, kvh, :], QS[:, 0, kvh, :])
                act(ptn[:], Sn[0:16, 0:16], AF.Exp)
                tt("dve", ptn[:].re("p (g q) -> p g q", g=4), ptn[:].re("p (g q) -> p g q", g=4),
                   newm[:, b, :].un(1).bc([16, 4, 4]), ALU.mult)
                for g in range(4):
                    mm(accW[kvh][0:4, g * 65:(g + 1) * 65], ptn[:, 4 * g:4 * g + 4], vnS[:, 1, kvh, :], st=False, sp=True)
                cp("dve", obr[kvh][2][:].re("p g d -> p (g d)"), accW[kvh][0:4, 0:260])
            for nm_, src_, c0 in (("s_kw", "stk", 512), ("s_vw", "stv", 640)):
                dma(V(None, S[nm_].a[b, 0:508, :]), V(None, S[src_].a[b, 4:512, :]))
                dma(V(None, S[nm_].a[b, 508:512, :]), pkv[4 * b:4 * b + 4, c0:c0 + 128])
            for kvh in range(2):
                for br in range(1, 3):
                    ts("dve", rd[:, br, :], obr[kvh][br][:, :, 64], 1e-30, None, op0=ALU.max)
                    self.recip(rd[:, br, :], rd[:, br, :])
                ts("dve", rd[:, 0, :], obr[kvh][0][:, :, 64], 1e-30, None, op0=ALU.max)
                self.recip(rd[:, 0, :], rd[:, 0, :])
                gvw = gtb[:, 12 * kvh:12 * kvh + 12].re("p (g b) -> p b g", b=3)
                tt("dve", sc3[:], rd[:], gvw, ALU.mult)
                tt("dve", onsa[:], obr[kvh][0][:, :, 0:64], sc3[:, 0, :].un(2).bc([4, 4, 64]), ALU.mult)
                for br in (1, 2):
                    tt("dve", otmp[:], obr[kvh][br][:, :, 0:64], sc3[:, br, :].un(2).bc([4, 4, 64]), ALU.mult)
                    tt("dve", onsa[:], onsa[:], otmp[:], ALU.add)
                tt("dve", otmp[:], onsa[:], onsa[:], ALU.mult)
                self.rsum(ss4[:], otmp[:])
                self.rstd(rs4[:], ss4[:], 1.0 / 64)
                tt("dve", onb[:, 256 * kvh:256 * kvh + 256].re("p (g d) -> p g d", g=4), onsa[:],
                   rs4[:].un(2).bc([4, 4, 64]), ALU.mult)
            dma(mixin[4 * b:4 * b + 4, 0:512], onb[:])
        fw.barrier()
        s2.close()
        self.sample_mlstm(s, L, hT, mixin)
        mT = fw.sb(s, [128, 8, 128], BF16, "mTS")
        x1t = fw.sb(s, [128, D], F32, "x1tS")
        ptm = npt()
        for k in range(8):
            tr(ptm[:, k * 128:(k + 1) * 128], mixin[:, k * 128:(k + 1) * 128], ident[:])
        cp("act", mT[:].re("p k t -> p (k t)"), ptm[:])
        for g in range(2):
            ps = nps()
            for k in range(8):
                mm(ps[:, :], mT[:, k, :], wout_b[:, k, g * 512:(g + 1) * 512], st=(k == 0), sp=(k == 7))
            tt("dve", x1t[:, g * 512:(g + 1) * 512], xt[:, g * 512:(g + 1) * 512], ps[:, :], ALU.add)
        dma(S["x1s"][:, :], x1t[0:16, :])
        fw.barrier()


Builder.sample_pass1 = sample_pass1


def sample_mlstm(self, s, L, hT, mixin):
    fw, S = self.fw, self.S
    mm, tr, act, ts, tt, stt, cp, ms, iota, dma, dmas = (self.mm, self.tr, self.act, self.ts, self.tt, self.stt,
                                                         self.cp, self.ms, self.iota, self.dma, self.dmas)
    win_b, wqm_b, wkm_b = L["win_b"], L["wqm_b"], L["wkm_b"]
    identf, nps, cw, cb, bif, tmpf = L["identf"], L["nps"], L["cw"], L["cb"], L["bif"], L["tmpf"]
    KSC = 128.0 ** -0.5
    E = fw.sb(s, [4, 128], F32, "E4")
    iota(E[:], [[1, 128]], base=0, cm=-4)
    Eb = fw.sb(s, [4, 128], F32, "E4b")
    ts("pool", Eb[:], E[:], 0.0, None, op0=ALU.is_ge)
    ts("pool", E[:], E[:], 3.0, None, op0=ALU.is_le)
    tt("pool", E[:], E[:], Eb[:], ALU.mult)
    triS = fw.sb(s, [128, 128], F32, "triS")
    ps = nps()
    mm(ps[:, 0:128], E[:], E[:])
    tt("dve", triS[:], ps[:, 0:128], L["tri_le"][:], ALU.mult)
    bdS = fw.sb(s, [16, 16], BF16, "bdS")
    cp("dve", bdS[:], triS[0:16, 0:16])
    cselS = fw.sb(s, [128, 4, 128], F32, "cselS")
    d4 = fw.sb(s, [4, 4, 128], F32, "d4")
    cp("dve", d4[:], identf[0:4, 0:4].un(2).bc([4, 4, 128]))
    ps = nps()
    for b in range(4):
        mm(ps[:, b * 128:(b + 1) * 128], E[:], d4[:, b, :])
    cp("dve", cselS[:].re("p b m -> p (b m)"), ps[:, :])
    psV, psO, psG = nps(), nps(), nps()
    for k in range(8):
        mm(psV[:, 0:512], hT[:, k, :], win_b[:, k, 1816:2328], st=(k == 0), sp=(k == 7))
    for k in range(8):
        mm(psO[:, 0:512], hT[:, k, :], win_b[:, k, 2328:2840], st=(k == 0), sp=(k == 7))
    for k in range(8):
        mm(psG[:, 0:8], hT[:, k, :], win_b[:, k, 2840:2848], st=(k == 0), sp=(k == 7))
    gif = fw.sb(s, [128, 8], F32, "gifS")
    l1 = fw.sb(s, [128, 4], F32, "l1S")
    sigo = fw.sb(s, [128, 512], F32, "sigoS")
    tt("dve", gif[:], psG[:, 0:8], bif[:], ALU.add)
    act(l1[:], gif[:, 4:8], AF.Exp, scale=-1.0)
    act(l1[:], l1[:], AF.Ln, bias=1.0)
    self.sigm(sigo[:], psO[:, 0:512])
    psC = nps()
    mm(psC[:, 0:4], triS[:], l1[:])
    for b in range(4):
        mm(psC[:, 4 + 4 * b:8 + 4 * b], cselS[:, b, :], l1[:])
    gsb = fw.sb(s, [128, 20], F32, "gsbS")
    cp("dve", gsb[:], psC[:, 0:20])
    wl = fw.sb(s, [128, 4], F32, "wlS")
    ul = fw.sb(s, [128, 4], F32, "ulS")
    tmp4 = fw.sb(s, [128, 4], F32, "tmp4S")
    own = fw.sb(s, [128, 4], F32, "ownS")
    dec = fw.sb(s, [128, 4], F32, "decS")
    ebt = fw.sb(s, [128, 16], F32, "ebtS")
    act(wl[:], gsb[:, 0:4], AF.Exp, scale=-1.0)
    tt("dve", tmp4[:], gif[:, 0:4], gsb[:, 0:4], ALU.add)
    act(ul[:], tmp4[:], AF.Exp)
    act(ebt[:], gsb[:, 4:20], AF.Exp, scale=-1.0)
    ts("dve", own[:], gsb[:, 4:8], cselS[:, 0, 0:1], None, op0=ALU.mult)
    for b in range(1, 4):
        stt("dve", own[:], gsb[:, 4 + 4 * b:8 + 4 * b], cselS[:, b, 0:1], own[:], ALU.mult, ALU.add)
    tt("dve", dec[:], tmp4[:], own[:], ALU.subtract)
    vmu = fw.sb(s, [128, 4, 129], BF16, "vmuS")
    tt("dve", vmu[:, :, 0:128], psV[:, 0:512].re("p (h e) -> p h e", h=4), ul[:].un(2).bc([128, 4, 128]), ALU.mult)
    cp("dve", vmu[:, :, 128], ul[:])
    psT = nps()
    tr(psT[0:4, 0:128], dec[:], identf[:])
    mm(psT[0:4, 128:132], l1[:], cselS[:, :, 0])
    tsb = fw.sb(s, [4, 132], F32, "tsbS")
    cp("dve", tsb[:], psT[0:4, 0:132])
    Dm = fw.sb(s, [4, 4], F32, "DmS")
    self.rmax(Dm[:], tsb[:, 0:16].re("p (b i) -> p b i", b=4))
    R = fw.sb(s, [4, 4], F32, "RS")
    dmas(R[:], V(None, S["sm"].a.rearrange("(b h) -> h b", h=4)))
    tt("dve", R[:], R[:], tsb[:, 128:132], ALU.subtract)
    tt("dve", R[:], R[:], Dm[:], ALU.max)
    dmas(V(None, S["s_m"].a.rearrange("(b h) -> h b", h=4)), R[:])
    xcv = fw.sb(s, [128, 4, 4, 7], F32, "xcvS")
    for b in range(4):
        for ch in range(4):
            dmas(xcv[:, ch, b, 0:3], V(None, S["sconv"].a[b, :, ch * 128:(ch + 1) * 128].rearrange("j p -> p j")))
    psX = nps()
    for ch in range(4):
        for k in range(8):
            mm(psX[:, ch * 16:(ch + 1) * 16], win_b[:, k, 1304 + ch * 128:1432 + ch * 128], hT[:, k, 0:16],
               st=(k == 0), sp=(k == 7))
    cp("act", xcv[:, :, :, 3:7], psX[:, 0:64].re("p (c b i) -> p c b i", c=4, b=4))
    cacc = fw.sb(s, [128, 4, 16], F32, "caccS")
    for ch in range(4):
        cv = cacc[:, ch, :].re("p (b i) -> p b i", b=4)
        ts("dve", cv, xcv[:, ch, :, 0:4], cw[:, ch, 0:1], cb[:, ch:ch + 1], op0=ALU.mult, op1=ALU.add)
        for j in range(1, 4):
            stt("dve", cv, xcv[:, ch, :, j:j + 4], cw[:, ch, j:j + 1], cv, ALU.mult, ALU.add)
    xc = fw.sb(s, [128, 4, 16], BF16, "xcS")
    sgc = fw.sb(s, [128, 4, 16], F32, "sgcS")
    self.sigm(sgc[:], cacc[:])
    tt("dve", xc[:], cacc[:], sgc[:], ALU.mult)
    for b in range(4):
        for j in range(3):
            dmas(V(None, S["s_conv"].a[b, j].rearrange("(c p) -> p c", p=128)), xcv[:, :, b, 4 + j])
    qmT = fw.sb(s, [128, 4, 16], BF16, "qmTS")
    kmT = fw.sb(s, [128, 4, 16], BF16, "kmTS")
    qmS = [fw.sb(s, [128, 4, 16], BF16, f"qmSS{b}") for b in range(4)]
    kmS = [fw.sb(s, [16, 4, 128], BF16, f"kmSS{b}") for b in range(4)]
    psq = nps()
    for h in range(4):
        mm(psq[:, h * 16:(h + 1) * 16], wqm_b[:, h, :], xc[:, h, :])
    cp("act", qmT[:].re("p h t -> p (h t)"), psq[:, 0:64])
    for b in range(4):
        ms("pool", qmS[b][:], 0.0)
        cp("dve", qmS[b][:, :, 4 * b:4 * b + 4], psq[:, 0:64].re("p (h t) -> p h t", h=4)[:, :, 4 * b:4 * b + 4])
    psk = nps()
    for h in range(4):
        mm(psk[:, h * 16:(h + 1) * 16], wkm_b[:, h, :], xc[:, h, :])
    act(kmT[:].re("p h t -> p (h t)"), psk[:, 0:64], AF.Copy, scale=KSC)
    pskt = nps()
    for h in range(4):
        mm(pskt[0:16, h * 128:(h + 1) * 128], xc[:, h, :], wkm_b[:, h, :])
    for b in range(4):
        ts("dve", kmS[b][:].re("p h t -> p (h t)"), pskt[0:16, :], cselS[0:16, b, 0:1], KSC, op0=ALU.mult, op1=ALU.mult)
    psqk = nps()
    for h in range(4):
        mm(psqk[0:16, h * 16:(h + 1) * 16], kmT[:, h, :], qmT[:, h, :])
    mqk = fw.sb(s, [16, 4, 16], BF16, "mqkS")
    tt("dve", mqk[:], psqk[0:16, 0:64].re("p (h t) -> p h t", h=4), bdS[:].un(1).bc([16, 4, 16]), ALU.mult)
    em0 = fw.sb(s, [128, 16], F32, "em0")
    dma(em0[:], V(None, S["sm"].a.partition_broadcast(128)))
    act(em0[:], em0[:], AF.Exp)
    Sf = [fw.sb(s, [128, 4, 129], F32, f"SfS{b}") for b in range(4)]
    Sb0 = [fw.sb(s, [128, 4, 129], BF16, f"Sb0S{b}") for b in range(4)]
    cst_ = [fw.sb(s, [128, 128], F32, f"c0st{i}") for i in range(2)]
    dS = fw.sb(s, [128, 4, 129], F32, "dSS")
    for b in range(4):
        for h in range(4):
            st = cst_[h % 2]
            dma(st[:], V(None, S["sC"].a[b, h]))
            ps = nps()
            tr(ps[:, 0:128], st[:], identf[:])
            cp("dve", Sf[b][:, h, 0:128], ps[:, 0:128])
        dmas(Sf[b][:, :, 128], V(None, S["sn"].a[b].rearrange("h d -> d h")))
        tt("dve", Sf[b][:], Sf[b][:], em0[:, 4 * b:4 * b + 4].un(2).bc([128, 4, 129]), ALU.mult)
        cp("pool", Sb0[b][:], Sf[b][:])
        pd = [nps(), nps()]
        for h in range(4):
            mm(pd[h // 2][:, (h % 2) * 129:(h % 2) * 129 + 129], kmS[b][:, h, :], vmu[0:16, h, :])
        tt("dve", Sf[b][:], Sf[b][:], ebt[:, 4 * b:4 * b + 4].un(2).bc([128, 4, 129]), ALU.mult)
        for hh in range(2):
            tt("dve", dS[:, 2 * hh:2 * hh + 2, :], pd[hh][:, 0:258].re("p (h e) -> p h e", h=2),
               ebt[:, 4 * b + 2 * hh:4 * b + 2 * hh + 2].un(2).bc([128, 2, 129]), ALU.mult)
        tt("dve", Sf[b][:], Sf[b][:], dS[:], ALU.add)
    pa = [nps(), nps()]
    for h in range(4):
        o_ = pa[h // 2][0:16, (h % 2) * 129:(h % 2) * 129 + 129]
        mm(o_, mqk[:, h, :], vmu[0:16, h, :], st=True, sp=False)
        for b in range(4):
            mm(o_, qmS[b][:, h, :], Sb0[b][:, h, :], st=False, sp=(b == 3))
    dn = fw.sb(s, [16, 4], F32, "dnS")
    t4 = fw.sb(s, [16, 4], F32, "t4S")
    hout = fw.sb(s, [16, 4, 128], F32, "houtS")
    hsq = fw.sb(s, [16, 4, 128], F32, "hsqS")
    ss4 = fw.sb(s, [16, 4], F32, "ss4m")
    rs4 = fw.sb(s, [16, 4], F32, "rs4m")
    for hh in range(2):
        av = pa[hh][0:16, 0:258].re("p (h e) -> p h e", h=2)
        tt("dve", dn[:, 2 * hh:2 * hh + 2], av[:, :, 128], wl[0:16, 2 * hh:2 * hh + 2], ALU.mult)
    stt("dve", t4[:], dn[:], -1.0, dn[:], ALU.mult, ALU.max)
    ts("dve", t4[:], t4[:], 1.0, None, op0=ALU.max)
    self.recip(t4[:], t4[:])
    tt("dve", t4[:], t4[:], wl[0:16, :], ALU.mult)
    for hh in range(2):
        av = pa[hh][0:16, 0:258].re("p (h e) -> p h e", h=2)
        tt("dve", hout[:, 2 * hh:2 * hh + 2, :], av[:, :, 0:128], t4[:, 2 * hh:2 * hh + 2].un(2).bc([16, 2, 128]), ALU.mult)
    tt("dve", hsq[:], hout[:], hout[:], ALU.mult)
    self.rsum(ss4[:], hsq[:])
    self.rstd(rs4[:], ss4[:], 1.0 / 128)
    tt("dve", hout[:], hout[:], rs4[:].un(2).bc([16, 4, 128]), ALU.mult)
    tt("dve", mixin[0:16, 512:1024], hout[:].re("p h e -> p (h e)"), sigo[0:16, :], ALU.mult)
    Rd = fw.sb(s, [4, 4, 4], F32, "RdS")
    tt("dve", Rd[:], R[:].un(2).bc([4, 4, 4]), identf[0:4, 0:4].un(1).bc([4, 4, 4]), ALU.mult)
    ones4 = fw.sb(s, [4, 128], F32, "ones4S")
    ms("pool", ones4[:], 1.0)
    ps = nps()
    mm(ps[:, 0:16], ones4[:], Rd[:].re("p b h -> p (b h)"))
    esc = fw.sb(s, [128, 16], F32, "escS")
    act(esc[:], ps[:, 0:16], AF.Exp, scale=-1.0)
    for b in range(4):
        tt("dve", Sf[b][:], Sf[b][:], esc[:, 4 * b:4 * b + 4].un(2).bc([128, 4, 129]), ALU.mult)
        dmas(V(None, S["s_n"].a[b].rearrange("h d -> d h")), Sf[b][:, :, 128])
        for h in range(4):
            ps = nps()
            tr(ps[:, 0:128], Sf[b][:, h, 0:128], identf[:])
            st = cst_[h % 2]
            cp("dve", st[:], ps[:, 0:128])
            dma(V(None, S["s_C"].a[b, h]), st[:])


Builder.sample_mlstm = sample_mlstm
Builder.sample_s0 = sample_s0

W_NAMES = ["w_in", "g_mix", "b_gate", "cmp_pe_k", "cmp_w1_k", "cmp_b1_k", "cmp_w2_k", "cmp_pe_v", "cmp_w1_v",
           "cmp_b1_v", "cmp_w2_v", "g_head_nsa", "conv_w", "conv_b", "w_qm", "w_km", "b_i", "b_f", "g_head_m",
           "w_out", "g_xa", "g_mem", "w_xq", "w_xk", "w_xv", "w_xo", "g_ffn", "w_gate", "w_up", "w_down", "g_final"]


def build_program(NT=32, sample=True, debug=False):
    nc = bass.Bass("TRN2", target_bir_lowering=False)
    b = Builder(nc, NT=NT, sample=sample, debug=debug)
    b.build()
    return nc, b


def core_inputs(inp, c, b, NT=32):
    f = lambda a: np.ascontiguousarray(a, dtype=np.float32)
    T = NT * 128
    m = {"xp": f(inp["x_prompt"][c, :T]), "memp": f(inp["mem_prompt"][c])}
    for n in W_NAMES:
        a = np.asarray(inp[n])
        if n != "g_final":
            a = a[0]
        m[n] = f(a).reshape(b.io[n].shape)
    if b.sample:
        sl = slice(4 * c, 4 * c + 4)
        m["xs"] = f(inp["x_sample"][sl]).reshape(16, D)
        for n, k in (("pool_kc", "cache_k_cmp"), ("pool_vc", "cache_v_cmp"), ("pool_ks", "cache_k_slc"), ("pool_vs", "cache_v_slc")):
            m[n] = np.asarray(inp[k][0], dtype=np.float32).reshape(5120, 16384)
        m["ptab"] = np.ascontiguousarray(inp["page_table"][sl], dtype=np.int32)
        m["stk"] = f(inp["state_k_win"][0, sl]).reshape(4, 512, 128)
        m["stv"] = f(inp["state_v_win"][0, sl]).reshape(4, 512, 128)
        m["sconv"] = f(inp["state_conv"][0, sl])
        m["sC"] = f(inp["state_C"][0, sl])
        m["sn"] = f(inp["state_n"][0, sl])
        m["sm"] = f(inp["state_m"][0, sl]).reshape(16)
        m["cmk"] = f(inp["cache_mem_k"][0, sl]).reshape(4, 256, D)
        m["cmv"] = f(inp["cache_mem_v"][0, sl]).reshape(4, 256, D)
    return {k: v for k, v in m.items() if k in b.io}


_PROG = {}


def kernel(**inp):
    n = 8
    if "p" not in _PROG:
        _PROG["p"] = build_program()
    nc, b = _PROG["p"]
    in_maps = [core_inputs(inp, c, b) for c in range(n)]
    res = run_bass_kernel_spmd(nc, in_maps, core_ids=list(range(n)))
    R = res.results

    def st(name, shp, lead):
        a = np.stack([np.asarray(R[c][name], dtype=np.float32).reshape(shp) for c in range(n)])
        return np.ascontiguousarray(a.reshape(lead))

    outs = [st("y_p", (4096, D), (8, 4096, D)), st("y_s", (4, 4, D), (32, 4, D))]
    for nm in ("p_kc", "p_vc", "p_ks", "p_vs"):
        outs.append(st(nm, (4096, 2, 64), (1, 8, 4096, 2, 64)))
    for nm in ("p_kw", "p_vw"):
        outs.append(st(nm, (512, 2, 64), (1, 8, 512, 2, 64)))
    outs.append(st("p_C", (4, 128, 128), (1, 8, 4, 128, 128)))
    outs.append(st("p_n", (4, 128), (1, 8, 4, 128)))
    outs.append(st("p_m", (4,), (1, 8, 4)))
    outs.append(st("p_conv", (3, 512), (1, 8, 3, 512)))
    outs.append(st("p_mk", (256, 4, 256), (1, 8, 256, 4, 256)))
    outs.append(st("p_mv", (256, 4, 256), (1, 8, 256, 4, 256)))
    for nm in ("s_kc", "s_vc", "s_ks", "s_vs"):
        outs.append(st(nm, (4, 4, 2, 64), (1, 32, 4, 2, 64)))
    for nm in ("s_kw", "s_vw"):
        outs.append(st(nm, (4, 512, 2, 64), (1, 32, 512, 2, 64)))
    outs.append(st("s_C", (4, 4, 128, 128), (1, 32, 4, 128, 128)))
    outs.append(st("s_n", (4, 4, 128), (1, 32, 4, 128)))
    outs.append(st("s_m", (4, 4), (1, 32, 4)))
    outs.append(st("s_conv", (4, 3, 512), (1, 32, 3, 512)))
    return tuple(outs)
```

```python
import numpy as np
from contextlib import ExitStack
import concourse.bass as bass
import concourse.mybir as mybir
from concourse.bass_utils import run_bass_kernel_spmd

F32 = mybir.dt.float32
BF16 = mybir.dt.bfloat16
I32 = mybir.dt.int32
AF = mybir.ActivationFunctionType
ALU = mybir.AluOpType
AX = mybir.AxisListType

D = 1024
NEG = -30000.0
EPS = 1e-6
IN_COLS = 2848
DFF = 2816


class V:
    __slots__ = ("b", "a")

    def __init__(self, b, a):
        self.b = b
        self.a = a

    def __getitem__(self, k):
        return V(self.b, self.a[k])

    def re(self, p, **kw):
        return V(self.b, self.a.rearrange(p, **kw))

    def bc(self, shape):
        return V(self.b, self.a.to_broadcast(list(shape)))

    def un(self, ax):
        return V(self.b, self.a.unsqueeze(ax))


class Buf:
    __slots__ = ("t", "w", "r", "name", "psum", "fresh", "quads")

    def __init__(self, t, name="", psum=False):
        self.t = t
        self.w = None
        self.r = []
        self.name = name
        self.psum = psum
        self.fresh = True
        self.quads = set()

    def __getitem__(self, k):
        return V(self, self.t[k])


class DSem:
    def __init__(self, nc, name):
        self.sem = nc.alloc_semaphore(name)
        self.val = 0


class FW:
    ENG = ("pe", "act", "dve", "pool", "sp")

    def __init__(self, nc, n_dsem=10, same_engine_sync=True):
        self.nc = nc
        self.e = {"pe": nc.tensor, "act": nc.scalar, "dve": nc.vector, "pool": nc.gpsimd, "sp": nc.sync}
        self.gen = {k: 0 for k in self.ENG}
        self.sem = {k: nc.alloc_semaphore("S_" + k) for k in self.ENG}
        self.cnt = {k: 0 for k in self.ENG}
        self.seen = {k: {} for k in self.ENG}
        self.same = same_engine_sync
        self.dsems = {q: [DSem(nc, f"D{q}{i}") for i in range(n_dsem)] for q in ("sp", "pool", "act")}
        self.dnext = {q: 0 for q in self.dsems}
        self.nbuf = 0
        self.nins = 0

    def sb(self, stack, shape, dt=F32, name=None):
        self.nbuf += 1
        name = name or f"b{self.nbuf}"
        return Buf(stack.enter_context(self.nc.sbuf_tensor(name, list(shape), dt)), name)

    def ps(self, stack, shape, dt=F32, name=None):
        self.nbuf += 1
        name = name or f"p{self.nbuf}"
        return Buf(stack.enter_context(self.nc.psum_tensor(name, list(shape), dt)), name, psum=True)

    def _need(self, e, dep, waits):
        if dep is None:
            return
        kind, key, val, semh = dep
        if kind == "e" and key[0] == e and (not self.same or e == "pe"):
            return
        k = (kind, key if kind == "e" else id(key))
        if self.seen[e].get(k, 0) >= val:
            return
        cur = waits.get(k)
        if cur is None or cur[1] < val:
            waits[k] = (semh, val)

    def _emit_waits(self, e, reads, writes):
        waits = {}
        for b in reads:
            if b is not None:
                self._need(e, b.w, waits)
        for b in writes:
            if b is not None:
                self._need(e, b.w, waits)
                for d in b.r:
                    self._need(e, d, waits)
        eng = self.e[e]
        for k, (semh, val) in waits.items():
            eng.wait_ge(semh, val)
            self.seen[e][k] = val

    def op(self, e, fn, reads=(), writes=()):
        px = [b for b in reads if b is not None and b.psum]
        if px:
            reads = [b for b in reads if not (b is not None and b.psum)]
            writes = list(writes) + [b for b in px if b not in writes]
            if e != "pe":
                for b in px:
                    b.fresh = True
        self._emit_waits(e, reads, writes)
        ins = fn(self.e[e])
        if self.cnt[e] >= 50000:
            self.gen[e] += 1
            self.sem[e] = self.nc.alloc_semaphore(f"S_{e}_{self.gen[e]}")
            self.cnt[e] = 0
        self.cnt[e] += 1
        self.nins += 1
        ins.then_inc(self.sem[e], 1)
        dep = ("e", (e, self.gen[e]), self.cnt[e], self.sem[e])
        for b in reads:
            if b is not None:
                b.r.append(dep)
                if len(b.r) > 16:
                    b.r = self._compact(b.r)
        for b in writes:
            if b is not None:
                b.w = dep
                b.r = []
        return ins

    @staticmethod
    def _compact(lst):
        best = {}
        for d in lst:
            k = (d[0], d[1] if d[0] == "e" else id(d[1]))
            if k not in best or best[k][2] < d[2]:
                best[k] = d
        return list(best.values())

    def dma(self, q, o, i, fn=None, extra_reads=(), **kw):
        reads = [i.b] + list(extra_reads)
        writes = [o.b]
        self._emit_waits(q, reads, writes)
        ds = self.dsems[q][self.dnext[q]]
        self.dnext[q] = (self.dnext[q] + 1) % len(self.dsems[q])
        if ds.val > 0 and self.seen[q].get(("d", id(ds)), 0) < ds.val:
            self.e[q].wait_ge(ds.sem, ds.val)
            self.seen[q][("d", id(ds))] = ds.val
        if fn is None:
            ins = self.e[q].dma_start(out=o.a, in_=i.a, **kw)
        else:
            ins = fn(self.e[q])
        ds.val += 16
        self.nins += 1
        ins.then_inc(ds.sem, 16)
        dep = ("d", ds, ds.val, ds.sem)
        for b in reads:
            if b is not None:
                b.r.append(dep)
                if len(b.r) > 16:
                    b.r = self._compact(b.r)
        for b in writes:
            if b is not None:
                b.w = dep
                b.r = []
        return ins

    def barrier(self):
        for e in self.ENG:
            eng = self.e[e]
            for f in self.ENG:
                if f != e and self.cnt[f] > 0:
                    k = ("e", (f, self.gen[f]))
                    if self.seen[e].get(k, 0) < self.cnt[f]:
                        eng.wait_ge(self.sem[f], self.cnt[f])
                        self.seen[e][k] = self.cnt[f]
            for q in self.dsems:
                for ds in self.dsems[q]:
                    k = ("d", id(ds))
                    if ds.val > 0 and self.seen[e].get(k, 0) < ds.val:
                        eng.wait_ge(ds.sem, ds.val)
                        self.seen[e][k] = ds.val

    def finish(self):
        eng = self.e["sp"]
        for q in self.dsems:
            for ds in self.dsems[q]:
                if ds.val > 0:
                    eng.wait_ge(ds.sem, ds.val)


class RR:
    def __init__(self, items):
        self.items = list(items)
        self.i = 0

    def __call__(self):
        x = self.items[self.i]
        self.i = (self.i + 1) % len(self.items)
        return x


class Builder:
    def __init__(self, nc, NT=32, sample=True, debug=False):
        self.debug = debug
        self.nc = nc
        self.fw = FW(nc)
        self.NT = NT
        self.T = NT * 128
        self.sample = sample
        self.io = {}

    def din(self, name, shape, dt=F32):
        t = self.nc.dram_tensor(name, list(shape), dt, kind="ExternalInput").ap()
        self.io[name] = t
        return V(None, t)

    def dout(self, name, shape, dt=F32):
        t = self.nc.dram_tensor(name, list(shape), dt, kind="ExternalOutput").ap()
        self.io[name] = t
        return V(None, t)

    def dscr(self, name, shape, dt=F32):
        t = self.nc.dram_tensor(name, list(shape), dt, kind="ExternalOutput" if self.debug else "Internal").ap()
        if self.debug:
            self.io[name] = t
        return Buf(t, name)

    def dbg(self, name, v, dt=F32):
        if not self.debug:
            return
        o = self.dout("dbg_" + name, list(v.a.shape), dt)
        self.fw.dma("sp", o, v)

    def mm(self, o, l, r, st=True, sp=True):
        b = o.b
        p0 = o.a.base_partition() if hasattr(o.a, "base_partition") else 0
        q = set(range(p0 // 32, (p0 + o.a.shape[0] + 31) // 32))
        start = False
        if st:
            if b.fresh:
                start = True
                b.fresh = False
                b.quads = set(q)
            else:
                assert q <= b.quads, (b.name, q, b.quads)
        self.fw.op("pe", lambda e: e.matmul(o.a, lhsT=l.a, rhs=r.a, start=start, stop=sp, skip_group_check=True),
                   reads=[l.b, r.b], writes=[o.b])

    def tr(self, o, i, ident):
        self.fw.op("pe", lambda e: e.transpose(out=o.a, in_=i.a, identity=ident.a), reads=[i.b, ident.b], writes=[o.b])

    def act(self, o, i, f, scale=1.0, bias=0.0, acc=None):
        reads = [i.b]
        writes = [o.b]
        kw = {}
        if isinstance(bias, V):
            reads.append(bias.b)
            kw["bias"] = bias.a
        elif bias != 0.0:
            kw["bias"] = float(bias)
        if isinstance(scale, V):
            reads.append(scale.b)
            kw["scale"] = scale.a
        elif scale != 1.0:
            kw["scale"] = float(scale)
        if acc is not None:
            writes.append(acc.b)
            kw["accum_out"] = acc.a
        self.fw.op("act", lambda e: e.activation(out=o.a, in_=i.a, func=f, **kw), reads=reads, writes=writes)

    def ts(self, eng, o, i, s1, s2=None, op0=ALU.mult, op1=None):
        reads = [i.b]
        a1 = s1
        a2 = s2
        if isinstance(s1, V):
            reads.append(s1.b)
            a1 = s1.a
        if isinstance(s2, V):
            reads.append(s2.b)
            a2 = s2.a
        kw = {}
        if op1 is not None:
            kw["op1"] = op1
        self.fw.op(eng, lambda e: e.tensor_scalar(out=o.a, in0=i.a, scalar1=a1, scalar2=a2, op0=op0, **kw), reads=reads, writes=[o.b])

    def tt(self, eng, o, a, b, op):
        self.fw.op(eng, lambda e: e.tensor_tensor(out=o.a, in0=a.a, in1=b.a, op=op), reads=[a.b, b.b], writes=[o.b])

    def stt(self, eng, o, a, s, b, op0, op1):
        reads = [a.b, b.b]
        sa = s
        if isinstance(s, V):
            reads.append(s.b)
            sa = s.a
        self.fw.op(eng, lambda e: e.scalar_tensor_tensor(out=o.a, in0=a.a, scalar=sa, in1=b.a, op0=op0, op1=op1), reads=reads, writes=[o.b])

    def cp(self, eng, o, i):
        if eng == "act":
            self.fw.op("act", lambda e: e.copy(out=o.a, in_=i.a), reads=[i.b], writes=[o.b])
        else:
            self.fw.op(eng, lambda e: e.tensor_copy(out=o.a, in_=i.a), reads=[i.b], writes=[o.b])

    def ms(self, eng, o, val):
        self.fw.op(eng, lambda e: e.memset(o.a, val), writes=[o.b])

    def iota(self, o, pattern, base=0, cm=0):
        self.fw.op("pool", lambda e: e.iota(o.a, pattern=pattern, base=base, channel_multiplier=cm,
                                            allow_small_or_imprecise_dtypes=True), writes=[o.b])

    def sigm(self, o, i):
        self.act(o, i, AF.Exp, scale=-1.0)
        self.ts("dve", o, o, 1.0, None, op0=ALU.add)
        self.recip(o, o)

    def recip(self, o, i):
        self.fw.op("dve", lambda e: e.reciprocal(out=o.a, in_=i.a), reads=[i.b], writes=[o.b])

    def rsum(self, o, i):
        self.fw.op("dve", lambda e: e.reduce_sum(out=o.a, in_=i.a, axis=AX.X), reads=[i.b], writes=[o.b])

    def rmax(self, o, i):
        self.fw.op("dve", lambda e: e.reduce_max(out=o.a, in_=i.a, axis=AX.X), reads=[i.b], writes=[o.b])

    def dma(self, o, i, q="sp", **kw):
        self.fw.dma(q, o, i, **kw)

    def dmas(self, o, i, q="sp"):
        self.fw.dma(q, o, i, allow_slow_non_contiguous=True)

    def put_row(self, dst, pattern, base, n, const=None):
        rowt, rowb = self.rowt, self.rowb
        if const is None:
            rv = rowt[0:1, 0:n]
            if len(pattern) == 2:
                rv = rv.re("p (a b) -> p a b", b=pattern[1][1])
            self.iota(rv, pattern, base=base, cm=0)
        else:
            self.ms("pool", rowt[0:1, 0:n], const)
        self.cp("pool", rowb[0:1, 0:n], rowt[0:1, 0:n])
        self.dma(dst, rowb[0:1, 0:n])

    def rstd(self, o, ss, inv_n):
        self.act(o, ss, AF.Ln, scale=inv_n, bias=self.epsc[0:o.a.shape[0], :])
        self.act(o, o, AF.Exp, scale=-0.5)

    def build(self):
        nc, fw, NT, T = self.nc, self.fw, self.NT, self.T
        din, dout = self.din, self.dout
        mm, tr, act, ts, tt, stt, cp, ms, iota, dma, dmas = (self.mm, self.tr, self.act, self.ts, self.tt, self.stt,
                                                             self.cp, self.ms, self.iota, self.dma, self.dmas)
        xp = din("xp", [T, D])
        memp = din("memp", [256, D])
        w_in = din("w_in", [D, IN_COLS])
        g_mix = din("g_mix", [D])
        b_gate = din("b_gate", [24])
        cmp_in = {}
        for kv in "kv":
            cmp_in[kv] = (din(f"cmp_pe_{kv}", [32, 64]), din(f"cmp_w1_{kv}", [2048, 256]),
                          din(f"cmp_b1_{kv}", [256]), din(f"cmp_w2_{kv}", [256, 64]))
        g_head_nsa = din("g_head_nsa", [512])
        conv_w = din("conv_w", [4, 512])
        conv_b = din("conv_b", [512])
        w_qm = din("w_qm", [4, 128, 128])
        w_km = din("w_km", [4, 128, 128])
        b_i = din("b_i", [4])
        b_f = din("b_f", [4])
        g_head_m = din("g_head_m", [512])
        w_out = din("w_out", [D, D])
        g_xa = din("g_xa", [D])
        g_mem = din("g_mem", [D])
        w_xq = din("w_xq", [D, D])
        w_xk = din("w_xk", [D, D])
        w_xv = din("w_xv", [D, D])
        w_xo = din("w_xo", [D, D])
        g_ffn = din("g_ffn", [D])
        w_gate = din("w_gate", [D, DFF])
        w_up = din("w_up", [D, DFF])
        w_down = din("w_down", [DFF, D])
        g_final = din("g_final", [D])

        y_p = dout("y_p", [T, D])
        p_kv = {n: dout(n, [T, 128]) for n in ("p_kc", "p_vc", "p_ks", "p_vs")}
        WT = min(512, T)
        p_kw = dout("p_kw", [WT, 128])
        p_vw = dout("p_vw", [WT, 128])
        p_C = dout("p_C", [4, 128, 128])
        p_n = dout("p_n", [4, 128])
        p_m = dout("p_m", [4])
        p_conv = dout("p_conv", [3, 512])
        p_mk = dout("p_mk", [256, D])
        p_mv = dout("p_mv", [256, D])

        if self.sample:
            self.sample_io()
        x1d = self.dscr("x1_scr", [T, D])
        x2d = self.dscr("x2_scr", [T, D])

        top = ExitStack()
        with top:
            psf = [fw.ps(top, [128, 512], F32, f"psf{i}") for i in range(4)]
            pst = [fw.ps(top, [128, 1024], BF16, f"pst{i}") for i in range(1)]
            psa = [fw.ps(top, [128, 512], F32, f"psa{i}") for i in range(3)]
            nps = RR(psf)
            nps_s = RR(psf[0:3])
            nps_m = RR([psf[3], psa[2]])
            npt = RR(pst)

            ident = fw.sb(top, [128, 128], BF16, "ident")
            identf = fw.sb(top, [128, 128], F32, "identf")
            for idt in (ident, identf):
                ms("pool", idt[:], 0.0)
                fw.op("pool", lambda e, idt=idt: e.affine_select(out=idt[:].a, in_=idt[:].a, pattern=[[-1, 128]],
                                                                compare_op=ALU.not_equal, fill=1.0, base=0,
                                                                channel_multiplier=1), reads=[idt], writes=[idt])
            self.epsc = fw.sb(top, [128, 1], F32, "epsc")[:]
            ms("pool", self.epsc, EPS)
            if self.sample:
                self.sample_s0(locals())
            sw = ExitStack()
            tmpf = fw.sb(sw, [128, 512], F32, "tmpf")
            iota(tmpf[:].re("p (g r) -> p g r", g=4), [[0, 4], [-1, 128]], base=0, cm=1)
            bd01 = fw.sb(sw, [128, 128], BF16, "bd01")
            ts("pool", bd01[:], tmpf[:, 0:128], 0.0, None, op0=ALU.is_le)
            ms("pool", bd01[0:64, 64:128], 0.0)
            tri_le = fw.sb(sw, [128, 128], F32, "tri_le")
            ts("pool", tri_le[:], tmpf[:, 0:128], 0.0, None, op0=ALU.is_le)
            tri2 = fw.sb(sw, [128, 128], F32, "tri2")
            cp("pool", tri2[:], bd01[:])
            csel = fw.sb(sw, [128, 2, 128], F32, "csel")
            ms("pool", csel[:], 0.0)
            ms("pool", csel[0:64, 0, :], 1.0)
            ms("pool", csel[64:128, 1, :], 1.0)
            mimp = fw.sb(sw, [128, 2, 64], BF16, "mimp")
            for ct in range(2):
                iota(tmpf[:, 0:64], [[-4, 64]], base=ct * 128 - 1, cm=1)
                stt("dve", tmpf[:, 64:128], tmpf[:, 0:64], -1.0, tmpf[:, 0:64], ALU.mult, ALU.max)
                ts("pool", tmpf[:, 128:192], tmpf[:, 64:128], 2.0, 0.5, op0=ALU.is_le, op1=ALU.mult)
                ts("pool", tmpf[:, 192:256], tmpf[:, 64:128], 1.0, 0.5, op0=ALU.is_le, op1=ALU.mult)
                tt("pool", mimp[:, ct, :], tmpf[:, 128:192], tmpf[:, 192:256], ALU.add)
                ms("pool", mimp[:, ct, 63:64], 1.0)
            if True:
                win_b = fw.sb(sw, [128, 8, IN_COLS], BF16, "win_b")
                wout_b = fw.sb(sw, [128, 8, D], BF16, "wout_b")
                wqm_b = fw.sb(sw, [128, 4, 128], BF16, "wqm_b")
                wkm_b = fw.sb(sw, [128, 4, 128], BF16, "wkm_b")
                gcol = fw.sb(sw, [128, 16], F32, "gcol")
                dmas(gcol[:, 0:8], V(None, g_mix.a.rearrange("(k p) -> p k", p=128)))
                dmas(gcol[:, 8:12], V(None, g_head_nsa.a.rearrange("(k p) -> p k", p=128)))
                dmas(gcol[:, 12:16], V(None, g_head_m.a.rearrange("(k p) -> p k", p=128)))
                cw = fw.sb(sw, [128, 4, 4], F32, "cw")
                for j in range(4):
                    dmas(cw[:, :, j], V(None, conv_w.a[j].rearrange("(c p) -> p c", p=128)))
                cb = fw.sb(sw, [128, 4], F32, "cb")
                dmas(cb[:], V(None, conv_b.a.rearrange("(c p) -> p c", p=128)))
                bgate = fw.sb(sw, [128, 24], F32, "bgate")
                dma(bgate[:], V(None, b_gate.a.partition_broadcast(128)))
                bif = fw.sb(sw, [128, 8], F32, "bif")
                dma(bif[:, 0:4], V(None, b_i.a.partition_broadcast(128)))
                dma(bif[:, 4:8], V(None, b_f.a.partition_broadcast(128)))
                with ExitStack() as s0:
                    stg = [fw.sb(s0, [128, IN_COLS], F32, f"stg{i}") for i in range(2)]
                    for k in range(8):
                        st = stg[k % 2]
                        dma(st[:], w_in[k * 128:(k + 1) * 128, :])
                        if k % 2 == 0:
                            ts("dve", win_b[:, k, :], st[:], gcol[:, k:k + 1], None, op0=ALU.mult)
                        else:
                            act(win_b[:, k, :], st[:], AF.Copy, scale=gcol[:, k:k + 1])
                    for k in range(8):
                        st = stg[k % 2]
                        dma(st[:, 0:D], w_out[k * 128:(k + 1) * 128, :])
                        if k % 2 == 0:
                            ts("dve", wout_b[:, k, :], st[:, 0:D], gcol[:, 8 + k:9 + k], None, op0=ALU.mult)
                        else:
                            act(wout_b[:, k, :], st[:, 0:D], AF.Copy, scale=gcol[:, 8 + k:9 + k])
                    st = stg[0]
                    dma(st[:, 0:512].re("p (h e) -> p h e", h=4), V(None, w_qm.a.rearrange("h d e -> d h e")))
                    cp("dve", wqm_b[:], st[:, 0:512].re("p (h e) -> p h e", h=4))
                    st = stg[1]
                    dma(st[:, 0:512].re("p (h e) -> p h e", h=4), V(None, w_km.a.rearrange("h d e -> d h e")))
                    cp("dve", wkm_b[:], st[:, 0:512].re("p (h e) -> p h e", h=4))
                    fw.barrier()

                kcp = fw.sb(sw, [68, 2, 256], BF16, "kcp")
                vcp = fw.sb(sw, [128, 2, 2, 65], BF16, "vcp")
                ms("pool", vcp[:], 1.0)
                put_row = self.put_row
                with ExitStack() as tmps:
                    self.rowt = fw.sb(tmps, [1, 4096], F32, "rowt")
                    self.rowb = fw.sb(tmps, [1, 4096], BF16, "rowb")
                    for kvh in range(2):
                        put_row(kcp[64:65, kvh, :], [[128, 32], [0, 8]], 0, 256)
                        put_row(kcp[65:66, kvh, :], [[0, 32], [16, 8]], 31, 256)
                        put_row(kcp[66:67, kvh, :], None, 0, 256, const=1.0)
                        put_row(kcp[67:68, kvh, :], None, 0, 256, const=1.0)
                    fw.barrier()

                self.pass0_prompt(sw, xp, win_b, cmp_in, kcp, vcp, ident, nps, npt)
                self.dbg("kcp", kcp[:], BF16)
                self.dbg("vcp", vcp[:], BF16)
                self.pass1_prompt(sw, locals())
                if self.sample:
                    self.sample_pass1(locals())
            fw.barrier()
            sw.close()
            self.pass2(top, locals())
            fw.finish()

    def norm_T(self, src, xt, nb, hT, ident, npt, rows=128):
        self.norm_A(src, xt, nb, rows)
        self.norm_B(nb, hT, ident, npt)

    def norm_A(self, src, xt, nb, rows=128):
        if rows < 128:
            self.ms("pool", xt[:], 0.0)
        self.dma(xt[0:rows, :], src)
        self.ms("dve", nb["ss"][:], 0.0)
        self.act(nb["junk"][:], xt[:], AF.Square, acc=nb["ss"][:])
        self.rstd(nb["rs"][:], nb["ss"][:], 1.0 / D)
        self.ts("dve", nb["xn"][:], xt[:], nb["rs"][:, 0:1], None, op0=ALU.mult)

    def norm_B(self, nb, hT, ident, npt):
        pt = npt()
        for k in range(8):
            self.tr(pt[:, k * 128:(k + 1) * 128], nb["xn"][:, k * 128:(k + 1) * 128], ident[:])
        self.cp("act", hT[:].re("p k t -> p (k t)"), pt[:])

    def norm_bufs(self, s, tag):
        fw = self.fw
        xn = fw.sb(s, [128, D], BF16, "xn" + tag)
        return {"junk": xn, "ss": fw.sb(s, [128, 1], F32, "ss" + tag), "rs": fw.sb(s, [128, 1], F32, "rs" + tag), "xn": xn}

    def pass0_prompt(self, sw, xp, win_b, cmp_in, kcp, vcp, ident, nps, npt):
        fw, NT, T = self.fw, self.NT, self.T
        mm, act, tt, cp, ms, dma, dmas = self.mm, self.act, self.tt, self.cp, self.ms, self.dma, self.dmas
        NCB = T // 16
        with ExitStack() as s:
            srcT = {kv: fw.sb(s, [64, 2, 16, NCB + 1], BF16, "srcT" + kv) for kv in "kv"}
            for kv in "kv":
                ms("pool", srcT[kv][:, :, :, NCB:NCB + 1], 0.0)
            ms("pool", kcp[0:64, :, :], 0.0)
            xts = [fw.sb(s, [128, D], F32, f"x0_{i}") for i in range(2)]
            nb = self.norm_bufs(s, "0")
            hT = fw.sb(s, [128, 8, 128], BF16, "hT0")
            for t_ in range(NT):
                xt = xts[t_ % 2]
                self.norm_T(xp[t_ * 128:(t_ + 1) * 128, :], xt, nb, hT, ident, npt)
                ps = nps()
                for gi in range(4):
                    for k in range(8):
                        mm(ps[0:64, gi * 128:(gi + 1) * 128], win_b[:, k, 512 + 64 * gi:576 + 64 * gi], hT[:, k, :],
                           st=(k == 0), sp=(k == 7))
                for h in range(2):
                    cp("act", srcT["k"][:, h, :, 8 * t_:8 * t_ + 8], ps[0:64, h * 128:(h + 1) * 128].re("p (c j) -> p j c", j=16))
                    cp("dve", srcT["v"][:, h, :, 8 * t_:8 * t_ + 8], ps[0:64, 256 + h * 128:384 + h * 128].re("p (c j) -> p j c", j=16))
            w1s = [fw.sb(s, [64, 8, 256], F32, f"w1s{i}") for i in range(2)]
            for kv in "kv":
                pe, w1, b1, w2 = cmp_in[kv]
                w1b = fw.sb(s, [64, 32, 256], BF16, "w1b" + kv)
                for jb in range(4):
                    st = w1s[jb % 2]
                    dma(st[:], V(None, w1.a.rearrange("(j d) n -> d j n", d=64)[:, jb * 8:(jb + 1) * 8, :]))
                    cp("dve", w1b[:, jb * 8:(jb + 1) * 8, :], st[:])
                peT = fw.sb(s, [64, 32], F32, "peT" + kv)
                dmas(peT[:], V(None, pe.a.rearrange("j d -> d j")))
                peTb = fw.sb(s, [64, 32], BF16, "peTb" + kv)
                cp("dve", peTb[:], peT[:])
                b1c = fw.sb(s, [128, 2], F32, "b1c" + kv)
                dmas(b1c[:], V(None, b1.a.rearrange("(c p) -> p c", p=128)))
                w2s = fw.sb(s, [128, 2, 64], F32, "w2s" + kv)
                dma(w2s[:], V(None, w2.a.rearrange("(c p) n -> p c n", p=128)))
                w2b = fw.sb(s, [128, 2, 64], BF16, "w2b" + kv)
                cp("dve", w2b[:], w2s[:])
                cst = fw.sb(s, [128, 2], F32, "cst" + kv)
                for hc in range(2):
                    ps = nps()
                    for j in range(32):
                        mm(ps[:, 0:1], w1b[:, j, hc * 128:(hc + 1) * 128], peTb[:, j:j + 1], st=(j == 0), sp=(j == 31))
                    tt("dve", cst[:, hc:hc + 1], ps[:, 0:1], b1c[:, hc:hc + 1], ALU.add)
                gT = fw.sb(s, [128, 2, 256], BF16, "gT" + kv)
                if NCB < 256:
                    ms("pool", gT[:], 0.0)
                for kvh in range(2):
                    for hc in range(2):
                        ps = nps()
                        for j in range(32):
                            rv = srcT[kv][:, kvh, j, 0:NCB] if j < 16 else srcT[kv][:, kvh, j - 16, 1:NCB + 1]
                            mm(ps[:, 0:NCB], w1b[:, j, hc * 128:(hc + 1) * 128], rv, st=(j == 0), sp=(j == 31))
                        act(gT[:, hc, 0:NCB], ps[:, 0:NCB], AF.Gelu_apprx_tanh, bias=cst[:, hc:hc + 1])
                    if kv == "k":
                        ps = nps()
                        for hc in range(2):
                            mm(ps[0:64, 0:256], w2b[:, hc, :], gT[:, hc, :], st=(hc == 0), sp=(hc == 1))
                        cp("dve", kcp[0:64, kvh, :], ps[0:64, 0:256])
                    else:
                        for ct in range(2):
                            ps = nps()
                            for hc in range(2):
                                mm(ps[:, 0:64], gT[:, hc, ct * 128:(ct + 1) * 128], w2b[:, hc, :], st=(hc == 0), sp=(hc == 1))
                            cp("dve", vcp[:, ct, kvh, 0:64], ps[:, 0:64])
            fw.barrier()

    def pass1_prompt(self, sw, L):
        fw, NT, T = self.fw, self.NT, self.T
        mm, tr, act, ts, tt, stt, cp, ms, iota, dma, dmas = (self.mm, self.tr, self.act, self.ts, self.tt, self.stt,
                                                             self.cp, self.ms, self.iota, self.dma, self.dmas)
        xp, win_b, wout_b, wqm_b, wkm_b = L["xp"], L["win_b"], L["wout_b"], L["wqm_b"], L["wkm_b"]
        kcp, vcp, ident, identf, nps, npt, psa = L["kcp"], L["vcp"], L["ident"], L["identf"], L["nps"], L["npt"], L["psa"]
        nps_s, nps_m = L["nps_s"], L["nps_m"]
        bd01, tri2, csel, mimp, tmpf = L["bd01"], L["tri2"], L["csel"], L["mimp"], L["tmpf"]
        cw, cb, bgate, bif, put_row, x1d = L["cw"], L["cb"], L["bgate"], L["bif"], L["put_row"], L["x1d"]
        p_kv, p_kw, p_vw, p_C, p_n, p_m, p_conv = L["p_kv"], L["p_kw"], L["p_vw"], L["p_C"], L["p_n"], L["p_m"], L["p_conv"]
        with ExitStack() as s:
            iota(tmpf[:].re("p (g r) -> p g r", g=4), [[0, 4], [-1, 128]], base=0, cm=1)
            caus_add = fw.sb(s, [128, 512], BF16, "caus_add")
            ts("pool", caus_add[:], tmpf[:], 0.0, NEG, op0=ALU.is_gt, op1=ALU.mult)
            win_add = fw.sb(s, [128, 512], BF16, "win_add")
            ts("pool", win_add[:], tmpf[:], 0.0, NEG, op0=ALU.is_le, op1=ALU.mult)
            e0 = fw.sb(s, [128, 512], F32, "e0")
            iota(e0[:].re("p (g r) -> p g r", g=4), [[0, 4], [-1, 128]], base=0, cm=16)
            expand = fw.sb(s, [64, T], BF16, "expand")
            for c0 in range(0, T, 512):
                iota(tmpf[0:64, :], [[1, 512]], base=c0, cm=-64)
                ts("pool", tmpf[0:64, :], tmpf[0:64, :], 31.5, None, op0=ALU.subtract)
                stt("dve", tmpf[0:64, :], tmpf[0:64, :], -1.0, tmpf[0:64, :], ALU.mult, ALU.max)
                ts("pool", expand[:, c0:c0 + 512], tmpf[0:64, :], 32.0, None, op0=ALU.is_le)
            ksT = fw.sb(s, [68, 2, T], BF16, "ksT")
            NW = min(8, NT)
            kwT = fw.sb(s, [68, 2, NW * 128], BF16, "kwT")
            phr = fw.sb(s, [1, 128], BF16, "phr")
            vsp = fw.sb(s, [128, NT, 2, 65], BF16, "vsp")
            vwp = fw.sb(s, [128, NW, 2, 65], BF16, "vwp")
            ms("pool", vsp[:], 1.0)
            ms("pool", vwp[:], 1.0)
            with ExitStack() as tmps:
                self.rowt = fw.sb(tmps, [1, 4096], F32, "rowt1")
                self.rowb = fw.sb(tmps, [1, 4096], BF16, "rowb1")
                for kvh in range(2):
                    put_row(ksT[64:65, kvh, :], [[128, NT], [0, 128]], 0, T)
                    put_row(ksT[65:66, kvh, :], [[0, NT], [1, 128]], 0, T)
                    put_row(ksT[66:67, kvh, :], None, 0, T, const=1.0)
                    put_row(ksT[67:68, kvh, :], None, 0, T, const=1.0)
                    put_row(kwT[65:66, kvh, :], [[0, NW], [1, 128]], 0, NW * 128)
                    put_row(kwT[66:67, kvh, :], None, 0, NW * 128, const=1.0)
                    put_row(kwT[67:68, kvh, :], None, 0, NW * 128, const=1.0)
                fw.barrier()
            qps = [fw.sb(s, [68, 2, 4, 128], BF16, f"qp{i}") for i in range(2)]
            srow = fw.sb(s, [1, 8, 128], F32, "srow")
            for h in range(8):
                ms("pool", srow[0:1, h, :], 2.0 ** (-(h + 1)))
            r67 = fw.sb(s, [1, 8, 128], F32, "r67")
            iota(r67[:], [[0, 8], [1, 128]], base=0, cm=0)
            tt("pool", r67[:], r67[:], srow[:], ALU.mult)
            ts("pool", r67[:], r67[:], -1.0, None, op0=ALU.mult)
            srb = fw.sb(s, [1, 8, 128], BF16, "srb")
            r67b = fw.sb(s, [1, 8, 128], BF16, "r67b")
            r66b = fw.sb(s, [1, 8, 128], BF16, "r66b")
            cp("pool", srb[:], srow[:])
            cp("pool", r67b[:], r67[:])
            for qp in qps:
                for kvh in range(2):
                    dma(qp[64:65, kvh], srb[0:1, 4 * kvh:4 * kvh + 4, :])
                    dma(qp[65:66, kvh], srb[0:1, 4 * kvh:4 * kvh + 4, :])
                    dma(qp[67:68, kvh], r67b[0:1, 4 * kvh:4 * kvh + 4, :])
            xts = [fw.sb(s, [128, D], F32, f"x1_{i}") for i in range(2)]
            nb = self.norm_bufs(s, "1")
            hTs = [fw.sb(s, [128, 8, 128], BF16, f"hT1_{i}") for i in range(2)]
            pkv = fw.sb(s, [128, 792], F32, "pkv")
            gt = fw.sb(s, [128, 24], F32, "gt")
            pts = RR([fw.sb(s, [128, 512], BF16, f"pt{i}") for i in range(4)])
            mks = RR([fw.sb(s, [128, 512], BF16, f"mk{i}") for i in range(2)])
            obr = [[fw.sb(s, [128, 4, 65], F32, f"obr{k}{i}") for i in range(3)] for k in range(2)]
            rdc = fw.sb(s, [128, 4], F32, "rdc")
            imp4 = fw.sb(s, [128, 4, 64], F32, "imp4")
            imp = fw.sb(s, [128, 64], F32, "imp")
            imp2 = fw.sb(s, [128, 64], F32, "imp2")
            mx1 = fw.sb(s, [128, 8], F32, "mx1")
            mx2 = fw.sb(s, [128, 8], F32, "mx2")
            selm = [fw.sb(s, [128, 64], BF16, f"selm{k}") for k in range(2)]
            selT = [fw.sb(s, [64, 4, 128], BF16, f"selT{k}") for k in range(2)]
            fpb = fw.sb(s, [128, 3], F32, "fpb")
            ms("pool", fpb[:], -1.0)
            rd = fw.sb(s, [128, 3, 4], F32, "rd")
            sc3 = fw.sb(s, [128, 3, 4], F32, "sc3")
            onsa = fw.sb(s, [128, 4, 64], F32, "onsa")
            otmp = fw.sb(s, [128, 4, 64], F32, "otmp")
            ss4m = fw.sb(s, [128, 4], F32, "ss4pm")
            rs4m = fw.sb(s, [128, 4], F32, "rs4pm")
            ss4 = fw.sb(s, [128, 4], F32, "ss4")
            rs4 = fw.sb(s, [128, 4], F32, "rs4")
            mixin = fw.sb(s, [128, D], BF16, "mixin")
            mT = fw.sb(s, [128, 8, 128], BF16, "mT")
            x1t = fw.sb(s, [128, D], F32, "x1t")
            gif = fw.sb(s, [128, 8], F32, "gif")
            l1 = fw.sb(s, [128, 4], F32, "l1")
            gsb = fw.sb(s, [128, 12], F32, "gsb")
            wl = fw.sb(s, [128, 4], F32, "wl")
            ul = fw.sb(s, [128, 4], F32, "ul")
            tmp4 = fw.sb(s, [128, 4], F32, "tmp4")
            dec = fw.sb(s, [128, 4], F32, "dec")
            ebt = fw.sb(s, [128, 8], F32, "ebt")
            vmu = fw.sb(s, [128, 4, 129], BF16, "vmu")
            sigo = fw.sb(s, [128, 512], F32, "sigo")
            xcv = [fw.sb(s, [128, 4, 131], F32, f"xcv{i}") for i in range(2)]
            ms("pool", xcv[0][:], 0.0)
            cacc = fw.sb(s, [128, 4, 128], F32, "cacc")
            xc = fw.sb(s, [128, 4, 128], BF16, "xc")
            qmT = fw.sb(s, [128, 4, 128], BF16, "qmT")
            qmS = [fw.sb(s, [128, 4, 128], BF16, f"qmS{i}") for i in range(2)]
            for q_ in qmS:
                ms("pool", q_[:], 0.0)
            kmT = fw.sb(s, [128, 4, 128], BF16, "kmT")
            kmS = [fw.sb(s, [128, 4, 128], BF16, f"kmS{i}") for i in range(2)]
            mqk = fw.sb(s, [128, 4, 128], BF16, "mqk")
            Sf = fw.sb(s, [128, 4, 129], F32, "Sf")
            ms("pool", Sf[:], 0.0)
            Sb = [fw.sb(s, [128, 4, 129], BF16, f"Sb{i}") for i in range(3)]
            ms("pool", Sb[0][:], 0.0)
            dS = fw.sb(s, [128, 4, 129], F32, "dS")
            dn = fw.sb(s, [128, 4], F32, "dn")
            hout = fw.sb(s, [128, 4, 128], F32, "hout")
            hsq = cacc
            m4 = fw.sb(s, [4, 8], F32, "m4")
            R = fw.sb(s, [4, 1], F32, "Rm")
            ms("pool", R[:], 0.0)
            tsb = fw.sb(s, [4, 384], F32, "tsb")
            segs = [(0, 64), (64, 128)]
            KSC = 128.0 ** -0.5

            for t_ in range(NT):
                xt = xts[t_ % 2]
                qp = qps[t_ % 2]
                ts("pool", r66b[:], srow[:], -128.0 * t_, None, op0=ALU.mult)
                for kvh in range(2):
                    dma(qp[66:67, kvh], r66b[0:1, 4 * kvh:4 * kvh + 4, :])
                hT = hTs[t_ % 2]
                if t_ == 0:
                    self.norm_T(xp[0:128, :], xt, nb, hT, ident, npt)
                psA = nps()
                psB = nps()
                for k in range(8):
                    mm(psA[:, 0:512], hT[:, k, :], win_b[:, k, 512:1024], st=(k == 0), sp=(k == 7))
                for k in range(8):
                    mm(psB[:, 0:280], hT[:, k, :], win_b[:, k, 1024:1304], st=(k == 0), sp=(k == 7))
                cp("dve", pkv[:, 0:512], psA[:, 0:512])
                cp("act", pkv[:, 512:792], psB[:, 0:280])
                r0 = t_ * 128
                for i_, n_ in enumerate(("p_kc", "p_vc", "p_ks", "p_vs")):
                    dma(p_kv[n_][r0:r0 + 128, :], pkv[:, i_ * 128:(i_ + 1) * 128])
                if r0 >= T - 512:
                    w0 = r0 - (T - min(512, T))
                    dma(p_kw[w0:w0 + 128, :], pkv[:, 512:640])
                    dma(p_vw[w0:w0 + 128, :], pkv[:, 640:768])
                cp("pool", vsp[:, t_, :, 0:64], pkv[:, 384:512].re("p (h d) -> p h d", h=2))
                ws = t_ % NW
                cp("pool", vwp[:, ws, :, 0:64], pkv[:, 640:768].re("p (h d) -> p h d", h=2))
                tt("dve", gt[:], pkv[:, 768:792], bgate[:], ALU.add)
                self.sigm(gt[:], gt[:])
                psQ0 = nps()
                psQ1 = nps()
                for h in range(8):
                    ps = psQ0 if h < 4 else psQ1
                    for k in range(8):
                        mm(ps[0:64, (h % 4) * 128:(h % 4 + 1) * 128], win_b[:, k, 64 * h:64 * h + 64], hT[:, k, :],
                           st=(k == 0), sp=(k == 7))
                act(qp[0:64, 0].re("p g t -> p (g t)"), psQ0[0:64, :], AF.Copy, scale=0.125)
                act(qp[0:64, 1].re("p g t -> p (g t)"), psQ1[0:64, :], AF.Copy, scale=0.125)
                psK = nps()
                for gi, c0 in enumerate((768, 832, 1024, 1088)):
                    for k in range(8):
                        mm(psK[0:64, gi * 128:(gi + 1) * 128], win_b[:, k, c0:c0 + 64], hT[:, k, :], st=(k == 0), sp=(k == 7))
                cp("dve", ksT[0:64, :, r0:r0 + 128], psK[0:64, 0:256].re("p (h t) -> p h t", h=2))
                cp("dve", kwT[0:64, :, ws * 128:(ws + 1) * 128], psK[0:64, 256:512].re("p (h t) -> p h t", h=2))
                ms("pool", phr[:], 128.0 * t_)
                for kvh in range(2):
                    dma(kwT[64:65, kvh, ws * 128:(ws + 1) * 128], phr[:])

                def mlstm_gen():
                    psG = nps_m()
                    for k in range(8):
                        mm(psG[:, 0:8], hT[:, k, :], win_b[:, k, 2840:2848], st=(k == 0), sp=(k == 7))
                    tt("dve", gif[:], psG[:, 0:8], bif[:], ALU.add)
                    act(l1[:], gif[:, 4:8], AF.Exp, scale=-1.0)
                    act(l1[:], l1[:], AF.Ln, bias=1.0)
                    yield
                    psC = nps_m()
                    mm(psC[:, 0:4], tri2[:], l1[:])
                    mm(psC[:, 4:8], csel[:, 0, :], l1[:])
                    mm(psC[:, 8:12], csel[:, 1, :], l1[:])
                    cp("dve", gsb[:], psC[:, 0:12])
                    act(wl[:], gsb[:, 0:4], AF.Exp, scale=-1.0)
                    tt("dve", tmp4[:], gif[:, 0:4], gsb[:, 0:4], ALU.add)
                    act(ul[:], tmp4[:], AF.Exp)
                    act(ebt[:], gsb[:, 4:12], AF.Exp, scale=-1.0)
                    tt("dve", dec[0:64, :], tmp4[0:64, :], gsb[0:64, 4:8], ALU.subtract)
                    tt("dve", dec[64:128, :], tmp4[64:128, :], gsb[64:128, 8:12], ALU.subtract)
                    yield
                    psV = nps_m()
                    for k in range(8):
                        mm(psV[:, 0:512], hT[:, k, :], win_b[:, k, 1816:2328], st=(k == 0), sp=(k == 7))
                    tt("dve", vmu[:, :, 0:128], psV[:, 0:512].re("p (h e) -> p h e", h=4), ul[:].un(2).bc([128, 4, 128]), ALU.mult)
                    cp("dve", vmu[:, :, 128], ul[:])
                    yield
                    psO = nps_m()
                    for k in range(8):
                        mm(psO[:, 0:512], hT[:, k, :], win_b[:, k, 2328:2840], st=(k == 0), sp=(k == 7))
                    self.sigm(sigo[:], psO[:, 0:512])
                    yield
                    xcur, xnext = xcv[t_ % 2], xcv[(t_ + 1) % 2]
                    psX = nps_m()
                    for ch in range(4):
                        for k in range(8):
                            mm(psX[:, ch * 128:(ch + 1) * 128], win_b[:, k, 1304 + ch * 128:1432 + ch * 128], hT[:, k, :],
                               st=(k == 0), sp=(k == 7))
                    cp("act", xcur[:, :, 3:131], psX[:, :].re("p (c t) -> p c t", c=4))
                    cp("pool", xnext[:, :, 0:3], xcur[:, :, 128:131])
                    yield
                    for ch in range(4):
                        ts("dve", cacc[:, ch, :], xcur[:, ch, 0:128], cw[:, ch, 0:1], cb[:, ch:ch + 1], op0=ALU.mult, op1=ALU.add)
                        for j in range(1, 4):
                            stt("dve", cacc[:, ch, :], xcur[:, ch, j:j + 128], cw[:, ch, j:j + 1], cacc[:, ch, :], ALU.mult, ALU.add)
                        if ch % 2 == 1:
                            yield
                    self.sigm(hout[:], cacc[:])
                    tt("dve", xc[:], cacc[:], hout[:], ALU.mult)
                    if t_ == NT - 1:
                        for j in range(3):
                            dmas(V(None, p_conv.a[j].rearrange("(c p) -> p c", p=128)), xcur[:, :, 128 + j])
                    yield
                    psq = nps_m()
                    for h in range(4):
                        mm(psq[:, h * 128:(h + 1) * 128], wqm_b[:, h, :], xc[:, h, :])
                    cp("act", qmT[:].re("p h t -> p (h t)"), psq[:, :])
                    for si, (a_, b_) in enumerate(segs):
                        cp("dve", qmS[si][:, :, a_:b_], psq[:, :].re("p (h t) -> p h t", h=4)[:, :, a_:b_])
                    psk = nps_m()
                    for h in range(4):
                        mm(psk[:, h * 128:(h + 1) * 128], wkm_b[:, h, :], xc[:, h, :])
                    act(kmT[:].re("p h t -> p (h t)"), psk[:, :], AF.Copy, scale=KSC)
                    yield
                    pskt = nps_m()
                    for h in range(4):
                        mm(pskt[:, h * 128:(h + 1) * 128], xc[:, h, :], wkm_b[:, h, :])
                    for si in range(2):
                        ts("dve", kmS[si][:].re("p h t -> p (h t)"), pskt[:, :], csel[:, si, 0:1], KSC, op0=ALU.mult, op1=ALU.mult)
                    psqk = nps_m()
                    for h in range(4):
                        mm(psqk[:, h * 128:(h + 1) * 128], kmT[:, h, :], qmT[:, h, :])
                    tt("dve", mqk[:], psqk[:, :].re("p (h t) -> p h t", h=4), bd01[:].un(1).bc([128, 4, 128]), ALU.mult)
                    yield
                    sbs = [Sb[(2 * t_) % 3], Sb[(2 * t_ + 1) % 3], Sb[(2 * t_ + 2) % 3]]
                    for si in range(2):
                        pd = [nps_m(), nps_m()]
                        for h in range(4):
                            mm(pd[h // 2][:, (h % 2) * 129:(h % 2) * 129 + 129], kmS[si][:, h, :], vmu[:, h, :])
                        eb = ebt[:, 4 * si:4 * si + 4].un(2).bc([128, 4, 129])
                        tt("dve", Sf[:], Sf[:], eb, ALU.mult)
                        for hh in range(2):
                            tt("dve", dS[:, 2 * hh:2 * hh + 2, :], pd[hh][:, 0:258].re("p (h e) -> p h e", h=2),
                               ebt[:, 4 * si + 2 * hh:4 * si + 2 * hh + 2].un(2).bc([128, 2, 129]), ALU.mult)
                        tt("dve", Sf[:], Sf[:], dS[:], ALU.add)
                        cp("act", sbs[si + 1][:], Sf[:])
                        yield
                    pa = [nps_m(), nps_m()]
                    for h in range(4):
                        o_ = pa[h // 2][:, (h % 2) * 129:(h % 2) * 129 + 129]
                        mm(o_, mqk[:, h, :], vmu[:, h, :], st=True, sp=False)
                        mm(o_, qmS[0][:, h, :], sbs[0][:, h, :], st=False, sp=False)
                        mm(o_, qmS[1][:, h, :], sbs[1][:, h, :], st=False, sp=True)
                    for hh in range(2):
                        av = pa[hh][:, 0:258].re("p (h e) -> p h e", h=2)
                        tt("dve", dn[:, 2 * hh:2 * hh + 2], av[:, :, 128], wl[:, 2 * hh:2 * hh + 2], ALU.mult)
                    stt("dve", tmp4[:], dn[:], -1.0, dn[:], ALU.mult, ALU.max)
                    ts("dve", tmp4[:], tmp4[:], 1.0, None, op0=ALU.max)
                    self.recip(tmp4[:], tmp4[:])
                    tt("dve", tmp4[:], tmp4[:], wl[:], ALU.mult)
                    for hh in range(2):
                        av = pa[hh][:, 0:258].re("p (h e) -> p h e", h=2)
                        tt("dve", hout[:, 2 * hh:2 * hh + 2, :], av[:, :, 0:128],
                           tmp4[:, 2 * hh:2 * hh + 2].un(2).bc([128, 2, 128]), ALU.mult)
                    yield
                    tt("dve", hsq[:], hout[:], hout[:], ALU.mult)
                    self.rsum(ss4m[:], hsq[:])
                    self.rstd(rs4m[:], ss4m[:], 1.0 / 128)
                    tt("dve", hout[:], hout[:], rs4m[:].un(2).bc([128, 4, 128]), ALU.mult)
                    tt("dve", mixin[:, 512:1024], hout[:].re("p h e -> p (h e)"), sigo[:], ALU.mult)
                    yield
                    psT = nps_m()
                    tr(psT[0:4, 0:128], dec[:], identf[:])
                    tr(psT[0:4, 128:256], gsb[:, 4:8], identf[:])
                    tr(psT[0:4, 256:384], gsb[:, 8:12], identf[:])
                    cp("dve", tsb[:], psT[0:4, 0:384])
                    self.rmax(m4[:, 0:1], tsb[:, 0:64])
                    self.rmax(m4[:, 1:2], tsb[:, 64:128])
                    stt("dve", R[:], R[:], tsb[:, 128:129], m4[:, 0:1], ALU.subtract, ALU.max)
                    stt("dve", R[:], R[:], tsb[:, 256:257], m4[:, 1:2], ALU.subtract, ALU.max)

                mg = mlstm_gen()
                steps = []
                for kvh in range(2):
                    cts = [0] if t_ < 16 else [0, 1]
                    for ci, ct in enumerate(cts):
                        steps.append(("cmp", kvh, ct, ci == 0, ci == len(cts) - 1))
                for kvh in range(2):
                    k0 = max(0, t_ - 4)
                    for kt in range(k0, t_ + 1):
                        steps.append(("win", kvh, kt, kt == k0, kt == t_))
                for kvh in range(2):
                    for kt in range(t_ + 1):
                        steps.append(("sel", kvh, kt, kt == 0, kt == t_))
                pend = {}

                def score(i):
                    kind, kvh, k, first, last = steps[i]
                    qv = qp[:, kvh].re("p g t -> p (g t)")
                    S = nps_s()
                    if kind == "cmp":
                        Kq = 128 * t_ - 2048 * k - 31
                        need_mask = Kq < 2032
                        mm(S[:, :], kcp[:, kvh, k * 128:(k + 1) * 128], qv, st=True, sp=not need_mask)
                        if need_mask:
                            mk = mks()
                            ts("dve", mk[:], e0[:], float(Kq), NEG, op0=ALU.is_gt, op1=ALU.mult)
                            mm(S[:, :], ident[:], mk[:], st=False, sp=True)
                    elif kind == "win":
                        madd = caus_add if k == t_ else (win_add if k == t_ - 4 else None)
                        mm(S[:, :], kwT[:, kvh, (k % NW) * 128:(k % NW + 1) * 128], qv, st=True, sp=(madd is None))
                        if madd is not None:
                            mm(S[:, :], ident[:], madd[:], st=False, sp=True)
                    else:
                        if first and kvh == 0:
                            flush_sel()
                        mm(S[:, :], ksT[:, kvh, k * 128:(k + 1) * 128], qv, st=True, sp=False)
                        if k < t_:
                            mm(S[:, :], expand[:, k * 128:(k + 1) * 128], selT[kvh][:].re("p g t -> p (g t)"), st=False, sp=True)
                        else:
                            mm(S[:, :], ident[:], caus_add[:], st=False, sp=True)
                    pt = pts()
                    act(pt[:], S[:, :], AF.Exp)
                    pend[i] = pt

                def finish_cmp(kvh):
                    bank = psa[kvh]
                    cp("dve", obr[kvh][0][:, :, 0:64], bank[:, 0:256].re("p (g d) -> p g d", g=4))
                    cp("act", imp4[:].re("p g j -> p (g j)"), bank[:, 256:512])
                    cp("dve", obr[kvh][0][:, :, 64], imp4[:, :, 63])
                    ts("dve", rdc[:], obr[kvh][0][:, :, 64], 1e-30, None, op0=ALU.max)
                    self.recip(rdc[:], rdc[:])
                    ts("dve", imp[:], imp4[:, 0, :], rdc[:, 0:1], None, op0=ALU.mult)
                    for g in range(1, 4):
                        stt("dve", imp[:], imp4[:, g, :], rdc[:, g:g + 1], imp[:], ALU.mult, ALU.add)
                    ms("dve", imp[:, 63:64], 0.0)
                    if t_ == 0:
                        tt("dve", imp[:, 0:2], imp[:, 0:2], fpb[:, 1:3], ALU.max)
                    else:
                        tt("dve", imp[:, 2 * t_ - 1:2 * t_ + 2], imp[:, 2 * t_ - 1:2 * t_ + 2], fpb[:, 0:3], ALU.max)
                        ms("dve", imp[:, 0:1], 3e9)
                    fw.op("dve", lambda e: e.max(out=mx1[:].a, in_=imp[:].a), reads=[imp], writes=[mx1])
                    fw.op("dve", lambda e: e.match_replace(out=imp2[:].a, in_to_replace=mx1[:].a, in_values=imp[:].a,
                                                           imm_value=-1e30), reads=[imp, mx1], writes=[imp2])
                    fw.op("dve", lambda e: e.max(out=mx2[:].a, in_=imp2[:].a), reads=[imp2], writes=[mx2])
                    ts("dve", selm[kvh][:], imp[:], mx2[:, 7:8], NEG, op0=ALU.is_lt, op1=ALU.mult)

                def flush_sel():
                    for kvh_ in range(2):
                        ptr = npt()
                        tr(ptr[0:64, 0:128], selm[kvh_][:], ident[:])
                        cp("dve", selT[kvh_][:], ptr[0:64, 0:128].un(1).bc([64, 4, 128]))

                def pv(i):
                    kind, kvh, k, first, last = steps[i]
                    pt = pend.pop(i)
                    acc = psa[kvh]
                    if kind == "cmp":
                        for g in range(4):
                            mm(acc[:, g * 64:(g + 1) * 64], pt[:, g * 128:(g + 1) * 128], vcp[:, k, kvh, 0:64], st=first, sp=last)
                            mm(acc[:, 256 + g * 64:320 + g * 64], pt[:, g * 128:(g + 1) * 128], mimp[:, k, :], st=first, sp=last)
                    else:
                        vv = vwp[:, k % NW, kvh, :] if kind == "win" else vsp[:, k, kvh, :]
                        for g in range(4):
                            mm(acc[:, g * 65:(g + 1) * 65], pt[:, g * 128:(g + 1) * 128], vv, st=first, sp=last)
                    if last:
                        if kind == "cmp":
                            finish_cmp(kvh)
                        elif kind == "win":
                            cp("act", obr[kvh][2][:].re("p g d -> p (g d)"), acc[:, 0:260])
                        else:
                            cp("dve", obr[kvh][1][:].re("p g d -> p (g d)"), acc[:, 0:260])

                base = 1e9 + 1e6 * (2 * t_)
                ms("dve", fpb[0:64, 0:1], base - 1e6)
                ms("dve", fpb[0:64, 1:2], base)
                ms("dve", fpb[64:128, 1:2], base)
                ms("dve", fpb[64:128, 2:3], base + 1e6)
                LA = 2
                for i in range(min(LA, len(steps))):
                    score(i)
                iB = min(8, len(steps) - 1)
                pace = max(1, (len(steps) - 2) // 18)
                for i in range(len(steps)):
                    if i + LA < len(steps):
                        score(i + LA)
                    pv(i)
                    if i >= 1 and (i - 1) % pace == 0:
                        next(mg, None)
                    if t_ + 1 < NT:
                        if i == 0:
                            self.norm_A(xp[(t_ + 1) * 128:(t_ + 2) * 128, :], xts[(t_ + 1) % 2], nb)
                        if i == iB:
                            self.norm_B(nb, hTs[(t_ + 1) % 2], ident, npt)
                for _ in mg:
                    pass
                for kvh in range(2):
                    for br in range(3):
                        ts("dve", rd[:, br, :], obr[kvh][br][:, :, 64], 1e-30, None, op0=ALU.max)
                    self.recip(rd[:].re("p b g -> p (b g)"), rd[:].re("p b g -> p (b g)"))
                    gv = gt[:, 12 * kvh:12 * kvh + 12].re("p (g b) -> p b g", b=3)
                    tt("dve", sc3[:], rd[:], gv, ALU.mult)
                    tt("dve", onsa[:], obr[kvh][0][:, :, 0:64], sc3[:, 0, :].un(2).bc([128, 4, 64]), ALU.mult)
                    for br in (1, 2):
                        tt("dve", otmp[:], obr[kvh][br][:, :, 0:64], sc3[:, br, :].un(2).bc([128, 4, 64]), ALU.mult)
                        tt("dve", onsa[:], onsa[:], otmp[:], ALU.add)
                    tt("dve", otmp[:], onsa[:], onsa[:], ALU.mult)
                    self.rsum(ss4[:], otmp[:])
                    self.rstd(rs4[:], ss4[:], 1.0 / 64)
                    tt("dve", mixin[:, 256 * kvh:256 * kvh + 256].re("p (g d) -> p g d", g=4), onsa[:],
                       rs4[:].un(2).bc([128, 4, 64]), ALU.mult)

                ptm = npt()
                for k in range(8):
                    tr(ptm[:, k * 128:(k + 1) * 128], mixin[:, k * 128:(k + 1) * 128], ident[:])
                cp("act", mT[:].re("p k t -> p (k t)"), ptm[:])
                for g in range(2):
                    ps = nps()
                    for k in range(8):
                        mm(ps[:, :], mT[:, k, :], wout_b[:, k, g * 512:(g + 1) * 512], st=(k == 0), sp=(k == 7))
                    tt("dve", x1t[:, g * 512:(g + 1) * 512], xt[:, g * 512:(g + 1) * 512], ps[:, :], ALU.add)
                dma(x1d[r0:r0 + 128, :], x1t[:])

            dma(V(None, p_m.a.rearrange("(h o) -> h o", o=1)), R[:])
            ones4 = fw.sb(s, [4, 128], F32, "ones4")
            ms("pool", ones4[:], 1.0)
            ts("dve", ones4[:], ones4[:], R[:, 0:1], None, op0=ALU.mult)
            ps = nps()
            mm(ps[:, 0:4], ones4[:], identf[0:4, 0:4])
            act(tmp4[:], ps[:, 0:4], AF.Exp, scale=-1.0)
            tt("dve", Sf[:], Sf[:], tmp4[:].un(2).bc([128, 4, 129]), ALU.mult)
            dmas(V(None, p_n.a.rearrange("h d -> d h")), Sf[:, :, 128])
            for h in range(4):
                ps = nps()
                tr(ps[:, 0:128], Sf[:, h, 0:128], identf[:])
                cp("dve", hsq[:, h, :], ps[:, 0:128])
                dma(p_C[h], hsq[:, h, :])
            fw.barrier()

    def load_w(self, dst, src, rows_chunks, ncols, stg, gcol=None, g0=0):
        for k in range(rows_chunks):
            st = stg[k % 2]
            self.dma(st[:, 0:ncols], src[k * 128:(k + 1) * 128, :])
            if k % 2 == 0:
                if gcol is None:
                    self.cp("dve", dst[:, k, :], st[:, 0:ncols])
                else:
                    self.ts("dve", dst[:, k, :], st[:, 0:ncols], gcol[:, g0 + k:g0 + k + 1], None, op0=ALU.mult)
            elif gcol is None:
                self.cp("act", dst[:, k, :], st[:, 0:ncols])
            else:
                self.act(dst[:, k, :], st[:, 0:ncols], AF.Copy, scale=gcol[:, g0 + k:g0 + k + 1])

    def pass2(self, top, L):
        fw, NT, T = self.fw, self.NT, self.T
        mm, tr, act, ts, tt, stt, cp, ms, dma, dmas = (self.mm, self.tr, self.act, self.ts, self.tt, self.stt,
                                                       self.cp, self.ms, self.dma, self.dmas)
        ident, nps, npt, psa = L["ident"], L["nps"], L["npt"], L["psa"]
        x1d, x2d, y_p = L["x1d"], L["x2d"], L["y_p"]
        g2 = fw.sb(top, [128, 32], F32, "g2")
        for i_, g_ in enumerate((L["g_xa"], L["g_mem"], L["g_ffn"])):
            dmas(g2[:, 8 * i_:8 * i_ + 8], V(None, g_.a.rearrange("(k p) -> p k", p=128)))
        with ExitStack() as s:
            wxq_b = fw.sb(s, [128, 8, D], BF16, "wxq_b")
            wxo_b = fw.sb(s, [128, 8, D], BF16, "wxo_b")
            mkT = fw.sb(s, [128, 8, 256], BF16, "mkT")
            mvp = fw.sb(s, [128, 2, 4, 257], BF16, "mvp")
            ms("pool", mvp[:], 1.0)
            xts = [fw.sb(s, [128, D], F32, f"x2_{i}") for i in range(2)]
            nb = self.norm_bufs(s, "2")
            hT = fw.sb(s, [128, 8, 128], BF16, "hT2")
            with ExitStack() as s2:
                stg = [fw.sb(s2, [128, D], F32, f"stg2{i}") for i in range(2)]
                wxk_b = fw.sb(s2, [128, 8, D], BF16, "wxk_b")
                wxv_b = fw.sb(s2, [128, 8, D], BF16, "wxv_b")
                self.load_w(wxq_b, L["w_xq"], 8, D, stg, g2, 0)
                self.load_w(wxo_b, L["w_xo"], 8, D, stg)
                self.load_w(wxk_b, L["w_xk"], 8, D, stg, g2, 8)
                self.load_w(wxv_b, L["w_xv"], 8, D, stg, g2, 8)
                mo = fw.sb(s2, [128, D], F32, "mo")
                for mt in range(2):
                    xt = xts[mt % 2]
                    self.norm_T(L["memp"][mt * 128:(mt + 1) * 128, :], xt, nb, hT, ident, npt)
                    for wi, (wb_, po) in enumerate(((wxk_b, L["p_mk"]), (wxv_b, L["p_mv"]))):
                        for g in range(2):
                            ps = nps()
                            for k in range(8):
                                mm(ps[:, :], hT[:, k, :], wb_[:, k, g * 512:(g + 1) * 512], st=(k == 0), sp=(k == 7))
                            cp("dve" if g == 0 else "act", mo[:, g * 512:(g + 1) * 512], ps[:, :])
                        dma(po[mt * 128:(mt + 1) * 128, :], mo[:])
                        if wi == 1:
                            cp("pool", mvp[:, mt, :, 0:256], mo[:].re("p (h d) -> p h d", h=4))
                    for c4 in range(2):
                        ps = nps()
                        for cc in range(4):
                            c = c4 * 4 + cc
                            for k in range(8):
                                mm(ps[:, cc * 128:(cc + 1) * 128], wxk_b[:, k, c * 128:(c + 1) * 128], hT[:, k, :],
                                   st=(k == 0), sp=(k == 7))
                        cp("dve", mkT[:, c4 * 4:c4 * 4 + 4, mt * 128:(mt + 1) * 128], ps[:, :].re("p (c t) -> p c t", c=4))
                fw.barrier()
            qxT = fw.sb(s, [128, 8, 128], BF16, "qxT")
            pts = [fw.sb(s, [128, 512], BF16, f"pxt{i}") for i in range(2)]
            ox = fw.sb(s, [128, D], BF16, "ox")
            oxT = fw.sb(s, [128, 8, 128], BF16, "oxT")
            rdx = fw.sb(s, [128, 1], F32, "rdx")
            x2t = fw.sb(s, [128, D], F32, "x2t")
            for t_ in range(NT):
                xt = xts[t_ % 2]
                r0 = t_ * 128
                self.norm_T(x1d[r0:r0 + 128, :], xt, nb, hT, ident, npt)
                for c4 in range(2):
                    ps = nps()
                    for cc in range(4):
                        c = c4 * 4 + cc
                        for k in range(8):
                            mm(ps[:, cc * 128:(cc + 1) * 128], wxq_b[:, k, c * 128:(c + 1) * 128], hT[:, k, :],
                               st=(k == 0), sp=(k == 7))
                    act(qxT[:, c4 * 4:c4 * 4 + 4, :].re("p c t -> p (c t)"), ps[:, :], AF.Copy, scale=1.0 / 16)
                for mt in range(2):
                    S = nps()
                    for h in range(4):
                        for hf in range(2):
                            mm(S[:, h * 128:(h + 1) * 128], mkT[:, 2 * h + hf, mt * 128:(mt + 1) * 128], qxT[:, 2 * h + hf, :],
                               st=(hf == 0), sp=(hf == 1))
                    act(pts[mt][:], S[:, :], AF.Exp)
                for h in range(4):
                    acc = psa[h % 2]
                    for mt in range(2):
                        mm(acc[:, 0:257], pts[mt][:, h * 128:(h + 1) * 128], mvp[:, mt, h, :], st=(mt == 0), sp=(mt == 1))
                    self.recip(rdx[:], acc[:, 256:257])
                    ts("dve", ox[:, h * 256:(h + 1) * 256], acc[:, 0:256], rdx[:, 0:1], None, op0=ALU.mult)
                pto = npt()
                for k in range(8):
                    tr(pto[:, k * 128:(k + 1) * 128], ox[:, k * 128:(k + 1) * 128], ident[:])
                cp("act", oxT[:].re("p k t -> p (k t)"), pto[:])
                for g in range(2):
                    ps = nps()
                    for k in range(8):
                        mm(ps[:, :], oxT[:, k, :], wxo_b[:, k, g * 512:(g + 1) * 512], st=(k == 0), sp=(k == 7))
                    tt("dve", x2t[:, g * 512:(g + 1) * 512], xt[:, g * 512:(g + 1) * 512], ps[:, :], ALU.add)
                dma(x2d[r0:r0 + 128, :], x2t[:])
            if self.sample:
                S = self.S
                xt = xts[0]
                self.norm_T(S["x1s"][:, :], xt, nb, hT, ident, npt, rows=16)
                qxs = fw.sb(s, [128, 8, 16], BF16, "qxs")
                ps = nps()
                for c in range(8):
                    for k in range(8):
                        mm(ps[:, c * 16:(c + 1) * 16], wxq_b[:, k, c * 128:(c + 1) * 128], hT[:, k, 0:16], st=(k == 0), sp=(k == 7))
                act(qxs[:].re("p c t -> p (c t)"), ps[:, 0:128], AF.Copy, scale=1.0 / 16)
                ms("pool", ox[:], 0.0)
                msg = fw.sb(s, [128, 2, D], F32, "msg")
                mkb = fw.sb(s, [128, 2, D], BF16, "mkb")
                ptx = [fw.sb(s, [128, 16], BF16, f"ptx{i}") for i in range(2)]
                oxb = fw.sb(s, [4, D], BF16, "oxb")
                for b in range(4):
                    dma(msg[:], V(None, S["cmk"].a[b].rearrange("(t p) f -> p t f", p=128)))
                    cp("dve", mkb[:, 0, :], msg[:, 0, :])
                    cp("pool", mkb[:, 1, :], msg[:, 1, :])
                    for mt in range(2):
                        pt = npt()
                        for c in range(8):
                            tr(pt[:, c * 128:(c + 1) * 128], mkb[:, mt, c * 128:(c + 1) * 128], ident[:])
                        cp("act", mkT[:, :, mt * 128:(mt + 1) * 128], pt[:, :].re("p (c t) -> p c t", c=8))
                    dma(msg[:], V(None, S["cmv"].a[b].rearrange("(t p) f -> p t f", p=128)))
                    for mt in range(2):
                        cp("dve" if mt == 0 else "pool", mvp[:, mt, :, 0:256], msg[:, mt, :].re("p (h d) -> p h d", h=4))
                    for mt in range(2):
                        Sx = nps()
                        for h in range(4):
                            for hf in range(2):
                                mm(Sx[:, h * 4:(h + 1) * 4], mkT[:, 2 * h + hf, mt * 128:(mt + 1) * 128],
                                   qxs[:, 2 * h + hf, 4 * b:4 * b + 4], st=(hf == 0), sp=(hf == 1))
                        act(ptx[mt][:], Sx[:, 0:16], AF.Exp)
                    for h in range(4):
                        acc = psa[h % 2]
                        for mt in range(2):
                            mm(acc[0:4, 0:257], ptx[mt][:, 4 * h:4 * h + 4], mvp[:, mt, h, :], st=(mt == 0), sp=(mt == 1))
                        self.recip(rdx[0:4, :], acc[0:4, 256:257])
                        ts("dve", oxb[:, h * 256:(h + 1) * 256], acc[0:4, 0:256], rdx[0:4, 0:1], None, op0=ALU.mult)
                    dma(ox[4 * b:4 * b + 4, :], oxb[:])
                pto = npt()
                for k in range(8):
                    tr(pto[:, k * 128:(k + 1) * 128], ox[:, k * 128:(k + 1) * 128], ident[:])
                cp("act", oxT[:].re("p k t -> p (k t)"), pto[:])
                for g in range(2):
                    ps = nps()
                    for k in range(8):
                        mm(ps[:, :], oxT[:, k, :], wxo_b[:, k, g * 512:(g + 1) * 512], st=(k == 0), sp=(k == 7))
                    tt("dve", x2t[:, g * 512:(g + 1) * 512], xt[:, g * 512:(g + 1) * 512], ps[:, :], ALU.add)
                dma(S["x2s"][:, :], x2t[0:16, :])
            fw.barrier()
        with ExitStack() as s:
            wg_b = fw.sb(s, [128, 8, DFF], BF16, "wg_b")
            wu_b = fw.sb(s, [128, 8, DFF], BF16, "wu_b")
            wd_b = fw.sb(s, [128, 22, D], BF16, "wd_b")
            gfin = fw.sb(s, [128, D], F32, "gfin")
            dma(gfin[:], V(None, L["g_final"].a.partition_broadcast(128)))
            with ExitStack() as s2:
                stg = [fw.sb(s2, [128, DFF], F32, f"stg3{i}") for i in range(2)]
                self.load_w(wg_b, L["w_gate"], 8, DFF, stg, g2, 16)
                self.load_w(wu_b, L["w_up"], 8, DFF, stg, g2, 16)
                self.load_w(wd_b, L["w_down"], 22, D, stg)
                fw.barrier()
            xts = [fw.sb(s, [128, D], F32, f"x3_{i}") for i in range(4)]
            nb = self.norm_bufs(s, "3")
            hT4 = fw.sb(s, [128, 8, 512], BF16, "hT3")
            hT = fw.sb(s, [128, 8, 128], BF16, "hT3s")
            aT4 = fw.sb(s, [128, 22, 512], BF16, "aT4")
            aT = fw.sb(s, [128, 22, 128], BF16, "aT")
            sgs = [fw.sb(s, [128, 512], F32, f"sg{i}") for i in range(2)]
            sg = sgs[0]
            x3t = fw.sb(s, [128, D], F32, "x3t")

            def final_norm(dst_, rows_):
                ms("dve", nb["ss"][:], 0.0)
                act(nb["junk"][:], x3t[:], AF.Square, acc=nb["ss"][:])
                self.rstd(nb["rs"][:], nb["ss"][:], 1.0 / D)
                stt("dve", x3t[:], x3t[:], nb["rs"][:, 0:1], gfin[:], ALU.mult, ALU.mult)
                dma(dst_, x3t[0:rows_, :])

            for st_ in range(NT // 4):
                for j in range(4):
                    r0 = (st_ * 4 + j) * 128
                    self.norm_A(x2d[r0:r0 + 128, :], xts[j], nb)
                    pt = npt()
                    for k in range(8):
                        tr(pt[:, k * 128:(k + 1) * 128], nb["xn"][:, k * 128:(k + 1) * 128], ident[:])
                    cp("act", hT4[:, :, j * 128:(j + 1) * 128], pt[:].re("p (k t) -> p k t", k=8))
                for c in range(22):
                    pg, pu = nps(), nps()
                    for k in range(8):
                        mm(pg[:, :], wg_b[:, k, c * 128:(c + 1) * 128], hT4[:, k, :], st=(k == 0), sp=(k == 7))
                    for k in range(8):
                        mm(pu[:, :], wu_b[:, k, c * 128:(c + 1) * 128], hT4[:, k, :], st=(k == 0), sp=(k == 7))
                    sgc = sgs[c % 2]
                    act(sgc[:], pg[:, :], AF.Silu)
                    tt("dve", aT4[:, c, :], sgc[:], pu[:, :], ALU.mult)
                for j in range(4):
                    r0 = (st_ * 4 + j) * 128
                    for g in range(2):
                        ps = nps()
                        for c in range(22):
                            mm(ps[:, :], aT4[:, c, j * 128:(j + 1) * 128], wd_b[:, c, g * 512:(g + 1) * 512], st=(c == 0), sp=(c == 21))
                        tt("dve", x3t[:, g * 512:(g + 1) * 512], xts[j][:, g * 512:(g + 1) * 512], ps[:, :], ALU.add)
                    final_norm(y_p[r0:r0 + 128, :], 128)
            tiles = []
            if self.sample:
                tiles.append((self.S["x2s"][:, :], self.S["y_s"], 16))
            for t_, (src_, dst_, rows_) in enumerate(tiles):
                xt = xts[t_ % 2]
                self.norm_T(src_, xt, nb, hT, ident, npt, rows=rows_)
                for c0 in range(0, 22, 4):
                    n = min(4, 22 - c0)
                    pg = nps()
                    pu = nps()
                    for cc in range(n):
                        c = c0 + cc
                        for k in range(8):
                            mm(pg[:, cc * 128:(cc + 1) * 128], wg_b[:, k, c * 128:(c + 1) * 128], hT[:, k, :], st=(k == 0), sp=(k == 7))
                        for k in range(8):
                            mm(pu[:, cc * 128:(cc + 1) * 128], wu_b[:, k, c * 128:(c + 1) * 128], hT[:, k, :], st=(k == 0), sp=(k == 7))
                    act(sg[:, 0:n * 128], pg[:, 0:n * 128], AF.Silu)
                    tt("dve", aT[:, c0:c0 + n, :].re("p c t -> p (c t)"), sg[:, 0:n * 128], pu[:, 0:n * 128], ALU.mult)
                for g in range(2):
                    ps = nps()
                    for c in range(22):
                        mm(ps[:, :], aT[:, c, :], wd_b[:, c, g * 512:(g + 1) * 512], st=(c == 0), sp=(c == 21))
                    tt("dve", x3t[:, g * 512:(g + 1) * 512], xt[:, g * 512:(g + 1) * 512], ps[:, :], ALU.add)
                ms("dve", nb["ss"][:], 0.0)
                act(nb["junk"][:], x3t[:], AF.Square, acc=nb["ss"][:])
                self.rstd(nb["rs"][:], nb["ss"][:], 1.0 / D)
                stt("dve", x3t[:], x3t[:], nb["rs"][:, 0:1], gfin[:], ALU.mult, ALU.mult)
                dma(dst_, x3t[0:rows_, :])
            fw.barrier()


def sample_io(self):
    din, dout = self.din, self.dout
    S = {"xs": din("xs", [16, D]), "ptab": din("ptab", [4, 128], I32)}
    for n in ("pool_kc", "pool_vc", "pool_ks", "pool_vs"):
        S[n] = din(n, [5120, 16384])
    S["stk"] = din("stk", [4, 512, 128])
    S["stv"] = din("stv", [4, 512, 128])
    S["sconv"] = din("sconv", [4, 3, 512])
    S["sC"] = din("sC", [4, 4, 128, 128])
    S["sn"] = din("sn", [4, 4, 128])
    S["sm"] = din("sm", [16])
    S["cmk"] = din("cmk", [4, 256, D])
    S["cmv"] = din("cmv", [4, 256, D])
    S["y_s"] = dout("y_s", [16, D])
    for n in ("s_kc", "s_vc", "s_ks", "s_vs"):
        S[n] = dout(n, [16, 128])
    S["s_kw"] = dout("s_kw", [4, 512, 128])
    S["s_vw"] = dout("s_vw", [4, 512, 128])
    S["s_C"] = dout("s_C", [4, 4, 128, 128])
    S["s_n"] = dout("s_n", [4, 4, 128])
    S["s_m"] = dout("s_m", [16])
    S["s_conv"] = dout("s_conv", [4, 3, 512])
    S["kcS_d"] = self.dscr("kcS_d", [4, 2, 64, 1024], BF16)
    S["vcS_d"] = self.dscr("vcS_d", [4, 128, 8, 2, 64], BF16)
    S["x1s"] = self.dscr("x1s", [16, D])
    S["x2s"] = self.dscr("x2s", [16, D])
    self.S = S
    return S


def load_cmp(self, s, kv, cmp_in, nps):
    fw = self.fw
    mm, tt, cp, dma, dmas = self.mm, self.tt, self.cp, self.dma, self.dmas
    pe, w1, b1, w2 = cmp_in[kv]
    w1b = fw.sb(s, [64, 32, 256], BF16, "Sw1b" + kv)
    with ExitStack() as t:
        w1s = [fw.sb(t, [64, 8, 256], F32, f"Sw1s{kv}{i}") for i in range(2)]
        for jb in range(4):
            st = w1s[jb % 2]
            dma(st[:], V(None, w1.a.rearrange("(j d) n -> d j n", d=64)[:, jb * 8:(jb + 1) * 8, :]))
            cp("dve", w1b[:, jb * 8:(jb + 1) * 8, :], st[:])
        fw.barrier()
    peT = fw.sb(s, [64, 32], F32, "SpeT" + kv)
    dmas(peT[:], V(None, pe.a.rearrange("j d -> d j")))
    peTb = fw.sb(s, [64, 32], BF16, "SpeTb" + kv)
    cp("dve", peTb[:], peT[:])
    b1c = fw.sb(s, [128, 2], F32, "Sb1c" + kv)
    dmas(b1c[:], V(None, b1.a.rearrange("(c p) -> p c", p=128)))
    w2s = fw.sb(s, [128, 2, 64], F32, "Sw2s" + kv)
    dma(w2s[:], V(None, w2.a.rearrange("(c p) n -> p c n", p=128)))
    w2b = fw.sb(s, [128, 2, 64], BF16, "Sw2b" + kv)
    cp("dve", w2b[:], w2s[:])
    cst = fw.sb(s, [128, 2], F32, "Scst" + kv)
    for hc in range(2):
        ps = nps()
        for j in range(32):
            mm(ps[:, 0:1], w1b[:, j, hc * 128:(hc + 1) * 128], peTb[:, j:j + 1], st=(j == 0), sp=(j == 31))
        tt("dve", cst[:, hc:hc + 1], ps[:, 0:1], b1c[:, hc:hc + 1], ALU.add)
    return w1b, w2b, cst


def gather(self, dst, pool, idx, r0):
    self.fw.dma("pool", dst, pool, extra_reads=[idx.b],
                fn=lambda e: e.indirect_dma_start(out=dst.a, out_offset=None, in_=pool.a,
                                                  in_offset=bass.IndirectOffsetOnAxis(ap=idx.a, axis=0),
                                                  element_offset=r0 * 128))


def sample_s0(self, L):
    fw, S = self.fw, self.S
    mm, tr, act, cp, ms, dma = self.mm, self.tr, self.act, self.cp, self.ms, self.dma
    ident, nps, npt, cmp_in = L["ident"], L["nps"], L["npt"], L["cmp_in"]
    with ExitStack() as s:
        idx = [fw.sb(s, [128, 1], I32, f"S0idx{b}") for b in range(4)]
        for b in range(4):
            dma(idx[b][:], V(None, S["ptab"].a[b].rearrange("(p o) -> p o", o=1)))
        cw_ = {kv: load_cmp(self, s, kv, cmp_in, nps) for kv in "kv"}
        srcS = fw.sb(s, [64, 2, 128, 129], BF16, "srcS")
        ms("pool", srcS[:, :, :, 128:129], 0.0)
        gchs = [fw.sb(s, [128, 4096], F32, f"gch{i}") for i in range(2)]
        gbf = fw.sb(s, [128, 32, 128], BF16, "gbf")
        chunks0 = [(b, kv, i) for b in range(4) for kv in "kv" for i in range(4)]

        def issue0(n):
            b_, kv_, i_ = chunks0[n]
            gather(self, gchs[n % 2][:], S["pool_" + kv_ + "c"], idx[b_][:, :], 32 * i_)

        issue0(0)
        n0 = 0
        gT = fw.sb(s, [128, 2, 1024], BF16, "SgT")
        ko = fw.sb(s, [64, 1024], BF16, "Sko")
        vo = fw.sb(s, [128, 8, 64], BF16, "Svo")
        for b in range(4):
            for kv in "kv":
                w1b, w2b, cst = cw_[kv]
                pool = S["pool_" + kv + "c"]
                for i in range(4):
                    gch = gchs[n0 % 2]
                    if n0 + 1 < len(chunks0):
                        issue0(n0 + 1)
                    n0 += 1
                    cp("dve", gbf[:, 0:16, :].re("p r f -> p (r f)"), gch[:, 0:2048])
                    cp("act", gbf[:, 16:32, :].re("p r f -> p (r f)"), gch[:, 2048:4096])
                    for kvh in range(2):
                        for g8 in range(4):
                            pt = npt()
                            for r8 in range(8):
                                tr(pt[0:64, r8 * 128:(r8 + 1) * 128], gbf[:, 8 * g8 + r8, kvh * 64:(kvh + 1) * 64], ident[:])
                            cp("act" if g8 % 2 == 0 else "dve", srcS[:, kvh, 32 * i + 8 * g8:32 * i + 8 * g8 + 8, 0:128],
                               pt[0:64, :].re("p (r t) -> p r t", r=8))
                for kvh in range(2):
                    for hc in range(2):
                        wv = lambda j: w1b[:, j, hc * 128:(hc + 1) * 128]
                        for bank in range(2):
                            ps = nps()
                            o4 = ps[:, :].re("p (a t) -> p a t", a=4)
                            for j in range(32):
                                if j < 16:
                                    r0 = 64 * bank + j
                                    mm(o4, wv(j), srcS[:, kvh, r0:r0 + 49:16, 0:128], st=(j == 0), sp=False)
                                elif bank == 0:
                                    r0 = 16 + (j - 16)
                                    mm(o4, wv(j), srcS[:, kvh, r0:r0 + 49:16, 0:128], st=False, sp=(j == 31))
                                else:
                                    r0 = 80 + (j - 16)
                                    mm(o4[:, 0:3, :], wv(j), srcS[:, kvh, r0:r0 + 33:16, 0:128], st=False, sp=False)
                                    mm(ps[:, 384:512], wv(j), srcS[:, kvh, j - 16, 1:129], st=False, sp=(j == 31))
                            act(gT[:, hc, bank * 512:(bank + 1) * 512], ps[:, :], AF.Gelu_apprx_tanh, bias=cst[:, hc:hc + 1])
                    if kv == "k":
                        for bank in range(2):
                            ps = nps()
                            for hc in range(2):
                                mm(ps[0:64, :], w2b[:, hc, :], gT[:, hc, bank * 512:(bank + 1) * 512], st=(hc == 0), sp=(hc == 1))
                            cp("dve", ko[:, bank * 512:(bank + 1) * 512], ps[0:64, :])
                        dma(S["kcS_d"][b, kvh], ko[:])
                    else:
                        ps = nps()
                        for rb in range(8):
                            for hc in range(2):
                                mm(ps[:, rb * 64:(rb + 1) * 64], gT[:, hc, rb * 128:(rb + 1) * 128], w2b[:, hc, :],
                                   st=(hc == 0), sp=(hc == 1))
                        cp("dve", vo[:].re("p r d -> p (r d)"), ps[:, :])
                        dma(S["vcS_d"][b][:, :, kvh, :], vo[:])
        fw.barrier()


Builder.sample_io = sample_io

def sample_pass1(self, L):
    fw, S = self.fw, self.S
    mm, tr, act, ts, tt, stt, cp, ms, iota, dma, dmas = (self.mm, self.tr, self.act, self.ts, self.tt, self.stt,
                                                         self.cp, self.ms, self.iota, self.dma, self.dmas)
    win_b, wout_b, wqm_b, wkm_b = L["win_b"], L["wout_b"], L["wqm_b"], L["wkm_b"]
    ident, identf, nps, npt, psa = L["ident"], L["identf"], L["nps"], L["npt"], L["psa"]
    cw, cb, bgate, bif, tmpf = L["cw"], L["cb"], L["bgate"], L["bif"], L["tmpf"]
    put_row = self.put_row
    KSC = 128.0 ** -0.5
    with ExitStack() as s:
        xt = fw.sb(s, [128, D], F32, "xS")
        nb = self.norm_bufs(s, "S")
        hT = fw.sb(s, [128, 8, 128], BF16, "hTS")
        pkv = fw.sb(s, [128, 792], F32, "pkvS")
        gt = fw.sb(s, [128, 24], F32, "gtS")
        vnS = fw.sb(s, [16, 2, 2, 65], BF16, "vnS")
        qTs = fw.sb(s, [64, 8, 16], BF16, "qTs")
        mixin = fw.sb(s, [128, D], BF16, "mixinS")
        s2 = ExitStack()
        QS = fw.sb(s2, [68, 4, 2, 16], BF16, "QS")
        kcSb = fw.sb(s2, [68, 2, 8, 128], BF16, "kcSb")
        vcSb = fw.sb(s2, [128, 8, 2, 65], BF16, "vcSb")
        ksS = fw.sb(s2, [68, 2, 32, 128], BF16, "ksS")
        kwS = fw.sb(s2, [68, 2, 4, 128], BF16, "kwS")
        KnS = fw.sb(s2, [68, 2, 2, 16], BF16, "KnS")
        ms("pool", vcSb[:], 1.0)
        with ExitStack() as tmps:
            self.rowt = fw.sb(tmps, [1, 4096], F32, "rowtS")
            self.rowb = fw.sb(tmps, [1, 4096], BF16, "rowbS")
            sr = fw.sb(tmps, [1, 2, 4, 4], F32, "srS")
            for h in range(8):
                ms("pool", sr[0:1, h // 4, h % 4, :], 2.0 ** (-(h + 1)))
            qi = fw.sb(tmps, [1, 2, 4, 4], F32, "qiS")
            iota(qi[:].re("p k g q -> p (k g) q"), [[0, 8], [1, 4]], base=0, cm=0)
            rw = fw.sb(tmps, [1, 4, 2, 16], F32, "rwS")
            rwb = fw.sb(tmps, [1, 4, 2, 16], BF16, "rwbS")
            srv = sr[:].re("p k g q -> p k (g q)")
            for row in range(4):
                for i in range(4):
                    if row == 0:
                        ts("pool", rw[0:1, i], srv, -1.0, None, op0=ALU.mult)
                    elif row == 1:
                        cp("pool", rw[0:1, i], srv)
                    elif row == 2:
                        tt("pool", rw[0:1, i], srv, qi[:].re("p k g q -> p k (g q)"), ALU.mult)
                        ts("pool", rw[0:1, i], rw[0:1, i], -1.0, None, op0=ALU.mult)
                    else:
                        ts("pool", rw[0:1, i], srv, 32.0 * i, None, op0=ALU.mult)
                cp("pool", rwb[:], rw[:])
                dma(QS[64 + row:65 + row], rwb[:])
            for kvh in range(2):
                put_row(kcSb[64:65, kvh].re("p r t -> p (r t)"), [[0, 8], [-128, 128]], 16384, 1024)
                put_row(kcSb[65:66, kvh].re("p r t -> p (r t)"), [[16, 8], [0, 128]], 31, 1024)
                put_row(kcSb[66:67, kvh].re("p r t -> p (r t)"), None, 0, 1024, const=1.0)
                put_row(kcSb[67:68, kvh].re("p r t -> p (r t)"), None, 0, 1024, const=0.0)
                put_row(ksS[64:65, kvh].re("p r t -> p (r t)"), [[0, 32], [-128, 128]], 16384, 4096)
                put_row(ksS[65:66, kvh].re("p r t -> p (r t)"), [[1, 32], [0, 128]], 0, 4096)
                put_row(ksS[66:67, kvh].re("p r t -> p (r t)"), None, 0, 4096, const=1.0)
                put_row(ksS[67:68, kvh].re("p r t -> p (r t)"), None, 0, 4096, const=1.0)
                put_row(kwS[64:65, kvh].re("p r t -> p (r t)"), [[-128, 4], [0, 128]], 512, 512)
                put_row(kwS[65:66, kvh].re("p r t -> p (r t)"), [[0, 4], [1, 128]], 0, 512)
                put_row(kwS[66:67, kvh].re("p r t -> p (r t)"), None, 0, 512, const=1.0)
                put_row(kwS[67:68, kvh].re("p r t -> p (r t)"), None, 0, 512, const=0.0)
                for sw_ in range(2):
                    put_row(KnS[64:65, sw_, kvh], None, 0, 16, const=0.0)
                    put_row(KnS[65:66, sw_, kvh], [[0, 4], [1, 4]], 0, 16)
                    put_row(KnS[66:67, sw_, kvh], None, 0, 16, const=1.0)
                    put_row(KnS[67:68, sw_, kvh], None, 0, 16, const=0.0)
            fw.barrier()
        maskC7 = fw.sb(s2, [128, 1], F32, "maskC7")
        iota(tmpf[:, 0:1], [[0, 1]], base=0, cm=1)
        ts("pool", maskC7[:], tmpf[:, 0:1], 127.0, None, op0=ALU.is_lt)
        winm0 = fw.sb(s2, [128, 4], BF16, "winm0")
        iota(tmpf[:, 0:4], [[-1, 4]], base=0, cm=1)
        ts("pool", winm0[:], tmpf[:, 0:4], 0.0, None, op0=ALU.is_gt)
        newm = fw.sb(s2, [16, 4, 4], BF16, "newm")
        for b in range(4):
            iota(tmpf[0:16, 0:4], [[-1, 4]], base=-4 * b, cm=1)
            ts("pool", tmpf[0:16, 4:8], tmpf[0:16, 0:4], 0.0, None, op0=ALU.is_le)
            iota(tmpf[0:16, 8:12], [[0, 4]], base=-4 * b, cm=1)
            ts("pool", tmpf[0:16, 8:12], tmpf[0:16, 8:12], 0.0, None, op0=ALU.is_ge)
            tt("pool", newm[:, b, :], tmpf[0:16, 4:8], tmpf[0:16, 8:12], ALU.mult)
        mimpS = fw.sb(s2, [128, 8, 256], BF16, "mimpS")
        idx = [fw.sb(s2, [128, 1], I32, f"S1idx{b}") for b in range(4)]
        for b in range(4):
            dma(idx[b][:], V(None, S["ptab"].a[b].rearrange("(p o) -> p o", o=1)))
        with ExitStack() as tm:
            mtmp = fw.sb(tm, [128, 3, 256], F32, "mtmp")
            for rb in range(8):
                iota(mtmp[:, 0, :], [[-4, 256]], base=rb - 1, cm=8)
                stt("dve", mtmp[:, 1, :], mtmp[:, 0, :], -1.0, mtmp[:, 0, :], ALU.mult, ALU.max)
                ts("pool", mtmp[:, 0, :], mtmp[:, 1, :], 2.0, 0.5, op0=ALU.is_le, op1=ALU.mult)
                ts("pool", mtmp[:, 2, :], mtmp[:, 1, :], 1.0, 0.5, op0=ALU.is_le, op1=ALU.mult)
                tt("pool", mimpS[:, rb, :], mtmp[:, 0, :], mtmp[:, 2, :], ALU.add)
            fw.barrier()

        self.norm_T(S["xs"], xt, nb, hT, ident, npt, rows=16)
        psA, psB = nps(), nps()
        for k in range(8):
            mm(psA[:, 0:512], hT[:, k, :], win_b[:, k, 512:1024], st=(k == 0), sp=(k == 7))
        for k in range(8):
            mm(psB[:, 0:280], hT[:, k, :], win_b[:, k, 1024:1304], st=(k == 0), sp=(k == 7))
        cp("dve", pkv[:, 0:512], psA[:, 0:512])
        cp("act", pkv[:, 512:792], psB[:, 0:280])
        for i_, n_ in enumerate(("s_kc", "s_vc", "s_ks", "s_vs")):
            dma(S[n_], pkv[0:16, i_ * 128:(i_ + 1) * 128])
        tt("dve", gt[:], pkv[:, 768:792], bgate[:], ALU.add)
        self.sigm(gt[:], gt[:])
        ms("pool", vnS[:], 1.0)
        cp("dve", vnS[:, 0, :, 0:64], pkv[0:16, 384:512].re("p (h d) -> p h d", h=2))
        cp("dve", vnS[:, 1, :, 0:64], pkv[0:16, 640:768].re("p (h d) -> p h d", h=2))
        psQ = nps()
        for h in range(8):
            for k in range(8):
                mm(psQ[0:64, h * 16:(h + 1) * 16], win_b[:, k, 64 * h:64 * h + 64], hT[:, k, 0:16], st=(k == 0), sp=(k == 7))
        act(qTs[:].re("p h t -> p (h t)"), psQ[0:64, 0:128], AF.Copy, scale=0.125)
        psK = nps()
        for gi, c0 in enumerate((768, 832, 1024, 1088)):
            for k in range(8):
                mm(psK[0:64, gi * 16:(gi + 1) * 16], win_b[:, k, c0:c0 + 64], hT[:, k, 0:16], st=(k == 0), sp=(k == 7))
        cp("dve", KnS[0:64].re("p a k t -> p (a k t)"), psK[0:64, 0:64])

        gks = [fw.sb(s2, [128, 4096], F32, f"gk{i}") for i in range(2)]
        gvs = [fw.sb(s2, [128, 4096], F32, f"gv{i}") for i in range(2)]
        wks, wvs = gks[1][:, 0:512], gvs[1][:, 0:512]
        chunks1 = [(b_, i_) for b_ in range(4) for i_ in range(4)]

        def issue1(n):
            b_, i_ = chunks1[n]
            gather(self, gks[n % 2][:], S["pool_ks"], idx[b_][:, :], 32 * i_)
            gather(self, gvs[n % 2][:], S["pool_vs"], idx[b_][:, :], 32 * i_)

        issue1(0)
        n1 = 0
        kb = fw.sb(s2, [128, 32, 128], BF16, "kbS")
        vbp = fw.sb(s2, [128, 32, 2, 65], BF16, "vbp")
        ms("pool", vbp[:], 1.0)
        vwS = fw.sb(s2, [128, 4, 2, 65], BF16, "vwS")
        ms("pool", vwS[:], 1.0)
        ptc = fw.sb(s2, [128, 8, 16], BF16, "ptc")
        ptsb = fw.sb(s2, [128, 32, 16], BF16, "ptsb")
        ptw = fw.sb(s2, [128, 4, 16], BF16, "ptw")
        ptn = fw.sb(s2, [16, 16], BF16, "ptn")
        maskEO = [fw.sb(s2, [128, 2, 4, 4], BF16, f"maskEO{k}") for k in range(2)]
        obr = [[fw.sb(s2, [4, 4, 65], F32, f"obrS{k}{i}") for i in range(3)] for k in range(2)]
        imp = fw.sb(s2, [4, 256], F32, "impS")
        imp2 = fw.sb(s2, [4, 256], F32, "imp2S")
        mx1 = fw.sb(s2, [4, 8], F32, "mx1S")
        mx2 = fw.sb(s2, [4, 8], F32, "mx2S")
        sel01 = imp2
        selT = fw.sb(s2, [128, 8], F32, "selTS")
        gtb = fw.sb(s2, [4, 24], F32, "gtb")
        rd = fw.sb(s2, [4, 3, 4], F32, "rdS")
        sc3 = fw.sb(s2, [4, 3, 4], F32, "sc3S")
        onsa = fw.sb(s2, [4, 4, 64], F32, "onsaS")
        otmp = fw.sb(s2, [4, 4, 64], F32, "otmpS")
        ss4 = fw.sb(s2, [4, 4], F32, "ss4S")
        rs4 = fw.sb(s2, [4, 4], F32, "rs4S")
        onb = fw.sb(s2, [4, 512], BF16, "onbS")
        ms("pool", mixin[:], 0.0)
        for b in range(4):
            cp("dve", QS[0:64].re("p i k (g q) -> p i (k g) q", g=4),
               qTs[:, :, 4 * b:4 * b + 4].un(1).bc([64, 4, 8, 4]))
            dma(kcSb[0:64].re("p k r t -> p k (r t)"), V(S["kcS_d"], S["kcS_d"][b].a.rearrange("k d n -> d k n")))
            dma(vcSb[:].re("p r k d -> p (r k) d")[:, :, 0:64], V(S["vcS_d"], S["vcS_d"][b].a.rearrange("p r k d -> p (r k) d")))
            dma(gtb[:], gt[4 * b:4 * b + 4, :])
            for kvh in range(2):
                Sc = nps()
                for rb in range(8):
                    mm(Sc[:, rb * 16:(rb + 1) * 16], kcSb[:, kvh, rb, :], QS[:, 0, kvh, :])
                act(ptc[:].re("p r c -> p (r c)"), Sc[:, 0:128], AF.Exp)
                ts("dve", ptc[:, 7, :], ptc[:, 7, :], maskC7[:, 0:1], None, op0=ALU.mult)
                accC = psa[0]
                impP = [nps(), nps()]
                for rb in range(8):
                    for g in range(4):
                        mm(accC[0:4, g * 65:(g + 1) * 65], ptc[:, rb, 4 * g:4 * g + 4], vcSb[:, rb, kvh, :],
                           st=(rb == 0), sp=(rb == 7))
                        mm(impP[g // 2][0:4, (g % 2) * 256:(g % 2) * 256 + 256], ptc[:, rb, 4 * g:4 * g + 4],
                           mimpS[:, rb, :], st=(rb == 0), sp=(rb == 7))
                cp("dve", obr[kvh][0][:].re("p g d -> p (g d)"), accC[0:4, 0:260])
                ts("dve", rd[:, 0, :], obr[kvh][0][:, :, 64], 1e-30, None, op0=ALU.max)
                self.recip(rd[:, 0, :], rd[:, 0, :])
                ts("dve", imp[:], impP[0][0:4, 0:256], rd[:, 0, 0:1], None, op0=ALU.mult)
                for g in range(1, 4):
                    stt("dve", imp[:], impP[g // 2][0:4, (g % 2) * 256:(g % 2) * 256 + 256], rd[:, 0, g:g + 1], imp[:],
                        ALU.mult, ALU.add)
                ms("dve", imp[:, 0:1], 3e9)
                ms("dve", imp[:, 255:256], 1e9)
                fw.op("dve", lambda e: e.max(out=mx1[:].a, in_=imp[:].a), reads=[imp], writes=[mx1])
                fw.op("dve", lambda e: e.match_replace(out=imp2[:].a, in_to_replace=mx1[:].a, in_values=imp[:].a,
                                                       imm_value=-1e30), reads=[imp, mx1], writes=[imp2])
                fw.op("dve", lambda e: e.max(out=mx2[:].a, in_=imp2[:].a), reads=[imp2], writes=[mx2])
                ts("dve", sel01[:], imp[:], mx2[:, 6:7], None, op0=ALU.is_ge)
                psT = nps()
                tr(psT[:, 0:4], sel01[0:4, 0:256:2], identf[0:4, 0:4])
                tr(psT[:, 4:8], sel01[0:4, 1:256:2], identf[0:4, 0:4])
                cp("dve", selT[:], psT[:, 0:8])
                cp("dve", maskEO[kvh][:], selT[:].re("p (e q) -> p e q", e=2).un(2).bc([128, 2, 4, 4]))
            accS = [psa[0], psa[1]]
            for i in range(4):
                gk, gv = gks[n1 % 2], gvs[n1 % 2]
                if n1 + 1 < len(chunks1):
                    issue1(n1 + 1)
                n1 += 1
                cp("dve", kb[:, 0:16, :].re("p r f -> p (r f)"), gk[:, 0:2048])
                cp("act", kb[:, 16:32, :].re("p r f -> p (r f)"), gk[:, 2048:4096])
                cp("dve", vbp[:, 0:16, :, 0:64], gv[:, 0:2048].re("p (r h d) -> p r h d", r=16, h=2))
                cp("act", vbp[:, 16:32, :, 0:64], gv[:, 2048:4096].re("p (r h d) -> p r h d", r=16, h=2))
                for kvh in range(2):
                    for g8 in range(4):
                        pt = npt()
                        for r8 in range(8):
                            tr(pt[0:64, r8 * 128:(r8 + 1) * 128], kb[:, 8 * g8 + r8, kvh * 64:(kvh + 1) * 64], ident[:])
                        cp("act" if g8 % 2 == 0 else "dve", ksS[0:64, kvh, 8 * g8:8 * g8 + 8, :],
                           pt[0:64, :].re("p (r t) -> p r t", r=8))
                for kvh in range(2):
                    Ss = nps()
                    for r_ in range(32):
                        mm(Ss[:, r_ * 16:(r_ + 1) * 16], ksS[:, kvh, r_, :], QS[:, i, kvh, :])
                    act(ptsb[:].re("p r c -> p (r c)"), Ss[:, :], AF.Exp)
                    tt("dve", ptsb[:].re("p r (g q) -> p r g q", g=4), ptsb[:].re("p r (g q) -> p r g q", g=4),
                       maskEO[kvh][:, i // 2].un(1).bc([128, 32, 4, 4]), ALU.mult)
                    for r_ in range(32):
                        for g in range(4):
                            mm(accS[kvh][0:4, g * 65:(g + 1) * 65], ptsb[:, r_, 4 * g:4 * g + 4], vbp[:, r_, kvh, :],
                               st=(i == 0 and r_ == 0), sp=False)
            for kvh in range(2):
                Sn = nps()
                mm(Sn[0:16, 0:16], KnS[:, 0, kvh, :], QS[:, 0, kvh, :])
                act(ptn[:], Sn[0:16, 0:16], AF.Exp)
                tt("dve", ptn[:].re("p (g q) -> p g q", g=4), ptn[:].re("p (g q) -> p g q", g=4),
                   newm[:, b, :].un(1).bc([16, 4, 4]), ALU.mult)
                for g in range(4):
                    mm(accS[kvh][0:4, g * 65:(g + 1) * 65], ptn[:, 4 * g:4 * g + 4], vnS[:, 0, kvh, :], st=False, sp=True)
                cp("dve", obr[kvh][1][:].re("p g d -> p (g d)"), accS[kvh][0:4, 0:260])
            dma(wks.re("p (a f) -> p a f", a=4), V(None, S["stk"].a[b].rearrange("(a p) f -> p a f", p=128)))
            dma(wvs.re("p (a f) -> p a f", a=4), V(None, S["stv"].a[b].rearrange("(a p) f -> p a f", p=128)))
            cp("dve", kb[:, 0:4, :].re("p r f -> p (r f)"), wks)
            cp("dve", vwS[:, :, :, 0:64], wvs.re("p (a h d) -> p a h d", a=4, h=2))
            pt = npt()
            for kvh in range(2):
                for a in range(4):
                    tr(pt[0:64, (kvh * 4 + a) * 128:(kvh * 4 + a + 1) * 128], kb[:, a, kvh * 64:(kvh + 1) * 64], ident[:])
            cp("act", kwS[0:64].re("p k a t -> p (k a t)"), pt[0:64, :])
            accW = [psa[0], psa[1]]
            for kvh in range(2):
                Sw = nps()
                for a in range(4):
                    mm(Sw[:, a * 16:(a + 1) * 16], kwS[:, kvh, a, :], QS[:, 0, kvh, :])
                act(ptw[:].re("p a c -> p (a c)"), Sw[:, 0:64], AF.Exp)
                tt("dve", ptw[:, 0, :].re("p (g q) -> p g q", g=4), ptw[:, 0, :].re("p (g q) -> p g q", g=4),
                   winm0[:].un(1).bc([128, 4, 4]), ALU.mult)
                for a in range(4):
                    for g in range(4):
                        mm(accW[kvh][0:4, g * 65:(g + 1) * 65], ptw[:, a, 4 * g:4 * g + 4], vwS[:, a, kvh, :],
                           st=(a == 0), sp=False)
                Sn = nps()
                mm(Sn[0:16, 0:16], KnS[:, 1, kvh, :], QS[:, 0, kvh, :])
                act(ptn[:], Sn[0:16, 0:16], AF.Exp)
                tt("dve", ptn[:].re("p (g q) -> p g q", g=4), ptn[:].re("p (g q) -> p g q", g=4),
                   newm[:, b, :].un(1).bc([16, 4, 4]), ALU.mult)
                for g in range(4):
                    mm(accW[kvh][0:4, g * 65:(g + 1) * 65], ptn[:, 4 * g:4 * g + 4], vnS[:, 1, kvh, :], st=False, sp=True)
                cp("dve", obr[kvh][2][:].re("p g d -> p (g d)"), accW[kvh][0:4, 0:260])
            for nm_, src_, c0 in (("s_kw", "stk", 512), ("s_vw", "stv", 640)):
                dma(V(None, S[nm_].a[b, 0:508, :]), V(None, S[src_].a[b, 4:512, :]))
                dma(V(None, S[nm_].a[b, 508:512, :]), pkv[4 * b:4 * b + 4, c0:c0 + 128])
            for kvh in range(2):
                for br in range(1, 3):
                    ts("dve", rd[:, br, :], obr[kvh][br][:, :, 64], 1e-30, None, op0=ALU.max)
                    self.recip(rd[:, br, :], rd[:, br, :])
                ts("dve", rd[:, 0, :], obr[kvh][0][:, :, 64], 1e-30, None, op0=ALU.max)
                self.recip(rd[:, 0, :], rd[:, 0, :])
                gvw = gtb[:, 12 * kvh:12 * kvh + 12].re("p (g b) -> p b g", b=3)
                tt("dve", sc3[:], rd[:], gvw, ALU.mult)
                tt("dve", onsa[:], obr[kvh][0][:, :, 0:64], sc3[:, 0, :].un(2).bc([4, 4, 64]), ALU.mult)
                for br in (1, 2):
                    tt("dve", otmp[:], obr[kvh][br][:, :, 0:64], sc3[:, br, :].un(2).bc([4, 4, 64]), ALU.mult)
                    tt("dve", onsa[:], onsa[:], otmp[:], ALU.add)
                tt("dve", otmp[:], onsa[:], onsa[:], ALU.mult)
                self.rsum(ss4[:], otmp[:])
                self.rstd(rs4[:], ss4[:], 1.0 / 64)
                tt("dve", onb[:, 256 * kvh:256 * kvh + 256].re("p (g d) -> p g d", g=4), onsa[:],
                   rs4[:].un(2).bc([4, 4, 64]), ALU.mult)
            dma(mixin[4 * b:4 * b + 4, 0:512], onb[:])
        fw.barrier()
        s2.close()
        self.sample_mlstm(s, L, hT, mixin)
        mT = fw.sb(s, [128, 8, 128], BF16, "mTS")
        x1t = fw.sb(s, [128, D], F32, "x1tS")
        ptm = npt()
        for k in range(8):
            tr(ptm[:, k * 128:(k + 1) * 128], mixin[:, k * 128:(k + 1) * 128], ident[:])
        cp("act", mT[:].re("p k t -> p (k t)"), ptm[:])
        for g in range(2):
            ps = nps()
            for k in range(8):
                mm(ps[:, :], mT[:, k, :], wout_b[:, k, g * 512:(g + 1) * 512], st=(k == 0), sp=(k == 7))
            tt("dve", x1t[:, g * 512:(g + 1) * 512], xt[:, g * 512:(g + 1) * 512], ps[:, :], ALU.add)
        dma(S["x1s"][:, :], x1t[0:16, :])
        fw.barrier()


Builder.sample_pass1 = sample_pass1


def sample_mlstm(self, s, L, hT, mixin):
    fw, S = self.fw, self.S
    mm, tr, act, ts, tt, stt, cp, ms, iota, dma, dmas = (self.mm, self.tr, self.act, self.ts, self.tt, self.stt,
                                                         self.cp, self.ms, self.iota, self.dma, self.dmas)
    win_b, wqm_b, wkm_b = L["win_b"], L["wqm_b"], L["wkm_b"]
    identf, nps, cw, cb, bif, tmpf = L["identf"], L["nps"], L["cw"], L["cb"], L["bif"], L["tmpf"]
    KSC = 128.0 ** -0.5
    E = fw.sb(s, [4, 128], F32, "E4")
    iota(E[:], [[1, 128]], base=0, cm=-4)
    Eb = fw.sb(s, [4, 128], F32, "E4b")
    ts("pool", Eb[:], E[:], 0.0, None, op0=ALU.is_ge)
    ts("pool", E[:], E[:], 3.0, None, op0=ALU.is_le)
    tt("pool", E[:], E[:], Eb[:], ALU.mult)
    triS = fw.sb(s, [128, 128], F32, "triS")
    ps = nps()
    mm(ps[:, 0:128], E[:], E[:])
    tt("dve", triS[:], ps[:, 0:128], L["tri_le"][:], ALU.mult)
    bdS = fw.sb(s, [16, 16], BF16, "bdS")
    cp("dve", bdS[:], triS[0:16, 0:16])
    cselS = fw.sb(s, [128, 4, 128], F32, "cselS")
    d4 = fw.sb(s, [4, 4, 128], F32, "d4")
    cp("dve", d4[:], identf[0:4, 0:4].un(2).bc([4, 4, 128]))
    ps = nps()
    for b in range(4):
        mm(ps[:, b * 128:(b + 1) * 128], E[:], d4[:, b, :])
    cp("dve", cselS[:].re("p b m -> p (b m)"), ps[:, :])
    psV, psO, psG = nps(), nps(), nps()
    for k in range(8):
        mm(psV[:, 0:512], hT[:, k, :], win_b[:, k, 1816:2328], st=(k == 0), sp=(k == 7))
    for k in range(8):
        mm(psO[:, 0:512], hT[:, k, :], win_b[:, k, 2328:2840], st=(k == 0), sp=(k == 7))
    for k in range(8):
        mm(psG[:, 0:8], hT[:, k, :], win_b[:, k, 2840:2848], st=(k == 0), sp=(k == 7))
    gif = fw.sb(s, [128, 8], F32, "gifS")
    l1 = fw.sb(s, [128, 4], F32, "l1S")
    sigo = fw.sb(s, [128, 512], F32, "sigoS")
    tt("dve", gif[:], psG[:, 0:8], bif[:], ALU.add)
    act(l1[:], gif[:, 4:8], AF.Exp, scale=-1.0)
    act(l1[:], l1[:], AF.Ln, bias=1.0)
    self.sigm(sigo[:], psO[:, 0:512])
    psC = nps()
    mm(psC[:, 0:4], triS[:], l1[:])
    for b in range(4):
        mm(psC[:, 4 + 4 * b:8 + 4 * b], cselS[:, b, :], l1[:])
    gsb = fw.sb(s, [128, 20], F32, "gsbS")
    cp("dve", gsb[:], psC[:, 0:20])
    wl = fw.sb(s, [128, 4], F32, "wlS")
    ul = fw.sb(s, [128, 4], F32, "ulS")
    tmp4 = fw.sb(s, [128, 4], F32, "tmp4S")
    own = fw.sb(s, [128, 4], F32, "ownS")
    dec = fw.sb(s, [128, 4], F32, "decS")
    ebt = fw.sb(s, [128, 16], F32, "ebtS")
    act(wl[:], gsb[:, 0:4], AF.Exp, scale=-1.0)
    tt("dve", tmp4[:], gif[:, 0:4], gsb[:, 0:4], ALU.add)
    act(ul[:], tmp4[:], AF.Exp)
    act(ebt[:], gsb[:, 4:20], AF.Exp, scale=-1.0)
    ts("dve", own[:], gsb[:, 4:8], cselS[:, 0, 0:1], None, op0=ALU.mult)
    for b in range(1, 4):
        stt("dve", own[:], gsb[:, 4 + 4 * b:8 + 4 * b], cselS[:, b, 0:1], own[:], ALU.mult, ALU.add)
    tt("dve", dec[:], tmp4[:], own[:], ALU.subtract)
    vmu = fw.sb(s, [128, 4, 129], BF16, "vmuS")
    tt("dve", vmu[:, :, 0:128], psV[:, 0:512].re("p (h e) -> p h e", h=4), ul[:].un(2).bc([128, 4, 128]), ALU.mult)
    cp("dve", vmu[:, :, 128], ul[:])
    psT = nps()
    tr(psT[0:4, 0:128], dec[:], identf[:])
    mm(psT[0:4, 128:132], l1[:], cselS[:, :, 0])
    tsb = fw.sb(s, [4, 132], F32, "tsbS")
    cp("dve", tsb[:], psT[0:4, 0:132])
    Dm = fw.sb(s, [4, 4], F32, "DmS")
    self.rmax(Dm[:], tsb[:, 0:16].re("p (b i) -> p b i", b=4))
    R = fw.sb(s, [4, 4], F32, "RS")
    dmas(R[:], V(None, S["sm"].a.rearrange("(b h) -> h b", h=4)))
    tt("dve", R[:], R[:], tsb[:, 128:132], ALU.subtract)
    tt("dve", R[:], R[:], Dm[:], ALU.max)
    dmas(V(None, S["s_m"].a.rearrange("(b h) -> h b", h=4)), R[:])
    xcv = fw.sb(s, [128, 4, 4, 7], F32, "xcvS")
    for b in range(4):
        for ch in range(4):
            dmas(xcv[:, ch, b, 0:3], V(None, S["sconv"].a[b, :, ch * 128:(ch + 1) * 128].rearrange("j p -> p j")))
    psX = nps()
    for ch in range(4):
        for k in range(8):
            mm(psX[:, ch * 16:(ch + 1) * 16], win_b[:, k, 1304 + ch * 128:1432 + ch * 128], hT[:, k, 0:16],
               st=(k == 0), sp=(k == 7))
    cp("act", xcv[:, :, :, 3:7], psX[:, 0:64].re("p (c b i) -> p c b i", c=4, b=4))
    cacc = fw.sb(s, [128, 4, 16], F32, "caccS")
    for ch in range(4):
        cv = cacc[:, ch, :].re("p (b i) -> p b i", b=4)
        ts("dve", cv, xcv[:, ch, :, 0:4], cw[:, ch, 0:1], cb[:, ch:ch + 1], op0=ALU.mult, op1=ALU.add)
        for j in range(1, 4):
            stt("dve", cv, xcv[:, ch, :, j:j + 4], cw[:, ch, j:j + 1], cv, ALU.mult, ALU.add)
    xc = fw.sb(s, [128, 4, 16], BF16, "xcS")
    sgc = fw.sb(s, [128, 4, 16], F32, "sgcS")
    self.sigm(sgc[:], cacc[:])
    tt("dve", xc[:], cacc[:], sgc[:], ALU.mult)
    for b in range(4):
        for j in range(3):
            dmas(V(None, S["s_conv"].a[b, j].rearrange("(c p) -> p c", p=128)), xcv[:, :, b, 4 + j])
    qmT = fw.sb(s, [128, 4, 16], BF16, "qmTS")
    kmT = fw.sb(s, [128, 4, 16], BF16, "kmTS")
    qmS = [fw.sb(s, [128, 4, 16], BF16, f"qmSS{b}") for b in range(4)]
    kmS = [fw.sb(s, [16, 4, 128], BF16, f"kmSS{b}") for b in range(4)]
    psq = nps()
    for h in range(4):
        mm(psq[:, h * 16:(h + 1) * 16], wqm_b[:, h, :], xc[:, h, :])
    cp("act", qmT[:].re("p h t -> p (h t)"), psq[:, 0:64])
    for b in range(4):
        ms("pool", qmS[b][:], 0.0)
        cp("dve", qmS[b][:, :, 4 * b:4 * b + 4], psq[:, 0:64].re("p (h t) -> p h t", h=4)[:, :, 4 * b:4 * b + 4])
    psk = nps()
    for h in range(4):
        mm(psk[:, h * 16:(h + 1) * 16], wkm_b[:, h, :], xc[:, h, :])
    act(kmT[:].re("p h t -> p (h t)"), psk[:, 0:64], AF.Copy, scale=KSC)
    pskt = nps()
    for h in range(4):
        mm(pskt[0:16, h * 128:(h + 1) * 128], xc[:, h, :], wkm_b[:, h, :])
    for b in range(4):
        ts("dve", kmS[b][:].re("p h t -> p (h t)"), pskt[0:16, :], cselS[0:16, b, 0:1], KSC, op0=ALU.mult, op1=ALU.mult)
    psqk = nps()
    for h in range(4):
        mm(psqk[0:16, h * 16:(h + 1) * 16], kmT[:, h, :], qmT[:, h, :])
    mqk = fw.sb(s, [16, 4, 16], BF16, "mqkS")
    tt("dve", mqk[:], psqk[0:16, 0:64].re("p (h t) -> p h t", h=4), bdS[:].un(1).bc([16, 4, 16]), ALU.mult)
    em0 = fw.sb(s, [128, 16], F32, "em0")
    dma(em0[:], V(None, S["sm"].a.partition_broadcast(128)))
    act(em0[:], em0[:], AF.Exp)
    Sf = [fw.sb(s, [128, 4, 129], F32, f"SfS{b}") for b in range(4)]
    Sb0 = [fw.sb(s, [128, 4, 129], BF16, f"Sb0S{b}") for b in range(4)]
    cst_ = [fw.sb(s, [128, 128], F32, f"c0st{i}") for i in range(2)]
    dS = fw.sb(s, [128, 4, 129], F32, "dSS")
    for b in range(4):
        for h in range(4):
            st = cst_[h % 2]
            dma(st[:], V(None, S["sC"].a[b, h]))
            ps = nps()
            tr(ps[:, 0:128], st[:], identf[:])
            cp("dve", Sf[b][:, h, 0:128], ps[:, 0:128])
        dmas(Sf[b][:, :, 128], V(None, S["sn"].a[b].rearrange("h d -> d h")))
        tt("dve", Sf[b][:], Sf[b][:], em0[:, 4 * b:4 * b + 4].un(2).bc([128, 4, 129]), ALU.mult)
        cp("pool", Sb0[b][:], Sf[b][:])
        pd = [nps(), nps()]
        for h in range(4):
            mm(pd[h // 2][:, (h % 2) * 129:(h % 2) * 129 + 129], kmS[b][:, h, :], vmu[0:16, h, :])
        tt("dve", Sf[b][:], Sf[b][:], ebt[:, 4 * b:4 * b + 4].un(2).bc([128, 4, 129]), ALU.mult)
        for hh in range(2):
            tt("dve", dS[:, 2 * hh:2 * hh + 2, :], pd[hh][:, 0:258].re("p (h e) -> p h e", h=2),
               ebt[:, 4 * b + 2 * hh:4 * b + 2 * hh + 2].un(2).bc([128, 2, 129]), ALU.mult)
        tt("dve", Sf[b][:], Sf[b][:], dS[:], ALU.add)
    pa = [nps(), nps()]
    for h in range(4):
        o_ = pa[h // 2][0:16, (h % 2) * 129:(h % 2) * 129 + 129]
        mm(o_, mqk[:, h, :], vmu[0:16, h, :], st=True, sp=False)
        for b in range(4):
            mm(o_, qmS[b][:, h, :], Sb0[b][:, h, :], st=False, sp=(b == 3))
    dn = fw.sb(s, [16, 4], F32, "dnS")
    t4 = fw.sb(s, [16, 4], F32, "t4S")
    hout = fw.sb(s, [16, 4, 128], F32, "houtS")
    hsq = fw.sb(s, [16, 4, 128], F32, "hsqS")
    ss4 = fw.sb(s, [16, 4], F32, "ss4m")
    rs4 = fw.sb(s, [16, 4], F32, "rs4m")
    for hh in range(2):
        av = pa[hh][0:16, 0:258].re("p (h e) -> p h e", h=2)
        tt("dve", dn[:, 2 * hh:2 * hh + 2], av[:, :, 128], wl[0:16, 2 * hh:2 * hh + 2], ALU.mult)
    stt("dve", t4[:], dn[:], -1.0, dn[:], ALU.mult, ALU.max)
    ts("dve", t4[:], t4[:], 1.0, None, op0=ALU.max)
    self.recip(t4[:], t4[:])
    tt("dve", t4[:], t4[:], wl[0:16, :], ALU.mult)
    for hh in range(2):
        av = pa[hh][0:16, 0:258].re("p (h e) -> p h e", h=2)
        tt("dve", hout[:, 2 * hh:2 * hh + 2, :], av[:, :, 0:128], t4[:, 2 * hh:2 * hh + 2].un(2).bc([16, 2, 128]), ALU.mult)
    tt("dve", hsq[:], hout[:], hout[:], ALU.mult)
    self.rsum(ss4[:], hsq[:])
    self.rstd(rs4[:], ss4[:], 1.0 / 128)
    tt("dve", hout[:], hout[:], rs4[:].un(2).bc([16, 4, 128]), ALU.mult)
    tt("dve", mixin[0:16, 512:1024], hout[:].re("p h e -> p (h e)"), sigo[0:16, :], ALU.mult)
    Rd = fw.sb(s, [4, 4, 4], F32, "RdS")
    tt("dve", Rd[:], R[:].un(2).bc([4, 4, 4]), identf[0:4, 0:4].un(1).bc([4, 4, 4]), ALU.mult)
    ones4 = fw.sb(s, [4, 128], F32, "ones4S")
    ms("pool", ones4[:], 1.0)
    ps = nps()
    mm(ps[:, 0:16], ones4[:], Rd[:].re("p b h -> p (b h)"))
    esc = fw.sb(s, [128, 16], F32, "escS")
    act(esc[:], ps[:, 0:16], AF.Exp, scale=-1.0)
    for b in range(4):
        tt("dve", Sf[b][:], Sf[b][:], esc[:, 4 * b:4 * b + 4].un(2).bc([128, 4, 129]), ALU.mult)
        dmas(V(None, S["s_n"].a[b].rearrange("h d -> d h")), Sf[b][:, :, 128])
        for h in range(4):
            ps = nps()
            tr(ps[:, 0:128], Sf[b][:, h, 0:128], identf[:])
            st = cst_[h % 2]
            cp("dve", st[:], ps[:, 0:128])
            dma(V(None, S["s_C"].a[b, h]), st[:])


Builder.sample_mlstm = sample_mlstm
Builder.sample_s0 = sample_s0

W_NAMES = ["w_in", "g_mix", "b_gate", "cmp_pe_k", "cmp_w1_k", "cmp_b1_k", "cmp_w2_k", "cmp_pe_v", "cmp_w1_v",
           "cmp_b1_v", "cmp_w2_v", "g_head_nsa", "conv_w", "conv_b", "w_qm", "w_km", "b_i", "b_f", "g_head_m",
           "w_out", "g_xa", "g_mem", "w_xq", "w_xk", "w_xv", "w_xo", "g_ffn", "w_gate", "w_up", "w_down", "g_final"]


def build_program(NT=32, sample=True, debug=False):
    nc = bass.Bass("TRN2", target_bir_lowering=False)
    b = Builder(nc, NT=NT, sample=sample, debug=debug)
    b.build()
    return nc, b


def core_inputs(inp, c, b, NT=32):
    f = lambda a: np.ascontiguousarray(a, dtype=np.float32)
    T = NT * 128
    m = {"xp": f(inp["x_prompt"][c, :T]), "memp": f(inp["mem_prompt"][c])}
    for n in W_NAMES:
        a = np.asarray(inp[n])
        if n != "g_final":
            a = a[0]
        m[n] = f(a).reshape(b.io[n].shape)
    if b.sample:
        sl = slice(4 * c, 4 * c + 4)
        m["xs"] = f(inp["x_sample"][sl]).reshape(16, D)
        for n, k in (("pool_kc", "cache_k_cmp"), ("pool_vc", "cache_v_cmp"), ("pool_ks", "cache_k_slc"), ("pool_vs", "cache_v_slc")):
            m[n] = np.asarray(inp[k][0], dtype=np.float32).reshape(5120, 16384)
        m["ptab"] = np.ascontiguousarray(inp["page_table"][sl], dtype=np.int32)
        m["stk"] = f(inp["state_k_win"][0, sl]).reshape(4, 512, 128)
        m["stv"] = f(inp["state_v_win"][0, sl]).reshape(4, 512, 128)
        m["sconv"] = f(inp["state_conv"][0, sl])
        m["sC"] = f(inp["state_C"][0, sl])
        m["sn"] = f(inp["state_n"][0, sl])
        m["sm"] = f(inp["state_m"][0, sl]).reshape(16)
        m["cmk"] = f(inp["cache_mem_k"][0, sl]).reshape(4, 256, D)
        m["cmv"] = f(inp["cache_mem_v"][0, sl]).reshape(4, 256, D)
    return {k: v for k, v in m.items() if k in b.io}


_PROG = {}


def kernel(**inp):
    n = 8
    if "p" not in _PROG:
        _PROG["p"] = build_program()
    nc, b = _PROG["p"]
    in_maps = [core_inputs(inp, c, b) for c in range(n)]
    res = run_bass_kernel_spmd(nc, in_maps, core_ids=list(range(n)))
    R = res.results

    def st(name, shp, lead):
        a = np.stack([np.asarray(R[c][name], dtype=np.float32).reshape(shp) for c in range(n)])
        return np.ascontiguousarray(a.reshape(lead))

    outs = [st("y_p", (4096, D), (8, 4096, D)), st("y_s", (4, 4, D), (32, 4, D))]
    for nm in ("p_kc", "p_vc", "p_ks", "p_vs"):
        outs.append(st(nm, (4096, 2, 64), (1, 8, 4096, 2, 64)))
    for nm in ("p_kw", "p_vw"):
        outs.append(st(nm, (512, 2, 64), (1, 8, 512, 2, 64)))
    outs.append(st("p_C", (4, 128, 128), (1, 8, 4, 128, 128)))
    outs.append(st("p_n", (4, 128), (1, 8, 4, 128)))
    outs.append(st("p_m", (4,), (1, 8, 4)))
    outs.append(st("p_conv", (3, 512), (1, 8, 3, 512)))
    outs.append(st("p_mk", (256, 4, 256), (1, 8, 256, 4, 256)))
    outs.append(st("p_mv", (256, 4, 256), (1, 8, 256, 4, 256)))
    for nm in ("s_kc", "s_vc", "s_ks", "s_vs"):
        outs.append(st(nm, (4, 4, 2, 64), (1, 32, 4, 2, 64)))
    for nm in ("s_kw", "s_vw"):
        outs.append(st(nm, (4, 512, 2, 64), (1, 32, 512, 2, 64)))
    outs.append(st("s_C", (4, 4, 128, 128), (1, 32, 4, 128, 128)))
    outs.append(st("s_n", (4, 4, 128), (1, 32, 4, 128)))
    outs.append(st("s_m", (4, 4), (1, 32, 4)))
    outs.append(st("s_conv", (4, 3, 512), (1, 32, 3, 512)))
    return tuple(outs)
```

```python
import numpy as np
from contextlib import ExitStack
import concourse.bass as bass
import concourse.mybir as mybir
from concourse.bass_utils import run_bass_kernel_spmd

F32 = mybir.dt.float32
BF16 = mybir.dt.bfloat16
I32 = mybir.dt.int32
AF = mybir.ActivationFunctionType
ALU = mybir.AluOpType
AX = mybir.AxisListType

D = 1024
NEG = -30000.0
EPS = 1e-6
IN_COLS = 2848
DFF = 2816


class V:
    __slots__ = ("b", "a")

    def __init__(self, b, a):
        self.b = b
        self.a = a

    def __getitem__(self, k):
        return V(self.b, self.a[k])

    def re(self, p, **kw):
        return V(self.b, self.a.rearrange(p, **kw))

    def bc(self, shape):
        return V(self.b, self.a.to_broadcast(list(shape)))

    def un(self, ax):
        return V(self.b, self.a.unsqueeze(ax))


class Buf:
    __slots__ = ("t", "w", "r", "name", "psum", "fresh", "quads")

    def __init__(self, t, name="", psum=False):
        self.t = t
        self.w = None
        self.r = []
        self.name = name
        self.psum = psum
        self.fresh = True
        self.quads = set()

    def __getitem__(self, k):
        return V(self, self.t[k])


class DSem:
    def __init__(self, nc, name):
        self.sem = nc.alloc_semaphore(name)
        self.val = 0


class FW:
    ENG = ("pe", "act", "dve", "pool", "sp")

    def __init__(self, nc, n_dsem=10, same_engine_sync=True):
        self.nc = nc
        self.e = {"pe": nc.tensor, "act": nc.scalar, "dve": nc.vector, "pool": nc.gpsimd, "sp": nc.sync}
        self.gen = {k: 0 for k in self.ENG}
        self.sem = {k: nc.alloc_semaphore("S_" + k) for k in self.ENG}
        self.cnt = {k: 0 for k in self.ENG}
        self.seen = {k: {} for k in self.ENG}
        self.same = same_engine_sync
        self.dsems = {q: [DSem(nc, f"D{q}{i}") for i in range(n_dsem)] for q in ("sp", "pool", "act")}
        self.dnext = {q: 0 for q in self.dsems}
        self.nbuf = 0
        self.nins = 0

    def sb(self, stack, shape, dt=F32, name=None):
        self.nbuf += 1
        name = name or f"b{self.nbuf}"
        return Buf(stack.enter_context(self.nc.sbuf_tensor(name, list(shape), dt)), name)

    def ps(self, stack, shape, dt=F32, name=None):
        self.nbuf += 1
        name = name or f"p{self.nbuf}"
        return Buf(stack.enter_context(self.nc.psum_tensor(name, list(shape), dt)), name, psum=True)

    def _need(self, e, dep, waits):
        if dep is None:
            return
        kind, key, val, semh = dep
        if kind == "e" and key[0] == e and (not self.same or e == "pe"):
            return
        k = (kind, key if kind == "e" else id(key))
        if self.seen[e].get(k, 0) >= val:
            return
        cur = waits.get(k)
        if cur is None or cur[1] < val:
            waits[k] = (semh, val)

    def _emit_waits(self, e, reads, writes):
        waits = {}
        for b in reads:
            if b is not None:
                self._need(e, b.w, waits)
        for b in writes:
            if b is not None:
                self._need(e, b.w, waits)
                for d in b.r:
                    self._need(e, d, waits)
        eng = self.e[e]
        for k, (semh, val) in waits.items():
            eng.wait_ge(semh, val)
            self.seen[e][k] = val

    def op(self, e, fn, reads=(), writes=()):
        px = [b for b in reads if b is not None and b.psum]
        if px:
            reads = [b for b in reads if not (b is not None and b.psum)]
            writes = list(writes) + [b for b in px if b not in writes]
            if e != "pe":
                for b in px:
                    b.fresh = True
        self._emit_waits(e, reads, writes)
        ins = fn(self.e[e])
        if self.cnt[e] >= 50000:
            self.gen[e] += 1
            self.sem[e] = self.nc.alloc_semaphore(f"S_{e}_{self.gen[e]}")
            self.cnt[e] = 0
        self.cnt[e] += 1
        self.nins += 1
        ins.then_inc(self.sem[e], 1)
        dep = ("e", (e, self.gen[e]), self.cnt[e], self.sem[e])
        for b in reads:
            if b is not None:
                b.r.append(dep)
                if len(b.r) > 16:
                    b.r = self._compact(b.r)
        for b in writes:
            if b is not None:
                b.w = dep
                b.r = []
        return ins

    @staticmethod
    def _compact(lst):
        best = {}
        for d in lst:
            k = (d[0], d[1] if d[0] == "e" else id(d[1]))
            if k not in best or best[k][2] < d[2]:
                best[k] = d
        return list(best.values())

    def dma(self, q, o, i, fn=None, extra_reads=(), **kw):
        reads = [i.b] + list(extra_reads)
        writes = [o.b]
        self._emit_waits(q, reads, writes)
        ds = self.dsems[q][self.dnext[q]]
        self.dnext[q] = (self.dnext[q] + 1) % len(self.dsems[q])
        if ds.val > 0 and self.seen[q].get(("d", id(ds)), 0) < ds.val:
            self.e[q].wait_ge(ds.sem, ds.val)
            self.seen[q][("d", id(ds))] = ds.val
        if fn is None:
            ins = self.e[q].dma_start(out=o.a, in_=i.a, **kw)
        else:
            ins = fn(self.e[q])
        ds.val += 16
        self.nins += 1
        ins.then_inc(ds.sem, 16)
        dep = ("d", ds, ds.val, ds.sem)
        for b in reads:
            if b is not None:
                b.r.append(dep)
                if len(b.r) > 16:
                    b.r = self._compact(b.r)
        for b in writes:
            if b is not None:
                b.w = dep
                b.r = []
        return ins

    def barrier(self):
        for e in self.ENG:
            eng = self.e[e]
            for f in self.ENG:
                if f != e and self.cnt[f] > 0:
                    k = ("e", (f, self.gen[f]))
                    if self.seen[e].get(k, 0) < self.cnt[f]:
                        eng.wait_ge(self.sem[f], self.cnt[f])
                        self.seen[e][k] = self.cnt[f]
            for q in self.dsems:
                for ds in self.dsems[q]:
                    k = ("d", id(ds))
                    if ds.val > 0 and self.seen[e].get(k, 0) < ds.val:
                        eng.wait_ge(ds.sem, ds.val)
                        self.seen[e][k] = ds.val

    def finish(self):
        eng = self.e["sp"]
        for q in self.dsems:
            for ds in self.dsems[q]:
                if ds.val > 0:
                    eng.wait_ge(ds.sem, ds.val)


class RR:
    def __init__(self, items):
        self.items = list(items)
        self.i = 0

    def __call__(self):
        x = self.items[self.i]
        self.i = (self.i + 1) % len(self.items)
        return x


class Builder:
    def __init__(self, nc, NT=32, sample=True, debug=False):
        self.debug = debug
        self.nc = nc
        self.fw = FW(nc)
        self.NT = NT
        self.T = NT * 128
        self.sample = sample
        self.io = {}

    def din(self, name, shape, dt=F32):
        t = self.nc.dram_tensor(name, list(shape), dt, kind="ExternalInput").ap()
        self.io[name] = t
        return V(None, t)

    def dout(self, name, shape, dt=F32):
        t = self.nc.dram_tensor(name, list(shape), dt, kind="ExternalOutput").ap()
        self.io[name] = t
        return V(None, t)

    def dscr(self, name, shape, dt=F32):
        t = self.nc.dram_tensor(name, list(shape), dt, kind="ExternalOutput" if self.debug else "Internal").ap()
        if self.debug:
            self.io[name] = t
        return Buf(t, name)

    def dbg(self, name, v, dt=F32):
        if not self.debug:
            return
        o = self.dout("dbg_" + name, list(v.a.shape), dt)
        self.fw.dma("sp", o, v)

    def mm(self, o, l, r, st=True, sp=True):
        b = o.b
        p0 = o.a.base_partition() if hasattr(o.a, "base_partition") else 0
        q = set(range(p0 // 32, (p0 + o.a.shape[0] + 31) // 32))
        start = False
        if st:
            if b.fresh:
                start = True
                b.fresh = False
                b.quads = set(q)
            else:
                assert q <= b.quads, (b.name, q, b.quads)
        self.fw.op("pe", lambda e: e.matmul(o.a, lhsT=l.a, rhs=r.a, start=start, stop=sp, skip_group_check=True),
                   reads=[l.b, r.b], writes=[o.b])

    def tr(self, o, i, ident):
        self.fw.op("pe", lambda e: e.transpose(out=o.a, in_=i.a, identity=ident.a), reads=[i.b, ident.b], writes=[o.b])

    def act(self, o, i, f, scale=1.0, bias=0.0, acc=None):
        reads = [i.b]
        writes = [o.b]
        kw = {}
        if isinstance(bias, V):
            reads.append(bias.b)
            kw["bias"] = bias.a
        elif bias != 0.0:
            kw["bias"] = float(bias)
        if isinstance(scale, V):
            reads.append(scale.b)
            kw["scale"] = scale.a
        elif scale != 1.0:
            kw["scale"] = float(scale)
        if acc is not None:
            writes.append(acc.b)
            kw["accum_out"] = acc.a
        self.fw.op("act", lambda e: e.activation(out=o.a, in_=i.a, func=f, **kw), reads=reads, writes=writes)

    def ts(self, eng, o, i, s1, s2=None, op0=ALU.mult, op1=None):
        reads = [i.b]
        a1 = s1
        a2 = s2
        if isinstance(s1, V):
            reads.append(s1.b)
            a1 = s1.a
        if isinstance(s2, V):
            reads.append(s2.b)
            a2 = s2.a
        kw = {}
        if op1 is not None:
            kw["op1"] = op1
        self.fw.op(eng, lambda e: e.tensor_scalar(out=o.a, in0=i.a, scalar1=a1, scalar2=a2, op0=op0, **kw), reads=reads, writes=[o.b])

    def tt(self, eng, o, a, b, op):
        self.fw.op(eng, lambda e: e.tensor_tensor(out=o.a, in0=a.a, in1=b.a, op=op), reads=[a.b, b.b], writes=[o.b])

    def stt(self, eng, o, a, s, b, op0, op1):
        reads = [a.b, b.b]
        sa = s
        if isinstance(s, V):
            reads.append(s.b)
            sa = s.a
        self.fw.op(eng, lambda e: e.scalar_tensor_tensor(out=o.a, in0=a.a, scalar=sa, in1=b.a, op0=op0, op1=op1), reads=reads, writes=[o.b])

    def cp(self, eng, o, i):
        if eng == "act":
            self.fw.op("act", lambda e: e.copy(out=o.a, in_=i.a), reads=[i.b], writes=[o.b])
        else:
            self.fw.op(eng, lambda e: e.tensor_copy(out=o.a, in_=i.a), reads=[i.b], writes=[o.b])

    def ms(self, eng, o, val):
        self.fw.op(eng, lambda e: e.memset(o.a, val), writes=[o.b])

    def iota(self, o, pattern, base=0, cm=0):
        self.fw.op("pool", lambda e: e.iota(o.a, pattern=pattern, base=base, channel_multiplier=cm,
                                            allow_small_or_imprecise_dtypes=True), writes=[o.b])

    def sigm(self, o, i):
        self.act(o, i, AF.Exp, scale=-1.0)
        self.ts("dve", o, o, 1.0, None, op0=ALU.add)
        self.recip(o, o)

    def recip(self, o, i):
        self.fw.op("dve", lambda e: e.reciprocal(out=o.a, in_=i.a), reads=[i.b], writes=[o.b])

    def rsum(self, o, i):
        self.fw.op("dve", lambda e: e.reduce_sum(out=o.a, in_=i.a, axis=AX.X), reads=[i.b], writes=[o.b])

    def rmax(self, o, i):
        self.fw.op("dve", lambda e: e.reduce_max(out=o.a, in_=i.a, axis=AX.X), reads=[i.b], writes=[o.b])

    def dma(self, o, i, q="sp", **kw):
        self.fw.dma(q, o, i, **kw)

    def dmas(self, o, i, q="sp"):
        self.fw.dma(q, o, i, allow_slow_non_contiguous=True)

    def put_row(self, dst, pattern, base, n, const=None):
        rowt, rowb = self.rowt, self.rowb
        if const is None:
            rv = rowt[0:1, 0:n]
            if len(pattern) == 2:
                rv = rv.re("p (a b) -> p a b", b=pattern[1][1])
            self.iota(rv, pattern, base=base, cm=0)
        else:
            self.ms("pool", rowt[0:1, 0:n], const)
        self.cp("pool", rowb[0:1, 0:n], rowt[0:1, 0:n])
        self.dma(dst, rowb[0:1, 0:n])

    def rstd(self, o, ss, inv_n):
        self.act(o, ss, AF.Ln, scale=inv_n, bias=self.epsc[0:o.a.shape[0], :])
        self.act(o, o, AF.Exp, scale=-0.5)

    def build(self):
        nc, fw, NT, T = self.nc, self.fw, self.NT, self.T
        din, dout = self.din, self.dout
        mm, tr, act, ts, tt, stt, cp, ms, iota, dma, dmas = (self.mm, self.tr, self.act, self.ts, self.tt, self.stt,
                                                             self.cp, self.ms, self.iota, self.dma, self.dmas)
        xp = din("xp", [T, D])
        memp = din("memp", [256, D])
        w_in = din("w_in", [D, IN_COLS])
        g_mix = din("g_mix", [D])
        b_gate = din("b_gate", [24])
        cmp_in = {}
        for kv in "kv":
            cmp_in[kv] = (din(f"cmp_pe_{kv}", [32, 64]), din(f"cmp_w1_{kv}", [2048, 256]),
                          din(f"cmp_b1_{kv}", [256]), din(f"cmp_w2_{kv}", [256, 64]))
        g_head_nsa = din("g_head_nsa", [512])
        conv_w = din("conv_w", [4, 512])
        conv_b = din("conv_b", [512])
        w_qm = din("w_qm", [4, 128, 128])
        w_km = din("w_km", [4, 128, 128])
        b_i = din("b_i", [4])
        b_f = din("b_f", [4])
        g_head_m = din("g_head_m", [512])
        w_out = din("w_out", [D, D])
        g_xa = din("g_xa", [D])
        g_mem = din("g_mem", [D])
        w_xq = din("w_xq", [D, D])
        w_xk = din("w_xk", [D, D])
        w_xv = din("w_xv", [D, D])
        w_xo = din("w_xo", [D, D])
        g_ffn = din("g_ffn", [D])
        w_gate = din("w_gate", [D, DFF])
        w_up = din("w_up", [D, DFF])
        w_down = din("w_down", [DFF, D])
        g_final = din("g_final", [D])

        y_p = dout("y_p", [T, D])
        p_kv = {n: dout(n, [T, 128]) for n in ("p_kc", "p_vc", "p_ks", "p_vs")}
        WT = min(512, T)
        p_kw = dout("p_kw", [WT, 128])
        p_vw = dout("p_vw", [WT, 128])
        p_C = dout("p_C", [4, 128, 128])
        p_n = dout("p_n", [4, 128])
        p_m = dout("p_m", [4])
        p_conv = dout("p_conv", [3, 512])
        p_mk = dout("p_mk", [256, D])
        p_mv = dout("p_mv", [256, D])

        if self.sample:
            self.sample_io()
        x1d = self.dscr("x1_scr", [T, D])
        x2d = self.dscr("x2_scr", [T, D])

        top = ExitStack()
        with top:
            psf = [fw.ps(top, [128, 512], F32, f"psf{i}") for i in range(4)]
            pst = [fw.ps(top, [128, 1024], BF16, f"pst{i}") for i in range(1)]
            psa = [fw.ps(top, [128, 512], F32, f"psa{i}") for i in range(3)]
            nps = RR(psf)
            nps_s = RR(psf[0:3])
            nps_m = RR([psf[3], psa[2]])
            npt = RR(pst)

            ident = fw.sb(top, [128, 128], BF16, "ident")
            identf = fw.sb(top, [128, 128], F32, "identf")
            for idt in (ident, identf):
                ms("pool", idt[:], 0.0)
                fw.op("pool", lambda e, idt=idt: e.affine_select(out=idt[:].a, in_=idt[:].a, pattern=[[-1, 128]],
                                                                compare_op=ALU.not_equal, fill=1.0, base=0,
                                                                channel_multiplier=1), reads=[idt], writes=[idt])
            self.epsc = fw.sb(top, [128, 1], F32, "epsc")[:]
            ms("pool", self.epsc, EPS)
            if self.sample:
                self.sample_s0(locals())
            sw = ExitStack()
            tmpf = fw.sb(sw, [128, 512], F32, "tmpf")
            iota(tmpf[:].re("p (g r) -> p g r", g=4), [[0, 4], [-1, 128]], base=0, cm=1)
            bd01 = fw.sb(sw, [128, 128], BF16, "bd01")
            ts("pool", bd01[:], tmpf[:, 0:128], 0.0, None, op0=ALU.is_le)
            ms("pool", bd01[0:64, 64:128], 0.0)
            tri_le = fw.sb(sw, [128, 128], F32, "tri_le")
            ts("pool", tri_le[:], tmpf[:, 0:128], 0.0, None, op0=ALU.is_le)
            tri2 = fw.sb(sw, [128, 128], F32, "tri2")
            cp("pool", tri2[:], bd01[:])
            csel = fw.sb(sw, [128, 2, 128], F32, "csel")
            ms("pool", csel[:], 0.0)
            ms("pool", csel[0:64, 0, :], 1.0)
            ms("pool", csel[64:128, 1, :], 1.0)
            mimp = fw.sb(sw, [128, 2, 64], BF16, "mimp")
            for ct in range(2):
                iota(tmpf[:, 0:64], [[-4, 64]], base=ct * 128 - 1, cm=1)
                stt("dve", tmpf[:, 64:128], tmpf[:, 0:64], -1.0, tmpf[:, 0:64], ALU.mult, ALU.max)
                ts("pool", tmpf[:, 128:192], tmpf[:, 64:128], 2.0, 0.5, op0=ALU.is_le, op1=ALU.mult)
                ts("pool", tmpf[:, 192:256], tmpf[:, 64:128], 1.0, 0.5, op0=ALU.is_le, op1=ALU.mult)
                tt("pool", mimp[:, ct, :], tmpf[:, 128:192], tmpf[:, 192:256], ALU.add)
                ms("pool", mimp[:, ct, 63:64], 1.0)
            if True:
                win_b = fw.sb(sw, [128, 8, IN_COLS], BF16, "win_b")
                wout_b = fw.sb(sw, [128, 8, D], BF16, "wout_b")
                wqm_b = fw.sb(sw, [128, 4, 128], BF16, "wqm_b")
                wkm_b = fw.sb(sw, [128, 4, 128], BF16, "wkm_b")
                gcol = fw.sb(sw, [128, 16], F32, "gcol")
                dmas(gcol[:, 0:8], V(None, g_mix.a.rearrange("(k p) -> p k", p=128)))
                dmas(gcol[:, 8:12], V(None, g_head_nsa.a.rearrange("(k p) -> p k", p=128)))
                dmas(gcol[:, 12:16], V(None, g_head_m.a.rearrange("(k p) -> p k", p=128)))
                cw = fw.sb(sw, [128, 4, 4], F32, "cw")
                for j in range(4):
                    dmas(cw[:, :, j], V(None, conv_w.a[j].rearrange("(c p) -> p c", p=128)))
                cb = fw.sb(sw, [128, 4], F32, "cb")
                dmas(cb[:], V(None, conv_b.a.rearrange("(c p) -> p c", p=128)))
                bgate = fw.sb(sw, [128, 24], F32, "bgate")
                dma(bgate[:], V(None, b_gate.a.partition_broadcast(128)))
                bif = fw.sb(sw, [128, 8], F32, "bif")
                dma(bif[:, 0:4], V(None, b_i.a.partition_broadcast(128)))
                dma(bif[:, 4:8], V(None, b_f.a.partition_broadcast(128)))
                with ExitStack() as s0:
                    stg = [fw.sb(s0, [128, IN_COLS], F32, f"stg{i}") for i in range(2)]
                    for k in range(8):
                        st = stg[k % 2]
                        dma(st[:], w_in[k * 128:(k + 1) * 128, :])
                        if k % 2 == 0:
                            ts("dve", win_b[:, k, :], st[:], gcol[:, k:k + 1], None, op0=ALU.mult)
                        else:
                            act(win_b[:, k, :], st[:], AF.Copy, scale=gcol[:, k:k + 1])
                    for k in range(8):
                        st = stg[k % 2]
                        dma(st[:, 0:D], w_out[k * 128:(k + 1) * 128, :])
                        if k % 2 == 0:
                            ts("dve", wout_b[:, k, :], st[:, 0:D], gcol[:, 8 + k:9 + k], None, op0=ALU.mult)
                        else:
                            act(wout_b[:, k, :], st[:, 0:D], AF.Copy, scale=gcol[:, 8 + k:9 + k])
                    st = stg[0]
                    dma(st[:, 0:512].re("p (h e) -> p h e", h=4), V(None, w_qm.a.rearrange("h d e -> d h e")))
                    cp("dve", wqm_b[:], st[:, 0:512].re("p (h e) -> p h e", h=4))
                    st = stg[1]
                    dma(st[:, 0:512].re("p (h e) -> p h e", h=4), V(None, w_km.a.rearrange("h d e -> d h e")))
                    cp("dve", wkm_b[:], st[:, 0:512].re("p (h e) -> p h e", h=4))
                    fw.barrier()

                kcp = fw.sb(sw, [68, 2, 256], BF16, "kcp")
                vcp = fw.sb(sw, [128, 2, 2, 65], BF16, "vcp")
                ms("pool", vcp[:], 1.0)
                put_row = self.put_row
                with ExitStack() as tmps:
                    self.rowt = fw.sb(tmps, [1, 4096], F32, "rowt")
                    self.rowb = fw.sb(tmps, [1, 4096], BF16, "rowb")
                    for kvh in range(2):
                        put_row(kcp[64:65, kvh, :], [[128, 32], [0, 8]], 0, 256)
                        put_row(kcp[65:66, kvh, :], [[0, 32], [16, 8]], 31, 256)
                        put_row(kcp[66:67, kvh, :], None, 0, 256, const=1.0)
                        put_row(kcp[67:68, kvh, :], None, 0, 256, const=1.0)
                    fw.barrier()

                self.pass0_prompt(sw, xp, win_b, cmp_in, kcp, vcp, ident, nps, npt)
                self.dbg("kcp", kcp[:], BF16)
                self.dbg("vcp", vcp[:], BF16)
                self.pass1_prompt(sw, locals())
                if self.sample:
                    self.sample_pass1(locals())
            fw.barrier()
            sw.close()
            self.pass2(top, locals())
            fw.finish()

    def norm_T(self, src, xt, nb, hT, ident, npt, rows=128):
        self.norm_A(src, xt, nb, rows)
        self.norm_B(nb, hT, ident, npt)

    def norm_A(self, src, xt, nb, rows=128):
        if rows < 128:
            self.ms("pool", xt[:], 0.0)
        self.dma(xt[0:rows, :], src)
        self.ms("dve", nb["ss"][:], 0.0)
        self.act(nb["junk"][:], xt[:], AF.Square, acc=nb["ss"][:])
        self.rstd(nb["rs"][:], nb["ss"][:], 1.0 / D)
        self.ts("dve", nb["xn"][:], xt[:], nb["rs"][:, 0:1], None, op0=ALU.mult)

    def norm_B(self, nb, hT, ident, npt):
        pt = npt()
        for k in range(8):
            self.tr(pt[:, k * 128:(k + 1) * 128], nb["xn"][:, k * 128:(k + 1) * 128], ident[:])
        self.cp("act", hT[:].re("p k t -> p (k t)"), pt[:])

    def norm_bufs(self, s, tag):
        fw = self.fw
        xn = fw.sb(s, [128, D], BF16, "xn" + tag)
        return {"junk": xn, "ss": fw.sb(s, [128, 1], F32, "ss" + tag), "rs": fw.sb(s, [128, 1], F32, "rs" + tag), "xn": xn}

    def pass0_prompt(self, sw, xp, win_b, cmp_in, kcp, vcp, ident, nps, npt):
        fw, NT, T = self.fw, self.NT, self.T
        mm, act, tt, cp, ms, dma, dmas = self.mm, self.act, self.tt, self.cp, self.ms, self.dma, self.dmas
        NCB = T // 16
        with ExitStack() as s:
            srcT = {kv: fw.sb(s, [64, 2, 16, NCB + 1], BF16, "srcT" + kv) for kv in "kv"}
            for kv in "kv":
                ms("pool", srcT[kv][:, :, :, NCB:NCB + 1], 0.0)
            ms("pool", kcp[0:64, :, :], 0.0)
            xts = [fw.sb(s, [128, D], F32, f"x0_{i}") for i in range(2)]
            nb = self.norm_bufs(s, "0")
            hT = fw.sb(s, [128, 8, 128], BF16, "hT0")
            for t_ in range(NT):
                xt = xts[t_ % 2]
                self.norm_T(xp[t_ * 128:(t_ + 1) * 128, :], xt, nb, hT, ident, npt)
                ps = nps()
                for gi in range(4):
                    for k in range(8):
                        mm(ps[0:64, gi * 128:(gi + 1) * 128], win_b[:, k, 512 + 64 * gi:576 + 64 * gi], hT[:, k, :],
                           st=(k == 0), sp=(k == 7))
                for h in range(2):
                    cp("act", srcT["k"][:, h, :, 8 * t_:8 * t_ + 8], ps[0:64, h * 128:(h + 1) * 128].re("p (c j) -> p j c", j=16))
                    cp("dve", srcT["v"][:, h, :, 8 * t_:8 * t_ + 8], ps[0:64, 256 + h * 128:384 + h * 128].re("p (c j) -> p j c", j=16))
            w1s = [fw.sb(s, [64, 8, 256], F32, f"w1s{i}") for i in range(2)]
            for kv in "kv":
                pe, w1, b1, w2 = cmp_in[kv]
                w1b = fw.sb(s, [64, 32, 256], BF16, "w1b" + kv)
                for jb in range(4):
                    st = w1s[jb % 2]
                    dma(st[:], V(None, w1.a.rearrange("(j d) n -> d j n", d=64)[:, jb * 8:(jb + 1) * 8, :]))
                    cp("dve", w1b[:, jb * 8:(jb + 1) * 8, :], st[:])
                peT = fw.sb(s, [64, 32], F32, "peT" + kv)
                dmas(peT[:], V(None, pe.a.rearrange("j d -> d j")))
                peTb = fw.sb(s, [64, 32], BF16, "peTb" + kv)
                cp("dve", peTb[:], peT[:])
                b1c = fw.sb(s, [128, 2], F32, "b1c" + kv)
                dmas(b1c[:], V(None, b1.a.rearrange("(c p) -> p c", p=128)))
                w2s = fw.sb(s, [128, 2, 64], F32, "w2s" + kv)
                dma(w2s[:], V(None, w2.a.rearrange("(c p) n -> p c n", p=128)))
                w2b = fw.sb(s, [128, 2, 64], BF16, "w2b" + kv)
                cp("dve", w2b[:], w2s[:])
                cst = fw.sb(s, [128, 2], F32, "cst" + kv)
                for hc in range(2):
                    ps = nps()
                    for j in range(32):
                        mm(ps[:, 0:1], w1b[:, j, hc * 128:(hc + 1) * 128], peTb[:, j:j + 1], st=(j == 0), sp=(j == 31))
                    tt("dve", cst[:, hc:hc + 1], ps[:, 0:1], b1c[:, hc:hc + 1], ALU.add)
                gT = fw.sb(s, [128, 2, 256], BF16, "gT" + kv)
                if NCB < 256:
                    ms("pool", gT[:], 0.0)
                for kvh in range(2):
                    for hc in range(2):
                        ps = nps()
                        for j in range(32):
                            rv = srcT[kv][:, kvh, j, 0:NCB] if j < 16 else srcT[kv][:, kvh, j - 16, 1:NCB + 1]
                            mm(ps[:, 0:NCB], w1b[:, j, hc * 128:(hc + 1) * 128], rv, st=(j == 0), sp=(j == 31))
                        act(gT[:, hc, 0:NCB], ps[:, 0:NCB], AF.Gelu_apprx_tanh, bias=cst[:, hc:hc + 1])
                    if kv == "k":
                        ps = nps()
                        for hc in range(2):
                            mm(ps[0:64, 0:256], w2b[:, hc, :], gT[:, hc, :], st=(hc == 0), sp=(hc == 1))
                        cp("dve", kcp[0:64, kvh, :], ps[0:64, 0:256])
                    else:
                        for ct in range(2):
                            ps = nps()
                            for hc in range(2):
                                mm(ps[:, 0:64], gT[:, hc, ct * 128:(ct + 1) * 128], w2b[:, hc, :], st=(hc == 0), sp=(hc == 1))
                            cp("dve", vcp[:, ct, kvh, 0:64], ps[:, 0:64])
            fw.barrier()

    def pass1_prompt(self, sw, L):
        fw, NT, T = self.fw, self.NT, self.T
        mm, tr, act, ts, tt, stt, cp, ms, iota, dma, dmas = (self.mm, self.tr, self.act, self.ts, self.tt, self.stt,
                                                             self.cp, self.ms, self.iota, self.dma, self.dmas)
        xp, win_b, wout_b, wqm_b, wkm_b = L["xp"], L["win_b"], L["wout_b"], L["wqm_b"], L["wkm_b"]
        kcp, vcp, ident, identf, nps, npt, psa = L["kcp"], L["vcp"], L["ident"], L["identf"], L["nps"], L["npt"], L["psa"]
        nps_s, nps_m = L["nps_s"], L["nps_m"]
        bd01, tri2, csel, mimp, tmpf = L["bd01"], L["tri2"], L["csel"], L["mimp"], L["tmpf"]
        cw, cb, bgate, bif, put_row, x1d = L["cw"], L["cb"], L["bgate"], L["bif"], L["put_row"], L["x1d"]
        p_kv, p_kw, p_vw, p_C, p_n, p_m, p_conv = L["p_kv"], L["p_kw"], L["p_vw"], L["p_C"], L["p_n"], L["p_m"], L["p_conv"]
        with ExitStack() as s:
            iota(tmpf[:].re("p (g r) -> p g r", g=4), [[0, 4], [-1, 128]], base=0, cm=1)
            caus_add = fw.sb(s, [128, 512], BF16, "caus_add")
            ts("pool", caus_add[:], tmpf[:], 0.0, NEG, op0=ALU.is_gt, op1=ALU.mult)
            win_add = fw.sb(s, [128, 512], BF16, "win_add")
            ts("pool", win_add[:], tmpf[:], 0.0, NEG, op0=ALU.is_le, op1=ALU.mult)
            e0 = fw.sb(s, [128, 512], F32, "e0")
            iota(e0[:].re("p (g r) -> p g r", g=4), [[0, 4], [-1, 128]], base=0, cm=16)
            expand = fw.sb(s, [64, T], BF16, "expand")
            for c0 in range(0, T, 512):
                iota(tmpf[0:64, :], [[1, 512]], base=c0, cm=-64)
                ts("pool", tmpf[0:64, :], tmpf[0:64, :], 31.5, None, op0=ALU.subtract)
                stt("dve", tmpf[0:64, :], tmpf[0:64, :], -1.0, tmpf[0:64, :], ALU.mult, ALU.max)
                ts("pool", expand[:, c0:c0 + 512], tmpf[0:64, :], 32.0, None, op0=ALU.is_le)
            ksT = fw.sb(s, [68, 2, T], BF16, "ksT")
            NW = min(8, NT)
            kwT = fw.sb(s, [68, 2, NW * 128], BF16, "kwT")
            phr = fw.sb(s, [1, 128], BF16, "phr")
            vsp = fw.sb(s, [128, NT, 2, 65], BF16, "vsp")
            vwp = fw.sb(s, [128, NW, 2, 65], BF16, "vwp")
            ms("pool", vsp[:], 1.0)
            ms("pool", vwp[:], 1.0)
            with ExitStack() as tmps:
                self.rowt = fw.sb(tmps, [1, 4096], F32, "rowt1")
                self.rowb = fw.sb(tmps, [1, 4096], BF16, "rowb1")
                for kvh in range(2):
                    put_row(ksT[64:65, kvh, :], [[128, NT], [0, 128]], 0, T)
                    put_row(ksT[65:66, kvh, :], [[0, NT], [1, 128]], 0, T)
                    put_row(ksT[66:67, kvh, :], None, 0, T, const=1.0)
                    put_row(ksT[67:68, kvh, :], None, 0, T, const=1.0)
                    put_row(kwT[65:66, kvh, :], [[0, NW], [1, 128]], 0, NW * 128)
                    put_row(kwT[66:67, kvh, :], None, 0, NW * 128, const=1.0)
                    put_row(kwT[67:68, kvh, :], None, 0, NW * 128, const=1.0)
                fw.barrier()
            qps = [fw.sb(s, [68, 2, 4, 128], BF16, f"qp{i}") for i in range(2)]
            srow = fw.sb(s, [1, 8, 128], F32, "srow")
            for h in range(8):
                ms("pool", srow[0:1, h, :], 2.0 ** (-(h + 1)))
            r67 = fw.sb(s, [1, 8, 128], F32, "r67")
            iota(r67[:], [[0, 8], [1, 128]], base=0, cm=0)
            tt("pool", r67[:], r67[:], srow[:], ALU.mult)
            ts("pool", r67[:], r67[:], -1.0, None, op0=ALU.mult)
            srb = fw.sb(s, [1, 8, 128], BF16, "srb")
            r67b = fw.sb(s, [1, 8, 128], BF16, "r67b")
            r66b = fw.sb(s, [1, 8, 128], BF16, "r66b")
            cp("pool", srb[:], srow[:])
            cp("pool", r67b[:], r67[:])
            for qp in qps:
                for kvh in range(2):
                    dma(qp[64:65, kvh], srb[0:1, 4 * kvh:4 * kvh + 4, :])
                    dma(qp[65:66, kvh], srb[0:1, 4 * kvh:4 * kvh + 4, :])
                    dma(qp[67:68, kvh], r67b[0:1, 4 * kvh:4 * kvh + 4, :])
            xts = [fw.sb(s, [128, D], F32, f"x1_{i}") for i in range(2)]
            nb = self.norm_bufs(s, "1")
            hTs = [fw.sb(s, [128, 8, 128], BF16, f"hT1_{i}") for i in range(2)]
            pkv = fw.sb(s, [128, 792], F32, "pkv")
            qtok = fw.sb(s, [128, 512], BF16, "qtok")
            kstok = fw.sb(s, [128, 256], BF16, "kstok")
            gt = fw.sb(s, [128, 24], F32, "gt")
            pts = RR([fw.sb(s, [128, 512], BF16, f"pt{i}") for i in range(4)])
            mks = RR([fw.sb(s, [128, 512], BF16, f"mk{i}") for i in range(2)])
            obr = [[fw.sb(s, [128, 4, 65], F32, f"obr{k}{i}") for i in range(3)] for k in range(2)]
            rdc = fw.sb(s, [128, 4], F32, "rdc")
            imp4 = fw.sb(s, [128, 4, 64], F32, "imp4")
            imp = fw.sb(s, [128, 64], F32, "imp")
            imp2 = fw.sb(s, [128, 64], F32, "imp2")
            mx1 = fw.sb(s, [128, 8], F32, "mx1")
            mx2 = fw.sb(s, [128, 8], F32, "mx2")
            selm = [fw.sb(s, [128, 64], BF16, f"selm{k}") for k in range(2)]
            selT = [fw.sb(s, [64, 4, 128], BF16, f"selT{k}") for k in range(2)]
            fpb = fw.sb(s, [128, 3], F32, "fpb")
            ms("pool", fpb[:], -1.0)
            rd = fw.sb(s, [128, 3, 4], F32, "rd")
            sc3 = fw.sb(s, [128, 3, 4], F32, "sc3")
            onsa = fw.sb(s, [128, 4, 64], F32, "onsa")
            otmp = fw.sb(s, [128, 4, 64], F32, "otmp")
            ss4m = fw.sb(s, [128, 4], F32, "ss4pm")
            rs4m = fw.sb(s, [128, 4], F32, "rs4pm")
            ss4 = fw.sb(s, [128, 4], F32, "ss4")
            rs4 = fw.sb(s, [128, 4], F32, "rs4")
            mixin = fw.sb(s, [128, D], BF16, "mixin")
            mT = fw.sb(s, [128, 8, 128], BF16, "mT")
            x1t = fw.sb(s, [128, D], F32, "x1t")
            gif = fw.sb(s, [128, 8], F32, "gif")
            l1 = fw.sb(s, [128, 4], F32, "l1")
            gsb = fw.sb(s, [128, 12], F32, "gsb")
            wl = fw.sb(s, [128, 4], F32, "wl")
            ul = fw.sb(s, [128, 4], F32, "ul")
            tmp4 = fw.sb(s, [128, 4], F32, "tmp4")
            dec = fw.sb(s, [128, 4], F32, "dec")
            ebt = fw.sb(s, [128, 8], F32, "ebt")
            vmu = fw.sb(s, [128, 4, 129], BF16, "vmu")
            sigo = fw.sb(s, [128, 512], F32, "sigo")
            xcv = [fw.sb(s, [128, 4, 131], F32, f"xcv{i}") for i in range(2)]
            ms("pool", xcv[0][:], 0.0)
            cacc = fw.sb(s, [128, 4, 128], F32, "cacc")
            xc = fw.sb(s, [128, 4, 128], BF16, "xc")
            qmT = fw.sb(s, [128, 4, 128], BF16, "qmT")
            qmS = [fw.sb(s, [128, 4, 128], BF16, f"qmS{i}") for i in range(2)]
            for q_ in qmS:
                ms("pool", q_[:], 0.0)
            kmT = fw.sb(s, [128, 4, 128], BF16, "kmT")
            kmS = [fw.sb(s, [128, 4, 128], BF16, f"kmS{i}") for i in range(2)]
            mqk = fw.sb(s, [128, 4, 128], BF16, "mqk")
            Sf = fw.sb(s, [128, 4, 129], F32, "Sf")
            ms("pool", Sf[:], 0.0)
            Sb = [fw.sb(s, [128, 4, 129], BF16, f"Sb{i}") for i in range(3)]
            ms("pool", Sb[0][:], 0.0)
            dS = fw.sb(s, [128, 4, 129], F32, "dS")
            dn = fw.sb(s, [128, 4], F32, "dn")
            hout = fw.sb(s, [128, 4, 128], F32, "hout")
            hsq = cacc
            m4 = fw.sb(s, [4, 8], F32, "m4")
            R = fw.sb(s, [4, 1], F32, "Rm")
            ms("pool", R[:], 0.0)
            tsb = fw.sb(s, [4, 384], F32, "tsb")
            segs = [(0, 64), (64, 128)]
            KSC = 128.0 ** -0.5

            for t_ in range(NT):
                xt = xts[t_ % 2]
                qp = qps[t_ % 2]
                ts("pool", r66b[:], srow[:], -128.0 * t_, None, op0=ALU.mult)
                for kvh in range(2):
                    dma(qp[66:67, kvh], r66b[0:1, 4 * kvh:4 * kvh + 4, :])
                hT = hTs[t_ % 2]
                if t_ == 0:
                    self.norm_T(xp[0:128, :], xt, nb, hT, ident, npt)
                psA = nps()
                psB = nps()
                for k in range(8):
                    mm(psA[:, 0:512], hT[:, k, :], win_b[:, k, 512:1024], st=(k == 0), sp=(k == 7))
                for k in range(8):
                    mm(psB[:, 0:280], hT[:, k, :], win_b[:, k, 1024:1304], st=(k == 0), sp=(k == 7))
                cp("dve", pkv[:, 0:512], psA[:, 0:512])
                cp("act", pkv[:, 512:792], psB[:, 0:280])
                r0 = t_ * 128
                for i_, n_ in enumerate(("p_kc", "p_vc", "p_ks", "p_vs")):
                    dma(p_kv[n_][r0:r0 + 128, :], pkv[:, i_ * 128:(i_ + 1) * 128])
                if r0 >= T - 512:
                    w0 = r0 - (T - min(512, T))
                    dma(p_kw[w0:w0 + 128, :], pkv[:, 512:640])
                    dma(p_vw[w0:w0 + 128, :], pkv[:, 640:768])
                cp("pool", vsp[:, t_, :, 0:64], pkv[:, 384:512].re("p (h d) -> p h d", h=2))
                ws = t_ % NW
                cp("pool", vwp[:, ws, :, 0:64], pkv[:, 640:768].re("p (h d) -> p h d", h=2))
                tt("dve", gt[:], pkv[:, 768:792], bgate[:], ALU.add)
                self.sigm(gt[:], gt[:])
                psQt = nps()
                for k in range(8):
                    mm(psQt[:, 0:512], hT[:, k, :], win_b[:, k, 0:512], st=(k == 0), sp=(k == 7))
                act(qtok[:], psQt[:, 0:512], AF.Copy, scale=0.125)
                cp("dve", kstok[:, 0:128], pkv[:, 256:384])
                cp("dve", kstok[:, 128:256], pkv[:, 512:640])
                ptq = npt()
                for h in range(8):
                    tr(ptq[0:64, h * 128:(h + 1) * 128], qtok[:, h * 64:(h + 1) * 64], ident[:])
                cp("act", qp[0:64].re("p k g t -> p (k g t)"), ptq[0:64, :])
                ptk = npt()
                for i_ in range(4):
                    tr(ptk[0:64, i_ * 128:(i_ + 1) * 128], kstok[:, i_ * 64:(i_ + 1) * 64], ident[:])
                ws = t_ % NW
                cp("dve", ksT[0:64, :, r0:r0 + 128], ptk[0:64, 0:256].re("p (h t) -> p h t", h=2))
                cp("dve", kwT[0:64, :, ws * 128:(ws + 1) * 128], ptk[0:64, 256:512].re("p (h t) -> p h t", h=2))
                ms("pool", phr[:], 128.0 * t_)
                for kvh in range(2):
                    dma(kwT[64:65, kvh, ws * 128:(ws + 1) * 128], phr[:])

                def mlstm_gen():
                    psG = nps_m()
                    for k in range(8):
                        mm(psG[:, 0:8], hT[:, k, :], win_b[:, k, 2840:2848], st=(k == 0), sp=(k == 7))
                    tt("dve", gif[:], psG[:, 0:8], bif[:], ALU.add)
                    act(l1[:], gif[:, 4:8], AF.Exp, scale=-1.0)
                    act(l1[:], l1[:], AF.Ln, bias=1.0)
                    yield
                    psC = nps_m()
                    mm(psC[:, 0:4], tri2[:], l1[:])
                    mm(psC[:, 4:8], csel[:, 0, :], l1[:])
                    mm(psC[:, 8:12], csel[:, 1, :], l1[:])
                    cp("dve", gsb[:], psC[:, 0:12])
                    act(wl[:], gsb[:, 0:4], AF.Exp, scale=-1.0)
                    tt("dve", tmp4[:], gif[:, 0:4], gsb[:, 0:4], ALU.add)
                    act(ul[:], tmp4[:], AF.Exp)
                    act(ebt[:], gsb[:, 4:12], AF.Exp, scale=-1.0)
                    tt("dve", dec[0:64, :], tmp4[0:64, :], gsb[0:64, 4:8], ALU.subtract)
                    tt("dve", dec[64:128, :], tmp4[64:128, :], gsb[64:128, 8:12], ALU.subtract)
                    yield
                    psV = nps_m()
                    for k in range(8):
                        mm(psV[:, 0:512], hT[:, k, :], win_b[:, k, 1816:2328], st=(k == 0), sp=(k == 7))
                    tt("dve", vmu[:, :, 0:128], psV[:, 0:512].re("p (h e) -> p h e", h=4), ul[:].un(2).bc([128, 4, 128]), ALU.mult)
                    cp("dve", vmu[:, :, 128], ul[:])
                    yield
                    psO = nps_m()
                    for k in range(8):
                        mm(psO[:, 0:512], hT[:, k, :], win_b[:, k, 2328:2840], st=(k == 0), sp=(k == 7))
                    self.sigm(sigo[:], psO[:, 0:512])
                    yield
                    xcur, xnext = xcv[t_ % 2], xcv[(t_ + 1) % 2]
                    psX = nps_m()
                    for ch in range(4):
                        for k in range(8):
                            mm(psX[:, ch * 128:(ch + 1) * 128], win_b[:, k, 1304 + ch * 128:1432 + ch * 128], hT[:, k, :],
                               st=(k == 0), sp=(k == 7))
                    cp("act", xcur[:, :, 3:131], psX[:, :].re("p (c t) -> p c t", c=4))
                    cp("pool", xnext[:, :, 0:3], xcur[:, :, 128:131])
                    yield
                    for ch in range(4):
                        ts("dve", cacc[:, ch, :], xcur[:, ch, 0:128], cw[:, ch, 0:1], cb[:, ch:ch + 1], op0=ALU.mult, op1=ALU.add)
                        for j in range(1, 4):
                            stt("dve", cacc[:, ch, :], xcur[:, ch, j:j + 128], cw[:, ch, j:j + 1], cacc[:, ch, :], ALU.mult, ALU.add)
                        if ch % 2 == 1:
                            yield
                    self.sigm(hout[:], cacc[:])
                    tt("dve", xc[:], cacc[:], hout[:], ALU.mult)
                    if t_ == NT - 1:
                        for j in range(3):
                            dmas(V(None, p_conv.a[j].rearrange("(c p) -> p c", p=128)), xcur[:, :, 128 + j])
                    yield
                    psq = nps_m()
                    for h in range(4):
                        mm(psq[:, h * 128:(h + 1) * 128], wqm_b[:, h, :], xc[:, h, :])
                    cp("act", qmT[:].re("p h t -> p (h t)"), psq[:, :])
                    for si, (a_, b_) in enumerate(segs):
                        cp("dve", qmS[si][:, :, a_:b_], psq[:, :].re("p (h t) -> p h t", h=4)[:, :, a_:b_])
                    psk = nps_m()
                    for h in range(4):
                        mm(psk[:, h * 128:(h + 1) * 128], wkm_b[:, h, :], xc[:, h, :])
                    act(kmT[:].re("p h t -> p (h t)"), psk[:, :], AF.Copy, scale=KSC)
                    yield
                    pskt = nps_m()
                    for h in range(4):
                        mm(pskt[:, h * 128:(h + 1) * 128], xc[:, h, :], wkm_b[:, h, :])
                    for si in range(2):
                        ts("dve", kmS[si][:].re("p h t -> p (h t)"), pskt[:, :], csel[:, si, 0:1], KSC, op0=ALU.mult, op1=ALU.mult)
                    psqk = nps_m()
                    for h in range(4):
                        mm(psqk[:, h * 128:(h + 1) * 128], kmT[:, h, :], qmT[:, h, :])
                    tt("dve", mqk[:], psqk[:, :].re("p (h t) -> p h t", h=4), bd01[:].un(1).bc([128, 4, 128]), ALU.mult)
                    yield
                    sbs = [Sb[(2 * t_) % 3], Sb[(2 * t_ + 1) % 3], Sb[(2 * t_ + 2) % 3]]
                    for si in range(2):
                        pd = [nps_m(), nps_m()]
                        for h in range(4):
                            mm(pd[h // 2][:, (h % 2) * 129:(h % 2) * 129 + 129], kmS[si][:, h, :], vmu[:, h, :])
                        eb = ebt[:, 4 * si:4 * si + 4].un(2).bc([128, 4, 129])
                        tt("dve", Sf[:], Sf[:], eb, ALU.mult)
                        for hh in range(2):
                            tt("dve", dS[:, 2 * hh:2 * hh + 2, :], pd[hh][:, 0:258].re("p (h e) -> p h e", h=2),
                               ebt[:, 4 * si + 2 * hh:4 * si + 2 * hh + 2].un(2).bc([128, 2, 129]), ALU.mult)
                        tt("dve", Sf[:], Sf[:], dS[:], ALU.add)
                        cp("act", sbs[si + 1][:], Sf[:])
                        yield
                    pa = [nps_m(), nps_m()]
                    for h in range(4):
                        o_ = pa[h // 2][:, (h % 2) * 129:(h % 2) * 129 + 129]
                        mm(o_, mqk[:, h, :], vmu[:, h, :], st=True, sp=False)
                        mm(o_, qmS[0][:, h, :], sbs[0][:, h, :], st=False, sp=False)
                        mm(o_, qmS[1][:, h, :], sbs[1][:, h, :], st=False, sp=True)
                    for hh in range(2):
                        av = pa[hh][:, 0:258].re("p (h e) -> p h e", h=2)
                        tt("dve", dn[:, 2 * hh:2 * hh + 2], av[:, :, 128], wl[:, 2 * hh:2 * hh + 2], ALU.mult)
                    stt("dve", tmp4[:], dn[:], -1.0, dn[:], ALU.mult, ALU.max)
                    ts("dve", tmp4[:], tmp4[:], 1.0, None, op0=ALU.max)
                    self.recip(tmp4[:], tmp4[:])
                    tt("dve", tmp4[:], tmp4[:], wl[:], ALU.mult)
                    for hh in range(2):
                        av = pa[hh][:, 0:258].re("p (h e) -> p h e", h=2)
                        tt("dve", hout[:, 2 * hh:2 * hh + 2, :], av[:, :, 0:128],
                           tmp4[:, 2 * hh:2 * hh + 2].un(2).bc([128, 2, 128]), ALU.mult)
                    yield
                    tt("dve", hsq[:], hout[:], hout[:], ALU.mult)
                    self.rsum(ss4m[:], hsq[:])
                    self.rstd(rs4m[:], ss4m[:], 1.0 / 128)
                    tt("dve", hout[:], hout[:], rs4m[:].un(2).bc([128, 4, 128]), ALU.mult)
                    tt("dve", mixin[:, 512:1024], hout[:].re("p h e -> p (h e)"), sigo[:], ALU.mult)
                    yield
                    psT = nps_m()
                    tr(psT[0:4, 0:128], dec[:], identf[:])
                    tr(psT[0:4, 128:256], gsb[:, 4:8], identf[:])
                    tr(psT[0:4, 256:384], gsb[:, 8:12], identf[:])
                    cp("dve", tsb[:], psT[0:4, 0:384])
                    self.rmax(m4[:, 0:1], tsb[:, 0:64])
                    self.rmax(m4[:, 1:2], tsb[:, 64:128])
                    stt("dve", R[:], R[:], tsb[:, 128:129], m4[:, 0:1], ALU.subtract, ALU.max)
                    stt("dve", R[:], R[:], tsb[:, 256:257], m4[:, 1:2], ALU.subtract, ALU.max)

                mg = mlstm_gen()
                steps = []
                for kvh in range(2):
                    cts = [0] if t_ < 16 else [0, 1]
                    for ci, ct in enumerate(cts):
                        steps.append(("cmp", kvh, ct, ci == 0, ci == len(cts) - 1))
                for kvh in range(2):
                    k0 = max(0, t_ - 4)
                    for kt in range(k0, t_ + 1):
                        steps.append(("win", kvh, kt, kt == k0, kt == t_))
                for kvh in range(2):
                    for kt in range(t_ + 1):
                        steps.append(("sel", kvh, kt, kt == 0, kt == t_))
                pend = {}

                def score(i):
                    kind, kvh, k, first, last = steps[i]
                    qv = qp[:, kvh].re("p g t -> p (g t)")
                    S = nps_s()
                    if kind == "cmp":
                        Kq = 128 * t_ - 2048 * k - 31
                        need_mask = Kq < 2032
                        mm(S[:, :], kcp[:, kvh, k * 128:(k + 1) * 128], qv, st=True, sp=not need_mask)
                        if need_mask:
                            mk = mks()
                            ts("dve", mk[:], e0[:], float(Kq), NEG, op0=ALU.is_gt, op1=ALU.mult)
                            mm(S[:, :], ident[:], mk[:], st=False, sp=True)
                    elif kind == "win":
                        madd = caus_add if k == t_ else (win_add if k == t_ - 4 else None)
                        mm(S[:, :], kwT[:, kvh, (k % NW) * 128:(k % NW + 1) * 128], qv, st=True, sp=(madd is None))
                        if madd is not None:
                            mm(S[:, :], ident[:], madd[:], st=False, sp=True)
                    else:
                        if first and kvh == 0:
                            flush_sel()
                        mm(S[:, :], ksT[:, kvh, k * 128:(k + 1) * 128], qv, st=True, sp=False)
                        if k < t_:
                            mm(S[:, :], expand[:, k * 128:(k + 1) * 128], selT[kvh][:].re("p g t -> p (g t)"), st=False, sp=True)
                        else:
                            mm(S[:, :], ident[:], caus_add[:], st=False, sp=True)
                    pt = pts()
                    act(pt[:], S[:, :], AF.Exp)
                    pend[i] = pt

                def finish_cmp(kvh):
                    bank = psa[kvh]
                    cp("dve", obr[kvh][0][:, :, 0:64], bank[:, 0:256].re("p (g d) -> p g d", g=4))
                    cp("act", imp4[:].re("p g j -> p (g j)"), bank[:, 256:512])
                    cp("dve", obr[kvh][0][:, :, 64], imp4[:, :, 63])
                    ts("dve", rdc[:], obr[kvh][0][:, :, 64], 1e-30, None, op0=ALU.max)
                    self.recip(rdc[:], rdc[:])
                    ts("dve", imp[:], imp4[:, 0, :], rdc[:, 0:1], None, op0=ALU.mult)
                    for g in range(1, 4):
                        stt("dve", imp[:], imp4[:, g, :], rdc[:, g:g + 1], imp[:], ALU.mult, ALU.add)
                    ms("dve", imp[:, 63:64], 0.0)
                    if t_ == 0:
                        tt("dve", imp[:, 0:2], imp[:, 0:2], fpb[:, 1:3], ALU.max)
                    else:
                        tt("dve", imp[:, 2 * t_ - 1:2 * t_ + 2], imp[:, 2 * t_ - 1:2 * t_ + 2], fpb[:, 0:3], ALU.max)
                        ms("dve", imp[:, 0:1], 3e9)
                    fw.op("dve", lambda e: e.max(out=mx1[:].a, in_=imp[:].a), reads=[imp], writes=[mx1])
                    fw.op("dve", lambda e: e.match_replace(out=imp2[:].a, in_to_replace=mx1[:].a, in_values=imp[:].a,
                                                           imm_value=-1e30), reads=[imp, mx1], writes=[imp2])
                    fw.op("dve", lambda e: e.max(out=mx2[:].a, in_=imp2[:].a), reads=[imp2], writes=[mx2])
                    ts("dve", selm[kvh][:], imp[:], mx2[:, 7:8], NEG, op0=ALU.is_lt, op1=ALU.mult)

                def flush_sel():
                    for kvh_ in range(2):
                        ptr = npt()
                        tr(ptr[0:64, 0:128], selm[kvh_][:], ident[:])
                        cp("dve", selT[kvh_][:], ptr[0:64, 0:128].un(1).bc([64, 4, 128]))

                def pv(i):
                    kind, kvh, k, first, last = steps[i]
                    pt = pend.pop(i)
                    acc = psa[kvh]
                    if kind == "cmp":
                        for g in range(4):
                            mm(acc[:, g * 64:(g + 1) * 64], pt[:, g * 128:(g + 1) * 128], vcp[:, k, kvh, 0:64], st=first, sp=last)
                            mm(acc[:, 256 + g * 64:320 + g * 64], pt[:, g * 128:(g + 1) * 128], mimp[:, k, :], st=first, sp=last)
                    else:
                        vv = vwp[:, k % NW, kvh, :] if kind == "win" else vsp[:, k, kvh, :]
                        for g in range(4):
                            mm(acc[:, g * 65:(g + 1) * 65], pt[:, g * 128:(g + 1) * 128], vv, st=first, sp=last)
                    if last:
                        if kind == "cmp":
                            finish_cmp(kvh)
                        elif kind == "win":
                            cp("act", obr[kvh][2][:].re("p g d -> p (g d)"), acc[:, 0:260])
                        else:
                            cp("dve", obr[kvh][1][:].re("p g d -> p (g d)"), acc[:, 0:260])

                base = 1e9 + 1e6 * (2 * t_)
                ms("dve", fpb[0:64, 0:1], base - 1e6)
                ms("dve", fpb[0:64, 1:2], base)
                ms("dve", fpb[64:128, 1:2], base)
                ms("dve", fpb[64:128, 2:3], base + 1e6)
                LA = 2
                for i in range(min(LA, len(steps))):
                    score(i)
                iB = min(8, len(steps) - 1)
                pace = max(1, (len(steps) - 2) // 18)
                for i in range(len(steps)):
                    if i + LA < len(steps):
                        score(i + LA)
                    pv(i)
                    if i >= 1 and (i - 1) % pace == 0:
                        next(mg, None)
                    if t_ + 1 < NT:
                        if i == 0:
                            self.norm_A(xp[(t_ + 1) * 128:(t_ + 2) * 128, :], xts[(t_ + 1) % 2], nb)
                        if i == iB:
                            self.norm_B(nb, hTs[(t_ + 1) % 2], ident, npt)
                for _ in mg:
                    pass
                for kvh in range(2):
                    for br in range(3):
                        ts("dve", rd[:, br, :], obr[kvh][br][:, :, 64], 1e-30, None, op0=ALU.max)
                    self.recip(rd[:].re("p b g -> p (b g)"), rd[:].re("p b g -> p (b g)"))
                    gv = gt[:, 12 * kvh:12 * kvh + 12].re("p (g b) -> p b g", b=3)
                    tt("dve", sc3[:], rd[:], gv, ALU.mult)
                    tt("dve", onsa[:], obr[kvh][0][:, :, 0:64], sc3[:, 0, :].un(2).bc([128, 4, 64]), ALU.mult)
                    for br in (1, 2):
                        tt("dve", otmp[:], obr[kvh][br][:, :, 0:64], sc3[:, br, :].un(2).bc([128, 4, 64]), ALU.mult)
                        tt("dve", onsa[:], onsa[:], otmp[:], ALU.add)
                    tt("dve", otmp[:], onsa[:], onsa[:], ALU.mult)
                    self.rsum(ss4[:], otmp[:])
                    self.rstd(rs4[:], ss4[:], 1.0 / 64)
                    tt("dve", mixin[:, 256 * kvh:256 * kvh + 256].re("p (g d) -> p g d", g=4), onsa[:],
                       rs4[:].un(2).bc([128, 4, 64]), ALU.mult)

                ptm = npt()
                for k in range(8):
                    tr(ptm[:, k * 128:(k + 1) * 128], mixin[:, k * 128:(k + 1) * 128], ident[:])
                cp("act", mT[:].re("p k t -> p (k t)"), ptm[:])
                for g in range(2):
                    ps = nps()
                    for k in range(8):
                        mm(ps[:, :], mT[:, k, :], wout_b[:, k, g * 512:(g + 1) * 512], st=(k == 0), sp=(k == 7))
                    tt("dve", x1t[:, g * 512:(g + 1) * 512], xt[:, g * 512:(g + 1) * 512], ps[:, :], ALU.add)
                dma(x1d[r0:r0 + 128, :], x1t[:])

            dma(V(None, p_m.a.rearrange("(h o) -> h o", o=1)), R[:])
            ones4 = fw.sb(s, [4, 128], F32, "ones4")
            ms("pool", ones4[:], 1.0)
            ts("dve", ones4[:], ones4[:], R[:, 0:1], None, op0=ALU.mult)
            ps = nps()
            mm(ps[:, 0:4], ones4[:], identf[0:4, 0:4])
            act(tmp4[:], ps[:, 0:4], AF.Exp, scale=-1.0)
            tt("dve", Sf[:], Sf[:], tmp4[:].un(2).bc([128, 4, 129]), ALU.mult)
            dmas(V(None, p_n.a.rearrange("h d -> d h")), Sf[:, :, 128])
            for h in range(4):
                ps = nps()
                tr(ps[:, 0:128], Sf[:, h, 0:128], identf[:])
                cp("dve", hsq[:, h, :], ps[:, 0:128])
                dma(p_C[h], hsq[:, h, :])
            fw.barrier()

    def load_w(self, dst, src, rows_chunks, ncols, stg, gcol=None, g0=0):
        for k in range(rows_chunks):
            st = stg[k % 2]
            self.dma(st[:, 0:ncols], src[k * 128:(k + 1) * 128, :])
            if k % 2 == 0:
                if gcol is None:
                    self.cp("dve", dst[:, k, :], st[:, 0:ncols])
                else:
                    self.ts("dve", dst[:, k, :], st[:, 0:ncols], gcol[:, g0 + k:g0 + k + 1], None, op0=ALU.mult)
            elif gcol is None:
                self.cp("act", dst[:, k, :], st[:, 0:ncols])
            else:
                self.act(dst[:, k, :], st[:, 0:ncols], AF.Copy, scale=gcol[:, g0 + k:g0 + k + 1])

    def pass2(self, top, L):
        fw, NT, T = self.fw, self.NT, self.T
        mm, tr, act, ts, tt, stt, cp, ms, dma, dmas = (self.mm, self.tr, self.act, self.ts, self.tt, self.stt,
                                                       self.cp, self.ms, self.dma, self.dmas)
        ident, nps, npt, psa = L["ident"], L["nps"], L["npt"], L["psa"]
        x1d, x2d, y_p = L["x1d"], L["x2d"], L["y_p"]
        g2 = fw.sb(top, [128, 32], F32, "g2")
        for i_, g_ in enumerate((L["g_xa"], L["g_mem"], L["g_ffn"])):
            dmas(g2[:, 8 * i_:8 * i_ + 8], V(None, g_.a.rearrange("(k p) -> p k", p=128)))
        with ExitStack() as s:
            wxq_b = fw.sb(s, [128, 8, D], BF16, "wxq_b")
            wxo_b = fw.sb(s, [128, 8, D], BF16, "wxo_b")
            mkT = fw.sb(s, [128, 8, 256], BF16, "mkT")
            mvp = fw.sb(s, [128, 2, 4, 257], BF16, "mvp")
            ms("pool", mvp[:], 1.0)
            xts = [fw.sb(s, [128, D], F32, f"x2_{i}") for i in range(2)]
            nb = self.norm_bufs(s, "2")
            hT = fw.sb(s, [128, 8, 128], BF16, "hT2")
            with ExitStack() as s2:
                stg = [fw.sb(s2, [128, D], F32, f"stg2{i}") for i in range(2)]
                wxk_b = fw.sb(s2, [128, 8, D], BF16, "wxk_b")
                wxv_b = fw.sb(s2, [128, 8, D], BF16, "wxv_b")
                self.load_w(wxq_b, L["w_xq"], 8, D, stg, g2, 0)
                self.load_w(wxo_b, L["w_xo"], 8, D, stg)
                self.load_w(wxk_b, L["w_xk"], 8, D, stg, g2, 8)
                self.load_w(wxv_b, L["w_xv"], 8, D, stg, g2, 8)
                mo = fw.sb(s2, [128, D], F32, "mo")
                for mt in range(2):
                    xt = xts[mt % 2]
                    self.norm_T(L["memp"][mt * 128:(mt + 1) * 128, :], xt, nb, hT, ident, npt)
                    for wi, (wb_, po) in enumerate(((wxk_b, L["p_mk"]), (wxv_b, L["p_mv"]))):
                        for g in range(2):
                            ps = nps()
                            for k in range(8):
                                mm(ps[:, :], hT[:, k, :], wb_[:, k, g * 512:(g + 1) * 512], st=(k == 0), sp=(k == 7))
                            cp("dve" if g == 0 else "act", mo[:, g * 512:(g + 1) * 512], ps[:, :])
                        dma(po[mt * 128:(mt + 1) * 128, :], mo[:])
                        if wi == 1:
                            cp("pool", mvp[:, mt, :, 0:256], mo[:].re("p (h d) -> p h d", h=4))
                    for c4 in range(2):
                        ps = nps()
                        for cc in range(4):
                            c = c4 * 4 + cc
                            for k in range(8):
                                mm(ps[:, cc * 128:(cc + 1) * 128], wxk_b[:, k, c * 128:(c + 1) * 128], hT[:, k, :],
                                   st=(k == 0), sp=(k == 7))
                        cp("dve", mkT[:, c4 * 4:c4 * 4 + 4, mt * 128:(mt + 1) * 128], ps[:, :].re("p (c t) -> p c t", c=4))
                fw.barrier()
            qxT = fw.sb(s, [128, 8, 128], BF16, "qxT")
            pts = [fw.sb(s, [128, 512], BF16, f"pxt{i}") for i in range(2)]
            ox = fw.sb(s, [128, D], BF16, "ox")
            oxT = fw.sb(s, [128, 8, 128], BF16, "oxT")
            rdx = fw.sb(s, [128, 1], F32, "rdx")
            x2t = fw.sb(s, [128, D], F32, "x2t")
            for t_ in range(NT):
                xt = xts[t_ % 2]
                r0 = t_ * 128
                self.norm_T(x1d[r0:r0 + 128, :], xt, nb, hT, ident, npt)
                for c4 in range(2):
                    ps = nps()
                    for cc in range(4):
                        c = c4 * 4 + cc
                        for k in range(8):
                            mm(ps[:, cc * 128:(cc + 1) * 128], wxq_b[:, k, c * 128:(c + 1) * 128], hT[:, k, :],
                               st=(k == 0), sp=(k == 7))
                    act(qxT[:, c4 * 4:c4 * 4 + 4, :].re("p c t -> p (c t)"), ps[:, :], AF.Copy, scale=1.0 / 16)
                for mt in range(2):
                    S = nps()
                    for h in range(4):
                        for hf in range(2):
                            mm(S[:, h * 128:(h + 1) * 128], mkT[:, 2 * h + hf, mt * 128:(mt + 1) * 128], qxT[:, 2 * h + hf, :],
                               st=(hf == 0), sp=(hf == 1))
                    act(pts[mt][:], S[:, :], AF.Exp)
                for h in range(4):
                    acc = psa[h % 2]
                    for mt in range(2):
                        mm(acc[:, 0:257], pts[mt][:, h * 128:(h + 1) * 128], mvp[:, mt, h, :], st=(mt == 0), sp=(mt == 1))
                    self.recip(rdx[:], acc[:, 256:257])
                    ts("dve", ox[:, h * 256:(h + 1) * 256], acc[:, 0:256], rdx[:, 0:1], None, op0=ALU.mult)
                pto = npt()
                for k in range(8):
                    tr(pto[:, k * 128:(k + 1) * 128], ox[:, k * 128:(k + 1) * 128], ident[:])
                cp("act", oxT[:].re("p k t -> p (k t)"), pto[:])
                for g in range(2):
                    ps = nps()
                    for k in range(8):
                        mm(ps[:, :], oxT[:, k, :], wxo_b[:, k, g * 512:(g + 1) * 512], st=(k == 0), sp=(k == 7))
                    tt("dve", x2t[:, g * 512:(g + 1) * 512], xt[:, g * 512:(g + 1) * 512], ps[:, :], ALU.add)
                dma(x2d[r0:r0 + 128, :], x2t[:])
            if self.sample:
                S = self.S
                xt = xts[0]
                self.norm_T(S["x1s"][:, :], xt, nb, hT, ident, npt, rows=16)
                qxs = fw.sb(s, [128, 8, 16], BF16, "qxs")
                ps = nps()
                for c in range(8):
                    for k in range(8):
                        mm(ps[:, c * 16:(c + 1) * 16], wxq_b[:, k, c * 128:(c + 1) * 128], hT[:, k, 0:16], st=(k == 0), sp=(k == 7))
                act(qxs[:].re("p c t -> p (c t)"), ps[:, 0:128], AF.Copy, scale=1.0 / 16)
                ms("pool", ox[:], 0.0)
                msg = fw.sb(s, [128, 2, D], F32, "msg")
                mkb = fw.sb(s, [128, 2, D], BF16, "mkb")
                ptx = [fw.sb(s, [128, 16], BF16, f"ptx{i}") for i in range(2)]
                oxb = fw.sb(s, [4, D], BF16, "oxb")
                for b in range(4):
                    dma(msg[:], V(None, S["cmk"].a[b].rearrange("(t p) f -> p t f", p=128)))
                    cp("dve", mkb[:, 0, :], msg[:, 0, :])
                    cp("pool", mkb[:, 1, :], msg[:, 1, :])
                    for mt in range(2):
                        pt = npt()
                        for c in range(8):
                            tr(pt[:, c * 128:(c + 1) * 128], mkb[:, mt, c * 128:(c + 1) * 128], ident[:])
                        cp("act", mkT[:, :, mt * 128:(mt + 1) * 128], pt[:, :].re("p (c t) -> p c t", c=8))
                    dma(msg[:], V(None, S["cmv"].a[b].rearrange("(t p) f -> p t f", p=128)))
                    for mt in range(2):
                        cp("dve" if mt == 0 else "pool", mvp[:, mt, :, 0:256], msg[:, mt, :].re("p (h d) -> p h d", h=4))
                    for mt in range(2):
                        Sx = nps()
                        for h in range(4):
                            for hf in range(2):
                                mm(Sx[:, h * 4:(h + 1) * 4], mkT[:, 2 * h + hf, mt * 128:(mt + 1) * 128],
                                   qxs[:, 2 * h + hf, 4 * b:4 * b + 4], st=(hf == 0), sp=(hf == 1))
                        act(ptx[mt][:], Sx[:, 0:16], AF.Exp)
                    for h in range(4):
                        acc = psa[h % 2]
                        for mt in range(2):
                            mm(acc[0:4, 0:257], ptx[mt][:, 4 * h:4 * h + 4], mvp[:, mt, h, :], st=(mt == 0), sp=(mt == 1))
                        self.recip(rdx[0:4, :], acc[0:4, 256:257])
                        ts("dve", oxb[:, h * 256:(h + 1) * 256], acc[0:4, 0:256], rdx[0:4, 0:1], None, op0=ALU.mult)
                    dma(ox[4 * b:4 * b + 4, :], oxb[:])
                pto = npt()
                for k in range(8):
                    tr(pto[:, k * 128:(k + 1) * 128], ox[:, k * 128:(k + 1) * 128], ident[:])
                cp("act", oxT[:].re("p k t -> p (k t)"), pto[:])
                for g in range(2):
                    ps = nps()
                    for k in range(8):
                        mm(ps[:, :], oxT[:, k, :], wxo_b[:, k, g * 512:(g + 1) * 512], st=(k == 0), sp=(k == 7))
                    tt("dve", x2t[:, g * 512:(g + 1) * 512], xt[:, g * 512:(g + 1) * 512], ps[:, :], ALU.add)
                dma(S["x2s"][:, :], x2t[0:16, :])
            fw.barrier()
        with ExitStack() as s:
            wg_b = fw.sb(s, [128, 8, DFF], BF16, "wg_b")
            wu_b = fw.sb(s, [128, 8, DFF], BF16, "wu_b")
            wd_b = fw.sb(s, [128, 22, D], BF16, "wd_b")
            gfin = fw.sb(s, [128, D], F32, "gfin")
            dma(gfin[:], V(None, L["g_final"].a.partition_broadcast(128)))
            with ExitStack() as s2:
                stg = [fw.sb(s2, [128, DFF], F32, f"stg3{i}") for i in range(2)]
                self.load_w(wg_b, L["w_gate"], 8, DFF, stg, g2, 16)
                self.load_w(wu_b, L["w_up"], 8, DFF, stg, g2, 16)
                self.load_w(wd_b, L["w_down"], 22, D, stg)
                fw.barrier()
            xts = [fw.sb(s, [128, D], F32, f"x3_{i}") for i in range(4)]
            nb = self.norm_bufs(s, "3")
            hT4 = fw.sb(s, [128, 8, 512], BF16, "hT3")
            hT = fw.sb(s, [128, 8, 128], BF16, "hT3s")
            aT4 = fw.sb(s, [128, 22, 512], BF16, "aT4")
            aT = fw.sb(s, [128, 22, 128], BF16, "aT")
            sgs = [fw.sb(s, [128, 512], F32, f"sg{i}") for i in range(2)]
            sg = sgs[0]
            x3t = fw.sb(s, [128, D], F32, "x3t")

            def final_norm(dst_, rows_):
                ms("dve", nb["ss"][:], 0.0)
                act(nb["junk"][:], x3t[:], AF.Square, acc=nb["ss"][:])
                self.rstd(nb["rs"][:], nb["ss"][:], 1.0 / D)
                stt("dve", x3t[:], x3t[:], nb["rs"][:, 0:1], gfin[:], ALU.mult, ALU.mult)
                dma(dst_, x3t[0:rows_, :])

            for st_ in range(NT // 4):
                for j in range(4):
                    r0 = (st_ * 4 + j) * 128
                    self.norm_A(x2d[r0:r0 + 128, :], xts[j], nb)
                    pt = npt()
                    for k in range(8):
                        tr(pt[:, k * 128:(k + 1) * 128], nb["xn"][:, k * 128:(k + 1) * 128], ident[:])
                    cp("act", hT4[:, :, j * 128:(j + 1) * 128], pt[:].re("p (k t) -> p k t", k=8))
                for c in range(22):
                    pg, pu = nps(), nps()
                    for k in range(8):
                        mm(pg[:, :], wg_b[:, k, c * 128:(c + 1) * 128], hT4[:, k, :], st=(k == 0), sp=(k == 7))
                    for k in range(8):
                        mm(pu[:, :], wu_b[:, k, c * 128:(c + 1) * 128], hT4[:, k, :], st=(k == 0), sp=(k == 7))
                    sgc = sgs[c % 2]
                    act(sgc[:], pg[:, :], AF.Silu)
                    tt("dve", aT4[:, c, :], sgc[:], pu[:, :], ALU.mult)
                for j in range(4):
                    r0 = (st_ * 4 + j) * 128
                    for g in range(2):
                        ps = nps()
                        for c in range(22):
                            mm(ps[:, :], aT4[:, c, j * 128:(j + 1) * 128], wd_b[:, c, g * 512:(g + 1) * 512], st=(c == 0), sp=(c == 21))
                        tt("dve", x3t[:, g * 512:(g + 1) * 512], xts[j][:, g * 512:(g + 1) * 512], ps[:, :], ALU.add)
                    final_norm(y_p[r0:r0 + 128, :], 128)
            tiles = []
            if self.sample:
                tiles.append((self.S["x2s"][:, :], self.S["y_s"], 16))
            for t_, (src_, dst_, rows_) in enumerate(tiles):
                xt = xts[t_ % 2]
                self.norm_T(src_, xt, nb, hT, ident, npt, rows=rows_)
                for c0 in range(0, 22, 4):
                    n = min(4, 22 - c0)
                    pg = nps()
                    pu = nps()
                    for cc in range(n):
                        c = c0 + cc
                        for k in range(8):
                            mm(pg[:, cc * 128:(cc + 1) * 128], wg_b[:, k, c * 128:(c + 1) * 128], hT[:, k, :], st=(k == 0), sp=(k == 7))
                        for k in range(8):
                            mm(pu[:, cc * 128:(cc + 1) * 128], wu_b[:, k, c * 128:(c + 1) * 128], hT[:, k, :], st=(k == 0), sp=(k == 7))
                    act(sg[:, 0:n * 128], pg[:, 0:n * 128], AF.Silu)
                    tt("dve", aT[:, c0:c0 + n, :].re("p c t -> p (c t)"), sg[:, 0:n * 128], pu[:, 0:n * 128], ALU.mult)
                for g in range(2):
                    ps = nps()
                    for c in range(22):
                        mm(ps[:, :], aT[:, c, :], wd_b[:, c, g * 512:(g + 1) * 512], st=(c == 0), sp=(c == 21))
                    tt("dve", x3t[:, g * 512:(g + 1) * 512], xt[:, g * 512:(g + 1) * 512], ps[:, :], ALU.add)
                ms("dve", nb["ss"][:], 0.0)
                act(nb["junk"][:], x3t[:], AF.Square, acc=nb["ss"][:])
                self.rstd(nb["rs"][:], nb["ss"][:], 1.0 / D)
                stt("dve", x3t[:], x3t[:], nb["rs"][:, 0:1], gfin[:], ALU.mult, ALU.mult)
                dma(dst_, x3t[0:rows_, :])
            fw.barrier()


def sample_io(self):
    din, dout = self.din, self.dout
    S = {"xs": din("xs", [16, D]), "ptab": din("ptab", [4, 128], I32)}
    for n in ("pool_kc", "pool_vc", "pool_ks", "pool_vs"):
        S[n] = din(n, [5120, 16384])
    S["stk"] = din("stk", [4, 512, 128])
    S["stv"] = din("stv", [4, 512, 128])
    S["sconv"] = din("sconv", [4, 3, 512])
    S["sC"] = din("sC", [4, 4, 128, 128])
    S["sn"] = din("sn", [4, 4, 128])
    S["sm"] = din("sm", [16])
    S["cmk"] = din("cmk", [4, 256, D])
    S["cmv"] = din("cmv", [4, 256, D])
    S["y_s"] = dout("y_s", [16, D])
    for n in ("s_kc", "s_vc", "s_ks", "s_vs"):
        S[n] = dout(n, [16, 128])
    S["s_kw"] = dout("s_kw", [4, 512, 128])
    S["s_vw"] = dout("s_vw", [4, 512, 128])
    S["s_C"] = dout("s_C", [4, 4, 128, 128])
    S["s_n"] = dout("s_n", [4, 4, 128])
    S["s_m"] = dout("s_m", [16])
    S["s_conv"] = dout("s_conv", [4, 3, 512])
    S["kcS_d"] = self.dscr("kcS_d", [4, 2, 64, 1024], BF16)
    S["vcS_d"] = self.dscr("vcS_d", [4, 128, 8, 2, 64], BF16)
    S["x1s"] = self.dscr("x1s", [16, D])
    S["x2s"] = self.dscr("x2s", [16, D])
    self.S = S
    return S


def load_cmp(self, s, kv, cmp_in, nps):
    fw = self.fw
    mm, tt, cp, dma, dmas = self.mm, self.tt, self.cp, self.dma, self.dmas
    pe, w1, b1, w2 = cmp_in[kv]
    w1b = fw.sb(s, [128, 16, 256], BF16, "Sw1b" + kv)
    w1v = w1.a.rearrange("(jp jj d) n -> jj d jp n", jj=2, d=64)
    with ExitStack() as t:
        w1s = [fw.sb(t, [128, 8, 256], F32, f"Sw1s{kv}{i}") for i in range(2)]
        for jb in range(2):
            st = w1s[jb % 2]
            for jj in range(2):
                dma(st[jj * 64:(jj + 1) * 64], V(None, w1v[jj][:, jb * 8:(jb + 1) * 8, :]))
            cp("dve", w1b[:, jb * 8:(jb + 1) * 8, :], st[:])
        fw.barrier()
    peT = fw.sb(s, [128, 16], F32, "SpeT" + kv)
    pev = pe.a.rearrange("(jp jj) d -> jj d jp", jj=2)
    for jj in range(2):
        dmas(peT[jj * 64:(jj + 1) * 64, :], V(None, pev[jj]))
    peTb = fw.sb(s, [128, 16], BF16, "SpeTb" + kv)
    cp("dve", peTb[:], peT[:])
    b1c = fw.sb(s, [128, 2], F32, "Sb1c" + kv)
    dmas(b1c[:], V(None, b1.a.rearrange("(c p) -> p c", p=128)))
    w2s = fw.sb(s, [128, 2, 64], F32, "Sw2s" + kv)
    dma(w2s[:], V(None, w2.a.rearrange("(c p) n -> p c n", p=128)))
    w2b = fw.sb(s, [128, 2, 64], BF16, "Sw2b" + kv)
    cp("dve", w2b[:], w2s[:])
    cst = fw.sb(s, [128, 2], F32, "Scst" + kv)
    for hc in range(2):
        ps = nps()
        for jp in range(16):
            mm(ps[:, 0:1], w1b[:, jp, hc * 128:(hc + 1) * 128], peTb[:, jp:jp + 1], st=(jp == 0), sp=(jp == 15))
        tt("dve", cst[:, hc:hc + 1], ps[:, 0:1], b1c[:, hc:hc + 1], ALU.add)
    return w1b, w2b, cst


def gather(self, dst, pool, idx, r0):
    self.fw.dma("pool", dst, pool, extra_reads=[idx.b],
                fn=lambda e: e.indirect_dma_start(out=dst.a, out_offset=None, in_=pool.a,
                                                  in_offset=bass.IndirectOffsetOnAxis(ap=idx.a, axis=0),
                                                  element_offset=r0 * 128))


def sample_s0(self, L):
    fw, S = self.fw, self.S
    mm, tr, act, cp, ms, dma = self.mm, self.tr, self.act, self.cp, self.ms, self.dma
    ident, nps, npt, cmp_in = L["ident"], L["nps"], L["npt"], L["cmp_in"]
    with ExitStack() as s:
        idx = [fw.sb(s, [128, 1], I32, f"S0idx{b}") for b in range(4)]
        for b in range(4):
            dma(idx[b][:], V(None, S["ptab"].a[b].rearrange("(p o) -> p o", o=1)))
        cw_ = {kv: load_cmp(self, s, kv, cmp_in, nps) for kv in "kv"}
        srcS = fw.sb(s, [128, 2, 64, 129], BF16, "srcS")
        ms("pool", srcS[:, :, :, 128:129], 0.0)
        gchs = [fw.sb(s, [128, 4096], F32, f"gch{i}") for i in range(2)]
        gbf = fw.sb(s, [128, 2, 32, 64], BF16, "gbf")
        chunks0 = [(b, kv, i) for b in range(4) for kv in "kv" for i in range(4)]

        def issue0(n):
            b_, kv_, i_ = chunks0[n]
            gather(self, gchs[n % 2][:], S["pool_" + kv_ + "c"], idx[b_][:, :], 32 * i_)

        issue0(0)
        n0 = 0
        gT = fw.sb(s, [128, 2, 1024], BF16, "SgT")
        ko = fw.sb(s, [64, 1024], BF16, "Sko")
        vo = fw.sb(s, [128, 8, 64], BF16, "Svo")
        for b in range(4):
            for kv in "kv":
                w1b, w2b, cst = cw_[kv]
                pool = S["pool_" + kv + "c"]
                for i in range(4):
                    gch = gchs[n0 % 2]
                    if n0 + 1 < len(chunks0):
                        issue0(n0 + 1)
                    n0 += 1
                    gv4 = gch[:].re("p (r k d) -> p k r d", r=32, k=2)
                    cp("dve", gbf[:, 0], gv4[:, 0])
                    cp("act", gbf[:, 1], gv4[:, 1])
                    for kvh in range(2):
                        for g8 in range(2):
                            pt = npt()
                            for r8 in range(8):
                                rp = 8 * g8 + r8
                                tr(pt[:, r8 * 128:(r8 + 1) * 128], gbf[:, kvh, 2 * rp:2 * rp + 2, :].re("p r d -> p (r d)"), ident[:])
                            cp("act" if g8 % 2 == 0 else "dve", srcS[:, kvh, 16 * i + 8 * g8:16 * i + 8 * g8 + 8, 0:128],
                               pt[:, :].re("p (r t) -> p r t", r=8))
                for kvh in range(2):
                    for hc in range(2):
                        wv = lambda j: w1b[:, j, hc * 128:(hc + 1) * 128]
                        for bank in range(2):
                            ps = nps()
                            o4 = ps[:, :].re("p (a t) -> p a t", a=4)
                            for jp in range(16):
                                r0 = 32 * bank + jp
                                if bank == 0 or jp < 8:
                                    mm(o4, wv(jp), srcS[:, kvh, r0:r0 + 25:8, 0:128], st=(jp == 0), sp=(jp == 15))
                                else:
                                    mm(o4[:, 0:3, :], wv(jp), srcS[:, kvh, r0:r0 + 17:8, 0:128], st=False, sp=False)
                                    mm(ps[:, 384:512], wv(jp), srcS[:, kvh, jp - 8, 1:129], st=False, sp=(jp == 15))
                            act(gT[:, hc, bank * 512:(bank + 1) * 512], ps[:, :], AF.Gelu_apprx_tanh, bias=cst[:, hc:hc + 1])
                    if kv == "k":
                        for bank in range(2):
                            ps = nps()
                            for hc in range(2):
                                mm(ps[0:64, :], w2b[:, hc, :], gT[:, hc, bank * 512:(bank + 1) * 512], st=(hc == 0), sp=(hc == 1))
                            cp("dve", ko[:, bank * 512:(bank + 1) * 512], ps[0:64, :])
                        dma(S["kcS_d"][b, kvh], ko[:])
                    else:
                        ps = nps()
                        for rb in range(8):
                            for hc in range(2):
                                mm(ps[:, rb * 64:(rb + 1) * 64], gT[:, hc, rb * 128:(rb + 1) * 128], w2b[:, hc, :],
                                   st=(hc == 0), sp=(hc == 1))
                        cp("dve", vo[:].re("p r d -> p (r d)"), ps[:, :])
                        dma(S["vcS_d"][b][:, :, kvh, :], vo[:])
        fw.barrier()


Builder.sample_io = sample_io

def sample_pass1(self, L):
    fw, S = self.fw, self.S
    mm, tr, act, ts, tt, stt, cp, ms, iota, dma, dmas = (self.mm, self.tr, self.act, self.ts, self.tt, self.stt,
                                                         self.cp, self.ms, self.iota, self.dma, self.dmas)
    win_b, wout_b, wqm_b, wkm_b = L["win_b"], L["wout_b"], L["wqm_b"], L["wkm_b"]
    ident, identf, nps, npt, psa = L["ident"], L["identf"], L["nps"], L["npt"], L["psa"]
    cw, cb, bgate, bif, tmpf = L["cw"], L["cb"], L["bgate"], L["bif"], L["tmpf"]
    put_row = self.put_row
    KSC = 128.0 ** -0.5
    with ExitStack() as s:
        xt = fw.sb(s, [128, D], F32, "xS")
        nb = self.norm_bufs(s, "S")
        hT = fw.sb(s, [128, 8, 128], BF16, "hTS")
        pkv = fw.sb(s, [128, 792], F32, "pkvS")
        gt = fw.sb(s, [128, 24], F32, "gtS")
        vnS = fw.sb(s, [16, 2, 2, 65], BF16, "vnS")
        qTs = fw.sb(s, [64, 8, 16], BF16, "qTs")
        mixin = fw.sb(s, [128, D], BF16, "mixinS")
        s2 = ExitStack()
        QS = fw.sb(s2, [68, 4, 2, 16], BF16, "QS")
        kcSb = fw.sb(s2, [68, 2, 8, 128], BF16, "kcSb")
        vcSb = fw.sb(s2, [128, 8, 2, 65], BF16, "vcSb")
        ksS = fw.sb(s2, [68, 2, 32, 128], BF16, "ksS")
        kwS = fw.sb(s2, [68, 2, 4, 128], BF16, "kwS")
        KnS = fw.sb(s2, [68, 2, 2, 16], BF16, "KnS")
        ms("pool", vcSb[:], 1.0)
        with ExitStack() as tmps:
            self.rowt = fw.sb(tmps, [1, 4096], F32, "rowtS")
            self.rowb = fw.sb(tmps, [1, 4096], BF16, "rowbS")
            sr = fw.sb(tmps, [1, 2, 4, 4], F32, "srS")
            for h in range(8):
                ms("pool", sr[0:1, h // 4, h % 4, :], 2.0 ** (-(h + 1)))
            qi = fw.sb(tmps, [1, 2, 4, 4], F32, "qiS")
            iota(qi[:].re("p k g q -> p (k g) q"), [[0, 8], [1, 4]], base=0, cm=0)
            rw = fw.sb(tmps, [1, 4, 2, 16], F32, "rwS")
            rwb = fw.sb(tmps, [1, 4, 2, 16], BF16, "rwbS")
            srv = sr[:].re("p k g q -> p k (g q)")
            for row in range(4):
                for i in range(4):
                    if row == 0:
                        ts("pool", rw[0:1, i], srv, -1.0, None, op0=ALU.mult)
                    elif row == 1:
                        cp("pool", rw[0:1, i], srv)
                    elif row == 2:
                        tt("pool", rw[0:1, i], srv, qi[:].re("p k g q -> p k (g q)"), ALU.mult)
                        ts("pool", rw[0:1, i], rw[0:1, i], -1.0, None, op0=ALU.mult)
                    else:
                        ts("pool", rw[0:1, i], srv, 32.0 * i, None, op0=ALU.mult)
                cp("pool", rwb[:], rw[:])
                dma(QS[64 + row:65 + row], rwb[:])
            for kvh in range(2):
                put_row(kcSb[64:65, kvh].re("p r t -> p (r t)"), [[0, 8], [-128, 128]], 16384, 1024)
                put_row(kcSb[65:66, kvh].re("p r t -> p (r t)"), [[16, 8], [0, 128]], 31, 1024)
                put_row(kcSb[66:67, kvh].re("p r t -> p (r t)"), None, 0, 1024, const=1.0)
                put_row(kcSb[67:68, kvh].re("p r t -> p (r t)"), None, 0, 1024, const=0.0)
                put_row(ksS[64:65, kvh].re("p r t -> p (r t)"), [[0, 32], [-128, 128]], 16384, 4096)
                put_row(ksS[65:66, kvh].re("p r t -> p (r t)"), [[1, 32], [0, 128]], 0, 4096)
                put_row(ksS[66:67, kvh].re("p r t -> p (r t)"), None, 0, 4096, const=1.0)
                put_row(ksS[67:68, kvh].re("p r t -> p (r t)"), None, 0, 4096, const=1.0)
                put_row(kwS[64:65, kvh].re("p r t -> p (r t)"), [[-128, 4], [0, 128]], 512, 512)
                put_row(kwS[65:66, kvh].re("p r t -> p (r t)"), [[0, 4], [1, 128]], 0, 512)
                put_row(kwS[66:67, kvh].re("p r t -> p (r t)"), None, 0, 512, const=1.0)
                put_row(kwS[67:68, kvh].re("p r t -> p (r t)"), None, 0, 512, const=0.0)
                for sw_ in range(2):
                    put_row(KnS[64:65, sw_, kvh], None, 0, 16, const=0.0)
                    put_row(KnS[65:66, sw_, kvh], [[0, 4], [1, 4]], 0, 16)
                    put_row(KnS[66:67, sw_, kvh], None, 0, 16, const=1.0)
                    put_row(KnS[67:68, sw_, kvh], None, 0, 16, const=0.0)
            fw.barrier()
        maskC7 = fw.sb(s2, [128, 1], F32, "maskC7")
        iota(tmpf[:, 0:1], [[0, 1]], base=0, cm=1)
        ts("pool", maskC7[:], tmpf[:, 0:1], 127.0, None, op0=ALU.is_lt)
        winm0 = fw.sb(s2, [128, 4], BF16, "winm0")
        iota(tmpf[:, 0:4], [[-1, 4]], base=0, cm=1)
        ts("pool", winm0[:], tmpf[:, 0:4], 0.0, None, op0=ALU.is_gt)
        newm = fw.sb(s2, [16, 4, 4], BF16, "newm")
        for b in range(4):
            iota(tmpf[0:16, 0:4], [[-1, 4]], base=-4 * b, cm=1)
            ts("pool", tmpf[0:16, 4:8], tmpf[0:16, 0:4], 0.0, None, op0=ALU.is_le)
            iota(tmpf[0:16, 8:12], [[0, 4]], base=-4 * b, cm=1)
            ts("pool", tmpf[0:16, 8:12], tmpf[0:16, 8:12], 0.0, None, op0=ALU.is_ge)
            tt("pool", newm[:, b, :], tmpf[0:16, 4:8], tmpf[0:16, 8:12], ALU.mult)
        mimpS = fw.sb(s2, [128, 8, 256], BF16, "mimpS")
        idx = [fw.sb(s2, [128, 1], I32, f"S1idx{b}") for b in range(4)]
        for b in range(4):
            dma(idx[b][:], V(None, S["ptab"].a[b].rearrange("(p o) -> p o", o=1)))
        with ExitStack() as tm:
            mtmp = fw.sb(tm, [128, 3, 256], F32, "mtmp")
            for rb in range(8):
                iota(mtmp[:, 0, :], [[-4, 256]], base=rb - 1, cm=8)
                stt("dve", mtmp[:, 1, :], mtmp[:, 0, :], -1.0, mtmp[:, 0, :], ALU.mult, ALU.max)
                ts("pool", mtmp[:, 0, :], mtmp[:, 1, :], 2.0, 0.5, op0=ALU.is_le, op1=ALU.mult)
                ts("pool", mtmp[:, 2, :], mtmp[:, 1, :], 1.0, 0.5, op0=ALU.is_le, op1=ALU.mult)
                tt("pool", mimpS[:, rb, :], mtmp[:, 0, :], mtmp[:, 2, :], ALU.add)
            fw.barrier()

        self.norm_T(S["xs"], xt, nb, hT, ident, npt, rows=16)
        psA, psB = nps(), nps()
        for k in range(8):
            mm(psA[:, 0:512], hT[:, k, :], win_b[:, k, 512:1024], st=(k == 0), sp=(k == 7))
        for k in range(8):
            mm(psB[:, 0:280], hT[:, k, :], win_b[:, k, 1024:1304], st=(k == 0), sp=(k == 7))
        cp("dve", pkv[:, 0:512], psA[:, 0:512])
        cp("act", pkv[:, 512:792], psB[:, 0:280])
        for i_, n_ in enumerate(("s_kc", "s_vc", "s_ks", "s_vs")):
            dma(S[n_], pkv[0:16, i_ * 128:(i_ + 1) * 128])
        tt("dve", gt[:], pkv[:, 768:792], bgate[:], ALU.add)
        self.sigm(gt[:], gt[:])
        ms("pool", vnS[:], 1.0)
        cp("dve", vnS[:, 0, :, 0:64], pkv[0:16, 384:512].re("p (h d) -> p h d", h=2))
        cp("dve", vnS[:, 1, :, 0:64], pkv[0:16, 640:768].re("p (h d) -> p h d", h=2))
        psQ = nps()
        for h in range(8):
            for k in range(8):
                mm(psQ[0:64, h * 16:(h + 1) * 16], win_b[:, k, 64 * h:64 * h + 64], hT[:, k, 0:16], st=(k == 0), sp=(k == 7))
        act(qTs[:].re("p h t -> p (h t)"), psQ[0:64, 0:128], AF.Copy, scale=0.125)
        psK = nps()
        for gi, c0 in enumerate((768, 832, 1024, 1088)):
            for k in range(8):
                mm(psK[0:64, gi * 16:(gi + 1) * 16], win_b[:, k, c0:c0 + 64], hT[:, k, 0:16], st=(k == 0), sp=(k == 7))
        cp("dve", KnS[0:64].re("p a k t -> p (a k t)"), psK[0:64, 0:64])

        gks = [fw.sb(s2, [128, 4096], F32, f"gk{i}") for i in range(2)]
        gvs = [fw.sb(s2, [128, 4096], F32, f"gv{i}") for i in range(2)]
        wks, wvs = gks[1][:, 0:512], gvs[1][:, 0:512]
        chunks1 = [(b_, i_) for b_ in range(4) for i_ in range(4)]

        def issue1(n):
            b_, i_ = chunks1[n]
            gather(self, gks[n % 2][:], S["pool_ks"], idx[b_][:, :], 32 * i_)
            gather(self, gvs[n % 2][:], S["pool_vs"], idx[b_][:, :], 32 * i_)

        issue1(0)
        n1 = 0
        kb = fw.sb(s2, [128, 32, 128], BF16, "kbS")
        vbp = fw.sb(s2, [128, 32, 2, 65], BF16, "vbp")
        ms("pool", vbp[:], 1.0)
        vwS = fw.sb(s2, [128, 4, 2, 65], BF16, "vwS")
        ms("pool", vwS[:], 1.0)
        ptc = fw.sb(s2, [128, 8, 16], BF16, "ptc")
        ptsb = fw.sb(s2, [128, 32, 16], BF16, "ptsb")
        ptw = fw.sb(s2, [128, 4, 16], BF16, "ptw")
        ptn = fw.sb(s2, [16, 16], BF16, "ptn")
        maskEO = [fw.sb(s2, [128, 2, 4, 4], BF16, f"maskEO{k}") for k in range(2)]
        obr = [[fw.sb(s2, [4, 4, 65], F32, f"obrS{k}{i}") for i in range(3)] for k in range(2)]
        imp = fw.sb(s2, [4, 256], F32, "impS")
        imp2 = fw.sb(s2, [4, 256], F32, "imp2S")
        mx1 = fw.sb(s2, [4, 8], F32, "mx1S")
        mx2 = fw.sb(s2, [4, 8], F32, "mx2S")
        sel01 = imp2
        selT = fw.sb(s2, [128, 8], F32, "selTS")
        gtb = fw.sb(s2, [4, 24], F32, "gtb")
        rd = fw.sb(s2, [4, 3, 4], F32, "rdS")
        sc3 = fw.sb(s2, [4, 3, 4], F32, "sc3S")
        onsa = fw.sb(s2, [4, 4, 64], F32, "onsaS")
        otmp = fw.sb(s2, [4, 4, 64], F32, "otmpS")
        ss4 = fw.sb(s2, [4, 4], F32, "ss4S")
        rs4 = fw.sb(s2, [4, 4], F32, "rs4S")
        onb = fw.sb(s2, [4, 512], BF16, "onbS")
        ms("pool", mixin[:], 0.0)
        for b in range(4):
            cp("dve", QS[0:64].re("p i k (g q) -> p i (k g) q", g=4),
               qTs[:, :, 4 * b:4 * b + 4].un(1).bc([64, 4, 8, 4]))
            dma(kcSb[0:64].re("p k r t -> p k (r t)"), V(S["kcS_d"], S["kcS_d"][b].a.rearrange("k d n -> d k n")))
            dma(vcSb[:].re("p r k d -> p (r k) d")[:, :, 0:64], V(S["vcS_d"], S["vcS_d"][b].a.rearrange("p r k d -> p (r k) d")))
            dma(gtb[:], gt[4 * b:4 * b + 4, :])
            for kvh in range(2):
                Sc = nps()
                for rb in range(8):
                    mm(Sc[:, rb * 16:(rb + 1) * 16], kcSb[:, kvh, rb, :], QS[:, 0, kvh, :])
                act(ptc[:].re("p r c -> p (r c)"), Sc[:, 0:128], AF.Exp)
                ts("dve", ptc[:, 7, :], ptc[:, 7, :], maskC7[:, 0:1], None, op0=ALU.mult)
                accC = psa[0]
                impP = [nps(), nps()]
                for rb in range(8):
                    for g in range(4):
                        mm(accC[0:4, g * 65:(g + 1) * 65], ptc[:, rb, 4 * g:4 * g + 4], vcSb[:, rb, kvh, :],
                           st=(rb == 0), sp=(rb == 7))
                        mm(impP[g // 2][0:4, (g % 2) * 256:(g % 2) * 256 + 256], ptc[:, rb, 4 * g:4 * g + 4],
                           mimpS[:, rb, :], st=(rb == 0), sp=(rb == 7))
                cp("dve", obr[kvh][0][:].re("p g d -> p (g d)"), accC[0:4, 0:260])
                ts("dve", rd[:, 0, :], obr[kvh][0][:, :, 64], 1e-30, None, op0=ALU.max)
                self.recip(rd[:, 0, :], rd[:, 0, :])
                ts("dve", imp[:], impP[0][0:4, 0:256], rd[:, 0, 0:1], None, op0=ALU.mult)
                for g in range(1, 4):
                    stt("dve", imp[:], impP[g // 2][0:4, (g % 2) * 256:(g % 2) * 256 + 256], rd[:, 0, g:g + 1], imp[:],
                        ALU.mult, ALU.add)
                ms("dve", imp[:, 0:1], 3e9)
                ms("dve", imp[:, 255:256], 1e9)
                fw.op("dve", lambda e: e.max(out=mx1[:].a, in_=imp[:].a), reads=[imp], writes=[mx1])
                fw.op("dve", lambda e: e.match_replace(out=imp2[:].a, in_to_replace=mx1[:].a, in_values=imp[:].a,
                                                       imm_value=-1e30), reads=[imp, mx1], writes=[imp2])
                fw.op("dve", lambda e: e.max(out=mx2[:].a, in_=imp2[:].a), reads=[imp2], writes=[mx2])
                ts("dve", sel01[:], imp[:], mx2[:, 6:7], None, op0=ALU.is_ge)
                psT = nps()
                tr(psT[:, 0:4], sel01[0:4, 0:256:2], identf[0:4, 0:4])
                tr(psT[:, 4:8], sel01[0:4, 1:256:2], identf[0:4, 0:4])
                cp("dve", selT[:], psT[:, 0:8])
                cp("dve", maskEO[kvh][:], selT[:].re("p (e q) -> p e q", e=2).un(2).bc([128, 2, 4, 4]))
            accS = [psa[0], psa[1]]
            for i in range(4):
                gk, gv = gks[n1 % 2], gvs[n1 % 2]
                if n1 + 1 < len(chunks1):
                    issue1(n1 + 1)
                n1 += 1
                cp("dve", kb[:, 0:16, :].re("p r f -> p (r f)"), gk[:, 0:2048])
                cp("act", kb[:, 16:32, :].re("p r f -> p (r f)"), gk[:, 2048:4096])
                cp("dve", vbp[:, 0:16, :, 0:64], gv[:, 0:2048].re("p (r h d) -> p r h d", r=16, h=2))
                cp("act", vbp[:, 16:32, :, 0:64], gv[:, 2048:4096].re("p (r h d) -> p r h d", r=16, h=2))
                for kvh in range(2):
                    for g8 in range(4):
                        pt = npt()
                        for r8 in range(8):
                            tr(pt[0:64, r8 * 128:(r8 + 1) * 128], kb[:, 8 * g8 + r8, kvh * 64:(kvh + 1) * 64], ident[:])
                        cp("act" if g8 % 2 == 0 else "dve", ksS[0:64, kvh, 8 * g8:8 * g8 + 8, :],
                           pt[0:64, :].re("p (r t) -> p r t", r=8))
                for kvh in range(2):
                    Ss = nps()
                    for r_ in range(32):
                        mm(Ss[:, r_ * 16:(r_ + 1) * 16], ksS[:, kvh, r_, :], QS[:, i, kvh, :])
                    act(ptsb[:].re("p r c -> p (r c)"), Ss[:, :], AF.Exp)
                    tt("dve", ptsb[:].re("p r (g q) -> p r g q", g=4), ptsb[:].re("p r (g q) -> p r g q", g=4),
                       maskEO[kvh][:, i // 2].un(1).bc([128, 32, 4, 4]), ALU.mult)
                    for r_ in range(32):
                        for g in range(4):
                            mm(accS[kvh][0:4, g * 65:(g + 1) * 65], ptsb[:, r_, 4 * g:4 * g + 4], vbp[:, r_, kvh, :],
                               st=(i == 0 and r_ == 0), sp=False)
            for kvh in range(2):
                Sn = nps()
                mm(Sn[0:16, 0:16], KnS[:, 0, kvh, :], QS[:, 0, kvh, :])
                act(ptn[:], Sn[0:16, 0:16], AF.Exp)
                tt("dve", ptn[:].re("p (g q) -> p g q", g=4), ptn[:].re("p (g q) -> p g q", g=4),
                   newm[:, b, :].un(1).bc([16, 4, 4]), ALU.mult)
                for g in range(4):
                    mm(accS[kvh][0:4, g * 65:(g + 1) * 65], ptn[:, 4 * g:4 * g + 4], vnS[:, 0, kvh, :], st=False, sp=True)
                cp("dve", obr[kvh][1][:].re("p g d -> p (g d)"), accS[kvh][0:4, 0:260])
            dma(wks.re("p (a f) -> p a f", a=4), V(None, S["stk"].a[b].rearrange("(a p) f -> p a f", p=128)))
            dma(wvs.re("p (a f) -> p a f", a=4), V(None, S["stv"].a[b].rearrange("(a p) f -> p a f", p=128)))
            cp("dve", kb[:, 0:4, :].re("p r f -> p (r f)"), wks)
            cp("dve", vwS[:, :, :, 0:64], wvs.re("p (a h d) -> p a h d", a=4, h=2))
            pt = npt()
            for kvh in range(2):
                for a in range(4):
                    tr(pt[0:64, (kvh * 4 + a) * 128:(kvh * 4 + a + 1) * 128], kb[:, a, kvh * 64:(kvh + 1) * 64], ident[:])
            cp("act", kwS[0:64].re("p k a t -> p (k a t)"), pt[0:64, :])
            accW = [psa[0], psa[1]]
            for kvh in range(2):
                Sw = nps()
                for a in range(4):
                    mm(Sw[:, a * 16:(a + 1) * 16], kwS[:, kvh, a, :], QS[:, 0, kvh, :])
                act(ptw[:].re("p a c -> p (a c)"), Sw[:, 0:64], AF.Exp)
                tt("dve", ptw[:, 0, :].re("p (g q) -> p g q", g=4), ptw[:, 0, :].re("p (g q) -> p g q", g=4),
                   winm0[:].un(1).bc([128, 4, 4]), ALU.mult)
                for a in range(4):
                    for g in range(4):
                        mm(accW[kvh][0:4, g * 65:(g + 1) * 65], ptw[:, a, 4 * g:4 * g + 4], vwS[:, a, kvh, :],
                           st=(a == 0), sp=False)
                Sn = nps()
                mm(Sn[0:16, 0:16], KnS[:, 1, kvh, :], QS[:, 0, kvh, :])
                act(ptn[:], Sn[0:16, 0:16], AF.Exp)
                tt("dve", ptn[:].re("p (g q) -> p g q", g=4), ptn[:].re("p (g q) -> p g q", g=4),
                   newm[:, b, :].un(1).bc([16, 4, 4]), ALU.mult)
                for g in range(4):
                    mm(accW[kvh][0:4, g * 65:(g + 1) * 65], ptn[:, 4 * g:4 * g + 4], vnS[:, 1, kvh, :], st=False, sp=True)
                cp("dve", obr[kvh][2][:].re("p g d -> p (g d)"), accW[kvh][0:4, 0:260])
            for nm_, src_, c0 in (("s_kw", "stk", 512), ("s_vw", "stv", 640)):
                dma(V(None, S[nm_].a[b, 0:508, :]), V(None, S[src_].a[b, 4:512, :]))
                dma(V(None, S[nm_].a[b, 508:512, :]), pkv[4 * b:4 * b + 4, c0:c0 + 128])
            for kvh in range(2):
                for br in range(1, 3):
                    ts("dve", rd[:, br, :], obr[kvh][br][:, :, 64], 1e-30, None, op0=ALU.max)
                    self.recip(rd[:, br, :], rd[:, br, :])
                ts("dve", rd[:, 0, :], obr[kvh][0][:, :, 64], 1e-30, None, op0=ALU.max)
                self.recip(rd[:, 0, :], rd[:, 0, :])
                gvw = gtb[:, 12 * kvh:12 * kvh + 12].re("p (g b) -> p b g", b=3)
                tt("dve", sc3[:], rd[:], gvw, ALU.mult)
                tt("dve", onsa[:], obr[kvh][0][:, :, 0:64], sc3[:, 0, :].un(2).bc([4, 4, 64]), ALU.mult)
                for br in (1, 2):
                    tt("dve", otmp[:], obr[kvh][br][:, :, 0:64], sc3[:, br, :].un(2).bc([4, 4, 64]), ALU.mult)
                    tt("dve", onsa[:], onsa[:], otmp[:], ALU.add)
                tt("dve", otmp[:], onsa[:], onsa[:], ALU.mult)
                self.rsum(ss4[:], otmp[:])
                self.rstd(rs4[:], ss4[:], 1.0 / 64)
                tt("dve", onb[:, 256 * kvh:256 * kvh + 256].re("p (g d) -> p g d", g=4), onsa[:],
                   rs4[:].un(2).bc([4, 4, 64]), ALU.mult)
            dma(mixin[4 * b:4 * b + 4, 0:512], onb[:])
        fw.barrier()
        s2.close()
        self.sample_mlstm(s, L, hT, mixin)
        mT = fw.sb(s, [128, 8, 128], BF16, "mTS")
        x1t = fw.sb(s, [128, D], F32, "x1tS")
        ptm = npt()
        for k in range(8):
            tr(ptm[:, k * 128:(k + 1) * 128], mixin[:, k * 128:(k + 1) * 128], ident[:])
        cp("act", mT[:].re("p k t -> p (k t)"), ptm[:])
        for g in range(2):
            ps = nps()
            for k in range(8):
                mm(ps[:, :], mT[:, k, :], wout_b[:, k, g * 512:(g + 1) * 512], st=(k == 0), sp=(k == 7))
            tt("dve", x1t[:, g * 512:(g + 1) * 512], xt[:, g * 512:(g + 1) * 512], ps[:, :], ALU.add)
        dma(S["x1s"][:, :], x1t[0:16, :])
        fw.barrier()


Builder.sample_pass1 = sample_pass1


def sample_mlstm(self, s, L, hT, mixin):
    fw, S = self.fw, self.S
    mm, tr, act, ts, tt, stt, cp, ms, iota, dma, dmas = (self.mm, self.tr, self.act, self.ts, self.tt, self.stt,
                                                         self.cp, self.ms, self.iota, self.dma, self.dmas)
    win_b, wqm_b, wkm_b = L["win_b"], L["wqm_b"], L["wkm_b"]
    identf, nps, cw, cb, bif, tmpf = L["identf"], L["nps"], L["cw"], L["cb"], L["bif"], L["tmpf"]
    KSC = 128.0 ** -0.5
    E = fw.sb(s, [4, 128], F32, "E4")
    iota(E[:], [[1, 128]], base=0, cm=-4)
    Eb = fw.sb(s, [4, 128], F32, "E4b")
    ts("pool", Eb[:], E[:], 0.0, None, op0=ALU.is_ge)
    ts("pool", E[:], E[:], 3.0, None, op0=ALU.is_le)
    tt("pool", E[:], E[:], Eb[:], ALU.mult)
    triS = fw.sb(s, [128, 128], F32, "triS")
    ps = nps()
    mm(ps[:, 0:128], E[:], E[:])
    tt("dve", triS[:], ps[:, 0:128], L["tri_le"][:], ALU.mult)
    bdS = fw.sb(s, [16, 16], BF16, "bdS")
    cp("dve", bdS[:], triS[0:16, 0:16])
    cselS = fw.sb(s, [128, 4, 128], F32, "cselS")
    d4 = fw.sb(s, [4, 4, 128], F32, "d4")
    cp("dve", d4[:], identf[0:4, 0:4].un(2).bc([4, 4, 128]))
    ps = nps()
    for b in range(4):
        mm(ps[:, b * 128:(b + 1) * 128], E[:], d4[:, b, :])
    cp("dve", cselS[:].re("p b m -> p (b m)"), ps[:, :])
    psV, psO, psG = nps(), nps(), nps()
    for k in range(8):
        mm(psV[:, 0:512], hT[:, k, :], win_b[:, k, 1816:2328], st=(k == 0), sp=(k == 7))
    for k in range(8):
        mm(psO[:, 0:512], hT[:, k, :], win_b[:, k, 2328:2840], st=(k == 0), sp=(k == 7))
    for k in range(8):
        mm(psG[:, 0:8], hT[:, k, :], win_b[:, k, 2840:2848], st=(k == 0), sp=(k == 7))
    gif = fw.sb(s, [128, 8], F32, "gifS")
    l1 = fw.sb(s, [128, 4], F32, "l1S")
    sigo = fw.sb(s, [128, 512], F32, "sigoS")
    tt("dve", gif[:], psG[:, 0:8], bif[:], ALU.add)
    act(l1[:], gif[:, 4:8], AF.Exp, scale=-1.0)
    act(l1[:], l1[:], AF.Ln, bias=1.0)
    self.sigm(sigo[:], psO[:, 0:512])
    psC = nps()
    mm(psC[:, 0:4], triS[:], l1[:])
    for b in range(4):
        mm(psC[:, 4 + 4 * b:8 + 4 * b], cselS[:, b, :], l1[:])
    gsb = fw.sb(s, [128, 20], F32, "gsbS")
    cp("dve", gsb[:], psC[:, 0:20])
    wl = fw.sb(s, [128, 4], F32, "wlS")
    ul = fw.sb(s, [128, 4], F32, "ulS")
    tmp4 = fw.sb(s, [128, 4], F32, "tmp4S")
    own = fw.sb(s, [128, 4], F32, "ownS")
    dec = fw.sb(s, [128, 4], F32, "decS")
    ebt = fw.sb(s, [128, 16], F32, "ebtS")
    act(wl[:], gsb[:, 0:4], AF.Exp, scale=-1.0)
    tt("dve", tmp4[:], gif[:, 0:4], gsb[:, 0:4], ALU.add)
    act(ul[:], tmp4[:], AF.Exp)
    act(ebt[:], gsb[:, 4:20], AF.Exp, scale=-1.0)
    ts("dve", own[:], gsb[:, 4:8], cselS[:, 0, 0:1], None, op0=ALU.mult)
    for b in range(1, 4):
        stt("dve", own[:], gsb[:, 4 + 4 * b:8 + 4 * b], cselS[:, b, 0:1], own[:], ALU.mult, ALU.add)
    tt("dve", dec[:], tmp4[:], own[:], ALU.subtract)
    vmu = fw.sb(s, [128, 4, 129], BF16, "vmuS")
    tt("dve", vmu[:, :, 0:128], psV[:, 0:512].re("p (h e) -> p h e", h=4), ul[:].un(2).bc([128, 4, 128]), ALU.mult)
    cp("dve", vmu[:, :, 128], ul[:])
    psT = nps()
    tr(psT[0:4, 0:128], dec[:], identf[:])
    mm(psT[0:4, 128:132], l1[:], cselS[:, :, 0])
    tsb = fw.sb(s, [4, 132], F32, "tsbS")
    cp("dve", tsb[:], psT[0:4, 0:132])
    Dm = fw.sb(s, [4, 4], F32, "DmS")
    self.rmax(Dm[:], tsb[:, 0:16].re("p (b i) -> p b i", b=4))
    R = fw.sb(s, [4, 4], F32, "RS")
    dmas(R[:], V(None, S["sm"].a.rearrange("(b h) -> h b", h=4)))
    tt("dve", R[:], R[:], tsb[:, 128:132], ALU.subtract)
    tt("dve", R[:], R[:], Dm[:], ALU.max)
    dmas(V(None, S["s_m"].a.rearrange("(b h) -> h b", h=4)), R[:])
    xcv = fw.sb(s, [128, 4, 4, 7], F32, "xcvS")
    for b in range(4):
        for ch in range(4):
            dmas(xcv[:, ch, b, 0:3], V(None, S["sconv"].a[b, :, ch * 128:(ch + 1) * 128].rearrange("j p -> p j")))
    psX = nps()
    for ch in range(4):
        for k in range(8):
            mm(psX[:, ch * 16:(ch + 1) * 16], win_b[:, k, 1304 + ch * 128:1432 + ch * 128], hT[:, k, 0:16],
               st=(k == 0), sp=(k == 7))
    cp("act", xcv[:, :, :, 3:7], psX[:, 0:64].re("p (c b i) -> p c b i", c=4, b=4))
    cacc = fw.sb(s, [128, 4, 16], F32, "caccS")
    for ch in range(4):
        cv = cacc[:, ch, :].re("p (b i) -> p b i", b=4)
        ts("dve", cv, xcv[:, ch, :, 0:4], cw[:, ch, 0:1], cb[:, ch:ch + 1], op0=ALU.mult, op1=ALU.add)
        for j in range(1, 4):
            stt("dve", cv, xcv[:, ch, :, j:j + 4], cw[:, ch, j:j + 1], cv, ALU.mult, ALU.add)
    xc = fw.sb(s, [128, 4, 16], BF16, "xcS")
    sgc = fw.sb(s, [128, 4, 16], F32, "sgcS")
    self.sigm(sgc[:], cacc[:])
    tt("dve", xc[:], cacc[:], sgc[:], ALU.mult)
    for b in range(4):
        for j in range(3):
            dmas(V(None, S["s_conv"].a[b, j].rearrange("(c p) -> p c", p=128)), xcv[:, :, b, 4 + j])
    qmT = fw.sb(s, [128, 4, 16], BF16, "qmTS")
    kmT = fw.sb(s, [128, 4, 16], BF16, "kmTS")
    qmS = [fw.sb(s, [128, 4, 16], BF16, f"qmSS{b}") for b in range(4)]
    kmS = [fw.sb(s, [16, 4, 128], BF16, f"kmSS{b}") for b in range(4)]
    psq = nps()
    for h in range(4):
        mm(psq[:, h * 16:(h + 1) * 16], wqm_b[:, h, :], xc[:, h, :])
    cp("act", qmT[:].re("p h t -> p (h t)"), psq[:, 0:64])
    for b in range(4):
        ms("pool", qmS[b][:], 0.0)
        cp("dve", qmS[b][:, :, 4 * b:4 * b + 4], psq[:, 0:64].re("p (h t) -> p h t", h=4)[:, :, 4 * b:4 * b + 4])
    psk = nps()
    for h in range(4):
        mm(psk[:, h * 16:(h + 1) * 16], wkm_b[:, h, :], xc[:, h, :])
    act(kmT[:].re("p h t -> p (h t)"), psk[:, 0:64], AF.Copy, scale=KSC)
    pskt = nps()
    for h in range(4):
        mm(pskt[0:16, h * 128:(h + 1) * 128], xc[:, h, :], wkm_b[:, h, :])
    for b in range(4):
        ts("dve", kmS[b][:].re("p h t -> p (h t)"), pskt[0:16, :], cselS[0:16, b, 0:1], KSC, op0=ALU.mult, op1=ALU.mult)
    psqk = nps()
    for h in range(4):
        mm(psqk[0:16, h * 16:(h + 1) * 16], kmT[:, h, :], qmT[:, h, :])
    mqk = fw.sb(s, [16, 4, 16], BF16, "mqkS")
    tt("dve", mqk[:], psqk[0:16, 0:64].re("p (h t) -> p h t", h=4), bdS[:].un(1).bc([16, 4, 16]), ALU.mult)
    em0 = fw.sb(s, [128, 16], F32, "em0")
    dma(em0[:], V(None, S["sm"].a.partition_broadcast(128)))
    act(em0[:], em0[:], AF.Exp)
    Sf = [fw.sb(s, [128, 4, 129], F32, f"SfS{b}") for b in range(4)]
    Sb0 = [fw.sb(s, [128, 4, 129], BF16, f"Sb0S{b}") for b in range(4)]
    cst_ = [fw.sb(s, [128, 128], F32, f"c0st{i}") for i in range(2)]
    dS = fw.sb(s, [128, 4, 129], F32, "dSS")
    for b in range(4):
        for h in range(4):
            st = cst_[h % 2]
            dma(st[:], V(None, S["sC"].a[b, h]))
            ps = nps()
            tr(ps[:, 0:128], st[:], identf[:])
            cp("dve", Sf[b][:, h, 0:128], ps[:, 0:128])
        dmas(Sf[b][:, :, 128], V(None, S["sn"].a[b].rearrange("h d -> d h")))
        tt("dve", Sf[b][:], Sf[b][:], em0[:, 4 * b:4 * b + 4].un(2).bc([128, 4, 129]), ALU.mult)
        cp("pool", Sb0[b][:], Sf[b][:])
        pd = [nps(), nps()]
        for h in range(4):
            mm(pd[h // 2][:, (h % 2) * 129:(h % 2) * 129 + 129], kmS[b][:, h, :], vmu[0:16, h, :])
        tt("dve", Sf[b][:], Sf[b][:], ebt[:, 4 * b:4 * b + 4].un(2).bc([128, 4, 129]), ALU.mult)
        for hh in range(2):
            tt("dve", dS[:, 2 * hh:2 * hh + 2, :], pd[hh][:, 0:258].re("p (h e) -> p h e", h=2),
               ebt[:, 4 * b + 2 * hh:4 * b + 2 * hh + 2].un(2).bc([128, 2, 129]), ALU.mult)
        tt("dve", Sf[b][:], Sf[b][:], dS[:], ALU.add)
    pa = [nps(), nps()]
    for h in range(4):
        o_ = pa[h // 2][0:16, (h % 2) * 129:(h % 2) * 129 + 129]
        mm(o_, mqk[:, h, :], vmu[0:16, h, :], st=True, sp=False)
        for b in range(4):
            mm(o_, qmS[b][:, h, :], Sb0[b][:, h, :], st=False, sp=(b == 3))
    dn = fw.sb(s, [16, 4], F32, "dnS")
    t4 = fw.sb(s, [16, 4], F32, "t4S")
    hout = fw.sb(s, [16, 4, 128], F32, "houtS")
    hsq = fw.sb(s, [16, 4, 128], F32, "hsqS")
    ss4 = fw.sb(s, [16, 4], F32, "ss4m")
    rs4 = fw.sb(s, [16, 4], F32, "rs4m")
    for hh in range(2):
        av = pa[hh][0:16, 0:258].re("p (h e) -> p h e", h=2)
        tt("dve", dn[:, 2 * hh:2 * hh + 2], av[:, :, 128], wl[0:16, 2 * hh:2 * hh + 2], ALU.mult)
    stt("dve", t4[:], dn[:], -1.0, dn[:], ALU.mult, ALU.max)
    ts("dve", t4[:], t4[:], 1.0, None, op0=ALU.max)
    self.recip(t4[:], t4[:])
    tt("dve", t4[:], t4[:], wl[0:16, :], ALU.mult)
    for hh in range(2):
        av = pa[hh][0:16, 0:258].re("p (h e) -> p h e", h=2)
        tt("dve", hout[:, 2 * hh:2 * hh + 2, :], av[:, :, 0:128], t4[:, 2 * hh:2 * hh + 2].un(2).bc([16, 2, 128]), ALU.mult)
    tt("dve", hsq[:], hout[:], hout[:], ALU.mult)
    self.rsum(ss4[:], hsq[:])
    self.rstd(rs4[:], ss4[:], 1.0 / 128)
    tt("dve", hout[:], hout[:], rs4[:].un(2).bc([16, 4, 128]), ALU.mult)
    tt("dve", mixin[0:16, 512:1024], hout[:].re("p h e -> p (h e)"), sigo[0:16, :], ALU.mult)
    Rd = fw.sb(s, [4, 4, 4], F32, "RdS")
    tt("dve", Rd[:], R[:].un(2).bc([4, 4, 4]), identf[0:4, 0:4].un(1).bc([4, 4, 4]), ALU.mult)
    ones4 = fw.sb(s, [4, 128], F32, "ones4S")
    ms("pool", ones4[:], 1.0)
    ps = nps()
    mm(ps[:, 0:16], ones4[:], Rd[:].re("p b h -> p (b h)"))
    esc = fw.sb(s, [128, 16], F32, "escS")
    act(esc[:], ps[:, 0:16], AF.Exp, scale=-1.0)
    for b in range(4):
        tt("dve", Sf[b][:], Sf[b][:], esc[:, 4 * b:4 * b + 4].un(2).bc([128, 4, 129]), ALU.mult)
        dmas(V(None, S["s_n"].a[b].rearrange("h d -> d h")), Sf[b][:, :, 128])
        for h in range(4):
            ps = nps()
            tr(ps[:, 0:128], Sf[b][:, h, 0:128], identf[:])
            st = cst_[h % 2]
            cp("dve", st[:], ps[:, 0:128])
            dma(V(None, S["s_C"].a[b, h]), st[:])


Builder.sample_mlstm = sample_mlstm
Builder.sample_s0 = sample_s0

W_NAMES = ["w_in", "g_mix", "b_gate", "cmp_pe_k", "cmp_w1_k", "cmp_b1_k", "cmp_w2_k", "cmp_pe_v", "cmp_w1_v",
           "cmp_b1_v", "cmp_w2_v", "g_head_nsa", "conv_w", "conv_b", "w_qm", "w_km", "b_i", "b_f", "g_head_m",
           "w_out", "g_xa", "g_mem", "w_xq", "w_xk", "w_xv", "w_xo", "g_ffn", "w_gate", "w_up", "w_down", "g_final"]


def build_program(NT=32, sample=True, debug=False):
    nc = bass.Bass("TRN2", target_bir_lowering=False)
    b = Builder(nc, NT=NT, sample=sample, debug=debug)
    b.build()
    return nc, b


def core_inputs(inp, c, b, NT=32):
    f = lambda a: np.ascontiguousarray(a, dtype=np.float32)
    T = NT * 128
    m = {"xp": f(inp["x_prompt"][c, :T]), "memp": f(inp["mem_prompt"][c])}
    for n in W_NAMES:
        a = np.asarray(inp[n])
        if n != "g_final":
            a = a[0]
        m[n] = f(a).reshape(b.io[n].shape)
    if b.sample:
        sl = slice(4 * c, 4 * c + 4)
        m["xs"] = f(inp["x_sample"][sl]).reshape(16, D)
        for n, k in (("pool_kc", "cache_k_cmp"), ("pool_vc", "cache_v_cmp"), ("pool_ks", "cache_k_slc"), ("pool_vs", "cache_v_slc")):
            m[n] = np.asarray(inp[k][0], dtype=np.float32).reshape(5120, 16384)
        m["ptab"] = np.ascontiguousarray(inp["page_table"][sl], dtype=np.int32)
        m["stk"] = f(inp["state_k_win"][0, sl]).reshape(4, 512, 128)
        m["stv"] = f(inp["state_v_win"][0, sl]).reshape(4, 512, 128)
        m["sconv"] = f(inp["state_conv"][0, sl])
        m["sC"] = f(inp["state_C"][0, sl])
        m["sn"] = f(inp["state_n"][0, sl])
        m["sm"] = f(inp["state_m"][0, sl]).reshape(16)
        m["cmk"] = f(inp["cache_mem_k"][0, sl]).reshape(4, 256, D)
        m["cmv"] = f(inp["cache_mem_v"][0, sl]).reshape(4, 256, D)
    return {k: v for k, v in m.items() if k in b.io}


_PROG = {}


def kernel(**inp):
    n = 8
    if "p" not in _PROG:
        _PROG["p"] = build_program()
    nc, b = _PROG["p"]
    in_maps = [core_inputs(inp, c, b) for c in range(n)]
    res = run_bass_kernel_spmd(nc, in_maps, core_ids=list(range(n)))
    R = res.results

    def st(name, shp, lead):
        a = np.stack([np.asarray(R[c][name], dtype=np.float32).reshape(shp) for c in range(n)])
        return np.ascontiguousarray(a.reshape(lead))

    outs = [st("y_p", (4096, D), (8, 4096, D)), st("y_s", (4, 4, D), (32, 4, D))]
    for nm in ("p_kc", "p_vc", "p_ks", "p_vs"):
        outs.append(st(nm, (4096, 2, 64), (1, 8, 4096, 2, 64)))
    for nm in ("p_kw", "p_vw"):
        outs.append(st(nm, (512, 2, 64), (1, 8, 512, 2, 64)))
    outs.append(st("p_C", (4, 128, 128), (1, 8, 4, 128, 128)))
    outs.append(st("p_n", (4, 128), (1, 8, 4, 128)))
    outs.append(st("p_m", (4,), (1, 8, 4)))
    outs.append(st("p_conv", (3, 512), (1, 8, 3, 512)))
    outs.append(st("p_mk", (256, 4, 256), (1, 8, 256, 4, 256)))
    outs.append(st("p_mv", (256, 4, 256), (1, 8, 256, 4, 256)))
    for nm in ("s_kc", "s_vc", "s_ks", "s_vs"):
        outs.append(st(nm, (4, 4, 2, 64), (1, 32, 4, 2, 64)))
    for nm in ("s_kw", "s_vw"):
        outs.append(st(nm, (4, 512, 2, 64), (1, 32, 512, 2, 64)))
    outs.append(st("s_C", (4, 4, 128, 128), (1, 32, 4, 128, 128)))
    outs.append(st("s_n", (4, 4, 128), (1, 32, 4, 128)))
    outs.append(st("s_m", (4, 4), (1, 32, 4)))
    outs.append(st("s_conv", (4, 3, 512), (1, 32, 3, 512)))
    return tuple(outs)
```

```python
import numpy as np
from contextlib import ExitStack
import concourse.bass as bass
import concourse.mybir as mybir
from concourse.bass_utils import run_bass_kernel_spmd

F32 = mybir.dt.float32
BF16 = mybir.dt.bfloat16
I32 = mybir.dt.int32
AF = mybir.ActivationFunctionType
ALU = mybir.AluOpType
AX = mybir.AxisListType

D = 1024
NEG = -30000.0
EPS = 1e-6
IN_COLS = 2848
DFF = 2816


class V:
    __slots__ = ("b", "a")

    def __init__(self, b, a):
        self.b = b
        self.a = a

    def __getitem__(self, k):
        return V(self.b, self.a[k])

    def re(self, p, **kw):
        return V(self.b, self.a.rearrange(p, **kw))

    def bc(self, shape):
        return V(self.b, self.a.to_broadcast(list(shape)))

    def un(self, ax):
        return V(self.b, self.a.unsqueeze(ax))


class Buf:
    __slots__ = ("t", "w", "r", "name", "psum", "fresh", "quads")

    def __init__(self, t, name="", psum=False):
        self.t = t
        self.w = None
        self.r = []
        self.name = name
        self.psum = psum
        self.fresh = True
        self.quads = set()

    def __getitem__(self, k):
        return V(self, self.t[k])


class DSem:
    def __init__(self, nc, name):
        self.sem = nc.alloc_semaphore(name)
        self.val = 0


class FW:
    ENG = ("pe", "act", "dve", "pool", "sp")

    def __init__(self, nc, n_dsem=10, same_engine_sync=True):
        self.nc = nc
        self.e = {"pe": nc.tensor, "act": nc.scalar, "dve": nc.vector, "pool": nc.gpsimd, "sp": nc.sync}
        self.gen = {k: 0 for k in self.ENG}
        self.sem = {k: nc.alloc_semaphore("S_" + k) for k in self.ENG}
        self.cnt = {k: 0 for k in self.ENG}
        self.seen = {k: {} for k in self.ENG}
        self.same = same_engine_sync
        self.dsems = {q: [DSem(nc, f"D{q}{i}") for i in range(n_dsem)] for q in ("sp", "pool", "act")}
        self.dnext = {q: 0 for q in self.dsems}
        self.nbuf = 0
        self.nins = 0

    def sb(self, stack, shape, dt=F32, name=None):
        self.nbuf += 1
        name = name or f"b{self.nbuf}"
        return Buf(stack.enter_context(self.nc.sbuf_tensor(name, list(shape), dt)), name)

    def ps(self, stack, shape, dt=F32, name=None):
        self.nbuf += 1
        name = name or f"p{self.nbuf}"
        return Buf(stack.enter_context(self.nc.psum_tensor(name, list(shape), dt)), name, psum=True)

    def _need(self, e, dep, waits):
        if dep is None:
            return
        kind, key, val, semh = dep
        if kind == "e" and key[0] == e and (not self.same or e == "pe"):
            return
        k = (kind, key if kind == "e" else id(key))
        if self.seen[e].get(k, 0) >= val:
            return
        cur = waits.get(k)
        if cur is None or cur[1] < val:
            waits[k] = (semh, val)

    def _emit_waits(self, e, reads, writes):
        waits = {}
        for b in reads:
            if b is not None:
                self._need(e, b.w, waits)
        for b in writes:
            if b is not None:
                self._need(e, b.w, waits)
                for d in b.r:
                    self._need(e, d, waits)
        eng = self.e[e]
        for k, (semh, val) in waits.items():
            eng.wait_ge(semh, val)
            self.seen[e][k] = val

    def op(self, e, fn, reads=(), writes=()):
        px = [b for b in reads if b is not None and b.psum]
        if px:
            reads = [b for b in reads if not (b is not None and b.psum)]
            writes = list(writes) + [b for b in px if b not in writes]
            if e != "pe":
                for b in px:
                    b.fresh = True
        self._emit_waits(e, reads, writes)
        ins = fn(self.e[e])
        if self.cnt[e] >= 50000:
            self.gen[e] += 1
            self.sem[e] = self.nc.alloc_semaphore(f"S_{e}_{self.gen[e]}")
            self.cnt[e] = 0
        self.cnt[e] += 1
        self.nins += 1
        ins.then_inc(self.sem[e], 1)
        dep = ("e", (e, self.gen[e]), self.cnt[e], self.sem[e])
        for b in reads:
            if b is not None:
                b.r.append(dep)
                if len(b.r) > 16:
                    b.r = self._compact(b.r)
        for b in writes:
            if b is not None:
                b.w = dep
                b.r = []
        return ins

    @staticmethod
    def _compact(lst):
        best = {}
        for d in lst:
            k = (d[0], d[1] if d[0] == "e" else id(d[1]))
            if k not in best or best[k][2] < d[2]:
                best[k] = d
        return list(best.values())

    def dma(self, q, o, i, fn=None, extra_reads=(), **kw):
        reads = [i.b] + list(extra_reads)
        writes = [o.b]
        self._emit_waits(q, reads, writes)
        ds = self.dsems[q][self.dnext[q]]
        self.dnext[q] = (self.dnext[q] + 1) % len(self.dsems[q])
        if ds.val > 0 and self.seen[q].get(("d", id(ds)), 0) < ds.val:
            self.e[q].wait_ge(ds.sem, ds.val)
            self.seen[q][("d", id(ds))] = ds.val
        if fn is None:
            ins = self.e[q].dma_start(out=o.a, in_=i.a, **kw)
        else:
            ins = fn(self.e[q])
        ds.val += 16
        self.nins += 1
        ins.then_inc(ds.sem, 16)
        dep = ("d", ds, ds.val, ds.sem)
        for b in reads:
            if b is not None:
                b.r.append(dep)
                if len(b.r) > 16:
                    b.r = self._compact(b.r)
        for b in writes:
            if b is not None:
                b.w = dep
                b.r = []
        return ins

    def barrier(self):
        for e in self.ENG:
            eng = self.e[e]
            for f in self.ENG:
                if f != e and self.cnt[f] > 0:
                    k = ("e", (f, self.gen[f]))
                    if self.seen[e].get(k, 0) < self.cnt[f]:
                        eng.wait_ge(self.sem[f], self.cnt[f])
                        self.seen[e][k] = self.cnt[f]
            for q in self.dsems:
                for ds in self.dsems[q]:
                    k = ("d", id(ds))
                    if ds.val > 0 and self.seen[e].get(k, 0) < ds.val:
                        eng.wait_ge(ds.sem, ds.val)
                        self.seen[e][k] = ds.val

    def finish(self):
        eng = self.e["sp"]
        for q in self.dsems:
            for ds in self.dsems[q]:
                if ds.val > 0:
                    eng.wait_ge(ds.sem, ds.val)


class RR:
    def __init__(self, items):
        self.items = list(items)
        self.i = 0

    def __call__(self):
        x = self.items[self.i]
        self.i = (self.i + 1) % len(self.items)
        return x


class Builder:
    def __init__(self, nc, NT=32, sample=True, debug=False):
        self.debug = debug
        self.nc = nc
        self.fw = FW(nc)
        self.NT = NT
        self.T = NT * 128
        self.sample = sample
        self.io = {}

    def din(self, name, shape, dt=F32):
        t = self.nc.dram_tensor(name, list(shape), dt, kind="ExternalInput").ap()
        self.io[name] = t
        return V(None, t)

    def dout(self, name, shape, dt=F32):
        t = self.nc.dram_tensor(name, list(shape), dt, kind="ExternalOutput").ap()
        self.io[name] = t
        return V(None, t)

    def dscr(self, name, shape, dt=F32):
        t = self.nc.dram_tensor(name, list(shape), dt, kind="ExternalOutput" if self.debug else "Internal").ap()
        if self.debug:
            self.io[name] = t
        return Buf(t, name)

    def dbg(self, name, v, dt=F32):
        if not self.debug:
            return
        o = self.dout("dbg_" + name, list(v.a.shape), dt)
        self.fw.dma("sp", o, v)

    def mm(self, o, l, r, st=True, sp=True):
        b = o.b
        p0 = o.a.base_partition() if hasattr(o.a, "base_partition") else 0
        q = set(range(p0 // 32, (p0 + o.a.shape[0] + 31) // 32))
        start = False
        if st:
            if b.fresh:
                start = True
                b.fresh = False
                b.quads = set(q)
            else:
                assert q <= b.quads, (b.name, q, b.quads)
        self.fw.op("pe", lambda e: e.matmul(o.a, lhsT=l.a, rhs=r.a, start=start, stop=sp, skip_group_check=True),
                   reads=[l.b, r.b], writes=[o.b])

    def tr(self, o, i, ident):
        self.fw.op("pe", lambda e: e.transpose(out=o.a, in_=i.a, identity=ident.a), reads=[i.b, ident.b], writes=[o.b])

    def act(self, o, i, f, scale=1.0, bias=0.0, acc=None):
        reads = [i.b]
        writes = [o.b]
        kw = {}
        if isinstance(bias, V):
            reads.append(bias.b)
            kw["bias"] = bias.a
        elif bias != 0.0:
            kw["bias"] = float(bias)
        if isinstance(scale, V):
            reads.append(scale.b)
            kw["scale"] = scale.a
        elif scale != 1.0:
            kw["scale"] = float(scale)
        if acc is not None:
            writes.append(acc.b)
            kw["accum_out"] = acc.a
        self.fw.op("act", lambda e: e.activation(out=o.a, in_=i.a, func=f, **kw), reads=reads, writes=writes)

    def ts(self, eng, o, i, s1, s2=None, op0=ALU.mult, op1=None):
        reads = [i.b]
        a1 = s1
        a2 = s2
        if isinstance(s1, V):
            reads.append(s1.b)
            a1 = s1.a
        if isinstance(s2, V):
            reads.append(s2.b)
            a2 = s2.a
        kw = {}
        if op1 is not None:
            kw["op1"] = op1
        self.fw.op(eng, lambda e: e.tensor_scalar(out=o.a, in0=i.a, scalar1=a1, scalar2=a2, op0=op0, **kw), reads=reads, writes=[o.b])

    def tt(self, eng, o, a, b, op):
        self.fw.op(eng, lambda e: e.tensor_tensor(out=o.a, in0=a.a, in1=b.a, op=op), reads=[a.b, b.b], writes=[o.b])

    def stt(self, eng, o, a, s, b, op0, op1):
        reads = [a.b, b.b]
        sa = s
        if isinstance(s, V):
            reads.append(s.b)
            sa = s.a
        self.fw.op(eng, lambda e: e.scalar_tensor_tensor(out=o.a, in0=a.a, scalar=sa, in1=b.a, op0=op0, op1=op1), reads=reads, writes=[o.b])

    def cp(self, eng, o, i):
        if eng == "act":
            self.fw.op("act", lambda e: e.copy(out=o.a, in_=i.a), reads=[i.b], writes=[o.b])
        else:
            self.fw.op(eng, lambda e: e.tensor_copy(out=o.a, in_=i.a), reads=[i.b], writes=[o.b])

    def ms(self, eng, o, val):
        self.fw.op(eng, lambda e: e.memset(o.a, val), writes=[o.b])

    def iota(self, o, pattern, base=0, cm=0):
        self.fw.op("pool", lambda e: e.iota(o.a, pattern=pattern, base=base, channel_multiplier=cm,
                                            allow_small_or_imprecise_dtypes=True), writes=[o.b])

    def sigm(self, o, i):
        self.act(o, i, AF.Exp, scale=-1.0)
        self.ts("dve", o, o, 1.0, None, op0=ALU.add)
        self.recip(o, o)

    def recip(self, o, i):
        self.fw.op("dve", lambda e: e.reciprocal(out=o.a, in_=i.a), reads=[i.b], writes=[o.b])

    def rsum(self, o, i):
        self.fw.op("dve", lambda e: e.reduce_sum(out=o.a, in_=i.a, axis=AX.X), reads=[i.b], writes=[o.b])

    def rmax(self, o, i):
        self.fw.op("dve", lambda e: e.reduce_max(out=o.a, in_=i.a, axis=AX.X), reads=[i.b], writes=[o.b])

    def dma(self, o, i, q="sp", **kw):
        self.fw.dma(q, o, i, **kw)

    def dmas(self, o, i, q="sp"):
        self.fw.dma(q, o, i, allow_slow_non_contiguous=True)

    def put_row(self, dst, pattern, base, n, const=None):
        rowt, rowb = self.rowt, self.rowb
        if const is None:
            rv = rowt[0:1, 0:n]
            if len(pattern) == 2:
                rv = rv.re("p (a b) -> p a b", b=pattern[1][1])
            self.iota(rv, pattern, base=base, cm=0)
        else:
            self.ms("pool", rowt[0:1, 0:n], const)
        self.cp("pool", rowb[0:1, 0:n], rowt[0:1, 0:n])
        self.dma(dst, rowb[0:1, 0:n])

    def rstd(self, o, ss, inv_n):
        self.act(o, ss, AF.Ln, scale=inv_n, bias=self.epsc[0:o.a.shape[0], :])
        self.act(o, o, AF.Exp, scale=-0.5)

    def build(self):
        nc, fw, NT, T = self.nc, self.fw, self.NT, self.T
        din, dout = self.din, self.dout
        mm, tr, act, ts, tt, stt, cp, ms, iota, dma, dmas = (self.mm, self.tr, self.act, self.ts, self.tt, self.stt,
                                                             self.cp, self.ms, self.iota, self.dma, self.dmas)
        xp = din("xp", [T, D])
        memp = din("memp", [256, D])
        w_in = din("w_in", [D, IN_COLS])
        g_mix = din("g_mix", [D])
        b_gate = din("b_gate", [24])
        cmp_in = {}
        for kv in "kv":
            cmp_in[kv] = (din(f"cmp_pe_{kv}", [32, 64]), din(f"cmp_w1_{kv}", [2048, 256]),
                          din(f"cmp_b1_{kv}", [256]), din(f"cmp_w2_{kv}", [256, 64]))
        g_head_nsa = din("g_head_nsa", [512])
        conv_w = din("conv_w", [4, 512])
        conv_b = din("conv_b", [512])
        w_qm = din("w_qm", [4, 128, 128])
        w_km = din("w_km", [4, 128, 128])
        b_i = din("b_i", [4])
        b_f = din("b_f", [4])
        g_head_m = din("g_head_m", [512])
        w_out = din("w_out", [D, D])
        g_xa = din("g_xa", [D])
        g_mem = din("g_mem", [D])
        w_xq = din("w_xq", [D, D])
        w_xk = din("w_xk", [D, D])
        w_xv = din("w_xv", [D, D])
        w_xo = din("w_xo", [D, D])
        g_ffn = din("g_ffn", [D])
        w_gate = din("w_gate", [D, DFF])
        w_up = din("w_up", [D, DFF])
        w_down = din("w_down", [DFF, D])
        g_final = din("g_final", [D])

        y_p = dout("y_p", [T, D])
        p_kv = {n: dout(n, [T, 128]) for n in ("p_kc", "p_vc", "p_ks", "p_vs")}
        WT = min(512, T)
        p_kw = dout("p_kw", [WT, 128])
        p_vw = dout("p_vw", [WT, 128])
        p_C = dout("p_C", [4, 128, 128])
        p_n = dout("p_n", [4, 128])
        p_m = dout("p_m", [4])
        p_conv = dout("p_conv", [3, 512])
        p_mk = dout("p_mk", [256, D])
        p_mv = dout("p_mv", [256, D])

        if self.sample:
            self.sample_io()
        x1d = self.dscr("x1_scr", [T, D])
        x2d = self.dscr("x2_scr", [T, D])

        top = ExitStack()
        with top:
            psf = [fw.ps(top, [128, 512], F32, f"psf{i}") for i in range(4)]
            pst = [fw.ps(top, [128, 1024], BF16, f"pst{i}") for i in range(1)]
            psa = [fw.ps(top, [128, 512], F32, f"psa{i}") for i in range(3)]
            nps = RR(psf)
            nps_s = RR(psf[0:3])
            nps_m = RR([psf[3], psa[2]])
            npt = RR(pst)

            ident = fw.sb(top, [128, 128], BF16, "ident")
            identf = fw.sb(top, [128, 128], F32, "identf")
            for idt in (ident, identf):
                ms("pool", idt[:], 0.0)
                fw.op("pool", lambda e, idt=idt: e.affine_select(out=idt[:].a, in_=idt[:].a, pattern=[[-1, 128]],
                                                                compare_op=ALU.not_equal, fill=1.0, base=0,
                                                                channel_multiplier=1), reads=[idt], writes=[idt])
            self.epsc = fw.sb(top, [128, 1], F32, "epsc")[:]
            ms("pool", self.epsc, EPS)
            if self.sample:
                self.sample_s0(locals())
            sw = ExitStack()
            tmpf = fw.sb(sw, [128, 512], F32, "tmpf")
            iota(tmpf[:].re("p (g r) -> p g r", g=4), [[0, 4], [-1, 128]], base=0, cm=1)
            bd01 = fw.sb(sw, [128, 128], BF16, "bd01")
            ts("pool", bd01[:], tmpf[:, 0:128], 0.0, None, op0=ALU.is_le)
            ms("pool", bd01[0:64, 64:128], 0.0)
            tri_le = fw.sb(sw, [128, 128], F32, "tri_le")
            ts("pool", tri_le[:], tmpf[:, 0:128], 0.0, None, op0=ALU.is_le)
            tri2 = fw.sb(sw, [128, 128], F32, "tri2")
            cp("pool", tri2[:], bd01[:])
            csel = fw.sb(sw, [128, 2, 128], F32, "csel")
            ms("pool", csel[:], 0.0)
            ms("pool", csel[0:64, 0, :], 1.0)
            ms("pool", csel[64:128, 1, :], 1.0)
            mimp = fw.sb(sw, [128, 2, 64], BF16, "mimp")
            for ct in range(2):
                iota(tmpf[:, 0:64], [[-4, 64]], base=ct * 128 - 1, cm=1)
                stt("dve", tmpf[:, 64:128], tmpf[:, 0:64], -1.0, tmpf[:, 0:64], ALU.mult, ALU.max)
                ts("pool", tmpf[:, 128:192], tmpf[:, 64:128], 2.0, 0.5, op0=ALU.is_le, op1=ALU.mult)
                ts("pool", tmpf[:, 192:256], tmpf[:, 64:128], 1.0, 0.5, op0=ALU.is_le, op1=ALU.mult)
                tt("pool", mimp[:, ct, :], tmpf[:, 128:192], tmpf[:, 192:256], ALU.add)
                ms("pool", mimp[:, ct, 63:64], 1.0)
            if True:
                win_b = fw.sb(sw, [128, 8, IN_COLS], BF16, "win_b")
                wout_b = fw.sb(sw, [128, 8, D], BF16, "wout_b")
                wqm_b = fw.sb(sw, [128, 4, 128], BF16, "wqm_b")
                wkm_b = fw.sb(sw, [128, 4, 128], BF16, "wkm_b")
                gcol = fw.sb(sw, [128, 16], F32, "gcol")
                dmas(gcol[:, 0:8], V(None, g_mix.a.rearrange("(k p) -> p k", p=128)))
                dmas(gcol[:, 8:12], V(None, g_head_nsa.a.rearrange("(k p) -> p k", p=128)))
                dmas(gcol[:, 12:16], V(None, g_head_m.a.rearrange("(k p) -> p k", p=128)))
                cw = fw.sb(sw, [128, 4, 4], F32, "cw")
                for j in range(4):
                    dmas(cw[:, :, j], V(None, conv_w.a[j].rearrange("(c p) -> p c", p=128)))
                cb = fw.sb(sw, [128, 4], F32, "cb")
                dmas(cb[:], V(None, conv_b.a.rearrange("(c p) -> p c", p=128)))
                bgate = fw.sb(sw, [128, 24], F32, "bgate")
                dma(bgate[:], V(None, b_gate.a.partition_broadcast(128)))
                bif = fw.sb(sw, [128, 8], F32, "bif")
                dma(bif[:, 0:4], V(None, b_i.a.partition_broadcast(128)))
                dma(bif[:, 4:8], V(None, b_f.a.partition_broadcast(128)))
                with ExitStack() as s0:
                    stg = [fw.sb(s0, [128, IN_COLS], F32, f"stg{i}") for i in range(2)]
                    for k in range(8):
                        st = stg[k % 2]
                        dma(st[:], w_in[k * 128:(k + 1) * 128, :])
                        if k % 2 == 0:
                            ts("dve", win_b[:, k, :], st[:], gcol[:, k:k + 1], None, op0=ALU.mult)
                        else:
                            act(win_b[:, k, :], st[:], AF.Copy, scale=gcol[:, k:k + 1])
                    for k in range(8):
                        st = stg[k % 2]
                        dma(st[:, 0:D], w_out[k * 128:(k + 1) * 128, :])
                        if k % 2 == 0:
                            ts("dve", wout_b[:, k, :], st[:, 0:D], gcol[:, 8 + k:9 + k], None, op0=ALU.mult)
                        else:
                            act(wout_b[:, k, :], st[:, 0:D], AF.Copy, scale=gcol[:, 8 + k:9 + k])
                    st = stg[0]
                    dma(st[:, 0:512].re("p (h e) -> p h e", h=4), V(None, w_qm.a.rearrange("h d e -> d h e")))
                    cp("dve", wqm_b[:], st[:, 0:512].re("p (h e) -> p h e", h=4))
                    st = stg[1]
                    dma(st[:, 0:512].re("p (h e) -> p h e", h=4), V(None, w_km.a.rearrange("h d e -> d h e")))
                    cp("dve", wkm_b[:], st[:, 0:512].re("p (h e) -> p h e", h=4))
                    fw.barrier()

                kcp = fw.sb(sw, [68, 2, 256], BF16, "kcp")
                vcp = fw.sb(sw, [128, 2, 2, 65], BF16, "vcp")
                ms("pool", vcp[:], 1.0)
                put_row = self.put_row
                with ExitStack() as tmps:
                    self.rowt = fw.sb(tmps, [1, 4096], F32, "rowt")
                    self.rowb = fw.sb(tmps, [1, 4096], BF16, "rowb")
                    for kvh in range(2):
                        put_row(kcp[64:65, kvh, :], [[128, 32], [0, 8]], 0, 256)
                        put_row(kcp[65:66, kvh, :], [[0, 32], [16, 8]], 31, 256)
                        put_row(kcp[66:67, kvh, :], None, 0, 256, const=1.0)
                        put_row(kcp[67:68, kvh, :], None, 0, 256, const=1.0)
                    fw.barrier()

                self.pass0_prompt(sw, xp, win_b, cmp_in, kcp, vcp, ident, nps, npt)
                self.dbg("kcp", kcp[:], BF16)
                self.dbg("vcp", vcp[:], BF16)
                self.pass1_prompt(sw, locals())
                if self.sample:
                    self.sample_pass1(locals())
            fw.barrier()
            sw.close()
            self.pass2(top, locals())
            fw.finish()

    def norm_T(self, src, xt, nb, hT, ident, npt, rows=128):
        self.norm_A(src, xt, nb, rows)
        self.norm_B(nb, hT, ident, npt)

    def norm_A(self, src, xt, nb, rows=128):
        if rows < 128:
            self.ms("pool", xt[:], 0.0)
        self.dma(xt[0:rows, :], src)
        self.ms("dve", nb["ss"][:], 0.0)
        self.act(nb["junk"][:], xt[:], AF.Square, acc=nb["ss"][:])
        self.rstd(nb["rs"][:], nb["ss"][:], 1.0 / D)
        self.ts("dve", nb["xn"][:], xt[:], nb["rs"][:, 0:1], None, op0=ALU.mult)

    def norm_B(self, nb, hT, ident, npt):
        pt = npt()
        for k in range(8):
            self.tr(pt[:, k * 128:(k + 1) * 128], nb["xn"][:, k * 128:(k + 1) * 128], ident[:])
        self.cp("act", hT[:].re("p k t -> p (k t)"), pt[:])

    def norm_bufs(self, s, tag):
        fw = self.fw
        xn = fw.sb(s, [128, D], BF16, "xn" + tag)
        return {"junk": xn, "ss": fw.sb(s, [128, 1], F32, "ss" + tag), "rs": fw.sb(s, [128, 1], F32, "rs" + tag), "xn": xn}

    def pass0_prompt(self, sw, xp, win_b, cmp_in, kcp, vcp, ident, nps, npt):
        fw, NT, T = self.fw, self.NT, self.T
        mm, act, tt, cp, ms, dma, dmas = self.mm, self.act, self.tt, self.cp, self.ms, self.dma, self.dmas
        NCB = T // 16
        with ExitStack() as s:
            srcT = {kv: fw.sb(s, [64, 2, 16, NCB + 1], BF16, "srcT" + kv) for kv in "kv"}
            for kv in "kv":
                ms("pool", srcT[kv][:, :, :, NCB:NCB + 1], 0.0)
            ms("pool", kcp[0:64, :, :], 0.0)
            xts = [fw.sb(s, [128, D], F32, f"x0_{i}") for i in range(2)]
            nb = self.norm_bufs(s, "0")
            hT = fw.sb(s, [128, 8, 128], BF16, "hT0")
            kvtok = fw.sb(s, [128, 256], BF16, "kvtok")
            for t_ in range(NT):
                xt = xts[t_ % 2]
                self.norm_T(xp[t_ * 128:(t_ + 1) * 128, :], xt, nb, hT, ident, npt)
                ps = nps()
                for k in range(8):
                    mm(ps[:, 0:256], hT[:, k, :], win_b[:, k, 512:768], st=(k == 0), sp=(k == 7))
                cp("act", kvtok[:], ps[:, 0:256])
                ptk = npt()
                for gi in range(4):
                    self.tr(ptk[0:64, gi * 128:(gi + 1) * 128], kvtok[:, gi * 64:(gi + 1) * 64], ident[:])
                for h in range(2):
                    cp("act", srcT["k"][:, h, :, 8 * t_:8 * t_ + 8], ptk[0:64, h * 128:(h + 1) * 128].re("p (c j) -> p j c", j=16))
                    cp("dve", srcT["v"][:, h, :, 8 * t_:8 * t_ + 8], ptk[0:64, 256 + h * 128:384 + h * 128].re("p (c j) -> p j c", j=16))
            w1s = [fw.sb(s, [64, 8, 256], F32, f"w1s{i}") for i in range(2)]
            for kv in "kv":
                pe, w1, b1, w2 = cmp_in[kv]
                w1b = fw.sb(s, [64, 32, 256], BF16, "w1b" + kv)
                for jb in range(4):
                    st = w1s[jb % 2]
                    dma(st[:], V(None, w1.a.rearrange("(j d) n -> d j n", d=64)[:, jb * 8:(jb + 1) * 8, :]))
                    cp("dve", w1b[:, jb * 8:(jb + 1) * 8, :], st[:])
                peT = fw.sb(s, [64, 32], F32, "peT" + kv)
                dmas(peT[:], V(None, pe.a.rearrange("j d -> d j")))
                peTb = fw.sb(s, [64, 32], BF16, "peTb" + kv)
                cp("dve", peTb[:], peT[:])
                b1c = fw.sb(s, [128, 2], F32, "b1c" + kv)
                dmas(b1c[:], V(None, b1.a.rearrange("(c p) -> p c", p=128)))
                w2s = fw.sb(s, [128, 2, 64], F32, "w2s" + kv)
                dma(w2s[:], V(None, w2.a.rearrange("(c p) n -> p c n", p=128)))
                w2b = fw.sb(s, [128, 2, 64], BF16, "w2b" + kv)
                cp("dve", w2b[:], w2s[:])
                cst = fw.sb(s, [128, 2], F32, "cst" + kv)
                for hc in range(2):
                    ps = nps()
                    for j in range(32):
                        mm(ps[:, 0:1], w1b[:, j, hc * 128:(hc + 1) * 128], peTb[:, j:j + 1], st=(j == 0), sp=(j == 31))
                    tt("dve", cst[:, hc:hc + 1], ps[:, 0:1], b1c[:, hc:hc + 1], ALU.add)
                gT = fw.sb(s, [128, 2, 256], BF16, "gT" + kv)
                if NCB < 256:
                    ms("pool", gT[:], 0.0)
                for kvh in range(2):
                    for hc in range(2):
                        ps = nps()
                        for j in range(32):
                            rv = srcT[kv][:, kvh, j, 0:NCB] if j < 16 else srcT[kv][:, kvh, j - 16, 1:NCB + 1]
                            mm(ps[:, 0:NCB], w1b[:, j, hc * 128:(hc + 1) * 128], rv, st=(j == 0), sp=(j == 31))
                        act(gT[:, hc, 0:NCB], ps[:, 0:NCB], AF.Gelu_apprx_tanh, bias=cst[:, hc:hc + 1])
                    if kv == "k":
                        ps = nps()
                        for hc in range(2):
                            mm(ps[0:64, 0:256], w2b[:, hc, :], gT[:, hc, :], st=(hc == 0), sp=(hc == 1))
                        cp("dve", kcp[0:64, kvh, :], ps[0:64, 0:256])
                    else:
                        for ct in range(2):
                            ps = nps()
                            for hc in range(2):
                                mm(ps[:, 0:64], gT[:, hc, ct * 128:(ct + 1) * 128], w2b[:, hc, :], st=(hc == 0), sp=(hc == 1))
                            cp("dve", vcp[:, ct, kvh, 0:64], ps[:, 0:64])
            fw.barrier()

    def pass1_prompt(self, sw, L):
        fw, NT, T = self.fw, self.NT, self.T
        mm, tr, act, ts, tt, stt, cp, ms, iota, dma, dmas = (self.mm, self.tr, self.act, self.ts, self.tt, self.stt,
                                                             self.cp, self.ms, self.iota, self.dma, self.dmas)
        xp, win_b, wout_b, wqm_b, wkm_b = L["xp"], L["win_b"], L["wout_b"], L["wqm_b"], L["wkm_b"]
        kcp, vcp, ident, identf, nps, npt, psa = L["kcp"], L["vcp"], L["ident"], L["identf"], L["nps"], L["npt"], L["psa"]
        nps_s, nps_m = L["nps_s"], L["nps_m"]
        bd01, tri2, csel, mimp, tmpf = L["bd01"], L["tri2"], L["csel"], L["mimp"], L["tmpf"]
        cw, cb, bgate, bif, put_row, x1d = L["cw"], L["cb"], L["bgate"], L["bif"], L["put_row"], L["x1d"]
        p_kv, p_kw, p_vw, p_C, p_n, p_m, p_conv = L["p_kv"], L["p_kw"], L["p_vw"], L["p_C"], L["p_n"], L["p_m"], L["p_conv"]
        with ExitStack() as s:
            iota(tmpf[:].re("p (g r) -> p g r", g=4), [[0, 4], [-1, 128]], base=0, cm=1)
            caus_add = fw.sb(s, [128, 512], BF16, "caus_add")
            ts("pool", caus_add[:], tmpf[:], 0.0, NEG, op0=ALU.is_gt, op1=ALU.mult)
            win_add = fw.sb(s, [128, 512], BF16, "win_add")
            ts("pool", win_add[:], tmpf[:], 0.0, NEG, op0=ALU.is_le, op1=ALU.mult)
            e0 = fw.sb(s, [128, 512], F32, "e0")
            iota(e0[:].re("p (g r) -> p g r", g=4), [[0, 4], [-1, 128]], base=0, cm=16)
            expand = fw.sb(s, [64, T], BF16, "expand")
            for c0 in range(0, T, 512):
                iota(tmpf[0:64, :], [[1, 512]], base=c0, cm=-64)
                ts("pool", tmpf[0:64, :], tmpf[0:64, :], 31.5, None, op0=ALU.subtract)
                stt("dve", tmpf[0:64, :], tmpf[0:64, :], -1.0, tmpf[0:64, :], ALU.mult, ALU.max)
                ts("pool", expand[:, c0:c0 + 512], tmpf[0:64, :], 32.0, None, op0=ALU.is_le)
            ksT = fw.sb(s, [68, 2, T], BF16, "ksT")
            NW = min(8, NT)
            kwT = fw.sb(s, [68, 2, NW * 128], BF16, "kwT")
            phr = fw.sb(s, [1, 128], BF16, "phr")
            vsp = fw.sb(s, [128, NT, 2, 65], BF16, "vsp")
            vwp = fw.sb(s, [128, NW, 2, 65], BF16, "vwp")
            ms("pool", vsp[:], 1.0)
            ms("pool", vwp[:], 1.0)
            with ExitStack() as tmps:
                self.rowt = fw.sb(tmps, [1, 4096], F32, "rowt1")
                self.rowb = fw.sb(tmps, [1, 4096], BF16, "rowb1")
                for kvh in range(2):
                    put_row(ksT[64:65, kvh, :], [[128, NT], [0, 128]], 0, T)
                    put_row(ksT[65:66, kvh, :], [[0, NT], [1, 128]], 0, T)
                    put_row(ksT[66:67, kvh, :], None, 0, T, const=1.0)
                    put_row(ksT[67:68, kvh, :], None, 0, T, const=1.0)
                    put_row(kwT[65:66, kvh, :], [[0, NW], [1, 128]], 0, NW * 128)
                    put_row(kwT[66:67, kvh, :], None, 0, NW * 128, const=1.0)
                    put_row(kwT[67:68, kvh, :], None, 0, NW * 128, const=1.0)
                fw.barrier()
            qps = [fw.sb(s, [68, 2, 4, 128], BF16, f"qp{i}") for i in range(2)]
            srow = fw.sb(s, [1, 8, 128], F32, "srow")
            for h in range(8):
                ms("pool", srow[0:1, h, :], 2.0 ** (-(h + 1)))
            r67 = fw.sb(s, [1, 8, 128], F32, "r67")
            iota(r67[:], [[0, 8], [1, 128]], base=0, cm=0)
            tt("pool", r67[:], r67[:], srow[:], ALU.mult)
            ts("pool", r67[:], r67[:], -1.0, None, op0=ALU.mult)
            srb = fw.sb(s, [1, 8, 128], BF16, "srb")
            r67b = fw.sb(s, [1, 8, 128], BF16, "r67b")
            r66b = fw.sb(s, [1, 8, 128], BF16, "r66b")
            cp("pool", srb[:], srow[:])
            cp("pool", r67b[:], r67[:])
            for qp in qps:
                for kvh in range(2):
                    dma(qp[64:65, kvh], srb[0:1, 4 * kvh:4 * kvh + 4, :])
                    dma(qp[65:66, kvh], srb[0:1, 4 * kvh:4 * kvh + 4, :])
                    dma(qp[67:68, kvh], r67b[0:1, 4 * kvh:4 * kvh + 4, :])
            xts = [fw.sb(s, [128, D], F32, f"x1_{i}") for i in range(2)]
            nb = self.norm_bufs(s, "1")
            hTs = [fw.sb(s, [128, 8, 128], BF16, f"hT1_{i}") for i in range(2)]
            pkv = fw.sb(s, [128, 792], F32, "pkv")
            qtok = fw.sb(s, [128, 512], BF16, "qtok")
            kstok = fw.sb(s, [128, 256], BF16, "kstok")
            gt = fw.sb(s, [128, 24], F32, "gt")
            pts = RR([fw.sb(s, [128, 512], BF16, f"pt{i}") for i in range(4)])
            mks = RR([fw.sb(s, [128, 512], BF16, f"mk{i}") for i in range(2)])
            obr = [[fw.sb(s, [128, 4, 65], F32, f"obr{k}{i}") for i in range(3)] for k in range(2)]
            rdc = fw.sb(s, [128, 4], F32, "rdc")
            imp4 = fw.sb(s, [128, 4, 64], F32, "imp4")
            imp = fw.sb(s, [128, 64], F32, "imp")
            imp2 = fw.sb(s, [128, 64], F32, "imp2")
            mx1 = fw.sb(s, [128, 8], F32, "mx1")
            mx2 = fw.sb(s, [128, 8], F32, "mx2")
            selm = [fw.sb(s, [128, 64], BF16, f"selm{k}") for k in range(2)]
            selT = [fw.sb(s, [64, 4, 128], BF16, f"selT{k}") for k in range(2)]
            fpb = fw.sb(s, [128, 3], F32, "fpb")
            ms("pool", fpb[:], -1.0)
            rd = fw.sb(s, [128, 3, 4], F32, "rd")
            sc3 = fw.sb(s, [128, 3, 4], F32, "sc3")
            onsa = fw.sb(s, [128, 4, 64], F32, "onsa")
            otmp = fw.sb(s, [128, 4, 64], F32, "otmp")
            ss4m = fw.sb(s, [128, 4], F32, "ss4pm")
            rs4m = fw.sb(s, [128, 4], F32, "rs4pm")
            ss4 = fw.sb(s, [128, 4], F32, "ss4")
            rs4 = fw.sb(s, [128, 4], F32, "rs4")
            mixin = fw.sb(s, [128, D], BF16, "mixin")
            mT = fw.sb(s, [128, 8, 128], BF16, "mT")
            x1t = fw.sb(s, [128, D], F32, "x1t")
            gif = fw.sb(s, [128, 8], F32, "gif")
            l1 = fw.sb(s, [128, 4], F32, "l1")
            gsb = fw.sb(s, [128, 12], F32, "gsb")
            wl = fw.sb(s, [128, 4], F32, "wl")
            ul = fw.sb(s, [128, 4], F32, "ul")
            tmp4 = fw.sb(s, [128, 4], F32, "tmp4")
            dec = fw.sb(s, [128, 4], F32, "dec")
            ebt = fw.sb(s, [128, 8], F32, "ebt")
            vmu = fw.sb(s, [128, 4, 129], BF16, "vmu")
            sigo = fw.sb(s, [128, 512], F32, "sigo")
            xcv = [fw.sb(s, [128, 4, 131], F32, f"xcv{i}") for i in range(2)]
            ms("pool", xcv[0][:], 0.0)
            cacc = fw.sb(s, [128, 4, 128], F32, "cacc")
            xc = fw.sb(s, [128, 4, 128], BF16, "xc")
            qmT = fw.sb(s, [128, 4, 128], BF16, "qmT")
            qmS = [fw.sb(s, [128, 4, 128], BF16, f"qmS{i}") for i in range(2)]
            for q_ in qmS:
                ms("pool", q_[:], 0.0)
            kmT = fw.sb(s, [128, 4, 128], BF16, "kmT")
            kmS = [fw.sb(s, [128, 4, 128], BF16, f"kmS{i}") for i in range(2)]
            mqk = fw.sb(s, [128, 4, 128], BF16, "mqk")
            Sf = fw.sb(s, [128, 4, 129], F32, "Sf")
            ms("pool", Sf[:], 0.0)
            Sb = [fw.sb(s, [128, 4, 129], BF16, f"Sb{i}") for i in range(3)]
            ms("pool", Sb[0][:], 0.0)
            dS = fw.sb(s, [128, 4, 129], F32, "dS")
            dn = fw.sb(s, [128, 4], F32, "dn")
            hout = fw.sb(s, [128, 4, 128], F32, "hout")
            hsq = cacc
            m4 = fw.sb(s, [4, 8], F32, "m4")
            R = fw.sb(s, [4, 1], F32, "Rm")
            ms("pool", R[:], 0.0)
            tsb = fw.sb(s, [4, 384], F32, "tsb")
            segs = [(0, 64), (64, 128)]
            KSC = 128.0 ** -0.5

            for t_ in range(NT):
                xt = xts[t_ % 2]
                qp = qps[t_ % 2]
                ts("pool", r66b[:], srow[:], -128.0 * t_, None, op0=ALU.mult)
                for kvh in range(2):
                    dma(qp[66:67, kvh], r66b[0:1, 4 * kvh:4 * kvh + 4, :])
                hT = hTs[t_ % 2]
                if t_ == 0:
                    self.norm_T(xp[0:128, :], xt, nb, hT, ident, npt)
                psA = nps()
                psB = nps()
                for k in range(8):
                    mm(psA[:, 0:512], hT[:, k, :], win_b[:, k, 512:1024], st=(k == 0), sp=(k == 7))
                for k in range(8):
                    mm(psB[:, 0:280], hT[:, k, :], win_b[:, k, 1024:1304], st=(k == 0), sp=(k == 7))
                cp("dve", pkv[:, 0:512], psA[:, 0:512])
                cp("act", pkv[:, 512:792], psB[:, 0:280])
                r0 = t_ * 128
                for i_, n_ in enumerate(("p_kc", "p_vc", "p_ks", "p_vs")):
                    dma(p_kv[n_][r0:r0 + 128, :], pkv[:, i_ * 128:(i_ + 1) * 128])
                if r0 >= T - 512:
                    w0 = r0 - (T - min(512, T))
                    dma(p_kw[w0:w0 + 128, :], pkv[:, 512:640])
                    dma(p_vw[w0:w0 + 128, :], pkv[:, 640:768])
                cp("pool", vsp[:, t_, :, 0:64], pkv[:, 384:512].re("p (h d) -> p h d", h=2))
                ws = t_ % NW
                cp("pool", vwp[:, ws, :, 0:64], pkv[:, 640:768].re("p (h d) -> p h d", h=2))
                tt("dve", gt[:], pkv[:, 768:792], bgate[:], ALU.add)
                self.sigm(gt[:], gt[:])
                psQt = nps()
                for k in range(8):
                    mm(psQt[:, 0:512], hT[:, k, :], win_b[:, k, 0:512], st=(k == 0), sp=(k == 7))
                act(qtok[:], psQt[:, 0:512], AF.Copy, scale=0.125)
                cp("dve", kstok[:, 0:128], pkv[:, 256:384])
                cp("dve", kstok[:, 128:256], pkv[:, 512:640])
                ptq = npt()
                for h in range(8):
                    tr(ptq[0:64, h * 128:(h + 1) * 128], qtok[:, h * 64:(h + 1) * 64], ident[:])
                cp("act", qp[0:64].re("p k g t -> p (k g t)"), ptq[0:64, :])
                ptk = npt()
                for i_ in range(4):
                    tr(ptk[0:64, i_ * 128:(i_ + 1) * 128], kstok[:, i_ * 64:(i_ + 1) * 64], ident[:])
                ws = t_ % NW
                cp("dve", ksT[0:64, :, r0:r0 + 128], ptk[0:64, 0:256].re("p (h t) -> p h t", h=2))
                cp("dve", kwT[0:64, :, ws * 128:(ws + 1) * 128], ptk[0:64, 256:512].re("p (h t) -> p h t", h=2))
                ms("pool", phr[:], 128.0 * t_)
                for kvh in range(2):
                    dma(kwT[64:65, kvh, ws * 128:(ws + 1) * 128], phr[:])

                def mlstm_gen():
                    psG = nps_m()
                    for k in range(8):
                        mm(psG[:, 0:8], hT[:, k, :], win_b[:, k, 2840:2848], st=(k == 0), sp=(k == 7))
                    tt("dve", gif[:], psG[:, 0:8], bif[:], ALU.add)
                    act(l1[:], gif[:, 4:8], AF.Exp, scale=-1.0)
                    act(l1[:], l1[:], AF.Ln, bias=1.0)
                    yield
                    psC = nps_m()
                    mm(psC[:, 0:4], tri2[:], l1[:])
                    mm(psC[:, 4:8], csel[:, 0, :], l1[:])
                    mm(psC[:, 8:12], csel[:, 1, :], l1[:])
                    cp("dve", gsb[:], psC[:, 0:12])
                    act(wl[:], gsb[:, 0:4], AF.Exp, scale=-1.0)
                    tt("dve", tmp4[:], gif[:, 0:4], gsb[:, 0:4], ALU.add)
                    act(ul[:], tmp4[:], AF.Exp)
                    act(ebt[:], gsb[:, 4:12], AF.Exp, scale=-1.0)
                    tt("dve", dec[0:64, :], tmp4[0:64, :], gsb[0:64, 4:8], ALU.subtract)
                    tt("dve", dec[64:128, :], tmp4[64:128, :], gsb[64:128, 8:12], ALU.subtract)
                    yield
                    psV = nps_m()
                    for k in range(8):
                        mm(psV[:, 0:512], hT[:, k, :], win_b[:, k, 1816:2328], st=(k == 0), sp=(k == 7))
                    tt("dve", vmu[:, :, 0:128], psV[:, 0:512].re("p (h e) -> p h e", h=4), ul[:].un(2).bc([128, 4, 128]), ALU.mult)
                    cp("dve", vmu[:, :, 128], ul[:])
                    yield
                    psO = nps_m()
                    for k in range(8):
                        mm(psO[:, 0:512], hT[:, k, :], win_b[:, k, 2328:2840], st=(k == 0), sp=(k == 7))
                    self.sigm(sigo[:], psO[:, 0:512])
                    yield
                    xcur, xnext = xcv[t_ % 2], xcv[(t_ + 1) % 2]
                    psX = nps_m()
                    for ch in range(4):
                        for k in range(8):
                            mm(psX[:, ch * 128:(ch + 1) * 128], win_b[:, k, 1304 + ch * 128:1432 + ch * 128], hT[:, k, :],
                               st=(k == 0), sp=(k == 7))
                    cp("act", xcur[:, :, 3:131], psX[:, :].re("p (c t) -> p c t", c=4))
                    cp("pool", xnext[:, :, 0:3], xcur[:, :, 128:131])
                    yield
                    for ch in range(4):
                        ts("dve", cacc[:, ch, :], xcur[:, ch, 0:128], cw[:, ch, 0:1], cb[:, ch:ch + 1], op0=ALU.mult, op1=ALU.add)
                        for j in range(1, 4):
                            stt("dve", cacc[:, ch, :], xcur[:, ch, j:j + 128], cw[:, ch, j:j + 1], cacc[:, ch, :], ALU.mult, ALU.add)
                        if ch % 2 == 1:
                            yield
                    self.sigm(hout[:], cacc[:])
                    tt("dve", xc[:], cacc[:], hout[:], ALU.mult)
                    if t_ == NT - 1:
                        for j in range(3):
                            dmas(V(None, p_conv.a[j].rearrange("(c p) -> p c", p=128)), xcur[:, :, 128 + j])
                    yield
                    psq = nps_m()
                    for h in range(4):
                        mm(psq[:, h * 128:(h + 1) * 128], wqm_b[:, h, :], xc[:, h, :])
                    cp("act", qmT[:].re("p h t -> p (h t)"), psq[:, :])
                    for si, (a_, b_) in enumerate(segs):
                        cp("dve", qmS[si][:, :, a_:b_], psq[:, :].re("p (h t) -> p h t", h=4)[:, :, a_:b_])
                    psk = nps_m()
                    for h in range(4):
                        mm(psk[:, h * 128:(h + 1) * 128], wkm_b[:, h, :], xc[:, h, :])
                    act(kmT[:].re("p h t -> p (h t)"), psk[:, :], AF.Copy, scale=KSC)
                    yield
                    pskt = nps_m()
                    for h in range(4):
                        mm(pskt[:, h * 128:(h + 1) * 128], xc[:, h, :], wkm_b[:, h, :])
                    for si in range(2):
                        ts("dve", kmS[si][:].re("p h t -> p (h t)"), pskt[:, :], csel[:, si, 0:1], KSC, op0=ALU.mult, op1=ALU.mult)
                    psqk = nps_m()
                    for h in range(4):
                        mm(psqk[:, h * 128:(h + 1) * 128], kmT[:, h, :], qmT[:, h, :])
                    tt("dve", mqk[:], psqk[:, :].re("p (h t) -> p h t", h=4), bd01[:].un(1).bc([128, 4, 128]), ALU.mult)
                    yield
                    sbs = [Sb[(2 * t_) % 3], Sb[(2 * t_ + 1) % 3], Sb[(2 * t_ + 2) % 3]]
                    for si in range(2):
                        pd = [nps_m(), nps_m()]
                        for h in range(4):
                            mm(pd[h // 2][:, (h % 2) * 129:(h % 2) * 129 + 129], kmS[si][:, h, :], vmu[:, h, :])
                        eb = ebt[:, 4 * si:4 * si + 4].un(2).bc([128, 4, 129])
                        tt("dve", Sf[:], Sf[:], eb, ALU.mult)
                        for hh in range(2):
                            tt("dve", dS[:, 2 * hh:2 * hh + 2, :], pd[hh][:, 0:258].re("p (h e) -> p h e", h=2),
                               ebt[:, 4 * si + 2 * hh:4 * si + 2 * hh + 2].un(2).bc([128, 2, 129]), ALU.mult)
                        tt("dve", Sf[:], Sf[:], dS[:], ALU.add)
                        cp("act", sbs[si + 1][:], Sf[:])
                        yield
                    pa = [nps_m(), nps_m()]
                    for h in range(4):
                        o_ = pa[h // 2][:, (h % 2) * 129:(h % 2) * 129 + 129]
                        mm(o_, mqk[:, h, :], vmu[:, h, :], st=True, sp=False)
                        mm(o_, qmS[0][:, h, :], sbs[0][:, h, :], st=False, sp=False)
                        mm(o_, qmS[1][:, h, :], sbs[1][:, h, :], st=False, sp=True)
                    for hh in range(2):
                        av = pa[hh][:, 0:258].re("p (h e) -> p h e", h=2)
                        tt("dve", dn[:, 2 * hh:2 * hh + 2], av[:, :, 128], wl[:, 2 * hh:2 * hh + 2], ALU.mult)
                    stt("dve", tmp4[:], dn[:], -1.0, dn[:], ALU.mult, ALU.max)
                    ts("dve", tmp4[:], tmp4[:], 1.0, None, op0=ALU.max)
                    self.recip(tmp4[:], tmp4[:])
                    tt("dve", tmp4[:], tmp4[:], wl[:], ALU.mult)
                    for hh in range(2):
                        av = pa[hh][:, 0:258].re("p (h e) -> p h e", h=2)
                        tt("dve", hout[:, 2 * hh:2 * hh + 2, :], av[:, :, 0:128],
                           tmp4[:, 2 * hh:2 * hh + 2].un(2).bc([128, 2, 128]), ALU.mult)
                    yield
                    tt("dve", hsq[:], hout[:], hout[:], ALU.mult)
                    self.rsum(ss4m[:], hsq[:])
                    self.rstd(rs4m[:], ss4m[:], 1.0 / 128)
                    tt("dve", hout[:], hout[:], rs4m[:].un(2).bc([128, 4, 128]), ALU.mult)
                    tt("dve", mixin[:, 512:1024], hout[:].re("p h e -> p (h e)"), sigo[:], ALU.mult)
                    yield
                    psT = nps_m()
                    tr(psT[0:4, 0:128], dec[:], identf[:])
                    tr(psT[0:4, 128:256], gsb[:, 4:8], identf[:])
                    tr(psT[0:4, 256:384], gsb[:, 8:12], identf[:])
                    cp("dve", tsb[:], psT[0:4, 0:384])
                    self.rmax(m4[:, 0:1], tsb[:, 0:64])
                    self.rmax(m4[:, 1:2], tsb[:, 64:128])
                    stt("dve", R[:], R[:], tsb[:, 128:129], m4[:, 0:1], ALU.subtract, ALU.max)
                    stt("dve", R[:], R[:], tsb[:, 256:257], m4[:, 1:2], ALU.subtract, ALU.max)

                mg = mlstm_gen()
                steps = []
                for kvh in range(2):
                    cts = [0] if t_ < 16 else [0, 1]
                    for ci, ct in enumerate(cts):
                        steps.append(("cmp", kvh, ct, ci == 0, ci == len(cts) - 1))
                for kvh in range(2):
                    k0 = max(0, t_ - 4)
                    for kt in range(k0, t_ + 1):
                        steps.append(("win", kvh, kt, kt == k0, kt == t_))
                for kvh in range(2):
                    for kt in range(t_ + 1):
                        steps.append(("sel", kvh, kt, kt == 0, kt == t_))
                pend = {}

                def score(i):
                    kind, kvh, k, first, last = steps[i]
                    qv = qp[:, kvh].re("p g t -> p (g t)")
                    S = nps_s()
                    if kind == "cmp":
                        Kq = 128 * t_ - 2048 * k - 31
                        need_mask = Kq < 2032
                        mm(S[:, :], kcp[:, kvh, k * 128:(k + 1) * 128], qv, st=True, sp=not need_mask)
                        if need_mask:
                            mk = mks()
                            ts("dve", mk[:], e0[:], float(Kq), NEG, op0=ALU.is_gt, op1=ALU.mult)
                            mm(S[:, :], ident[:], mk[:], st=False, sp=True)
                    elif kind == "win":
                        madd = caus_add if k == t_ else (win_add if k == t_ - 4 else None)
                        mm(S[:, :], kwT[:, kvh, (k % NW) * 128:(k % NW + 1) * 128], qv, st=True, sp=(madd is None))
                        if madd is not None:
                            mm(S[:, :], ident[:], madd[:], st=False, sp=True)
                    else:
                        if first and kvh == 0:
                            flush_sel()
                        mm(S[:, :], ksT[:, kvh, k * 128:(k + 1) * 128], qv, st=True, sp=False)
                        if k < t_:
                            mm(S[:, :], expand[:, k * 128:(k + 1) * 128], selT[kvh][:].re("p g t -> p (g t)"), st=False, sp=True)
                        else:
                            mm(S[:, :], ident[:], caus_add[:], st=False, sp=True)
                    pt = pts()
                    act(pt[:], S[:, :], AF.Exp)
                    pend[i] = pt

                def finish_cmp(kvh):
                    bank = psa[kvh]
                    cp("dve", obr[kvh][0][:, :, 0:64], bank[:, 0:256].re("p (g d) -> p g d", g=4))
                    cp("act", imp4[:].re("p g j -> p (g j)"), bank[:, 256:512])
                    cp("dve", obr[kvh][0][:, :, 64], imp4[:, :, 63])
                    ts("dve", rdc[:], obr[kvh][0][:, :, 64], 1e-30, None, op0=ALU.max)
                    self.recip(rdc[:], rdc[:])
                    ts("dve", imp[:], imp4[:, 0, :], rdc[:, 0:1], None, op0=ALU.mult)
                    for g in range(1, 4):
                        stt("dve", imp[:], imp4[:, g, :], rdc[:, g:g + 1], imp[:], ALU.mult, ALU.add)
                    ms("dve", imp[:, 63:64], 0.0)
                    if t_ == 0:
                        tt("dve", imp[:, 0:2], imp[:, 0:2], fpb[:, 1:3], ALU.max)
                    else:
                        tt("dve", imp[:, 2 * t_ - 1:2 * t_ + 2], imp[:, 2 * t_ - 1:2 * t_ + 2], fpb[:, 0:3], ALU.max)
                        ms("dve", imp[:, 0:1], 3e9)
                    fw.op("dve", lambda e: e.max(out=mx1[:].a, in_=imp[:].a), reads=[imp], writes=[mx1])
                    fw.op("dve", lambda e: e.match_replace(out=imp2[:].a, in_to_replace=mx1[:].a, in_values=imp[:].a,
                                                           imm_value=-1e30), reads=[imp, mx1], writes=[imp2])
                    fw.op("dve", lambda e: e.max(out=mx2[:].a, in_=imp2[:].a), reads=[imp2], writes=[mx2])
                    ts("dve", selm[kvh][:], imp[:], mx2[:, 7:8], NEG, op0=ALU.is_lt, op1=ALU.mult)

                def flush_sel():
                    for kvh_ in range(2):
                        ptr = npt()
                        tr(ptr[0:64, 0:128], selm[kvh_][:], ident[:])
                        cp("dve", selT[kvh_][:], ptr[0:64, 0:128].un(1).bc([64, 4, 128]))

                def pv(i):
                    kind, kvh, k, first, last = steps[i]
                    pt = pend.pop(i)
                    acc = psa[kvh]
                    if kind == "cmp":
                        for g in range(4):
                            mm(acc[:, g * 64:(g + 1) * 64], pt[:, g * 128:(g + 1) * 128], vcp[:, k, kvh, 0:64], st=first, sp=last)
                            mm(acc[:, 256 + g * 64:320 + g * 64], pt[:, g * 128:(g + 1) * 128], mimp[:, k, :], st=first, sp=last)
                    else:
                        vv = vwp[:, k % NW, kvh, :] if kind == "win" else vsp[:, k, kvh, :]
                        for g in range(4):
                            mm(acc[:, g * 65:(g + 1) * 65], pt[:, g * 128:(g + 1) * 128], vv, st=first, sp=last)
                    if last:
                        if kind == "cmp":
                            finish_cmp(kvh)
                        elif kind == "win":
                            cp("act", obr[kvh][2][:].re("p g d -> p (g d)"), acc[:, 0:260])
                        else:
                            cp("dve", obr[kvh][1][:].re("p g d -> p (g d)"), acc[:, 0:260])

                base = 1e9 + 1e6 * (2 * t_)
                ms("dve", fpb[0:64, 0:1], base - 1e6)
                ms("dve", fpb[0:64, 1:2], base)
                ms("dve", fpb[64:128, 1:2], base)
                ms("dve", fpb[64:128, 2:3], base + 1e6)
                LA = 2
                for i in range(min(LA, len(steps))):
                    score(i)
                iB = min(8, len(steps) - 1)
                pace = max(1, (len(steps) - 2) // 18)
                for i in range(len(steps)):
                    if i + LA < len(steps):
                        score(i + LA)
                    pv(i)
                    if i >= 1 and (i - 1) % pace == 0:
                        next(mg, None)
                    if t_ + 1 < NT:
                        if i == 0:
                            self.norm_A(xp[(t_ + 1) * 128:(t_ + 2) * 128, :], xts[(t_ + 1) % 2], nb)
                        if i == iB:
                            self.norm_B(nb, hTs[(t_ + 1) % 2], ident, npt)
                for _ in mg:
                    pass
                for kvh in range(2):
                    for br in range(3):
                        ts("dve", rd[:, br, :], obr[kvh][br][:, :, 64], 1e-30, None, op0=ALU.max)
                    self.recip(rd[:].re("p b g -> p (b g)"), rd[:].re("p b g -> p (b g)"))
                    gv = gt[:, 12 * kvh:12 * kvh + 12].re("p (g b) -> p b g", b=3)
                    tt("dve", sc3[:], rd[:], gv, ALU.mult)
                    tt("dve", onsa[:], obr[kvh][0][:, :, 0:64], sc3[:, 0, :].un(2).bc([128, 4, 64]), ALU.mult)
                    for br in (1, 2):
                        tt("dve", otmp[:], obr[kvh][br][:, :, 0:64], sc3[:, br, :].un(2).bc([128, 4, 64]), ALU.mult)
                        tt("dve", onsa[:], onsa[:], otmp[:], ALU.add)
                    tt("dve", otmp[:], onsa[:], onsa[:], ALU.mult)
                    self.rsum(ss4[:], otmp[:])
                    self.rstd(rs4[:], ss4[:], 1.0 / 64)
                    tt("dve", mixin[:, 256 * kvh:256 * kvh + 256].re("p (g d) -> p g d", g=4), onsa[:],
                       rs4[:].un(2).bc([128, 4, 64]), ALU.mult)

                ptm = npt()
                for k in range(8):
                    tr(ptm[:, k * 128:(k + 1) * 128], mixin[:, k * 128:(k + 1) * 128], ident[:])
                cp("act", mT[:].re("p k t -> p (k t)"), ptm[:])
                for g in range(2):
                    ps = nps()
                    for k in range(8):
                        mm(ps[:, :], mT[:, k, :], wout_b[:, k, g * 512:(g + 1) * 512], st=(k == 0), sp=(k == 7))
                    tt("dve", x1t[:, g * 512:(g + 1) * 512], xt[:, g * 512:(g + 1) * 512], ps[:, :], ALU.add)
                dma(x1d[r0:r0 + 128, :], x1t[:])

            dma(V(None, p_m.a.rearrange("(h o) -> h o", o=1)), R[:])
            ones4 = fw.sb(s, [4, 128], F32, "ones4")
            ms("pool", ones4[:], 1.0)
            ts("dve", ones4[:], ones4[:], R[:, 0:1], None, op0=ALU.mult)
            ps = nps()
            mm(ps[:, 0:4], ones4[:], identf[0:4, 0:4])
            act(tmp4[:], ps[:, 0:4], AF.Exp, scale=-1.0)
            tt("dve", Sf[:], Sf[:], tmp4[:].un(2).bc([128, 4, 129]), ALU.mult)
            dmas(V(None, p_n.a.rearrange("h d -> d h")), Sf[:, :, 128])
            for h in range(4):
                ps = nps()
                tr(ps[:, 0:128], Sf[:, h, 0:128], identf[:])
                cp("dve", hsq[:, h, :], ps[:, 0:128])
                dma(p_C[h], hsq[:, h, :])
            fw.barrier()

    def load_w(self, dst, src, rows_chunks, ncols, stg, gcol=None, g0=0):
        for k in range(rows_chunks):
            st = stg[k % 2]
            self.dma(st[:, 0:ncols], src[k * 128:(k + 1) * 128, :])
            if k % 2 == 0:
                if gcol is None:
                    self.cp("dve", dst[:, k, :], st[:, 0:ncols])
                else:
                    self.ts("dve", dst[:, k, :], st[:, 0:ncols], gcol[:, g0 + k:g0 + k + 1], None, op0=ALU.mult)
            elif gcol is None:
                self.cp("act", dst[:, k, :], st[:, 0:ncols])
            else:
                self.act(dst[:, k, :], st[:, 0:ncols], AF.Copy, scale=gcol[:, g0 + k:g0 + k + 1])

    def pass2(self, top, L):
        fw, NT, T = self.fw, self.NT, self.T
        mm, tr, act, ts, tt, stt, cp, ms, dma, dmas = (self.mm, self.tr, self.act, self.ts, self.tt, self.stt,
                                                       self.cp, self.ms, self.dma, self.dmas)
        ident, nps, npt, psa = L["ident"], L["nps"], L["npt"], L["psa"]
        x1d, x2d, y_p = L["x1d"], L["x2d"], L["y_p"]
        g2 = fw.sb(top, [128, 32], F32, "g2")
        for i_, g_ in enumerate((L["g_xa"], L["g_mem"], L["g_ffn"])):
            dmas(g2[:, 8 * i_:8 * i_ + 8], V(None, g_.a.rearrange("(k p) -> p k", p=128)))
        with ExitStack() as s:
            wxq_b = fw.sb(s, [128, 8, D], BF16, "wxq_b")
            wxo_b = fw.sb(s, [128, 8, D], BF16, "wxo_b")
            mkT = fw.sb(s, [128, 8, 256], BF16, "mkT")
            mvp = fw.sb(s, [128, 2, 4, 257], BF16, "mvp")
            ms("pool", mvp[:], 1.0)
            xts = [fw.sb(s, [128, D], F32, f"x2_{i}") for i in range(4)]
            nb = self.norm_bufs(s, "2")
            hT = fw.sb(s, [128, 8, 128], BF16, "hT2")
            hT4 = fw.sb(s, [128, 8, 512], BF16, "hT2_4")
            qxT4 = fw.sb(s, [128, 8, 512], BF16, "qxT4")
            with ExitStack() as s2:
                stg = [fw.sb(s2, [128, D], F32, f"stg2{i}") for i in range(2)]
                wxk_b = fw.sb(s2, [128, 8, D], BF16, "wxk_b")
                wxv_b = fw.sb(s2, [128, 8, D], BF16, "wxv_b")
                self.load_w(wxq_b, L["w_xq"], 8, D, stg, g2, 0)
                self.load_w(wxo_b, L["w_xo"], 8, D, stg)
                self.load_w(wxk_b, L["w_xk"], 8, D, stg, g2, 8)
                self.load_w(wxv_b, L["w_xv"], 8, D, stg, g2, 8)
                mo = fw.sb(s2, [128, D], F32, "mo")
                for mt in range(2):
                    xt = xts[mt % 2]
                    self.norm_T(L["memp"][mt * 128:(mt + 1) * 128, :], xt, nb, hT, ident, npt)
                    for wi, (wb_, po) in enumerate(((wxk_b, L["p_mk"]), (wxv_b, L["p_mv"]))):
                        for g in range(2):
                            ps = nps()
                            for k in range(8):
                                mm(ps[:, :], hT[:, k, :], wb_[:, k, g * 512:(g + 1) * 512], st=(k == 0), sp=(k == 7))
                            cp("dve" if g == 0 else "act", mo[:, g * 512:(g + 1) * 512], ps[:, :])
                        dma(po[mt * 128:(mt + 1) * 128, :], mo[:])
                        if wi == 1:
                            cp("pool", mvp[:, mt, :, 0:256], mo[:].re("p (h d) -> p h d", h=4))
                    for c4 in range(2):
                        ps = nps()
                        for cc in range(4):
                            c = c4 * 4 + cc
                            for k in range(8):
                                mm(ps[:, cc * 128:(cc + 1) * 128], wxk_b[:, k, c * 128:(c + 1) * 128], hT[:, k, :],
                                   st=(k == 0), sp=(k == 7))
                        cp("dve", mkT[:, c4 * 4:c4 * 4 + 4, mt * 128:(mt + 1) * 128], ps[:, :].re("p (c t) -> p c t", c=4))
                fw.barrier()
            pts = [fw.sb(s, [128, 512], BF16, f"pxt{i}") for i in range(2)]
            ox = fw.sb(s, [128, D], BF16, "ox")
            oxT = fw.sb(s, [128, 8, 128], BF16, "oxT")
            rdx = fw.sb(s, [128, 1], F32, "rdx")
            x2t = fw.sb(s, [128, D], F32, "x2t")
            for t_ in range(NT):
                jq = t_ % 4
                xt = xts[jq]
                r0 = t_ * 128
                if jq == 0:
                    for j in range(4):
                        self.norm_A(x1d[r0 + j * 128:r0 + (j + 1) * 128, :], xts[j], nb)
                        pt = npt()
                        for k in range(8):
                            tr(pt[:, k * 128:(k + 1) * 128], nb["xn"][:, k * 128:(k + 1) * 128], ident[:])
                        cp("act", hT4[:, :, j * 128:(j + 1) * 128], pt[:].re("p (k t) -> p k t", k=8))
                    for c in range(8):
                        ps = nps()
                        for k in range(8):
                            mm(ps[:, :], wxq_b[:, k, c * 128:(c + 1) * 128], hT4[:, k, :], st=(k == 0), sp=(k == 7))
                        act(qxT4[:, c, :], ps[:, :], AF.Copy, scale=1.0 / 16)
                qxT = qxT4[:, :, jq * 128:(jq + 1) * 128]
                for mt in range(2):
                    S = nps()
                    for h in range(4):
                        for hf in range(2):
                            mm(S[:, h * 128:(h + 1) * 128], mkT[:, 2 * h + hf, mt * 128:(mt + 1) * 128], qxT[:, 2 * h + hf, :],
                               st=(hf == 0), sp=(hf == 1))
                    act(pts[mt][:], S[:, :], AF.Exp)
                for h in range(4):
                    acc = psa[h % 2]
                    for mt in range(2):
                        mm(acc[:, 0:257], pts[mt][:, h * 128:(h + 1) * 128], mvp[:, mt, h, :], st=(mt == 0), sp=(mt == 1))
                    self.recip(rdx[:], acc[:, 256:257])
                    ts("dve", ox[:, h * 256:(h + 1) * 256], acc[:, 0:256], rdx[:, 0:1], None, op0=ALU.mult)
                pto = npt()
                for k in range(8):
                    tr(pto[:, k * 128:(k + 1) * 128], ox[:, k * 128:(k + 1) * 128], ident[:])
                cp("act", oxT[:].re("p k t -> p (k t)"), pto[:])
                for g in range(2):
                    ps = nps()
                    for k in range(8):
                        mm(ps[:, :], oxT[:, k, :], wxo_b[:, k, g * 512:(g + 1) * 512], st=(k == 0), sp=(k == 7))
                    tt("dve", x2t[:, g * 512:(g + 1) * 512], xt[:, g * 512:(g + 1) * 512], ps[:, :], ALU.add)
                dma(x2d[r0:r0 + 128, :], x2t[:])
            if self.sample:
                S = self.S
                xt = xts[0]
                self.norm_T(S["x1s"][:, :], xt, nb, hT, ident, npt, rows=16)
                qxs = fw.sb(s, [128, 8, 16], BF16, "qxs")
                ps = nps()
                for c in range(8):
                    for k in range(8):
                        mm(ps[:, c * 16:(c + 1) * 16], wxq_b[:, k, c * 128:(c + 1) * 128], hT[:, k, 0:16], st=(k == 0), sp=(k == 7))
                act(qxs[:].re("p c t -> p (c t)"), ps[:, 0:128], AF.Copy, scale=1.0 / 16)
                ms("pool", ox[:], 0.0)
                msg = fw.sb(s, [128, 2, D], F32, "msg")
                mkb = fw.sb(s, [128, 2, D], BF16, "mkb")
                ptx = [fw.sb(s, [128, 16], BF16, f"ptx{i}") for i in range(2)]
                oxb = fw.sb(s, [4, D], BF16, "oxb")
                for b in range(4):
                    dma(msg[:], V(None, S["cmk"].a[b].rearrange("(t p) f -> p t f", p=128)))
                    cp("dve", mkb[:, 0, :], msg[:, 0, :])
                    cp("pool", mkb[:, 1, :], msg[:, 1, :])
                    for mt in range(2):
                        pt = npt()
                        for c in range(8):
                            tr(pt[:, c * 128:(c + 1) * 128], mkb[:, mt, c * 128:(c + 1) * 128], ident[:])
                        cp("act", mkT[:, :, mt * 128:(mt + 1) * 128], pt[:, :].re("p (c t) -> p c t", c=8))
                    dma(msg[:], V(None, S["cmv"].a[b].rearrange("(t p) f -> p t f", p=128)))
                    for mt in range(2):
                        cp("dve" if mt == 0 else "pool", mvp[:, mt, :, 0:256], msg[:, mt, :].re("p (h d) -> p h d", h=4))
                    for mt in range(2):
                        Sx = nps()
                        for h in range(4):
                            for hf in range(2):
                                mm(Sx[:, h * 4:(h + 1) * 4], mkT[:, 2 * h + hf, mt * 128:(mt + 1) * 128],
                                   qxs[:, 2 * h + hf, 4 * b:4 * b + 4], st=(hf == 0), sp=(hf == 1))
                        act(ptx[mt][:], Sx[:, 0:16], AF.Exp)
                    for h in range(4):
                        acc = psa[h % 2]
                        for mt in range(2):
                            mm(acc[0:4, 0:257], ptx[mt][:, 4 * h:4 * h + 4], mvp[:, mt, h, :], st=(mt == 0), sp=(mt == 1))
                        self.recip(rdx[0:4, :], acc[0:4, 256:257])
                        ts("dve", oxb[:, h * 256:(h + 1) * 256], acc[0:4, 0:256], rdx[0:4, 0:1], None, op0=ALU.mult)
                    dma(ox[4 * b:4 * b + 4, :], oxb[:])
                pto = npt()
                for k in range(8):
                    tr(pto[:, k * 128:(k + 1) * 128], ox[:, k * 128:(k + 1) * 128], ident[:])
                cp("act", oxT[:].re("p k t -> p (k t)"), pto[:])
                for g in range(2):
                    ps = nps()
                    for k in range(8):
                        mm(ps[:, :], oxT[:, k, :], wxo_b[:, k, g * 512:(g + 1) * 512], st=(k == 0), sp=(k == 7))
                    tt("dve", x2t[:, g * 512:(g + 1) * 512], xt[:, g * 512:(g + 1) * 512], ps[:, :], ALU.add)
                dma(S["x2s"][:, :], x2t[0:16, :])
            fw.barrier()
        with ExitStack() as s:
            wg_b = fw.sb(s, [128, 8, DFF], BF16, "wg_b")
            wu_b = fw.sb(s, [128, 8, DFF], BF16, "wu_b")
            wd_b = fw.sb(s, [128, 22, D], BF16, "wd_b")
            gfin = fw.sb(s, [128, D], F32, "gfin")
            dma(gfin[:], V(None, L["g_final"].a.partition_broadcast(128)))
            with ExitStack() as s2:
                stg = [fw.sb(s2, [128, DFF], F32, f"stg3{i}") for i in range(2)]
                self.load_w(wg_b, L["w_gate"], 8, DFF, stg, g2, 16)
                self.load_w(wu_b, L["w_up"], 8, DFF, stg, g2, 16)
                self.load_w(wd_b, L["w_down"], 22, D, stg)
                fw.barrier()
            xts = [fw.sb(s, [128, D], F32, f"x3_{i}") for i in range(4)]
            nb = self.norm_bufs(s, "3")
            hT4 = fw.sb(s, [128, 8, 512], BF16, "hT3")
            hT = fw.sb(s, [128, 8, 128], BF16, "hT3s")
            aT4 = fw.sb(s, [128, 22, 512], BF16, "aT4")
            aT = fw.sb(s, [128, 22, 128], BF16, "aT")
            sgs = [fw.sb(s, [128, 512], F32, f"sg{i}") for i in range(2)]
            sg = sgs[0]
            x3t = fw.sb(s, [128, D], F32, "x3t")

            def final_norm(dst_, rows_):
                ms("dve", nb["ss"][:], 0.0)
                act(nb["junk"][:], x3t[:], AF.Square, acc=nb["ss"][:])
                self.rstd(nb["rs"][:], nb["ss"][:], 1.0 / D)
                stt("dve", x3t[:], x3t[:], nb["rs"][:, 0:1], gfin[:], ALU.mult, ALU.mult)
                dma(dst_, x3t[0:rows_, :])

            for st_ in range(NT // 4):
                for j in range(4):
                    r0 = (st_ * 4 + j) * 128
                    self.norm_A(x2d[r0:r0 + 128, :], xts[j], nb)
                    pt = npt()
                    for k in range(8):
                        tr(pt[:, k * 128:(k + 1) * 128], nb["xn"][:, k * 128:(k + 1) * 128], ident[:])
                    cp("act", hT4[:, :, j * 128:(j + 1) * 128], pt[:].re("p (k t) -> p k t", k=8))
                for c in range(22):
                    pg, pu = nps(), nps()
                    for k in range(8):
                        mm(pg[:, :], wg_b[:, k, c * 128:(c + 1) * 128], hT4[:, k, :], st=(k == 0), sp=(k == 7))
                    for k in range(8):
                        mm(pu[:, :], wu_b[:, k, c * 128:(c + 1) * 128], hT4[:, k, :], st=(k == 0), sp=(k == 7))
                    sgc = sgs[c % 2]
                    act(sgc[:], pg[:, :], AF.Silu)
                    tt("dve", aT4[:, c, :], sgc[:], pu[:, :], ALU.mult)
                for j in range(4):
                    r0 = (st_ * 4 + j) * 128
                    for g in range(2):
                        ps = nps()
                        for c in range(22):
                            mm(ps[:, :], aT4[:, c, j * 128:(j + 1) * 128], wd_b[:, c, g * 512:(g + 1) * 512], st=(c == 0), sp=(c == 21))
                        tt("dve", x3t[:, g * 512:(g + 1) * 512], xts[j][:, g * 512:(g + 1) * 512], ps[:, :], ALU.add)
                    final_norm(y_p[r0:r0 + 128, :], 128)
            tiles = []
            if self.sample:
                tiles.append((self.S["x2s"][:, :], self.S["y_s"], 16))
            for t_, (src_, dst_, rows_) in enumerate(tiles):
                xt = xts[t_ % 2]
                self.norm_T(src_, xt, nb, hT, ident, npt, rows=rows_)
                for c0 in range(0, 22, 4):
                    n = min(4, 22 - c0)
                    pg = nps()
                    pu = nps()
                    for cc in range(n):
                        c = c0 + cc
                        for k in range(8):
                            mm(pg[:, cc * 128:(cc + 1) * 128], wg_b[:, k, c * 128:(c + 1) * 128], hT[:, k, :], st=(k == 0), sp=(k == 7))
                        for k in range(8):
                            mm(pu[:, cc * 128:(cc + 1) * 128], wu_b[:, k, c * 128:(c + 1) * 128], hT[:, k, :], st=(k == 0), sp=(k == 7))
                    act(sg[:, 0:n * 128], pg[:, 0:n * 128], AF.Silu)
                    tt("dve", aT[:, c0:c0 + n, :].re("p c t -> p (c t)"), sg[:, 0:n * 128], pu[:, 0:n * 128], ALU.mult)
                for g in range(2):
                    ps = nps()
                    for c in range(22):
                        mm(ps[:, :], aT[:, c, :], wd_b[:, c, g * 512:(g + 1) * 512], st=(c == 0), sp=(c == 21))
                    tt("dve", x3t[:, g * 512:(g + 1) * 512], xt[:, g * 512:(g + 1) * 512], ps[:, :], ALU.add)
                ms("dve", nb["ss"][:], 0.0)
                act(nb["junk"][:], x3t[:], AF.Square, acc=nb["ss"][:])
                self.rstd(nb["rs"][:], nb["ss"][:], 1.0 / D)
                stt("dve", x3t[:], x3t[:], nb["rs"][:, 0:1], gfin[:], ALU.mult, ALU.mult)
                dma(dst_, x3t[0:rows_, :])
            fw.barrier()


def sample_io(self):
    din, dout = self.din, self.dout
    S = {"xs": din("xs", [16, D]), "ptab": din("ptab", [4, 128], I32)}
    for n in ("pool_kc", "pool_vc", "pool_ks", "pool_vs"):
        S[n] = din(n, [5120, 16384])
    S["stk"] = din("stk", [4, 512, 128])
    S["stv"] = din("stv", [4, 512, 128])
    S["sconv"] = din("sconv", [4, 3, 512])
    S["sC"] = din("sC", [4, 4, 128, 128])
    S["sn"] = din("sn", [4, 4, 128])
    S["sm"] = din("sm", [16])
    S["cmk"] = din("cmk", [4, 256, D])
    S["cmv"] = din("cmv", [4, 256, D])
    S["y_s"] = dout("y_s", [16, D])
    for n in ("s_kc", "s_vc", "s_ks", "s_vs"):
        S[n] = dout(n, [16, 128])
    S["s_kw"] = dout("s_kw", [4, 512, 128])
    S["s_vw"] = dout("s_vw", [4, 512, 128])
    S["s_C"] = dout("s_C", [4, 4, 128, 128])
    S["s_n"] = dout("s_n", [4, 4, 128])
    S["s_m"] = dout("s_m", [16])
    S["s_conv"] = dout("s_conv", [4, 3, 512])
    S["kcS_d"] = self.dscr("kcS_d", [4, 2, 64, 1024], BF16)
    S["vcS_d"] = self.dscr("vcS_d", [4, 128, 8, 2, 64], BF16)
    S["x1s"] = self.dscr("x1s", [16, D])
    S["x2s"] = self.dscr("x2s", [16, D])
    self.S = S
    return S


def load_cmp(self, s, kv, cmp_in, nps):
    fw = self.fw
    mm, tt, cp, dma, dmas = self.mm, self.tt, self.cp, self.dma, self.dmas
    pe, w1, b1, w2 = cmp_in[kv]
    w1b = fw.sb(s, [128, 16, 256], BF16, "Sw1b" + kv)
    w1v = w1.a.rearrange("(jp jj d) n -> jj d jp n", jj=2, d=64)
    with ExitStack() as t:
        w1s = [fw.sb(t, [128, 8, 256], F32, f"Sw1s{kv}{i}") for i in range(2)]
        for jb in range(2):
            st = w1s[jb % 2]
            for jj in range(2):
                dma(st[jj * 64:(jj + 1) * 64], V(None, w1v[jj][:, jb * 8:(jb + 1) * 8, :]))
            cp("dve", w1b[:, jb * 8:(jb + 1) * 8, :], st[:])
        fw.barrier()
    peT = fw.sb(s, [128, 16], F32, "SpeT" + kv)
    pev = pe.a.rearrange("(jp jj) d -> jj d jp", jj=2)
    for jj in range(2):
        dmas(peT[jj * 64:(jj + 1) * 64, :], V(None, pev[jj]))
    peTb = fw.sb(s, [128, 16], BF16, "SpeTb" + kv)
    cp("dve", peTb[:], peT[:])
    b1c = fw.sb(s, [128, 2], F32, "Sb1c" + kv)
    dmas(b1c[:], V(None, b1.a.rearrange("(c p) -> p c", p=128)))
    w2s = fw.sb(s, [128, 2, 64], F32, "Sw2s" + kv)
    dma(w2s[:], V(None, w2.a.rearrange("(c p) n -> p c n", p=128)))
    w2b = fw.sb(s, [128, 2, 64], BF16, "Sw2b" + kv)
    cp("dve", w2b[:], w2s[:])
    cst = fw.sb(s, [128, 2], F32, "Scst" + kv)
    for hc in range(2):
        ps = nps()
        for jp in range(16):
            mm(ps[:, 0:1], w1b[:, jp, hc * 128:(hc + 1) * 128], peTb[:, jp:jp + 1], st=(jp == 0), sp=(jp == 15))
        tt("dve", cst[:, hc:hc + 1], ps[:, 0:1], b1c[:, hc:hc + 1], ALU.add)
    return w1b, w2b, cst


def gather(self, dst, pool, idx, r0):
    self.fw.dma("pool", dst, pool, extra_reads=[idx.b],
                fn=lambda e: e.indirect_dma_start(out=dst.a, out_offset=None, in_=pool.a,
                                                  in_offset=bass.IndirectOffsetOnAxis(ap=idx.a, axis=0),
                                                  element_offset=r0 * 128))


def sample_s0(self, L):
    fw, S = self.fw, self.S
    mm, tr, act, cp, ms, dma = self.mm, self.tr, self.act, self.cp, self.ms, self.dma
    ident, nps, npt, cmp_in = L["ident"], L["nps"], L["npt"], L["cmp_in"]
    with ExitStack() as s:
        idx = [fw.sb(s, [128, 1], I32, f"S0idx{b}") for b in range(4)]
        for b in range(4):
            dma(idx[b][:], V(None, S["ptab"].a[b].rearrange("(p o) -> p o", o=1)))
        cw_ = {kv: load_cmp(self, s, kv, cmp_in, nps) for kv in "kv"}
        srcS = fw.sb(s, [128, 2, 64, 129], BF16, "srcS")
        ms("pool", srcS[:, :, :, 128:129], 0.0)
        gchs = [fw.sb(s, [128, 4096], F32, f"gch{i}") for i in range(2)]
        gbf = fw.sb(s, [128, 2, 32, 64], BF16, "gbf")
        chunks0 = [(b, kv, i) for b in range(4) for kv in "kv" for i in range(4)]

        def issue0(n):
            b_, kv_, i_ = chunks0[n]
            gather(self, gchs[n % 2][:], S["pool_" + kv_ + "c"], idx[b_][:, :], 32 * i_)

        issue0(0)
        n0 = 0
        gT = fw.sb(s, [128, 2, 1024], BF16, "SgT")
        ko = fw.sb(s, [64, 1024], BF16, "Sko")
        vo = fw.sb(s, [128, 8, 64], BF16, "Svo")
        for b in range(4):
            for kv in "kv":
                w1b, w2b, cst = cw_[kv]
                pool = S["pool_" + kv + "c"]
                for i in range(4):
                    gch = gchs[n0 % 2]
                    if n0 + 1 < len(chunks0):
                        issue0(n0 + 1)
                    n0 += 1
                    gv4 = gch[:].re("p (r k d) -> p k r d", r=32, k=2)
                    cp("dve", gbf[:, 0], gv4[:, 0])
                    cp("act", gbf[:, 1], gv4[:, 1])
                    for kvh in range(2):
                        for g8 in range(2):
                            pt = npt()
                            for r8 in range(8):
                                rp = 8 * g8 + r8
                                tr(pt[:, r8 * 128:(r8 + 1) * 128], gbf[:, kvh, 2 * rp:2 * rp + 2, :].re("p r d -> p (r d)"), ident[:])
                            cp("act" if g8 % 2 == 0 else "dve", srcS[:, kvh, 16 * i + 8 * g8:16 * i + 8 * g8 + 8, 0:128],
                               pt[:, :].re("p (r t) -> p r t", r=8))
                for kvh in range(2):
                    for hc in range(2):
                        wv = lambda j: w1b[:, j, hc * 128:(hc + 1) * 128]
                        for bank in range(2):
                            ps = nps()
                            o4 = ps[:, :].re("p (a t) -> p a t", a=4)
                            for jp in range(16):
                                r0 = 32 * bank + jp
                                if bank == 0 or jp < 8:
                                    mm(o4, wv(jp), srcS[:, kvh, r0:r0 + 25:8, 0:128], st=(jp == 0), sp=(jp == 15))
                                else:
                                    mm(o4[:, 0:3, :], wv(jp), srcS[:, kvh, r0:r0 + 17:8, 0:128], st=False, sp=False)
                                    mm(ps[:, 384:512], wv(jp), srcS[:, kvh, jp - 8, 1:129], st=False, sp=(jp == 15))
                            act(gT[:, hc, bank * 512:(bank + 1) * 512], ps[:, :], AF.Gelu_apprx_tanh, bias=cst[:, hc:hc + 1])
                    if kv == "k":
                        for bank in range(2):
                            ps = nps()
                            for hc in range(2):
                                mm(ps[0:64, :], w2b[:, hc, :], gT[:, hc, bank * 512:(bank + 1) * 512], st=(hc == 0), sp=(hc == 1))
                            cp("dve", ko[:, bank * 512:(bank + 1) * 512], ps[0:64, :])
                        dma(S["kcS_d"][b, kvh], ko[:])
                    else:
                        ps = nps()
                        for rb in range(8):
                            for hc in range(2):
                                mm(ps[:, rb * 64:(rb + 1) * 64], gT[:, hc, rb * 128:(rb + 1) * 128], w2b[:, hc, :],
                                   st=(hc == 0), sp=(hc == 1))
                        cp("dve", vo[:].re("p r d -> p (r d)"), ps[:, :])
                        dma(S["vcS_d"][b][:, :, kvh, :], vo[:])
        fw.barrier()


Builder.sample_io = sample_io

def sample_pass1(self, L):
    fw, S = self.fw, self.S
    mm, tr, act, ts, tt, stt, cp, ms, iota, dma, dmas = (self.mm, self.tr, self.act, self.ts, self.tt, self.stt,
                                                         self.cp, self.ms, self.iota, self.dma, self.dmas)
    win_b, wout_b, wqm_b, wkm_b = L["win_b"], L["wout_b"], L["wqm_b"], L["wkm_b"]
    ident, identf, nps, npt, psa = L["ident"], L["identf"], L["nps"], L["npt"], L["psa"]
    cw, cb, bgate, bif, tmpf = L["cw"], L["cb"], L["bgate"], L["bif"], L["tmpf"]
    put_row = self.put_row
    KSC = 128.0 ** -0.5
    with ExitStack() as s:
        xt = fw.sb(s, [128, D], F32, "xS")
        nb = self.norm_bufs(s, "S")
        hT = fw.sb(s, [128, 8, 128], BF16, "hTS")
        pkv = fw.sb(s, [128, 792], F32, "pkvS")
        gt = fw.sb(s, [128, 24], F32, "gtS")
        vnS = fw.sb(s, [16, 2, 2, 65], BF16, "vnS")
        qTs = fw.sb(s, [64, 8, 16], BF16, "qTs")
        mixin = fw.sb(s, [128, D], BF16, "mixinS")
        s2 = ExitStack()
        QS = fw.sb(s2, [68, 4, 2, 16], BF16, "QS")
        kcSb = fw.sb(s2, [68, 2, 8, 128], BF16, "kcSb")
        vcSb = fw.sb(s2, [128, 8, 2, 65], BF16, "vcSb")
        ksS = fw.sb(s2, [68, 2, 32, 128], BF16, "ksS")
        kwS = fw.sb(s2, [68, 2, 4, 128], BF16, "kwS")
        KnS = fw.sb(s2, [68, 2, 2, 16], BF16, "KnS")
        ms("pool", vcSb[:], 1.0)
        with ExitStack() as tmps:
            self.rowt = fw.sb(tmps, [1, 4096], F32, "rowtS")
            self.rowb = fw.sb(tmps, [1, 4096], BF16, "rowbS")
            sr = fw.sb(tmps, [1, 2, 4, 4], F32, "srS")
            for h in range(8):
                ms("pool", sr[0:1, h // 4, h % 4, :], 2.0 ** (-(h + 1)))
            qi = fw.sb(tmps, [1, 2, 4, 4], F32, "qiS")
            iota(qi[:].re("p k g q -> p (k g) q"), [[0, 8], [1, 4]], base=0, cm=0)
            rw = fw.sb(tmps, [1, 4, 2, 16], F32, "rwS")
            rwb = fw.sb(tmps, [1, 4, 2, 16], BF16, "rwbS")
            srv = sr[:].re("p k g q -> p k (g q)")
            for row in range(4):
                for i in range(4):
                    if row == 0:
                        ts("pool", rw[0:1, i], srv, -1.0, None, op0=ALU.mult)
                    elif row == 1:
                        cp("pool", rw[0:1, i], srv)
                    elif row == 2:
                        tt("pool", rw[0:1, i], srv, qi[:].re("p k g q -> p k (g q)"), ALU.mult)
                        ts("pool", rw[0:1, i], rw[0:1, i], -1.0, None, op0=ALU.mult)
                    else:
                        ts("pool", rw[0:1, i], srv, 32.0 * i, None, op0=ALU.mult)
                cp("pool", rwb[:], rw[:])
                dma(QS[64 + row:65 + row], rwb[:])
            for kvh in range(2):
                put_row(kcSb[64:65, kvh].re("p r t -> p (r t)"), [[0, 8], [-128, 128]], 16384, 1024)
                put_row(kcSb[65:66, kvh].re("p r t -> p (r t)"), [[16, 8], [0, 128]], 31, 1024)
                put_row(kcSb[66:67, kvh].re("p r t -> p (r t)"), None, 0, 1024, const=1.0)
                put_row(kcSb[67:68, kvh].re("p r t -> p (r t)"), None, 0, 1024, const=0.0)
                put_row(ksS[64:65, kvh].re("p r t -> p (r t)"), [[0, 32], [-128, 128]], 16384, 4096)
                put_row(ksS[65:66, kvh].re("p r t -> p (r t)"), [[1, 32], [0, 128]], 0, 4096)
                put_row(ksS[66:67, kvh].re("p r t -> p (r t)"), None, 0, 4096, const=1.0)
                put_row(ksS[67:68, kvh].re("p r t -> p (r t)"), None, 0, 4096, const=1.0)
                put_row(kwS[64:65, kvh].re("p r t -> p (r t)"), [[-128, 4], [0, 128]], 512, 512)
                put_row(kwS[65:66, kvh].re("p r t -> p (r t)"), [[0, 4], [1, 128]], 0, 512)
                put_row(kwS[66:67, kvh].re("p r t -> p (r t)"), None, 0, 512, const=1.0)
                put_row(kwS[67:68, kvh].re("p r t -> p (r t)"), None, 0, 512, const=0.0)
                for sw_ in range(2):
                    put_row(KnS[64:65, sw_, kvh], None, 0, 16, const=0.0)
                    put_row(KnS[65:66, sw_, kvh], [[0, 4], [1, 4]], 0, 16)
                    put_row(KnS[66:67, sw_, kvh], None, 0, 16, const=1.0)
                    put_row(KnS[67:68, sw_, kvh], None, 0, 16, const=0.0)
            fw.barrier()
        maskC7 = fw.sb(s2, [128, 1], F32, "maskC7")
        iota(tmpf[:, 0:1], [[0, 1]], base=0, cm=1)
        ts("pool", maskC7[:], tmpf[:, 0:1], 127.0, None, op0=ALU.is_lt)
        winm0 = fw.sb(s2, [128, 4], BF16, "winm0")
        iota(tmpf[:, 0:4], [[-1, 4]], base=0, cm=1)
        ts("pool", winm0[:], tmpf[:, 0:4], 0.0, None, op0=ALU.is_gt)
        newm = fw.sb(s2, [16, 4, 4], BF16, "newm")
        for b in range(4):
            iota(tmpf[0:16, 0:4], [[-1, 4]], base=-4 * b, cm=1)
            ts("pool", tmpf[0:16, 4:8], tmpf[0:16, 0:4], 0.0, None, op0=ALU.is_le)
            iota(tmpf[0:16, 8:12], [[0, 4]], base=-4 * b, cm=1)
            ts("pool", tmpf[0:16, 8:12], tmpf[0:16, 8:12], 0.0, None, op0=ALU.is_ge)
            tt("pool", newm[:, b, :], tmpf[0:16, 4:8], tmpf[0:16, 8:12], ALU.mult)
        mimpS = fw.sb(s2, [128, 8, 256], BF16, "mimpS")
        idx = [fw.sb(s2, [128, 1], I32, f"S1idx{b}") for b in range(4)]
        for b in range(4):
            dma(idx[b][:], V(None, S["ptab"].a[b].rearrange("(p o) -> p o", o=1)))
        with ExitStack() as tm:
            mtmp = fw.sb(tm, [128, 3, 256], F32, "mtmp")
            for rb in range(8):
                iota(mtmp[:, 0, :], [[-4, 256]], base=rb - 1, cm=8)
                stt("dve", mtmp[:, 1, :], mtmp[:, 0, :], -1.0, mtmp[:, 0, :], ALU.mult, ALU.max)
                ts("pool", mtmp[:, 0, :], mtmp[:, 1, :], 2.0, 0.5, op0=ALU.is_le, op1=ALU.mult)
                ts("pool", mtmp[:, 2, :], mtmp[:, 1, :], 1.0, 0.5, op0=ALU.is_le, op1=ALU.mult)
                tt("pool", mimpS[:, rb, :], mtmp[:, 0, :], mtmp[:, 2, :], ALU.add)
            fw.barrier()

        self.norm_T(S["xs"], xt, nb, hT, ident, npt, rows=16)
        psA, psB = nps(), nps()
        for k in range(8):
            mm(psA[:, 0:512], hT[:, k, :], win_b[:, k, 512:1024], st=(k == 0), sp=(k == 7))
        for k in range(8):
            mm(psB[:, 0:280], hT[:, k, :], win_b[:, k, 1024:1304], st=(k == 0), sp=(k == 7))
        cp("dve", pkv[:, 0:512], psA[:, 0:512])
        cp("act", pkv[:, 512:792], psB[:, 0:280])
        for i_, n_ in enumerate(("s_kc", "s_vc", "s_ks", "s_vs")):
            dma(S[n_], pkv[0:16, i_ * 128:(i_ + 1) * 128])
        tt("dve", gt[:], pkv[:, 768:792], bgate[:], ALU.add)
        self.sigm(gt[:], gt[:])
        ms("pool", vnS[:], 1.0)
        cp("dve", vnS[:, 0, :, 0:64], pkv[0:16, 384:512].re("p (h d) -> p h d", h=2))
        cp("dve", vnS[:, 1, :, 0:64], pkv[0:16, 640:768].re("p (h d) -> p h d", h=2))
        psQ = nps()
        for h in range(8):
            for k in range(8):
                mm(psQ[0:64, h * 16:(h + 1) * 16], win_b[:, k, 64 * h:64 * h + 64], hT[:, k, 0:16], st=(k == 0), sp=(k == 7))
        act(qTs[:].re("p h t -> p (h t)"), psQ[0:64, 0:128], AF.Copy, scale=0.125)
        psK = nps()
        for gi, c0 in enumerate((768, 832, 1024, 1088)):
            for k in range(8):
                mm(psK[0:64, gi * 16:(gi + 1) * 16], win_b[:, k, c0:c0 + 64], hT[:, k, 0:16], st=(k == 0), sp=(k == 7))
        cp("dve", KnS[0:64].re("p a k t -> p (a k t)"), psK[0:64, 0:64])

        gks = [fw.sb(s2, [128, 4096], F32, f"gk{i}") for i in range(2)]
        gvs = [fw.sb(s2, [128, 4096], F32, f"gv{i}") for i in range(2)]
        wks, wvs = gks[1][:, 0:512], gvs[1][:, 0:512]
        chunks1 = [(b_, i_) for b_ in range(4) for i_ in range(4)]

        def issue1(n):
            b_, i_ = chunks1[n]
            gather(self, gks[n % 2][:], S["pool_ks"], idx[b_][:, :], 32 * i_)
            gather(self, gvs[n % 2][:], S["pool_vs"], idx[b_][:, :], 32 * i_)

        issue1(0)
        n1 = 0
        kb = fw.sb(s2, [128, 32, 128], BF16, "kbS")
        vbp = fw.sb(s2, [128, 32, 2, 65], BF16, "vbp")
        ms("pool", vbp[:], 1.0)
        vwS = fw.sb(s2, [128, 4, 2, 65], BF16, "vwS")
        ms("pool", vwS[:], 1.0)
        ptc = fw.sb(s2, [128, 8, 16], BF16, "ptc")
        ptsb = fw.sb(s2, [128, 32, 16], BF16, "ptsb")
        ptw = fw.sb(s2, [128, 4, 16], BF16, "ptw")
        ptn = fw.sb(s2, [16, 16], BF16, "ptn")
        maskEO = [fw.sb(s2, [128, 2, 4, 4], BF16, f"maskEO{k}") for k in range(2)]
        obr = [[fw.sb(s2, [4, 4, 65], F32, f"obrS{k}{i}") for i in range(3)] for k in range(2)]
        imp = fw.sb(s2, [4, 256], F32, "impS")
        imp2 = fw.sb(s2, [4, 256], F32, "imp2S")
        mx1 = fw.sb(s2, [4, 8], F32, "mx1S")
        mx2 = fw.sb(s2, [4, 8], F32, "mx2S")
        sel01 = imp2
        selT = fw.sb(s2, [128, 8], F32, "selTS")
        gtb = fw.sb(s2, [4, 24], F32, "gtb")
        rd = fw.sb(s2, [4, 3, 4], F32, "rdS")
        sc3 = fw.sb(s2, [4, 3, 4], F32, "sc3S")
        onsa = fw.sb(s2, [4, 4, 64], F32, "onsaS")
        otmp = fw.sb(s2, [4, 4, 64], F32, "otmpS")
        ss4 = fw.sb(s2, [4, 4], F32, "ss4S")
        rs4 = fw.sb(s2, [4, 4], F32, "rs4S")
        onb = fw.sb(s2, [4, 512], BF16, "onbS")
        ms("pool", mixin[:], 0.0)
        for b in range(4):
            cp("dve", QS[0:64].re("p i k (g q) -> p i (k g) q", g=4),
               qTs[:, :, 4 * b:4 * b + 4].un(1).bc([64, 4, 8, 4]))
            dma(kcSb[0:64].re("p k r t -> p k (r t)"), V(S["kcS_d"], S["kcS_d"][b].a.rearrange("k d n -> d k n")))
            dma(vcSb[:].re("p r k d -> p (r k) d")[:, :, 0:64], V(S["vcS_d"], S["vcS_d"][b].a.rearrange("p r k d -> p (r k) d")))
            dma(gtb[:], gt[4 * b:4 * b + 4, :])
            for kvh in range(2):
                Sc = nps()
                for rb in range(8):
                    mm(Sc[:, rb * 16:(rb + 1) * 16], kcSb[:, kvh, rb, :], QS[:, 0, kvh, :])
                act(ptc[:].re("p r c -> p (r c)"), Sc[:, 0:128], AF.Exp)
                ts("dve", ptc[:, 7, :], ptc[:, 7, :], maskC7[:, 0:1], None, op0=ALU.mult)
                accC = psa[0]
                impP = [nps(), nps()]
                for rb in range(8):
                    for g in range(4):
                        mm(accC[0:4, g * 65:(g + 1) * 65], ptc[:, rb, 4 * g:4 * g + 4], vcSb[:, rb, kvh, :],
                           st=(rb == 0), sp=(rb == 7))
                        mm(impP[g // 2][0:4, (g % 2) * 256:(g % 2) * 256 + 256], ptc[:, rb, 4 * g:4 * g + 4],
                           mimpS[:, rb, :], st=(rb == 0), sp=(rb == 7))
                cp("dve", obr[kvh][0][:].re("p g d -> p (g d)"), accC[0:4, 0:260])
                ts("dve", rd[:, 0, :], obr[kvh][0][:, :, 64], 1e-30, None, op0=ALU.max)
                self.recip(rd[:, 0, :], rd[:, 0, :])
                ts("dve", imp[:], impP[0][0:4, 0:256], rd[:, 0, 0:1], None, op0=ALU.mult)
                for g in range(1, 4):
                    stt("dve", imp[:], impP[g // 2][0:4, (g % 2) * 256:(g % 2) * 256 + 256], rd[:, 0, g:g + 1], imp[:],
                        ALU.mult, ALU.add)
                ms("dve", imp[:, 0:1], 3e9)
                ms("dve", imp[:, 255:256], 1e9)
                fw.op("dve", lambda e: e.max(out=mx1[:].a, in_=imp[:].a), reads=[imp], writes=[mx1])
                fw.op("dve", lambda e: e.match_replace(out=imp2[:].a, in_to_replace=mx1[:].a, in_values=imp[:].a,
                                                       imm_value=-1e30), reads=[imp, mx1], writes=[imp2])
                fw.op("dve", lambda e: e.max(out=mx2[:].a, in_=imp2[:].a), reads=[imp2], writes=[mx2])
                ts("dve", sel01[:], imp[:], mx2[:, 6:7], None, op0=ALU.is_ge)
                psT = nps()
                tr(psT[:, 0:4], sel01[0:4, 0:256:2], identf[0:4, 0:4])
                tr(psT[:, 4:8], sel01[0:4, 1:256:2], identf[0:4, 0:4])
                cp("dve", selT[:], psT[:, 0:8])
                cp("dve", maskEO[kvh][:], selT[:].re("p (e q) -> p e q", e=2).un(2).bc([128, 2, 4, 4]))
            accS = [psa[0], psa[1]]
            for i in range(4):
                gk, gv = gks[n1 % 2], gvs[n1 % 2]
                if n1 + 1 < len(chunks1):
                    issue1(n1 + 1)
                n1 += 1
                cp("dve", kb[:, 0:16, :].re("p r f -> p (r f)"), gk[:, 0:2048])
                cp("act", kb[:, 16:32, :].re("p r f -> p (r f)"), gk[:, 2048:4096])
                cp("dve", vbp[:, 0:16, :, 0:64], gv[:, 0:2048].re("p (r h d) -> p r h d", r=16, h=2))
                cp("act", vbp[:, 16:32, :, 0:64], gv[:, 2048:4096].re("p (r h d) -> p r h d", r=16, h=2))
                for kvh in range(2):
                    for g8 in range(4):
                        pt = npt()
                        for r8 in range(8):
                            tr(pt[0:64, r8 * 128:(r8 + 1) * 128], kb[:, 8 * g8 + r8, kvh * 64:(kvh + 1) * 64], ident[:])
                        cp("act" if g8 % 2 == 0 else "dve", ksS[0:64, kvh, 8 * g8:8 * g8 + 8, :],
                           pt[0:64, :].re("p (r t) -> p r t", r=8))
                for kvh in range(2):
                    Ss = nps()
                    for r_ in range(32):
                        mm(Ss[:, r_ * 16:(r_ + 1) * 16], ksS[:, kvh, r_, :], QS[:, i, kvh, :])
                    act(ptsb[:].re("p r c -> p (r c)"), Ss[:, :], AF.Exp)
                    tt("dve", ptsb[:].re("p r (g q) -> p r g q", g=4), ptsb[:].re("p r (g q) -> p r g q", g=4),
                       maskEO[kvh][:, i // 2].un(1).bc([128, 32, 4, 4]), ALU.mult)
                    for r_ in range(32):
                        for g in range(4):
                            mm(accS[kvh][0:4, g * 65:(g + 1) * 65], ptsb[:, r_, 4 * g:4 * g + 4], vbp[:, r_, kvh, :],
                               st=(i == 0 and r_ == 0), sp=False)
            for kvh in range(2):
                Sn = nps()
                mm(Sn[0:16, 0:16], KnS[:, 0, kvh, :], QS[:, 0, kvh, :])
                act(ptn[:], Sn[0:16, 0:16], AF.Exp)
                tt("dve", ptn[:].re("p (g q) -> p g q", g=4), ptn[:].re("p (g q) -> p g q", g=4),
                   newm[:, b, :].un(1).bc([16, 4, 4]), ALU.mult)
                for g in range(4):
                    mm(accS[kvh][0:4, g * 65:(g + 1) * 65], ptn[:, 4 * g:4 * g + 4], vnS[:, 0, kvh, :], st=False, sp=True)
                cp("dve", obr[kvh][1][:].re("p g d -> p (g d)"), accS[kvh][0:4, 0:260])
            dma(wks.re("p (a f) -> p a f", a=4), V(None, S["stk"].a[b].rearrange("(a p) f -> p a f", p=128)))
            dma(wvs.re("p (a f) -> p a f", a=4), V(None, S["stv"].a[b].rearrange("(a p) f -> p a f", p=128)))
            cp("dve", kb[:, 0:4, :].re("p r f -> p (r f)"), wks)
            cp("dve", vwS[:, :, :, 0:64], wvs.re("p (a h d) -> p a h d", a=4, h=2))
            pt = npt()
            for kvh in range(2):
                for a in range(4):
                    tr(pt[0:64, (kvh * 4 + a) * 128:(kvh * 4 + a + 1) * 128], kb[:, a, kvh * 64:(kvh + 1) * 64], ident[:])
            cp("act", kwS[0:64].re("p k a t -> p (k a t)"), pt[0:64, :])
            accW = [psa[0], psa[1]]
            for kvh in range(2):
                Sw = nps()
                for a in range(4):
                    mm(Sw[:, a * 16:(a + 1) * 16], kwS[:, kvh, a, :], QS[:, 0, kvh, :])
                act(ptw[:].re("p a c -> p (a c)"), Sw[:, 0:64], AF.Exp)
                tt("dve", ptw[:, 0, :].re("p (g q) -> p g q", g=4), ptw[:, 0, :].re("p (g q) -> p g q", g=4),
                   winm0[:].un(1).bc([128, 4, 4]), ALU.mult)
                for a in range(4):
                    for g in range(4):
                        mm(accW[kvh][0:4, g * 65:(g + 1) * 65], ptw[:, a, 4 * g:4 * g + 4], vwS[:, a, kvh, :],
                           st=(a == 0), sp=False)
                Sn = nps()
                mm(Sn[0:16, 0:16], KnS[:, 1, kvh, :], QS[:, 0, kvh, :])
                act(ptn[:], Sn[0:16, 0:16], AF.Exp)
                tt("dve", ptn[:].re("p (g q) -> p g q", g=4), ptn[:].re("p (g q) -> p g q", g=4),
                   newm[:, b, :].un(1).bc([16, 4, 4]), ALU.mult)
                for g in range(4):
                    mm(accW[kvh][0:4, g * 65:(g + 1) * 65], ptn[:, 4 * g:4 * g + 4], vnS[:, 1, kvh, :], st=False, sp=True)
                cp("dve", obr[kvh][2][:].re("p g d -> p (g d)"), accW[kvh][0:4, 0:260])
            for nm_, src_, c0 in (("s_kw", "stk", 512), ("s_vw", "stv", 640)):
                dma(V(None, S[nm_].a[b, 0:508, :]), V(None, S[src_].a[b, 4:512, :]))
                dma(V(None, S[nm_].a[b, 508:512, :]), pkv[4 * b:4 * b + 4, c0:c0 + 128])
            for kvh in range(2):
                for br in range(1, 3):
                    ts("dve", rd[:, br, :], obr[kvh][br][:, :, 64], 1e-30, None, op0=ALU.max)
                    self.recip(rd[:, br, :], rd[:, br, :])
                ts("dve", rd[:, 0, :], obr[kvh][0][:, :, 64], 1e-30, None, op0=ALU.max)
                self.recip(rd[:, 0, :], rd[:, 0, :])
                gvw = gtb[:, 12 * kvh:12 * kvh + 12].re("p (g b) -> p b g", b=3)
                tt("dve", sc3[:], rd[:], gvw, ALU.mult)
                tt("dve", onsa[:], obr[kvh][0][:, :, 0:64], sc3[:, 0, :].un(2).bc([4, 4, 64]), ALU.mult)
                for br in (1, 2):
                    tt("dve", otmp[:], obr[kvh][br][:, :, 0:64], sc3[:, br, :].un(2).bc([4, 4, 64]), ALU.mult)
                    tt("dve", onsa[:], onsa[:], otmp[:], ALU.add)
                tt("dve", otmp[:], onsa[:], onsa[:], ALU.mult)
                self.rsum(ss4[:], otmp[:])
                self.rstd(rs4[:], ss4[:], 1.0 / 64)
                tt("dve", onb[:, 256 * kvh:256 * kvh + 256].re("p (g d) -> p g d", g=4), onsa[:],
                   rs4[:].un(2).bc([4, 4, 64]), ALU.mult)
            dma(mixin[4 * b:4 * b + 4, 0:512], onb[:])
        fw.barrier()
        s2.close()
        self.sample_mlstm(s, L, hT, mixin)
        mT = fw.sb(s, [128, 8, 128], BF16, "mTS")
        x1t = fw.sb(s, [128, D], F32, "x1tS")
        ptm = npt()
        for k in range(8):
            tr(ptm[:, k * 128:(k + 1) * 128], mixin[:, k * 128:(k + 1) * 128], ident[:])
        cp("act", mT[:].re("p k t -> p (k t)"), ptm[:])
        for g in range(2):
            ps = nps()
            for k in range(8):
                mm(ps[:, :], mT[:, k, :], wout_b[:, k, g * 512:(g + 1) * 512], st=(k == 0), sp=(k == 7))
            tt("dve", x1t[:, g * 512:(g + 1) * 512], xt[:, g * 512:(g + 1) * 512], ps[:, :], ALU.add)
        dma(S["x1s"][:, :], x1t[0:16, :])
        fw.barrier()


Builder.sample_pass1 = sample_pass1


def sample_mlstm(self, s, L, hT, mixin):
    fw, S = self.fw, self.S
    mm, tr, act, ts, tt, stt, cp, ms, iota, dma, dmas = (self.mm, self.tr, self.act, self.ts, self.tt, self.stt,
                                                         self.cp, self.ms, self.iota, self.dma, self.dmas)
    win_b, wqm_b, wkm_b = L["win_b"], L["wqm_b"], L["wkm_b"]
    identf, nps, cw, cb, bif, tmpf = L["identf"], L["nps"], L["cw"], L["cb"], L["bif"], L["tmpf"]
    KSC = 128.0 ** -0.5
    E = fw.sb(s, [4, 128], F32, "E4")
    iota(E[:], [[1, 128]], base=0, cm=-4)
    Eb = fw.sb(s, [4, 128], F32, "E4b")
    ts("pool", Eb[:], E[:], 0.0, None, op0=ALU.is_ge)
    ts("pool", E[:], E[:], 3.0, None, op0=ALU.is_le)
    tt("pool", E[:], E[:], Eb[:], ALU.mult)
    triS = fw.sb(s, [128, 128], F32, "triS")
    ps = nps()
    mm(ps[:, 0:128], E[:], E[:])
    tt("dve", triS[:], ps[:, 0:128], L["tri_le"][:], ALU.mult)
    bdS = fw.sb(s, [16, 16], BF16, "bdS")
    cp("dve", bdS[:], triS[0:16, 0:16])
    cselS = fw.sb(s, [128, 4, 128], F32, "cselS")
    d4 = fw.sb(s, [4, 4, 128], F32, "d4")
    cp("dve", d4[:], identf[0:4, 0:4].un(2).bc([4, 4, 128]))
    ps = nps()
    for b in range(4):
        mm(ps[:, b * 128:(b + 1) * 128], E[:], d4[:, b, :])
    cp("dve", cselS[:].re("p b m -> p (b m)"), ps[:, :])
    psV, psO, psG = nps(), nps(), nps()
    for k in range(8):
        mm(psV[:, 0:512], hT[:, k, :], win_b[:, k, 1816:2328], st=(k == 0), sp=(k == 7))
    for k in range(8):
        mm(psO[:, 0:512], hT[:, k, :], win_b[:, k, 2328:2840], st=(k == 0), sp=(k == 7))
    for k in range(8):
        mm(psG[:, 0:8], hT[:, k, :], win_b[:, k, 2840:2848], st=(k == 0), sp=(k == 7))
    gif = fw.sb(s, [128, 8], F32, "gifS")
    l1 = fw.sb(s, [128, 4], F32, "l1S")
    sigo = fw.sb(s, [128, 512], F32, "sigoS")
    tt("dve", gif[:], psG[:, 0:8], bif[:], ALU.add)
    act(l1[:], gif[:, 4:8], AF.Exp, scale=-1.0)
    act(l1[:], l1[:], AF.Ln, bias=1.0)
    self.sigm(sigo[:], psO[:, 0:512])
    psC = nps()
    mm(psC[:, 0:4], triS[:], l1[:])
    for b in range(4):
        mm(psC[:, 4 + 4 * b:8 + 4 * b], cselS[:, b, :], l1[:])
    gsb = fw.sb(s, [128, 20], F32, "gsbS")
    cp("dve", gsb[:], psC[:, 0:20])
    wl = fw.sb(s, [128, 4], F32, "wlS")
    ul = fw.sb(s, [128, 4], F32, "ulS")
    tmp4 = fw.sb(s, [128, 4], F32, "tmp4S")
    own = fw.sb(s, [128, 4], F32, "ownS")
    dec = fw.sb(s, [128, 4], F32, "decS")
    ebt = fw.sb(s, [128, 16], F32, "ebtS")
    act(wl[:], gsb[:, 0:4], AF.Exp, scale=-1.0)
    tt("dve", tmp4[:], gif[:, 0:4], gsb[:, 0:4], ALU.add)
    act(ul[:], tmp4[:], AF.Exp)
    act(ebt[:], gsb[:, 4:20], AF.Exp, scale=-1.0)
    ts("dve", own[:], gsb[:, 4:8], cselS[:, 0, 0:1], None, op0=ALU.mult)
    for b in range(1, 4):
        stt("dve", own[:], gsb[:, 4 + 4 * b:8 + 4 * b], cselS[:, b, 0:1], own[:], ALU.mult, ALU.add)
    tt("dve", dec[:], tmp4[:], own[:], ALU.subtract)
    vmu = fw.sb(s, [128, 4, 129], BF16, "vmuS")
    tt("dve", vmu[:, :, 0:128], psV[:, 0:512].re("p (h e) -> p h e", h=4), ul[:].un(2).bc([128, 4, 128]), ALU.mult)
    cp("dve", vmu[:, :, 128], ul[:])
    psT = nps()
    tr(psT[0:4, 0:128], dec[:], identf[:])
    mm(psT[0:4, 128:132], l1[:], cselS[:, :, 0])
    tsb = fw.sb(s, [4, 132], F32, "tsbS")
    cp("dve", tsb[:], psT[0:4, 0:132])
    Dm = fw.sb(s, [4, 4], F32, "DmS")
    self.rmax(Dm[:], tsb[:, 0:16].re("p (b i) -> p b i", b=4))
    R = fw.sb(s, [4, 4], F32, "RS")
    dmas(R[:], V(None, S["sm"].a.rearrange("(b h) -> h b", h=4)))
    tt("dve", R[:], R[:], tsb[:, 128:132], ALU.subtract)
    tt("dve", R[:], R[:], Dm[:], ALU.max)
    dmas(V(None, S["s_m"].a.rearrange("(b h) -> h b", h=4)), R[:])
    xcv = fw.sb(s, [128, 4, 4, 7], F32, "xcvS")
    for b in range(4):
        for ch in range(4):
            dmas(xcv[:, ch, b, 0:3], V(None, S["sconv"].a[b, :, ch * 128:(ch + 1) * 128].rearrange("j p -> p j")))
    psX = nps()
    for ch in range(4):
        for k in range(8):
            mm(psX[:, ch * 16:(ch + 1) * 16], win_b[:, k, 1304 + ch * 128:1432 + ch * 128], hT[:, k, 0:16],
               st=(k == 0), sp=(k == 7))
    cp("act", xcv[:, :, :, 3:7], psX[:, 0:64].re("p (c b i) -> p c b i", c=4, b=4))
    cacc = fw.sb(s, [128, 4, 16], F32, "caccS")
    for ch in range(4):
        cv = cacc[:, ch, :].re("p (b i) -> p b i", b=4)
        ts("dve", cv, xcv[:, ch, :, 0:4], cw[:, ch, 0:1], cb[:, ch:ch + 1], op0=ALU.mult, op1=ALU.add)
        for j in range(1, 4):
            stt("dve", cv, xcv[:, ch, :, j:j + 4], cw[:, ch, j:j + 1], cv, ALU.mult, ALU.add)
    xc = fw.sb(s, [128, 4, 16], BF16, "xcS")
    sgc = fw.sb(s, [128, 4, 16], F32, "sgcS")
    self.sigm(sgc[:], cacc[:])
    tt("dve", xc[:], cacc[:], sgc[:], ALU.mult)
    for b in range(4):
        for j in range(3):
            dmas(V(None, S["s_conv"].a[b, j].rearrange("(c p) -> p c", p=128)), xcv[:, :, b, 4 + j])
    qmT = fw.sb(s, [128, 4, 16], BF16, "qmTS")
    kmT = fw.sb(s, [128, 4, 16], BF16, "kmTS")
    qmS = [fw.sb(s, [128, 4, 16], BF16, f"qmSS{b}") for b in range(4)]
    kmS = [fw.sb(s, [16, 4, 128], BF16, f"kmSS{b}") for b in range(4)]
    psq = nps()
    for h in range(4):
        mm(psq[:, h * 16:(h + 1) * 16], wqm_b[:, h, :], xc[:, h, :])
    cp("act", qmT[:].re("p h t -> p (h t)"), psq[:, 0:64])
    for b in range(4):
        ms("pool", qmS[b][:], 0.0)
        cp("dve", qmS[b][:, :, 4 * b:4 * b + 4], psq[:, 0:64].re("p (h t) -> p h t", h=4)[:, :, 4 * b:4 * b + 4])
    psk = nps()
    for h in range(4):
        mm(psk[:, h * 16:(h + 1) * 16], wkm_b[:, h, :], xc[:, h, :])
    act(kmT[:].re("p h t -> p (h t)"), psk[:, 0:64], AF.Copy, scale=KSC)
    pskt = nps()
    for h in range(4):
        mm(pskt[0:16, h * 128:(h + 1) * 128], xc[:, h, :], wkm_b[:, h, :])
    for b in range(4):
        ts("dve", kmS[b][:].re("p h t -> p (h t)"), pskt[0:16, :], cselS[0:16, b, 0:1], KSC, op0=ALU.mult, op1=ALU.mult)
    psqk = nps()
    for h in range(4):
        mm(psqk[0:16, h * 16:(h + 1) * 16], kmT[:, h, :], qmT[:, h, :])
    mqk = fw.sb(s, [16, 4, 16], BF16, "mqkS")
    tt("dve", mqk[:], psqk[0:16, 0:64].re("p (h t) -> p h t", h=4), bdS[:].un(1).bc([16, 4, 16]), ALU.mult)
    em0 = fw.sb(s, [128, 16], F32, "em0")
    dma(em0[:], V(None, S["sm"].a.partition_broadcast(128)))
    act(em0[:], em0[:], AF.Exp)
    Sf = [fw.sb(s, [128, 4, 129], F32, f"SfS{b}") for b in range(4)]
    Sb0 = [fw.sb(s, [128, 4, 129], BF16, f"Sb0S{b}") for b in range(4)]
    cst_ = [fw.sb(s, [128, 128], F32, f"c0st{i}") for i in range(2)]
    dS = fw.sb(s, [128, 4, 129], F32, "dSS")
    for b in range(4):
        for h in range(4):
            st = cst_[h % 2]
            dma(st[:], V(None, S["sC"].a[b, h]))
            ps = nps()
            tr(ps[:, 0:128], st[:], identf[:])
            cp("dve", Sf[b][:, h, 0:128], ps[:, 0:128])
        dmas(Sf[b][:, :, 128], V(None, S["sn"].a[b].rearrange("h d -> d h")))
        tt("dve", Sf[b][:], Sf[b][:], em0[:, 4 * b:4 * b + 4].un(2).bc([128, 4, 129]), ALU.mult)
        cp("pool", Sb0[b][:], Sf[b][:])
        pd = [nps(), nps()]
        for h in range(4):
            mm(pd[h // 2][:, (h % 2) * 129:(h % 2) * 129 + 129], kmS[b][:, h, :], vmu[0:16, h, :])
        tt("dve", Sf[b][:], Sf[b][:], ebt[:, 4 * b:4 * b + 4].un(2).bc([128, 4, 129]), ALU.mult)
        for hh in range(2):
            tt("dve", dS[:, 2 * hh:2 * hh + 2, :], pd[hh][:, 0:258].re("p (h e) -> p h e", h=2),
               ebt[:, 4 * b + 2 * hh:4 * b + 2 * hh + 2].un(2).bc([128, 2, 129]), ALU.mult)
        tt("dve", Sf[b][:], Sf[b][:], dS[:], ALU.add)
    pa = [nps(), nps()]
    for h in range(4):
        o_ = pa[h // 2][0:16, (h % 2) * 129:(h % 2) * 129 + 129]
        mm(o_, mqk[:, h, :], vmu[0:16, h, :], st=True, sp=False)
        for b in range(4):
            mm(o_, qmS[b][:, h, :], Sb0[b][:, h, :], st=False, sp=(b == 3))
    dn = fw.sb(s, [16, 4], F32, "dnS")
    t4 = fw.sb(s, [16, 4], F32, "t4S")
    hout = fw.sb(s, [16, 4, 128], F32, "houtS")
    hsq = fw.sb(s, [16, 4, 128], F32, "hsqS")
    ss4 = fw.sb(s, [16, 4], F32, "ss4m")
    rs4 = fw.sb(s, [16, 4], F32, "rs4m")
    for hh in range(2):
        av = pa[hh][0:16, 0:258].re("p (h e) -> p h e", h=2)
        tt("dve", dn[:, 2 * hh:2 * hh + 2], av[:, :, 128], wl[0:16, 2 * hh:2 * hh + 2], ALU.mult)
    stt("dve", t4[:], dn[:], -1.0, dn[:], ALU.mult, ALU.max)
    ts("dve", t4[:], t4[:], 1.0, None, op0=ALU.max)
    self.recip(t4[:], t4[:])
    tt("dve", t4[:], t4[:], wl[0:16, :], ALU.mult)
    for hh in range(2):
        av = pa[hh][0:16, 0:258].re("p (h e) -> p h e", h=2)
        tt("dve", hout[:, 2 * hh:2 * hh + 2, :], av[:, :, 0:128], t4[:, 2 * hh:2 * hh + 2].un(2).bc([16, 2, 128]), ALU.mult)
    tt("dve", hsq[:], hout[:], hout[:], ALU.mult)
    self.rsum(ss4[:], hsq[:])
    self.rstd(rs4[:], ss4[:], 1.0 / 128)
    tt("dve", hout[:], hout[:], rs4[:].un(2).bc([16, 4, 128]), ALU.mult)
    tt("dve", mixin[0:16, 512:1024], hout[:].re("p h e -> p (h e)"), sigo[0:16, :], ALU.mult)
    Rd = fw.sb(s, [4, 4, 4], F32, "RdS")
    tt("dve", Rd[:], R[:].un(2).bc([4, 4, 4]), identf[0:4, 0:4].un(1).bc([4, 4, 4]), ALU.mult)
    ones4 = fw.sb(s, [4, 128], F32, "ones4S")
    ms("pool", ones4[:], 1.0)
    ps = nps()
    mm(ps[:, 0:16], ones4[:], Rd[:].re("p b h -> p (b h)"))
    esc = fw.sb(s, [128, 16], F32, "escS")
    act(esc[:], ps[:, 0:16], AF.Exp, scale=-1.0)
    for b in range(4):
        tt("dve", Sf[b][:], Sf[b][:], esc[:, 4 * b:4 * b + 4].un(2).bc([128, 4, 129]), ALU.mult)
        dmas(V(None, S["s_n"].a[b].rearrange("h d -> d h")), Sf[b][:, :, 128])
        for h in range(4):
            ps = nps()
            tr(ps[:, 0:128], Sf[b][:, h, 0:128], identf[:])
            st = cst_[h % 2]
            cp("dve", st[:], ps[:, 0:128])
            dma(V(None, S["s_C"].a[b, h]), st[:])


Builder.sample_mlstm = sample_mlstm
Builder.sample_s0 = sample_s0

W_NAMES = ["w_in", "g_mix", "b_gate", "cmp_pe_k", "cmp_w1_k", "cmp_b1_k", "cmp_w2_k", "cmp_pe_v", "cmp_w1_v",
           "cmp_b1_v", "cmp_w2_v", "g_head_nsa", "conv_w", "conv_b", "w_qm", "w_km", "b_i", "b_f", "g_head_m",
           "w_out", "g_xa", "g_mem", "w_xq", "w_xk", "w_xv", "w_xo", "g_ffn", "w_gate", "w_up", "w_down", "g_final"]


def build_program(NT=32, sample=True, debug=False):
    nc = bass.Bass("TRN2", target_bir_lowering=False)
    b = Builder(nc, NT=NT, sample=sample, debug=debug)
    b.build()
    return nc, b


def core_inputs(inp, c, b, NT=32):
    f = lambda a: np.ascontiguousarray(a, dtype=np.float32)
    T = NT * 128
    m = {"xp": f(inp["x_prompt"][c, :T]), "memp": f(inp["mem_prompt"][c])}
    for n in W_NAMES:
        a = np.asarray(inp[n])
        if n != "g_final":
            a = a[0]
        m[n] = f(a).reshape(b.io[n].shape)
    if b.sample:
        sl = slice(4 * c, 4 * c + 4)
        m["xs"] = f(inp["x_sample"][sl]).reshape(16, D)
        for n, k in (("pool_kc", "cache_k_cmp"), ("pool_vc", "cache_v_cmp"), ("pool_ks", "cache_k_slc"), ("pool_vs", "cache_v_slc")):
            m[n] = np.asarray(inp[k][0], dtype=np.float32).reshape(5120, 16384)
        m["ptab"] = np.ascontiguousarray(inp["page_table"][sl], dtype=np.int32)
        m["stk"] = f(inp["state_k_win"][0, sl]).reshape(4, 512, 128)
        m["stv"] = f(inp["state_v_win"][0, sl]).reshape(4, 512, 128)
        m["sconv"] = f(inp["state_conv"][0, sl])
        m["sC"] = f(inp["state_C"][0, sl])
        m["sn"] = f(inp["state_n"][0, sl])
        m["sm"] = f(inp["state_m"][0, sl]).reshape(16)
        m["cmk"] = f(inp["cache_mem_k"][0, sl]).reshape(4, 256, D)
        m["cmv"] = f(inp["cache_mem_v"][0, sl]).reshape(4, 256, D)
    return {k: v for k, v in m.items() if k in b.io}


_PROG = {}


def kernel(**inp):
    n = 8
    if "p" not in _PROG:
        _PROG["p"] = build_program()
    nc, b = _PROG["p"]
    in_maps = [core_inputs(inp, c, b) for c in range(n)]
    res = run_bass_kernel_spmd(nc, in_maps, core_ids=list(range(n)))
    R = res.results

    def st(name, shp, lead):
        a = np.stack([np.asarray(R[c][name], dtype=np.float32).reshape(shp) for c in range(n)])
        return np.ascontiguousarray(a.reshape(lead))

    outs = [st("y_p", (4096, D), (8, 4096, D)), st("y_s", (4, 4, D), (32, 4, D))]
    for nm in ("p_kc", "p_vc", "p_ks", "p_vs"):
        outs.append(st(nm, (4096, 2, 64), (1, 8, 4096, 2, 64)))
    for nm in ("p_kw", "p_vw"):
        outs.append(st(nm, (512, 2, 64), (1, 8, 512, 2, 64)))
    outs.append(st("p_C", (4, 128, 128), (1, 8, 4, 128, 128)))
    outs.append(st("p_n", (4, 128), (1, 8, 4, 128)))
    outs.append(st("p_m", (4,), (1, 8, 4)))
    outs.append(st("p_conv", (3, 512), (1, 8, 3, 512)))
    outs.append(st("p_mk", (256, 4, 256), (1, 8, 256, 4, 256)))
    outs.append(st("p_mv", (256, 4, 256), (1, 8, 256, 4, 256)))
    for nm in ("s_kc", "s_vc", "s_ks", "s_vs"):
        outs.append(st(nm, (4, 4, 2, 64), (1, 32, 4, 2, 64)))
    for nm in ("s_kw", "s_vw"):
        outs.append(st(nm, (4, 512, 2, 64), (1, 32, 512, 2, 64)))
    outs.append(st("s_C", (4, 4, 128, 128), (1, 32, 4, 128, 128)))
    outs.append(st("s_n", (4, 4, 128), (1, 32, 4, 128)))
    outs.append(st("s_m", (4, 4), (1, 32, 4)))
    outs.append(st("s_conv", (4, 3, 512), (1, 32, 3, 512)))
    return tuple(outs)
```
